# Optimizing a Trainium2 kernel written in Bass

```python
import math
import jax, jax.numpy as jnp
from jax import lax
import numpy as np

D_MODEL = 1024
BATCH = 4
SEQ = 8192
DEPTH = 1

CTX_LEN = 256
GRID_W = 64
GDN_HEADS = 8
GDN_DK = 128
GDN_DV = 128
GDN_CONV = 3
ML_HEADS = 4
ML_DK = 128
ML_DV = 256
CHUNK = 64
GATE_CAP = 15.0
D_FF = 2816
FFN_CONV = 3
N_MOD = 6
EPS = 1e-6

GDN_QK = GDN_HEADS * GDN_DK
GDN_V = GDN_HEADS * GDN_DV
ML_QK = ML_HEADS * ML_DK
ML_V = ML_HEADS * ML_DV
STATE_SIZES = (2 * GDN_QK + GDN_V, 2 * GDN_HEADS, 2 * GDN_HEADS, ML_QK, ML_QK, ML_V, 2 * ML_HEADS, 2 * ML_HEADS)
OUT_SIZES = (GDN_V, ML_V, D_MODEL, D_MODEL)
N_STATE_COLS = sum(STATE_SIZES)
N_IN_COLS = N_STATE_COLS + sum(OUT_SIZES)

kernel_name = 'hybrid_gdn_mlstm_convffn_ctx_prefix'


def rms_norm(x, w):
    xf = x.astype(jnp.float32)
    y = xf * lax.rsqrt(jnp.mean(xf * xf, axis=-1, keepdims=True) + EPS)
    return (y * w.astype(jnp.float32)).astype(x.dtype)


def l2_normalize(x):
    xf = x.astype(jnp.float32)
    return xf * lax.rsqrt(jnp.sum(xf * xf, axis=-1, keepdims=True) + EPS)


def soft_cap(x):
    return GATE_CAP * jnp.tanh(x / GATE_CAP)


def split_cols(z, sizes):
    return jnp.split(z, [int(s) for s in np.cumsum(sizes)[:-1]], axis=-1)


def to_heads(a, n_heads):
    b, t, _ = a.shape
    return a.reshape(b, t, n_heads, -1).transpose(0, 2, 1, 3)


def dir_heads(a, n_heads):
    b, t, _ = a.shape
    return a.reshape(b, t, 2, n_heads).transpose(2, 0, 3, 1)


def flip_t(a):
    return jnp.flip(a, axis=2)


def dwconv_seq(x, w):
    k = w.shape[0]
    return lax.conv_general_dilated(x, w[:, None, :], window_strides=(1,), padding=[(k // 2, k // 2)],
                                    dimension_numbers=('NWC', 'WIO', 'NWC'), feature_group_count=x.shape[-1])


def dwconv_grid(x, w):
    b, t, ch = x.shape
    rows = t // GRID_W
    kh, kw = w.shape[0], w.shape[1]
    y = lax.conv_general_dilated(x.reshape(b, rows, GRID_W, ch), w[:, :, None, :], window_strides=(1, 1),
                                 padding=[(kh // 2, kh // 2), (kw // 2, kw // 2)],
                                 dimension_numbers=('NHWC', 'HWIO', 'NHWC'), feature_group_count=ch)
    return y.reshape(b, t, ch)


def gdn_chunked(q, k, v, g, beta, s0, with_output):
    f32 = jnp.float32
    b, h, t, dk = q.shape
    dv = v.shape[-1]
    n = t // CHUNK
    k = k.astype(f32).reshape(b, h, n, CHUNK, dk)
    v = v.astype(f32).reshape(b, h, n, CHUNK, dv)
    beta = beta.astype(f32).reshape(b, h, n, CHUNK)
    gc = jnp.cumsum(g.astype(f32).reshape(b, h, n, CHUNK), axis=-1)
    lower = jnp.tril(jnp.ones((CHUNK, CHUNK), dtype=bool))
    decay = jnp.exp(jnp.where(lower, gc[..., :, None] - gc[..., None, :], -jnp.inf))
    kb = k * beta[..., None]
    a_strict = jnp.einsum('bhnld,bhnsd->bhnls', kb, k) * decay
    rhs = jnp.concatenate([v * beta[..., None], kb * jnp.exp(gc)[..., None]], axis=-1)
    sol = lax.linalg.triangular_solve(a_strict, rhs, left_side=True, lower=True, unit_diagonal=True)
    u, w = sol[..., :dv], sol[..., dv:]
    k_tail = k * jnp.exp(gc[..., -1:] - gc)[..., None]
    g_tot = jnp.exp(gc[..., -1])
    xs = [u, w, k_tail, g_tot]
    if with_output:
        q = q.astype(f32).reshape(b, h, n, CHUNK, dk) * (dk ** -0.5)
        attn = jnp.einsum('bhnld,bhnsd->bhnls', q, k) * decay
        xs += [q * jnp.exp(gc)[..., None], attn]
    xs = tuple(jnp.moveaxis(a, 2, 0) for a in xs)

    def step(s, xc):
        u_n, w_n, kt_n, gt_n = xc[:4]
        v_new = u_n - jnp.einsum('bhld,bhde->bhle', w_n, s)
        s_next = s * gt_n[..., None, None] + jnp.einsum('bhld,bhle->bhde', kt_n, v_new)
        if not with_output:
            return s_next, None
        qd_n, at_n = xc[4:]
        o = jnp.einsum('bhld,bhde->bhle', qd_n, s) + jnp.einsum('bhls,bhse->bhle', at_n, v_new)
        return s_next, o

    s_fin, o = lax.scan(step, s0, xs)
    if with_output:
        o = jnp.moveaxis(o, 0, 2).reshape(b, h, t, dv)
    return o, s_fin


def mlstm_chunked(q, k, v, ig, lf, state, with_output):
    f32 = jnp.float32
    b, h, t, dk = q.shape
    dv = v.shape[-1]
    n = t // CHUNK
    k = k.astype(f32).reshape(b, h, n, CHUNK, dk)
    v = v.astype(f32).reshape(b, h, n, CHUNK, dv)
    ig = ig.astype(f32).reshape(b, h, n, CHUNK)
    bc = jnp.cumsum(lf.astype(f32).reshape(b, h, n, CHUNK), axis=-1)
    b_last = bc[..., -1]
    tail = b_last[..., None] - bc + ig
    tail_max = jnp.max(tail, axis=-1)
    xs = [k, v, b_last, tail, tail_max]
    if with_output:
        q = q.astype(f32).reshape(b, h, n, CHUNK, dk) * (dk ** -0.5)
        lower = jnp.tril(jnp.ones((CHUNK, CHUNK), dtype=bool))
        dmat = jnp.where(lower, bc[..., :, None] - bc[..., None, :] + ig[..., None, :], -jnp.inf)
        qk = jnp.einsum('bhnld,bhnsd->bhnls', q, k)
        xs += [q, bc, dmat, jnp.max(dmat, axis=-1), qk]
    xs = tuple(jnp.moveaxis(a, 2, 0) for a in xs)

    def step(carry, xc):
        c_st, n_st, m = carry
        k_n, v_n, bl_n, tl_n, tm_n = xc[:5]
        m_new = jnp.maximum(bl_n + m, tm_n)
        wt = jnp.exp(tl_n - m_new[..., None])
        dec = jnp.exp(bl_n + m - m_new)
        c_next = dec[..., None, None] * c_st + jnp.einsum('bhld,bhle->bhde', k_n * wt[..., None], v_n)
        n_next = dec[..., None] * n_st + jnp.einsum('bhl,bhld->bhd', wt, k_n)
        if not with_output:
            return (c_next, n_next, m_new), None
        q_n, b_n, d_n, dm_n, qk_n = xc[5:]
        m_t = jnp.maximum(b_n + m[..., None], dm_n)
        inter = jnp.exp(b_n + m[..., None] - m_t)
        p = qk_n * jnp.exp(d_n - m_t[..., None])
        num = inter[..., None] * jnp.einsum('bhld,bhde->bhle', q_n, c_st) + jnp.einsum('bhls,bhse->bhle', p, v_n)
        den = inter * jnp.einsum('bhld,bhd->bhl', q_n, n_st) + jnp.sum(p, axis=-1)
        out = num / jnp.maximum(jnp.abs(den), jnp.exp(-m_t))[..., None]
        return (c_next, n_next, m_new), out

    fin, out = lax.scan(step, state, xs)
    if with_output:
        out = jnp.moveaxis(out, 0, 2).reshape(b, h, t, dv)
    return out, fin


def token_mixer(hn, states, with_output, w_in, gdn_conv, gdn_a_log, gdn_dt_bias, gdn_norm_w,
                ml_igate_b, ml_fgate_b, ml_norm_w, w_branch_gdn, w_branch_ml, w_out):
    bsz, t, _ = hn.shape
    s_gdn_f, s_gdn_b, s_ml_f, s_ml_b = states
    zs = hn @ w_in[:, :N_STATE_COLS]
    gdn_qkv, gdn_a, gdn_b, ml_q, ml_k, ml_v, ml_i, ml_f = split_cols(zs, STATE_SIZES)
    gdn_qkv = jax.nn.silu(dwconv_seq(gdn_qkv, gdn_conv))
    gq, gk, gv = split_cols(gdn_qkv, (GDN_QK, GDN_QK, GDN_V))
    gq = l2_normalize(to_heads(gq, GDN_HEADS))
    gk = l2_normalize(to_heads(gk, GDN_HEADS))
    gv = to_heads(gv, GDN_HEADS)
    a = dir_heads(gdn_a, GDN_HEADS).astype(jnp.float32)
    log_decay = -jnp.exp(gdn_a_log.astype(jnp.float32))[:, None, :, None] * jax.nn.softplus(
        a + gdn_dt_bias.astype(jnp.float32)[:, None, :, None])
    beta = jax.nn.sigmoid(dir_heads(gdn_b, GDN_HEADS))
    o_f, s_gdn_f = gdn_chunked(gq, gk, gv, log_decay[0], beta[0], s_gdn_f, with_output)
    o_b, s_gdn_b = gdn_chunked(flip_t(gq), flip_t(gk), flip_t(gv), flip_t(log_decay[1]), flip_t(beta[1]),
                               s_gdn_b, with_output)
    mq = to_heads(ml_q, ML_HEADS)
    mk = to_heads(ml_k, ML_HEADS)
    mv = to_heads(ml_v, ML_HEADS)
    ig = soft_cap(dir_heads(ml_i, ML_HEADS).astype(jnp.float32) + ml_igate_b.astype(jnp.float32)[:, None, :, None])
    lf = jax.nn.log_sigmoid(soft_cap(dir_heads(ml_f, ML_HEADS).astype(jnp.float32)
                                     + ml_fgate_b.astype(jnp.float32)[:, None, :, None]))
    h_f, s_ml_f = mlstm_chunked(mq, mk, mv, ig[0], lf[0], s_ml_f, with_output)
    h_b, s_ml_b = mlstm_chunked(flip_t(mq), flip_t(mk), flip_t(mv), flip_t(ig[1]), flip_t(lf[1]),
                                s_ml_b, with_output)
    new_states = (s_gdn_f, s_gdn_b, s_ml_f, s_ml_b)
    if not with_output:
        return None, new_states
    zo = hn @ w_in[:, N_STATE_COLS:]
    gdn_z, ml_o, gate_gdn, gate_ml = split_cols(zo, OUT_SIZES)
    o = (o_f + flip_t(o_b)).astype(hn.dtype).transpose(0, 2, 1, 3)
    o = rms_norm(o, gdn_norm_w) * jax.nn.silu(gdn_z.reshape(bsz, t, GDN_HEADS, GDN_DV))
    y_gdn = o.reshape(bsz, t, GDN_V) @ w_branch_gdn
    hm = (h_f + flip_t(h_b)).astype(hn.dtype).transpose(0, 2, 1, 3)
    hm = rms_norm(hm, ml_norm_w) * jax.nn.sigmoid(ml_o.reshape(bsz, t, ML_HEADS, ML_DV))
    y_ml = hm.reshape(bsz, t, ML_V) @ w_branch_ml
    merged = jax.nn.sigmoid(gate_gdn) * y_gdn + jax.nn.sigmoid(gate_ml) * y_ml
    return merged @ w_out, new_states


def conv_ffn(hn, w_up, conv_w, w_down, on_grid):
    u = hn @ w_up
    u = dwconv_grid(u, conv_w) if on_grid else dwconv_seq(u, conv_w[FFN_CONV // 2])
    gate, val = u[..., :D_FF], u[..., D_FF:]
    return (jax.nn.silu(gate) * val) @ w_down


def setup_inputs(seed: int = 0) -> dict:
    key = jax.random.key(seed)
    ks = jax.random.split(key, 32)
    f32 = jnp.float32

    def nrm(k, shape, scale):
        return jax.random.normal(k, shape, f32) * scale

    L, D = DEPTH, D_MODEL
    dt = jnp.exp(jax.random.uniform(ks[10], (L, 2, GDN_HEADS), f32, math.log(1e-3), math.log(0.1)))
    return {
        'x': nrm(ks[0], (BATCH, SEQ, D), 1.0),
        'c': nrm(ks[1], (BATCH, D), 1.0),
        'ctx': nrm(ks[2], (BATCH, CTX_LEN, D), 1.0),
        'c_ctx': nrm(ks[3], (D,), 1.0),
        'w_ada': nrm(ks[4], (L, D, N_MOD * D), 0.5 * D ** -0.5),
        'b_ada': nrm(ks[5], (L, N_MOD * D), 0.02),
        'norm1_w': 1.0 + nrm(ks[6], (L, D), 0.02),
        'w_in': nrm(ks[7], (L, D, N_IN_COLS), D ** -0.5),
        'gdn_conv': nrm(ks[8], (L, GDN_CONV, 2 * GDN_QK + GDN_V), GDN_CONV ** -0.5),
        'gdn_a_log': jnp.log(jax.random.uniform(ks[9], (L, 2, GDN_HEADS), f32, 1.0, 16.0)),
        'gdn_dt_bias': dt + jnp.log(-jnp.expm1(-dt)),
        'gdn_norm_w': 1.0 + nrm(ks[11], (L, GDN_DV), 0.02),
        'ml_igate_b': nrm(ks[12], (L, 2, ML_HEADS), 0.1),
        'ml_fgate_b': 3.0 + nrm(ks[13], (L, 2, ML_HEADS), 0.5),
        'ml_norm_w': 1.0 + nrm(ks[14], (L, ML_HEADS, ML_DV), 0.02),
        'w_branch_gdn': nrm(ks[15], (L, GDN_V, D), GDN_V ** -0.5),
        'w_branch_ml': nrm(ks[16], (L, ML_V, D), ML_V ** -0.5),
        'w_out': nrm(ks[17], (L, D, D), D ** -0.5),
        'norm2_w': 1.0 + nrm(ks[18], (L, D), 0.02),
        'w_up': nrm(ks[19], (L, D, 2 * D_FF), D ** -0.5),
        'ffn_conv': nrm(ks[20], (L, FFN_CONV, FFN_CONV, 2 * D_FF), 1.0 / FFN_CONV),
        'w_down': nrm(ks[21], (L, D_FF, D), D_FF ** -0.5),
        'norm_out_w': 1.0 + nrm(ks[22], (D,), 0.02),
    }


def reference(x, c, ctx, c_ctx, w_ada, b_ada, norm1_w, w_in, gdn_conv, gdn_a_log, gdn_dt_bias, gdn_norm_w,
              ml_igate_b, ml_fgate_b, ml_norm_w, w_branch_gdn, w_branch_ml, w_out, norm2_w, w_up, ffn_conv,
              w_down, norm_out_w):
    f32 = jnp.float32
    bsz = x.shape[0]
    s_gdn0 = jnp.zeros((bsz, GDN_HEADS, GDN_DK, GDN_DV), f32)
    s_ml0 = (jnp.zeros((bsz, ML_HEADS, ML_DK, ML_DV), f32), jnp.zeros((bsz, ML_HEADS, ML_DK), f32),
             jnp.zeros((bsz, ML_HEADS), f32))
    zero_states = (s_gdn0, s_gdn0, s_ml0, s_ml0)
    for l in range(DEPTH):
        last = l == DEPTH - 1
        mix_w = (w_in[l], gdn_conv[l], gdn_a_log[l], gdn_dt_bias[l], gdn_norm_w[l], ml_igate_b[l],
                 ml_fgate_b[l], ml_norm_w[l], w_branch_gdn[l], w_branch_ml[l], w_out[l])
        mod_x = (jax.nn.silu(c) @ w_ada[l] + b_ada[l]).reshape(bsz, N_MOD, 1, D_MODEL)
        mod_c = (jax.nn.silu(c_ctx) @ w_ada[l] + b_ada[l]).reshape(N_MOD, D_MODEL)
        hc = rms_norm(ctx, norm1_w[l]) * (1.0 + mod_c[1]) + mod_c[0]
        ctx_mix, ctx_states = token_mixer(hc, zero_states, not last, *mix_w)
        hx = rms_norm(x, norm1_w[l]) * (1.0 + mod_x[:, 1]) + mod_x[:, 0]
        x_mix, _ = token_mixer(hx, ctx_states, True, *mix_w)
        x = x + mod_x[:, 2] * x_mix
        hx = rms_norm(x, norm2_w[l]) * (1.0 + mod_x[:, 4]) + mod_x[:, 3]
        x = x + mod_x[:, 5] * conv_ffn(hx, w_up[l], ffn_conv[l], w_down[l], True)
        if not last:
            ctx = ctx + mod_c[2] * ctx_mix
            hc = rms_norm(ctx, norm2_w[l]) * (1.0 + mod_c[4]) + mod_c[3]
            ctx = ctx + mod_c[5] * conv_ffn(hc, w_up[l], ffn_conv[l], w_down[l], False)
    return rms_norm(x, norm_out_w)
```

```python
import numpy as np
from contextlib import ExitStack

import concourse.bass as bass
import concourse.mybir as mybir
from concourse.bass_utils import run_bass_kernel_spmd

F32 = mybir.dt.float32
BF16 = mybir.dt.bfloat16
AF = mybir.ActivationFunctionType
ALU = mybir.AluOpType
AX = mybir.AxisListType

D = 1024
T = 8192
TC = 256
KD = 8
EPS = 1e-6
NEG = -1.0e30


class Buf:
    __slots__ = ("name", "w", "r", "dsem")

    def __init__(self, name):
        self.name = name
        self.w = None
        self.r = {}
        self.dsem = None


class Builder:
    def __init__(self):
        self.nc = bass.Bass("TRN2", target_bir_lowering=False)
        nc = self.nc
        self.es = ExitStack()
        self.es.enter_context(nc.allow_low_precision("bf16 matmul operands, fp32 accumulation"))
        self.engs = {"pe": nc.tensor, "act": nc.scalar, "dve": nc.vector, "pool": nc.gpsimd, "sp": nc.sync}
        self.sems = {}
        self.cnt = {}
        self.seen = {e: {} for e in self.engs}
        for e in self.engs:
            self.sems[e] = self.es.enter_context(nc.semaphore("s_" + e))
            self.cnt[e] = 0
        self.ndsem = 0
        self.nins = 0

    def sb(self, stack, name, shape, dt):
        return stack.enter_context(self.nc.sbuf_tensor(name, list(shape), dt))

    def ps(self, stack, name, shape, dt=F32):
        return stack.enter_context(self.nc.psum_tensor(name, list(shape), dt))

    def dram(self, name, shape, dt, kind="Internal"):
        return self.nc.dram_tensor(name, list(shape), dt, kind=kind).ap()

    def new_dsem(self):
        k = "d%d" % self.ndsem
        self.ndsem += 1
        self.sems[k] = self.es.enter_context(self.nc.semaphore(k))
        self.cnt[k] = 0
        return k

    def _deps(self, eng, reads, writes):
        deps = {}

        def add(k, v):
            if deps.get(k, 0) < v:
                deps[k] = v

        for b in reads:
            if b.w is not None:
                add(*b.w)
        for b in writes:
            if b.w is not None and b.w[0] != eng:
                add(*b.w)
            for k, v in b.r.items():
                if k != eng:
                    add(k, v)
        return deps

    def _emit_waits(self, eng, deps):
        e = self.engs[eng]
        seen = self.seen[eng]
        for k, v in deps.items():
            if seen.get(k, 0) >= v:
                continue
            assert v <= self.cnt[k], "wait on %s=%d never reached (issued %d)" % (k, v, self.cnt[k])
            e.wait_ge(self.sems[k], v)
            seen[k] = v

    def op(self, eng, fn, reads=(), writes=(), inc=True):
        self._emit_waits(eng, self._deps(eng, reads, writes))
        ins = fn(self.engs[eng])
        self.nins += 1
        if inc:
            self.cnt[eng] += 1
            ins.then_inc(self.sems[eng], 1)
            tok = (eng, self.cnt[eng])
        else:
            tok = (eng, self.cnt[eng] + 1)
        for b in reads:
            if b.r.get(eng, 0) < tok[1]:
                b.r[eng] = tok[1]
        for b in writes:
            b.w = tok
            b.r = {}
        return tok

    def dma(self, q, out, in_, sem_buf, reads=(), writes=()):
        self._emit_waits(q, self._deps("__dma__", reads, writes))
        ins = self.engs[q].dma_start(out=out, in_=in_)
        self.nins += 1
        if sem_buf.dsem is None:
            sem_buf.dsem = self.new_dsem()
        k = sem_buf.dsem
        self.cnt[k] += 16
        ins.then_inc(self.sems[k], 16)
        tok = (k, self.cnt[k])
        for b in reads:
            if b.r.get(k, 0) < tok[1]:
                b.r[k] = tok[1]
        for b in writes:
            b.w = tok
            b.r = {}
        return tok

    def barrier(self):
        for e in self.engs:
            self._emit_waits(e, {k: v for k, v in self.cnt.items() if k != e and v > 0})

    def finish(self):
        self.barrier()
        self.es.close()
        return self.nc


OFF_QKV, OFF_A, OFF_B, OFF_MQ, OFF_MK, OFF_MV, OFF_MI, OFF_MF, OFF_Z, OFF_MO, OFF_GG, OFF_GM, OFF_END = (
    0, 3072, 3088, 3104, 3616, 4128, 5152, 5160, 5168, 6192, 7216, 8240, 9264)
NCH = 66
OWN_T = 4224


def _col(v, n=128):
    v = np.asarray(v, np.float32).reshape(-1, n)
    return np.ascontiguousarray(v.T)


def _rep(v):
    v = np.asarray(v, np.float32).reshape(1, -1)
    return np.ascontiguousarray(np.repeat(v, 128, axis=0))


def _swapdir(a, flip):
    if not flip:
        return a
    h = a.shape[-1] // 2
    return np.concatenate([a[..., h:], a[..., :h]], axis=-1)


def prep_core(inp, core):
    b = core // 2
    flip = core % 2
    f32 = np.float32
    x = inp["x"][b]
    ctx = inp["ctx"][b]
    if flip:
        x = x[::-1]
        ctx = ctx[::-1]
    w_in = inp["w_in"][0]
    m = {}
    m["x"] = np.ascontiguousarray(x, dtype=f32)
    m["ctx"] = np.ascontiguousarray(ctx, dtype=f32)
    m["c_col"] = _col(inp["c"][b])
    m["cc_col"] = _col(inp["c_ctx"])
    m["w_ada"] = np.ascontiguousarray(inp["w_ada"][0], dtype=f32)
    b_ada = inp["b_ada"][0]
    m["b_ada_col"] = _col(b_ada)
    m["b_ada_g"] = np.ascontiguousarray(np.concatenate([_rep(b_ada[2048:3072]), _rep(b_ada[5120:6144])], axis=1))
    m["n1_col"] = _col(inp["norm1_w"][0])
    m["n2_col"] = _col(inp["norm2_w"][0])
    m["w_qkv"] = np.ascontiguousarray(w_in[:, OFF_QKV:OFF_A])
    wg = np.concatenate([_swapdir(w_in[:, OFF_A:OFF_B], flip), _swapdir(w_in[:, OFF_B:OFF_MQ], flip),
                         _swapdir(w_in[:, OFF_MI:OFF_MF], flip), _swapdir(w_in[:, OFF_MF:OFF_Z], flip)], axis=1)
    m["w_gate"] = np.ascontiguousarray(wg)
    m["w_ml"] = np.ascontiguousarray(w_in[:, OFF_MQ:OFF_MI])
    m["w_o"] = np.ascontiguousarray(w_in[:, OFF_Z:OFF_END])
    gp = np.concatenate([_swapdir(inp["gdn_dt_bias"][0].reshape(-1), flip), _swapdir(inp["gdn_a_log"][0].reshape(-1), flip),
                         _swapdir(inp["ml_igate_b"][0].reshape(-1), flip), _swapdir(inp["ml_fgate_b"][0].reshape(-1), flip)])
    m["gate_p"] = _rep(gp)
    gc = inp["gdn_conv"][0]
    if flip:
        gc = gc[::-1]
    m["gdn_cw"] = np.ascontiguousarray(gc.T.reshape(24, 128, 3).transpose(1, 0, 2), dtype=f32)
    fc = inp["ffn_conv"][0]
    if flip:
        fc = fc[::-1, ::-1]
    m["ffn_cw"] = np.ascontiguousarray(fc.reshape(9, 44, 128).transpose(2, 1, 0), dtype=f32)
    m["gnw_bc"] = _rep(np.tile(inp["gdn_norm_w"][0], 8))
    m["mnw_bc"] = _rep(inp["ml_norm_w"][0].reshape(-1))
    m["now_bc"] = _rep(inp["norm_out_w"])
    m["w_bg"] = np.ascontiguousarray(inp["w_branch_gdn"][0], dtype=f32)
    m["w_bm"] = np.ascontiguousarray(inp["w_branch_ml"][0], dtype=f32)
    m["w_out"] = np.ascontiguousarray(inp["w_out"][0], dtype=f32)
    m["w_up"] = np.ascontiguousarray(inp["w_up"][0], dtype=f32)
    m["w_down"] = np.ascontiguousarray(inp["w_down"][0], dtype=f32)
    m["smask"] = make_smask()
    return m


def make_smask():
    idx = np.arange(128)
    i = idx[None, :]
    j = idx[:, None]
    out = np.zeros((128, 14, 128), np.float32)
    for lev in range(7):
        b = 1 << lev
        same = (i // (2 * b)) == (j // (2 * b))
        f = same & ((i % (2 * b)) < b) & ((j % (2 * b)) >= b)
        g = same & ((j % (2 * b)) < b) & ((i % (2 * b)) >= b)
        out[:, lev, :] = np.where(f, -1.0, 0.0) + np.eye(128)
        out[:, 7 + lev, :] = np.where(g, -1.0, 0.0) + np.eye(128)
    return out


IN_SHAPES = {
    "x": [T, D], "ctx": [TC, D], "c_col": [128, 8], "cc_col": [128, 8], "w_ada": [D, 6144],
    "b_ada_col": [128, 48], "b_ada_g": [128, 2048], "n1_col": [128, 8], "n2_col": [128, 8],
    "w_qkv": [D, 3072], "w_gate": [D, 48], "w_ml": [D, 2048], "w_o": [D, 4096], "gate_p": [128, 48],
    "gdn_cw": [128, 24, 3], "ffn_cw": [128, 44, 9], "gnw_bc": [128, 1024], "mnw_bc": [128, 1024],
    "now_bc": [128, 1024], "w_bg": [D, D], "w_bm": [D, D], "w_out": [D, D], "w_up": [D, 5632], "w_down": [2816, D],
    "smask": [128, 14, 128],
}


class Ring:
    def __init__(self, B, stack, name, n, shape, dt, psum=False):
        self.slots = []
        for i in range(n):
            t = (B.ps if psum else B.sb)(stack, "%s%d" % (name, i), shape, dt)
            self.slots.append((t, Buf("%s%d" % (name, i))))
        self.i = 0

    def next(self):
        s = self.slots[self.i % len(self.slots)]
        self.i += 1
        return s


class Prog:
    def __init__(self, debug=None):
        self.debug = debug or {}
        self.B = Builder()
        self.nc = self.B.nc
        self.top = ExitStack()
        self.inp = {}
        for k, shp in IN_SHAPES.items():
            self.inp[k] = self.nc.dram_tensor(k, list(shp), F32, kind="ExternalInput").ap()
        self.out = self.nc.dram_tensor("out", [4096, D], F32, kind="ExternalOutput").ap()
        dk = "ExternalOutput" if self.debug.get("scratch_out") else "Internal"
        B = self.B
        self.KT = B.dram("KT", [NCH, 128, 8, 128], BF16, dk)
        self.QT = B.dram("QT", [NCH, 128, 8, 128], BF16, dk)
        self.VG = B.dram("VG", [NCH, 128, 1024], BF16, dk)
        self.MQT = B.dram("MQT", [NCH, 128, 4, 128], BF16, dk)
        self.MKT = B.dram("MKT", [NCH, 128, 4, 128], BF16, dk)
        self.MV = B.dram("MV", [NCH, 128, 1024], BF16, dk)
        self.GT = B.dram("GT", [NCH, 128, 48], F32, dk)
        self.OF = B.dram("OF", [33, 128, 1024], F32, dk)
        self.OB = B.dram("OB", [33, 128, 1024], F32, dk)
        self.HF = B.dram("HF", [33, 128, 1024], F32, dk)
        self.HB = B.dram("HB", [33, 128, 1024], F32, dk)
        self.X1 = B.dram("X1", [64 + OWN_T, D], F32, dk)
        self.consts()

    def consts(self):
        B, st = self.B, self.top
        self.ident_f = B.sb(st, "ident_f", [128, 128], F32)
        self.ident_b = B.sb(st, "ident_b", [128, 128], BF16)
        self.ones_f = B.sb(st, "ones_f", [128, 128], F32)
        self.ones_b = B.sb(st, "ones_b", [128, 128], BF16)
        self.nhalf = B.sb(st, "nhalf", [128, 512], F32)
        self.cb = Buf("consts")
        cb = self.cb
        B.op("pool", lambda e: e.memset(self.ones_f[:], 1.0), writes=[cb])
        B.op("pool", lambda e: e.memset(self.ones_b[:], 1.0), writes=[cb])
        B.op("pool", lambda e: e.memset(self.nhalf[:], -0.5), writes=[cb])
        B.op("pool", lambda e: e.memset(self.ident_f[:], 1.0), writes=[cb])
        B.op("pool", lambda e: e.affine_select(self.ident_f[:], self.ident_f[:], pattern=[[-1, 128]], compare_op=ALU.is_equal,
                                               fill=0.0, base=0, channel_multiplier=1), reads=[cb], writes=[cb])
        B.op("dve", lambda e: e.tensor_copy(out=self.ident_b[:], in_=self.ident_f[:]), reads=[cb], writes=[cb])
        self.modc = B.sb(st, "modc", [128, 6, 8], F32)
        self.bmod = Buf("modc")
        self.gate_bc = B.sb(st, "gate_bc", [128, 2, 1024], F32)
        self.bgate = Buf("gate_bc")

    def mask(self, stack, name, cmp_pat, dt=F32, val=1.0, fill=0.0):
        B = self.B
        base, cm, step, cmp = cmp_pat
        t = B.sb(stack, name, [128, 128], dt)
        tf = t
        if dt != F32:
            tf = B.sb(stack, name + "_f", [128, 128], F32)
        b = Buf(name)
        B.op("pool", lambda e: e.memset(tf[:], val), writes=[b])
        B.op("pool", lambda e: e.affine_select(tf[:], tf[:], pattern=[[step, 128]], compare_op=cmp, fill=fill,
                                               base=base, channel_multiplier=cm), reads=[b], writes=[b])
        if dt != F32:
            B.op("dve", lambda e: e.tensor_copy(out=t[:], in_=tf[:]), reads=[b], writes=[b])
        return t, b

    def phase0(self):
        B, nc, inp = self.B, self.nc, self.inp
        st = ExitStack()
        sc = B.sb(st, "p0_sc", [128, 16], F32)
        bsc = Buf("p0_sc")
        scb = B.sb(st, "p0_scb", [128, 8, 128], F32)
        bscb = Buf("p0_scb")
        bcol = B.sb(st, "p0_bcol", [128, 48], F32)
        n12 = B.sb(st, "p0_n12", [128, 16], F32)
        bg = B.sb(st, "p0_bg", [128, 2048], F32)
        bsm = Buf("p0_small")
        B.dma("sp", sc[:, 0:8], inp["c_col"][:, :], bsc, writes=[bsc])
        B.dma("sp", sc[:, 8:16], inp["cc_col"][:, :], bsc, writes=[bsc])
        B.dma("sp", bcol[:], inp["b_ada_col"][:, :], bsm, writes=[bsm])
        B.dma("sp", n12[:, 0:8], inp["n1_col"][:, :], bsm, writes=[bsm])
        B.dma("sp", n12[:, 8:16], inp["n2_col"][:, :], bsm, writes=[bsm])
        B.dma("sp", bg[:], inp["b_ada_g"][:, :], bsm, writes=[bsm])
        B.op("act", lambda e: e.activation(out=sc[:], in_=sc[:], func=AF.Silu), reads=[bsc], writes=[bsc])
        for k in range(8):
            B.op("dve", lambda e, k=k: e.tensor_scalar(out=scb[:, k, :], in0=self.ones_f[:], scalar1=sc[:, k:k + 1], scalar2=None,
                                                       op0=ALU.mult), reads=[bsc, self.cb], writes=[bscb])
        wring = Ring(B, st, "p0_w", 2, [128, 8, 512], F32)
        pcol = B.ps(st, "p0_pcol", [128, 64], F32)
        bpcol = Buf("p0_pcol")
        prow = Ring(B, st, "p0_prow", 2, [128, 512], F32, psum=True)
        wv = inp["w_ada"].rearrange("(k p) n -> p k n", p=128)
        xslot = {0: 0, 1: 1, 3: 2, 4: 3}
        for nb in range(12):
            v, half = nb // 2, nb % 2
            w, bw = wring.next()
            B.dma("sp", w[:], wv[:, :, nb * 512:(nb + 1) * 512], bw, writes=[bw])
            if v in (2, 5):
                p, bp = prow.next()
                for k in range(8):
                    B.op("pe", lambda e, k=k, p=p, w=w: e.matmul(p[:], lhsT=scb[:, k, :], rhs=w[:, k, :], start=(k == 0), stop=(k == 7)),
                         reads=[bscb, bw], writes=[bp], inc=(k == 7))
                gi = 0 if v == 2 else 1
                B.op("dve", lambda e, p=p, gi=gi, half=half: e.tensor_tensor(
                    out=self.gate_bc[:, gi, half * 512:(half + 1) * 512], in0=p[:], in1=bg[:, gi * 1024 + half * 512: gi * 1024 + (half + 1) * 512],
                    op=ALU.add), reads=[bp, bsm], writes=[self.bgate])
            else:
                for cc in range(4):
                    col = xslot[v] * 8 + half * 4 + cc
                    for k in range(8):
                        B.op("pe", lambda e, k=k, w=w, cc=cc, col=col: e.matmul(pcol[:, col:col + 1], lhsT=w[:, k, cc * 128:(cc + 1) * 128],
                                                                                 rhs=sc[:, k:k + 1], start=(k == 0), stop=(k == 7)),
                             reads=[bw, bsc], writes=[bpcol], inc=(k == 7))
                    if v in (0, 1):
                        col2 = 32 + v * 8 + half * 4 + cc
                        for k in range(8):
                            B.op("pe", lambda e, k=k, w=w, cc=cc, col2=col2: e.matmul(pcol[:, col2:col2 + 1], lhsT=w[:, k, cc * 128:(cc + 1) * 128],
                                                                                       rhs=sc[:, 8 + k:9 + k], start=(k == 0), stop=(k == 7)),
                                 reads=[bw, bsc], writes=[bpcol], inc=(k == 7))
        mc = B.sb(st, "p0_mc", [128, 6, 8], F32)
        bmc = Buf("p0_mc")
        for i, v in enumerate((0, 1, 3, 4)):
            B.op("dve", lambda e, i=i, v=v: e.tensor_tensor(out=mc[:, i, :], in0=pcol[:, i * 8:(i + 1) * 8], in1=bcol[:, v * 8:(v + 1) * 8], op=ALU.add),
                 reads=[bpcol, bsm], writes=[bmc])
        for i, v in enumerate((0, 1)):
            B.op("dve", lambda e, i=i, v=v: e.tensor_tensor(out=mc[:, 4 + i, :], in0=pcol[:, 32 + i * 8:32 + (i + 1) * 8], in1=bcol[:, v * 8:(v + 1) * 8],
                                                            op=ALU.add), reads=[bpcol, bsm], writes=[bmc])
        md = self.modc
        for dst, (sci, shi, nw) in {0: (1, 0, 0), 2: (5, 4, 0), 4: (3, 2, 1)}.items():
            B.op("dve", lambda e, dst=dst, sci=sci, nw=nw: e.scalar_tensor_tensor(out=md[:, dst, :], in0=mc[:, sci, :], scalar=1.0, in1=n12[:, nw * 8:(nw + 1) * 8],
                                                                                  op0=ALU.add, op1=ALU.mult), reads=[bmc, bsm], writes=[self.bmod])
            B.op("dve", lambda e, dst=dst, shi=shi: e.tensor_copy(out=md[:, dst + 1, :], in_=mc[:, shi, :]), reads=[bmc], writes=[self.bmod])
        B.barrier()
        st.close()

    def load_w_bf16(self, stack, name, ap, kchunks, ncols, nsplit=4):
        B = self.B
        t = B.sb(stack, name, [128, kchunks, ncols], BF16)
        b = Buf(name)
        v = ap.rearrange("(k p) n -> p k n", p=128)
        step = (ncols + nsplit - 1) // nsplit
        for i in range(0, ncols, step):
            j = min(ncols, i + step)
            B.dma("pool", t[:, :, i:j], v[:, :, i:j], b, writes=[b])
        return t, b

    def norm_transpose(self, xt, bxts, ns, xn, bxn, sq, bsq, junk, bjunk, ptr_ring, hxT, bhx, col0, ai, npart=128):
        B = self.B
        for s in range(ns):
            B.op("act", lambda e, s=s: e.activation(out=junk[0:npart, :], in_=xt[0:npart, s, :], func=AF.Square, accum_out=sq[0:npart, s:s + 1]),
                 reads=[bxts[s]], writes=[bjunk, bsq])
        B.op("dve", lambda e: e.tensor_scalar(out=sq[0:npart, 8:8 + ns], in0=sq[0:npart, 0:ns], scalar1=float(D * EPS), scalar2=None, op0=ALU.add),
             reads=[bsq], writes=[bsq])
        B.op("pool", lambda e: e.tensor_tensor(out=sq[0:npart, 16:16 + ns], in0=sq[0:npart, 8:8 + ns], in1=self.nhalf[0:npart, 0:ns], op=ALU.pow),
             reads=[bsq, self.cb], writes=[bsq])
        for s in range(ns):
            B.op("dve", lambda e, s=s: e.tensor_scalar(out=xn[0:npart, s, :], in0=xt[0:npart, s, :], scalar1=sq[0:npart, 16 + s:17 + s], scalar2=32.0,
                                                       op0=ALU.mult, op1=ALU.mult), reads=[bxts[s], bsq], writes=[bxn])
        for k in range(KD):
            p, bp = ptr_ring.next()
            pb = p[:].bitcast(BF16)
            for s in range(ns):
                B.op("pe", lambda e, s=s, k=k, pb=pb: e.transpose(pb[:, s * npart:(s + 1) * npart], xn[0:npart, s, k * 128:(k + 1) * 128],
                                                                  self.ident_b[0:npart, 0:npart]),
                     reads=[bxn, self.cb], writes=[bp], inc=(s == ns - 1))
            B.op("act", lambda e, k=k, pb=pb: e.activation(out=hxT[:, k, col0:col0 + ns * npart], in_=pb[:, 0:ns * npart], func=AF.Identity,
                                                           scale=self.modc[:, ai, k:k + 1], bias=self.modc[:, ai + 1, k:k + 1]),
                 reads=[bp, self.bmod], writes=[bhx])

    def phaseA(self):
        B, nc, inp = self.B, self.nc, self.inp
        st = ExitStack()
        wqkv, bwqkv = self.load_w_bf16(st, "a_wqkv", inp["w_qkv"], 8, 3072, 6)
        wml, bwml = self.load_w_bf16(st, "a_wml", inp["w_ml"], 8, 2048, 4)
        wgt, bwgt = self.load_w_bf16(st, "a_wgt", inp["w_gate"], 8, 48, 1)
        cw = B.sb(st, "a_cw", [128, 24, 3], F32)
        gp = B.sb(st, "a_gp", [128, 48], F32)
        bsm = Buf("a_small")
        B.dma("sp", cw[:], inp["gdn_cw"][:, :, :], bsm, writes=[bsm])
        B.dma("sp", gp[:], inp["gate_p"][:, :], bsm, writes=[bsm])
        B.op("act", lambda e: e.activation(out=gp[:, 16:32], in_=gp[:, 16:32], func=AF.Exp), reads=[bsm], writes=[bsm])
        B.op("dve", lambda e: e.tensor_scalar(out=gp[:, 16:32], in0=gp[:, 16:32], scalar1=-1.0, scalar2=None, op0=ALU.mult), reads=[bsm], writes=[bsm])
        xt = B.sb(st, "a_x", [128, 4, 1024], F32)
        bxts = [Buf("a_x%d" % i) for i in range(4)]
        xh = B.sb(st, "a_xh", [2, 1, 1024], F32); bxh = Buf("a_xh")
        xn = B.sb(st, "a_xn", [128, 4, 1024], BF16); bxn = Buf("a_xn")
        xnh = B.sb(st, "a_xnh", [2, 1, 1024], BF16); bxnh = Buf("a_xnh")
        sq = B.sb(st, "a_sq", [128, 24], F32); bsq = Buf("a_sq")
        sqh = B.sb(st, "a_sqh", [128, 24], F32); bsqh = Buf("a_sqh")
        junk = B.sb(st, "a_junk", [128, 1024], BF16); bjunk = Buf("a_junk")
        hxT = B.sb(st, "a_hxT", [128, 8, 514], BF16); bhx = Buf("a_hxT")
        hxh = B.sb(st, "a_hxh", [128, 8, 2], BF16); bhxh = Buf("a_hxh")
        ptr = Ring(B, st, "a_ptr", 2, [128, 512], F32, psum=True)
        pz = Ring(B, st, "a_pz", 2, [128, 512], F32, psum=True)
        pmisc = B.ps(st, "a_pmisc", [128, 512], F32)
        pzh = pmisc[:, 0:64]; bpzh = Buf("a_pzh")
        pn = Ring(B, st, "a_pn", 2, [128, 512], F32, psum=True)
        zb = Ring(B, st, "a_zb", 2, [128, 514], F32)
        y1 = Ring(B, st, "a_y1", 2, [128, 512], F32)
        y2 = Ring(B, st, "a_y2", 2, [128, 512], F32)
        sbr = Ring(B, st, "a_sb", 2, [128, 512], F32)
        sqb = Ring(B, st, "a_sqb", 2, [128, 512], BF16)
        tt = Ring(B, st, "a_t", 2, [128, 512], F32)
        rn = Ring(B, st, "a_rn", 2, [128, 512], F32)
        kst = B.sb(st, "a_kst", [128, 4, 8, 128], BF16); bkst = Buf("a_kst")
        qst = B.sb(st, "a_qst", [128, 4, 8, 128], BF16); bqst = Buf("a_qst")
        vT = B.sb(st, "a_vT", [128, 8, 512], BF16); bvT = Buf("a_vT")
        vst = Ring(B, st, "a_vst", 1, [128, 4, 1024], BF16)
        mqst = B.sb(st, "a_mqst", [128, 4, 4, 128], BF16); bmqst = Buf("a_mqst")
        mkst = B.sb(st, "a_mkst", [128, 4, 4, 128], BF16); bmkst = Buf("a_mkst")
        graw = B.sb(st, "a_graw", [128, 4, 48], F32); bgraw = Buf("a_graw")
        gwk = B.sb(st, "a_gwk", [128, 4, 48], F32); bgwk = Buf("a_gwk")
        gsb = Ring(B, st, "a_gsb", 2, [128, 4, 48], F32)
        pg = pmisc[:, 64:256].rearrange("p (s g) -> p s g", g=48); bpg = Buf("a_pg")
        dkr = float(128 ** -0.5)

        tiles = [(inp["ctx"], 0, 2, 0, False, False)]
        for i in range(16):
            tiles.append((inp["x"], i * 512, 4, 2 + 4 * i, i > 0, i < 15))
        if self.debug.get("a_tiles"):
            tiles = tiles[: self.debug["a_tiles"]]
        for (src, t0, ns, c0, hl, hr) in tiles:
            n = ns * 128
            ai = 2 if src is inp["ctx"] else 0
            for s in range(ns):
                B.dma("sp", xt[:, s, :], src[t0 + s * 128:t0 + (s + 1) * 128, :], bxts[s], writes=[bxts[s]])
            tl = t0 - 1 if hl else t0
            tr = t0 + n if hr else t0
            B.dma("sp", xh[0:1, 0, :], src[tl:tl + 1, :], bxh, writes=[bxh])
            B.dma("sp", xh[1:2, 0, :], src[tr:tr + 1, :], bxh, writes=[bxh])
            self.norm_transpose(xt, bxts, ns, xn, bxn, sq, bsq, junk, bjunk, ptr, hxT, bhx, 1, ai)
            self.norm_transpose(xh, [bxh], 1, xnh, bxnh, sqh, bsqh, junk, bjunk, ptr, hxh, bhxh, 0, ai, npart=2)
            for j in range(24):
                p, bp = pz.next()
                for k in range(8):
                    B.op("pe", lambda e, k=k, p=p, j=j: e.matmul(p[:, 0:n], lhsT=wqkv[:, k, j * 128:(j + 1) * 128], rhs=hxT[:, k, 1:1 + n],
                                                                  start=(k == 0), stop=(k == 7)), reads=[bwqkv, bhx], writes=[bp], inc=(k == 7))
                for k in range(8):
                    B.op("pe", lambda e, k=k, j=j: e.matmul(pzh[:, 2 * j:2 * j + 2], lhsT=wqkv[:, k, j * 128:(j + 1) * 128], rhs=hxh[:, k, :],
                                                             start=(k == 0), stop=(k == 7)), reads=[bwqkv, bhxh], writes=[bpzh], inc=(k == 7))
                z, bz = zb.next()
                B.op("act", lambda e, z=z, p=p: e.activation(out=z[:, 1:1 + n], in_=p[:, 0:n], func=AF.Identity), reads=[bp], writes=[bz])
                B.op("act", lambda e, z=z, j=j: e.activation(out=z[:, 0:1], in_=pzh[:, 2 * j:2 * j + 1], func=AF.Identity), reads=[bpzh], writes=[bz])
                B.op("act", lambda e, z=z, j=j: e.activation(out=z[:, n + 1:n + 2], in_=pzh[:, 2 * j + 1:2 * j + 2], func=AF.Identity), reads=[bpzh], writes=[bz])
                if not hl:
                    B.op("pool", lambda e, z=z: e.memset(z[:, 0:1], 0.0), writes=[bz])
                if not hr:
                    B.op("pool", lambda e, z=z: e.memset(z[:, n + 1:n + 2], 0.0), writes=[bz])
                a1, ba1 = y1.next()
                a2, ba2 = y2.next()
                B.op("dve", lambda e, z=z, a1=a1, j=j: e.tensor_scalar(out=a1[:, 0:n], in0=z[:, 1:1 + n], scalar1=cw[:, j, 1:2], scalar2=None, op0=ALU.mult),
                     reads=[bz, bsm], writes=[ba1])
                B.op("dve", lambda e, z=z, a1=a1, a2=a2, j=j: e.scalar_tensor_tensor(out=a2[:, 0:n], in0=z[:, 0:n], scalar=cw[:, j, 0:1], in1=a1[:, 0:n],
                                                                                    op0=ALU.mult, op1=ALU.add), reads=[bz, bsm, ba1], writes=[ba2])
                B.op("dve", lambda e, z=z, a1=a1, a2=a2, j=j: e.scalar_tensor_tensor(out=a1[:, 0:n], in0=z[:, 2:2 + n], scalar=cw[:, j, 2:3], in1=a2[:, 0:n],
                                                                                    op0=ALU.mult, op1=ALU.add), reads=[bz, bsm, ba2], writes=[ba1])
                if j < 16:
                    s_, bs_ = sbr.next()
                    B.op("act", lambda e, s_=s_, a1=a1: e.activation(out=s_[:, 0:n], in_=a1[:, 0:n], func=AF.Silu), reads=[ba1], writes=[bs_])
                    q2, bq2 = sqb.next()
                    B.op("pool", lambda e, s_=s_, q2=q2: e.tensor_tensor(out=q2[:, 0:n], in0=s_[:, 0:n], in1=s_[:, 0:n], op=ALU.mult), reads=[bs_], writes=[bq2])
                    pp, bpp = pn.next()
                    B.op("pe", lambda e, pp=pp, q2=q2: e.matmul(pp[:, 0:n], lhsT=self.ones_b[:], rhs=q2[:, 0:n], start=True, stop=True),
                         reads=[bq2, self.cb], writes=[bpp])
                    t_, bt_ = tt.next()
                    B.op("act", lambda e, t_=t_, pp=pp: e.activation(out=t_[:, 0:n], in_=pp[:, 0:n], func=AF.Identity, bias=float(EPS)), reads=[bpp], writes=[bt_])
                    r_, br_ = rn.next()
                    B.op("pool", lambda e, t_=t_, r_=r_: e.tensor_tensor(out=r_[:, 0:n], in0=t_[:, 0:n], in1=self.nhalf[:, 0:n], op=ALU.pow),
                         reads=[bt_, self.cb], writes=[br_])
                    if j < 8:
                        B.op("dve", lambda e, s_=s_, r_=r_, j=j: e.scalar_tensor_tensor(
                            out=qst[:, 0:ns, j, :], in0=s_[:, 0:n].rearrange("p (s t) -> p s t", t=128), scalar=dkr,
                            in1=r_[:, 0:n].rearrange("p (s t) -> p s t", t=128), op0=ALU.mult, op1=ALU.mult), reads=[bs_, br_], writes=[bqst])
                    else:
                        B.op("dve", lambda e, s_=s_, r_=r_, j=j: e.tensor_tensor(
                            out=kst[:, 0:ns, j - 8, :], in0=s_[:, 0:n].rearrange("p (s t) -> p s t", t=128),
                            in1=r_[:, 0:n].rearrange("p (s t) -> p s t", t=128), op=ALU.mult), reads=[bs_, br_], writes=[bkst])
                else:
                    B.op("act", lambda e, a1=a1, j=j: e.activation(out=vT[:, j - 16, 0:n], in_=a1[:, 0:n], func=AF.Silu), reads=[ba1], writes=[bvT])
            self.v_transposes(vT, bvT, ns, vst, pn, self.VG, c0)
            B.dma("sp", self.KT[c0:c0 + ns].rearrange("c d h t -> d c h t"), kst[:, 0:ns], bkst, reads=[bkst])
            B.dma("sp", self.QT[c0:c0 + ns].rearrange("c d h t -> d c h t"), qst[:, 0:ns], bqst, reads=[bqst])
            for j in range(16):
                p, bp = pz.next()
                for k in range(8):
                    B.op("pe", lambda e, k=k, p=p, j=j: e.matmul(p[:, 0:n], lhsT=wml[:, k, j * 128:(j + 1) * 128], rhs=hxT[:, k, 1:1 + n],
                                                                  start=(k == 0), stop=(k == 7)), reads=[bwml, bhx], writes=[bp], inc=(k == 7))
                if j < 4:
                    B.op("act", lambda e, p=p, j=j: e.activation(out=mqst[:, 0:ns, j, :], in_=p[:, 0:n].rearrange("p (s t) -> p s t", t=128),
                                                                  func=AF.Identity, scale=dkr), reads=[bp], writes=[bmqst])
                elif j < 8:
                    B.op("act", lambda e, p=p, j=j: e.activation(out=mkst[:, 0:ns, j - 4, :], in_=p[:, 0:n].rearrange("p (s t) -> p s t", t=128),
                                                                  func=AF.Identity), reads=[bp], writes=[bmkst])
                else:
                    B.op("act", lambda e, p=p, j=j: e.activation(out=vT[:, j - 8, 0:n], in_=p[:, 0:n], func=AF.Identity), reads=[bp], writes=[bvT])
            self.v_transposes(vT, bvT, ns, vst, pn, self.MV, c0)
            B.dma("sp", self.MQT[c0:c0 + ns].rearrange("c d h t -> d c h t"), mqst[:, 0:ns], bmqst, reads=[bmqst])
            B.dma("sp", self.MKT[c0:c0 + ns].rearrange("c d h t -> d c h t"), mkst[:, 0:ns], bmkst, reads=[bmkst])
            for s in range(ns):
                for k in range(8):
                    B.op("pe", lambda e, k=k, s=s: e.matmul(pg[:, s, :], lhsT=hxT[:, k, 1 + s * 128:1 + (s + 1) * 128], rhs=wgt[:, k, :],
                                                             start=(k == 0), stop=(k == 7)), reads=[bhx, bwgt], writes=[bpg], inc=(k == 7))
            g, bg_ = gsb.next()
            self.gate_math(pg, bpg, graw, bgraw, gwk, bgwk, g, bg_, gp, bsm, ns)
            B.dma("sp", self.GT[c0:c0 + ns].rearrange("c t g -> t c g"), g[:, 0:ns, :], bg_, reads=[bg_])
        B.barrier()
        st.close()

    def v_transposes(self, vT, bvT, ns, vst, pn, dst, c0):
        B = self.B
        v, bv = vst.next()
        for s in range(ns):
            pp, bpp = pn.next()
            ppb = pp[:].bitcast(BF16)
            for h in range(8):
                B.op("pe", lambda e, s=s, h=h, ppb=ppb: e.transpose(ppb[:, h * 128:(h + 1) * 128], vT[:, h, s * 128:(s + 1) * 128], self.ident_b[:]),
                     reads=[bvT, self.cb], writes=[bpp], inc=(h == 7))
            B.op("act", lambda e, s=s, ppb=ppb, v=v: e.activation(out=v[:, s, :], in_=ppb[:, 0:1024], func=AF.Identity), reads=[bpp], writes=[bv])
        B.dma("sp", dst[c0:c0 + ns].rearrange("c t e -> t c e"), v[:, 0:ns, :], bv, reads=[bv])

    def gate_math(self, pg, bpg, graw, bgraw, wk, bwk, g, bg_, gp, bgp, ns):
        B = self.B
        S = slice(0, ns)

        def bc(lo, hi):
            return gp[:, lo:hi].unsqueeze(1).to_broadcast([128, ns, hi - lo])

        B.op("act", lambda e: e.activation(out=graw[:, S, :], in_=pg[:, S, :], func=AF.Identity), reads=[bpg], writes=[bgraw])
        B.op("dve", lambda e: e.tensor_tensor(out=wk[:, S, 0:16], in0=graw[:, S, 0:16], in1=bc(0, 16), op=ALU.add), reads=[bgraw, bgp], writes=[bwk])
        B.op("act", lambda e: e.activation(out=wk[:, S, 0:16], in_=wk[:, S, 0:16], func=AF.Exp), reads=[bwk], writes=[bwk])
        B.op("act", lambda e: e.activation(out=wk[:, S, 0:16], in_=wk[:, S, 0:16], func=AF.Ln, bias=1.0), reads=[bwk], writes=[bwk])
        B.op("dve", lambda e: e.tensor_tensor(out=g[:, S, 0:16], in0=wk[:, S, 0:16], in1=bc(16, 32), op=ALU.mult), reads=[bwk, bgp], writes=[bg_])
        B.op("act", lambda e: e.activation(out=wk[:, S, 16:32], in_=graw[:, S, 16:32], func=AF.Exp, scale=-1.0), reads=[bgraw], writes=[bwk])
        B.op("dve", lambda e: e.tensor_scalar(out=wk[:, S, 16:32], in0=wk[:, S, 16:32], scalar1=1.0, scalar2=None, op0=ALU.add), reads=[bwk], writes=[bwk])
        B.op("dve", lambda e: e.reciprocal(out=g[:, S, 16:32], in_=wk[:, S, 16:32]), reads=[bwk], writes=[bg_])
        B.op("dve", lambda e: e.tensor_tensor(out=wk[:, S, 32:48], in0=graw[:, S, 32:48], in1=bc(32, 48), op=ALU.add), reads=[bgraw, bgp], writes=[bwk])
        B.op("act", lambda e: e.activation(out=wk[:, S, 32:48], in_=wk[:, S, 32:48], func=AF.Exp, scale=float(2.0 / 15.0)), reads=[bwk], writes=[bwk])
        B.op("dve", lambda e: e.tensor_scalar(out=wk[:, S, 32:48], in0=wk[:, S, 32:48], scalar1=1.0, scalar2=None, op0=ALU.add), reads=[bwk], writes=[bwk])
        B.op("dve", lambda e: e.reciprocal(out=wk[:, S, 32:48], in_=wk[:, S, 32:48]), reads=[bwk], writes=[bwk])
        B.op("dve", lambda e: e.tensor_scalar(out=g[:, S, 32:48], in0=wk[:, S, 32:48], scalar1=-30.0, scalar2=15.0, op0=ALU.mult, op1=ALU.add),
             reads=[bwk], writes=[bg_])
        B.op("act", lambda e: e.activation(out=wk[:, S, 40:48], in_=g[:, S, 40:48], func=AF.Exp, scale=-1.0), reads=[bg_], writes=[bwk])
        B.op("act", lambda e: e.activation(out=wk[:, S, 40:48], in_=wk[:, S, 40:48], func=AF.Ln, bias=1.0), reads=[bwk], writes=[bwk])
        B.op("dve", lambda e: e.tensor_scalar(out=g[:, S, 40:48], in0=wk[:, S, 40:48], scalar1=-1.0, scalar2=None, op0=ALU.mult), reads=[bwk], writes=[bg_])

    def phaseB(self):
        B, nc, inp = self.B, self.nc, self.inp
        st = ExitStack()
        dbg = self.debug
        LE, bLE = self.mask(st, "b_LE", (0, -1, 1, ALU.is_ge))
        LT, bLT = self.mask(st, "b_LT", (-1, -1, 1, ALU.is_ge))
        GE, bGE = self.mask(st, "b_GE", (0, 1, -1, ALU.is_ge))
        GT_, bGT = self.mask(st, "b_GT", (-1, 1, -1, ALU.is_ge))
        MBf, bMBf = self.mask(st, "b_MBf", (0, 1, -1, ALU.is_ge), val=0.0, fill=NEG)
        MBb, bMBb = self.mask(st, "b_MBb", (0, -1, 1, ALU.is_ge), val=0.0, fill=NEG)
        SELf, bSELf = self.mask(st, "b_SELf", (-127, 1, 0, ALU.is_equal))
        SELb, bSELb = self.mask(st, "b_SELb", (0, 1, 0, ALU.is_equal))
        smask = B.sb(st, "b_smask", [128, 14, 128], BF16)
        bsm = Buf("b_smask")
        B.dma("pool", smask[:], inp["smask"][:, :, :], bsm, writes=[bsm])
        cbufs = [bLE, bLT, bGE, bGT, bMBf, bMBb, bSELf, bSELb, bsm, self.cb]
        dirc = [dict(U=LE, S=GT_, incl=LE, strict=LT, MB=MBf, SEL=SELf),
                dict(U=GE, S=LT, incl=GE, strict=GT_, MB=MBb, SEL=SELb)]
        S = [B.sb(st, "b_S%d" % d, [128, 8, 128], F32) for d in range(2)]
        bS = [[Buf("b_S%d_%d" % (d, g)) for g in range(2)] for d in range(2)]
        Sb = [[Ring(B, st, "b_Sb%d_%d_" % (d, g), 2, [128, 4, 128], BF16) for g in range(2)] for d in range(2)]
        Sb_cur = [[None, None], [None, None]]
        C = [B.sb(st, "b_C%d" % d, [128, 4, 256], F32) for d in range(2)]
        bC = [Buf("b_C%d" % d) for d in range(2)]
        Cb = [Ring(B, st, "b_Cb%d_" % d, 2, [128, 4, 256], BF16) for d in range(2)]
        Cb_cur = [None, None]
        nst = [Ring(B, st, "b_n%d_" % d, 2, [128, 8], F32) for d in range(2)]
        nbf = [Ring(B, st, "b_nb%d_" % d, 2, [128, 4], BF16) for d in range(2)]
        n_cur = [None, None]
        nb_cur = [None, None]
        mst = [Ring(B, st, "b_m%d_" % d, 2, [128, 4], F32) for d in range(2)]
        m_cur = [None, None]
        for d in range(2):
            B.op("pool", lambda e, d=d: e.memset(S[d][:], 0.0), writes=bS[d])
            B.op("pool", lambda e, d=d: e.memset(C[d][:], 0.0), writes=[bC[d]])
            for g in range(2):
                t, b = Sb[d][g].next()
                B.op("pool", lambda e, t=t: e.memset(t[:], 0.0), writes=[b])
                Sb_cur[d][g] = (t, b)
            t, b = Cb[d].next()
            B.op("pool", lambda e, t=t: e.memset(t[:], 0.0), writes=[b])
            Cb_cur[d] = (t, b)
            t, b = nst[d].next()
            B.op("pool", lambda e, t=t: e.memset(t[:], 0.0), writes=[b])
            n_cur[d] = (t, b)
            t, b = nbf[d].next()
            B.op("pool", lambda e, t=t: e.memset(t[:], 0.0), writes=[b])
            nb_cur[d] = (t, b)
            t, b = mst[d].next()
            B.op("pool", lambda e, t=t: e.memset(t[:], 0.0), writes=[b])
            m_cur[d] = (t, b)
        def dring(name, shape, dt):
            return [Ring(B, st, "b_%s%d_" % (name, d), 2, shape, dt) for d in range(2)]
        rKT = dring("KT", [128, 8, 128], BF16)
        rQT = dring("QT", [128, 8, 128], BF16)
        rVG = dring("VG", [128, 1024], BF16)
        rGT = dring("GT", [128, 48], F32)
        rMQ = dring("MQ", [128, 4, 128], BF16)
        rMK = dring("MK", [128, 4, 128], BF16)
        rMV = dring("MV", [128, 1024], BF16)
        rgs = dring("gs", [128, 64], F32)
        psr = Ring(B, st, "b_ps", 8, [128, 512], F32, psum=True)
        NG, NM = 3, 2
        gslots = []
        for i in range(NG):
            sl = {}
            for nm, shp, dt in (("A", [128, 4, 128], F32), ("Bt", [128, 4, 128], F32), ("Ct", [128, 4, 128], F32),
                                ("attnT", [128, 4, 128], BF16), ("Qm", [128, 4, 128], BF16), ("Qp", [128, 4, 128], BF16),
                                ("Kg", [128, 4, 128], BF16), ("kt", [128, 4, 128], BF16), ("G0", [128, 4, 128], BF16),
                                ("G1", [128, 4, 128], BF16), ("H0", [128, 4, 128], BF16), ("H1", [128, 4, 128], BF16),
                                ("IYT", [128, 4, 128], BF16), ("negW", [128, 4, 128], BF16), ("vnew", [128, 4, 128], BF16),
                                ("St", [128, 4, 128], F32)):
                sl[nm] = (B.sb(st, "b_g%d_%s" % (i, nm), shp, dt), Buf("b_g%d_%s" % (i, nm)))
            gslots.append(sl)
        mslots = []
        for i in range(NM):
            sl = {}
            for nm, shp, dt in (("X", [128, 4, 128], F32), ("Y", [128, 4, 128], F32), ("Pm", [128, 4, 128], BF16),
                                ("PT", [128, 4, 128], BF16), ("Kw", [128, 4, 128], BF16), ("sm", [128, 64], F32), ("Ct", [128, 4, 256], F32)):
                sl[nm] = (B.sb(st, "b_m%d_%s" % (i, nm), shp, dt), Buf("b_m%d_%s" % (i, nm)))
            mslots.append(sl)
        ring_o1 = Ring(B, st, "b_o1_", 2, [128, 4, 128], F32)
        ring_o = Ring(B, st, "b_o_", 2, [128, 4, 128], F32)
        ring_num = Ring(B, st, "b_num_", 2, [128, 4, 256], F32)
        ring_h = Ring(B, st, "b_h_", 2, [128, 4, 256], F32)

        def bc3(ap2, n):
            return ap2.unsqueeze(2).to_broadcast([128, 4, n])

        def bcm(ap2, n=4):
            return ap2.unsqueeze(1).to_broadcast([128, n, 128])

        nsteps = dbg.get("b_steps", NCH)
        order = [list(range(NCH)), [1, 0] + list(range(NCH - 1, 1, -1))]
        if dbg.get("b_order"):
            order = dbg["b_order"]
            nsteps = len(order[0])
        out_lo, out_hi = 2, 2 + 33

        data = {}

        def load_step(step, d):
            c = order[d][step]
            tk, bk = rKT[d].next(); tq, bq = rQT[d].next(); tv, bv = rVG[d].next(); tg, bg = rGT[d].next()
            tmq, bmq = rMQ[d].next(); tmk, bmk = rMK[d].next(); tmv, bmv = rMV[d].next()
            B.dma("sp", tg[:], self.GT[c], bg, writes=[bg])
            B.dma("sp", tk[:], self.KT[c], bk, writes=[bk])
            B.dma("sp", tq[:], self.QT[c], bq, writes=[bq])
            B.dma("sp", tv[:], self.VG[c], bv, writes=[bv])
            B.dma("sp", tmq[:], self.MQT[c], bmq, writes=[bmq])
            B.dma("sp", tmk[:], self.MKT[c], bmk, writes=[bmk])
            B.dma("sp", tmv[:], self.MV[c], bmv, writes=[bmv])
            data[(step, d)] = dict(c=c, KT=(tk, bk), QT=(tq, bq), VG=(tv, bv), GT=(tg, bg), MQ=(tmq, bmq), MK=(tmk, bmk), MV=(tmv, bmv))

        def shared_pre(step, d):
            dd = data[(step, d)]
            tg, bg = dd["GT"]
            gs, bgs = rgs[d].next()
            dc = dirc[d]
            p, bp = psr.next()
            B.op("pe", lambda e: e.matmul(p[:, 0:8], lhsT=dc["U"][:], rhs=tg[:, d * 8:(d + 1) * 8], start=True, stop=True), reads=[bg] + cbufs, writes=[bp], inc=False)
            B.op("pe", lambda e: e.matmul(p[:, 8:12], lhsT=dc["U"][:], rhs=tg[:, 40 + d * 4:44 + d * 4], start=True, stop=True), reads=[bg] + cbufs, writes=[bp], inc=False)
            B.op("pe", lambda e: e.matmul(p[:, 12:20], lhsT=self.ones_f[:], rhs=tg[:, d * 8:(d + 1) * 8], start=True, stop=True), reads=[bg] + cbufs, writes=[bp], inc=False)
            B.op("pe", lambda e: e.matmul(p[:, 20:24], lhsT=self.ones_f[:], rhs=tg[:, 40 + d * 4:44 + d * 4], start=True, stop=True), reads=[bg] + cbufs, writes=[bp])
            B.op("act", lambda e: e.activation(out=gs[:, 0:24], in_=p[:, 0:24], func=AF.Identity), reads=[bp], writes=[bgs])
            B.op("act", lambda e: e.activation(out=gs[:, 24:32], in_=gs[:, 0:8], func=AF.Exp), reads=[bgs], writes=[bgs])
            B.op("dve", lambda e: e.tensor_tensor(out=gs[:, 32:40], in0=gs[:, 12:20], in1=gs[:, 0:8], op=ALU.subtract), reads=[bgs], writes=[bgs])
            B.op("act", lambda e: e.activation(out=gs[:, 32:40], in_=gs[:, 32:40], func=AF.Exp), reads=[bgs], writes=[bgs])
            B.op("act", lambda e: e.activation(out=gs[:, 40:48], in_=gs[:, 12:20], func=AF.Exp), reads=[bgs], writes=[bgs])
            B.op("dve", lambda e: e.tensor_tensor(out=gs[:, 48:52], in0=tg[:, 32 + d * 4:36 + d * 4], in1=gs[:, 8:12], op=ALU.subtract), reads=[bgs, bg], writes=[bgs])
            dd["gs"] = (gs, bgs)

        def gdn_group(step, d, hg, sl):
            dd = data[(step, d)]
            dc = dirc[d]
            c = dd["c"]
            need_o = out_lo <= c < out_hi
            tk, bk = dd["KT"]; tq, bq = dd["QT"]; tv, bv = dd["VG"]; tg, bg = dd["GT"]; gs, bgs = dd["gs"]
            h0 = hg * 4
            A, bA = sl["A"]; Bt, bBt = sl["Bt"]; Ct, bCt = sl["Ct"]
            attnT, battn = sl["attnT"]; Qm, bQm = sl["Qm"]; Qp, bQp = sl["Qp"]; Kg, bKg = sl["Kg"]; kt, bkt = sl["kt"]
            IYT, bIYT = sl["IYT"]; negW, bnegW = sl["negW"]; vnew, bvnew = sl["vnew"]; St, bSt = sl["St"]
            gcol = tg[:, d * 8 + h0:d * 8 + h0 + 4]
            bcol = tg[:, 16 + d * 8 + h0:16 + d * 8 + h0 + 4]
            eg = gs[:, 24 + h0:24 + h0 + 4]
            ekt = gs[:, 32 + h0:32 + h0 + 4]
            gte = gs[:, 40 + h0:40 + h0 + 4]
            B.op("pool", lambda e: e.tensor_tensor(out=A[:], in0=bcm(dc["U"][:]), in1=bc3(gcol, 128), op=ALU.mult), reads=[bg] + cbufs, writes=[bA])
            pD, bpD = psr.next()
            for u in range(4):
                B.op("pe", lambda e, u=u: e.matmul(pD[:, u * 128:(u + 1) * 128], lhsT=dc["S"][:], rhs=A[:, u, :], start=True, stop=True),
                     reads=[bA] + cbufs, writes=[bpD], inc=(u == 3))
            B.op("act", lambda e: e.activation(out=Bt[:].rearrange("p u l -> p (u l)"), in_=pD[:, :], func=AF.Exp), reads=[bpD], writes=[bBt])
            B.op("pool", lambda e: e.tensor_tensor(out=A[:], in0=Bt[:], in1=bcm(dc["incl"][:]), op=ALU.mult), reads=[bBt] + cbufs, writes=[bA])
            B.op("pool", lambda e: e.tensor_tensor(out=Ct[:], in0=Bt[:], in1=bcm(dc["strict"][:]), op=ALU.mult), reads=[bBt] + cbufs, writes=[bCt])
            B.op("pool", lambda e: e.tensor_tensor(out=Ct[:], in0=Ct[:], in1=bc3(bcol, 128), op=ALU.mult), reads=[bCt, bg], writes=[bCt])
            pKK, bpKK = psr.next()
            pQK, bpQK = psr.next()
            pKt, bpKt = psr.next()
            pKtb = pKt[:].bitcast(BF16)
            for u in range(4):
                B.op("pe", lambda e, u=u: e.matmul(pKK[:, u * 128:(u + 1) * 128], lhsT=tk[:, h0 + u, :], rhs=tk[:, h0 + u, :], start=True, stop=True),
                     reads=[bk], writes=[bpKK], inc=(u == 3))
            for u in range(4):
                B.op("pe", lambda e, u=u: e.matmul(pQK[:, u * 128:(u + 1) * 128], lhsT=tk[:, h0 + u, :], rhs=tq[:, h0 + u, :], start=True, stop=True),
                     reads=[bk, bq], writes=[bpQK], inc=(u == 3))
            for u in range(4):
                B.op("pe", lambda e, u=u: e.transpose(pKtb[:, u * 128:(u + 1) * 128], tk[:, h0 + u, :], self.ident_b[:]),
                     reads=[bk] + cbufs, writes=[bpKt], inc=(u == 3))
            B.op("dve", lambda e: e.tensor_tensor(out=attnT[:], in0=pQK[:, :].rearrange("p (u l) -> p u l", u=4), in1=A[:], op=ALU.mult),
                 reads=[bpQK, bA], writes=[battn])
            B.op("dve", lambda e: e.tensor_tensor(out=Qm[:], in0=pKK[:, :].rearrange("p (u l) -> p u l", u=4), in1=Ct[:], op=ALU.mult),
                 reads=[bpKK, bCt], writes=[bQm])
            B.op("pool", lambda e: e.tensor_tensor(out=Qp[:], in0=Qm[:], in1=bcm(self.ident_b[:]), op=ALU.add), reads=[bQm] + cbufs, writes=[bQp])
            B.op("dve", lambda e: e.tensor_tensor(out=Kg[:], in0=pKtb[:, 0:512].rearrange("p (u l) -> p u l", u=4), in1=bc3(eg, 128), op=ALU.mult),
                 reads=[bpKt, bgs], writes=[bKg])
            B.op("dve", lambda e: e.tensor_tensor(out=kt[:], in0=pKtb[:, 0:512].rearrange("p (u l) -> p u l", u=4), in1=bc3(ekt, 128), op=ALU.mult),
                 reads=[bpKt, bgs], writes=[bkt])
            yield
            Gc = None
            Hc = None
            for lev in range(7):
                sm = smask[:, d * 7 + lev, :]
                pY, bpY = psr.next()
                for u in range(4):
                    rhsH = self.ident_b[:] if Hc is None else Hc[0][:, u, :]
                    B.op("pe", lambda e, u=u, rhsH=rhsH: e.matmul(pY[:, u * 128:(u + 1) * 128], lhsT=Qp[:, u, :], rhs=rhsH, start=True, stop=True),
                         reads=[bQp] + cbufs + ([] if Hc is None else [Hc[1]]), writes=[bpY], inc=(u == 3))
                B.op("dve", lambda e, sm=sm: e.tensor_tensor(out=IYT[:], in0=pY[:, :].rearrange("p (u l) -> p u l", u=4), in1=bcm(sm), op=ALU.mult),
                     reads=[bpY] + cbufs, writes=[bIYT])
                yield
                Gn = sl["G%d" % (lev % 2)]
                Hn = sl["H%d" % (lev % 2)]
                pG, bpG = psr.next()
                for u in range(4):
                    rhsG = self.ident_b[:] if Gc is None else Gc[0][:, u, :]
                    B.op("pe", lambda e, u=u, rhsG=rhsG: e.matmul(pG[:, u * 128:(u + 1) * 128], lhsT=IYT[:, u, :], rhs=rhsG, start=True, stop=True),
                         reads=[bIYT] + cbufs + ([] if Gc is None else [Gc[1]]), writes=[bpG], inc=(u == 3))
                B.op("act", lambda e, Gn=Gn: e.activation(out=Gn[0][:].rearrange("p u l -> p (u l)"), in_=pG[:, :], func=AF.Identity), reads=[bpG], writes=[Gn[1]])
                if lev < 6:
                    pH, bpH = psr.next()
                    for u in range(4):
                        lhsG = self.ident_b[:] if Gc is None else Gc[0][:, u, :]
                        B.op("pe", lambda e, u=u, lhsG=lhsG: e.matmul(pH[:, u * 128:(u + 1) * 128], lhsT=lhsG, rhs=IYT[:, u, :], start=True, stop=True),
                             reads=[bIYT] + cbufs + ([] if Gc is None else [Gc[1]]), writes=[bpH], inc=(u == 3))
                    B.op("act", lambda e, Hn=Hn: e.activation(out=Hn[0][:].rearrange("p u l -> p (u l)"), in_=pH[:, :], func=AF.Identity), reads=[bpH], writes=[Hn[1]])
                    Hc = Hn
                Gc = Gn
                yield
            G, bG = Gc
            pW, bpW = psr.next()
            for u in range(4):
                B.op("pe", lambda e, u=u: e.matmul(pW[:, u * 128:(u + 1) * 128], lhsT=Kg[:, u, :], rhs=G[:, u, :], start=True, stop=True),
                     reads=[bKg, bG], writes=[bpW], inc=(u == 3))
            B.op("act", lambda e: e.activation(out=negW[:].rearrange("p u l -> p (u l)"), in_=pW[:, :], func=AF.Identity, scale=-1.0), reads=[bpW], writes=[bnegW])
            yield
            sbt, bsb = Sb_cur[d][hg]
            pV, bpV = psr.next()
            for u in range(4):
                B.op("pe", lambda e, u=u: e.matmul(pV[:, u * 128:(u + 1) * 128], lhsT=G[:, u, :], rhs=tv[:, (h0 + u) * 128:(h0 + u + 1) * 128], start=True, stop=False),
                     reads=[bG, bv], writes=[bpV], inc=False)
                B.op("pe", lambda e, u=u: e.matmul(pV[:, u * 128:(u + 1) * 128], lhsT=negW[:, u, :], rhs=sbt[:, u, :], start=False, stop=True),
                     reads=[bnegW, bsb], writes=[bpV], inc=(u == 3))
            B.op("dve", lambda e: e.tensor_tensor(out=vnew[:], in0=pV[:, :].rearrange("p (u l) -> p u l", u=4), in1=bc3(bcol, 128), op=ALU.mult),
                 reads=[bpV, bg], writes=[bvnew])
            yield
            if need_o:
                pO1, bpO1 = psr.next()
                for u in range(4):
                    B.op("pe", lambda e, u=u: e.matmul(pO1[:, u * 128:(u + 1) * 128], lhsT=tq[:, h0 + u, :], rhs=sbt[:, u, :], start=True, stop=True),
                         reads=[bq, bsb], writes=[bpO1], inc=(u == 3))
                o1, bo1 = ring_o1.next()
                B.op("dve", lambda e: e.tensor_tensor(out=o1[:], in0=pO1[:, :].rearrange("p (u l) -> p u l", u=4), in1=bc3(eg, 128), op=ALU.mult),
                     reads=[bpO1, bgs], writes=[bo1])
                pO2, bpO2 = psr.next()
                for u in range(4):
                    B.op("pe", lambda e, u=u: e.matmul(pO2[:, u * 128:(u + 1) * 128], lhsT=attnT[:, u, :], rhs=vnew[:, u, :], start=True, stop=True),
                         reads=[battn, bvnew], writes=[bpO2], inc=(u == 3))
                o, bo = ring_o.next()
                B.op("dve", lambda e: e.tensor_tensor(out=o[:], in0=pO2[:, :].rearrange("p (u l) -> p u l", u=4), in1=o1[:], op=ALU.add),
                     reads=[bpO2, bo1], writes=[bo])
                dst = (self.OF if d == 0 else self.OB)[c - 2]
                B.dma("sp", dst[:, hg * 512:(hg + 1) * 512], o[:].rearrange("p u l -> p (u l)"), bo, reads=[bo])
            pS, bpS = psr.next()
            for u in range(4):
                B.op("pe", lambda e, u=u: e.matmul(pS[:, u * 128:(u + 1) * 128], lhsT=kt[:, u, :], rhs=vnew[:, u, :], start=True, stop=True),
                     reads=[bkt, bvnew], writes=[bpS], inc=(u == 3))
            Sg = S[d][:, h0:h0 + 4, :]
            B.op("pool", lambda e: e.tensor_tensor(out=St[:], in0=Sg, in1=bc3(gte, 128), op=ALU.mult), reads=[bS[d][hg], bgs], writes=[bSt])
            B.op("dve", lambda e: e.tensor_tensor(out=Sg, in0=pS[:, :].rearrange("p (u l) -> p u l", u=4), in1=St[:], op=ALU.add),
                 reads=[bpS, bSt], writes=[bS[d][hg]])
            nsb, bnsb = Sb[d][hg].next()
            B.op("act", lambda e: e.activation(out=nsb[:], in_=Sg, func=AF.Identity), reads=[bS[d][hg]], writes=[bnsb])
            Sb_cur[d][hg] = (nsb, bnsb)
            yield

        self._b_env = dict(data=data, dirc=dirc, cbufs=cbufs, psr=psr, order=order, out_lo=out_lo, out_hi=out_hi, bc3=bc3, bcm=bcm,
                           C=C, bC=bC, Cb=Cb, Cb_cur=Cb_cur, nst=nst, nbf=nbf, n_cur=n_cur, nb_cur=nb_cur, mst=mst, m_cur=m_cur,
                           ring_num=ring_num, ring_h=ring_h)
        ml_group = self.make_ml_group()

        from collections import deque
        pending = deque()
        for step in range(nsteps):
            for d in range(2):
                pending.append(("load", step, d))
            for hg in range(2):
                for d in range(2):
                    if not dbg.get("b_no_gdn"):
                        pending.append(("gdn", step, d, hg))
            for d in range(2):
                if not dbg.get("b_no_ml"):
                    pending.append(("ml", step, d))
        free_g = list(range(NG))
        free_m = list(range(NM))
        done = set()
        active = []
        loaded = set()
        while pending or active:
            while pending:
                it = pending[0]
                if it[0] == "load":
                    _, step, d = it
                    load_step(step, d)
                    shared_pre(step, d)
                    pending.popleft()
                    continue
                if it[0] == "gdn":
                    _, step, d, hg = it
                    key_prev = ("gdn", step - 1, d, hg)
                    if (step > 0 and key_prev not in done) or not free_g:
                        break
                    si = free_g.pop(0)
                    active.append((it, gdn_group(step, d, hg, gslots[si]), ("g", si)))
                    pending.popleft()
                    continue
                if it[0] == "ml":
                    _, step, d = it
                    key_prev = ("ml", step - 1, d)
                    if (step > 0 and key_prev not in done) or not free_m:
                        break
                    si = free_m.pop(0)
                    active.append((it, ml_group(step, d, mslots[si]), ("m", si)))
                    pending.popleft()
                    continue
            for ent in list(active):
                it, gen, (kind, si) = ent
                try:
                    next(gen)
                except StopIteration:
                    active.remove(ent)
                    done.add(it)
                    (free_g if kind == "g" else free_m).append(si)
        B.barrier()
        st.close()

    def make_ml_group(self):
        B = self.B
        env = self._b_env
        data, dirc, cbufs, psr = env["data"], env["dirc"], env["cbufs"], env["psr"]
        bc3, bcm = env["bc3"], env["bcm"]
        C, bC, Cb, Cb_cur = env["C"], env["bC"], env["Cb"], env["Cb_cur"]
        nst, nbf, n_cur, nb_cur, mst, m_cur = env["nst"], env["nbf"], env["n_cur"], env["nb_cur"], env["mst"], env["m_cur"]
        ring_num, ring_h = env["ring_num"], env["ring_h"]
        out_lo, out_hi = env["out_lo"], env["out_hi"]

        def bc3n(ap2, n):
            return ap2.unsqueeze(2).to_broadcast([128, ap2.shape[1], n])

        def ml_group(step, d, sl):
            dd = data[(step, d)]
            dc = dirc[d]
            c = dd["c"]
            need_o = out_lo <= c < out_hi
            tg, bg = dd["GT"]; gs, bgs = dd["gs"]
            mq, bmq = dd["MQ"]; mk, bmk = dd["MK"]; mv, bmv = dd["MV"]
            X, bX = sl["X"]; Y, bY = sl["Y"]; Pm, bPm = sl["Pm"]; PT, bPT = sl["PT"]; Kw, bKw = sl["Kw"]
            sm, bsm = sl["sm"]; Ct, bCt = sl["Ct"]
            bcc = gs[:, 8:12]
            blast = gs[:, 20:24]
            cvec = gs[:, 48:52]
            mprev, bmprev = m_cur[d]
            B.op("pool", lambda e: e.tensor_tensor(out=X[:], in0=bcm(self.ident_f[:]), in1=bc3(cvec, 128), op=ALU.mult), reads=[bgs] + cbufs, writes=[bX])
            pC, bpC = psr.next()
            for u in range(4):
                B.op("pe", lambda e, u=u: e.matmul(pC[:, u * 128:(u + 1) * 128], lhsT=self.ones_f[:], rhs=X[:, u, :], start=True, stop=True),
                     reads=[bX] + cbufs, writes=[bpC], inc=(u == 3))
            B.op("dve", lambda e: e.tensor_tensor(out=Y[:], in0=pC[:, :].rearrange("p (u l) -> p u l", u=4), in1=bc3(bcc, 128), op=ALU.add),
                 reads=[bpC, bgs], writes=[bY])
            B.op("pool", lambda e: e.tensor_tensor(out=Y[:], in0=Y[:], in1=bcm(dc["MB"][:]), op=ALU.add), reads=[bY] + cbufs, writes=[bY])
            B.op("dve", lambda e: e.tensor_reduce(out=sm[:, 0:4], in_=Y[:], axis=AX.X, op=ALU.max), reads=[bY], writes=[bsm])
            B.op("dve", lambda e: e.tensor_tensor(out=sm[:, 4:8], in0=bcc, in1=mprev[:, 0:4], op=ALU.add), reads=[bgs, bmprev], writes=[bsm])
            B.op("dve", lambda e: e.tensor_tensor(out=sm[:, 8:12], in0=sm[:, 0:4], in1=sm[:, 4:8], op=ALU.max), reads=[bsm], writes=[bsm])
            B.op("dve", lambda e: e.tensor_scalar(out=sm[:, 12:16], in0=sm[:, 8:12], scalar1=-1.0, scalar2=None, op0=ALU.mult), reads=[bsm], writes=[bsm])
            if need_o:
                pQK, bpQK = psr.next()
                for u in range(4):
                    B.op("pe", lambda e, u=u: e.matmul(pQK[:, u * 128:(u + 1) * 128], lhsT=mq[:, u, :], rhs=mk[:, u, :], start=True, stop=True),
                         reads=[bmq, bmk], writes=[bpQK], inc=(u == 3))
                for u in range(4):
                    B.op("act", lambda e, u=u: e.activation(out=X[:, u, :], in_=Y[:, u, :], func=AF.Exp, bias=sm[:, 12 + u:13 + u]), reads=[bY, bsm], writes=[bX])
                B.op("dve", lambda e: e.tensor_tensor(out=Pm[:], in0=pQK[:, :].rearrange("p (u l) -> p u l", u=4), in1=X[:], op=ALU.mult),
                     reads=[bpQK, bX], writes=[bPm])
            yield
            if need_o:
                pT, bpT = psr.next()
                pTb = pT[:].bitcast(BF16)
                for u in range(4):
                    B.op("pe", lambda e, u=u: e.transpose(pTb[:, u * 128:(u + 1) * 128], Pm[:, u, :], self.ident_b[:]), reads=[bPm] + cbufs, writes=[bpT], inc=(u == 3))
                B.op("act", lambda e: e.activation(out=PT[:].rearrange("p u l -> p (u l)"), in_=pTb[:, 0:512], func=AF.Identity), reads=[bpT], writes=[bPT])
                B.op("dve", lambda e: e.tensor_tensor(out=sm[:, 16:20], in0=sm[:, 4:8], in1=sm[:, 8:12], op=ALU.subtract), reads=[bsm], writes=[bsm])
                B.op("act", lambda e: e.activation(out=sm[:, 16:20], in_=sm[:, 16:20], func=AF.Exp), reads=[bsm], writes=[bsm])
                B.op("act", lambda e: e.activation(out=sm[:, 20:24], in_=sm[:, 12:16], func=AF.Exp), reads=[bsm], writes=[bsm])
                yield
                cbt, bcb = Cb_cur[d]
                nbt, bnb = nb_cur[d]
                num, bnum = ring_num.next()
                hh, bhh = ring_h.next()
                for pr in range(2):
                    pN1, bpN1 = psr.next()
                    pN2, bpN2 = psr.next()
                    for uu in range(2):
                        u = pr * 2 + uu
                        B.op("pe", lambda e, u=u, uu=uu, pN1=pN1: e.matmul(pN1[:, uu * 256:(uu + 1) * 256], lhsT=mq[:, u, :], rhs=cbt[:, u, :], start=True, stop=True),
                             reads=[bmq, bcb], writes=[bpN1], inc=(uu == 1))
                    for uu in range(2):
                        u = pr * 2 + uu
                        B.op("pe", lambda e, u=u, uu=uu, pN2=pN2: e.matmul(pN2[:, uu * 256:(uu + 1) * 256], lhsT=PT[:, u, :], rhs=mv[:, u * 256:(u + 1) * 256], start=True, stop=True),
                             reads=[bPT, bmv], writes=[bpN2], inc=(uu == 1))
                    B.op("dve", lambda e, pr=pr, pN1=pN1: e.tensor_tensor(out=num[:, pr * 2:pr * 2 + 2, :], in0=pN1[:, :].rearrange("p (u l) -> p u l", u=2),
                                                                         in1=bc3n(sm[:, 16 + pr * 2:18 + pr * 2], 256), op=ALU.mult), reads=[bpN1, bsm], writes=[bnum])
                    B.op("dve", lambda e, pr=pr, pN2=pN2: e.tensor_tensor(out=num[:, pr * 2:pr * 2 + 2, :], in0=pN2[:, :].rearrange("p (u l) -> p u l", u=2),
                                                                         in1=num[:, pr * 2:pr * 2 + 2, :], op=ALU.add), reads=[bpN2, bnum], writes=[bnum])
                pDn, bpDn = psr.next()
                for u in range(4):
                    B.op("pe", lambda e, u=u: e.matmul(pDn[:, u:u + 1], lhsT=mq[:, u, :], rhs=nbt[:, u:u + 1], start=True, stop=True), reads=[bmq, bnb], writes=[bpDn], inc=False)
                for u in range(4):
                    B.op("pe", lambda e, u=u: e.matmul(pDn[:, 4 + u:5 + u], lhsT=PT[:, u, :], rhs=self.ones_b[:, 0:1], start=True, stop=True),
                         reads=[bPT] + cbufs, writes=[bpDn], inc=(u == 3))
                B.op("dve", lambda e: e.tensor_tensor(out=sm[:, 24:28], in0=pDn[:, 0:4], in1=sm[:, 16:20], op=ALU.mult), reads=[bpDn, bsm], writes=[bsm])
                B.op("dve", lambda e: e.tensor_tensor(out=sm[:, 24:28], in0=pDn[:, 4:8], in1=sm[:, 24:28], op=ALU.add), reads=[bpDn, bsm], writes=[bsm])
                B.op("dve", lambda e: e.tensor_tensor(out=sm[:, 24:28], in0=sm[:, 24:28], in1=sm[:, 24:28], op=ALU.mult), reads=[bsm], writes=[bsm])
                B.op("dve", lambda e: e.tensor_tensor(out=sm[:, 28:32], in0=sm[:, 20:24], in1=sm[:, 20:24], op=ALU.mult), reads=[bsm], writes=[bsm])
                B.op("dve", lambda e: e.tensor_tensor(out=sm[:, 24:28], in0=sm[:, 24:28], in1=sm[:, 28:32], op=ALU.max), reads=[bsm], writes=[bsm])
                B.op("pool", lambda e: e.tensor_tensor(out=sm[:, 28:32], in0=sm[:, 24:28], in1=self.nhalf[:, 0:4], op=ALU.pow), reads=[bsm] + cbufs, writes=[bsm])
                B.op("dve", lambda e: e.tensor_tensor(out=hh[:], in0=num[:], in1=bc3n(sm[:, 28:32], 256), op=ALU.mult), reads=[bnum, bsm], writes=[bhh])
                dst = (self.HF if d == 0 else self.HB)[c - 2]
                B.dma("sp", dst[:, :], hh[:].rearrange("p u l -> p (u l)"), bhh, reads=[bhh])
                yield
            pSel, bpSel = psr.next()
            B.op("pe", lambda e: e.matmul(pSel[:, 0:4], lhsT=dc["SEL"][:], rhs=sm[:, 8:12], start=True, stop=True), reads=[bsm] + cbufs, writes=[bpSel])
            mnew, bmnew = mst[d].next()
            B.op("act", lambda e: e.activation(out=mnew[:], in_=pSel[:, 0:4], func=AF.Identity), reads=[bpSel], writes=[bmnew])
            B.op("dve", lambda e: e.tensor_tensor(out=sm[:, 32:36], in0=cvec, in1=blast, op=ALU.add), reads=[bgs], writes=[bsm])
            B.op("dve", lambda e: e.tensor_tensor(out=sm[:, 32:36], in0=sm[:, 32:36], in1=mnew[:], op=ALU.subtract), reads=[bsm, bmnew], writes=[bsm])
            B.op("dve", lambda e: e.tensor_tensor(out=sm[:, 36:40], in0=blast, in1=mprev[:, 0:4], op=ALU.add), reads=[bgs, bmprev], writes=[bsm])
            B.op("dve", lambda e: e.tensor_tensor(out=sm[:, 36:40], in0=sm[:, 36:40], in1=mnew[:], op=ALU.subtract), reads=[bsm, bmnew], writes=[bsm])
            B.op("act", lambda e: e.activation(out=sm[:, 32:40], in_=sm[:, 32:40], func=AF.Exp), reads=[bsm], writes=[bsm])
            pKt, bpKt = psr.next()
            pKtb = pKt[:].bitcast(BF16)
            for u in range(4):
                B.op("pe", lambda e, u=u: e.transpose(pKtb[:, u * 128:(u + 1) * 128], mk[:, u, :], self.ident_b[:]), reads=[bmk] + cbufs, writes=[bpKt], inc=(u == 3))
            B.op("dve", lambda e: e.tensor_tensor(out=Kw[:], in0=pKtb[:, 0:512].rearrange("p (u l) -> p u l", u=4), in1=bc3(sm[:, 32:36], 128), op=ALU.mult),
                 reads=[bpKt, bsm], writes=[bKw])
            m_cur[d] = (mnew, bmnew)
            yield
            for pr in range(2):
                pC2, bpC2 = psr.next()
                for uu in range(2):
                    u = pr * 2 + uu
                    B.op("pe", lambda e, u=u, uu=uu, pC2=pC2: e.matmul(pC2[:, uu * 256:(uu + 1) * 256], lhsT=Kw[:, u, :], rhs=mv[:, u * 256:(u + 1) * 256], start=True, stop=True),
                         reads=[bKw, bmv], writes=[bpC2], inc=(uu == 1))
                B.op("pool", lambda e, pr=pr: e.tensor_tensor(out=Ct[:, pr * 2:pr * 2 + 2, :], in0=C[d][:, pr * 2:pr * 2 + 2, :], in1=bc3n(sm[:, 36 + pr * 2:38 + pr * 2], 256), op=ALU.mult),
                     reads=[bC[d], bsm], writes=[bCt])
                B.op("dve", lambda e, pr=pr, pC2=pC2: e.tensor_tensor(out=C[d][:, pr * 2:pr * 2 + 2, :], in0=pC2[:, :].rearrange("p (u l) -> p u l", u=2), in1=Ct[:, pr * 2:pr * 2 + 2, :], op=ALU.add),
                     reads=[bpC2, bCt], writes=[bC[d]])
            pN, bpN = psr.next()
            for u in range(4):
                B.op("pe", lambda e, u=u: e.matmul(pN[:, u:u + 1], lhsT=Kw[:, u, :], rhs=self.ones_b[:, 0:1], start=True, stop=True), reads=[bKw] + cbufs, writes=[bpN], inc=(u == 3))
            nold, bnold = n_cur[d]
            nnew, bnnew = nst[d].next()
            B.op("dve", lambda e: e.tensor_tensor(out=nnew[:, 4:8], in0=nold[:, 0:4], in1=sm[:, 36:40], op=ALU.mult), reads=[bnold, bsm], writes=[bnnew])
            B.op("dve", lambda e: e.tensor_tensor(out=nnew[:, 0:4], in0=pN[:, 0:4], in1=nnew[:, 4:8], op=ALU.add), reads=[bpN, bnnew], writes=[bnnew])
            nbn, bnbn = nbf[d].next()
            B.op("act", lambda e: e.activation(out=nbn[:], in_=nnew[:, 0:4], func=AF.Identity), reads=[bnnew], writes=[bnbn])
            cbn, bcbn = Cb[d].next()
            B.op("act", lambda e: e.activation(out=cbn[:], in_=C[d][:], func=AF.Identity), reads=[bC[d]], writes=[bcbn])
            n_cur[d] = (nnew, bnnew)
            nb_cur[d] = (nbn, bnbn)
            Cb_cur[d] = (cbn, bcbn)
            yield

        return ml_group

    def phaseC1(self):
        B, nc, inp = self.B, self.nc, self.inp
        st = ExitStack()
        dbg = self.debug
        wo, bwo = self.load_w_bf16(st, "c_wo", inp["w_o"], 8, 4096, 8)
        wbg, bwbg = self.load_w_bf16(st, "c_wbg", inp["w_bg"], 8, 1024, 2)
        wbm, bwbm = self.load_w_bf16(st, "c_wbm", inp["w_bm"], 8, 1024, 2)
        wout, bwout = self.load_w_bf16(st, "c_wout", inp["w_out"], 8, 1024, 2)
        nwb = B.sb(st, "c_nwb", [128, 2, 1024], F32)
        bnwb = Buf("c_nwb")
        B.dma("sp", nwb[:, 0, :], inp["gnw_bc"][:, :], bnwb, writes=[bnwb])
        B.dma("sp", nwb[:, 1, :], inp["mnw_bc"][:, :], bnwb, writes=[bnwb])
        zero = B.sb(st, "c_zero", [64, 1024], F32)
        bzero = Buf("c_zero")
        B.op("pool", lambda e: e.memset(zero[:], 0.0), writes=[bzero])
        bX1 = Buf("X1")
        B.dma("sp", self.X1[0:64, :], zero[:], bzero, reads=[bzero], writes=[bX1])
        NS = 2
        xt = B.sb(st, "c_x", [128, NS, 1024], F32)
        bxts = [Buf("c_x%d" % i) for i in range(NS)]
        xn = B.sb(st, "c_xn", [128, NS, 1024], BF16); bxn = Buf("c_xn")
        sq = B.sb(st, "c_sq", [128, 24], F32); bsq = Buf("c_sq")
        junk = B.sb(st, "c_junk", [128, 1024], BF16); bjunk = Buf("c_junk")
        hxT = B.sb(st, "c_hxT", [128, 8, NS * 128], BF16); bhx = Buf("c_hxT")
        oa = B.sb(st, "c_oa", [128, 1024], F32); boa = Buf("c_oa")
        ob = B.sb(st, "c_ob", [128, 1024], F32); bob = Buf("c_ob")
        gt = B.sb(st, "c_gt", [128, 1024], F32); bgt = Buf("c_gt")
        osq = B.sb(st, "c_osq", [128, 1024], F32); bosq = Buf("c_osq")
        sm = B.sb(st, "c_sm", [128, 32], F32); bsm = Buf("c_sm")
        og = B.sb(st, "c_og", [128, 1024], BF16); bog = Buf("c_og")
        brT = [B.sb(st, "c_brT%d" % i, [128, 8, NS * 128], BF16) for i in range(2)]
        bbrT = [Buf("c_brT%d" % i) for i in range(2)]
        sg = Ring(B, st, "c_sg", 4, [128, NS * 128], F32)
        yt = Ring(B, st, "c_yt", 2, [128, NS * 128], F32)
        mT = B.sb(st, "c_mT", [128, 8, NS * 128], BF16); bmT = Buf("c_mT")
        tmp = Ring(B, st, "c_tmp", 2, [128, 512], F32)
        ptr = Ring(B, st, "c_ptr", 2, [128, 512], F32, psum=True)
        pmm = Ring(B, st, "c_pmm", 5, [128, 512], F32, psum=True)
        nt = OWN_T // 128
        sts = []
        i = 0
        while i < nt:
            ns = min(NS, nt - i)
            sts.append((i, ns))
            i += ns
        if dbg.get("c1_tiles"):
            sts = sts[: dbg["c1_tiles"]]
        for (t0, ns) in sts:
            n = ns * 128
            for s in range(ns):
                B.dma("sp", xt[:, s, :], inp["x"][(t0 + s) * 128:(t0 + s + 1) * 128, :], bxts[s], writes=[bxts[s]])
            self.norm_transpose(xt, bxts, ns, xn, bxn, sq, bsq, junk, bjunk, ptr, hxT, bhx, 0, 0)
            for br in range(2):
                nh, hd = (8, 128) if br == 0 else (4, 256)
                srcf, srcb = (self.OF, self.OB) if br == 0 else (self.HF, self.HB)
                for s in range(ns):
                    c = t0 + s
                    B.dma("sp", oa[:], srcf[c], boa, writes=[boa])
                    B.dma("sp", ob[:], srcb[c], bob, writes=[bob])
                    for hf in range(2):
                        p, bp = pmm.next()
                        for k in range(8):
                            B.op("pe", lambda e, k=k, p=p, s=s, hf=hf, br=br: e.matmul(
                                p[:], lhsT=hxT[:, k, s * 128:(s + 1) * 128], rhs=wo[:, k, br * 1024 + hf * 512: br * 1024 + (hf + 1) * 512],
                                start=(k == 0), stop=(k == 7)), reads=[bhx, bwo], writes=[bp], inc=(k == 7))
                        B.op("act", lambda e, p=p, hf=hf, br=br: e.activation(out=gt[:, hf * 512:(hf + 1) * 512], in_=p[:],
                                                                            func=(AF.Silu if br == 0 else AF.Sigmoid)), reads=[bp], writes=[bgt])
                    B.op("pool", lambda e, br=br: e.tensor_tensor(out=gt[:], in0=gt[:], in1=nwb[:, br, :], op=ALU.mult), reads=[bgt, bnwb], writes=[bgt])
                    B.op("dve", lambda e: e.tensor_tensor(out=oa[:], in0=oa[:], in1=ob[:], op=ALU.add), reads=[boa, bob], writes=[boa])
                    B.op("pool", lambda e: e.tensor_tensor(out=osq[:], in0=oa[:], in1=oa[:], op=ALU.mult), reads=[boa], writes=[bosq])
                    B.op("dve", lambda e, nh=nh: e.tensor_reduce(out=sm[:, 0:nh], in_=osq[:].rearrange("p (h e) -> p h e", h=nh), axis=AX.X, op=ALU.add),
                         reads=[bosq], writes=[bsm])
                    B.op("dve", lambda e, nh=nh, hd=hd: e.tensor_scalar(out=sm[:, 8:8 + nh], in0=sm[:, 0:nh], scalar1=float(1.0 / hd), scalar2=float(EPS),
                                                                      op0=ALU.mult, op1=ALU.add), reads=[bsm], writes=[bsm])
                    B.op("pool", lambda e, nh=nh: e.tensor_tensor(out=sm[:, 16:16 + nh], in0=sm[:, 8:8 + nh], in1=self.nhalf[:, 0:nh], op=ALU.pow),
                         reads=[bsm, self.cb], writes=[bsm])
                    B.op("dve", lambda e, nh=nh, hd=hd: e.tensor_tensor(out=osq[:].rearrange("p (h e) -> p h e", h=nh), in0=oa[:].rearrange("p (h e) -> p h e", h=nh),
                                                                      in1=sm[:, 16:16 + nh].unsqueeze(2).to_broadcast([128, nh, hd]), op=ALU.mult),
                         reads=[boa, bsm], writes=[bosq])
                    B.op("dve", lambda e: e.tensor_tensor(out=og[:], in0=osq[:], in1=gt[:], op=ALU.mult), reads=[bosq, bgt], writes=[bog])
                    p, bp = ptr.next()
                    pb = p[:].bitcast(BF16)
                    for k in range(8):
                        B.op("pe", lambda e, k=k, pb=pb: e.transpose(pb[:, k * 128:(k + 1) * 128], og[:, k * 128:(k + 1) * 128], self.ident_b[:]),
                             reads=[bog, self.cb], writes=[bp], inc=(k == 7))
                    B.op("act", lambda e, pb=pb, s=s, br=br: e.activation(out=brT[br][:, :, s * 128:(s + 1) * 128], in_=pb[:, 0:1024].rearrange("p (k t) -> p k t", k=8),
                                                                          func=AF.Identity), reads=[bp], writes=[bbrT[br]])
            for ncn in range(8):
                sgs = []
                for gi in range(2):
                    p, bp = pmm.next()
                    for k in range(8):
                        B.op("pe", lambda e, k=k, p=p, gi=gi, ncn=ncn: e.matmul(p[:, 0:n], lhsT=wo[:, k, 2048 + gi * 1024 + ncn * 128: 2048 + gi * 1024 + (ncn + 1) * 128],
                                                                               rhs=hxT[:, k, 0:n], start=(k == 0), stop=(k == 7)), reads=[bhx, bwo], writes=[bp], inc=(k == 7))
                    g_, bg_ = sg.next()
                    B.op("act", lambda e, p=p, g_=g_: e.activation(out=g_[:, 0:n], in_=p[:, 0:n], func=AF.Sigmoid), reads=[bp], writes=[bg_])
                    sgs.append((g_, bg_))
                ys = []
                for br, (w, bw) in enumerate(((wbg, bwbg), (wbm, bwbm))):
                    p, bp = pmm.next()
                    for k in range(8):
                        B.op("pe", lambda e, k=k, p=p, w=w, br=br, ncn=ncn: e.matmul(p[:, 0:n], lhsT=w[:, k, ncn * 128:(ncn + 1) * 128], rhs=brT[br][:, k, 0:n],
                                                                                    start=(k == 0), stop=(k == 7)), reads=[bbrT[br], bw], writes=[bp], inc=(k == 7))
                    ys.append((p, bp))
                y_, by_ = yt.next()
                B.op("dve", lambda e, y_=y_: e.tensor_tensor(out=y_[:, 0:n], in0=ys[0][0][:, 0:n], in1=sgs[0][0][:, 0:n], op=ALU.mult), reads=[ys[0][1], sgs[0][1]], writes=[by_])
                g1, bg1 = sgs[1]
                B.op("dve", lambda e, g1=g1: e.tensor_tensor(out=g1[:, 0:n], in0=ys[1][0][:, 0:n], in1=g1[:, 0:n], op=ALU.mult), reads=[ys[1][1], bg1], writes=[bg1])
                B.op("pool", lambda e, y_=y_, g1=g1, ncn=ncn: e.tensor_tensor(out=mT[:, ncn, 0:n], in0=y_[:, 0:n], in1=g1[:, 0:n], op=ALU.add), reads=[by_, bg1], writes=[bmT])
            for s in range(ns):
                for hf in range(2):
                    p, bp = pmm.next()
                    for k in range(8):
                        B.op("pe", lambda e, k=k, p=p, s=s, hf=hf: e.matmul(p[:], lhsT=mT[:, k, s * 128:(s + 1) * 128], rhs=wout[:, k, hf * 512:(hf + 1) * 512],
                                                                           start=(k == 0), stop=(k == 7)), reads=[bmT, bwout], writes=[bp], inc=(k == 7))
                    t_, bt_ = tmp.next()
                    B.op("dve", lambda e, p=p, t_=t_, hf=hf: e.tensor_tensor(out=t_[:], in0=p[:], in1=self.gate_bc[:, 0, hf * 512:(hf + 1) * 512], op=ALU.mult),
                         reads=[bp, self.bgate], writes=[bt_])
                    B.op("pool", lambda e, t_=t_, s=s, hf=hf: e.tensor_tensor(out=xt[:, s, hf * 512:(hf + 1) * 512], in0=xt[:, s, hf * 512:(hf + 1) * 512], in1=t_[:], op=ALU.add),
                         reads=[bt_, bxts[s]], writes=[bxts[s]])
                B.dma("sp", self.X1[64 + (t0 + s) * 128: 64 + (t0 + s + 1) * 128, :], xt[:, s, :], bxts[s], reads=[bxts[s]], writes=[bX1])
        B.barrier()
        st.close()

    def precast_wup(self):
        B = self.B
        self.WUPB = B.dram("WUPB", [44, 128, 8, 128], BF16)
        self.bwupb = Buf("WUPB")
        src = self.inp["w_up"].rearrange("(k p) (c j) -> c p k j", p=128, j=128)
        for c in range(44):
            B.dma("pool", self.WUPB[c], src[c], self.bwupb, writes=[self.bwupb])

    def phaseC2(self):
        B, nc, inp = self.B, self.nc, self.inp
        st = ExitStack()
        dbg = self.debug
        wd = B.sb(st, "d_wd", [128, 22, 1024], BF16)
        bwd = Buf("d_wd")
        wdv = inp["w_down"].rearrange("(c p) n -> p c n", p=128)
        for i in range(0, 22, 6):
            j = min(22, i + 6)
            B.dma("pool", wd[:, i:j, :], wdv[:, i:j, :], bwd, writes=[bwd])
        cw = B.sb(st, "d_cw", [128, 44, 9], F32)
        nob = B.sb(st, "d_nob", [128, 1024], F32)
        bsm0 = Buf("d_small")
        B.dma("sp", cw[:], inp["ffn_cw"][:, :, :], bsm0, writes=[bsm0])
        B.dma("sp", nob[:], inp["now_bc"][:, :], bsm0, writes=[bsm0])
        wup = Ring(B, st, "d_wup", 3, [128, 2, 8, 128], BF16)
        xt = B.sb(st, "d_x", [128, 5, 1024], F32)
        bxts = [Buf("d_x%d" % i) for i in range(5)]
        xn = B.sb(st, "d_xn", [128, 5, 1024], BF16); bxn = Buf("d_xn")
        sq = B.sb(st, "d_sq", [128, 24], F32); bsq = Buf("d_sq")
        junk = B.sb(st, "d_junk", [128, 1024], BF16); bjunk = Buf("d_junk")
        hxT = B.sb(st, "d_hxT", [128, 8, 640], BF16); bhx = Buf("d_hxT")
        upad = Ring(B, st, "d_up", 4, [128, 10, 66], F32)
        acc = Ring(B, st, "d_acc", 4, [128, 8, 64], F32)
        sgt = Ring(B, st, "d_sg", 2, [128, 512], F32)
        ctmp = Ring(B, st, "d_ctmp", 3, [128, 8, 64], F32)
        aT = B.sb(st, "d_aT", [128, 22, 512], BF16); baT = Buf("d_aT")
        xo = B.sb(st, "d_xo", [128, 4, 1024], F32)
        bxo = [Buf("d_xo%d" % i) for i in range(4)]
        t2 = Ring(B, st, "d_t2", 2, [128, 512], F32)
        sq2 = B.sb(st, "d_sq2", [128, 16], F32); bsq2 = Buf("d_sq2")
        ptr = Ring(B, st, "d_ptr", 2, [128, 512], F32, psum=True)
        pu = Ring(B, st, "d_pu", 4, [128, 512], F32, psum=True)
        pd = Ring(B, st, "d_pd", 2, [128, 512], F32, psum=True)
        for (u_, bu_) in upad.slots:
            B.op("pool", lambda e, u_=u_: e.memset(u_[:], 0.0), writes=[bu_])
        nblk = dbg.get("c2_blocks", 8)
        bX1 = Buf("X1r")
        for j in range(nblk):
            r0 = 512 * j
            for s in range(5):
                B.dma("sp", xt[:, s, :], self.X1[r0 + s * 128: r0 + (s + 1) * 128, :], bxts[s], writes=[bxts[s]])
            for s in range(4):
                B.dma("sp", xo[:, s, :], self.X1[r0 + 64 + s * 128: r0 + 64 + (s + 1) * 128, :], bxo[s], writes=[bxo[s]])
            self.norm_transpose(xt, bxts, 5, xn, bxn, sq, bsq, junk, bjunk, ptr, hxT, bhx, 0, 4)
            for c in range(22):
                w, bw = wup.next()
                B.dma("sp", w[:, 0], self.WUPB[c], bw, reads=[self.bwupb], writes=[bw])
                B.dma("sp", w[:, 1], self.WUPB[22 + c], bw, reads=[self.bwupb], writes=[bw])
                accs = []
                for part in range(2):
                    ch = c + 22 * part
                    p1, bp1 = pu.next()
                    p2, bp2 = pu.next()
                    for k in range(8):
                        B.op("pe", lambda e, k=k, p1=p1, w=w, part=part: e.matmul(p1[:], lhsT=w[:, part, k, :], rhs=hxT[:, k, 0:512], start=(k == 0), stop=(k == 7)),
                             reads=[bw, bhx], writes=[bp1], inc=(k == 7))
                    for k in range(8):
                        B.op("pe", lambda e, k=k, p2=p2, w=w, part=part: e.matmul(p2[:, 0:128], lhsT=w[:, part, k, :], rhs=hxT[:, k, 512:640], start=(k == 0), stop=(k == 7)),
                             reads=[bw, bhx], writes=[bp2], inc=(k == 7))
                    u_, bu_ = upad.next()
                    B.op("act", lambda e, u_=u_, p1=p1: e.activation(out=u_[:, 0:8, 1:65], in_=p1[:].rearrange("p (r c) -> p r c", c=64), func=AF.Identity),
                         reads=[bp1], writes=[bu_])
                    B.op("act", lambda e, u_=u_, p2=p2: e.activation(out=u_[:, 8:10, 1:65], in_=p2[:, 0:128].rearrange("p (r c) -> p r c", c=64), func=AF.Identity),
                         reads=[bp2], writes=[bu_])
                    if j == 0:
                        B.op("pool", lambda e, u_=u_: e.memset(u_[:, 0:1, :], 0.0), writes=[bu_])
                    a_, ba_ = acc.next()
                    first = True
                    for dr in range(3):
                        for dc_ in range(3):
                            wsc = cw[:, ch, dr * 3 + dc_: dr * 3 + dc_ + 1]
                            src = u_[:, dr:dr + 8, dc_:dc_ + 64]
                            if part == 0:
                                if first:
                                    B.op("dve", lambda e, a_=a_, src=src, wsc=wsc: e.tensor_scalar(out=a_[:], in0=src, scalar1=wsc, scalar2=None, op0=ALU.mult),
                                         reads=[bu_, bsm0], writes=[ba_])
                                else:
                                    B.op("dve", lambda e, a_=a_, src=src, wsc=wsc: e.scalar_tensor_tensor(out=a_[:], in0=src, scalar=wsc, in1=a_[:], op0=ALU.mult, op1=ALU.add),
                                         reads=[bu_, bsm0, ba_], writes=[ba_])
                            else:
                                if first:
                                    B.op("act", lambda e, a_=a_, src=src, wsc=wsc: e.activation(out=a_[:], in_=src, func=AF.Identity, scale=wsc), reads=[bu_, bsm0], writes=[ba_])
                                else:
                                    c_, bc_ = ctmp.next()
                                    B.op("act", lambda e, c_=c_, src=src, wsc=wsc: e.activation(out=c_[:], in_=src, func=AF.Identity, scale=wsc), reads=[bu_, bsm0], writes=[bc_])
                                    B.op("pool", lambda e, a_=a_, c_=c_: e.tensor_tensor(out=a_[:], in0=a_[:], in1=c_[:], op=ALU.add), reads=[ba_, bc_], writes=[ba_])
                            first = False
                    accs.append((a_, ba_))
                s_, bs_ = sgt.next()
                B.op("act", lambda e, s_=s_: e.activation(out=s_[:], in_=accs[0][0][:].rearrange("p r c -> p (r c)"), func=AF.Silu), reads=[accs[0][1]], writes=[bs_])
                B.op("dve", lambda e, s_=s_, c=c: e.tensor_tensor(out=aT[:, c, :], in0=s_[:], in1=accs[1][0][:].rearrange("p r c -> p (r c)"), op=ALU.mult),
                     reads=[bs_, accs[1][1]], writes=[baT])
            for s in range(4):
                for hf in range(2):
                    p, bp = pd.next()
                    for c in range(22):
                        B.op("pe", lambda e, c=c, p=p, s=s, hf=hf: e.matmul(p[:], lhsT=aT[:, c, s * 128:(s + 1) * 128], rhs=wd[:, c, hf * 512:(hf + 1) * 512],
                                                                           start=(c == 0), stop=(c == 21)), reads=[baT, bwd], writes=[bp], inc=(c == 21))
                    t_, bt_ = t2.next()
                    B.op("dve", lambda e, p=p, t_=t_, hf=hf: e.tensor_tensor(out=t_[:], in0=p[:], in1=self.gate_bc[:, 1, hf * 512:(hf + 1) * 512], op=ALU.mult),
                         reads=[bp, self.bgate], writes=[bt_])
                    B.op("dve", lambda e, t_=t_, s=s, hf=hf: e.tensor_tensor(out=xo[:, s, hf * 512:(hf + 1) * 512], in0=xo[:, s, hf * 512:(hf + 1) * 512], in1=t_[:], op=ALU.add),
                         reads=[bt_, bxo[s]], writes=[bxo[s]])
                B.op("act", lambda e, s=s: e.activation(out=junk[:], in_=xo[:, s, :], func=AF.Square, accum_out=sq2[:, s:s + 1]), reads=[bxo[s]], writes=[bjunk, bsq2])
                B.op("dve", lambda e, s=s: e.tensor_scalar(out=sq2[:, 4 + s:5 + s], in0=sq2[:, s:s + 1], scalar1=float(D * EPS), scalar2=None, op0=ALU.add), reads=[bsq2], writes=[bsq2])
                B.op("pool", lambda e, s=s: e.tensor_tensor(out=sq2[:, 8 + s:9 + s], in0=sq2[:, 4 + s:5 + s], in1=self.nhalf[:, 0:1], op=ALU.pow), reads=[bsq2, self.cb], writes=[bsq2])
                B.op("dve", lambda e, s=s: e.scalar_tensor_tensor(out=xo[:, s, :], in0=xo[:, s, :], scalar=sq2[:, 8 + s:9 + s], in1=nob[:], op0=ALU.mult, op1=ALU.mult),
                     reads=[bxo[s], bsq2, bsm0], writes=[bxo[s]])
                B.op("act", lambda e, s=s: e.activation(out=xo[:, s, :], in_=xo[:, s, :], func=AF.Identity, scale=32.0), reads=[bxo[s]], writes=[bxo[s]])
                B.dma("sp", self.out[j * 512 + s * 128: j * 512 + (s + 1) * 128, :], xo[:, s, :], bxo[s], reads=[bxo[s]])
        B.barrier()
        st.close()


def build_program(debug=None):
    P = Prog(debug=debug)
    P.precast_wup()
    P.phase0()
    P.phaseA()
    P.phaseB()
    P.phaseC1()
    P.phaseC2()
    P.top.close()
    return P.B.finish(), P


_CACHE = {}


def kernel(**inputs):
    inp = {k: np.asarray(v) for k, v in inputs.items()}
    if "nc" not in _CACHE:
        _CACHE["nc"] = build_program()[0]
    nc = _CACHE["nc"]
    in_maps = [prep_core(inp, core) for core in range(8)]
    res = run_bass_kernel_spmd(nc, in_maps, core_ids=list(range(8)))
    out = np.empty((4, T, D), np.float32)
    for core in range(8):
        o = np.asarray(res.results[core]["out"], np.float32)
        b = core // 2
        if core % 2 == 0:
            out[b, 0:4096] = o
        else:
            out[b, 4096:8192] = o[::-1]
    return out
```

```python
import numpy as np
from contextlib import ExitStack

import concourse.bass as bass
import concourse.mybir as mybir
from concourse.bass_utils import run_bass_kernel_spmd

F32 = mybir.dt.float32
BF16 = mybir.dt.bfloat16
AF = mybir.ActivationFunctionType
ALU = mybir.AluOpType
AX = mybir.AxisListType

D = 1024
T = 8192
TC = 256
KD = 8
EPS = 1e-6
NEG = -1.0e30


class Buf:
    __slots__ = ("name", "w", "r", "dsem")

    def __init__(self, name):
        self.name = name
        self.w = None
        self.r = {}
        self.dsem = None


class Builder:
    def __init__(self):
        self.nc = bass.Bass("TRN2", target_bir_lowering=False)
        nc = self.nc
        self.es = ExitStack()
        self.es.enter_context(nc.allow_low_precision("bf16 matmul operands, fp32 accumulation"))
        self.engs = {"pe": nc.tensor, "act": nc.scalar, "dve": nc.vector, "pool": nc.gpsimd, "sp": nc.sync}
        self.sems = {}
        self.cnt = {}
        self.seen = {e: {} for e in self.engs}
        for e in self.engs:
            self.sems[e] = self.es.enter_context(nc.semaphore("s_" + e))
            self.cnt[e] = 0
        self.ndsem = 0
        self.nins = 0

    def sb(self, stack, name, shape, dt):
        return stack.enter_context(self.nc.sbuf_tensor(name, list(shape), dt))

    def ps(self, stack, name, shape, dt=F32):
        return stack.enter_context(self.nc.psum_tensor(name, list(shape), dt))

    def dram(self, name, shape, dt, kind="Internal"):
        return self.nc.dram_tensor(name, list(shape), dt, kind=kind).ap()

    def new_dsem(self):
        k = "d%d" % self.ndsem
        self.ndsem += 1
        self.sems[k] = self.es.enter_context(self.nc.semaphore(k))
        self.cnt[k] = 0
        return k

    def _deps(self, eng, reads, writes):
        deps = {}

        def add(k, v):
            if deps.get(k, 0) < v:
                deps[k] = v

        for b in reads:
            if b.w is not None:
                add(*b.w)
        for b in writes:
            if b.w is not None and b.w[0] != eng:
                add(*b.w)
            for k, v in b.r.items():
                if k != eng:
                    add(k, v)
        return deps

    def _emit_waits(self, eng, deps):
        e = self.engs[eng]
        seen = self.seen[eng]
        for k, v in deps.items():
            if seen.get(k, 0) >= v:
                continue
            assert v <= self.cnt[k], "wait on %s=%d never reached (issued %d)" % (k, v, self.cnt[k])
            e.wait_ge(self.sems[k], v)
            seen[k] = v

    def op(self, eng, fn, reads=(), writes=(), inc=True):
        self._emit_waits(eng, self._deps(eng, reads, writes))
        ins = fn(self.engs[eng])
        self.nins += 1
        if inc:
            self.cnt[eng] += 1
            ins.then_inc(self.sems[eng], 1)
            tok = (eng, self.cnt[eng])
        else:
            tok = (eng, self.cnt[eng] + 1)
        for b in reads:
            if b.r.get(eng, 0) < tok[1]:
                b.r[eng] = tok[1]
        for b in writes:
            b.w = tok
            b.r = {}
        return tok

    def dma(self, q, out, in_, sem_buf, reads=(), writes=()):
        self._emit_waits(q, self._deps("__dma__", reads, writes))
        ins = self.engs[q].dma_start(out=out, in_=in_)
        self.nins += 1
        if sem_buf.dsem is None:
            sem_buf.dsem = self.new_dsem()
        k = sem_buf.dsem
        self.cnt[k] += 16
        ins.then_inc(self.sems[k], 16)
        tok = (k, self.cnt[k])
        for b in reads:
            if b.r.get(k, 0) < tok[1]:
                b.r[k] = tok[1]
        for b in writes:
            b.w = tok
            b.r = {}
        return tok

    def barrier(self):
        for e in self.engs:
            self._emit_waits(e, {k: v for k, v in self.cnt.items() if k != e and v > 0})

    def finish(self):
        self.barrier()
        self.es.close()
        return self.nc


OFF_QKV, OFF_A, OFF_B, OFF_MQ, OFF_MK, OFF_MV, OFF_MI, OFF_MF, OFF_Z, OFF_MO, OFF_GG, OFF_GM, OFF_END = (
    0, 3072, 3088, 3104, 3616, 4128, 5152, 5160, 5168, 6192, 7216, 8240, 9264)
NCH = 66
OWN_T = 4224


def _col(v, n=128):
    v = np.asarray(v, np.float32).reshape(-1, n)
    return np.ascontiguousarray(v.T)


def _rep(v):
    v = np.asarray(v, np.float32).reshape(1, -1)
    return np.ascontiguousarray(np.repeat(v, 128, axis=0))


def _swapdir(a, flip):
    if not flip:
        return a
    h = a.shape[-1] // 2
    return np.concatenate([a[..., h:], a[..., :h]], axis=-1)


def prep_core(inp, core):
    b = core // 2
    flip = core % 2
    f32 = np.float32
    x = inp["x"][b]
    ctx = inp["ctx"][b]
    if flip:
        x = x[::-1]
        ctx = ctx[::-1]
    w_in = inp["w_in"][0]
    m = {}
    m["x"] = np.ascontiguousarray(x, dtype=f32)
    m["ctx"] = np.ascontiguousarray(ctx, dtype=f32)
    m["c_col"] = _col(inp["c"][b])
    m["cc_col"] = _col(inp["c_ctx"])
    m["w_ada"] = np.ascontiguousarray(inp["w_ada"][0], dtype=f32)
    b_ada = inp["b_ada"][0]
    m["b_ada_col"] = _col(b_ada)
    m["b_ada_g"] = np.ascontiguousarray(np.concatenate([_rep(b_ada[2048:3072]), _rep(b_ada[5120:6144])], axis=1))
    m["n1_col"] = _col(inp["norm1_w"][0])
    m["n2_col"] = _col(inp["norm2_w"][0])
    m["w_qkv"] = np.ascontiguousarray(w_in[:, OFF_QKV:OFF_A])
    wg = np.concatenate([_swapdir(w_in[:, OFF_A:OFF_B], flip), _swapdir(w_in[:, OFF_B:OFF_MQ], flip),
                         _swapdir(w_in[:, OFF_MI:OFF_MF], flip), _swapdir(w_in[:, OFF_MF:OFF_Z], flip)], axis=1)
    m["w_gate"] = np.ascontiguousarray(wg)
    m["w_ml"] = np.ascontiguousarray(w_in[:, OFF_MQ:OFF_MI])
    m["w_o"] = np.ascontiguousarray(w_in[:, OFF_Z:OFF_END])
    gp = np.concatenate([_swapdir(inp["gdn_dt_bias"][0].reshape(-1), flip), _swapdir(inp["gdn_a_log"][0].reshape(-1), flip),
                         _swapdir(inp["ml_igate_b"][0].reshape(-1), flip), _swapdir(inp["ml_fgate_b"][0].reshape(-1), flip)])
    m["gate_p"] = _rep(gp)
    gc = inp["gdn_conv"][0]
    if flip:
        gc = gc[::-1]
    m["gdn_cw"] = np.ascontiguousarray(gc.T.reshape(24, 128, 3).transpose(1, 0, 2), dtype=f32)
    fc = inp["ffn_conv"][0]
    if flip:
        fc = fc[::-1, ::-1]
    m["ffn_cw"] = np.ascontiguousarray(fc.reshape(9, 44, 128).transpose(2, 1, 0), dtype=f32)
    m["gnw_bc"] = _rep(np.tile(inp["gdn_norm_w"][0], 8))
    m["mnw_bc"] = _rep(inp["ml_norm_w"][0].reshape(-1))
    m["now_bc"] = _rep(inp["norm_out_w"])
    m["w_bg"] = np.ascontiguousarray(inp["w_branch_gdn"][0], dtype=f32)
    m["w_bm"] = np.ascontiguousarray(inp["w_branch_ml"][0], dtype=f32)
    m["w_out"] = np.ascontiguousarray(inp["w_out"][0], dtype=f32)
    m["w_up"] = np.ascontiguousarray(inp["w_up"][0], dtype=f32)
    m["w_down"] = np.ascontiguousarray(inp["w_down"][0], dtype=f32)
    m["smask"] = make_smask()
    return m


def make_smask():
    idx = np.arange(128)
    i = idx[None, :]
    j = idx[:, None]
    out = np.zeros((128, 14, 128), np.float32)
    for lev in range(7):
        b = 1 << lev
        same = (i // (2 * b)) == (j // (2 * b))
        f = same & ((i % (2 * b)) < b) & ((j % (2 * b)) >= b)
        g = same & ((j % (2 * b)) < b) & ((i % (2 * b)) >= b)
        out[:, lev, :] = np.where(f, -1.0, 0.0) + np.eye(128)
        out[:, 7 + lev, :] = np.where(g, -1.0, 0.0) + np.eye(128)
    return out


IN_SHAPES = {
    "x": [T, D], "ctx": [TC, D], "c_col": [128, 8], "cc_col": [128, 8], "w_ada": [D, 6144],
    "b_ada_col": [128, 48], "b_ada_g": [128, 2048], "n1_col": [128, 8], "n2_col": [128, 8],
    "w_qkv": [D, 3072], "w_gate": [D, 48], "w_ml": [D, 2048], "w_o": [D, 4096], "gate_p": [128, 48],
    "gdn_cw": [128, 24, 3], "ffn_cw": [128, 44, 9], "gnw_bc": [128, 1024], "mnw_bc": [128, 1024],
    "now_bc": [128, 1024], "w_bg": [D, D], "w_bm": [D, D], "w_out": [D, D], "w_up": [D, 5632], "w_down": [2816, D],
    "smask": [128, 14, 128],
}


class Ring:
    def __init__(self, B, stack, name, n, shape, dt, psum=False):
        self.slots = []
        for i in range(n):
            t = (B.ps if psum else B.sb)(stack, "%s%d" % (name, i), shape, dt)
            self.slots.append((t, Buf("%s%d" % (name, i))))
        self.i = 0

    def next(self):
        s = self.slots[self.i % len(self.slots)]
        self.i += 1
        return s


class Prog:
    def __init__(self, debug=None):
        self.debug = debug or {}
        self.B = Builder()
        self.nc = self.B.nc
        self.top = ExitStack()
        self.inp = {}
        for k, shp in IN_SHAPES.items():
            self.inp[k] = self.nc.dram_tensor(k, list(shp), F32, kind="ExternalInput").ap()
        self.out = self.nc.dram_tensor("out", [4096, D], F32, kind="ExternalOutput").ap()
        dk = "ExternalOutput" if self.debug.get("scratch_out") else "Internal"
        B = self.B
        self.KT = B.dram("KT", [NCH, 128, 8, 128], BF16, dk)
        self.QT = B.dram("QT", [NCH, 128, 8, 128], BF16, dk)
        self.VG = B.dram("VG", [NCH, 128, 1024], BF16, dk)
        self.MQT = B.dram("MQT", [NCH, 128, 4, 128], BF16, dk)
        self.MKT = B.dram("MKT", [NCH, 128, 4, 128], BF16, dk)
        self.MV = B.dram("MV", [NCH, 128, 1024], BF16, dk)
        self.GT = B.dram("GT", [NCH, 128, 48], F32, dk)
        self.OF = B.dram("OF", [33, 128, 1024], F32, dk)
        self.OB = B.dram("OB", [33, 128, 1024], F32, dk)
        self.HF = B.dram("HF", [33, 128, 1024], F32, dk)
        self.HB = B.dram("HB", [33, 128, 1024], F32, dk)
        self.X1 = B.dram("X1", [64 + OWN_T, D], F32, dk)
        self.consts()

    def consts(self):
        B, st = self.B, self.top
        self.ident_f = B.sb(st, "ident_f", [128, 128], F32)
        self.ident_b = B.sb(st, "ident_b", [128, 128], BF16)
        self.ones_f = B.sb(st, "ones_f", [128, 128], F32)
        self.ones_b = B.sb(st, "ones_b", [128, 128], BF16)
        self.nhalf = B.sb(st, "nhalf", [128, 512], F32)
        self.cb = Buf("consts")
        cb = self.cb
        B.op("pool", lambda e: e.memset(self.ones_f[:], 1.0), writes=[cb])
        B.op("pool", lambda e: e.memset(self.ones_b[:], 1.0), writes=[cb])
        B.op("pool", lambda e: e.memset(self.nhalf[:], -0.5), writes=[cb])
        B.op("pool", lambda e: e.memset(self.ident_f[:], 1.0), writes=[cb])
        B.op("pool", lambda e: e.affine_select(self.ident_f[:], self.ident_f[:], pattern=[[-1, 128]], compare_op=ALU.is_equal,
                                               fill=0.0, base=0, channel_multiplier=1), reads=[cb], writes=[cb])
        B.op("dve", lambda e: e.tensor_copy(out=self.ident_b[:], in_=self.ident_f[:]), reads=[cb], writes=[cb])
        self.modc = B.sb(st, "modc", [128, 6, 8], F32)
        self.bmod = Buf("modc")
        self.gate_bc = B.sb(st, "gate_bc", [128, 2, 1024], F32)
        self.bgate = Buf("gate_bc")

    def mask(self, stack, name, cmp_pat, dt=F32, val=1.0, fill=0.0):
        B = self.B
        base, cm, step, cmp = cmp_pat
        t = B.sb(stack, name, [128, 128], dt)
        tf = t
        if dt != F32:
            tf = B.sb(stack, name + "_f", [128, 128], F32)
        b = Buf(name)
        B.op("pool", lambda e: e.memset(tf[:], val), writes=[b])
        B.op("pool", lambda e: e.affine_select(tf[:], tf[:], pattern=[[step, 128]], compare_op=cmp, fill=fill,
                                               base=base, channel_multiplier=cm), reads=[b], writes=[b])
        if dt != F32:
            B.op("dve", lambda e: e.tensor_copy(out=t[:], in_=tf[:]), reads=[b], writes=[b])
        return t, b

    def phase0(self):
        B, nc, inp = self.B, self.nc, self.inp
        st = ExitStack()
        sc = B.sb(st, "p0_sc", [128, 16], F32)
        bsc = Buf("p0_sc")
        scb = B.sb(st, "p0_scb", [128, 8, 128], F32)
        bscb = Buf("p0_scb")
        bcol = B.sb(st, "p0_bcol", [128, 48], F32)
        n12 = B.sb(st, "p0_n12", [128, 16], F32)
        bg = B.sb(st, "p0_bg", [128, 2048], F32)
        bsm = Buf("p0_small")
        B.dma("sp", sc[:, 0:8], inp["c_col"][:, :], bsc, writes=[bsc])
        B.dma("sp", sc[:, 8:16], inp["cc_col"][:, :], bsc, writes=[bsc])
        B.dma("sp", bcol[:], inp["b_ada_col"][:, :], bsm, writes=[bsm])
        B.dma("sp", n12[:, 0:8], inp["n1_col"][:, :], bsm, writes=[bsm])
        B.dma("sp", n12[:, 8:16], inp["n2_col"][:, :], bsm, writes=[bsm])
        B.dma("sp", bg[:], inp["b_ada_g"][:, :], bsm, writes=[bsm])
        B.op("act", lambda e: e.activation(out=sc[:], in_=sc[:], func=AF.Silu), reads=[bsc], writes=[bsc])
        for k in range(8):
            B.op("dve", lambda e, k=k: e.tensor_scalar(out=scb[:, k, :], in0=self.ones_f[:], scalar1=sc[:, k:k + 1], scalar2=None,
                                                       op0=ALU.mult), reads=[bsc, self.cb], writes=[bscb])
        wring = Ring(B, st, "p0_w", 2, [128, 8, 512], F32)
        pcol = B.ps(st, "p0_pcol", [128, 64], F32)
        bpcol = Buf("p0_pcol")
        prow = Ring(B, st, "p0_prow", 2, [128, 512], F32, psum=True)
        wv = inp["w_ada"].rearrange("(k p) n -> p k n", p=128)
        xslot = {0: 0, 1: 1, 3: 2, 4: 3}
        for nb in range(12):
            v, half = nb // 2, nb % 2
            w, bw = wring.next()
            B.dma("sp", w[:], wv[:, :, nb * 512:(nb + 1) * 512], bw, writes=[bw])
            if v in (2, 5):
                p, bp = prow.next()
                for k in range(8):
                    B.op("pe", lambda e, k=k, p=p, w=w: e.matmul(p[:], lhsT=scb[:, k, :], rhs=w[:, k, :], start=(k == 0), stop=(k == 7)),
                         reads=[bscb, bw], writes=[bp], inc=(k == 7))
                gi = 0 if v == 2 else 1
                B.op("dve", lambda e, p=p, gi=gi, half=half: e.tensor_tensor(
                    out=self.gate_bc[:, gi, half * 512:(half + 1) * 512], in0=p[:], in1=bg[:, gi * 1024 + half * 512: gi * 1024 + (half + 1) * 512],
                    op=ALU.add), reads=[bp, bsm], writes=[self.bgate])
            else:
                for cc in range(4):
                    col = xslot[v] * 8 + half * 4 + cc
                    for k in range(8):
                        B.op("pe", lambda e, k=k, w=w, cc=cc, col=col: e.matmul(pcol[:, col:col + 1], lhsT=w[:, k, cc * 128:(cc + 1) * 128],
                                                                                 rhs=sc[:, k:k + 1], start=(k == 0), stop=(k == 7)),
                             reads=[bw, bsc], writes=[bpcol], inc=(k == 7))
                    if v in (0, 1):
                        col2 = 32 + v * 8 + half * 4 + cc
                        for k in range(8):
                            B.op("pe", lambda e, k=k, w=w, cc=cc, col2=col2: e.matmul(pcol[:, col2:col2 + 1], lhsT=w[:, k, cc * 128:(cc + 1) * 128],
                                                                                       rhs=sc[:, 8 + k:9 + k], start=(k == 0), stop=(k == 7)),
                                 reads=[bw, bsc], writes=[bpcol], inc=(k == 7))
        mc = B.sb(st, "p0_mc", [128, 6, 8], F32)
        bmc = Buf("p0_mc")
        for i, v in enumerate((0, 1, 3, 4)):
            B.op("dve", lambda e, i=i, v=v: e.tensor_tensor(out=mc[:, i, :], in0=pcol[:, i * 8:(i + 1) * 8], in1=bcol[:, v * 8:(v + 1) * 8], op=ALU.add),
                 reads=[bpcol, bsm], writes=[bmc])
        for i, v in enumerate((0, 1)):
            B.op("dve", lambda e, i=i, v=v: e.tensor_tensor(out=mc[:, 4 + i, :], in0=pcol[:, 32 + i * 8:32 + (i + 1) * 8], in1=bcol[:, v * 8:(v + 1) * 8],
                                                            op=ALU.add), reads=[bpcol, bsm], writes=[bmc])
        md = self.modc
        for dst, (sci, shi, nw) in {0: (1, 0, 0), 2: (5, 4, 0), 4: (3, 2, 1)}.items():
            B.op("dve", lambda e, dst=dst, sci=sci, nw=nw: e.scalar_tensor_tensor(out=md[:, dst, :], in0=mc[:, sci, :], scalar=1.0, in1=n12[:, nw * 8:(nw + 1) * 8],
                                                                                  op0=ALU.add, op1=ALU.mult), reads=[bmc, bsm], writes=[self.bmod])
            B.op("dve", lambda e, dst=dst, shi=shi: e.tensor_copy(out=md[:, dst + 1, :], in_=mc[:, shi, :]), reads=[bmc], writes=[self.bmod])
        B.barrier()
        st.close()

    def load_w_bf16(self, stack, name, ap, kchunks, ncols, nsplit=4):
        B = self.B
        t = B.sb(stack, name, [128, kchunks, ncols], BF16)
        b = Buf(name)
        v = ap.rearrange("(k p) n -> p k n", p=128)
        step = (ncols + nsplit - 1) // nsplit
        for i in range(0, ncols, step):
            j = min(ncols, i + step)
            B.dma("pool", t[:, :, i:j], v[:, :, i:j], b, writes=[b])
        return t, b

    def norm_transpose(self, xt, bxts, ns, xn, bxn, sq, bsq, junk, bjunk, ptr_ring, hxT, bhx, col0, ai, npart=128):
        B = self.B
        for s in range(ns):
            B.op("act", lambda e, s=s: e.activation(out=xn[0:npart, s, :], in_=xt[0:npart, s, :], func=AF.Square, accum_out=sq[0:npart, s:s + 1]),
                 reads=[bxts[s]], writes=[bxn, bsq])
        B.op("dve", lambda e: e.tensor_scalar(out=sq[0:npart, 8:8 + ns], in0=sq[0:npart, 0:ns], scalar1=float(D * EPS), scalar2=None, op0=ALU.add),
             reads=[bsq], writes=[bsq])
        B.op("pool", lambda e: e.tensor_tensor(out=sq[0:npart, 16:16 + ns], in0=sq[0:npart, 8:8 + ns], in1=self.nhalf[0:npart, 0:ns], op=ALU.pow),
             reads=[bsq, self.cb], writes=[bsq])
        for s in range(ns):
            B.op("dve", lambda e, s=s: e.tensor_scalar(out=xn[0:npart, s, :], in0=xt[0:npart, s, :], scalar1=sq[0:npart, 16 + s:17 + s], scalar2=32.0,
                                                       op0=ALU.mult, op1=ALU.mult), reads=[bxts[s], bsq], writes=[bxn])
        for k in range(KD):
            p, bp = ptr_ring.next()
            pb = p[:].bitcast(BF16)
            for s in range(ns):
                B.op("pe", lambda e, s=s, k=k, pb=pb: e.transpose(pb[:, s * npart:(s + 1) * npart], xn[0:npart, s, k * 128:(k + 1) * 128],
                                                                  self.ident_b[0:npart, 0:npart]),
                     reads=[bxn, self.cb], writes=[bp], inc=(s == ns - 1))
            B.op("act", lambda e, k=k, pb=pb: e.activation(out=hxT[:, k, col0:col0 + ns * npart], in_=pb[:, 0:ns * npart], func=AF.Identity,
                                                           scale=self.modc[:, ai, k:k + 1], bias=self.modc[:, ai + 1, k:k + 1]),
                 reads=[bp, self.bmod], writes=[bhx])

    def phaseA(self):
        B, nc, inp = self.B, self.nc, self.inp
        st = ExitStack()
        wqkv, bwqkv = self.load_w_bf16(st, "a_wqkv", inp["w_qkv"], 8, 3072, 6)
        wml, bwml = self.load_w_bf16(st, "a_wml", inp["w_ml"], 8, 2048, 4)
        wgt, bwgt = self.load_w_bf16(st, "a_wgt", inp["w_gate"], 8, 48, 1)
        cw = B.sb(st, "a_cw", [128, 24, 3], F32)
        gp = B.sb(st, "a_gp", [128, 48], F32)
        bsm = Buf("a_small")
        B.dma("sp", cw[:], inp["gdn_cw"][:, :, :], bsm, writes=[bsm])
        B.dma("sp", gp[:], inp["gate_p"][:, :], bsm, writes=[bsm])
        B.op("act", lambda e: e.activation(out=gp[:, 16:32], in_=gp[:, 16:32], func=AF.Exp), reads=[bsm], writes=[bsm])
        B.op("dve", lambda e: e.tensor_scalar(out=gp[:, 16:32], in0=gp[:, 16:32], scalar1=-1.0, scalar2=None, op0=ALU.mult), reads=[bsm], writes=[bsm])
        xt = B.sb(st, "a_x", [128, 4, 1024], F32)
        bxts = [Buf("a_x%d" % i) for i in range(4)]
        xh = B.sb(st, "a_xh", [2, 1, 1024], F32); bxh = Buf("a_xh")
        xn = B.sb(st, "a_xn", [128, 4, 1024], BF16); bxn = Buf("a_xn")
        xnh = B.sb(st, "a_xnh", [2, 1, 1024], BF16); bxnh = Buf("a_xnh")
        sq = B.sb(st, "a_sq", [128, 24], F32); bsq = Buf("a_sq")
        sqh = B.sb(st, "a_sqh", [128, 24], F32); bsqh = Buf("a_sqh")
        junk = None; bjunk = None
        hxT = B.sb(st, "a_hxT", [128, 8, 514], BF16); bhx = Buf("a_hxT")
        hxh = B.sb(st, "a_hxh", [128, 8, 2], BF16); bhxh = Buf("a_hxh")
        ptr = Ring(B, st, "a_ptr", 2, [128, 512], F32, psum=True)
        pz = Ring(B, st, "a_pz", 2, [128, 512], F32, psum=True)
        pmisc = B.ps(st, "a_pmisc", [128, 512], F32)
        pzh = pmisc[:, 0:64]; bpzh = Buf("a_pzh")
        pn = Ring(B, st, "a_pn", 2, [128, 512], F32, psum=True)
        zb = Ring(B, st, "a_zb", 2, [128, 514], F32)
        y1 = Ring(B, st, "a_y1", 2, [128, 512], F32)
        sqb = Ring(B, st, "a_sqb", 2, [128, 512], BF16)
        skeep = B.sb(st, "a_skeep", [128, 8, 512], F32)
        bskeep = [Buf("a_skeep%d" % i) for i in range(8)]
        rnr = B.sb(st, "a_rnr", [8, 512], F32); brnr = Buf("a_rnr")
        ind = B.sb(st, "a_ind", [128, 8, 8], BF16); bind = Buf("a_ind")
        selr = B.sb(st, "a_selr", [8, 8, 128], F32); bselr = Buf("a_selr")
        B.op("pool", lambda e: e.memset(ind[:], 0.0), writes=[bind])
        for jj in range(8):
            B.op("pool", lambda e, jj=jj: e.memset(ind[:, jj, jj:jj + 1], 1.0), writes=[bind])
            B.op("dve", lambda e, jj=jj: e.tensor_copy(out=selr[:, jj, :], in_=self.ident_f[0:8, jj:jj + 1].to_broadcast([8, 128])), reads=[self.cb], writes=[bselr])
        pss = B.ps(st, "a_pss", [128, 512], F32); bpss = Buf("a_pss")
        kst = B.sb(st, "a_kst", [128, 4, 8, 128], BF16); bkst = Buf("a_kst")
        qst = B.sb(st, "a_qst", [128, 4, 8, 128], BF16); bqst = Buf("a_qst")
        vT = B.sb(st, "a_vT", [128, 8, 512], BF16); bvT = Buf("a_vT")
        vst = Ring(B, st, "a_vst", 1, [128, 4, 1024], BF16)
        mqst = B.sb(st, "a_mqst", [128, 4, 4, 128], BF16); bmqst = Buf("a_mqst")
        mkst = B.sb(st, "a_mkst", [128, 4, 4, 128], BF16); bmkst = Buf("a_mkst")
        graw = B.sb(st, "a_graw", [128, 4, 48], F32); bgraw = Buf("a_graw")
        gwk = B.sb(st, "a_gwk", [128, 4, 48], F32); bgwk = Buf("a_gwk")
        gsb = Ring(B, st, "a_gsb", 2, [128, 4, 48], F32)
        pg = pmisc[:, 64:256].rearrange("p (s g) -> p s g", g=48); bpg = Buf("a_pg")
        dkr = float(128 ** -0.5)

        tiles = [(inp["ctx"], 0, 2, 0, False, False)]
        for i in range(16):
            tiles.append((inp["x"], i * 512, 4, 2 + 4 * i, i > 0, i < 15))
        if self.debug.get("a_tiles"):
            tiles = tiles[: self.debug["a_tiles"]]
        for (src, t0, ns, c0, hl, hr) in tiles:
            n = ns * 128
            ai = 2 if src is inp["ctx"] else 0
            for s in range(ns):
                B.dma("sp", xt[:, s, :], src[t0 + s * 128:t0 + (s + 1) * 128, :], bxts[s], writes=[bxts[s]])
            tl = t0 - 1 if hl else t0
            tr = t0 + n if hr else t0
            B.dma("sp", xh[0:1, 0, :], src[tl:tl + 1, :], bxh, writes=[bxh])
            B.dma("sp", xh[1:2, 0, :], src[tr:tr + 1, :], bxh, writes=[bxh])
            self.norm_transpose(xt, bxts, ns, xn, bxn, sq, bsq, junk, bjunk, ptr, hxT, bhx, 1, ai)
            self.norm_transpose(xh, [bxh], 1, xnh, bxnh, sqh, bsqh, junk, bjunk, ptr, hxh, bhxh, 0, ai, npart=2)
            def conv_chunk(j):
                p, bp = pz.next()
                for k in range(8):
                    B.op("pe", lambda e, k=k, p=p, j=j: e.matmul(p[:, 0:n], lhsT=wqkv[:, k, j * 128:(j + 1) * 128], rhs=hxT[:, k, 1:1 + n],
                                                                  start=(k == 0), stop=(k == 7)), reads=[bwqkv, bhx], writes=[bp], inc=(k == 7))
                for k in range(8):
                    B.op("pe", lambda e, k=k, j=j: e.matmul(pzh[:, 2 * j:2 * j + 2], lhsT=wqkv[:, k, j * 128:(j + 1) * 128], rhs=hxh[:, k, :],
                                                             start=(k == 0), stop=(k == 7)), reads=[bwqkv, bhxh], writes=[bpzh], inc=(k == 7))
                z, bz = zb.next()
                B.op("act", lambda e, z=z, p=p: e.activation(out=z[:, 1:1 + n], in_=p[:, 0:n], func=AF.Identity), reads=[bp], writes=[bz])
                B.op("act", lambda e, z=z, j=j: e.activation(out=z[:, 0:1], in_=pzh[:, 2 * j:2 * j + 1], func=AF.Identity), reads=[bpzh], writes=[bz])
                B.op("act", lambda e, z=z, j=j: e.activation(out=z[:, n + 1:n + 2], in_=pzh[:, 2 * j + 1:2 * j + 2], func=AF.Identity), reads=[bpzh], writes=[bz])
                if not hl:
                    B.op("pool", lambda e, z=z: e.memset(z[:, 0:1], 0.0), writes=[bz])
                if not hr:
                    B.op("pool", lambda e, z=z: e.memset(z[:, n + 1:n + 2], 0.0), writes=[bz])
                a1, ba1 = y1.next()
                B.op("dve", lambda e, z=z, a1=a1, j=j: e.tensor_scalar(out=a1[:, 0:n], in0=z[:, 1:1 + n], scalar1=cw[:, j, 1:2], scalar2=None, op0=ALU.mult),
                     reads=[bz, bsm], writes=[ba1])
                B.op("dve", lambda e, z=z, a1=a1, j=j: e.scalar_tensor_tensor(out=a1[:, 0:n], in0=z[:, 0:n], scalar=cw[:, j, 0:1], in1=a1[:, 0:n],
                                                                             op0=ALU.mult, op1=ALU.add), reads=[bz, bsm, ba1], writes=[ba1])
                B.op("dve", lambda e, z=z, a1=a1, j=j: e.scalar_tensor_tensor(out=a1[:, 0:n], in0=z[:, 2:2 + n], scalar=cw[:, j, 2:3], in1=a1[:, 0:n],
                                                                             op0=ALU.mult, op1=ALU.add), reads=[bz, bsm, ba1], writes=[ba1])
                return a1, ba1

            for half in range(2):
                for jj in range(8):
                    j = half * 8 + jj
                    a1, ba1 = conv_chunk(j)
                    B.op("act", lambda e, a1=a1, jj=jj: e.activation(out=skeep[:, jj, 0:n], in_=a1[:, 0:n], func=AF.Silu), reads=[ba1], writes=[bskeep[jj]])
                    q2, bq2 = sqb.next()
                    B.op("pool", lambda e, q2=q2, jj=jj: e.tensor_tensor(out=q2[:, 0:n], in0=skeep[:, jj, 0:n], in1=skeep[:, jj, 0:n], op=ALU.mult),
                         reads=[bskeep[jj]], writes=[bq2])
                    B.op("pe", lambda e, q2=q2, jj=jj: e.matmul(pss[0:8, 0:n], lhsT=ind[:, jj, :], rhs=q2[:, 0:n], start=(jj == 0), stop=(jj == 7)),
                         reads=[bq2, bind], writes=[bpss], inc=(jj == 7))
                B.op("act", lambda e: e.activation(out=rnr[:, 0:n], in_=pss[0:8, 0:n], func=AF.Ln, bias=float(EPS)), reads=[bpss], writes=[brnr])
                B.op("act", lambda e: e.activation(out=rnr[:, 0:n], in_=rnr[:, 0:n], func=AF.Exp, scale=-0.5), reads=[brnr], writes=[brnr])
                for jj in range(8):
                    pp, bpp = pn.next()
                    B.op("pe", lambda e, pp=pp, jj=jj: e.matmul(pp[:, 0:n], lhsT=selr[:, jj, :], rhs=rnr[:, 0:n], start=True, stop=True),
                         reads=[brnr, bselr], writes=[bpp])
                    if half == 0:
                        B.op("dve", lambda e, pp=pp, jj=jj: e.scalar_tensor_tensor(
                            out=qst[:, 0:ns, jj, :], in0=skeep[:, jj, 0:n].rearrange("p (s t) -> p s t", t=128), scalar=dkr,
                            in1=pp[:, 0:n].rearrange("p (s t) -> p s t", t=128), op0=ALU.mult, op1=ALU.mult), reads=[bskeep[jj], bpp], writes=[bqst])
                    else:
                        B.op("dve", lambda e, pp=pp, jj=jj: e.tensor_tensor(
                            out=kst[:, 0:ns, jj, :], in0=skeep[:, jj, 0:n].rearrange("p (s t) -> p s t", t=128),
                            in1=pp[:, 0:n].rearrange("p (s t) -> p s t", t=128), op=ALU.mult), reads=[bskeep[jj], bpp], writes=[bkst])
            for s in range(ns):
                for k in range(8):
                    B.op("pe", lambda e, k=k, s=s: e.matmul(pg[:, s, :], lhsT=hxT[:, k, 1 + s * 128:1 + (s + 1) * 128], rhs=wgt[:, k, :],
                                                             start=(k == 0), stop=(k == 7)), reads=[bhx, bwgt], writes=[bpg], inc=(k == 7))
            g, bg_ = gsb.next()
            self.gate_math(pg, bpg, graw, bgraw, gwk, bgwk, g, bg_, gp, bsm, ns)
            B.dma("sp", self.GT[c0:c0 + ns].rearrange("c t g -> t c g"), g[:, 0:ns, :], bg_, reads=[bg_])
            for j in range(16, 24):
                a1, ba1 = conv_chunk(j)
                B.op("act", lambda e, a1=a1, j=j: e.activation(out=vT[:, j - 16, 0:n], in_=a1[:, 0:n], func=AF.Silu), reads=[ba1], writes=[bvT])
            self.v_transposes(vT, bvT, ns, vst, pn, self.VG, c0)
            B.dma("sp", self.KT[c0:c0 + ns].rearrange("c d h t -> d c h t"), kst[:, 0:ns], bkst, reads=[bkst])
            B.dma("sp", self.QT[c0:c0 + ns].rearrange("c d h t -> d c h t"), qst[:, 0:ns], bqst, reads=[bqst])
            for j in range(16):
                p, bp = pz.next()
                for k in range(8):
                    B.op("pe", lambda e, k=k, p=p, j=j: e.matmul(p[:, 0:n], lhsT=wml[:, k, j * 128:(j + 1) * 128], rhs=hxT[:, k, 1:1 + n],
                                                                  start=(k == 0), stop=(k == 7)), reads=[bwml, bhx], writes=[bp], inc=(k == 7))
                if j < 4:
                    B.op("act", lambda e, p=p, j=j: e.activation(out=mqst[:, 0:ns, j, :], in_=p[:, 0:n].rearrange("p (s t) -> p s t", t=128),
                                                                  func=AF.Identity, scale=dkr), reads=[bp], writes=[bmqst])
                elif j < 8:
                    B.op("act", lambda e, p=p, j=j: e.activation(out=mkst[:, 0:ns, j - 4, :], in_=p[:, 0:n].rearrange("p (s t) -> p s t", t=128),
                                                                  func=AF.Identity), reads=[bp], writes=[bmkst])
                else:
                    B.op("act", lambda e, p=p, j=j: e.activation(out=vT[:, j - 8, 0:n], in_=p[:, 0:n], func=AF.Identity), reads=[bp], writes=[bvT])
            self.v_transposes(vT, bvT, ns, vst, pn, self.MV, c0)
            B.dma("sp", self.MQT[c0:c0 + ns].rearrange("c d h t -> d c h t"), mqst[:, 0:ns], bmqst, reads=[bmqst])
            B.dma("sp", self.MKT[c0:c0 + ns].rearrange("c d h t -> d c h t"), mkst[:, 0:ns], bmkst, reads=[bmkst])
        B.barrier()
        st.close()

    def v_transposes(self, vT, bvT, ns, vst, pn, dst, c0):
        B = self.B
        v, bv = vst.next()
        for s in range(ns):
            pp, bpp = pn.next()
            ppb = pp[:].bitcast(BF16)
            for h in range(8):
                B.op("pe", lambda e, s=s, h=h, ppb=ppb: e.transpose(ppb[:, h * 128:(h + 1) * 128], vT[:, h, s * 128:(s + 1) * 128], self.ident_b[:]),
                     reads=[bvT, self.cb], writes=[bpp], inc=(h == 7))
            B.op("act", lambda e, s=s, ppb=ppb, v=v: e.activation(out=v[:, s, :], in_=ppb[:, 0:1024], func=AF.Identity), reads=[bpp], writes=[bv])
        B.dma("sp", dst[c0:c0 + ns].rearrange("c t e -> t c e"), v[:, 0:ns, :], bv, reads=[bv])

    def gate_math(self, pg, bpg, graw, bgraw, wk, bwk, g, bg_, gp, bgp, ns):
        B = self.B
        S = slice(0, ns)

        def bc(lo, hi):
            return gp[:, lo:hi].unsqueeze(1).to_broadcast([128, ns, hi - lo])

        B.op("act", lambda e: e.activation(out=graw[:, S, :], in_=pg[:, S, :], func=AF.Identity), reads=[bpg], writes=[bgraw])
        B.op("dve", lambda e: e.tensor_tensor(out=wk[:, S, 0:16], in0=graw[:, S, 0:16], in1=bc(0, 16), op=ALU.add), reads=[bgraw, bgp], writes=[bwk])
        B.op("act", lambda e: e.activation(out=wk[:, S, 0:16], in_=wk[:, S, 0:16], func=AF.Exp), reads=[bwk], writes=[bwk])
        B.op("act", lambda e: e.activation(out=wk[:, S, 0:16], in_=wk[:, S, 0:16], func=AF.Ln, bias=1.0), reads=[bwk], writes=[bwk])
        B.op("dve", lambda e: e.tensor_tensor(out=g[:, S, 0:16], in0=wk[:, S, 0:16], in1=bc(16, 32), op=ALU.mult), reads=[bwk, bgp], writes=[bg_])
        B.op("act", lambda e: e.activation(out=wk[:, S, 16:32], in_=graw[:, S, 16:32], func=AF.Exp, scale=-1.0), reads=[bgraw], writes=[bwk])
        B.op("dve", lambda e: e.tensor_scalar(out=wk[:, S, 16:32], in0=wk[:, S, 16:32], scalar1=1.0, scalar2=None, op0=ALU.add), reads=[bwk], writes=[bwk])
        B.op("dve", lambda e: e.reciprocal(out=g[:, S, 16:32], in_=wk[:, S, 16:32]), reads=[bwk], writes=[bg_])
        B.op("dve", lambda e: e.tensor_tensor(out=wk[:, S, 32:48], in0=graw[:, S, 32:48], in1=bc(32, 48), op=ALU.add), reads=[bgraw, bgp], writes=[bwk])
        B.op("act", lambda e: e.activation(out=wk[:, S, 32:48], in_=wk[:, S, 32:48], func=AF.Exp, scale=float(2.0 / 15.0)), reads=[bwk], writes=[bwk])
        B.op("dve", lambda e: e.tensor_scalar(out=wk[:, S, 32:48], in0=wk[:, S, 32:48], scalar1=1.0, scalar2=None, op0=ALU.add), reads=[bwk], writes=[bwk])
        B.op("dve", lambda e: e.reciprocal(out=wk[:, S, 32:48], in_=wk[:, S, 32:48]), reads=[bwk], writes=[bwk])
        B.op("dve", lambda e: e.tensor_scalar(out=g[:, S, 32:48], in0=wk[:, S, 32:48], scalar1=-30.0, scalar2=15.0, op0=ALU.mult, op1=ALU.add),
             reads=[bwk], writes=[bg_])
        B.op("act", lambda e: e.activation(out=wk[:, S, 40:48], in_=g[:, S, 40:48], func=AF.Exp, scale=-1.0), reads=[bg_], writes=[bwk])
        B.op("act", lambda e: e.activation(out=wk[:, S, 40:48], in_=wk[:, S, 40:48], func=AF.Ln, bias=1.0), reads=[bwk], writes=[bwk])
        B.op("dve", lambda e: e.tensor_scalar(out=g[:, S, 40:48], in0=wk[:, S, 40:48], scalar1=-1.0, scalar2=None, op0=ALU.mult), reads=[bwk], writes=[bg_])

    def phaseB(self):
        B, nc, inp = self.B, self.nc, self.inp
        st = ExitStack()
        dbg = self.debug
        LE, bLE = self.mask(st, "b_LE", (0, -1, 1, ALU.is_ge))
        LT, bLT = self.mask(st, "b_LT", (-1, -1, 1, ALU.is_ge))
        GE, bGE = self.mask(st, "b_GE", (0, 1, -1, ALU.is_ge))
        GT_, bGT = self.mask(st, "b_GT", (-1, 1, -1, ALU.is_ge))
        MBf, bMBf = self.mask(st, "b_MBf", (0, 1, -1, ALU.is_ge), val=0.0, fill=NEG)
        MBb, bMBb = self.mask(st, "b_MBb", (0, -1, 1, ALU.is_ge), val=0.0, fill=NEG)
        SELf, bSELf = self.mask(st, "b_SELf", (-127, 1, 0, ALU.is_equal))
        SELb, bSELb = self.mask(st, "b_SELb", (0, 1, 0, ALU.is_equal))
        smask = B.sb(st, "b_smask", [128, 14, 128], BF16)
        bsm = Buf("b_smask")
        B.dma("pool", smask[:], inp["smask"][:, :, :], bsm, writes=[bsm])
        cbufs = [bLE, bLT, bGE, bGT, bMBf, bMBb, bSELf, bSELb, bsm, self.cb]
        dirc = [dict(U=LE, S=GT_, incl=LE, strict=LT, MB=MBf, SEL=SELf),
                dict(U=GE, S=LT, incl=GE, strict=GT_, MB=MBb, SEL=SELb)]
        S = [B.sb(st, "b_S%d" % d, [128, 8, 128], F32) for d in range(2)]
        bS = [[Buf("b_S%d_%d" % (d, g)) for g in range(2)] for d in range(2)]
        Sb = [[Ring(B, st, "b_Sb%d_%d_" % (d, g), 2, [128, 4, 128], BF16) for g in range(2)] for d in range(2)]
        Sb_cur = [[None, None], [None, None]]
        C = [B.sb(st, "b_C%d" % d, [128, 4, 256], F32) for d in range(2)]
        bC = [Buf("b_C%d" % d) for d in range(2)]
        Cb = [Ring(B, st, "b_Cb%d_" % d, 2, [128, 4, 256], BF16) for d in range(2)]
        Cb_cur = [None, None]
        nst = [Ring(B, st, "b_n%d_" % d, 2, [128, 8], F32) for d in range(2)]
        nbf = [Ring(B, st, "b_nb%d_" % d, 2, [128, 4], BF16) for d in range(2)]
        n_cur = [None, None]
        nb_cur = [None, None]
        mst = [Ring(B, st, "b_m%d_" % d, 2, [128, 4], F32) for d in range(2)]
        m_cur = [None, None]
        for d in range(2):
            B.op("pool", lambda e, d=d: e.memset(S[d][:], 0.0), writes=bS[d])
            B.op("pool", lambda e, d=d: e.memset(C[d][:], 0.0), writes=[bC[d]])
            for g in range(2):
                t, b = Sb[d][g].next()
                B.op("pool", lambda e, t=t: e.memset(t[:], 0.0), writes=[b])
                Sb_cur[d][g] = (t, b)
            t, b = Cb[d].next()
            B.op("pool", lambda e, t=t: e.memset(t[:], 0.0), writes=[b])
            Cb_cur[d] = (t, b)
            t, b = nst[d].next()
            B.op("pool", lambda e, t=t: e.memset(t[:], 0.0), writes=[b])
            n_cur[d] = (t, b)
            t, b = nbf[d].next()
            B.op("pool", lambda e, t=t: e.memset(t[:], 0.0), writes=[b])
            nb_cur[d] = (t, b)
            t, b = mst[d].next()
            B.op("pool", lambda e, t=t: e.memset(t[:], 0.0), writes=[b])
            m_cur[d] = (t, b)
        def dring(name, shape, dt):
            return [Ring(B, st, "b_%s%d_" % (name, d), 2, shape, dt) for d in range(2)]
        rKT = dring("KT", [128, 8, 128], BF16)
        rQT = dring("QT", [128, 8, 128], BF16)
        rVG = dring("VG", [128, 1024], BF16)
        rGT = dring("GT", [128, 48], F32)
        rMQ = dring("MQ", [128, 4, 128], BF16)
        rMK = dring("MK", [128, 4, 128], BF16)
        rMV = dring("MV", [128, 1024], BF16)
        rgs = dring("gs", [128, 64], F32)
        psr = Ring(B, st, "b_ps", 8, [128, 512], F32, psum=True)
        NG, NM = 3, 2
        gslots = []
        for i in range(NG):
            sl = {}
            for nm, shp, dt in (("A", [128, 4, 128], F32), ("Bt", [128, 4, 128], F32), ("Ct", [128, 4, 128], F32),
                                ("attnT", [128, 4, 128], BF16), ("Qm", [128, 4, 128], BF16), ("Qp", [128, 4, 128], BF16),
                                ("Kg", [128, 4, 128], BF16), ("kt", [128, 4, 128], BF16), ("G0", [128, 4, 128], BF16),
                                ("G1", [128, 4, 128], BF16), ("H0", [128, 4, 128], BF16), ("H1", [128, 4, 128], BF16),
                                ("IYT", [128, 4, 128], BF16), ("negW", [128, 4, 128], BF16), ("vnew", [128, 4, 128], BF16),
                                ("St", [128, 4, 128], F32)):
                sl[nm] = (B.sb(st, "b_g%d_%s" % (i, nm), shp, dt), Buf("b_g%d_%s" % (i, nm)))
            gslots.append(sl)
        mslots = []
        for i in range(NM):
            sl = {}
            for nm, shp, dt in (("X", [128, 4, 128], F32), ("Y", [128, 4, 128], F32), ("Pm", [128, 4, 128], BF16),
                                ("PT", [128, 4, 128], BF16), ("Kw", [128, 4, 128], BF16), ("sm", [128, 64], F32), ("Ct", [128, 4, 256], F32)):
                sl[nm] = (B.sb(st, "b_m%d_%s" % (i, nm), shp, dt), Buf("b_m%d_%s" % (i, nm)))
            mslots.append(sl)
        ring_o1 = Ring(B, st, "b_o1_", 2, [128, 4, 128], F32)
        ring_o = Ring(B, st, "b_o_", 2, [128, 4, 128], F32)
        ring_num = Ring(B, st, "b_num_", 2, [128, 4, 256], F32)
        ring_h = Ring(B, st, "b_h_", 2, [128, 4, 256], F32)

        def bc3(ap2, n):
            return ap2.unsqueeze(2).to_broadcast([128, 4, n])

        def bcm(ap2, n=4):
            return ap2.unsqueeze(1).to_broadcast([128, n, 128])

        nsteps = dbg.get("b_steps", NCH)
        order = [list(range(NCH)), [1, 0] + list(range(NCH - 1, 1, -1))]
        if dbg.get("b_order"):
            order = dbg["b_order"]
            nsteps = len(order[0])
        out_lo, out_hi = 2, 2 + 33

        data = {}

        def load_step(step, d):
            c = order[d][step]
            tk, bk = rKT[d].next(); tq, bq = rQT[d].next(); tv, bv = rVG[d].next(); tg, bg = rGT[d].next()
            tmq, bmq = rMQ[d].next(); tmk, bmk = rMK[d].next(); tmv, bmv = rMV[d].next()
            B.dma("sp", tg[:], self.GT[c], bg, writes=[bg])
            B.dma("sp", tk[:], self.KT[c], bk, writes=[bk])
            B.dma("sp", tq[:], self.QT[c], bq, writes=[bq])
            B.dma("sp", tv[:], self.VG[c], bv, writes=[bv])
            B.dma("sp", tmq[:], self.MQT[c], bmq, writes=[bmq])
            B.dma("sp", tmk[:], self.MKT[c], bmk, writes=[bmk])
            B.dma("sp", tmv[:], self.MV[c], bmv, writes=[bmv])
            data[(step, d)] = dict(c=c, KT=(tk, bk), QT=(tq, bq), VG=(tv, bv), GT=(tg, bg), MQ=(tmq, bmq), MK=(tmk, bmk), MV=(tmv, bmv))

        def shared_pre(step, d):
            dd = data[(step, d)]
            tg, bg = dd["GT"]
            gs, bgs = rgs[d].next()
            dc = dirc[d]
            p, bp = psr.next()
            B.op("pe", lambda e: e.matmul(p[:, 0:8], lhsT=dc["U"][:], rhs=tg[:, d * 8:(d + 1) * 8], start=True, stop=True), reads=[bg] + cbufs, writes=[bp], inc=False)
            B.op("pe", lambda e: e.matmul(p[:, 8:12], lhsT=dc["U"][:], rhs=tg[:, 40 + d * 4:44 + d * 4], start=True, stop=True), reads=[bg] + cbufs, writes=[bp], inc=False)
            B.op("pe", lambda e: e.matmul(p[:, 12:20], lhsT=self.ones_f[:], rhs=tg[:, d * 8:(d + 1) * 8], start=True, stop=True), reads=[bg] + cbufs, writes=[bp], inc=False)
            B.op("pe", lambda e: e.matmul(p[:, 20:24], lhsT=self.ones_f[:], rhs=tg[:, 40 + d * 4:44 + d * 4], start=True, stop=True), reads=[bg] + cbufs, writes=[bp])
            B.op("act", lambda e: e.activation(out=gs[:, 0:24], in_=p[:, 0:24], func=AF.Identity), reads=[bp], writes=[bgs])
            B.op("act", lambda e: e.activation(out=gs[:, 24:32], in_=gs[:, 0:8], func=AF.Exp), reads=[bgs], writes=[bgs])
            B.op("dve", lambda e: e.tensor_tensor(out=gs[:, 32:40], in0=gs[:, 12:20], in1=gs[:, 0:8], op=ALU.subtract), reads=[bgs], writes=[bgs])
            B.op("act", lambda e: e.activation(out=gs[:, 32:40], in_=gs[:, 32:40], func=AF.Exp), reads=[bgs], writes=[bgs])
            B.op("act", lambda e: e.activation(out=gs[:, 40:48], in_=gs[:, 12:20], func=AF.Exp), reads=[bgs], writes=[bgs])
            B.op("dve", lambda e: e.tensor_tensor(out=gs[:, 48:52], in0=tg[:, 32 + d * 4:36 + d * 4], in1=gs[:, 8:12], op=ALU.subtract), reads=[bgs, bg], writes=[bgs])
            dd["gs"] = (gs, bgs)

        def gdn_group(step, d, hg, sl):
            dd = data[(step, d)]
            dc = dirc[d]
            c = dd["c"]
            need_o = out_lo <= c < out_hi
            tk, bk = dd["KT"]; tq, bq = dd["QT"]; tv, bv = dd["VG"]; tg, bg = dd["GT"]; gs, bgs = dd["gs"]
            h0 = hg * 4
            A, bA = sl["A"]; Bt, bBt = sl["Bt"]; Ct, bCt = sl["Ct"]
            attnT, battn = sl["attnT"]; Qm, bQm = sl["Qm"]; Qp, bQp = sl["Qp"]; Kg, bKg = sl["Kg"]; kt, bkt = sl["kt"]
            IYT, bIYT = sl["IYT"]; negW, bnegW = sl["negW"]; vnew, bvnew = sl["vnew"]; St, bSt = sl["St"]
            gcol = tg[:, d * 8 + h0:d * 8 + h0 + 4]
            bcol = tg[:, 16 + d * 8 + h0:16 + d * 8 + h0 + 4]
            eg = gs[:, 24 + h0:24 + h0 + 4]
            ekt = gs[:, 32 + h0:32 + h0 + 4]
            gte = gs[:, 40 + h0:40 + h0 + 4]
            B.op("pool", lambda e: e.tensor_tensor(out=A[:], in0=bcm(dc["U"][:]), in1=bc3(gcol, 128), op=ALU.mult), reads=[bg] + cbufs, writes=[bA])
            pD, bpD = psr.next()
            for u in range(4):
                B.op("pe", lambda e, u=u: e.matmul(pD[:, u * 128:(u + 1) * 128], lhsT=dc["S"][:], rhs=A[:, u, :], start=True, stop=True),
                     reads=[bA] + cbufs, writes=[bpD], inc=(u == 3))
            B.op("act", lambda e: e.activation(out=Bt[:].rearrange("p u l -> p (u l)"), in_=pD[:, :], func=AF.Exp), reads=[bpD], writes=[bBt])
            B.op("pool", lambda e: e.tensor_tensor(out=A[:], in0=Bt[:], in1=bcm(dc["incl"][:]), op=ALU.mult), reads=[bBt] + cbufs, writes=[bA])
            B.op("pool", lambda e: e.tensor_tensor(out=Ct[:], in0=Bt[:], in1=bcm(dc["strict"][:]), op=ALU.mult), reads=[bBt] + cbufs, writes=[bCt])
            B.op("pool", lambda e: e.tensor_tensor(out=Ct[:], in0=Ct[:], in1=bc3(bcol, 128), op=ALU.mult), reads=[bCt, bg], writes=[bCt])
            pKK, bpKK = psr.next()
            pQK, bpQK = psr.next()
            pKt, bpKt = psr.next()
            pKtb = pKt[:].bitcast(BF16)
            for u in range(4):
                B.op("pe", lambda e, u=u: e.matmul(pKK[:, u * 128:(u + 1) * 128], lhsT=tk[:, h0 + u, :], rhs=tk[:, h0 + u, :], start=True, stop=True),
                     reads=[bk], writes=[bpKK], inc=(u == 3))
            for u in range(4):
                B.op("pe", lambda e, u=u: e.matmul(pQK[:, u * 128:(u + 1) * 128], lhsT=tk[:, h0 + u, :], rhs=tq[:, h0 + u, :], start=True, stop=True),
                     reads=[bk, bq], writes=[bpQK], inc=(u == 3))
            for u in range(4):
                B.op("pe", lambda e, u=u: e.transpose(pKtb[:, u * 128:(u + 1) * 128], tk[:, h0 + u, :], self.ident_b[:]),
                     reads=[bk] + cbufs, writes=[bpKt], inc=(u == 3))
            B.op("dve", lambda e: e.tensor_tensor(out=attnT[:], in0=pQK[:, :].rearrange("p (u l) -> p u l", u=4), in1=A[:], op=ALU.mult),
                 reads=[bpQK, bA], writes=[battn])
            B.op("dve", lambda e: e.tensor_tensor(out=Qm[:], in0=pKK[:, :].rearrange("p (u l) -> p u l", u=4), in1=Ct[:], op=ALU.mult),
                 reads=[bpKK, bCt], writes=[bQm])
            B.op("pool", lambda e: e.tensor_tensor(out=Qp[:], in0=Qm[:], in1=bcm(self.ident_b[:]), op=ALU.add), reads=[bQm] + cbufs, writes=[bQp])
            B.op("dve", lambda e: e.tensor_tensor(out=Kg[:], in0=pKtb[:, 0:512].rearrange("p (u l) -> p u l", u=4), in1=bc3(eg, 128), op=ALU.mult),
                 reads=[bpKt, bgs], writes=[bKg])
            B.op("dve", lambda e: e.tensor_tensor(out=kt[:], in0=pKtb[:, 0:512].rearrange("p (u l) -> p u l", u=4), in1=bc3(ekt, 128), op=ALU.mult),
                 reads=[bpKt, bgs], writes=[bkt])
            yield
            Gc = None
            Hc = None
            for lev in range(7):
                sm = smask[:, d * 7 + lev, :]
                pY, bpY = psr.next()
                for u in range(4):
                    rhsH = self.ident_b[:] if Hc is None else Hc[0][:, u, :]
                    B.op("pe", lambda e, u=u, rhsH=rhsH: e.matmul(pY[:, u * 128:(u + 1) * 128], lhsT=Qp[:, u, :], rhs=rhsH, start=True, stop=True),
                         reads=[bQp] + cbufs + ([] if Hc is None else [Hc[1]]), writes=[bpY], inc=(u == 3))
                B.op("dve", lambda e, sm=sm: e.tensor_tensor(out=IYT[:], in0=pY[:, :].rearrange("p (u l) -> p u l", u=4), in1=bcm(sm), op=ALU.mult),
                     reads=[bpY] + cbufs, writes=[bIYT])
                yield
                Gn = sl["G%d" % (lev % 2)]
                Hn = sl["H%d" % (lev % 2)]
                pG, bpG = psr.next()
                for u in range(4):
                    rhsG = self.ident_b[:] if Gc is None else Gc[0][:, u, :]
                    B.op("pe", lambda e, u=u, rhsG=rhsG: e.matmul(pG[:, u * 128:(u + 1) * 128], lhsT=IYT[:, u, :], rhs=rhsG, start=True, stop=True),
                         reads=[bIYT] + cbufs + ([] if Gc is None else [Gc[1]]), writes=[bpG], inc=(u == 3))
                B.op("act", lambda e, Gn=Gn: e.activation(out=Gn[0][:].rearrange("p u l -> p (u l)"), in_=pG[:, :], func=AF.Identity), reads=[bpG], writes=[Gn[1]])
                if lev < 6:
                    pH, bpH = psr.next()
                    for u in range(4):
                        lhsG = self.ident_b[:] if Gc is None else Gc[0][:, u, :]
                        B.op("pe", lambda e, u=u, lhsG=lhsG: e.matmul(pH[:, u * 128:(u + 1) * 128], lhsT=lhsG, rhs=IYT[:, u, :], start=True, stop=True),
                             reads=[bIYT] + cbufs + ([] if Gc is None else [Gc[1]]), writes=[bpH], inc=(u == 3))
                    B.op("act", lambda e, Hn=Hn: e.activation(out=Hn[0][:].rearrange("p u l -> p (u l)"), in_=pH[:, :], func=AF.Identity), reads=[bpH], writes=[Hn[1]])
                    Hc = Hn
                Gc = Gn
                yield
            G, bG = Gc
            pW, bpW = psr.next()
            for u in range(4):
                B.op("pe", lambda e, u=u: e.matmul(pW[:, u * 128:(u + 1) * 128], lhsT=Kg[:, u, :], rhs=G[:, u, :], start=True, stop=True),
                     reads=[bKg, bG], writes=[bpW], inc=(u == 3))
            B.op("act", lambda e: e.activation(out=negW[:].rearrange("p u l -> p (u l)"), in_=pW[:, :], func=AF.Identity, scale=-1.0), reads=[bpW], writes=[bnegW])
            yield
            sbt, bsb = Sb_cur[d][hg]
            pV, bpV = psr.next()
            for u in range(4):
                B.op("pe", lambda e, u=u: e.matmul(pV[:, u * 128:(u + 1) * 128], lhsT=G[:, u, :], rhs=tv[:, (h0 + u) * 128:(h0 + u + 1) * 128], start=True, stop=False),
                     reads=[bG, bv], writes=[bpV], inc=False)
                B.op("pe", lambda e, u=u: e.matmul(pV[:, u * 128:(u + 1) * 128], lhsT=negW[:, u, :], rhs=sbt[:, u, :], start=False, stop=True),
                     reads=[bnegW, bsb], writes=[bpV], inc=(u == 3))
            B.op("dve", lambda e: e.tensor_tensor(out=vnew[:], in0=pV[:, :].rearrange("p (u l) -> p u l", u=4), in1=bc3(bcol, 128), op=ALU.mult),
                 reads=[bpV, bg], writes=[bvnew])
            yield
            if need_o:
                pO1, bpO1 = psr.next()
                for u in range(4):
                    B.op("pe", lambda e, u=u: e.matmul(pO1[:, u * 128:(u + 1) * 128], lhsT=tq[:, h0 + u, :], rhs=sbt[:, u, :], start=True, stop=True),
                         reads=[bq, bsb], writes=[bpO1], inc=(u == 3))
                o1, bo1 = ring_o1.next()
                B.op("dve", lambda e: e.tensor_tensor(out=o1[:], in0=pO1[:, :].rearrange("p (u l) -> p u l", u=4), in1=bc3(eg, 128), op=ALU.mult),
                     reads=[bpO1, bgs], writes=[bo1])
                pO2, bpO2 = psr.next()
                for u in range(4):
                    B.op("pe", lambda e, u=u: e.matmul(pO2[:, u * 128:(u + 1) * 128], lhsT=attnT[:, u, :], rhs=vnew[:, u, :], start=True, stop=True),
                         reads=[battn, bvnew], writes=[bpO2], inc=(u == 3))
                o, bo = ring_o.next()
                B.op("dve", lambda e: e.tensor_tensor(out=o[:], in0=pO2[:, :].rearrange("p (u l) -> p u l", u=4), in1=o1[:], op=ALU.add),
                     reads=[bpO2, bo1], writes=[bo])
                dst = (self.OF if d == 0 else self.OB)[c - 2]
                B.dma("sp", dst[:, hg * 512:(hg + 1) * 512], o[:].rearrange("p u l -> p (u l)"), bo, reads=[bo])
            pS, bpS = psr.next()
            for u in range(4):
                B.op("pe", lambda e, u=u: e.matmul(pS[:, u * 128:(u + 1) * 128], lhsT=kt[:, u, :], rhs=vnew[:, u, :], start=True, stop=True),
                     reads=[bkt, bvnew], writes=[bpS], inc=(u == 3))
            Sg = S[d][:, h0:h0 + 4, :]
            B.op("pool", lambda e: e.tensor_tensor(out=St[:], in0=Sg, in1=bc3(gte, 128), op=ALU.mult), reads=[bS[d][hg], bgs], writes=[bSt])
            B.op("dve", lambda e: e.tensor_tensor(out=Sg, in0=pS[:, :].rearrange("p (u l) -> p u l", u=4), in1=St[:], op=ALU.add),
                 reads=[bpS, bSt], writes=[bS[d][hg]])
            nsb, bnsb = Sb[d][hg].next()
            B.op("act", lambda e: e.activation(out=nsb[:], in_=Sg, func=AF.Identity), reads=[bS[d][hg]], writes=[bnsb])
            Sb_cur[d][hg] = (nsb, bnsb)
            yield

        self._b_env = dict(data=data, dirc=dirc, cbufs=cbufs, psr=psr, order=order, out_lo=out_lo, out_hi=out_hi, bc3=bc3, bcm=bcm,
                           C=C, bC=bC, Cb=Cb, Cb_cur=Cb_cur, nst=nst, nbf=nbf, n_cur=n_cur, nb_cur=nb_cur, mst=mst, m_cur=m_cur,
                           ring_num=ring_num, ring_h=ring_h)
        ml_group = self.make_ml_group()

        from collections import deque
        pending = deque()
        for step in range(nsteps):
            for d in range(2):
                pending.append(("load", step, d))
            for hg in range(2):
                for d in range(2):
                    if not dbg.get("b_no_gdn"):
                        pending.append(("gdn", step, d, hg))
            for d in range(2):
                if not dbg.get("b_no_ml"):
                    pending.append(("ml", step, d))
        free_g = list(range(NG))
        free_m = list(range(NM))
        done = set()
        active = []
        loaded = set()
        while pending or active:
            while pending:
                it = pending[0]
                if it[0] == "load":
                    _, step, d = it
                    load_step(step, d)
                    shared_pre(step, d)
                    pending.popleft()
                    continue
                if it[0] == "gdn":
                    _, step, d, hg = it
                    key_prev = ("gdn", step - 1, d, hg)
                    if (step > 0 and key_prev not in done) or not free_g:
                        break
                    si = free_g.pop(0)
                    active.append((it, gdn_group(step, d, hg, gslots[si]), ("g", si)))
                    pending.popleft()
                    continue
                if it[0] == "ml":
                    _, step, d = it
                    key_prev = ("ml", step - 1, d)
                    if (step > 0 and key_prev not in done) or not free_m:
                        break
                    si = free_m.pop(0)
                    active.append((it, ml_group(step, d, mslots[si]), ("m", si)))
                    pending.popleft()
                    continue
            for ent in list(active):
                it, gen, (kind, si) = ent
                try:
                    next(gen)
                except StopIteration:
                    active.remove(ent)
                    done.add(it)
                    (free_g if kind == "g" else free_m).append(si)
        B.barrier()
        st.close()

    def make_ml_group(self):
        B = self.B
        env = self._b_env
        data, dirc, cbufs, psr = env["data"], env["dirc"], env["cbufs"], env["psr"]
        bc3, bcm = env["bc3"], env["bcm"]
        C, bC, Cb, Cb_cur = env["C"], env["bC"], env["Cb"], env["Cb_cur"]
        nst, nbf, n_cur, nb_cur, mst, m_cur = env["nst"], env["nbf"], env["n_cur"], env["nb_cur"], env["mst"], env["m_cur"]
        ring_num, ring_h = env["ring_num"], env["ring_h"]
        out_lo, out_hi = env["out_lo"], env["out_hi"]

        def bc3n(ap2, n):
            return ap2.unsqueeze(2).to_broadcast([128, ap2.shape[1], n])

        def ml_group(step, d, sl):
            dd = data[(step, d)]
            dc = dirc[d]
            c = dd["c"]
            need_o = out_lo <= c < out_hi
            tg, bg = dd["GT"]; gs, bgs = dd["gs"]
            mq, bmq = dd["MQ"]; mk, bmk = dd["MK"]; mv, bmv = dd["MV"]
            X, bX = sl["X"]; Y, bY = sl["Y"]; Pm, bPm = sl["Pm"]; PT, bPT = sl["PT"]; Kw, bKw = sl["Kw"]
            sm, bsm = sl["sm"]; Ct, bCt = sl["Ct"]
            bcc = gs[:, 8:12]
            blast = gs[:, 20:24]
            cvec = gs[:, 48:52]
            mprev, bmprev = m_cur[d]
            B.op("pool", lambda e: e.tensor_tensor(out=X[:], in0=bcm(self.ident_f[:]), in1=bc3(cvec, 128), op=ALU.mult), reads=[bgs] + cbufs, writes=[bX])
            pC, bpC = psr.next()
            for u in range(4):
                B.op("pe", lambda e, u=u: e.matmul(pC[:, u * 128:(u + 1) * 128], lhsT=self.ones_f[:], rhs=X[:, u, :], start=True, stop=True),
                     reads=[bX] + cbufs, writes=[bpC], inc=(u == 3))
            B.op("dve", lambda e: e.tensor_tensor(out=Y[:], in0=pC[:, :].rearrange("p (u l) -> p u l", u=4), in1=bc3(bcc, 128), op=ALU.add),
                 reads=[bpC, bgs], writes=[bY])
            B.op("pool", lambda e: e.tensor_tensor(out=Y[:], in0=Y[:], in1=bcm(dc["MB"][:]), op=ALU.add), reads=[bY] + cbufs, writes=[bY])
            B.op("dve", lambda e: e.tensor_reduce(out=sm[:, 0:4], in_=Y[:], axis=AX.X, op=ALU.max), reads=[bY], writes=[bsm])
            B.op("dve", lambda e: e.tensor_tensor(out=sm[:, 4:8], in0=bcc, in1=mprev[:, 0:4], op=ALU.add), reads=[bgs, bmprev], writes=[bsm])
            B.op("dve", lambda e: e.tensor_tensor(out=sm[:, 8:12], in0=sm[:, 0:4], in1=sm[:, 4:8], op=ALU.max), reads=[bsm], writes=[bsm])
            B.op("dve", lambda e: e.tensor_scalar(out=sm[:, 12:16], in0=sm[:, 8:12], scalar1=-1.0, scalar2=None, op0=ALU.mult), reads=[bsm], writes=[bsm])
            if need_o:
                pQK, bpQK = psr.next()
                for u in range(4):
                    B.op("pe", lambda e, u=u: e.matmul(pQK[:, u * 128:(u + 1) * 128], lhsT=mq[:, u, :], rhs=mk[:, u, :], start=True, stop=True),
                         reads=[bmq, bmk], writes=[bpQK], inc=(u == 3))
                for u in range(4):
                    B.op("act", lambda e, u=u: e.activation(out=X[:, u, :], in_=Y[:, u, :], func=AF.Exp, bias=sm[:, 12 + u:13 + u]), reads=[bY, bsm], writes=[bX])
                B.op("dve", lambda e: e.tensor_tensor(out=Pm[:], in0=pQK[:, :].rearrange("p (u l) -> p u l", u=4), in1=X[:], op=ALU.mult),
                     reads=[bpQK, bX], writes=[bPm])
            yield
            if need_o:
                pT, bpT = psr.next()
                pTb = pT[:].bitcast(BF16)
                for u in range(4):
                    B.op("pe", lambda e, u=u: e.transpose(pTb[:, u * 128:(u + 1) * 128], Pm[:, u, :], self.ident_b[:]), reads=[bPm] + cbufs, writes=[bpT], inc=(u == 3))
                B.op("act", lambda e: e.activation(out=PT[:].rearrange("p u l -> p (u l)"), in_=pTb[:, 0:512], func=AF.Identity), reads=[bpT], writes=[bPT])
                B.op("dve", lambda e: e.tensor_tensor(out=sm[:, 16:20], in0=sm[:, 4:8], in1=sm[:, 8:12], op=ALU.subtract), reads=[bsm], writes=[bsm])
                B.op("act", lambda e: e.activation(out=sm[:, 16:20], in_=sm[:, 16:20], func=AF.Exp), reads=[bsm], writes=[bsm])
                B.op("act", lambda e: e.activation(out=sm[:, 20:24], in_=sm[:, 12:16], func=AF.Exp), reads=[bsm], writes=[bsm])
                yield
                cbt, bcb = Cb_cur[d]
                nbt, bnb = nb_cur[d]
                num, bnum = ring_num.next()
                hh, bhh = ring_h.next()
                for pr in range(2):
                    pN1, bpN1 = psr.next()
                    pN2, bpN2 = psr.next()
                    for uu in range(2):
                        u = pr * 2 + uu
                        B.op("pe", lambda e, u=u, uu=uu, pN1=pN1: e.matmul(pN1[:, uu * 256:(uu + 1) * 256], lhsT=mq[:, u, :], rhs=cbt[:, u, :], start=True, stop=True),
                             reads=[bmq, bcb], writes=[bpN1], inc=(uu == 1))
                    for uu in range(2):
                        u = pr * 2 + uu
                        B.op("pe", lambda e, u=u, uu=uu, pN2=pN2: e.matmul(pN2[:, uu * 256:(uu + 1) * 256], lhsT=PT[:, u, :], rhs=mv[:, u * 256:(u + 1) * 256], start=True, stop=True),
                             reads=[bPT, bmv], writes=[bpN2], inc=(uu == 1))
                    B.op("dve", lambda e, pr=pr, pN1=pN1: e.tensor_tensor(out=num[:, pr * 2:pr * 2 + 2, :], in0=pN1[:, :].rearrange("p (u l) -> p u l", u=2),
                                                                         in1=bc3n(sm[:, 16 + pr * 2:18 + pr * 2], 256), op=ALU.mult), reads=[bpN1, bsm], writes=[bnum])
                    B.op("dve", lambda e, pr=pr, pN2=pN2: e.tensor_tensor(out=num[:, pr * 2:pr * 2 + 2, :], in0=pN2[:, :].rearrange("p (u l) -> p u l", u=2),
                                                                         in1=num[:, pr * 2:pr * 2 + 2, :], op=ALU.add), reads=[bpN2, bnum], writes=[bnum])
                pDn, bpDn = psr.next()
                for u in range(4):
                    B.op("pe", lambda e, u=u: e.matmul(pDn[:, u:u + 1], lhsT=mq[:, u, :], rhs=nbt[:, u:u + 1], start=True, stop=True), reads=[bmq, bnb], writes=[bpDn], inc=False)
                for u in range(4):
                    B.op("pe", lambda e, u=u: e.matmul(pDn[:, 4 + u:5 + u], lhsT=PT[:, u, :], rhs=self.ones_b[:, 0:1], start=True, stop=True),
                         reads=[bPT] + cbufs, writes=[bpDn], inc=(u == 3))
                B.op("dve", lambda e: e.tensor_tensor(out=sm[:, 24:28], in0=pDn[:, 0:4], in1=sm[:, 16:20], op=ALU.mult), reads=[bpDn, bsm], writes=[bsm])
                B.op("dve", lambda e: e.tensor_tensor(out=sm[:, 24:28], in0=pDn[:, 4:8], in1=sm[:, 24:28], op=ALU.add), reads=[bpDn, bsm], writes=[bsm])
                B.op("dve", lambda e: e.tensor_tensor(out=sm[:, 24:28], in0=sm[:, 24:28], in1=sm[:, 24:28], op=ALU.mult), reads=[bsm], writes=[bsm])
                B.op("dve", lambda e: e.tensor_tensor(out=sm[:, 28:32], in0=sm[:, 20:24], in1=sm[:, 20:24], op=ALU.mult), reads=[bsm], writes=[bsm])
                B.op("dve", lambda e: e.tensor_tensor(out=sm[:, 24:28], in0=sm[:, 24:28], in1=sm[:, 28:32], op=ALU.max), reads=[bsm], writes=[bsm])
                B.op("pool", lambda e: e.tensor_tensor(out=sm[:, 28:32], in0=sm[:, 24:28], in1=self.nhalf[:, 0:4], op=ALU.pow), reads=[bsm] + cbufs, writes=[bsm])
                B.op("dve", lambda e: e.tensor_tensor(out=hh[:], in0=num[:], in1=bc3n(sm[:, 28:32], 256), op=ALU.mult), reads=[bnum, bsm], writes=[bhh])
                dst = (self.HF if d == 0 else self.HB)[c - 2]
                B.dma("sp", dst[:, :], hh[:].rearrange("p u l -> p (u l)"), bhh, reads=[bhh])
                yield
            pSel, bpSel = psr.next()
            B.op("pe", lambda e: e.matmul(pSel[:, 0:4], lhsT=dc["SEL"][:], rhs=sm[:, 8:12], start=True, stop=True), reads=[bsm] + cbufs, writes=[bpSel])
            mnew, bmnew = mst[d].next()
            B.op("act", lambda e: e.activation(out=mnew[:], in_=pSel[:, 0:4], func=AF.Identity), reads=[bpSel], writes=[bmnew])
            B.op("dve", lambda e: e.tensor_tensor(out=sm[:, 32:36], in0=cvec, in1=blast, op=ALU.add), reads=[bgs], writes=[bsm])
            B.op("dve", lambda e: e.tensor_tensor(out=sm[:, 32:36], in0=sm[:, 32:36], in1=mnew[:], op=ALU.subtract), reads=[bsm, bmnew], writes=[bsm])
            B.op("dve", lambda e: e.tensor_tensor(out=sm[:, 36:40], in0=blast, in1=mprev[:, 0:4], op=ALU.add), reads=[bgs, bmprev], writes=[bsm])
            B.op("dve", lambda e: e.tensor_tensor(out=sm[:, 36:40], in0=sm[:, 36:40], in1=mnew[:], op=ALU.subtract), reads=[bsm, bmnew], writes=[bsm])
            B.op("act", lambda e: e.activation(out=sm[:, 32:40], in_=sm[:, 32:40], func=AF.Exp), reads=[bsm], writes=[bsm])
            pKt, bpKt = psr.next()
            pKtb = pKt[:].bitcast(BF16)
            for u in range(4):
                B.op("pe", lambda e, u=u: e.transpose(pKtb[:, u * 128:(u + 1) * 128], mk[:, u, :], self.ident_b[:]), reads=[bmk] + cbufs, writes=[bpKt], inc=(u == 3))
            B.op("dve", lambda e: e.tensor_tensor(out=Kw[:], in0=pKtb[:, 0:512].rearrange("p (u l) -> p u l", u=4), in1=bc3(sm[:, 32:36], 128), op=ALU.mult),
                 reads=[bpKt, bsm], writes=[bKw])
            m_cur[d] = (mnew, bmnew)
            yield
            for pr in range(2):
                pC2, bpC2 = psr.next()
                for uu in range(2):
                    u = pr * 2 + uu
                    B.op("pe", lambda e, u=u, uu=uu, pC2=pC2: e.matmul(pC2[:, uu * 256:(uu + 1) * 256], lhsT=Kw[:, u, :], rhs=mv[:, u * 256:(u + 1) * 256], start=True, stop=True),
                         reads=[bKw, bmv], writes=[bpC2], inc=(uu == 1))
                B.op("pool", lambda e, pr=pr: e.tensor_tensor(out=Ct[:, pr * 2:pr * 2 + 2, :], in0=C[d][:, pr * 2:pr * 2 + 2, :], in1=bc3n(sm[:, 36 + pr * 2:38 + pr * 2], 256), op=ALU.mult),
                     reads=[bC[d], bsm], writes=[bCt])
                B.op("dve", lambda e, pr=pr, pC2=pC2: e.tensor_tensor(out=C[d][:, pr * 2:pr * 2 + 2, :], in0=pC2[:, :].rearrange("p (u l) -> p u l", u=2), in1=Ct[:, pr * 2:pr * 2 + 2, :], op=ALU.add),
                     reads=[bpC2, bCt], writes=[bC[d]])
            pN, bpN = psr.next()
            for u in range(4):
                B.op("pe", lambda e, u=u: e.matmul(pN[:, u:u + 1], lhsT=Kw[:, u, :], rhs=self.ones_b[:, 0:1], start=True, stop=True), reads=[bKw] + cbufs, writes=[bpN], inc=(u == 3))
            nold, bnold = n_cur[d]
            nnew, bnnew = nst[d].next()
            B.op("dve", lambda e: e.tensor_tensor(out=nnew[:, 4:8], in0=nold[:, 0:4], in1=sm[:, 36:40], op=ALU.mult), reads=[bnold, bsm], writes=[bnnew])
            B.op("dve", lambda e: e.tensor_tensor(out=nnew[:, 0:4], in0=pN[:, 0:4], in1=nnew[:, 4:8], op=ALU.add), reads=[bpN, bnnew], writes=[bnnew])
            nbn, bnbn = nbf[d].next()
            B.op("act", lambda e: e.activation(out=nbn[:], in_=nnew[:, 0:4], func=AF.Identity), reads=[bnnew], writes=[bnbn])
            cbn, bcbn = Cb[d].next()
            B.op("act", lambda e: e.activation(out=cbn[:], in_=C[d][:], func=AF.Identity), reads=[bC[d]], writes=[bcbn])
            n_cur[d] = (nnew, bnnew)
            nb_cur[d] = (nbn, bnbn)
            Cb_cur[d] = (cbn, bcbn)
            yield

        return ml_group

    def phaseC1(self):
        B, nc, inp = self.B, self.nc, self.inp
        st = ExitStack()
        dbg = self.debug
        wo, bwo = self.load_w_bf16(st, "c_wo", inp["w_o"], 8, 4096, 8)
        wbg, bwbg = self.load_w_bf16(st, "c_wbg", inp["w_bg"], 8, 1024, 2)
        wbm, bwbm = self.load_w_bf16(st, "c_wbm", inp["w_bm"], 8, 1024, 2)
        wout, bwout = self.load_w_bf16(st, "c_wout", inp["w_out"], 8, 1024, 2)
        nwb = B.sb(st, "c_nwb", [128, 2, 1024], F32)
        bnwb = Buf("c_nwb")
        B.dma("sp", nwb[:, 0, :], inp["gnw_bc"][:, :], bnwb, writes=[bnwb])
        B.dma("sp", nwb[:, 1, :], inp["mnw_bc"][:, :], bnwb, writes=[bnwb])
        zero = B.sb(st, "c_zero", [64, 1024], F32)
        bzero = Buf("c_zero")
        B.op("pool", lambda e: e.memset(zero[:], 0.0), writes=[bzero])
        bX1 = Buf("X1")
        B.dma("sp", self.X1[0:64, :], zero[:], bzero, reads=[bzero], writes=[bX1])
        NS = 2
        xt = B.sb(st, "c_x", [128, NS, 1024], F32)
        bxts = [Buf("c_x%d" % i) for i in range(NS)]
        xn = B.sb(st, "c_xn", [128, NS, 1024], BF16); bxn = Buf("c_xn")
        sq = B.sb(st, "c_sq", [128, 24], F32); bsq = Buf("c_sq")
        junk = None; bjunk = None
        hxT = B.sb(st, "c_hxT", [128, 8, NS * 128], BF16); bhx = Buf("c_hxT")
        oa = B.sb(st, "c_oa", [128, 1024], F32); boa = Buf("c_oa")
        ob = B.sb(st, "c_ob", [128, 1024], F32); bob = Buf("c_ob")
        gt = B.sb(st, "c_gt", [128, 1024], F32); bgt = Buf("c_gt")
        osq = B.sb(st, "c_osq", [128, 1024], F32); bosq = Buf("c_osq")
        sm = B.sb(st, "c_sm", [128, 32], F32); bsm = Buf("c_sm")
        og = B.sb(st, "c_og", [128, 1024], BF16); bog = Buf("c_og")
        brT = [B.sb(st, "c_brT%d" % i, [128, 8, NS * 128], BF16) for i in range(2)]
        bbrT = [Buf("c_brT%d" % i) for i in range(2)]
        sg = Ring(B, st, "c_sg", 4, [128, NS * 128], F32)
        yt = Ring(B, st, "c_yt", 2, [128, NS * 128], F32)
        mT = B.sb(st, "c_mT", [128, 8, NS * 128], BF16); bmT = Buf("c_mT")
        tmp = Ring(B, st, "c_tmp", 2, [128, 512], F32)
        ptr = Ring(B, st, "c_ptr", 2, [128, 512], F32, psum=True)
        pmm = Ring(B, st, "c_pmm", 5, [128, 512], F32, psum=True)
        nt = OWN_T // 128
        sts = []
        i = 0
        while i < nt:
            ns = min(NS, nt - i)
            sts.append((i, ns))
            i += ns
        if dbg.get("c1_tiles"):
            sts = sts[: dbg["c1_tiles"]]
        for (t0, ns) in sts:
            n = ns * 128
            for s in range(ns):
                B.dma("sp", xt[:, s, :], inp["x"][(t0 + s) * 128:(t0 + s + 1) * 128, :], bxts[s], writes=[bxts[s]])
            self.norm_transpose(xt, bxts, ns, xn, bxn, sq, bsq, junk, bjunk, ptr, hxT, bhx, 0, 0)
            for br in range(2):
                nh, hd = (8, 128) if br == 0 else (4, 256)
                srcf, srcb = (self.OF, self.OB) if br == 0 else (self.HF, self.HB)
                for s in range(ns):
                    c = t0 + s
                    B.dma("sp", oa[:], srcf[c], boa, writes=[boa])
                    B.dma("sp", ob[:], srcb[c], bob, writes=[bob])
                    for hf in range(2):
                        p, bp = pmm.next()
                        for k in range(8):
                            B.op("pe", lambda e, k=k, p=p, s=s, hf=hf, br=br: e.matmul(
                                p[:], lhsT=hxT[:, k, s * 128:(s + 1) * 128], rhs=wo[:, k, br * 1024 + hf * 512: br * 1024 + (hf + 1) * 512],
                                start=(k == 0), stop=(k == 7)), reads=[bhx, bwo], writes=[bp], inc=(k == 7))
                        B.op("act", lambda e, p=p, hf=hf, br=br: e.activation(out=gt[:, hf * 512:(hf + 1) * 512], in_=p[:],
                                                                            func=(AF.Silu if br == 0 else AF.Sigmoid)), reads=[bp], writes=[bgt])
                    B.op("pool", lambda e, br=br: e.tensor_tensor(out=gt[:], in0=gt[:], in1=nwb[:, br, :], op=ALU.mult), reads=[bgt, bnwb], writes=[bgt])
                    B.op("dve", lambda e: e.tensor_tensor(out=oa[:], in0=oa[:], in1=ob[:], op=ALU.add), reads=[boa, bob], writes=[boa])
                    B.op("pool", lambda e: e.tensor_tensor(out=osq[:], in0=oa[:], in1=oa[:], op=ALU.mult), reads=[boa], writes=[bosq])
                    B.op("dve", lambda e, nh=nh: e.tensor_reduce(out=sm[:, 0:nh], in_=osq[:].rearrange("p (h e) -> p h e", h=nh), axis=AX.X, op=ALU.add),
                         reads=[bosq], writes=[bsm])
                    B.op("dve", lambda e, nh=nh, hd=hd: e.tensor_scalar(out=sm[:, 8:8 + nh], in0=sm[:, 0:nh], scalar1=float(1.0 / hd), scalar2=float(EPS),
                                                                      op0=ALU.mult, op1=ALU.add), reads=[bsm], writes=[bsm])
                    B.op("pool", lambda e, nh=nh: e.tensor_tensor(out=sm[:, 16:16 + nh], in0=sm[:, 8:8 + nh], in1=self.nhalf[:, 0:nh], op=ALU.pow),
                         reads=[bsm, self.cb], writes=[bsm])
                    B.op("dve", lambda e, nh=nh, hd=hd: e.tensor_tensor(out=osq[:].rearrange("p (h e) -> p h e", h=nh), in0=oa[:].rearrange("p (h e) -> p h e", h=nh),
                                                                      in1=sm[:, 16:16 + nh].unsqueeze(2).to_broadcast([128, nh, hd]), op=ALU.mult),
                         reads=[boa, bsm], writes=[bosq])
                    B.op("dve", lambda e: e.tensor_tensor(out=og[:], in0=osq[:], in1=gt[:], op=ALU.mult), reads=[bosq, bgt], writes=[bog])
                    p, bp = ptr.next()
                    pb = p[:].bitcast(BF16)
                    for k in range(8):
                        B.op("pe", lambda e, k=k, pb=pb: e.transpose(pb[:, k * 128:(k + 1) * 128], og[:, k * 128:(k + 1) * 128], self.ident_b[:]),
                             reads=[bog, self.cb], writes=[bp], inc=(k == 7))
                    B.op("act", lambda e, pb=pb, s=s, br=br: e.activation(out=brT[br][:, :, s * 128:(s + 1) * 128], in_=pb[:, 0:1024].rearrange("p (k t) -> p k t", k=8),
                                                                          func=AF.Identity), reads=[bp], writes=[bbrT[br]])
            for ncn in range(8):
                sgs = []
                for gi in range(2):
                    p, bp = pmm.next()
                    for k in range(8):
                        B.op("pe", lambda e, k=k, p=p, gi=gi, ncn=ncn: e.matmul(p[:, 0:n], lhsT=wo[:, k, 2048 + gi * 1024 + ncn * 128: 2048 + gi * 1024 + (ncn + 1) * 128],
                                                                               rhs=hxT[:, k, 0:n], start=(k == 0), stop=(k == 7)), reads=[bhx, bwo], writes=[bp], inc=(k == 7))
                    g_, bg_ = sg.next()
                    B.op("act", lambda e, p=p, g_=g_: e.activation(out=g_[:, 0:n], in_=p[:, 0:n], func=AF.Sigmoid), reads=[bp], writes=[bg_])
                    sgs.append((g_, bg_))
                ys = []
                for br, (w, bw) in enumerate(((wbg, bwbg), (wbm, bwbm))):
                    p, bp = pmm.next()
                    for k in range(8):
                        B.op("pe", lambda e, k=k, p=p, w=w, br=br, ncn=ncn: e.matmul(p[:, 0:n], lhsT=w[:, k, ncn * 128:(ncn + 1) * 128], rhs=brT[br][:, k, 0:n],
                                                                                    start=(k == 0), stop=(k == 7)), reads=[bbrT[br], bw], writes=[bp], inc=(k == 7))
                    ys.append((p, bp))
                y_, by_ = yt.next()
                B.op("dve", lambda e, y_=y_: e.tensor_tensor(out=y_[:, 0:n], in0=ys[0][0][:, 0:n], in1=sgs[0][0][:, 0:n], op=ALU.mult), reads=[ys[0][1], sgs[0][1]], writes=[by_])
                g1, bg1 = sgs[1]
                B.op("dve", lambda e, g1=g1: e.tensor_tensor(out=g1[:, 0:n], in0=ys[1][0][:, 0:n], in1=g1[:, 0:n], op=ALU.mult), reads=[ys[1][1], bg1], writes=[bg1])
                B.op("pool", lambda e, y_=y_, g1=g1, ncn=ncn: e.tensor_tensor(out=mT[:, ncn, 0:n], in0=y_[:, 0:n], in1=g1[:, 0:n], op=ALU.add), reads=[by_, bg1], writes=[bmT])
            for s in range(ns):
                for hf in range(2):
                    p, bp = pmm.next()
                    for k in range(8):
                        B.op("pe", lambda e, k=k, p=p, s=s, hf=hf: e.matmul(p[:], lhsT=mT[:, k, s * 128:(s + 1) * 128], rhs=wout[:, k, hf * 512:(hf + 1) * 512],
                                                                           start=(k == 0), stop=(k == 7)), reads=[bmT, bwout], writes=[bp], inc=(k == 7))
                    t_, bt_ = tmp.next()
                    B.op("dve", lambda e, p=p, t_=t_, hf=hf: e.tensor_tensor(out=t_[:], in0=p[:], in1=self.gate_bc[:, 0, hf * 512:(hf + 1) * 512], op=ALU.mult),
                         reads=[bp, self.bgate], writes=[bt_])
                    B.op("pool", lambda e, t_=t_, s=s, hf=hf: e.tensor_tensor(out=xt[:, s, hf * 512:(hf + 1) * 512], in0=xt[:, s, hf * 512:(hf + 1) * 512], in1=t_[:], op=ALU.add),
                         reads=[bt_, bxts[s]], writes=[bxts[s]])
                B.dma("sp", self.X1[64 + (t0 + s) * 128: 64 + (t0 + s + 1) * 128, :], xt[:, s, :], bxts[s], reads=[bxts[s]], writes=[bX1])
        B.barrier()
        st.close()

    def precast_wup(self):
        B = self.B
        self.WUPB = B.dram("WUPB", [44, 128, 8, 128], BF16)
        self.bwupb = Buf("WUPB")
        src = self.inp["w_up"].rearrange("(k p) (c j) -> c p k j", p=128, j=128)
        for c in range(44):
            B.dma("pool", self.WUPB[c], src[c], self.bwupb, writes=[self.bwupb])

    def phaseC2(self):
        B, nc, inp = self.B, self.nc, self.inp
        st = ExitStack()
        dbg = self.debug
        wd = B.sb(st, "d_wd", [128, 22, 1024], BF16)
        bwd = Buf("d_wd")
        wdv = inp["w_down"].rearrange("(c p) n -> p c n", p=128)
        for i in range(0, 22, 6):
            j = min(22, i + 6)
            B.dma("pool", wd[:, i:j, :], wdv[:, i:j, :], bwd, writes=[bwd])
        cw = B.sb(st, "d_cw", [128, 44, 9], F32)
        nob = B.sb(st, "d_nob", [128, 1024], F32)
        bsm0 = Buf("d_small")
        B.dma("sp", cw[:], inp["ffn_cw"][:, :, :], bsm0, writes=[bsm0])
        B.dma("sp", nob[:], inp["now_bc"][:, :], bsm0, writes=[bsm0])
        wup = Ring(B, st, "d_wup", 3, [128, 2, 8, 128], BF16)
        xt = B.sb(st, "d_x", [128, 5, 1024], F32)
        bxts = [Buf("d_x%d" % i) for i in range(5)]
        xn = B.sb(st, "d_xn", [128, 5, 1024], BF16); bxn = Buf("d_xn")
        sq = B.sb(st, "d_sq", [128, 24], F32); bsq = Buf("d_sq")
        junk = B.sb(st, "d_junk", [128, 1024], BF16); bjunk = Buf("d_junk")
        hxT = B.sb(st, "d_hxT", [128, 8, 640], BF16); bhx = Buf("d_hxT")
        upad = Ring(B, st, "d_up", 4, [128, 10, 66], F32)
        acc = Ring(B, st, "d_acc", 4, [128, 8, 64], F32)
        sgt = Ring(B, st, "d_sg", 2, [128, 512], F32)
        ctmp = Ring(B, st, "d_ctmp", 3, [128, 8, 64], F32)
        aT = B.sb(st, "d_aT", [128, 22, 512], BF16); baT = Buf("d_aT")
        xo = B.sb(st, "d_xo", [128, 4, 1024], F32)
        bxo = [Buf("d_xo%d" % i) for i in range(4)]
        t2 = Ring(B, st, "d_t2", 2, [128, 512], F32)
        sq2 = B.sb(st, "d_sq2", [128, 16], F32); bsq2 = Buf("d_sq2")
        ptr = Ring(B, st, "d_ptr", 2, [128, 512], F32, psum=True)
        pu = Ring(B, st, "d_pu", 4, [128, 512], F32, psum=True)
        pd = Ring(B, st, "d_pd", 2, [128, 512], F32, psum=True)
        for (u_, bu_) in upad.slots:
            B.op("pool", lambda e, u_=u_: e.memset(u_[:], 0.0), writes=[bu_])
        nblk = dbg.get("c2_blocks", 8)
        bX1 = Buf("X1r")
        for j in range(nblk):
            r0 = 512 * j
            for s in range(5):
                B.dma("sp", xt[:, s, :], self.X1[r0 + s * 128: r0 + (s + 1) * 128, :], bxts[s], writes=[bxts[s]])
            for s in range(4):
                B.dma("sp", xo[:, s, :], self.X1[r0 + 64 + s * 128: r0 + 64 + (s + 1) * 128, :], bxo[s], writes=[bxo[s]])
            self.norm_transpose(xt, bxts, 5, xn, bxn, sq, bsq, junk, bjunk, ptr, hxT, bhx, 0, 4)
            for c in range(22):
                w, bw = wup.next()
                B.dma("sp", w[:, 0], self.WUPB[c], bw, reads=[self.bwupb], writes=[bw])
                B.dma("sp", w[:, 1], self.WUPB[22 + c], bw, reads=[self.bwupb], writes=[bw])
                accs = []
                for part in range(2):
                    ch = c + 22 * part
                    p1, bp1 = pu.next()
                    p2, bp2 = pu.next()
                    for k in range(8):
                        B.op("pe", lambda e, k=k, p1=p1, w=w, part=part: e.matmul(p1[:], lhsT=w[:, part, k, :], rhs=hxT[:, k, 0:512], start=(k == 0), stop=(k == 7)),
                             reads=[bw, bhx], writes=[bp1], inc=(k == 7))
                    for k in range(8):
                        B.op("pe", lambda e, k=k, p2=p2, w=w, part=part: e.matmul(p2[:, 0:128], lhsT=w[:, part, k, :], rhs=hxT[:, k, 512:640], start=(k == 0), stop=(k == 7)),
                             reads=[bw, bhx], writes=[bp2], inc=(k == 7))
                    u_, bu_ = upad.next()
                    B.op("act", lambda e, u_=u_, p1=p1: e.activation(out=u_[:, 0:8, 1:65], in_=p1[:].rearrange("p (r c) -> p r c", c=64), func=AF.Identity),
                         reads=[bp1], writes=[bu_])
                    B.op("act", lambda e, u_=u_, p2=p2: e.activation(out=u_[:, 8:10, 1:65], in_=p2[:, 0:128].rearrange("p (r c) -> p r c", c=64), func=AF.Identity),
                         reads=[bp2], writes=[bu_])
                    if j == 0:
                        B.op("pool", lambda e, u_=u_: e.memset(u_[:, 0:1, :], 0.0), writes=[bu_])
                    a_, ba_ = acc.next()
                    first = True
                    for dr in range(3):
                        for dc_ in range(3):
                            wsc = cw[:, ch, dr * 3 + dc_: dr * 3 + dc_ + 1]
                            src = u_[:, dr:dr + 8, dc_:dc_ + 64]
                            if part == 0:
                                if first:
                                    B.op("dve", lambda e, a_=a_, src=src, wsc=wsc: e.tensor_scalar(out=a_[:], in0=src, scalar1=wsc, scalar2=None, op0=ALU.mult),
                                         reads=[bu_, bsm0], writes=[ba_])
                                else:
                                    B.op("dve", lambda e, a_=a_, src=src, wsc=wsc: e.scalar_tensor_tensor(out=a_[:], in0=src, scalar=wsc, in1=a_[:], op0=ALU.mult, op1=ALU.add),
                                         reads=[bu_, bsm0, ba_], writes=[ba_])
                            else:
                                if first:
                                    B.op("act", lambda e, a_=a_, src=src, wsc=wsc: e.activation(out=a_[:], in_=src, func=AF.Identity, scale=wsc), reads=[bu_, bsm0], writes=[ba_])
                                else:
                                    c_, bc_ = ctmp.next()
                                    B.op("act", lambda e, c_=c_, src=src, wsc=wsc: e.activation(out=c_[:], in_=src, func=AF.Identity, scale=wsc), reads=[bu_, bsm0], writes=[bc_])
                                    B.op("pool", lambda e, a_=a_, c_=c_: e.tensor_tensor(out=a_[:], in0=a_[:], in1=c_[:], op=ALU.add), reads=[ba_, bc_], writes=[ba_])
                            first = False
                    accs.append((a_, ba_))
                s_, bs_ = sgt.next()
                B.op("act", lambda e, s_=s_: e.activation(out=s_[:], in_=accs[0][0][:].rearrange("p r c -> p (r c)"), func=AF.Silu), reads=[accs[0][1]], writes=[bs_])
                B.op("dve", lambda e, s_=s_, c=c: e.tensor_tensor(out=aT[:, c, :], in0=s_[:], in1=accs[1][0][:].rearrange("p r c -> p (r c)"), op=ALU.mult),
                     reads=[bs_, accs[1][1]], writes=[baT])
            for s in range(4):
                for hf in range(2):
                    p, bp = pd.next()
                    for c in range(22):
                        B.op("pe", lambda e, c=c, p=p, s=s, hf=hf: e.matmul(p[:], lhsT=aT[:, c, s * 128:(s + 1) * 128], rhs=wd[:, c, hf * 512:(hf + 1) * 512],
                                                                           start=(c == 0), stop=(c == 21)), reads=[baT, bwd], writes=[bp], inc=(c == 21))
                    t_, bt_ = t2.next()
                    B.op("dve", lambda e, p=p, t_=t_, hf=hf: e.tensor_tensor(out=t_[:], in0=p[:], in1=self.gate_bc[:, 1, hf * 512:(hf + 1) * 512], op=ALU.mult),
                         reads=[bp, self.bgate], writes=[bt_])
                    B.op("dve", lambda e, t_=t_, s=s, hf=hf: e.tensor_tensor(out=xo[:, s, hf * 512:(hf + 1) * 512], in0=xo[:, s, hf * 512:(hf + 1) * 512], in1=t_[:], op=ALU.add),
                         reads=[bt_, bxo[s]], writes=[bxo[s]])
                B.op("act", lambda e, s=s: e.activation(out=junk[:], in_=xo[:, s, :], func=AF.Square, accum_out=sq2[:, s:s + 1]), reads=[bxo[s]], writes=[bjunk, bsq2])
                B.op("dve", lambda e, s=s: e.tensor_scalar(out=sq2[:, 4 + s:5 + s], in0=sq2[:, s:s + 1], scalar1=float(D * EPS), scalar2=None, op0=ALU.add), reads=[bsq2], writes=[bsq2])
                B.op("pool", lambda e, s=s: e.tensor_tensor(out=sq2[:, 8 + s:9 + s], in0=sq2[:, 4 + s:5 + s], in1=self.nhalf[:, 0:1], op=ALU.pow), reads=[bsq2, self.cb], writes=[bsq2])
                B.op("dve", lambda e, s=s: e.scalar_tensor_tensor(out=xo[:, s, :], in0=xo[:, s, :], scalar=sq2[:, 8 + s:9 + s], in1=nob[:], op0=ALU.mult, op1=ALU.mult),
                     reads=[bxo[s], bsq2, bsm0], writes=[bxo[s]])
                B.op("act", lambda e, s=s: e.activation(out=xo[:, s, :], in_=xo[:, s, :], func=AF.Identity, scale=32.0), reads=[bxo[s]], writes=[bxo[s]])
                B.dma("sp", self.out[j * 512 + s * 128: j * 512 + (s + 1) * 128, :], xo[:, s, :], bxo[s], reads=[bxo[s]])
        B.barrier()
        st.close()


def build_program(debug=None):
    P = Prog(debug=debug)
    P.precast_wup()
    P.phase0()
    P.phaseA()
    P.phaseB()
    P.phaseC1()
    P.phaseC2()
    P.top.close()
    return P.B.finish(), P


_CACHE = {}


def kernel(**inputs):
    inp = {k: np.asarray(v) for k, v in inputs.items()}
    if "nc" not in _CACHE:
        _CACHE["nc"] = build_program()[0]
    nc = _CACHE["nc"]
    in_maps = [prep_core(inp, core) for core in range(8)]
    res = run_bass_kernel_spmd(nc, in_maps, core_ids=list(range(8)))
    out = np.empty((4, T, D), np.float32)
    for core in range(8):
        o = np.asarray(res.results[core]["out"], np.float32)
        b = core // 2
        if core % 2 == 0:
            out[b, 0:4096] = o
        else:
            out[b, 4096:8192] = o[::-1]
    return out
```

```python
import numpy as np
from contextlib import ExitStack

import concourse.bass as bass
import concourse.mybir as mybir
from concourse.bass_utils import run_bass_kernel_spmd

F32 = mybir.dt.float32
BF16 = mybir.dt.bfloat16
AF = mybir.ActivationFunctionType
ALU = mybir.AluOpType
AX = mybir.AxisListType

D = 1024
T = 8192
TC = 256
KD = 8
EPS = 1e-6
NEG = -1.0e30


class Buf:
    __slots__ = ("name", "w", "r", "dsem")

    def __init__(self, name):
        self.name = name
        self.w = None
        self.r = {}
        self.dsem = None


class Builder:
    def __init__(self):
        self.nc = bass.Bass("TRN2", target_bir_lowering=False)
        nc = self.nc
        self.es = ExitStack()
        self.es.enter_context(nc.allow_low_precision("bf16 matmul operands, fp32 accumulation"))
        self.engs = {"pe": nc.tensor, "act": nc.scalar, "dve": nc.vector, "pool": nc.gpsimd, "sp": nc.sync}
        self.sems = {}
        self.cnt = {}
        self.seen = {e: {} for e in self.engs}
        for e in self.engs:
            self.sems[e] = self.es.enter_context(nc.semaphore("s_" + e))
            self.cnt[e] = 0
        self.ndsem = 0
        self.nins = 0

    def sb(self, stack, name, shape, dt):
        return stack.enter_context(self.nc.sbuf_tensor(name, list(shape), dt))

    def ps(self, stack, name, shape, dt=F32):
        return stack.enter_context(self.nc.psum_tensor(name, list(shape), dt))

    def dram(self, name, shape, dt, kind="Internal"):
        return self.nc.dram_tensor(name, list(shape), dt, kind=kind).ap()

    def new_dsem(self):
        k = "d%d" % self.ndsem
        self.ndsem += 1
        self.sems[k] = self.es.enter_context(self.nc.semaphore(k))
        self.cnt[k] = 0
        return k

    def _deps(self, eng, reads, writes):
        deps = {}

        def add(k, v):
            if deps.get(k, 0) < v:
                deps[k] = v

        for b in reads:
            if b.w is not None:
                add(*b.w)
        for b in writes:
            if b.w is not None and b.w[0] != eng:
                add(*b.w)
            for k, v in b.r.items():
                if k != eng:
                    add(k, v)
        return deps

    def _emit_waits(self, eng, deps):
        e = self.engs[eng]
        seen = self.seen[eng]
        for k, v in deps.items():
            if seen.get(k, 0) >= v:
                continue
            assert v <= self.cnt[k], "wait on %s=%d never reached (issued %d)" % (k, v, self.cnt[k])
            e.wait_ge(self.sems[k], v)
            seen[k] = v

    def op(self, eng, fn, reads=(), writes=(), inc=True):
        self._emit_waits(eng, self._deps(eng, reads, writes))
        ins = fn(self.engs[eng])
        self.nins += 1
        if inc:
            self.cnt[eng] += 1
            ins.then_inc(self.sems[eng], 1)
            tok = (eng, self.cnt[eng])
        else:
            tok = (eng, self.cnt[eng] + 1)
        for b in reads:
            if b.r.get(eng, 0) < tok[1]:
                b.r[eng] = tok[1]
        for b in writes:
            b.w = tok
            b.r = {}
        return tok

    def dma(self, q, out, in_, sem_buf, reads=(), writes=()):
        self._emit_waits(q, self._deps("__dma__", reads, writes))
        ins = self.engs[q].dma_start(out=out, in_=in_)
        self.nins += 1
        if sem_buf.dsem is None:
            sem_buf.dsem = self.new_dsem()
        k = sem_buf.dsem
        self.cnt[k] += 16
        ins.then_inc(self.sems[k], 16)
        tok = (k, self.cnt[k])
        for b in reads:
            if b.r.get(k, 0) < tok[1]:
                b.r[k] = tok[1]
        for b in writes:
            b.w = tok
            b.r = {}
        return tok

    def barrier(self):
        for e in self.engs:
            self._emit_waits(e, {k: v for k, v in self.cnt.items() if k != e and v > 0})

    def finish(self):
        self.barrier()
        self.es.close()
        return self.nc


OFF_QKV, OFF_A, OFF_B, OFF_MQ, OFF_MK, OFF_MV, OFF_MI, OFF_MF, OFF_Z, OFF_MO, OFF_GG, OFF_GM, OFF_END = (
    0, 3072, 3088, 3104, 3616, 4128, 5152, 5160, 5168, 6192, 7216, 8240, 9264)
NCH = 66
OWN_T = 4224


def _col(v, n=128):
    v = np.asarray(v, np.float32).reshape(-1, n)
    return np.ascontiguousarray(v.T)


def _rep(v):
    v = np.asarray(v, np.float32).reshape(1, -1)
    return np.ascontiguousarray(np.repeat(v, 128, axis=0))


def _swapdir(a, flip):
    if not flip:
        return a
    h = a.shape[-1] // 2
    return np.concatenate([a[..., h:], a[..., :h]], axis=-1)


def prep_core(inp, core):
    b = core // 2
    flip = core % 2
    f32 = np.float32
    x = inp["x"][b]
    ctx = inp["ctx"][b]
    if flip:
        x = x[::-1]
        ctx = ctx[::-1]
    w_in = inp["w_in"][0]
    m = {}
    m["x"] = np.ascontiguousarray(x, dtype=f32)
    m["ctx"] = np.ascontiguousarray(ctx, dtype=f32)
    m["c_col"] = _col(inp["c"][b])
    m["cc_col"] = _col(inp["c_ctx"])
    m["w_ada"] = np.ascontiguousarray(inp["w_ada"][0], dtype=f32)
    b_ada = inp["b_ada"][0]
    m["b_ada_col"] = _col(b_ada)
    m["b_ada_g"] = np.ascontiguousarray(np.concatenate([_rep(b_ada[2048:3072]), _rep(b_ada[5120:6144])], axis=1))
    m["n1_col"] = _col(inp["norm1_w"][0])
    m["n2_col"] = _col(inp["norm2_w"][0])
    m["w_qkv"] = np.ascontiguousarray(w_in[:, OFF_QKV:OFF_A])
    wg = np.concatenate([_swapdir(w_in[:, OFF_A:OFF_B], flip), _swapdir(w_in[:, OFF_B:OFF_MQ], flip),
                         _swapdir(w_in[:, OFF_MI:OFF_MF], flip), _swapdir(w_in[:, OFF_MF:OFF_Z], flip)], axis=1)
    m["w_gate"] = np.ascontiguousarray(wg)
    m["w_ml"] = np.ascontiguousarray(w_in[:, OFF_MQ:OFF_MI])
    m["w_o"] = np.ascontiguousarray(w_in[:, OFF_Z:OFF_END])
    gp = np.concatenate([_swapdir(inp["gdn_dt_bias"][0].reshape(-1), flip), _swapdir(inp["gdn_a_log"][0].reshape(-1), flip),
                         _swapdir(inp["ml_igate_b"][0].reshape(-1), flip), _swapdir(inp["ml_fgate_b"][0].reshape(-1), flip)])
    m["gate_p"] = _rep(gp)
    gc = inp["gdn_conv"][0]
    if flip:
        gc = gc[::-1]
    m["gdn_cw"] = np.ascontiguousarray(gc.T.reshape(24, 128, 3).transpose(1, 0, 2), dtype=f32)
    fc = inp["ffn_conv"][0]
    if flip:
        fc = fc[::-1, ::-1]
    m["ffn_cw"] = np.ascontiguousarray(fc.reshape(9, 44, 128).transpose(2, 1, 0), dtype=f32)
    m["gnw_bc"] = _rep(np.tile(inp["gdn_norm_w"][0], 8))
    m["mnw_bc"] = _rep(inp["ml_norm_w"][0].reshape(-1))
    m["now_bc"] = _rep(inp["norm_out_w"])
    m["w_bg"] = np.ascontiguousarray(inp["w_branch_gdn"][0], dtype=f32)
    m["w_bm"] = np.ascontiguousarray(inp["w_branch_ml"][0], dtype=f32)
    m["w_out"] = np.ascontiguousarray(inp["w_out"][0], dtype=f32)
    m["w_up"] = np.ascontiguousarray(inp["w_up"][0], dtype=f32)
    m["w_down"] = np.ascontiguousarray(inp["w_down"][0], dtype=f32)
    m["smask"] = make_smask()
    return m


def make_smask():
    idx = np.arange(128)
    i = idx[None, :]
    j = idx[:, None]
    out = np.zeros((128, 14, 128), np.float32)
    for lev in range(7):
        b = 1 << lev
        same = (i // (2 * b)) == (j // (2 * b))
        f = same & ((i % (2 * b)) < b) & ((j % (2 * b)) >= b)
        g = same & ((j % (2 * b)) < b) & ((i % (2 * b)) >= b)
        out[:, lev, :] = np.where(f, -1.0, 0.0) + np.eye(128)
        out[:, 7 + lev, :] = np.where(g, -1.0, 0.0) + np.eye(128)
    return out


IN_SHAPES = {
    "x": [T, D], "ctx": [TC, D], "c_col": [128, 8], "cc_col": [128, 8], "w_ada": [D, 6144],
    "b_ada_col": [128, 48], "b_ada_g": [128, 2048], "n1_col": [128, 8], "n2_col": [128, 8],
    "w_qkv": [D, 3072], "w_gate": [D, 48], "w_ml": [D, 2048], "w_o": [D, 4096], "gate_p": [128, 48],
    "gdn_cw": [128, 24, 3], "ffn_cw": [128, 44, 9], "gnw_bc": [128, 1024], "mnw_bc": [128, 1024],
    "now_bc": [128, 1024], "w_bg": [D, D], "w_bm": [D, D], "w_out": [D, D], "w_up": [D, 5632], "w_down": [2816, D],
    "smask": [128, 14, 128],
}


class Ring:
    def __init__(self, B, stack, name, n, shape, dt, psum=False):
        self.slots = []
        for i in range(n):
            t = (B.ps if psum else B.sb)(stack, "%s%d" % (name, i), shape, dt)
            self.slots.append((t, Buf("%s%d" % (name, i))))
        self.i = 0

    def next(self):
        s = self.slots[self.i % len(self.slots)]
        self.i += 1
        return s


class Prog:
    def __init__(self, debug=None):
        self.debug = debug or {}
        self.B = Builder()
        self.nc = self.B.nc
        self.top = ExitStack()
        self.inp = {}
        for k, shp in IN_SHAPES.items():
            self.inp[k] = self.nc.dram_tensor(k, list(shp), F32, kind="ExternalInput").ap()
        self.out = self.nc.dram_tensor("out", [4096, D], F32, kind="ExternalOutput").ap()
        dk = "ExternalOutput" if self.debug.get("scratch_out") else "Internal"
        B = self.B
        self.KT = B.dram("KT", [NCH, 128, 8, 128], BF16, dk)
        self.QT = B.dram("QT", [NCH, 128, 8, 128], BF16, dk)
        self.VG = B.dram("VG", [NCH, 128, 1024], BF16, dk)
        self.MQT = B.dram("MQT", [NCH, 128, 4, 128], BF16, dk)
        self.MKT = B.dram("MKT", [NCH, 128, 4, 128], BF16, dk)
        self.MV = B.dram("MV", [NCH, 128, 1024], BF16, dk)
        self.GT = B.dram("GT", [NCH, 128, 48], F32, dk)
        self.OF = B.dram("OF", [33, 128, 1024], F32, dk)
        self.OB = B.dram("OB", [33, 128, 1024], F32, dk)
        self.HF = B.dram("HF", [33, 128, 1024], F32, dk)
        self.HB = B.dram("HB", [33, 128, 1024], F32, dk)
        self.X1 = B.dram("X1", [64 + OWN_T, D], F32, dk)
        self.consts()

    def consts(self):
        B, st = self.B, self.top
        self.ident_f = B.sb(st, "ident_f", [128, 128], F32)
        self.ident_b = B.sb(st, "ident_b", [128, 128], BF16)
        self.ones_f = B.sb(st, "ones_f", [128, 128], F32)
        self.ones_b = B.sb(st, "ones_b", [128, 128], BF16)
        self.nhalf = B.sb(st, "nhalf", [128, 512], F32)
        self.cb = Buf("consts")
        cb = self.cb
        B.op("pool", lambda e: e.memset(self.ones_f[:], 1.0), writes=[cb])
        B.op("pool", lambda e: e.memset(self.ones_b[:], 1.0), writes=[cb])
        B.op("pool", lambda e: e.memset(self.nhalf[:], -0.5), writes=[cb])
        B.op("pool", lambda e: e.memset(self.ident_f[:], 1.0), writes=[cb])
        B.op("pool", lambda e: e.affine_select(self.ident_f[:], self.ident_f[:], pattern=[[-1, 128]], compare_op=ALU.is_equal,
                                               fill=0.0, base=0, channel_multiplier=1), reads=[cb], writes=[cb])
        B.op("dve", lambda e: e.tensor_copy(out=self.ident_b[:], in_=self.ident_f[:]), reads=[cb], writes=[cb])
        self.modc = B.sb(st, "modc", [128, 6, 8], F32)
        self.bmod = Buf("modc")
        self.gate_bc = B.sb(st, "gate_bc", [128, 2, 1024], F32)
        self.bgate = Buf("gate_bc")

    def mask(self, stack, name, cmp_pat, dt=F32, val=1.0, fill=0.0):
        B = self.B
        base, cm, step, cmp = cmp_pat
        t = B.sb(stack, name, [128, 128], dt)
        tf = t
        if dt != F32:
            tf = B.sb(stack, name + "_f", [128, 128], F32)
        b = Buf(name)
        B.op("pool", lambda e: e.memset(tf[:], val), writes=[b])
        B.op("pool", lambda e: e.affine_select(tf[:], tf[:], pattern=[[step, 128]], compare_op=cmp, fill=fill,
                                               base=base, channel_multiplier=cm), reads=[b], writes=[b])
        if dt != F32:
            B.op("dve", lambda e: e.tensor_copy(out=t[:], in_=tf[:]), reads=[b], writes=[b])
        return t, b

    def phase0(self):
        B, nc, inp = self.B, self.nc, self.inp
        st = ExitStack()
        sc = B.sb(st, "p0_sc", [128, 16], F32)
        bsc = Buf("p0_sc")
        scb = B.sb(st, "p0_scb", [128, 8, 128], F32)
        bscb = Buf("p0_scb")
        bcol = B.sb(st, "p0_bcol", [128, 48], F32)
        n12 = B.sb(st, "p0_n12", [128, 16], F32)
        bg = B.sb(st, "p0_bg", [128, 2048], F32)
        bsm = Buf("p0_small")
        B.dma("sp", sc[:, 0:8], inp["c_col"][:, :], bsc, writes=[bsc])
        B.dma("sp", sc[:, 8:16], inp["cc_col"][:, :], bsc, writes=[bsc])
        B.dma("sp", bcol[:], inp["b_ada_col"][:, :], bsm, writes=[bsm])
        B.dma("sp", n12[:, 0:8], inp["n1_col"][:, :], bsm, writes=[bsm])
        B.dma("sp", n12[:, 8:16], inp["n2_col"][:, :], bsm, writes=[bsm])
        B.dma("sp", bg[:], inp["b_ada_g"][:, :], bsm, writes=[bsm])
        B.op("act", lambda e: e.activation(out=sc[:], in_=sc[:], func=AF.Silu), reads=[bsc], writes=[bsc])
        for k in range(8):
            B.op("dve", lambda e, k=k: e.tensor_scalar(out=scb[:, k, :], in0=self.ones_f[:], scalar1=sc[:, k:k + 1], scalar2=None,
                                                       op0=ALU.mult), reads=[bsc, self.cb], writes=[bscb])
        wring = Ring(B, st, "p0_w", 2, [128, 8, 512], F32)
        pcol = B.ps(st, "p0_pcol", [128, 64], F32)
        bpcol = Buf("p0_pcol")
        prow = Ring(B, st, "p0_prow", 2, [128, 512], F32, psum=True)
        wv = inp["w_ada"].rearrange("(k p) n -> p k n", p=128)
        xslot = {0: 0, 1: 1, 3: 2, 4: 3}
        for nb in range(12):
            v, half = nb // 2, nb % 2
            w, bw = wring.next()
            B.dma("sp", w[:], wv[:, :, nb * 512:(nb + 1) * 512], bw, writes=[bw])
            if v in (2, 5):
                p, bp = prow.next()
                for k in range(8):
                    B.op("pe", lambda e, k=k, p=p, w=w: e.matmul(p[:], lhsT=scb[:, k, :], rhs=w[:, k, :], start=(k == 0), stop=(k == 7)),
                         reads=[bscb, bw], writes=[bp], inc=(k == 7))
                gi = 0 if v == 2 else 1
                B.op("dve", lambda e, p=p, gi=gi, half=half: e.tensor_tensor(
                    out=self.gate_bc[:, gi, half * 512:(half + 1) * 512], in0=p[:], in1=bg[:, gi * 1024 + half * 512: gi * 1024 + (half + 1) * 512],
                    op=ALU.add), reads=[bp, bsm], writes=[self.bgate])
            else:
                for cc in range(4):
                    col = xslot[v] * 8 + half * 4 + cc
                    for k in range(8):
                        B.op("pe", lambda e, k=k, w=w, cc=cc, col=col: e.matmul(pcol[:, col:col + 1], lhsT=w[:, k, cc * 128:(cc + 1) * 128],
                                                                                 rhs=sc[:, k:k + 1], start=(k == 0), stop=(k == 7)),
                             reads=[bw, bsc], writes=[bpcol], inc=(k == 7))
                    if v in (0, 1):
                        col2 = 32 + v * 8 + half * 4 + cc
                        for k in range(8):
                            B.op("pe", lambda e, k=k, w=w, cc=cc, col2=col2: e.matmul(pcol[:, col2:col2 + 1], lhsT=w[:, k, cc * 128:(cc + 1) * 128],
                                                                                       rhs=sc[:, 8 + k:9 + k], start=(k == 0), stop=(k == 7)),
                                 reads=[bw, bsc], writes=[bpcol], inc=(k == 7))
        mc = B.sb(st, "p0_mc", [128, 6, 8], F32)
        bmc = Buf("p0_mc")
        for i, v in enumerate((0, 1, 3, 4)):
            B.op("dve", lambda e, i=i, v=v: e.tensor_tensor(out=mc[:, i, :], in0=pcol[:, i * 8:(i + 1) * 8], in1=bcol[:, v * 8:(v + 1) * 8], op=ALU.add),
                 reads=[bpcol, bsm], writes=[bmc])
        for i, v in enumerate((0, 1)):
            B.op("dve", lambda e, i=i, v=v: e.tensor_tensor(out=mc[:, 4 + i, :], in0=pcol[:, 32 + i * 8:32 + (i + 1) * 8], in1=bcol[:, v * 8:(v + 1) * 8],
                                                            op=ALU.add), reads=[bpcol, bsm], writes=[bmc])
        md = self.modc
        for dst, (sci, shi, nw) in {0: (1, 0, 0), 2: (5, 4, 0), 4: (3, 2, 1)}.items():
            B.op("dve", lambda e, dst=dst, sci=sci, nw=nw: e.scalar_tensor_tensor(out=md[:, dst, :], in0=mc[:, sci, :], scalar=1.0, in1=n12[:, nw * 8:(nw + 1) * 8],
                                                                                  op0=ALU.add, op1=ALU.mult), reads=[bmc, bsm], writes=[self.bmod])
            B.op("dve", lambda e, dst=dst, shi=shi: e.tensor_copy(out=md[:, dst + 1, :], in_=mc[:, shi, :]), reads=[bmc], writes=[self.bmod])
        B.barrier()
        st.close()

    @staticmethod
    def run_pipeline(makers, depth):
        active = []
        it = iter(makers)
        exhausted = False
        while True:
            for g in list(active):
                try:
                    next(g)
                except StopIteration:
                    active.remove(g)
            if not exhausted and len(active) < depth:
                try:
                    g = next(it)()
                    try:
                        next(g)
                        active.append(g)
                    except StopIteration:
                        pass
                except StopIteration:
                    exhausted = True
            if exhausted and not active:
                break

    def load_w_bf16(self, stack, name, ap, kchunks, ncols, nsplit=4):
        B = self.B
        t = B.sb(stack, name, [128, kchunks, ncols], BF16)
        b = Buf(name)
        v = ap.rearrange("(k p) n -> p k n", p=128)
        step = (ncols + nsplit - 1) // nsplit
        for i in range(0, ncols, step):
            j = min(ncols, i + step)
            B.dma("pool", t[:, :, i:j], v[:, :, i:j], b, writes=[b])
        return t, b

    def norm_transpose(self, xt, bxts, ns, xn, bxn, sq, bsq, junk, bjunk, ptr_ring, hxT, bhx, col0, ai, npart=128):
        B = self.B
        for s in range(ns):
            B.op("act", lambda e, s=s: e.activation(out=xn[0:npart, s, :], in_=xt[0:npart, s, :], func=AF.Square, accum_out=sq[0:npart, s:s + 1]),
                 reads=[bxts[s]], writes=[bxn, bsq])
        B.op("dve", lambda e: e.tensor_scalar(out=sq[0:npart, 8:8 + ns], in0=sq[0:npart, 0:ns], scalar1=float(D * EPS), scalar2=None, op0=ALU.add),
             reads=[bsq], writes=[bsq])
        B.op("pool", lambda e: e.tensor_tensor(out=sq[0:npart, 16:16 + ns], in0=sq[0:npart, 8:8 + ns], in1=self.nhalf[0:npart, 0:ns], op=ALU.pow),
             reads=[bsq, self.cb], writes=[bsq])
        for s in range(ns):
            B.op("dve", lambda e, s=s: e.tensor_scalar(out=xn[0:npart, s, :], in0=xt[0:npart, s, :], scalar1=sq[0:npart, 16 + s:17 + s], scalar2=32.0,
                                                       op0=ALU.mult, op1=ALU.mult), reads=[bxts[s], bsq], writes=[bxn])
        for k in range(KD):
            p, bp = ptr_ring.next()
            pb = p[:].bitcast(BF16)
            for s in range(ns):
                B.op("pe", lambda e, s=s, k=k, pb=pb: e.transpose(pb[:, s * npart:(s + 1) * npart], xn[0:npart, s, k * 128:(k + 1) * 128],
                                                                  self.ident_b[0:npart, 0:npart]),
                     reads=[bxn, self.cb], writes=[bp], inc=(s == ns - 1))
            B.op("act", lambda e, k=k, pb=pb: e.activation(out=hxT[:, k, col0:col0 + ns * npart], in_=pb[:, 0:ns * npart], func=AF.Identity,
                                                           scale=self.modc[:, ai, k:k + 1], bias=self.modc[:, ai + 1, k:k + 1]),
                 reads=[bp, self.bmod], writes=[bhx])

    def phaseA(self):
        B, nc, inp = self.B, self.nc, self.inp
        st = ExitStack()
        wqkv, bwqkv = self.load_w_bf16(st, "a_wqkv", inp["w_qkv"], 8, 3072, 6)
        wml, bwml = self.load_w_bf16(st, "a_wml", inp["w_ml"], 8, 2048, 4)
        wgt, bwgt = self.load_w_bf16(st, "a_wgt", inp["w_gate"], 8, 48, 1)
        cw = B.sb(st, "a_cw", [128, 24, 3], F32)
        gp = B.sb(st, "a_gp", [128, 48], F32)
        bsm = Buf("a_small")
        B.dma("sp", cw[:], inp["gdn_cw"][:, :, :], bsm, writes=[bsm])
        B.dma("sp", gp[:], inp["gate_p"][:, :], bsm, writes=[bsm])
        B.op("act", lambda e: e.activation(out=gp[:, 16:32], in_=gp[:, 16:32], func=AF.Exp), reads=[bsm], writes=[bsm])
        B.op("dve", lambda e: e.tensor_scalar(out=gp[:, 16:32], in0=gp[:, 16:32], scalar1=-1.0, scalar2=None, op0=ALU.mult), reads=[bsm], writes=[bsm])
        xt = B.sb(st, "a_x", [128, 4, 1024], F32)
        bxts = [Buf("a_x%d" % i) for i in range(4)]
        xh = B.sb(st, "a_xh", [2, 1, 1024], F32); bxh = Buf("a_xh")
        xn = B.sb(st, "a_xn", [128, 4, 1024], BF16); bxn = Buf("a_xn")
        xnh = B.sb(st, "a_xnh", [2, 1, 1024], BF16); bxnh = Buf("a_xnh")
        sq = B.sb(st, "a_sq", [128, 24], F32); bsq = Buf("a_sq")
        sqh = B.sb(st, "a_sqh", [128, 24], F32); bsqh = Buf("a_sqh")
        junk = None; bjunk = None
        hxT = B.sb(st, "a_hxT", [128, 8, 514], BF16); bhx = Buf("a_hxT")
        hxh = B.sb(st, "a_hxh", [128, 8, 2], BF16); bhxh = Buf("a_hxh")
        ptr = Ring(B, st, "a_ptr", 2, [128, 512], F32, psum=True)
        pz = Ring(B, st, "a_pz", 4, [128, 512], F32, psum=True)
        pmisc = B.ps(st, "a_pmisc", [128, 512], F32)
        pzh = pmisc[:, 0:64]; bpzh = Buf("a_pzh")
        pn = ptr
        zb = Ring(B, st, "a_zb", 4, [128, 514], F32)
        y1 = Ring(B, st, "a_y1", 4, [128, 512], F32)
        sqb = Ring(B, st, "a_sqb", 2, [128, 512], BF16)
        skeep = B.sb(st, "a_skeep", [128, 8, 512], BF16)
        bskeep = [Buf("a_skeep%d" % i) for i in range(8)]
        rnr = B.sb(st, "a_rnr", [8, 512], F32); brnr = Buf("a_rnr")
        ind = B.sb(st, "a_ind", [128, 8, 8], BF16); bind = Buf("a_ind")
        selr = B.sb(st, "a_selr", [8, 8, 128], F32); bselr = Buf("a_selr")
        B.op("pool", lambda e: e.memset(ind[:], 0.0), writes=[bind])
        for jj in range(8):
            B.op("pool", lambda e, jj=jj: e.memset(ind[:, jj, jj:jj + 1], 1.0), writes=[bind])
            B.op("dve", lambda e, jj=jj: e.tensor_copy(out=selr[:, jj, :], in_=self.ident_f[0:8, jj:jj + 1].to_broadcast([8, 128])), reads=[self.cb], writes=[bselr])
        pss = B.ps(st, "a_pss", [128, 512], F32); bpss = Buf("a_pss")
        kst = B.sb(st, "a_kst", [128, 4, 8, 128], BF16); bkst = Buf("a_kst")
        qst = B.sb(st, "a_qst", [128, 4, 8, 128], BF16); bqst = Buf("a_qst")
        vT = B.sb(st, "a_vT", [128, 8, 512], BF16); bvT = Buf("a_vT")
        vst = Ring(B, st, "a_vst", 1, [128, 4, 1024], BF16)
        mqst = B.sb(st, "a_mqst", [128, 4, 4, 128], BF16); bmqst = Buf("a_mqst")
        mkst = B.sb(st, "a_mkst", [128, 4, 4, 128], BF16); bmkst = Buf("a_mkst")
        graw = B.sb(st, "a_graw", [128, 4, 48], F32); bgraw = Buf("a_graw")
        gwk = B.sb(st, "a_gwk", [128, 4, 48], F32); bgwk = Buf("a_gwk")
        gsb = Ring(B, st, "a_gsb", 2, [128, 4, 48], F32)
        pg = pmisc[:, 64:256].rearrange("p (s g) -> p s g", g=48); bpg = Buf("a_pg")
        dkr = float(128 ** -0.5)

        tiles = [(inp["ctx"], 0, 2, 0, False, False)]
        for i in range(16):
            tiles.append((inp["x"], i * 512, 4, 2 + 4 * i, i > 0, i < 15))
        if self.debug.get("a_tiles"):
            tiles = tiles[: self.debug["a_tiles"]]
        for (src, t0, ns, c0, hl, hr) in tiles:
            n = ns * 128
            ai = 2 if src is inp["ctx"] else 0
            for s in range(ns):
                B.dma("sp", xt[:, s, :], src[t0 + s * 128:t0 + (s + 1) * 128, :], bxts[s], writes=[bxts[s]])
            tl = t0 - 1 if hl else t0
            tr = t0 + n if hr else t0
            B.dma("sp", xh[0:1, 0, :], src[tl:tl + 1, :], bxh, writes=[bxh])
            B.dma("sp", xh[1:2, 0, :], src[tr:tr + 1, :], bxh, writes=[bxh])
            self.norm_transpose(xt, bxts, ns, xn, bxn, sq, bsq, junk, bjunk, ptr, hxT, bhx, 1, ai)
            self.norm_transpose(xh, [bxh], 1, xnh, bxnh, sqh, bsqh, junk, bjunk, ptr, hxh, bhxh, 0, ai, npart=2)
            def chunk_gen(j, kind, jj):
                p, bp = pz.next()
                z, bz = zb.next()
                a1, ba1 = y1.next()
                for k in range(8):
                    B.op("pe", lambda e, k=k: e.matmul(p[:, 0:n], lhsT=wqkv[:, k, j * 128:(j + 1) * 128], rhs=hxT[:, k, 1:1 + n],
                                                         start=(k == 0), stop=(k == 7)), reads=[bwqkv, bhx], writes=[bp], inc=(k == 7))
                for k in range(8):
                    B.op("pe", lambda e, k=k: e.matmul(pzh[:, 2 * j:2 * j + 2], lhsT=wqkv[:, k, j * 128:(j + 1) * 128], rhs=hxh[:, k, :],
                                                         start=(k == 0), stop=(k == 7)), reads=[bwqkv, bhxh], writes=[bpzh], inc=(k == 7))
                yield
                B.op("act", lambda e: e.activation(out=z[:, 1:1 + n], in_=p[:, 0:n], func=AF.Identity), reads=[bp], writes=[bz])
                B.op("act", lambda e: e.activation(out=z[:, 0:1], in_=pzh[:, 2 * j:2 * j + 1], func=AF.Identity), reads=[bpzh], writes=[bz])
                B.op("act", lambda e: e.activation(out=z[:, n + 1:n + 2], in_=pzh[:, 2 * j + 1:2 * j + 2], func=AF.Identity), reads=[bpzh], writes=[bz])
                if not hl:
                    B.op("pool", lambda e: e.memset(z[:, 0:1], 0.0), writes=[bz])
                if not hr:
                    B.op("pool", lambda e: e.memset(z[:, n + 1:n + 2], 0.0), writes=[bz])
                yield
                B.op("dve", lambda e: e.tensor_scalar(out=a1[:, 0:n], in0=z[:, 1:1 + n], scalar1=cw[:, j, 1:2], scalar2=None, op0=ALU.mult),
                     reads=[bz, bsm], writes=[ba1])
                B.op("dve", lambda e: e.scalar_tensor_tensor(out=a1[:, 0:n], in0=z[:, 0:n], scalar=cw[:, j, 0:1], in1=a1[:, 0:n],
                                                            op0=ALU.mult, op1=ALU.add), reads=[bz, bsm, ba1], writes=[ba1])
                B.op("dve", lambda e: e.scalar_tensor_tensor(out=a1[:, 0:n], in0=z[:, 2:2 + n], scalar=cw[:, j, 2:3], in1=a1[:, 0:n],
                                                            op0=ALU.mult, op1=ALU.add), reads=[bz, bsm, ba1], writes=[ba1])
                yield
                if kind == "v":
                    B.op("act", lambda e: e.activation(out=vT[:, jj, 0:n], in_=a1[:, 0:n], func=AF.Silu), reads=[ba1], writes=[bvT])
                else:
                    B.op("act", lambda e: e.activation(out=skeep[:, jj, 0:n], in_=a1[:, 0:n], func=AF.Silu), reads=[ba1], writes=[bskeep[jj]])
                    q2, bq2 = sqb.next()
                    B.op("pool", lambda e: e.tensor_tensor(out=q2[:, 0:n], in0=skeep[:, jj, 0:n], in1=skeep[:, jj, 0:n], op=ALU.mult),
                         reads=[bskeep[jj]], writes=[bq2])
                    B.op("pe", lambda e: e.matmul(pss[0:8, 0:n], lhsT=ind[:, jj, :], rhs=q2[:, 0:n], start=(jj == 0), stop=(jj == 7)),
                         reads=[bq2, bind], writes=[bpss])

            for half in range(2):
                self.run_pipeline([(lambda jj=jj: chunk_gen(half * 8 + jj, "qk", jj)) for jj in range(8)], 4)
                B.op("act", lambda e: e.activation(out=rnr[:, 0:n], in_=pss[0:8, 0:n], func=AF.Ln, bias=float(EPS)), reads=[bpss], writes=[brnr])
                B.op("act", lambda e: e.activation(out=rnr[:, 0:n], in_=rnr[:, 0:n], func=AF.Exp, scale=-0.5), reads=[brnr], writes=[brnr])
                for jj in range(8):
                    pp, bpp = pn.next()
                    B.op("pe", lambda e, pp=pp, jj=jj: e.matmul(pp[:, 0:n], lhsT=selr[:, jj, :], rhs=rnr[:, 0:n], start=True, stop=True),
                         reads=[brnr, bselr], writes=[bpp])
                    if half == 0:
                        B.op("dve", lambda e, pp=pp, jj=jj: e.scalar_tensor_tensor(
                            out=qst[:, 0:ns, jj, :], in0=skeep[:, jj, 0:n].rearrange("p (s t) -> p s t", t=128), scalar=dkr,
                            in1=pp[:, 0:n].rearrange("p (s t) -> p s t", t=128), op0=ALU.mult, op1=ALU.mult), reads=[bskeep[jj], bpp], writes=[bqst])
                    else:
                        B.op("dve", lambda e, pp=pp, jj=jj: e.tensor_tensor(
                            out=kst[:, 0:ns, jj, :], in0=skeep[:, jj, 0:n].rearrange("p (s t) -> p s t", t=128),
                            in1=pp[:, 0:n].rearrange("p (s t) -> p s t", t=128), op=ALU.mult), reads=[bskeep[jj], bpp], writes=[bkst])
            for s in range(ns):
                for k in range(8):
                    B.op("pe", lambda e, k=k, s=s: e.matmul(pg[:, s, :], lhsT=hxT[:, k, 1 + s * 128:1 + (s + 1) * 128], rhs=wgt[:, k, :],
                                                             start=(k == 0), stop=(k == 7)), reads=[bhx, bwgt], writes=[bpg], inc=(k == 7))
            g, bg_ = gsb.next()
            self.gate_math(pg, bpg, graw, bgraw, gwk, bgwk, g, bg_, gp, bsm, ns)
            B.dma("sp", self.GT[c0:c0 + ns].rearrange("c t g -> t c g"), g[:, 0:ns, :], bg_, reads=[bg_])
            self.run_pipeline([(lambda jj=jj: chunk_gen(16 + jj, "v", jj)) for jj in range(8)], 4)
            self.v_transposes(vT, bvT, ns, vst, pn, self.VG, c0)
            B.dma("sp", self.KT[c0:c0 + ns].rearrange("c d h t -> d c h t"), kst[:, 0:ns], bkst, reads=[bkst])
            B.dma("sp", self.QT[c0:c0 + ns].rearrange("c d h t -> d c h t"), qst[:, 0:ns], bqst, reads=[bqst])
            for j in range(16):
                p, bp = pz.next()
                for k in range(8):
                    B.op("pe", lambda e, k=k, p=p, j=j: e.matmul(p[:, 0:n], lhsT=wml[:, k, j * 128:(j + 1) * 128], rhs=hxT[:, k, 1:1 + n],
                                                                  start=(k == 0), stop=(k == 7)), reads=[bwml, bhx], writes=[bp], inc=(k == 7))
                if j < 4:
                    B.op("act", lambda e, p=p, j=j: e.activation(out=mqst[:, 0:ns, j, :], in_=p[:, 0:n].rearrange("p (s t) -> p s t", t=128),
                                                                  func=AF.Identity, scale=dkr), reads=[bp], writes=[bmqst])
                elif j < 8:
                    B.op("act", lambda e, p=p, j=j: e.activation(out=mkst[:, 0:ns, j - 4, :], in_=p[:, 0:n].rearrange("p (s t) -> p s t", t=128),
                                                                  func=AF.Identity), reads=[bp], writes=[bmkst])
                else:
                    B.op("act", lambda e, p=p, j=j: e.activation(out=vT[:, j - 8, 0:n], in_=p[:, 0:n], func=AF.Identity), reads=[bp], writes=[bvT])
            self.v_transposes(vT, bvT, ns, vst, pn, self.MV, c0)
            B.dma("sp", self.MQT[c0:c0 + ns].rearrange("c d h t -> d c h t"), mqst[:, 0:ns], bmqst, reads=[bmqst])
            B.dma("sp", self.MKT[c0:c0 + ns].rearrange("c d h t -> d c h t"), mkst[:, 0:ns], bmkst, reads=[bmkst])
        B.barrier()
        st.close()

    def v_transposes(self, vT, bvT, ns, vst, pn, dst, c0):
        B = self.B
        v, bv = vst.next()
        for s in range(ns):
            pp, bpp = pn.next()
            ppb = pp[:].bitcast(BF16)
            for h in range(8):
                B.op("pe", lambda e, s=s, h=h, ppb=ppb: e.transpose(ppb[:, h * 128:(h + 1) * 128], vT[:, h, s * 128:(s + 1) * 128], self.ident_b[:]),
                     reads=[bvT, self.cb], writes=[bpp], inc=(h == 7))
            B.op("act", lambda e, s=s, ppb=ppb, v=v: e.activation(out=v[:, s, :], in_=ppb[:, 0:1024], func=AF.Identity), reads=[bpp], writes=[bv])
        B.dma("sp", dst[c0:c0 + ns].rearrange("c t e -> t c e"), v[:, 0:ns, :], bv, reads=[bv])

    def gate_math(self, pg, bpg, graw, bgraw, wk, bwk, g, bg_, gp, bgp, ns):
        B = self.B
        S = slice(0, ns)

        def bc(lo, hi):
            return gp[:, lo:hi].unsqueeze(1).to_broadcast([128, ns, hi - lo])

        B.op("act", lambda e: e.activation(out=graw[:, S, :], in_=pg[:, S, :], func=AF.Identity), reads=[bpg], writes=[bgraw])
        B.op("dve", lambda e: e.tensor_tensor(out=wk[:, S, 0:16], in0=graw[:, S, 0:16], in1=bc(0, 16), op=ALU.add), reads=[bgraw, bgp], writes=[bwk])
        B.op("act", lambda e: e.activation(out=wk[:, S, 0:16], in_=wk[:, S, 0:16], func=AF.Exp), reads=[bwk], writes=[bwk])
        B.op("act", lambda e: e.activation(out=wk[:, S, 0:16], in_=wk[:, S, 0:16], func=AF.Ln, bias=1.0), reads=[bwk], writes=[bwk])
        B.op("dve", lambda e: e.tensor_tensor(out=g[:, S, 0:16], in0=wk[:, S, 0:16], in1=bc(16, 32), op=ALU.mult), reads=[bwk, bgp], writes=[bg_])
        B.op("act", lambda e: e.activation(out=wk[:, S, 16:32], in_=graw[:, S, 16:32], func=AF.Exp, scale=-1.0), reads=[bgraw], writes=[bwk])
        B.op("dve", lambda e: e.tensor_scalar(out=wk[:, S, 16:32], in0=wk[:, S, 16:32], scalar1=1.0, scalar2=None, op0=ALU.add), reads=[bwk], writes=[bwk])
        B.op("dve", lambda e: e.reciprocal(out=g[:, S, 16:32], in_=wk[:, S, 16:32]), reads=[bwk], writes=[bg_])
        B.op("dve", lambda e: e.tensor_tensor(out=wk[:, S, 32:48], in0=graw[:, S, 32:48], in1=bc(32, 48), op=ALU.add), reads=[bgraw, bgp], writes=[bwk])
        B.op("act", lambda e: e.activation(out=wk[:, S, 32:48], in_=wk[:, S, 32:48], func=AF.Exp, scale=float(2.0 / 15.0)), reads=[bwk], writes=[bwk])
        B.op("dve", lambda e: e.tensor_scalar(out=wk[:, S, 32:48], in0=wk[:, S, 32:48], scalar1=1.0, scalar2=None, op0=ALU.add), reads=[bwk], writes=[bwk])
        B.op("dve", lambda e: e.reciprocal(out=wk[:, S, 32:48], in_=wk[:, S, 32:48]), reads=[bwk], writes=[bwk])
        B.op("dve", lambda e: e.tensor_scalar(out=g[:, S, 32:48], in0=wk[:, S, 32:48], scalar1=-30.0, scalar2=15.0, op0=ALU.mult, op1=ALU.add),
             reads=[bwk], writes=[bg_])
        B.op("act", lambda e: e.activation(out=wk[:, S, 40:48], in_=g[:, S, 40:48], func=AF.Exp, scale=-1.0), reads=[bg_], writes=[bwk])
        B.op("act", lambda e: e.activation(out=wk[:, S, 40:48], in_=wk[:, S, 40:48], func=AF.Ln, bias=1.0), reads=[bwk], writes=[bwk])
        B.op("dve", lambda e: e.tensor_scalar(out=g[:, S, 40:48], in0=wk[:, S, 40:48], scalar1=-1.0, scalar2=None, op0=ALU.mult), reads=[bwk], writes=[bg_])

    def phaseB(self):
        B, nc, inp = self.B, self.nc, self.inp
        st = ExitStack()
        dbg = self.debug
        LE, bLE = self.mask(st, "b_LE", (0, -1, 1, ALU.is_ge))
        LT, bLT = self.mask(st, "b_LT", (-1, -1, 1, ALU.is_ge))
        GE, bGE = self.mask(st, "b_GE", (0, 1, -1, ALU.is_ge))
        GT_, bGT = self.mask(st, "b_GT", (-1, 1, -1, ALU.is_ge))
        MBf, bMBf = self.mask(st, "b_MBf", (0, 1, -1, ALU.is_ge), val=0.0, fill=NEG)
        MBb, bMBb = self.mask(st, "b_MBb", (0, -1, 1, ALU.is_ge), val=0.0, fill=NEG)
        SELf, bSELf = self.mask(st, "b_SELf", (-127, 1, 0, ALU.is_equal))
        SELb, bSELb = self.mask(st, "b_SELb", (0, 1, 0, ALU.is_equal))
        smask = B.sb(st, "b_smask", [128, 14, 128], BF16)
        bsm = Buf("b_smask")
        B.dma("pool", smask[:], inp["smask"][:, :, :], bsm, writes=[bsm])
        cbufs = [bLE, bLT, bGE, bGT, bMBf, bMBb, bSELf, bSELb, bsm, self.cb]
        dirc = [dict(U=LE, S=GT_, incl=LE, strict=LT, MB=MBf, SEL=SELf),
                dict(U=GE, S=LT, incl=GE, strict=GT_, MB=MBb, SEL=SELb)]
        S = [B.sb(st, "b_S%d" % d, [128, 8, 128], F32) for d in range(2)]
        bS = [[Buf("b_S%d_%d" % (d, g)) for g in range(2)] for d in range(2)]
        Sb = [[Ring(B, st, "b_Sb%d_%d_" % (d, g), 2, [128, 4, 128], BF16) for g in range(2)] for d in range(2)]
        Sb_cur = [[None, None], [None, None]]
        C = [B.sb(st, "b_C%d" % d, [128, 4, 256], F32) for d in range(2)]
        bC = [Buf("b_C%d" % d) for d in range(2)]
        Cb = [Ring(B, st, "b_Cb%d_" % d, 2, [128, 4, 256], BF16) for d in range(2)]
        Cb_cur = [None, None]
        nst = [Ring(B, st, "b_n%d_" % d, 2, [128, 8], F32) for d in range(2)]
        nbf = [Ring(B, st, "b_nb%d_" % d, 2, [128, 4], BF16) for d in range(2)]
        n_cur = [None, None]
        nb_cur = [None, None]
        mst = [Ring(B, st, "b_m%d_" % d, 2, [128, 4], F32) for d in range(2)]
        m_cur = [None, None]
        for d in range(2):
            B.op("pool", lambda e, d=d: e.memset(S[d][:], 0.0), writes=bS[d])
            B.op("pool", lambda e, d=d: e.memset(C[d][:], 0.0), writes=[bC[d]])
            for g in range(2):
                t, b = Sb[d][g].next()
                B.op("pool", lambda e, t=t: e.memset(t[:], 0.0), writes=[b])
                Sb_cur[d][g] = (t, b)
            t, b = Cb[d].next()
            B.op("pool", lambda e, t=t: e.memset(t[:], 0.0), writes=[b])
            Cb_cur[d] = (t, b)
            t, b = nst[d].next()
            B.op("pool", lambda e, t=t: e.memset(t[:], 0.0), writes=[b])
            n_cur[d] = (t, b)
            t, b = nbf[d].next()
            B.op("pool", lambda e, t=t: e.memset(t[:], 0.0), writes=[b])
            nb_cur[d] = (t, b)
            t, b = mst[d].next()
            B.op("pool", lambda e, t=t: e.memset(t[:], 0.0), writes=[b])
            m_cur[d] = (t, b)
        def dring(name, shape, dt):
            return [Ring(B, st, "b_%s%d_" % (name, d), 2, shape, dt) for d in range(2)]
        rKT = dring("KT", [128, 8, 128], BF16)
        rQT = dring("QT", [128, 8, 128], BF16)
        rVG = dring("VG", [128, 1024], BF16)
        rGT = dring("GT", [128, 48], F32)
        rMQ = dring("MQ", [128, 4, 128], BF16)
        rMK = dring("MK", [128, 4, 128], BF16)
        rMV = dring("MV", [128, 1024], BF16)
        rgs = dring("gs", [128, 64], F32)
        psr = Ring(B, st, "b_ps", 8, [128, 512], F32, psum=True)
        NG, NM = 3, 2
        gslots = []
        for i in range(NG):
            sl = {}
            for nm, shp, dt in (("A", [128, 4, 128], F32), ("Bt", [128, 4, 128], F32), ("Ct", [128, 4, 128], F32),
                                ("attnT", [128, 4, 128], BF16), ("Qm", [128, 4, 128], BF16), ("Qp", [128, 4, 128], BF16),
                                ("Kg", [128, 4, 128], BF16), ("kt", [128, 4, 128], BF16), ("G0", [128, 4, 128], BF16),
                                ("G1", [128, 4, 128], BF16), ("H0", [128, 4, 128], BF16), ("H1", [128, 4, 128], BF16),
                                ("IYT", [128, 4, 128], BF16), ("negW", [128, 4, 128], BF16), ("vnew", [128, 4, 128], BF16),
                                ("St", [128, 4, 128], F32)):
                sl[nm] = (B.sb(st, "b_g%d_%s" % (i, nm), shp, dt), Buf("b_g%d_%s" % (i, nm)))
            gslots.append(sl)
        mslots = []
        for i in range(NM):
            sl = {}
            for nm, shp, dt in (("X", [128, 4, 128], F32), ("Y", [128, 4, 128], F32), ("Pm", [128, 4, 128], BF16),
                                ("PT", [128, 4, 128], BF16), ("Kw", [128, 4, 128], BF16), ("sm", [128, 64], F32), ("Ct", [128, 4, 256], F32)):
                sl[nm] = (B.sb(st, "b_m%d_%s" % (i, nm), shp, dt), Buf("b_m%d_%s" % (i, nm)))
            mslots.append(sl)
        ring_o1 = Ring(B, st, "b_o1_", 2, [128, 4, 128], F32)
        ring_o = Ring(B, st, "b_o_", 2, [128, 4, 128], F32)
        ring_num = Ring(B, st, "b_num_", 2, [128, 4, 256], F32)
        ring_h = Ring(B, st, "b_h_", 2, [128, 4, 256], F32)

        def bc3(ap2, n):
            return ap2.unsqueeze(2).to_broadcast([128, 4, n])

        def bcm(ap2, n=4):
            return ap2.unsqueeze(1).to_broadcast([128, n, 128])

        nsteps = dbg.get("b_steps", NCH)
        order = [list(range(NCH)), [1, 0] + list(range(NCH - 1, 1, -1))]
        if dbg.get("b_order"):
            order = dbg["b_order"]
            nsteps = len(order[0])
        out_lo, out_hi = 2, 2 + 33

        data = {}

        def load_step(step, d):
            c = order[d][step]
            tk, bk = rKT[d].next(); tq, bq = rQT[d].next(); tv, bv = rVG[d].next(); tg, bg = rGT[d].next()
            tmq, bmq = rMQ[d].next(); tmk, bmk = rMK[d].next(); tmv, bmv = rMV[d].next()
            B.dma("sp", tg[:], self.GT[c], bg, writes=[bg])
            B.dma("sp", tk[:], self.KT[c], bk, writes=[bk])
            B.dma("sp", tq[:], self.QT[c], bq, writes=[bq])
            B.dma("sp", tv[:], self.VG[c], bv, writes=[bv])
            B.dma("sp", tmq[:], self.MQT[c], bmq, writes=[bmq])
            B.dma("sp", tmk[:], self.MKT[c], bmk, writes=[bmk])
            B.dma("sp", tmv[:], self.MV[c], bmv, writes=[bmv])
            data[(step, d)] = dict(c=c, KT=(tk, bk), QT=(tq, bq), VG=(tv, bv), GT=(tg, bg), MQ=(tmq, bmq), MK=(tmk, bmk), MV=(tmv, bmv))

        def shared_pre(step, d):
            dd = data[(step, d)]
            tg, bg = dd["GT"]
            gs, bgs = rgs[d].next()
            dc = dirc[d]
            p, bp = psr.next()
            B.op("pe", lambda e: e.matmul(p[:, 0:8], lhsT=dc["U"][:], rhs=tg[:, d * 8:(d + 1) * 8], start=True, stop=True), reads=[bg] + cbufs, writes=[bp], inc=False)
            B.op("pe", lambda e: e.matmul(p[:, 8:12], lhsT=dc["U"][:], rhs=tg[:, 40 + d * 4:44 + d * 4], start=True, stop=True), reads=[bg] + cbufs, writes=[bp], inc=False)
            B.op("pe", lambda e: e.matmul(p[:, 12:20], lhsT=self.ones_f[:], rhs=tg[:, d * 8:(d + 1) * 8], start=True, stop=True), reads=[bg] + cbufs, writes=[bp], inc=False)
            B.op("pe", lambda e: e.matmul(p[:, 20:24], lhsT=self.ones_f[:], rhs=tg[:, 40 + d * 4:44 + d * 4], start=True, stop=True), reads=[bg] + cbufs, writes=[bp])
            B.op("act", lambda e: e.activation(out=gs[:, 0:24], in_=p[:, 0:24], func=AF.Identity), reads=[bp], writes=[bgs])
            B.op("act", lambda e: e.activation(out=gs[:, 24:32], in_=gs[:, 0:8], func=AF.Exp), reads=[bgs], writes=[bgs])
            B.op("dve", lambda e: e.tensor_tensor(out=gs[:, 32:40], in0=gs[:, 12:20], in1=gs[:, 0:8], op=ALU.subtract), reads=[bgs], writes=[bgs])
            B.op("act", lambda e: e.activation(out=gs[:, 32:40], in_=gs[:, 32:40], func=AF.Exp), reads=[bgs], writes=[bgs])
            B.op("act", lambda e: e.activation(out=gs[:, 40:48], in_=gs[:, 12:20], func=AF.Exp), reads=[bgs], writes=[bgs])
            B.op("dve", lambda e: e.tensor_tensor(out=gs[:, 48:52], in0=tg[:, 32 + d * 4:36 + d * 4], in1=gs[:, 8:12], op=ALU.subtract), reads=[bgs, bg], writes=[bgs])
            dd["gs"] = (gs, bgs)

        def gdn_group(step, d, hg, sl):
            dd = data[(step, d)]
            dc = dirc[d]
            c = dd["c"]
            need_o = out_lo <= c < out_hi
            tk, bk = dd["KT"]; tq, bq = dd["QT"]; tv, bv = dd["VG"]; tg, bg = dd["GT"]; gs, bgs = dd["gs"]
            h0 = hg * 4
            A, bA = sl["A"]; Bt, bBt = sl["Bt"]; Ct, bCt = sl["Ct"]
            attnT, battn = sl["attnT"]; Qm, bQm = sl["Qm"]; Qp, bQp = sl["Qp"]; Kg, bKg = sl["Kg"]; kt, bkt = sl["kt"]
            IYT, bIYT = sl["IYT"]; negW, bnegW = sl["negW"]; vnew, bvnew = sl["vnew"]; St, bSt = sl["St"]
            gcol = tg[:, d * 8 + h0:d * 8 + h0 + 4]
            bcol = tg[:, 16 + d * 8 + h0:16 + d * 8 + h0 + 4]
            eg = gs[:, 24 + h0:24 + h0 + 4]
            ekt = gs[:, 32 + h0:32 + h0 + 4]
            gte = gs[:, 40 + h0:40 + h0 + 4]
            B.op("pool", lambda e: e.tensor_tensor(out=A[:], in0=bcm(dc["U"][:]), in1=bc3(gcol, 128), op=ALU.mult), reads=[bg] + cbufs, writes=[bA])
            pD, bpD = psr.next()
            for u in range(4):
                B.op("pe", lambda e, u=u: e.matmul(pD[:, u * 128:(u + 1) * 128], lhsT=dc["S"][:], rhs=A[:, u, :], start=True, stop=True),
                     reads=[bA] + cbufs, writes=[bpD], inc=(u == 3))
            B.op("act", lambda e: e.activation(out=Bt[:].rearrange("p u l -> p (u l)"), in_=pD[:, :], func=AF.Exp), reads=[bpD], writes=[bBt])
            B.op("pool", lambda e: e.tensor_tensor(out=A[:], in0=Bt[:], in1=bcm(dc["incl"][:]), op=ALU.mult), reads=[bBt] + cbufs, writes=[bA])
            B.op("pool", lambda e: e.tensor_tensor(out=Ct[:], in0=Bt[:], in1=bcm(dc["strict"][:]), op=ALU.mult), reads=[bBt] + cbufs, writes=[bCt])
            B.op("pool", lambda e: e.tensor_tensor(out=Ct[:], in0=Ct[:], in1=bc3(bcol, 128), op=ALU.mult), reads=[bCt, bg], writes=[bCt])
            pKK, bpKK = psr.next()
            pQK, bpQK = psr.next()
            pKt, bpKt = psr.next()
            pKtb = pKt[:].bitcast(BF16)
            for u in range(4):
                B.op("pe", lambda e, u=u: e.matmul(pKK[:, u * 128:(u + 1) * 128], lhsT=tk[:, h0 + u, :], rhs=tk[:, h0 + u, :], start=True, stop=True),
                     reads=[bk], writes=[bpKK], inc=(u == 3))
            for u in range(4):
                B.op("pe", lambda e, u=u: e.matmul(pQK[:, u * 128:(u + 1) * 128], lhsT=tk[:, h0 + u, :], rhs=tq[:, h0 + u, :], start=True, stop=True),
                     reads=[bk, bq], writes=[bpQK], inc=(u == 3))
            for u in range(4):
                B.op("pe", lambda e, u=u: e.transpose(pKtb[:, u * 128:(u + 1) * 128], tk[:, h0 + u, :], self.ident_b[:]),
                     reads=[bk] + cbufs, writes=[bpKt], inc=(u == 3))
            B.op("dve", lambda e: e.tensor_tensor(out=attnT[:], in0=pQK[:, :].rearrange("p (u l) -> p u l", u=4), in1=A[:], op=ALU.mult),
                 reads=[bpQK, bA], writes=[battn])
            B.op("dve", lambda e: e.tensor_tensor(out=Qm[:], in0=pKK[:, :].rearrange("p (u l) -> p u l", u=4), in1=Ct[:], op=ALU.mult),
                 reads=[bpKK, bCt], writes=[bQm])
            B.op("pool", lambda e: e.tensor_tensor(out=Qp[:], in0=Qm[:], in1=bcm(self.ident_b[:]), op=ALU.add), reads=[bQm] + cbufs, writes=[bQp])
            B.op("dve", lambda e: e.tensor_tensor(out=Kg[:], in0=pKtb[:, 0:512].rearrange("p (u l) -> p u l", u=4), in1=bc3(eg, 128), op=ALU.mult),
                 reads=[bpKt, bgs], writes=[bKg])
            B.op("dve", lambda e: e.tensor_tensor(out=kt[:], in0=pKtb[:, 0:512].rearrange("p (u l) -> p u l", u=4), in1=bc3(ekt, 128), op=ALU.mult),
                 reads=[bpKt, bgs], writes=[bkt])
            yield
            Gc = None
            Hc = None
            for lev in range(7):
                sm = smask[:, d * 7 + lev, :]
                pY, bpY = psr.next()
                for u in range(4):
                    rhsH = self.ident_b[:] if Hc is None else Hc[0][:, u, :]
                    B.op("pe", lambda e, u=u, rhsH=rhsH: e.matmul(pY[:, u * 128:(u + 1) * 128], lhsT=Qp[:, u, :], rhs=rhsH, start=True, stop=True),
                         reads=[bQp] + cbufs + ([] if Hc is None else [Hc[1]]), writes=[bpY], inc=(u == 3))
                B.op("dve", lambda e, sm=sm: e.tensor_tensor(out=IYT[:], in0=pY[:, :].rearrange("p (u l) -> p u l", u=4), in1=bcm(sm), op=ALU.mult),
                     reads=[bpY] + cbufs, writes=[bIYT])
                yield
                Gn = sl["G%d" % (lev % 2)]
                Hn = sl["H%d" % (lev % 2)]
                pG, bpG = psr.next()
                for u in range(4):
                    rhsG = self.ident_b[:] if Gc is None else Gc[0][:, u, :]
                    B.op("pe", lambda e, u=u, rhsG=rhsG: e.matmul(pG[:, u * 128:(u + 1) * 128], lhsT=IYT[:, u, :], rhs=rhsG, start=True, stop=True),
                         reads=[bIYT] + cbufs + ([] if Gc is None else [Gc[1]]), writes=[bpG], inc=(u == 3))
                B.op("act", lambda e, Gn=Gn: e.activation(out=Gn[0][:].rearrange("p u l -> p (u l)"), in_=pG[:, :], func=AF.Identity), reads=[bpG], writes=[Gn[1]])
                if lev < 6:
                    pH, bpH = psr.next()
                    for u in range(4):
                        lhsG = self.ident_b[:] if Gc is None else Gc[0][:, u, :]
                        B.op("pe", lambda e, u=u, lhsG=lhsG: e.matmul(pH[:, u * 128:(u + 1) * 128], lhsT=lhsG, rhs=IYT[:, u, :], start=True, stop=True),
                             reads=[bIYT] + cbufs + ([] if Gc is None else [Gc[1]]), writes=[bpH], inc=(u == 3))
                    B.op("act", lambda e, Hn=Hn: e.activation(out=Hn[0][:].rearrange("p u l -> p (u l)"), in_=pH[:, :], func=AF.Identity), reads=[bpH], writes=[Hn[1]])
                    Hc = Hn
                Gc = Gn
                yield
            G, bG = Gc
            pW, bpW = psr.next()
            for u in range(4):
                B.op("pe", lambda e, u=u: e.matmul(pW[:, u * 128:(u + 1) * 128], lhsT=Kg[:, u, :], rhs=G[:, u, :], start=True, stop=True),
                     reads=[bKg, bG], writes=[bpW], inc=(u == 3))
            B.op("act", lambda e: e.activation(out=negW[:].rearrange("p u l -> p (u l)"), in_=pW[:, :], func=AF.Identity, scale=-1.0), reads=[bpW], writes=[bnegW])
            yield
            sbt, bsb = Sb_cur[d][hg]
            pV, bpV = psr.next()
            for u in range(4):
                B.op("pe", lambda e, u=u: e.matmul(pV[:, u * 128:(u + 1) * 128], lhsT=G[:, u, :], rhs=tv[:, (h0 + u) * 128:(h0 + u + 1) * 128], start=True, stop=False),
                     reads=[bG, bv], writes=[bpV], inc=False)
                B.op("pe", lambda e, u=u: e.matmul(pV[:, u * 128:(u + 1) * 128], lhsT=negW[:, u, :], rhs=sbt[:, u, :], start=False, stop=True),
                     reads=[bnegW, bsb], writes=[bpV], inc=(u == 3))
            B.op("dve", lambda e: e.tensor_tensor(out=vnew[:], in0=pV[:, :].rearrange("p (u l) -> p u l", u=4), in1=bc3(bcol, 128), op=ALU.mult),
                 reads=[bpV, bg], writes=[bvnew])
            yield
            if need_o:
                pO1, bpO1 = psr.next()
                for u in range(4):
                    B.op("pe", lambda e, u=u: e.matmul(pO1[:, u * 128:(u + 1) * 128], lhsT=tq[:, h0 + u, :], rhs=sbt[:, u, :], start=True, stop=True),
                         reads=[bq, bsb], writes=[bpO1], inc=(u == 3))
                o1, bo1 = ring_o1.next()
                B.op("dve", lambda e: e.tensor_tensor(out=o1[:], in0=pO1[:, :].rearrange("p (u l) -> p u l", u=4), in1=bc3(eg, 128), op=ALU.mult),
                     reads=[bpO1, bgs], writes=[bo1])
                pO2, bpO2 = psr.next()
                for u in range(4):
                    B.op("pe", lambda e, u=u: e.matmul(pO2[:, u * 128:(u + 1) * 128], lhsT=attnT[:, u, :], rhs=vnew[:, u, :], start=True, stop=True),
                         reads=[battn, bvnew], writes=[bpO2], inc=(u == 3))
                o, bo = ring_o.next()
                B.op("dve", lambda e: e.tensor_tensor(out=o[:], in0=pO2[:, :].rearrange("p (u l) -> p u l", u=4), in1=o1[:], op=ALU.add),
                     reads=[bpO2, bo1], writes=[bo])
                dst = (self.OF if d == 0 else self.OB)[c - 2]
                B.dma("sp", dst[:, hg * 512:(hg + 1) * 512], o[:].rearrange("p u l -> p (u l)"), bo, reads=[bo])
            pS, bpS = psr.next()
            for u in range(4):
                B.op("pe", lambda e, u=u: e.matmul(pS[:, u * 128:(u + 1) * 128], lhsT=kt[:, u, :], rhs=vnew[:, u, :], start=True, stop=True),
                     reads=[bkt, bvnew], writes=[bpS], inc=(u == 3))
            Sg = S[d][:, h0:h0 + 4, :]
            B.op("pool", lambda e: e.tensor_tensor(out=St[:], in0=Sg, in1=bc3(gte, 128), op=ALU.mult), reads=[bS[d][hg], bgs], writes=[bSt])
            B.op("dve", lambda e: e.tensor_tensor(out=Sg, in0=pS[:, :].rearrange("p (u l) -> p u l", u=4), in1=St[:], op=ALU.add),
                 reads=[bpS, bSt], writes=[bS[d][hg]])
            nsb, bnsb = Sb[d][hg].next()
            B.op("act", lambda e: e.activation(out=nsb[:], in_=Sg, func=AF.Identity), reads=[bS[d][hg]], writes=[bnsb])
            Sb_cur[d][hg] = (nsb, bnsb)
            yield

        self._b_env = dict(data=data, dirc=dirc, cbufs=cbufs, psr=psr, order=order, out_lo=out_lo, out_hi=out_hi, bc3=bc3, bcm=bcm,
                           C=C, bC=bC, Cb=Cb, Cb_cur=Cb_cur, nst=nst, nbf=nbf, n_cur=n_cur, nb_cur=nb_cur, mst=mst, m_cur=m_cur,
                           ring_num=ring_num, ring_h=ring_h)
        ml_group = self.make_ml_group()

        from collections import deque
        pending = deque()
        for step in range(nsteps):
            dirs = [d for d in range(2) if not (d == 0 and order[0][step] >= out_hi)]
            for d in dirs:
                pending.append(("load", step, d))
            for hg in range(2):
                for d in dirs:
                    if not dbg.get("b_no_gdn"):
                        pending.append(("gdn", step, d, hg))
            for d in dirs:
                if not dbg.get("b_no_ml"):
                    pending.append(("ml", step, d))
        free_g = list(range(NG))
        free_m = list(range(NM))
        done = set()
        active = []
        loaded = set()
        while pending or active:
            while pending:
                it = pending[0]
                if it[0] == "load":
                    _, step, d = it
                    load_step(step, d)
                    shared_pre(step, d)
                    pending.popleft()
                    continue
                if it[0] == "gdn":
                    _, step, d, hg = it
                    key_prev = ("gdn", step - 1, d, hg)
                    if (step > 0 and key_prev not in done) or not free_g:
                        break
                    si = free_g.pop(0)
                    active.append((it, gdn_group(step, d, hg, gslots[si]), ("g", si)))
                    pending.popleft()
                    continue
                if it[0] == "ml":
                    _, step, d = it
                    key_prev = ("ml", step - 1, d)
                    if (step > 0 and key_prev not in done) or not free_m:
                        break
                    si = free_m.pop(0)
                    active.append((it, ml_group(step, d, mslots[si]), ("m", si)))
                    pending.popleft()
                    continue
            for ent in list(active):
                it, gen, (kind, si) = ent
                try:
                    next(gen)
                except StopIteration:
                    active.remove(ent)
                    done.add(it)
                    (free_g if kind == "g" else free_m).append(si)
        B.barrier()
        st.close()

    def make_ml_group(self):
        B = self.B
        env = self._b_env
        data, dirc, cbufs, psr = env["data"], env["dirc"], env["cbufs"], env["psr"]
        bc3, bcm = env["bc3"], env["bcm"]
        C, bC, Cb, Cb_cur = env["C"], env["bC"], env["Cb"], env["Cb_cur"]
        nst, nbf, n_cur, nb_cur, mst, m_cur = env["nst"], env["nbf"], env["n_cur"], env["nb_cur"], env["mst"], env["m_cur"]
        ring_num, ring_h = env["ring_num"], env["ring_h"]
        out_lo, out_hi = env["out_lo"], env["out_hi"]

        def bc3n(ap2, n):
            return ap2.unsqueeze(2).to_broadcast([128, ap2.shape[1], n])

        def ml_group(step, d, sl):
            dd = data[(step, d)]
            dc = dirc[d]
            c = dd["c"]
            need_o = out_lo <= c < out_hi
            tg, bg = dd["GT"]; gs, bgs = dd["gs"]
            mq, bmq = dd["MQ"]; mk, bmk = dd["MK"]; mv, bmv = dd["MV"]
            X, bX = sl["X"]; Y, bY = sl["Y"]; Pm, bPm = sl["Pm"]; PT, bPT = sl["PT"]; Kw, bKw = sl["Kw"]
            sm, bsm = sl["sm"]; Ct, bCt = sl["Ct"]
            bcc = gs[:, 8:12]
            blast = gs[:, 20:24]
            cvec = gs[:, 48:52]
            mprev, bmprev = m_cur[d]
            B.op("pool", lambda e: e.tensor_tensor(out=X[:], in0=bcm(self.ident_f[:]), in1=bc3(cvec, 128), op=ALU.mult), reads=[bgs] + cbufs, writes=[bX])
            pC, bpC = psr.next()
            for u in range(4):
                B.op("pe", lambda e, u=u: e.matmul(pC[:, u * 128:(u + 1) * 128], lhsT=self.ones_f[:], rhs=X[:, u, :], start=True, stop=True),
                     reads=[bX] + cbufs, writes=[bpC], inc=(u == 3))
            B.op("dve", lambda e: e.tensor_tensor(out=Y[:], in0=pC[:, :].rearrange("p (u l) -> p u l", u=4), in1=bc3(bcc, 128), op=ALU.add),
                 reads=[bpC, bgs], writes=[bY])
            B.op("pool", lambda e: e.tensor_tensor(out=Y[:], in0=Y[:], in1=bcm(dc["MB"][:]), op=ALU.add), reads=[bY] + cbufs, writes=[bY])
            B.op("dve", lambda e: e.tensor_reduce(out=sm[:, 0:4], in_=Y[:], axis=AX.X, op=ALU.max), reads=[bY], writes=[bsm])
            B.op("dve", lambda e: e.tensor_tensor(out=sm[:, 4:8], in0=bcc, in1=mprev[:, 0:4], op=ALU.add), reads=[bgs, bmprev], writes=[bsm])
            B.op("dve", lambda e: e.tensor_tensor(out=sm[:, 8:12], in0=sm[:, 0:4], in1=sm[:, 4:8], op=ALU.max), reads=[bsm], writes=[bsm])
            B.op("dve", lambda e: e.tensor_scalar(out=sm[:, 12:16], in0=sm[:, 8:12], scalar1=-1.0, scalar2=None, op0=ALU.mult), reads=[bsm], writes=[bsm])
            if need_o:
                pQK, bpQK = psr.next()
                for u in range(4):
                    B.op("pe", lambda e, u=u: e.matmul(pQK[:, u * 128:(u + 1) * 128], lhsT=mq[:, u, :], rhs=mk[:, u, :], start=True, stop=True),
                         reads=[bmq, bmk], writes=[bpQK], inc=(u == 3))
                for u in range(4):
                    B.op("act", lambda e, u=u: e.activation(out=X[:, u, :], in_=Y[:, u, :], func=AF.Exp, bias=sm[:, 12 + u:13 + u]), reads=[bY, bsm], writes=[bX])
                B.op("dve", lambda e: e.tensor_tensor(out=Pm[:], in0=pQK[:, :].rearrange("p (u l) -> p u l", u=4), in1=X[:], op=ALU.mult),
                     reads=[bpQK, bX], writes=[bPm])
            yield
            if need_o:
                pT, bpT = psr.next()
                pTb = pT[:].bitcast(BF16)
                for u in range(4):
                    B.op("pe", lambda e, u=u: e.transpose(pTb[:, u * 128:(u + 1) * 128], Pm[:, u, :], self.ident_b[:]), reads=[bPm] + cbufs, writes=[bpT], inc=(u == 3))
                B.op("act", lambda e: e.activation(out=PT[:].rearrange("p u l -> p (u l)"), in_=pTb[:, 0:512], func=AF.Identity), reads=[bpT], writes=[bPT])
                B.op("dve", lambda e: e.tensor_tensor(out=sm[:, 16:20], in0=sm[:, 4:8], in1=sm[:, 8:12], op=ALU.subtract), reads=[bsm], writes=[bsm])
                B.op("act", lambda e: e.activation(out=sm[:, 16:20], in_=sm[:, 16:20], func=AF.Exp), reads=[bsm], writes=[bsm])
                B.op("act", lambda e: e.activation(out=sm[:, 20:24], in_=sm[:, 12:16], func=AF.Exp), reads=[bsm], writes=[bsm])
                yield
                cbt, bcb = Cb_cur[d]
                nbt, bnb = nb_cur[d]
                num, bnum = ring_num.next()
                hh, bhh = ring_h.next()
                for pr in range(2):
                    pN1, bpN1 = psr.next()
                    pN2, bpN2 = psr.next()
                    for uu in range(2):
                        u = pr * 2 + uu
                        B.op("pe", lambda e, u=u, uu=uu, pN1=pN1: e.matmul(pN1[:, uu * 256:(uu + 1) * 256], lhsT=mq[:, u, :], rhs=cbt[:, u, :], start=True, stop=True),
                             reads=[bmq, bcb], writes=[bpN1], inc=(uu == 1))
                    for uu in range(2):
                        u = pr * 2 + uu
                        B.op("pe", lambda e, u=u, uu=uu, pN2=pN2: e.matmul(pN2[:, uu * 256:(uu + 1) * 256], lhsT=PT[:, u, :], rhs=mv[:, u * 256:(u + 1) * 256], start=True, stop=True),
                             reads=[bPT, bmv], writes=[bpN2], inc=(uu == 1))
                    B.op("dve", lambda e, pr=pr, pN1=pN1: e.tensor_tensor(out=num[:, pr * 2:pr * 2 + 2, :], in0=pN1[:, :].rearrange("p (u l) -> p u l", u=2),
                                                                         in1=bc3n(sm[:, 16 + pr * 2:18 + pr * 2], 256), op=ALU.mult), reads=[bpN1, bsm], writes=[bnum])
                    B.op("dve", lambda e, pr=pr, pN2=pN2: e.tensor_tensor(out=num[:, pr * 2:pr * 2 + 2, :], in0=pN2[:, :].rearrange("p (u l) -> p u l", u=2),
                                                                         in1=num[:, pr * 2:pr * 2 + 2, :], op=ALU.add), reads=[bpN2, bnum], writes=[bnum])
                pDn, bpDn = psr.next()
                for u in range(4):
                    B.op("pe", lambda e, u=u: e.matmul(pDn[:, u:u + 1], lhsT=mq[:, u, :], rhs=nbt[:, u:u + 1], start=True, stop=True), reads=[bmq, bnb], writes=[bpDn], inc=False)
                for u in range(4):
                    B.op("pe", lambda e, u=u: e.matmul(pDn[:, 4 + u:5 + u], lhsT=PT[:, u, :], rhs=self.ones_b[:, 0:1], start=True, stop=True),
                         reads=[bPT] + cbufs, writes=[bpDn], inc=(u == 3))
                B.op("dve", lambda e: e.tensor_tensor(out=sm[:, 24:28], in0=pDn[:, 0:4], in1=sm[:, 16:20], op=ALU.mult), reads=[bpDn, bsm], writes=[bsm])
                B.op("dve", lambda e: e.tensor_tensor(out=sm[:, 24:28], in0=pDn[:, 4:8], in1=sm[:, 24:28], op=ALU.add), reads=[bpDn, bsm], writes=[bsm])
                B.op("dve", lambda e: e.tensor_tensor(out=sm[:, 24:28], in0=sm[:, 24:28], in1=sm[:, 24:28], op=ALU.mult), reads=[bsm], writes=[bsm])
                B.op("dve", lambda e: e.tensor_tensor(out=sm[:, 28:32], in0=sm[:, 20:24], in1=sm[:, 20:24], op=ALU.mult), reads=[bsm], writes=[bsm])
                B.op("dve", lambda e: e.tensor_tensor(out=sm[:, 24:28], in0=sm[:, 24:28], in1=sm[:, 28:32], op=ALU.max), reads=[bsm], writes=[bsm])
                B.op("pool", lambda e: e.tensor_tensor(out=sm[:, 28:32], in0=sm[:, 24:28], in1=self.nhalf[:, 0:4], op=ALU.pow), reads=[bsm] + cbufs, writes=[bsm])
                B.op("dve", lambda e: e.tensor_tensor(out=hh[:], in0=num[:], in1=bc3n(sm[:, 28:32], 256), op=ALU.mult), reads=[bnum, bsm], writes=[bhh])
                dst = (self.HF if d == 0 else self.HB)[c - 2]
                B.dma("sp", dst[:, :], hh[:].rearrange("p u l -> p (u l)"), bhh, reads=[bhh])
                yield
            pSel, bpSel = psr.next()
            B.op("pe", lambda e: e.matmul(pSel[:, 0:4], lhsT=dc["SEL"][:], rhs=sm[:, 8:12], start=True, stop=True), reads=[bsm] + cbufs, writes=[bpSel])
            mnew, bmnew = mst[d].next()
            B.op("act", lambda e: e.activation(out=mnew[:], in_=pSel[:, 0:4], func=AF.Identity), reads=[bpSel], writes=[bmnew])
            B.op("dve", lambda e: e.tensor_tensor(out=sm[:, 32:36], in0=cvec, in1=blast, op=ALU.add), reads=[bgs], writes=[bsm])
            B.op("dve", lambda e: e.tensor_tensor(out=sm[:, 32:36], in0=sm[:, 32:36], in1=mnew[:], op=ALU.subtract), reads=[bsm, bmnew], writes=[bsm])
            B.op("dve", lambda e: e.tensor_tensor(out=sm[:, 36:40], in0=blast, in1=mprev[:, 0:4], op=ALU.add), reads=[bgs, bmprev], writes=[bsm])
            B.op("dve", lambda e: e.tensor_tensor(out=sm[:, 36:40], in0=sm[:, 36:40], in1=mnew[:], op=ALU.subtract), reads=[bsm, bmnew], writes=[bsm])
            B.op("act", lambda e: e.activation(out=sm[:, 32:40], in_=sm[:, 32:40], func=AF.Exp), reads=[bsm], writes=[bsm])
            pKt, bpKt = psr.next()
            pKtb = pKt[:].bitcast(BF16)
            for u in range(4):
                B.op("pe", lambda e, u=u: e.transpose(pKtb[:, u * 128:(u + 1) * 128], mk[:, u, :], self.ident_b[:]), reads=[bmk] + cbufs, writes=[bpKt], inc=(u == 3))
            B.op("dve", lambda e: e.tensor_tensor(out=Kw[:], in0=pKtb[:, 0:512].rearrange("p (u l) -> p u l", u=4), in1=bc3(sm[:, 32:36], 128), op=ALU.mult),
                 reads=[bpKt, bsm], writes=[bKw])
            m_cur[d] = (mnew, bmnew)
            yield
            for pr in range(2):
                pC2, bpC2 = psr.next()
                for uu in range(2):
                    u = pr * 2 + uu
                    B.op("pe", lambda e, u=u, uu=uu, pC2=pC2: e.matmul(pC2[:, uu * 256:(uu + 1) * 256], lhsT=Kw[:, u, :], rhs=mv[:, u * 256:(u + 1) * 256], start=True, stop=True),
                         reads=[bKw, bmv], writes=[bpC2], inc=(uu == 1))
                B.op("pool", lambda e, pr=pr: e.tensor_tensor(out=Ct[:, pr * 2:pr * 2 + 2, :], in0=C[d][:, pr * 2:pr * 2 + 2, :], in1=bc3n(sm[:, 36 + pr * 2:38 + pr * 2], 256), op=ALU.mult),
                     reads=[bC[d], bsm], writes=[bCt])
                B.op("dve", lambda e, pr=pr, pC2=pC2: e.tensor_tensor(out=C[d][:, pr * 2:pr * 2 + 2, :], in0=pC2[:, :].rearrange("p (u l) -> p u l", u=2), in1=Ct[:, pr * 2:pr * 2 + 2, :], op=ALU.add),
                     reads=[bpC2, bCt], writes=[bC[d]])
            pN, bpN = psr.next()
            for u in range(4):
                B.op("pe", lambda e, u=u: e.matmul(pN[:, u:u + 1], lhsT=Kw[:, u, :], rhs=self.ones_b[:, 0:1], start=True, stop=True), reads=[bKw] + cbufs, writes=[bpN], inc=(u == 3))
            nold, bnold = n_cur[d]
            nnew, bnnew = nst[d].next()
            B.op("dve", lambda e: e.tensor_tensor(out=nnew[:, 4:8], in0=nold[:, 0:4], in1=sm[:, 36:40], op=ALU.mult), reads=[bnold, bsm], writes=[bnnew])
            B.op("dve", lambda e: e.tensor_tensor(out=nnew[:, 0:4], in0=pN[:, 0:4], in1=nnew[:, 4:8], op=ALU.add), reads=[bpN, bnnew], writes=[bnnew])
            nbn, bnbn = nbf[d].next()
            B.op("act", lambda e: e.activation(out=nbn[:], in_=nnew[:, 0:4], func=AF.Identity), reads=[bnnew], writes=[bnbn])
            cbn, bcbn = Cb[d].next()
            B.op("act", lambda e: e.activation(out=cbn[:], in_=C[d][:], func=AF.Identity), reads=[bC[d]], writes=[bcbn])
            n_cur[d] = (nnew, bnnew)
            nb_cur[d] = (nbn, bnbn)
            Cb_cur[d] = (cbn, bcbn)
            yield

        return ml_group

    def phaseC1(self):
        B, nc, inp = self.B, self.nc, self.inp
        st = ExitStack()
        dbg = self.debug
        wo, bwo = self.load_w_bf16(st, "c_wo", inp["w_o"], 8, 4096, 8)
        wbg, bwbg = self.load_w_bf16(st, "c_wbg", inp["w_bg"], 8, 1024, 2)
        wbm, bwbm = self.load_w_bf16(st, "c_wbm", inp["w_bm"], 8, 1024, 2)
        wout, bwout = self.load_w_bf16(st, "c_wout", inp["w_out"], 8, 1024, 2)
        nwb = B.sb(st, "c_nwb", [128, 2, 1024], F32)
        bnwb = Buf("c_nwb")
        B.dma("sp", nwb[:, 0, :], inp["gnw_bc"][:, :], bnwb, writes=[bnwb])
        B.dma("sp", nwb[:, 1, :], inp["mnw_bc"][:, :], bnwb, writes=[bnwb])
        zero = B.sb(st, "c_zero", [64, 1024], F32)
        bzero = Buf("c_zero")
        B.op("pool", lambda e: e.memset(zero[:], 0.0), writes=[bzero])
        bX1 = Buf("X1")
        B.dma("sp", self.X1[0:64, :], zero[:], bzero, reads=[bzero], writes=[bX1])
        NS = 2
        xt = B.sb(st, "c_x", [128, NS, 1024], F32)
        bxts = [Buf("c_x%d" % i) for i in range(NS)]
        xn = B.sb(st, "c_xn", [128, NS, 1024], BF16); bxn = Buf("c_xn")
        sq = B.sb(st, "c_sq", [128, 24], F32); bsq = Buf("c_sq")
        junk = None; bjunk = None
        hxT = B.sb(st, "c_hxT", [128, 8, NS * 128], BF16); bhx = Buf("c_hxT")
        oa = B.sb(st, "c_oa", [128, 1024], F32); boa = Buf("c_oa")
        ob = B.sb(st, "c_ob", [128, 1024], F32); bob = Buf("c_ob")
        gt = B.sb(st, "c_gt", [128, 1024], F32); bgt = Buf("c_gt")
        osq = B.sb(st, "c_osq", [128, 1024], F32); bosq = Buf("c_osq")
        sm = B.sb(st, "c_sm", [128, 32], F32); bsm = Buf("c_sm")
        og = B.sb(st, "c_og", [128, 1024], BF16); bog = Buf("c_og")
        brT = [B.sb(st, "c_brT%d" % i, [128, 8, NS * 128], BF16) for i in range(2)]
        bbrT = [Buf("c_brT%d" % i) for i in range(2)]
        sg = Ring(B, st, "c_sg", 4, [128, NS * 128], F32)
        yt = Ring(B, st, "c_yt", 2, [128, NS * 128], F32)
        mT = B.sb(st, "c_mT", [128, 8, NS * 128], BF16); bmT = Buf("c_mT")
        tmp = Ring(B, st, "c_tmp", 2, [128, 512], F32)
        ptr = Ring(B, st, "c_ptr", 2, [128, 512], F32, psum=True)
        pmm = Ring(B, st, "c_pmm", 5, [128, 512], F32, psum=True)
        nt = OWN_T // 128
        sts = []
        i = 0
        while i < nt:
            ns = min(NS, nt - i)
            sts.append((i, ns))
            i += ns
        if dbg.get("c1_tiles"):
            sts = sts[: dbg["c1_tiles"]]
        for (t0, ns) in sts:
            n = ns * 128
            for s in range(ns):
                B.dma("sp", xt[:, s, :], inp["x"][(t0 + s) * 128:(t0 + s + 1) * 128, :], bxts[s], writes=[bxts[s]])
            self.norm_transpose(xt, bxts, ns, xn, bxn, sq, bsq, junk, bjunk, ptr, hxT, bhx, 0, 0)
            for br in range(2):
                nh, hd = (8, 128) if br == 0 else (4, 256)
                srcf, srcb = (self.OF, self.OB) if br == 0 else (self.HF, self.HB)
                for s in range(ns):
                    c = t0 + s
                    B.dma("sp", oa[:], srcf[c], boa, writes=[boa])
                    B.dma("sp", ob[:], srcb[c], bob, writes=[bob])
                    for hf in range(2):
                        p, bp = pmm.next()
                        for k in range(8):
                            B.op("pe", lambda e, k=k, p=p, s=s, hf=hf, br=br: e.matmul(
                                p[:], lhsT=hxT[:, k, s * 128:(s + 1) * 128], rhs=wo[:, k, br * 1024 + hf * 512: br * 1024 + (hf + 1) * 512],
                                start=(k == 0), stop=(k == 7)), reads=[bhx, bwo], writes=[bp], inc=(k == 7))
                        B.op("act", lambda e, p=p, hf=hf, br=br: e.activation(out=gt[:, hf * 512:(hf + 1) * 512], in_=p[:],
                                                                            func=(AF.Silu if br == 0 else AF.Sigmoid)), reads=[bp], writes=[bgt])
                    B.op("pool", lambda e, br=br: e.tensor_tensor(out=gt[:], in0=gt[:], in1=nwb[:, br, :], op=ALU.mult), reads=[bgt, bnwb], writes=[bgt])
                    B.op("dve", lambda e: e.tensor_tensor(out=oa[:], in0=oa[:], in1=ob[:], op=ALU.add), reads=[boa, bob], writes=[boa])
                    B.op("pool", lambda e: e.tensor_tensor(out=osq[:], in0=oa[:], in1=oa[:], op=ALU.mult), reads=[boa], writes=[bosq])
                    B.op("dve", lambda e, nh=nh: e.tensor_reduce(out=sm[:, 0:nh], in_=osq[:].rearrange("p (h e) -> p h e", h=nh), axis=AX.X, op=ALU.add),
                         reads=[bosq], writes=[bsm])
                    B.op("dve", lambda e, nh=nh, hd=hd: e.tensor_scalar(out=sm[:, 8:8 + nh], in0=sm[:, 0:nh], scalar1=float(1.0 / hd), scalar2=float(EPS),
                                                                      op0=ALU.mult, op1=ALU.add), reads=[bsm], writes=[bsm])
                    B.op("pool", lambda e, nh=nh: e.tensor_tensor(out=sm[:, 16:16 + nh], in0=sm[:, 8:8 + nh], in1=self.nhalf[:, 0:nh], op=ALU.pow),
                         reads=[bsm, self.cb], writes=[bsm])
                    B.op("dve", lambda e, nh=nh, hd=hd: e.tensor_tensor(out=osq[:].rearrange("p (h e) -> p h e", h=nh), in0=oa[:].rearrange("p (h e) -> p h e", h=nh),
                                                                      in1=sm[:, 16:16 + nh].unsqueeze(2).to_broadcast([128, nh, hd]), op=ALU.mult),
                         reads=[boa, bsm], writes=[bosq])
                    B.op("dve", lambda e: e.tensor_tensor(out=og[:], in0=osq[:], in1=gt[:], op=ALU.mult), reads=[bosq, bgt], writes=[bog])
                    p, bp = ptr.next()
                    pb = p[:].bitcast(BF16)
                    for k in range(8):
                        B.op("pe", lambda e, k=k, pb=pb: e.transpose(pb[:, k * 128:(k + 1) * 128], og[:, k * 128:(k + 1) * 128], self.ident_b[:]),
                             reads=[bog, self.cb], writes=[bp], inc=(k == 7))
                    B.op("act", lambda e, pb=pb, s=s, br=br: e.activation(out=brT[br][:, :, s * 128:(s + 1) * 128], in_=pb[:, 0:1024].rearrange("p (k t) -> p k t", k=8),
                                                                          func=AF.Identity), reads=[bp], writes=[bbrT[br]])
            for ncn in range(8):
                sgs = []
                for gi in range(2):
                    p, bp = pmm.next()
                    for k in range(8):
                        B.op("pe", lambda e, k=k, p=p, gi=gi, ncn=ncn: e.matmul(p[:, 0:n], lhsT=wo[:, k, 2048 + gi * 1024 + ncn * 128: 2048 + gi * 1024 + (ncn + 1) * 128],
                                                                               rhs=hxT[:, k, 0:n], start=(k == 0), stop=(k == 7)), reads=[bhx, bwo], writes=[bp], inc=(k == 7))
                    g_, bg_ = sg.next()
                    B.op("act", lambda e, p=p, g_=g_: e.activation(out=g_[:, 0:n], in_=p[:, 0:n], func=AF.Sigmoid), reads=[bp], writes=[bg_])
                    sgs.append((g_, bg_))
                ys = []
                for br, (w, bw) in enumerate(((wbg, bwbg), (wbm, bwbm))):
                    p, bp = pmm.next()
                    for k in range(8):
                        B.op("pe", lambda e, k=k, p=p, w=w, br=br, ncn=ncn: e.matmul(p[:, 0:n], lhsT=w[:, k, ncn * 128:(ncn + 1) * 128], rhs=brT[br][:, k, 0:n],
                                                                                    start=(k == 0), stop=(k == 7)), reads=[bbrT[br], bw], writes=[bp], inc=(k == 7))
                    ys.append((p, bp))
                y_, by_ = yt.next()
                B.op("dve", lambda e, y_=y_: e.tensor_tensor(out=y_[:, 0:n], in0=ys[0][0][:, 0:n], in1=sgs[0][0][:, 0:n], op=ALU.mult), reads=[ys[0][1], sgs[0][1]], writes=[by_])
                g1, bg1 = sgs[1]
                B.op("dve", lambda e, g1=g1: e.tensor_tensor(out=g1[:, 0:n], in0=ys[1][0][:, 0:n], in1=g1[:, 0:n], op=ALU.mult), reads=[ys[1][1], bg1], writes=[bg1])
                B.op("pool", lambda e, y_=y_, g1=g1, ncn=ncn: e.tensor_tensor(out=mT[:, ncn, 0:n], in0=y_[:, 0:n], in1=g1[:, 0:n], op=ALU.add), reads=[by_, bg1], writes=[bmT])
            for s in range(ns):
                for hf in range(2):
                    p, bp = pmm.next()
                    for k in range(8):
                        B.op("pe", lambda e, k=k, p=p, s=s, hf=hf: e.matmul(p[:], lhsT=mT[:, k, s * 128:(s + 1) * 128], rhs=wout[:, k, hf * 512:(hf + 1) * 512],
                                                                           start=(k == 0), stop=(k == 7)), reads=[bmT, bwout], writes=[bp], inc=(k == 7))
                    t_, bt_ = tmp.next()
                    B.op("dve", lambda e, p=p, t_=t_, hf=hf: e.tensor_tensor(out=t_[:], in0=p[:], in1=self.gate_bc[:, 0, hf * 512:(hf + 1) * 512], op=ALU.mult),
                         reads=[bp, self.bgate], writes=[bt_])
                    B.op("pool", lambda e, t_=t_, s=s, hf=hf: e.tensor_tensor(out=xt[:, s, hf * 512:(hf + 1) * 512], in0=xt[:, s, hf * 512:(hf + 1) * 512], in1=t_[:], op=ALU.add),
                         reads=[bt_, bxts[s]], writes=[bxts[s]])
                B.dma("sp", self.X1[64 + (t0 + s) * 128: 64 + (t0 + s + 1) * 128, :], xt[:, s, :], bxts[s], reads=[bxts[s]], writes=[bX1])
        B.barrier()
        st.close()

    def precast_wup(self):
        B = self.B
        self.WUPB = B.dram("WUPB", [44, 128, 8, 128], BF16)
        self.bwupb = Buf("WUPB")
        src = self.inp["w_up"].rearrange("(k p) (c j) -> c p k j", p=128, j=128)
        for c in range(44):
            B.dma("pool", self.WUPB[c], src[c], self.bwupb, writes=[self.bwupb])

    def phaseC2(self):
        B, nc, inp = self.B, self.nc, self.inp
        st = ExitStack()
        dbg = self.debug
        wd = B.sb(st, "d_wd", [128, 22, 1024], BF16)
        bwd = Buf("d_wd")
        wdv = inp["w_down"].rearrange("(c p) n -> p c n", p=128)
        for i in range(0, 22, 6):
            j = min(22, i + 6)
            B.dma("pool", wd[:, i:j, :], wdv[:, i:j, :], bwd, writes=[bwd])
        cw = B.sb(st, "d_cw", [128, 44, 9], F32)
        nob = B.sb(st, "d_nob", [128, 1024], F32)
        bsm0 = Buf("d_small")
        B.dma("sp", cw[:], inp["ffn_cw"][:, :, :], bsm0, writes=[bsm0])
        B.dma("sp", nob[:], inp["now_bc"][:, :], bsm0, writes=[bsm0])
        wup = Ring(B, st, "d_wup", 3, [128, 2, 8, 128], BF16)
        xt = B.sb(st, "d_x", [128, 5, 1024], F32)
        bxts = [Buf("d_x%d" % i) for i in range(5)]
        xn = B.sb(st, "d_xn", [128, 5, 1024], BF16); bxn = Buf("d_xn")
        sq = B.sb(st, "d_sq", [128, 24], F32); bsq = Buf("d_sq")
        junk = B.sb(st, "d_junk", [128, 1024], BF16); bjunk = Buf("d_junk")
        hxT = B.sb(st, "d_hxT", [128, 8, 640], BF16); bhx = Buf("d_hxT")
        upad = Ring(B, st, "d_up", 4, [128, 10, 66], F32)
        acc = Ring(B, st, "d_acc", 4, [128, 8, 64], F32)
        sgt = Ring(B, st, "d_sg", 2, [128, 512], F32)
        ctmp = Ring(B, st, "d_ctmp", 3, [128, 8, 64], F32)
        aT = B.sb(st, "d_aT", [128, 22, 512], BF16); baT = Buf("d_aT")
        xo = B.sb(st, "d_xo", [128, 4, 1024], F32)
        bxo = [Buf("d_xo%d" % i) for i in range(4)]
        t2 = Ring(B, st, "d_t2", 2, [128, 512], F32)
        sq2 = B.sb(st, "d_sq2", [128, 16], F32); bsq2 = Buf("d_sq2")
        ptr = Ring(B, st, "d_ptr", 2, [128, 512], F32, psum=True)
        pu = Ring(B, st, "d_pu", 4, [128, 512], F32, psum=True)
        pd = Ring(B, st, "d_pd", 2, [128, 512], F32, psum=True)
        for (u_, bu_) in upad.slots:
            B.op("pool", lambda e, u_=u_: e.memset(u_[:], 0.0), writes=[bu_])
        nblk = dbg.get("c2_blocks", 8)
        bX1 = Buf("X1r")
        for j in range(nblk):
            r0 = 512 * j
            for s in range(5):
                B.dma("sp", xt[:, s, :], self.X1[r0 + s * 128: r0 + (s + 1) * 128, :], bxts[s], writes=[bxts[s]])
            for s in range(4):
                B.dma("sp", xo[:, s, :], self.X1[r0 + 64 + s * 128: r0 + 64 + (s + 1) * 128, :], bxo[s], writes=[bxo[s]])
            self.norm_transpose(xt, bxts, 5, xn, bxn, sq, bsq, junk, bjunk, ptr, hxT, bhx, 0, 4)
            for c in range(22):
                w, bw = wup.next()
                B.dma("sp", w[:, 0], self.WUPB[c], bw, reads=[self.bwupb], writes=[bw])
                B.dma("sp", w[:, 1], self.WUPB[22 + c], bw, reads=[self.bwupb], writes=[bw])
                accs = []
                for part in range(2):
                    ch = c + 22 * part
                    p1, bp1 = pu.next()
                    p2, bp2 = pu.next()
                    for k in range(8):
                        B.op("pe", lambda e, k=k, p1=p1, w=w, part=part: e.matmul(p1[:], lhsT=w[:, part, k, :], rhs=hxT[:, k, 0:512], start=(k == 0), stop=(k == 7)),
                             reads=[bw, bhx], writes=[bp1], inc=(k == 7))
                    for k in range(8):
                        B.op("pe", lambda e, k=k, p2=p2, w=w, part=part: e.matmul(p2[:, 0:128], lhsT=w[:, part, k, :], rhs=hxT[:, k, 512:640], start=(k == 0), stop=(k == 7)),
                             reads=[bw, bhx], writes=[bp2], inc=(k == 7))
                    u_, bu_ = upad.next()
                    B.op("act", lambda e, u_=u_, p1=p1: e.activation(out=u_[:, 0:8, 1:65], in_=p1[:].rearrange("p (r c) -> p r c", c=64), func=AF.Identity),
                         reads=[bp1], writes=[bu_])
                    B.op("act", lambda e, u_=u_, p2=p2: e.activation(out=u_[:, 8:10, 1:65], in_=p2[:, 0:128].rearrange("p (r c) -> p r c", c=64), func=AF.Identity),
                         reads=[bp2], writes=[bu_])
                    if j == 0:
                        B.op("pool", lambda e, u_=u_: e.memset(u_[:, 0:1, :], 0.0), writes=[bu_])
                    a_, ba_ = acc.next()
                    first = True
                    for dr in range(3):
                        for dc_ in range(3):
                            wsc = cw[:, ch, dr * 3 + dc_: dr * 3 + dc_ + 1]
                            src = u_[:, dr:dr + 8, dc_:dc_ + 64]
                            if part == 0:
                                if first:
                                    B.op("dve", lambda e, a_=a_, src=src, wsc=wsc: e.tensor_scalar(out=a_[:], in0=src, scalar1=wsc, scalar2=None, op0=ALU.mult),
                                         reads=[bu_, bsm0], writes=[ba_])
                                else:
                                    B.op("dve", lambda e, a_=a_, src=src, wsc=wsc: e.scalar_tensor_tensor(out=a_[:], in0=src, scalar=wsc, in1=a_[:], op0=ALU.mult, op1=ALU.add),
                                         reads=[bu_, bsm0, ba_], writes=[ba_])
                            else:
                                if first:
                                    B.op("act", lambda e, a_=a_, src=src, wsc=wsc: e.activation(out=a_[:], in_=src, func=AF.Identity, scale=wsc), reads=[bu_, bsm0], writes=[ba_])
                                else:
                                    c_, bc_ = ctmp.next()
                                    B.op("act", lambda e, c_=c_, src=src, wsc=wsc: e.activation(out=c_[:], in_=src, func=AF.Identity, scale=wsc), reads=[bu_, bsm0], writes=[bc_])
                                    B.op("pool", lambda e, a_=a_, c_=c_: e.tensor_tensor(out=a_[:], in0=a_[:], in1=c_[:], op=ALU.add), reads=[ba_, bc_], writes=[ba_])
                            first = False
                    accs.append((a_, ba_))
                s_, bs_ = sgt.next()
                B.op("act", lambda e, s_=s_: e.activation(out=s_[:], in_=accs[0][0][:].rearrange("p r c -> p (r c)"), func=AF.Silu), reads=[accs[0][1]], writes=[bs_])
                B.op("dve", lambda e, s_=s_, c=c: e.tensor_tensor(out=aT[:, c, :], in0=s_[:], in1=accs[1][0][:].rearrange("p r c -> p (r c)"), op=ALU.mult),
                     reads=[bs_, accs[1][1]], writes=[baT])
            for s in range(4):
                for hf in range(2):
                    p, bp = pd.next()
                    for c in range(22):
                        B.op("pe", lambda e, c=c, p=p, s=s, hf=hf: e.matmul(p[:], lhsT=aT[:, c, s * 128:(s + 1) * 128], rhs=wd[:, c, hf * 512:(hf + 1) * 512],
                                                                           start=(c == 0), stop=(c == 21)), reads=[baT, bwd], writes=[bp], inc=(c == 21))
                    t_, bt_ = t2.next()
                    B.op("dve", lambda e, p=p, t_=t_, hf=hf: e.tensor_tensor(out=t_[:], in0=p[:], in1=self.gate_bc[:, 1, hf * 512:(hf + 1) * 512], op=ALU.mult),
                         reads=[bp, self.bgate], writes=[bt_])
                    B.op("dve", lambda e, t_=t_, s=s, hf=hf: e.tensor_tensor(out=xo[:, s, hf * 512:(hf + 1) * 512], in0=xo[:, s, hf * 512:(hf + 1) * 512], in1=t_[:], op=ALU.add),
                         reads=[bt_, bxo[s]], writes=[bxo[s]])
                B.op("act", lambda e, s=s: e.activation(out=junk[:], in_=xo[:, s, :], func=AF.Square, accum_out=sq2[:, s:s + 1]), reads=[bxo[s]], writes=[bjunk, bsq2])
                B.op("dve", lambda e, s=s: e.tensor_scalar(out=sq2[:, 4 + s:5 + s], in0=sq2[:, s:s + 1], scalar1=float(D * EPS), scalar2=None, op0=ALU.add), reads=[bsq2], writes=[bsq2])
                B.op("pool", lambda e, s=s: e.tensor_tensor(out=sq2[:, 8 + s:9 + s], in0=sq2[:, 4 + s:5 + s], in1=self.nhalf[:, 0:1], op=ALU.pow), reads=[bsq2, self.cb], writes=[bsq2])
                B.op("dve", lambda e, s=s: e.scalar_tensor_tensor(out=xo[:, s, :], in0=xo[:, s, :], scalar=sq2[:, 8 + s:9 + s], in1=nob[:], op0=ALU.mult, op1=ALU.mult),
                     reads=[bxo[s], bsq2, bsm0], writes=[bxo[s]])
                B.op("act", lambda e, s=s: e.activation(out=xo[:, s, :], in_=xo[:, s, :], func=AF.Identity, scale=32.0), reads=[bxo[s]], writes=[bxo[s]])
                B.dma("sp", self.out[j * 512 + s * 128: j * 512 + (s + 1) * 128, :], xo[:, s, :], bxo[s], reads=[bxo[s]])
        B.barrier()
        st.close()


def build_program(debug=None):
    P = Prog(debug=debug)
    P.precast_wup()
    P.phase0()
    P.phaseA()
    P.phaseB()
    P.phaseC1()
    P.phaseC2()
    P.top.close()
    return P.B.finish(), P


_CACHE = {}


def kernel(**inputs):
    inp = {k: np.asarray(v) for k, v in inputs.items()}
    if "nc" not in _CACHE:
        _CACHE["nc"] = build_program()[0]
    nc = _CACHE["nc"]
    in_maps = [prep_core(inp, core) for core in range(8)]
    res = run_bass_kernel_spmd(nc, in_maps, core_ids=list(range(8)))
    out = np.empty((4, T, D), np.float32)
    for core in range(8):
        o = np.asarray(res.results[core]["out"], np.float32)
        b = core // 2
        if core % 2 == 0:
            out[b, 0:4096] = o
        else:
            out[b, 4096:8192] = o[::-1]
    return out
```

```python
import numpy as np
from contextlib import ExitStack

import concourse.bass as bass
import concourse.mybir as mybir
from concourse.bass_utils import run_bass_kernel_spmd

F32 = mybir.dt.float32
BF16 = mybir.dt.bfloat16
AF = mybir.ActivationFunctionType
ALU = mybir.AluOpType
AX = mybir.AxisListType

D = 1024
T = 8192
TC = 256
KD = 8
EPS = 1e-6
NEG = -1.0e30


class Buf:
    __slots__ = ("name", "w", "r", "dsem")

    def __init__(self, name):
        self.name = name
        self.w = None
        self.r = {}
        self.dsem = None


class Builder:
    def __init__(self):
        self.nc = bass.Bass("TRN2", target_bir_lowering=False)
        nc = self.nc
        self.es = ExitStack()
        self.es.enter_context(nc.allow_low_precision("bf16 matmul operands, fp32 accumulation"))
        self.engs = {"pe": nc.tensor, "act": nc.scalar, "dve": nc.vector, "pool": nc.gpsimd, "sp": nc.sync}
        self.sems = {}
        self.cnt = {}
        self.seen = {e: {} for e in self.engs}
        for e in self.engs:
            self.sems[e] = self.es.enter_context(nc.semaphore("s_" + e))
            self.cnt[e] = 0
        self.ndsem = 0
        self.nins = 0

    def sb(self, stack, name, shape, dt):
        return stack.enter_context(self.nc.sbuf_tensor(name, list(shape), dt))

    def ps(self, stack, name, shape, dt=F32):
        return stack.enter_context(self.nc.psum_tensor(name, list(shape), dt))

    def dram(self, name, shape, dt, kind="Internal"):
        return self.nc.dram_tensor(name, list(shape), dt, kind=kind).ap()

    def new_dsem(self):
        k = "d%d" % self.ndsem
        self.ndsem += 1
        self.sems[k] = self.es.enter_context(self.nc.semaphore(k))
        self.cnt[k] = 0
        return k

    def _deps(self, eng, reads, writes):
        deps = {}

        def add(k, v):
            if deps.get(k, 0) < v:
                deps[k] = v

        for b in reads:
            if b.w is not None:
                add(*b.w)
        for b in writes:
            if b.w is not None and b.w[0] != eng:
                add(*b.w)
            for k, v in b.r.items():
                if k != eng:
                    add(k, v)
        return deps

    def _emit_waits(self, eng, deps):
        e = self.engs[eng]
        seen = self.seen[eng]
        for k, v in deps.items():
            if seen.get(k, 0) >= v:
                continue
            assert v <= self.cnt[k], "wait on %s=%d never reached (issued %d)" % (k, v, self.cnt[k])
            e.wait_ge(self.sems[k], v)
            seen[k] = v

    def op(self, eng, fn, reads=(), writes=(), inc=True):
        self._emit_waits(eng, self._deps(eng, reads, writes))
        ins = fn(self.engs[eng])
        self.nins += 1
        if inc:
            self.cnt[eng] += 1
            ins.then_inc(self.sems[eng], 1)
            tok = (eng, self.cnt[eng])
        else:
            tok = (eng, self.cnt[eng] + 1)
        for b in reads:
            if b.r.get(eng, 0) < tok[1]:
                b.r[eng] = tok[1]
        for b in writes:
            b.w = tok
            b.r = {}
        return tok

    def dma(self, q, out, in_, sem_buf, reads=(), writes=()):
        self._emit_waits(q, self._deps("__dma__", reads, writes))
        ins = self.engs[q].dma_start(out=out, in_=in_)
        self.nins += 1
        if sem_buf.dsem is None:
            sem_buf.dsem = self.new_dsem()
        k = sem_buf.dsem
        self.cnt[k] += 16
        ins.then_inc(self.sems[k], 16)
        tok = (k, self.cnt[k])
        for b in reads:
            if b.r.get(k, 0) < tok[1]:
                b.r[k] = tok[1]
        for b in writes:
            b.w = tok
            b.r = {}
        return tok

    def barrier(self):
        for e in self.engs:
            self._emit_waits(e, {k: v for k, v in self.cnt.items() if k != e and v > 0})

    def finish(self):
        self.barrier()
        self.es.close()
        return self.nc


OFF_QKV, OFF_A, OFF_B, OFF_MQ, OFF_MK, OFF_MV, OFF_MI, OFF_MF, OFF_Z, OFF_MO, OFF_GG, OFF_GM, OFF_END = (
    0, 3072, 3088, 3104, 3616, 4128, 5152, 5160, 5168, 6192, 7216, 8240, 9264)
NCH = 66
OWN_T = 4224


def _col(v, n=128):
    v = np.asarray(v, np.float32).reshape(-1, n)
    return np.ascontiguousarray(v.T)


def _rep(v):
    v = np.asarray(v, np.float32).reshape(1, -1)
    return np.ascontiguousarray(np.repeat(v, 128, axis=0))


def _swapdir(a, flip):
    if not flip:
        return a
    h = a.shape[-1] // 2
    return np.concatenate([a[..., h:], a[..., :h]], axis=-1)


def prep_core(inp, core):
    b = core // 2
    flip = core % 2
    f32 = np.float32
    x = inp["x"][b]
    ctx = inp["ctx"][b]
    if flip:
        x = x[::-1]
        ctx = ctx[::-1]
    w_in = inp["w_in"][0]
    m = {}
    m["x"] = np.ascontiguousarray(x, dtype=f32)
    m["ctx"] = np.ascontiguousarray(ctx, dtype=f32)
    m["c_col"] = _col(inp["c"][b])
    m["cc_col"] = _col(inp["c_ctx"])
    m["w_ada"] = np.ascontiguousarray(inp["w_ada"][0], dtype=f32)
    b_ada = inp["b_ada"][0]
    m["b_ada_col"] = _col(b_ada)
    m["b_ada_g"] = np.ascontiguousarray(np.concatenate([_rep(b_ada[2048:3072]), _rep(b_ada[5120:6144])], axis=1))
    m["n1_col"] = _col(inp["norm1_w"][0])
    m["n2_col"] = _col(inp["norm2_w"][0])
    m["w_qkv"] = np.ascontiguousarray(w_in[:, OFF_QKV:OFF_A])
    wg = np.concatenate([_swapdir(w_in[:, OFF_A:OFF_B], flip), _swapdir(w_in[:, OFF_B:OFF_MQ], flip),
                         _swapdir(w_in[:, OFF_MI:OFF_MF], flip), _swapdir(w_in[:, OFF_MF:OFF_Z], flip)], axis=1)
    m["w_gate"] = np.ascontiguousarray(wg)
    m["w_ml"] = np.ascontiguousarray(w_in[:, OFF_MQ:OFF_MI])
    m["w_o"] = np.ascontiguousarray(w_in[:, OFF_Z:OFF_END])
    gp = np.concatenate([_swapdir(inp["gdn_dt_bias"][0].reshape(-1), flip), _swapdir(inp["gdn_a_log"][0].reshape(-1), flip),
                         _swapdir(inp["ml_igate_b"][0].reshape(-1), flip), _swapdir(inp["ml_fgate_b"][0].reshape(-1), flip)])
    m["gate_p"] = _rep(gp)
    gc = inp["gdn_conv"][0]
    if flip:
        gc = gc[::-1]
    m["gdn_cw"] = np.ascontiguousarray(gc.T.reshape(24, 128, 3).transpose(1, 0, 2), dtype=f32)
    fc = inp["ffn_conv"][0]
    if flip:
        fc = fc[::-1, ::-1]
    m["ffn_cw"] = np.ascontiguousarray(fc.reshape(9, 44, 128).transpose(2, 1, 0), dtype=f32)
    m["gnw_bc"] = _rep(np.tile(inp["gdn_norm_w"][0], 8))
    m["mnw_bc"] = _rep(inp["ml_norm_w"][0].reshape(-1))
    m["now_bc"] = _rep(inp["norm_out_w"])
    m["w_bg"] = np.ascontiguousarray(inp["w_branch_gdn"][0], dtype=f32)
    m["w_bm"] = np.ascontiguousarray(inp["w_branch_ml"][0], dtype=f32)
    m["w_out"] = np.ascontiguousarray(inp["w_out"][0], dtype=f32)
    m["w_up"] = np.ascontiguousarray(inp["w_up"][0], dtype=f32)
    m["w_down"] = np.ascontiguousarray(inp["w_down"][0], dtype=f32)
    m["smask"] = make_smask()
    return m


def make_smask():
    idx = np.arange(128)
    i = idx[None, :]
    j = idx[:, None]
    out = np.zeros((128, 14, 128), np.float32)
    for lev in range(7):
        b = 1 << lev
        same = (i // (2 * b)) == (j // (2 * b))
        f = same & ((i % (2 * b)) < b) & ((j % (2 * b)) >= b)
        g = same & ((j % (2 * b)) < b) & ((i % (2 * b)) >= b)
        out[:, lev, :] = np.where(f, -1.0, 0.0) + np.eye(128)
        out[:, 7 + lev, :] = np.where(g, -1.0, 0.0) + np.eye(128)
    return out


IN_SHAPES = {
    "x": [T, D], "ctx": [TC, D], "c_col": [128, 8], "cc_col": [128, 8], "w_ada": [D, 6144],
    "b_ada_col": [128, 48], "b_ada_g": [128, 2048], "n1_col": [128, 8], "n2_col": [128, 8],
    "w_qkv": [D, 3072], "w_gate": [D, 48], "w_ml": [D, 2048], "w_o": [D, 4096], "gate_p": [128, 48],
    "gdn_cw": [128, 24, 3], "ffn_cw": [128, 44, 9], "gnw_bc": [128, 1024], "mnw_bc": [128, 1024],
    "now_bc": [128, 1024], "w_bg": [D, D], "w_bm": [D, D], "w_out": [D, D], "w_up": [D, 5632], "w_down": [2816, D],
    "smask": [128, 14, 128],
}


class Ring:
    def __init__(self, B, stack, name, n, shape, dt, psum=False):
        self.slots = []
        for i in range(n):
            t = (B.ps if psum else B.sb)(stack, "%s%d" % (name, i), shape, dt)
            self.slots.append((t, Buf("%s%d" % (name, i))))
        self.i = 0

    def next(self):
        s = self.slots[self.i % len(self.slots)]
        self.i += 1
        return s


class Prog:
    def __init__(self, debug=None):
        self.debug = debug or {}
        self.B = Builder()
        self.nc = self.B.nc
        self.top = ExitStack()
        self.inp = {}
        for k, shp in IN_SHAPES.items():
            self.inp[k] = self.nc.dram_tensor(k, list(shp), F32, kind="ExternalInput").ap()
        self.out = self.nc.dram_tensor("out", [4096, D], F32, kind="ExternalOutput").ap()
        dk = "ExternalOutput" if self.debug.get("scratch_out") else "Internal"
        B = self.B
        self.KT = B.dram("KT", [NCH, 128, 8, 128], BF16, dk)
        self.QT = B.dram("QT", [NCH, 128, 8, 128], BF16, dk)
        self.VG = B.dram("VG", [NCH, 128, 1024], BF16, dk)
        self.MQT = B.dram("MQT", [NCH, 128, 4, 128], BF16, dk)
        self.MKT = B.dram("MKT", [NCH, 128, 4, 128], BF16, dk)
        self.MV = B.dram("MV", [NCH, 128, 1024], BF16, dk)
        self.GT = B.dram("GT", [NCH, 128, 48], F32, dk)
        self.OF = B.dram("OF", [33, 128, 1024], F32, dk)
        self.OB = B.dram("OB", [33, 128, 1024], F32, dk)
        self.HF = B.dram("HF", [33, 128, 1024], F32, dk)
        self.HB = B.dram("HB", [33, 128, 1024], F32, dk)
        self.X1 = B.dram("X1", [64 + OWN_T, D], F32, dk)
        self.consts()

    def consts(self):
        B, st = self.B, self.top
        self.ident_f = B.sb(st, "ident_f", [128, 128], F32)
        self.ident_b = B.sb(st, "ident_b", [128, 128], BF16)
        self.ones_f = B.sb(st, "ones_f", [128, 128], F32)
        self.ones_b = B.sb(st, "ones_b", [128, 128], BF16)
        self.nhalf = B.sb(st, "nhalf", [128, 512], F32)
        self.cb = Buf("consts")
        cb = self.cb
        B.op("pool", lambda e: e.memset(self.ones_f[:], 1.0), writes=[cb])
        B.op("pool", lambda e: e.memset(self.ones_b[:], 1.0), writes=[cb])
        B.op("pool", lambda e: e.memset(self.nhalf[:], -0.5), writes=[cb])
        B.op("pool", lambda e: e.memset(self.ident_f[:], 1.0), writes=[cb])
        B.op("pool", lambda e: e.affine_select(self.ident_f[:], self.ident_f[:], pattern=[[-1, 128]], compare_op=ALU.is_equal,
                                               fill=0.0, base=0, channel_multiplier=1), reads=[cb], writes=[cb])
        B.op("dve", lambda e: e.tensor_copy(out=self.ident_b[:], in_=self.ident_f[:]), reads=[cb], writes=[cb])
        self.modc = B.sb(st, "modc", [128, 6, 8], F32)
        self.bmod = Buf("modc")
        self.gate_bc = B.sb(st, "gate_bc", [128, 2, 1024], F32)
        self.bgate = Buf("gate_bc")

    def mask(self, stack, name, cmp_pat, dt=F32, val=1.0, fill=0.0):
        B = self.B
        base, cm, step, cmp = cmp_pat
        t = B.sb(stack, name, [128, 128], dt)
        tf = t
        if dt != F32:
            tf = B.sb(stack, name + "_f", [128, 128], F32)
        b = Buf(name)
        B.op("pool", lambda e: e.memset(tf[:], val), writes=[b])
        B.op("pool", lambda e: e.affine_select(tf[:], tf[:], pattern=[[step, 128]], compare_op=cmp, fill=fill,
                                               base=base, channel_multiplier=cm), reads=[b], writes=[b])
        if dt != F32:
            B.op("dve", lambda e: e.tensor_copy(out=t[:], in_=tf[:]), reads=[b], writes=[b])
        return t, b

    def phase0(self):
        B, nc, inp = self.B, self.nc, self.inp
        st = ExitStack()
        sc = B.sb(st, "p0_sc", [128, 16], F32)
        bsc = Buf("p0_sc")
        scb = B.sb(st, "p0_scb", [128, 8, 128], F32)
        bscb = Buf("p0_scb")
        bcol = B.sb(st, "p0_bcol", [128, 48], F32)
        n12 = B.sb(st, "p0_n12", [128, 16], F32)
        bg = B.sb(st, "p0_bg", [128, 2048], F32)
        bsm = Buf("p0_small")
        B.dma("sp", sc[:, 0:8], inp["c_col"][:, :], bsc, writes=[bsc])
        B.dma("sp", sc[:, 8:16], inp["cc_col"][:, :], bsc, writes=[bsc])
        B.dma("sp", bcol[:], inp["b_ada_col"][:, :], bsm, writes=[bsm])
        B.dma("sp", n12[:, 0:8], inp["n1_col"][:, :], bsm, writes=[bsm])
        B.dma("sp", n12[:, 8:16], inp["n2_col"][:, :], bsm, writes=[bsm])
        B.dma("sp", bg[:], inp["b_ada_g"][:, :], bsm, writes=[bsm])
        B.op("act", lambda e: e.activation(out=sc[:], in_=sc[:], func=AF.Silu), reads=[bsc], writes=[bsc])
        for k in range(8):
            B.op("dve", lambda e, k=k: e.tensor_scalar(out=scb[:, k, :], in0=self.ones_f[:], scalar1=sc[:, k:k + 1], scalar2=None,
                                                       op0=ALU.mult), reads=[bsc, self.cb], writes=[bscb])
        wring = Ring(B, st, "p0_w", 2, [128, 8, 512], F32)
        pcol = B.ps(st, "p0_pcol", [128, 64], F32)
        bpcol = Buf("p0_pcol")
        prow = Ring(B, st, "p0_prow", 2, [128, 512], F32, psum=True)
        wv = inp["w_ada"].rearrange("(k p) n -> p k n", p=128)
        xslot = {0: 0, 1: 1, 3: 2, 4: 3}
        for nb in range(12):
            v, half = nb // 2, nb % 2
            w, bw = wring.next()
            B.dma("sp", w[:], wv[:, :, nb * 512:(nb + 1) * 512], bw, writes=[bw])
            if v in (2, 5):
                p, bp = prow.next()
                for k in range(8):
                    B.op("pe", lambda e, k=k, p=p, w=w: e.matmul(p[:], lhsT=scb[:, k, :], rhs=w[:, k, :], start=(k == 0), stop=(k == 7)),
                         reads=[bscb, bw], writes=[bp], inc=(k == 7))
                gi = 0 if v == 2 else 1
                B.op("dve", lambda e, p=p, gi=gi, half=half: e.tensor_tensor(
                    out=self.gate_bc[:, gi, half * 512:(half + 1) * 512], in0=p[:], in1=bg[:, gi * 1024 + half * 512: gi * 1024 + (half + 1) * 512],
                    op=ALU.add), reads=[bp, bsm], writes=[self.bgate])
            else:
                for cc in range(4):
                    col = xslot[v] * 8 + half * 4 + cc
                    for k in range(8):
                        B.op("pe", lambda e, k=k, w=w, cc=cc, col=col: e.matmul(pcol[:, col:col + 1], lhsT=w[:, k, cc * 128:(cc + 1) * 128],
                                                                                 rhs=sc[:, k:k + 1], start=(k == 0), stop=(k == 7)),
                             reads=[bw, bsc], writes=[bpcol], inc=(k == 7))
                    if v in (0, 1):
                        col2 = 32 + v * 8 + half * 4 + cc
                        for k in range(8):
                            B.op("pe", lambda e, k=k, w=w, cc=cc, col2=col2: e.matmul(pcol[:, col2:col2 + 1], lhsT=w[:, k, cc * 128:(cc + 1) * 128],
                                                                                       rhs=sc[:, 8 + k:9 + k], start=(k == 0), stop=(k == 7)),
                                 reads=[bw, bsc], writes=[bpcol], inc=(k == 7))
        mc = B.sb(st, "p0_mc", [128, 6, 8], F32)
        bmc = Buf("p0_mc")
        for i, v in enumerate((0, 1, 3, 4)):
            B.op("dve", lambda e, i=i, v=v: e.tensor_tensor(out=mc[:, i, :], in0=pcol[:, i * 8:(i + 1) * 8], in1=bcol[:, v * 8:(v + 1) * 8], op=ALU.add),
                 reads=[bpcol, bsm], writes=[bmc])
        for i, v in enumerate((0, 1)):
            B.op("dve", lambda e, i=i, v=v: e.tensor_tensor(out=mc[:, 4 + i, :], in0=pcol[:, 32 + i * 8:32 + (i + 1) * 8], in1=bcol[:, v * 8:(v + 1) * 8],
                                                            op=ALU.add), reads=[bpcol, bsm], writes=[bmc])
        md = self.modc
        for dst, (sci, shi, nw) in {0: (1, 0, 0), 2: (5, 4, 0), 4: (3, 2, 1)}.items():
            B.op("dve", lambda e, dst=dst, sci=sci, nw=nw: e.scalar_tensor_tensor(out=md[:, dst, :], in0=mc[:, sci, :], scalar=1.0, in1=n12[:, nw * 8:(nw + 1) * 8],
                                                                                  op0=ALU.add, op1=ALU.mult), reads=[bmc, bsm], writes=[self.bmod])
            B.op("dve", lambda e, dst=dst, shi=shi: e.tensor_copy(out=md[:, dst + 1, :], in_=mc[:, shi, :]), reads=[bmc], writes=[self.bmod])
        B.barrier()
        st.close()

    @staticmethod
    def run_pipeline(makers, depth):
        active = []
        it = iter(makers)
        exhausted = False
        while True:
            for g in list(active):
                try:
                    next(g)
                except StopIteration:
                    active.remove(g)
            if not exhausted and len(active) < depth:
                try:
                    g = next(it)()
                    try:
                        next(g)
                        active.append(g)
                    except StopIteration:
                        pass
                except StopIteration:
                    exhausted = True
            if exhausted and not active:
                break

    def load_w_bf16(self, stack, name, ap, kchunks, ncols, nsplit=4):
        B = self.B
        t = B.sb(stack, name, [128, kchunks, ncols], BF16)
        b = Buf(name)
        v = ap.rearrange("(k p) n -> p k n", p=128)
        step = (ncols + nsplit - 1) // nsplit
        for i in range(0, ncols, step):
            j = min(ncols, i + step)
            B.dma("pool", t[:, :, i:j], v[:, :, i:j], b, writes=[b])
        return t, b

    def norm_transpose(self, xt, bxts, ns, xn, bxn, sq, bsq, junk, bjunk, ptr_ring, hxT, bhx, col0, ai, npart=128):
        B = self.B
        for s in range(ns):
            B.op("act", lambda e, s=s: e.activation(out=xn[0:npart, s, :], in_=xt[0:npart, s, :], func=AF.Square, accum_out=sq[0:npart, s:s + 1]),
                 reads=[bxts[s]], writes=[bxn, bsq])
        B.op("dve", lambda e: e.tensor_scalar(out=sq[0:npart, 8:8 + ns], in0=sq[0:npart, 0:ns], scalar1=float(D * EPS), scalar2=None, op0=ALU.add),
             reads=[bsq], writes=[bsq])
        B.op("pool", lambda e: e.tensor_tensor(out=sq[0:npart, 16:16 + ns], in0=sq[0:npart, 8:8 + ns], in1=self.nhalf[0:npart, 0:ns], op=ALU.pow),
             reads=[bsq, self.cb], writes=[bsq])
        for s in range(ns):
            B.op("dve", lambda e, s=s: e.tensor_scalar(out=xn[0:npart, s, :], in0=xt[0:npart, s, :], scalar1=sq[0:npart, 16 + s:17 + s], scalar2=32.0,
                                                       op0=ALU.mult, op1=ALU.mult), reads=[bxts[s], bsq], writes=[bxn])
        for k in range(KD):
            p, bp = ptr_ring.next()
            pb = p[:].bitcast(BF16)
            for s in range(ns):
                B.op("pe", lambda e, s=s, k=k, pb=pb: e.transpose(pb[:, s * npart:(s + 1) * npart], xn[0:npart, s, k * 128:(k + 1) * 128],
                                                                  self.ident_b[0:npart, 0:npart]),
                     reads=[bxn, self.cb], writes=[bp], inc=(s == ns - 1))
            B.op("act", lambda e, k=k, pb=pb: e.activation(out=hxT[:, k, col0:col0 + ns * npart], in_=pb[:, 0:ns * npart], func=AF.Identity,
                                                           scale=self.modc[:, ai, k:k + 1], bias=self.modc[:, ai + 1, k:k + 1]),
                 reads=[bp, self.bmod], writes=[bhx])

    def phaseA(self):
        B, nc, inp = self.B, self.nc, self.inp
        st = ExitStack()
        wqkv, bwqkv = self.load_w_bf16(st, "a_wqkv", inp["w_qkv"], 8, 3072, 6)
        wml, bwml = self.load_w_bf16(st, "a_wml", inp["w_ml"], 8, 2048, 4)
        wgt, bwgt = self.load_w_bf16(st, "a_wgt", inp["w_gate"], 8, 48, 1)
        cw = B.sb(st, "a_cw", [128, 24, 3], F32)
        gp = B.sb(st, "a_gp", [128, 48], F32)
        bsm = Buf("a_small")
        B.dma("sp", cw[:], inp["gdn_cw"][:, :, :], bsm, writes=[bsm])
        B.dma("sp", gp[:], inp["gate_p"][:, :], bsm, writes=[bsm])
        B.op("act", lambda e: e.activation(out=gp[:, 16:32], in_=gp[:, 16:32], func=AF.Exp), reads=[bsm], writes=[bsm])
        B.op("dve", lambda e: e.tensor_scalar(out=gp[:, 16:32], in0=gp[:, 16:32], scalar1=-1.0, scalar2=None, op0=ALU.mult), reads=[bsm], writes=[bsm])
        xt = B.sb(st, "a_x", [128, 4, 1024], F32)
        bxts = [Buf("a_x%d" % i) for i in range(4)]
        xh = B.sb(st, "a_xh", [2, 1, 1024], F32); bxh = Buf("a_xh")
        xn = B.sb(st, "a_xn", [128, 4, 1024], BF16); bxn = Buf("a_xn")
        xnh = B.sb(st, "a_xnh", [2, 1, 1024], BF16); bxnh = Buf("a_xnh")
        sq = B.sb(st, "a_sq", [128, 24], F32); bsq = Buf("a_sq")
        sqh = B.sb(st, "a_sqh", [128, 24], F32); bsqh = Buf("a_sqh")
        junk = None; bjunk = None
        hxT = B.sb(st, "a_hxT", [128, 8, 514], BF16); bhx = Buf("a_hxT")
        hxh = B.sb(st, "a_hxh", [128, 8, 2], BF16); bhxh = Buf("a_hxh")
        ptr = Ring(B, st, "a_ptr", 2, [128, 512], F32, psum=True)
        pz = Ring(B, st, "a_pz", 4, [128, 512], F32, psum=True)
        pmisc = B.ps(st, "a_pmisc", [128, 512], F32)
        pzh = pmisc[:, 0:64]; bpzh = Buf("a_pzh")
        pn = ptr
        zb = Ring(B, st, "a_zb", 4, [128, 514], F32)
        y1 = Ring(B, st, "a_y1", 4, [128, 512], F32)
        sqb = Ring(B, st, "a_sqb", 2, [128, 512], BF16)
        skeep = B.sb(st, "a_skeep", [128, 8, 512], BF16)
        bskeep = [Buf("a_skeep%d" % i) for i in range(8)]
        rnr = B.sb(st, "a_rnr", [8, 512], F32); brnr = Buf("a_rnr")
        ind = B.sb(st, "a_ind", [128, 8, 8], BF16); bind = Buf("a_ind")
        selr = B.sb(st, "a_selr", [8, 8, 128], F32); bselr = Buf("a_selr")
        B.op("pool", lambda e: e.memset(ind[:], 0.0), writes=[bind])
        for jj in range(8):
            B.op("pool", lambda e, jj=jj: e.memset(ind[:, jj, jj:jj + 1], 1.0), writes=[bind])
            B.op("dve", lambda e, jj=jj: e.tensor_copy(out=selr[:, jj, :], in_=self.ident_f[0:8, jj:jj + 1].to_broadcast([8, 128])), reads=[self.cb], writes=[bselr])
        pss = B.ps(st, "a_pss", [128, 512], F32); bpss = Buf("a_pss")
        kst = B.sb(st, "a_kst", [128, 4, 8, 128], BF16); bkst = Buf("a_kst")
        qst = B.sb(st, "a_qst", [128, 4, 8, 128], BF16); bqst = Buf("a_qst")
        vT = B.sb(st, "a_vT", [128, 8, 512], BF16); bvT = Buf("a_vT")
        vst = Ring(B, st, "a_vst", 1, [128, 4, 1024], BF16)
        mqst = B.sb(st, "a_mqst", [128, 4, 4, 128], BF16); bmqst = Buf("a_mqst")
        mkst = B.sb(st, "a_mkst", [128, 4, 4, 128], BF16); bmkst = Buf("a_mkst")
        graw = B.sb(st, "a_graw", [128, 4, 48], F32); bgraw = Buf("a_graw")
        gwk = B.sb(st, "a_gwk", [128, 4, 48], F32); bgwk = Buf("a_gwk")
        gsb = Ring(B, st, "a_gsb", 2, [128, 4, 48], F32)
        pg = pmisc[:, 64:256].rearrange("p (s g) -> p s g", g=48); bpg = Buf("a_pg")
        dkr = float(128 ** -0.5)

        tiles = [(inp["ctx"], 0, 2, 0, False, False)]
        for i in range(16):
            tiles.append((inp["x"], i * 512, 4, 2 + 4 * i, i > 0, i < 15))
        if self.debug.get("a_tiles"):
            tiles = tiles[: self.debug["a_tiles"]]
        for (src, t0, ns, c0, hl, hr) in tiles:
            n = ns * 128
            ai = 2 if src is inp["ctx"] else 0
            for s in range(ns):
                B.dma("sp", xt[:, s, :], src[t0 + s * 128:t0 + (s + 1) * 128, :], bxts[s], writes=[bxts[s]])
            tl = t0 - 1 if hl else t0
            tr = t0 + n if hr else t0
            B.dma("sp", xh[0:1, 0, :], src[tl:tl + 1, :], bxh, writes=[bxh])
            B.dma("sp", xh[1:2, 0, :], src[tr:tr + 1, :], bxh, writes=[bxh])
            self.norm_transpose(xt, bxts, ns, xn, bxn, sq, bsq, junk, bjunk, ptr, hxT, bhx, 1, ai)
            self.norm_transpose(xh, [bxh], 1, xnh, bxnh, sqh, bsqh, junk, bjunk, ptr, hxh, bhxh, 0, ai, npart=2)
            def chunk_gen(j, kind, jj):
                p, bp = pz.next()
                z, bz = zb.next()
                a1, ba1 = y1.next()
                for k in range(8):
                    B.op("pe", lambda e, k=k: e.matmul(p[:, 0:n], lhsT=wqkv[:, k, j * 128:(j + 1) * 128], rhs=hxT[:, k, 1:1 + n],
                                                         start=(k == 0), stop=(k == 7)), reads=[bwqkv, bhx], writes=[bp], inc=(k == 7))
                for k in range(8):
                    B.op("pe", lambda e, k=k: e.matmul(pzh[:, 2 * j:2 * j + 2], lhsT=wqkv[:, k, j * 128:(j + 1) * 128], rhs=hxh[:, k, :],
                                                         start=(k == 0), stop=(k == 7)), reads=[bwqkv, bhxh], writes=[bpzh], inc=(k == 7))
                yield
                B.op("act", lambda e: e.activation(out=z[:, 1:1 + n], in_=p[:, 0:n], func=AF.Identity), reads=[bp], writes=[bz])
                B.op("act", lambda e: e.activation(out=z[:, 0:1], in_=pzh[:, 2 * j:2 * j + 1], func=AF.Identity), reads=[bpzh], writes=[bz])
                B.op("act", lambda e: e.activation(out=z[:, n + 1:n + 2], in_=pzh[:, 2 * j + 1:2 * j + 2], func=AF.Identity), reads=[bpzh], writes=[bz])
                if not hl:
                    B.op("pool", lambda e: e.memset(z[:, 0:1], 0.0), writes=[bz])
                if not hr:
                    B.op("pool", lambda e: e.memset(z[:, n + 1:n + 2], 0.0), writes=[bz])
                yield
                B.op("dve", lambda e: e.tensor_scalar(out=a1[:, 0:n], in0=z[:, 1:1 + n], scalar1=cw[:, j, 1:2], scalar2=None, op0=ALU.mult),
                     reads=[bz, bsm], writes=[ba1])
                B.op("dve", lambda e: e.scalar_tensor_tensor(out=a1[:, 0:n], in0=z[:, 0:n], scalar=cw[:, j, 0:1], in1=a1[:, 0:n],
                                                            op0=ALU.mult, op1=ALU.add), reads=[bz, bsm, ba1], writes=[ba1])
                B.op("dve", lambda e: e.scalar_tensor_tensor(out=a1[:, 0:n], in0=z[:, 2:2 + n], scalar=cw[:, j, 2:3], in1=a1[:, 0:n],
                                                            op0=ALU.mult, op1=ALU.add), reads=[bz, bsm, ba1], writes=[ba1])
                yield
                if kind == "v":
                    B.op("act", lambda e: e.activation(out=vT[:, jj, 0:n], in_=a1[:, 0:n], func=AF.Silu), reads=[ba1], writes=[bvT])
                else:
                    B.op("act", lambda e: e.activation(out=skeep[:, jj, 0:n], in_=a1[:, 0:n], func=AF.Silu), reads=[ba1], writes=[bskeep[jj]])
                    q2, bq2 = sqb.next()
                    B.op("pool", lambda e: e.tensor_tensor(out=q2[:, 0:n], in0=skeep[:, jj, 0:n], in1=skeep[:, jj, 0:n], op=ALU.mult),
                         reads=[bskeep[jj]], writes=[bq2])
                    B.op("pe", lambda e: e.matmul(pss[0:8, 0:n], lhsT=ind[:, jj, :], rhs=q2[:, 0:n], start=(jj == 0), stop=(jj == 7)),
                         reads=[bq2, bind], writes=[bpss])

            for half in range(2):
                self.run_pipeline([(lambda jj=jj: chunk_gen(half * 8 + jj, "qk", jj)) for jj in range(8)], 4)
                B.op("act", lambda e: e.activation(out=rnr[:, 0:n], in_=pss[0:8, 0:n], func=AF.Ln, bias=float(EPS)), reads=[bpss], writes=[brnr])
                B.op("act", lambda e: e.activation(out=rnr[:, 0:n], in_=rnr[:, 0:n], func=AF.Exp, scale=-0.5), reads=[brnr], writes=[brnr])
                for jj in range(8):
                    pp, bpp = pn.next()
                    B.op("pe", lambda e, pp=pp, jj=jj: e.matmul(pp[:, 0:n], lhsT=selr[:, jj, :], rhs=rnr[:, 0:n], start=True, stop=True),
                         reads=[brnr, bselr], writes=[bpp])
                    if half == 0:
                        B.op("dve", lambda e, pp=pp, jj=jj: e.scalar_tensor_tensor(
                            out=qst[:, 0:ns, jj, :], in0=skeep[:, jj, 0:n].rearrange("p (s t) -> p s t", t=128), scalar=dkr,
                            in1=pp[:, 0:n].rearrange("p (s t) -> p s t", t=128), op0=ALU.mult, op1=ALU.mult), reads=[bskeep[jj], bpp], writes=[bqst])
                    else:
                        B.op("dve", lambda e, pp=pp, jj=jj: e.tensor_tensor(
                            out=kst[:, 0:ns, jj, :], in0=skeep[:, jj, 0:n].rearrange("p (s t) -> p s t", t=128),
                            in1=pp[:, 0:n].rearrange("p (s t) -> p s t", t=128), op=ALU.mult), reads=[bskeep[jj], bpp], writes=[bkst])
            for s in range(ns):
                for k in range(8):
                    B.op("pe", lambda e, k=k, s=s: e.matmul(pg[:, s, :], lhsT=hxT[:, k, 1 + s * 128:1 + (s + 1) * 128], rhs=wgt[:, k, :],
                                                             start=(k == 0), stop=(k == 7)), reads=[bhx, bwgt], writes=[bpg], inc=(k == 7))
            g, bg_ = gsb.next()
            self.gate_math(pg, bpg, graw, bgraw, gwk, bgwk, g, bg_, gp, bsm, ns)
            B.dma("sp", self.GT[c0:c0 + ns].rearrange("c t g -> t c g"), g[:, 0:ns, :], bg_, reads=[bg_])
            self.run_pipeline([(lambda jj=jj: chunk_gen(16 + jj, "v", jj)) for jj in range(8)], 4)
            self.v_transposes(vT, bvT, ns, vst, pn, self.VG, c0)
            B.dma("sp", self.KT[c0:c0 + ns].rearrange("c d h t -> d c h t"), kst[:, 0:ns], bkst, reads=[bkst])
            B.dma("sp", self.QT[c0:c0 + ns].rearrange("c d h t -> d c h t"), qst[:, 0:ns], bqst, reads=[bqst])
            for j in range(16):
                p, bp = pz.next()
                for k in range(8):
                    B.op("pe", lambda e, k=k, p=p, j=j: e.matmul(p[:, 0:n], lhsT=wml[:, k, j * 128:(j + 1) * 128], rhs=hxT[:, k, 1:1 + n],
                                                                  start=(k == 0), stop=(k == 7)), reads=[bwml, bhx], writes=[bp], inc=(k == 7))
                if j < 4:
                    B.op("act", lambda e, p=p, j=j: e.activation(out=mqst[:, 0:ns, j, :], in_=p[:, 0:n].rearrange("p (s t) -> p s t", t=128),
                                                                  func=AF.Identity, scale=dkr), reads=[bp], writes=[bmqst])
                elif j < 8:
                    B.op("act", lambda e, p=p, j=j: e.activation(out=mkst[:, 0:ns, j - 4, :], in_=p[:, 0:n].rearrange("p (s t) -> p s t", t=128),
                                                                  func=AF.Identity), reads=[bp], writes=[bmkst])
                else:
                    B.op("act", lambda e, p=p, j=j: e.activation(out=vT[:, j - 8, 0:n], in_=p[:, 0:n], func=AF.Identity), reads=[bp], writes=[bvT])
            self.v_transposes(vT, bvT, ns, vst, pn, self.MV, c0)
            B.dma("sp", self.MQT[c0:c0 + ns].rearrange("c d h t -> d c h t"), mqst[:, 0:ns], bmqst, reads=[bmqst])
            B.dma("sp", self.MKT[c0:c0 + ns].rearrange("c d h t -> d c h t"), mkst[:, 0:ns], bmkst, reads=[bmkst])
        B.barrier()
        st.close()

    def v_transposes(self, vT, bvT, ns, vst, pn, dst, c0):
        B = self.B
        v, bv = vst.next()
        for s in range(ns):
            pp, bpp = pn.next()
            ppb = pp[:].bitcast(BF16)
            for h in range(8):
                B.op("pe", lambda e, s=s, h=h, ppb=ppb: e.transpose(ppb[:, h * 128:(h + 1) * 128], vT[:, h, s * 128:(s + 1) * 128], self.ident_b[:]),
                     reads=[bvT, self.cb], writes=[bpp], inc=(h == 7))
            B.op("act", lambda e, s=s, ppb=ppb, v=v: e.activation(out=v[:, s, :], in_=ppb[:, 0:1024], func=AF.Identity), reads=[bpp], writes=[bv])
        B.dma("sp", dst[c0:c0 + ns].rearrange("c t e -> t c e"), v[:, 0:ns, :], bv, reads=[bv])

    def gate_math(self, pg, bpg, graw, bgraw, wk, bwk, g, bg_, gp, bgp, ns):
        B = self.B
        S = slice(0, ns)

        def bc(lo, hi):
            return gp[:, lo:hi].unsqueeze(1).to_broadcast([128, ns, hi - lo])

        B.op("act", lambda e: e.activation(out=graw[:, S, :], in_=pg[:, S, :], func=AF.Identity), reads=[bpg], writes=[bgraw])
        B.op("dve", lambda e: e.tensor_tensor(out=wk[:, S, 0:16], in0=graw[:, S, 0:16], in1=bc(0, 16), op=ALU.add), reads=[bgraw, bgp], writes=[bwk])
        B.op("act", lambda e: e.activation(out=wk[:, S, 0:16], in_=wk[:, S, 0:16], func=AF.Exp), reads=[bwk], writes=[bwk])
        B.op("act", lambda e: e.activation(out=wk[:, S, 0:16], in_=wk[:, S, 0:16], func=AF.Ln, bias=1.0), reads=[bwk], writes=[bwk])
        B.op("dve", lambda e: e.tensor_tensor(out=g[:, S, 0:16], in0=wk[:, S, 0:16], in1=bc(16, 32), op=ALU.mult), reads=[bwk, bgp], writes=[bg_])
        B.op("act", lambda e: e.activation(out=wk[:, S, 16:32], in_=graw[:, S, 16:32], func=AF.Exp, scale=-1.0), reads=[bgraw], writes=[bwk])
        B.op("dve", lambda e: e.tensor_scalar(out=wk[:, S, 16:32], in0=wk[:, S, 16:32], scalar1=1.0, scalar2=None, op0=ALU.add), reads=[bwk], writes=[bwk])
        B.op("dve", lambda e: e.reciprocal(out=g[:, S, 16:32], in_=wk[:, S, 16:32]), reads=[bwk], writes=[bg_])
        B.op("dve", lambda e: e.tensor_tensor(out=wk[:, S, 32:48], in0=graw[:, S, 32:48], in1=bc(32, 48), op=ALU.add), reads=[bgraw, bgp], writes=[bwk])
        B.op("act", lambda e: e.activation(out=wk[:, S, 32:48], in_=wk[:, S, 32:48], func=AF.Exp, scale=float(2.0 / 15.0)), reads=[bwk], writes=[bwk])
        B.op("dve", lambda e: e.tensor_scalar(out=wk[:, S, 32:48], in0=wk[:, S, 32:48], scalar1=1.0, scalar2=None, op0=ALU.add), reads=[bwk], writes=[bwk])
        B.op("dve", lambda e: e.reciprocal(out=wk[:, S, 32:48], in_=wk[:, S, 32:48]), reads=[bwk], writes=[bwk])
        B.op("dve", lambda e: e.tensor_scalar(out=g[:, S, 32:48], in0=wk[:, S, 32:48], scalar1=-30.0, scalar2=15.0, op0=ALU.mult, op1=ALU.add),
             reads=[bwk], writes=[bg_])
        B.op("act", lambda e: e.activation(out=wk[:, S, 40:48], in_=g[:, S, 40:48], func=AF.Exp, scale=-1.0), reads=[bg_], writes=[bwk])
        B.op("act", lambda e: e.activation(out=wk[:, S, 40:48], in_=wk[:, S, 40:48], func=AF.Ln, bias=1.0), reads=[bwk], writes=[bwk])
        B.op("dve", lambda e: e.tensor_scalar(out=g[:, S, 40:48], in0=wk[:, S, 40:48], scalar1=-1.0, scalar2=None, op0=ALU.mult), reads=[bwk], writes=[bg_])

    def phaseB(self):
        B, nc, inp = self.B, self.nc, self.inp
        st = ExitStack()
        dbg = self.debug
        LE, bLE = self.mask(st, "b_LE", (0, -1, 1, ALU.is_ge))
        LT, bLT = self.mask(st, "b_LT", (-1, -1, 1, ALU.is_ge))
        GE, bGE = self.mask(st, "b_GE", (0, 1, -1, ALU.is_ge))
        GT_, bGT = self.mask(st, "b_GT", (-1, 1, -1, ALU.is_ge))
        MBf, bMBf = self.mask(st, "b_MBf", (0, 1, -1, ALU.is_ge), val=0.0, fill=NEG)
        MBb, bMBb = self.mask(st, "b_MBb", (0, -1, 1, ALU.is_ge), val=0.0, fill=NEG)
        SELf, bSELf = self.mask(st, "b_SELf", (-127, 1, 0, ALU.is_equal))
        SELb, bSELb = self.mask(st, "b_SELb", (0, 1, 0, ALU.is_equal))
        smask = B.sb(st, "b_smask", [128, 14, 128], BF16)
        bsm = Buf("b_smask")
        B.dma("pool", smask[:], inp["smask"][:, :, :], bsm, writes=[bsm])
        cbufs = [bLE, bLT, bGE, bGT, bMBf, bMBb, bSELf, bSELb, bsm, self.cb]
        dirc = [dict(U=LE, S=GT_, incl=LE, strict=LT, MB=MBf, SEL=SELf),
                dict(U=GE, S=LT, incl=GE, strict=GT_, MB=MBb, SEL=SELb)]
        S = [B.sb(st, "b_S%d" % d, [128, 8, 128], F32) for d in range(2)]
        bS = [[Buf("b_S%d_%d" % (d, g)) for g in range(2)] for d in range(2)]
        Sb = [[Ring(B, st, "b_Sb%d_%d_" % (d, g), 2, [128, 4, 128], BF16) for g in range(2)] for d in range(2)]
        Sb_cur = [[None, None], [None, None]]
        C = [B.sb(st, "b_C%d" % d, [128, 4, 256], F32) for d in range(2)]
        bC = [Buf("b_C%d" % d) for d in range(2)]
        Cb = [Ring(B, st, "b_Cb%d_" % d, 2, [128, 4, 256], BF16) for d in range(2)]
        Cb_cur = [None, None]
        nst = [Ring(B, st, "b_n%d_" % d, 2, [128, 8], F32) for d in range(2)]
        nbf = [Ring(B, st, "b_nb%d_" % d, 2, [128, 4], BF16) for d in range(2)]
        n_cur = [None, None]
        nb_cur = [None, None]
        mst = [Ring(B, st, "b_m%d_" % d, 2, [128, 4], F32) for d in range(2)]
        m_cur = [None, None]
        for d in range(2):
            B.op("pool", lambda e, d=d: e.memset(S[d][:], 0.0), writes=bS[d])
            B.op("pool", lambda e, d=d: e.memset(C[d][:], 0.0), writes=[bC[d]])
            for g in range(2):
                t, b = Sb[d][g].next()
                B.op("pool", lambda e, t=t: e.memset(t[:], 0.0), writes=[b])
                Sb_cur[d][g] = (t, b)
            t, b = Cb[d].next()
            B.op("pool", lambda e, t=t: e.memset(t[:], 0.0), writes=[b])
            Cb_cur[d] = (t, b)
            t, b = nst[d].next()
            B.op("pool", lambda e, t=t: e.memset(t[:], 0.0), writes=[b])
            n_cur[d] = (t, b)
            t, b = nbf[d].next()
            B.op("pool", lambda e, t=t: e.memset(t[:], 0.0), writes=[b])
            nb_cur[d] = (t, b)
            t, b = mst[d].next()
            B.op("pool", lambda e, t=t: e.memset(t[:], 0.0), writes=[b])
            m_cur[d] = (t, b)
        def dring(name, shape, dt):
            return [Ring(B, st, "b_%s%d_" % (name, d), 2, shape, dt) for d in range(2)]
        rKT = dring("KT", [128, 8, 128], BF16)
        rQT = dring("QT", [128, 8, 128], BF16)
        rVG = dring("VG", [128, 1024], BF16)
        rGT = dring("GT", [128, 48], F32)
        rMQ = dring("MQ", [128, 4, 128], BF16)
        rMK = dring("MK", [128, 4, 128], BF16)
        rMV = dring("MV", [128, 1024], BF16)
        rgs = dring("gs", [128, 64], F32)
        psr = Ring(B, st, "b_ps", 8, [128, 512], F32, psum=True)
        NG, NM = 4, 2
        gslots = []
        for i in range(NG):
            sl = {}
            for nm, shp, dt in (("A", [128, 4, 128], F32), ("Bt", [128, 4, 128], F32), ("Ct", [128, 4, 128], F32),
                                ("attnT", [128, 4, 128], BF16), ("Qp", [128, 4, 128], BF16),
                                ("Kg", [128, 4, 128], BF16), ("kt", [128, 4, 128], BF16), ("G0", [128, 4, 128], BF16),
                                ("G1", [128, 4, 128], BF16), ("H0", [128, 4, 128], BF16), ("H1", [128, 4, 128], BF16),
                                ("IYT", [128, 4, 128], BF16), ("negW", [128, 4, 128], BF16), ("vnew", [128, 4, 128], BF16)):
                sl[nm] = (B.sb(st, "b_g%d_%s" % (i, nm), shp, dt), Buf("b_g%d_%s" % (i, nm)))
            gslots.append(sl)
        mslots = []
        for i in range(NM):
            sl = {}
            for nm, shp, dt in (("X", [128, 4, 128], F32), ("Y", [128, 4, 128], F32), ("Pm", [128, 4, 128], BF16),
                                ("PT", [128, 4, 128], BF16), ("Kw", [128, 4, 128], BF16), ("sm", [128, 64], F32), ("Ct", [128, 4, 256], F32)):
                sl[nm] = (B.sb(st, "b_m%d_%s" % (i, nm), shp, dt), Buf("b_m%d_%s" % (i, nm)))
            mslots.append(sl)
        ring_o1 = Ring(B, st, "b_o1_", 2, [128, 4, 128], F32)
        ring_o = Ring(B, st, "b_o_", 2, [128, 4, 128], F32)
        ring_num = Ring(B, st, "b_num_", 1, [128, 4, 256], F32)
        ring_h = Ring(B, st, "b_h_", 1, [128, 4, 256], F32)

        def bc3(ap2, n):
            return ap2.unsqueeze(2).to_broadcast([128, 4, n])

        def bcm(ap2, n=4):
            return ap2.unsqueeze(1).to_broadcast([128, n, 128])

        nsteps = dbg.get("b_steps", NCH)
        order = [list(range(NCH)), [1, 0] + list(range(NCH - 1, 1, -1))]
        if dbg.get("b_order"):
            order = dbg["b_order"]
            nsteps = len(order[0])
        out_lo, out_hi = 2, 2 + 33

        data = {}

        def load_step(step, d):
            c = order[d][step]
            tk, bk = rKT[d].next(); tq, bq = rQT[d].next(); tv, bv = rVG[d].next(); tg, bg = rGT[d].next()
            tmq, bmq = rMQ[d].next(); tmk, bmk = rMK[d].next(); tmv, bmv = rMV[d].next()
            B.dma("sp", tg[:], self.GT[c], bg, writes=[bg])
            B.dma("sp", tk[:], self.KT[c], bk, writes=[bk])
            B.dma("sp", tq[:], self.QT[c], bq, writes=[bq])
            B.dma("sp", tv[:], self.VG[c], bv, writes=[bv])
            B.dma("sp", tmq[:], self.MQT[c], bmq, writes=[bmq])
            B.dma("sp", tmk[:], self.MKT[c], bmk, writes=[bmk])
            B.dma("sp", tmv[:], self.MV[c], bmv, writes=[bmv])
            data[(step, d)] = dict(c=c, KT=(tk, bk), QT=(tq, bq), VG=(tv, bv), GT=(tg, bg), MQ=(tmq, bmq), MK=(tmk, bmk), MV=(tmv, bmv))

        def shared_pre(step, d):
            dd = data[(step, d)]
            tg, bg = dd["GT"]
            gs, bgs = rgs[d].next()
            dc = dirc[d]
            p, bp = psr.next()
            B.op("pe", lambda e: e.matmul(p[:, 0:8], lhsT=dc["U"][:], rhs=tg[:, d * 8:(d + 1) * 8], start=True, stop=True), reads=[bg] + cbufs, writes=[bp], inc=False)
            B.op("pe", lambda e: e.matmul(p[:, 8:12], lhsT=dc["U"][:], rhs=tg[:, 40 + d * 4:44 + d * 4], start=True, stop=True), reads=[bg] + cbufs, writes=[bp], inc=False)
            B.op("pe", lambda e: e.matmul(p[:, 12:20], lhsT=self.ones_f[:], rhs=tg[:, d * 8:(d + 1) * 8], start=True, stop=True), reads=[bg] + cbufs, writes=[bp], inc=False)
            B.op("pe", lambda e: e.matmul(p[:, 20:24], lhsT=self.ones_f[:], rhs=tg[:, 40 + d * 4:44 + d * 4], start=True, stop=True), reads=[bg] + cbufs, writes=[bp])
            B.op("act", lambda e: e.activation(out=gs[:, 0:24], in_=p[:, 0:24], func=AF.Identity), reads=[bp], writes=[bgs])
            B.op("act", lambda e: e.activation(out=gs[:, 24:32], in_=gs[:, 0:8], func=AF.Exp), reads=[bgs], writes=[bgs])
            B.op("dve", lambda e: e.tensor_tensor(out=gs[:, 32:40], in0=gs[:, 12:20], in1=gs[:, 0:8], op=ALU.subtract), reads=[bgs], writes=[bgs])
            B.op("act", lambda e: e.activation(out=gs[:, 32:40], in_=gs[:, 32:40], func=AF.Exp), reads=[bgs], writes=[bgs])
            B.op("act", lambda e: e.activation(out=gs[:, 40:48], in_=gs[:, 12:20], func=AF.Exp), reads=[bgs], writes=[bgs])
            B.op("dve", lambda e: e.tensor_tensor(out=gs[:, 48:52], in0=tg[:, 32 + d * 4:36 + d * 4], in1=gs[:, 8:12], op=ALU.subtract), reads=[bgs, bg], writes=[bgs])
            dd["gs"] = (gs, bgs)

        def gdn_group(step, d, hg, sl):
            dd = data[(step, d)]
            dc = dirc[d]
            c = dd["c"]
            need_o = out_lo <= c < out_hi
            tk, bk = dd["KT"]; tq, bq = dd["QT"]; tv, bv = dd["VG"]; tg, bg = dd["GT"]; gs, bgs = dd["gs"]
            h0 = hg * 4
            A, bA = sl["A"]; Bt, bBt = sl["Bt"]; Ct, bCt = sl["Ct"]
            attnT, battn = sl["attnT"]; Qp, bQp = sl["Qp"]; Kg, bKg = sl["Kg"]; kt, bkt = sl["kt"]
            IYT, bIYT = sl["IYT"]; negW, bnegW = sl["negW"]; vnew, bvnew = sl["vnew"]
            Qm, bQm = sl["IYT"]
            St, bSt = sl["A"]
            gcol = tg[:, d * 8 + h0:d * 8 + h0 + 4]
            bcol = tg[:, 16 + d * 8 + h0:16 + d * 8 + h0 + 4]
            eg = gs[:, 24 + h0:24 + h0 + 4]
            ekt = gs[:, 32 + h0:32 + h0 + 4]
            gte = gs[:, 40 + h0:40 + h0 + 4]
            B.op("pool", lambda e: e.tensor_tensor(out=A[:], in0=bcm(dc["U"][:]), in1=bc3(gcol, 128), op=ALU.mult), reads=[bg] + cbufs, writes=[bA])
            pD, bpD = psr.next()
            for u in range(4):
                B.op("pe", lambda e, u=u: e.matmul(pD[:, u * 128:(u + 1) * 128], lhsT=dc["S"][:], rhs=A[:, u, :], start=True, stop=True),
                     reads=[bA] + cbufs, writes=[bpD], inc=(u == 3))
            B.op("act", lambda e: e.activation(out=Bt[:].rearrange("p u l -> p (u l)"), in_=pD[:, :], func=AF.Exp), reads=[bpD], writes=[bBt])
            B.op("pool", lambda e: e.tensor_tensor(out=A[:], in0=Bt[:], in1=bcm(dc["incl"][:]), op=ALU.mult), reads=[bBt] + cbufs, writes=[bA])
            B.op("pool", lambda e: e.tensor_tensor(out=Ct[:], in0=Bt[:], in1=bcm(dc["strict"][:]), op=ALU.mult), reads=[bBt] + cbufs, writes=[bCt])
            B.op("pool", lambda e: e.tensor_tensor(out=Ct[:], in0=Ct[:], in1=bc3(bcol, 128), op=ALU.mult), reads=[bCt, bg], writes=[bCt])
            pKK, bpKK = psr.next()
            pQK, bpQK = psr.next()
            pKt, bpKt = psr.next()
            pKtb = pKt[:].bitcast(BF16)
            for u in range(4):
                B.op("pe", lambda e, u=u: e.matmul(pKK[:, u * 128:(u + 1) * 128], lhsT=tk[:, h0 + u, :], rhs=tk[:, h0 + u, :], start=True, stop=True),
                     reads=[bk], writes=[bpKK], inc=(u == 3))
            for u in range(4):
                B.op("pe", lambda e, u=u: e.matmul(pQK[:, u * 128:(u + 1) * 128], lhsT=tk[:, h0 + u, :], rhs=tq[:, h0 + u, :], start=True, stop=True),
                     reads=[bk, bq], writes=[bpQK], inc=(u == 3))
            for u in range(4):
                B.op("pe", lambda e, u=u: e.transpose(pKtb[:, u * 128:(u + 1) * 128], tk[:, h0 + u, :], self.ident_b[:]),
                     reads=[bk] + cbufs, writes=[bpKt], inc=(u == 3))
            B.op("dve", lambda e: e.tensor_tensor(out=attnT[:], in0=pQK[:, :].rearrange("p (u l) -> p u l", u=4), in1=A[:], op=ALU.mult),
                 reads=[bpQK, bA], writes=[battn])
            B.op("dve", lambda e: e.tensor_tensor(out=Qm[:], in0=pKK[:, :].rearrange("p (u l) -> p u l", u=4), in1=Ct[:], op=ALU.mult),
                 reads=[bpKK, bCt], writes=[bQm])
            B.op("pool", lambda e: e.tensor_tensor(out=Qp[:], in0=Qm[:], in1=bcm(self.ident_b[:]), op=ALU.add), reads=[bQm] + cbufs, writes=[bQp])
            B.op("dve", lambda e: e.tensor_tensor(out=Kg[:], in0=pKtb[:, 0:512].rearrange("p (u l) -> p u l", u=4), in1=bc3(eg, 128), op=ALU.mult),
                 reads=[bpKt, bgs], writes=[bKg])
            B.op("dve", lambda e: e.tensor_tensor(out=kt[:], in0=pKtb[:, 0:512].rearrange("p (u l) -> p u l", u=4), in1=bc3(ekt, 128), op=ALU.mult),
                 reads=[bpKt, bgs], writes=[bkt])
            yield
            Gc = None
            Hc = None
            for lev in range(7):
                sm = smask[:, d * 7 + lev, :]
                pY, bpY = psr.next()
                for u in range(4):
                    rhsH = self.ident_b[:] if Hc is None else Hc[0][:, u, :]
                    B.op("pe", lambda e, u=u, rhsH=rhsH: e.matmul(pY[:, u * 128:(u + 1) * 128], lhsT=Qp[:, u, :], rhs=rhsH, start=True, stop=True),
                         reads=[bQp] + cbufs + ([] if Hc is None else [Hc[1]]), writes=[bpY], inc=(u == 3))
                B.op("dve", lambda e, sm=sm: e.tensor_tensor(out=IYT[:], in0=pY[:, :].rearrange("p (u l) -> p u l", u=4), in1=bcm(sm), op=ALU.mult),
                     reads=[bpY] + cbufs, writes=[bIYT])
                yield
                Gn = sl["G%d" % (lev % 2)]
                Hn = sl["H%d" % (lev % 2)]
                pG, bpG = psr.next()
                for u in range(4):
                    rhsG = self.ident_b[:] if Gc is None else Gc[0][:, u, :]
                    B.op("pe", lambda e, u=u, rhsG=rhsG: e.matmul(pG[:, u * 128:(u + 1) * 128], lhsT=IYT[:, u, :], rhs=rhsG, start=True, stop=True),
                         reads=[bIYT] + cbufs + ([] if Gc is None else [Gc[1]]), writes=[bpG], inc=(u == 3))
                B.op("act", lambda e, Gn=Gn: e.activation(out=Gn[0][:].rearrange("p u l -> p (u l)"), in_=pG[:, :], func=AF.Identity), reads=[bpG], writes=[Gn[1]])
                if lev < 6:
                    pH, bpH = psr.next()
                    for u in range(4):
                        lhsG = self.ident_b[:] if Gc is None else Gc[0][:, u, :]
                        B.op("pe", lambda e, u=u, lhsG=lhsG: e.matmul(pH[:, u * 128:(u + 1) * 128], lhsT=lhsG, rhs=IYT[:, u, :], start=True, stop=True),
                             reads=[bIYT] + cbufs + ([] if Gc is None else [Gc[1]]), writes=[bpH], inc=(u == 3))
                    B.op("act", lambda e, Hn=Hn: e.activation(out=Hn[0][:].rearrange("p u l -> p (u l)"), in_=pH[:, :], func=AF.Identity), reads=[bpH], writes=[Hn[1]])
                    Hc = Hn
                Gc = Gn
                yield
            G, bG = Gc
            pW, bpW = psr.next()
            for u in range(4):
                B.op("pe", lambda e, u=u: e.matmul(pW[:, u * 128:(u + 1) * 128], lhsT=Kg[:, u, :], rhs=G[:, u, :], start=True, stop=True),
                     reads=[bKg, bG], writes=[bpW], inc=(u == 3))
            B.op("act", lambda e: e.activation(out=negW[:].rearrange("p u l -> p (u l)"), in_=pW[:, :], func=AF.Identity, scale=-1.0), reads=[bpW], writes=[bnegW])
            yield
            while step > 0 and ("gdn", step - 1, d, hg) not in done and ("gdn", step - 1, d, hg) in started:
                yield
            sbt, bsb = Sb_cur[d][hg]
            pV, bpV = psr.next()
            for u in range(4):
                B.op("pe", lambda e, u=u: e.matmul(pV[:, u * 128:(u + 1) * 128], lhsT=G[:, u, :], rhs=tv[:, (h0 + u) * 128:(h0 + u + 1) * 128], start=True, stop=False),
                     reads=[bG, bv], writes=[bpV], inc=False)
                B.op("pe", lambda e, u=u: e.matmul(pV[:, u * 128:(u + 1) * 128], lhsT=negW[:, u, :], rhs=sbt[:, u, :], start=False, stop=True),
                     reads=[bnegW, bsb], writes=[bpV], inc=(u == 3))
            B.op("dve", lambda e: e.tensor_tensor(out=vnew[:], in0=pV[:, :].rearrange("p (u l) -> p u l", u=4), in1=bc3(bcol, 128), op=ALU.mult),
                 reads=[bpV, bg], writes=[bvnew])
            yield
            if need_o:
                pO1, bpO1 = psr.next()
                for u in range(4):
                    B.op("pe", lambda e, u=u: e.matmul(pO1[:, u * 128:(u + 1) * 128], lhsT=tq[:, h0 + u, :], rhs=sbt[:, u, :], start=True, stop=True),
                         reads=[bq, bsb], writes=[bpO1], inc=(u == 3))
                o1, bo1 = ring_o1.next()
                B.op("dve", lambda e: e.tensor_tensor(out=o1[:], in0=pO1[:, :].rearrange("p (u l) -> p u l", u=4), in1=bc3(eg, 128), op=ALU.mult),
                     reads=[bpO1, bgs], writes=[bo1])
                pO2, bpO2 = psr.next()
                for u in range(4):
                    B.op("pe", lambda e, u=u: e.matmul(pO2[:, u * 128:(u + 1) * 128], lhsT=attnT[:, u, :], rhs=vnew[:, u, :], start=True, stop=True),
                         reads=[battn, bvnew], writes=[bpO2], inc=(u == 3))
                o, bo = ring_o.next()
                B.op("dve", lambda e: e.tensor_tensor(out=o[:], in0=pO2[:, :].rearrange("p (u l) -> p u l", u=4), in1=o1[:], op=ALU.add),
                     reads=[bpO2, bo1], writes=[bo])
                dst = (self.OF if d == 0 else self.OB)[c - 2]
                B.dma("sp", dst[:, hg * 512:(hg + 1) * 512], o[:].rearrange("p u l -> p (u l)"), bo, reads=[bo])
            pS, bpS = psr.next()
            for u in range(4):
                B.op("pe", lambda e, u=u: e.matmul(pS[:, u * 128:(u + 1) * 128], lhsT=kt[:, u, :], rhs=vnew[:, u, :], start=True, stop=True),
                     reads=[bkt, bvnew], writes=[bpS], inc=(u == 3))
            Sg = S[d][:, h0:h0 + 4, :]
            B.op("pool", lambda e: e.tensor_tensor(out=St[:], in0=Sg, in1=bc3(gte, 128), op=ALU.mult), reads=[bS[d][hg], bgs], writes=[bSt])
            B.op("dve", lambda e: e.tensor_tensor(out=Sg, in0=pS[:, :].rearrange("p (u l) -> p u l", u=4), in1=St[:], op=ALU.add),
                 reads=[bpS, bSt], writes=[bS[d][hg]])
            nsb, bnsb = Sb[d][hg].next()
            B.op("act", lambda e: e.activation(out=nsb[:], in_=Sg, func=AF.Identity), reads=[bS[d][hg]], writes=[bnsb])
            Sb_cur[d][hg] = (nsb, bnsb)
            yield

        self._b_env = dict(data=data, dirc=dirc, cbufs=cbufs, psr=psr, order=order, out_lo=out_lo, out_hi=out_hi, bc3=bc3, bcm=bcm,
                           C=C, bC=bC, Cb=Cb, Cb_cur=Cb_cur, nst=nst, nbf=nbf, n_cur=n_cur, nb_cur=nb_cur, mst=mst, m_cur=m_cur,
                           ring_num=ring_num, ring_h=ring_h)
        ml_group = self.make_ml_group()

        from collections import deque
        pending = deque()
        for step in range(nsteps):
            dirs = [d for d in range(2) if not (d == 0 and order[0][step] >= out_hi)]
            for d in dirs:
                pending.append(("load", step, d))
            for hg in range(2):
                for d in dirs:
                    if not dbg.get("b_no_gdn"):
                        pending.append(("gdn", step, d, hg))
            for d in dirs:
                if not dbg.get("b_no_ml"):
                    pending.append(("ml", step, d))
        free_g = list(range(NG))
        free_m = list(range(NM))
        done = set()
        started = set()
        active = []
        loaded = set()
        while pending or active:
            while pending:
                it = pending[0]
                if it[0] == "load":
                    _, step, d = it
                    if any((k[1] == step - 2 and k[2] == d and k not in done) for k in started):
                        break
                    load_step(step, d)
                    shared_pre(step, d)
                    pending.popleft()
                    continue
                if it[0] == "gdn":
                    _, step, d, hg = it
                    if not free_g:
                        break
                    si = free_g.pop(0)
                    started.add(it)
                    active.append((it, gdn_group(step, d, hg, gslots[si]), ("g", si)))
                    pending.popleft()
                    continue
                if it[0] == "ml":
                    _, step, d = it
                    key_prev = ("ml", step - 1, d)
                    if (step > 0 and key_prev not in done) or not free_m:
                        break
                    si = free_m.pop(0)
                    started.add(it)
                    active.append((it, ml_group(step, d, mslots[si]), ("m", si)))
                    pending.popleft()
                    continue
            for ent in list(active):
                it, gen, (kind, si) = ent
                try:
                    next(gen)
                except StopIteration:
                    active.remove(ent)
                    done.add(it)
                    (free_g if kind == "g" else free_m).append(si)
        B.barrier()
        st.close()

    def make_ml_group(self):
        B = self.B
        env = self._b_env
        data, dirc, cbufs, psr = env["data"], env["dirc"], env["cbufs"], env["psr"]
        bc3, bcm = env["bc3"], env["bcm"]
        C, bC, Cb, Cb_cur = env["C"], env["bC"], env["Cb"], env["Cb_cur"]
        nst, nbf, n_cur, nb_cur, mst, m_cur = env["nst"], env["nbf"], env["n_cur"], env["nb_cur"], env["mst"], env["m_cur"]
        ring_num, ring_h = env["ring_num"], env["ring_h"]
        out_lo, out_hi = env["out_lo"], env["out_hi"]

        def bc3n(ap2, n):
            return ap2.unsqueeze(2).to_broadcast([128, ap2.shape[1], n])

        def ml_group(step, d, sl):
            dd = data[(step, d)]
            dc = dirc[d]
            c = dd["c"]
            need_o = out_lo <= c < out_hi
            tg, bg = dd["GT"]; gs, bgs = dd["gs"]
            mq, bmq = dd["MQ"]; mk, bmk = dd["MK"]; mv, bmv = dd["MV"]
            X, bX = sl["X"]; Y, bY = sl["Y"]; Pm, bPm = sl["Pm"]; PT, bPT = sl["PT"]; Kw, bKw = sl["Kw"]
            sm, bsm = sl["sm"]; Ct, bCt = sl["Ct"]
            bcc = gs[:, 8:12]
            blast = gs[:, 20:24]
            cvec = gs[:, 48:52]
            mprev, bmprev = m_cur[d]
            B.op("pool", lambda e: e.tensor_tensor(out=X[:], in0=bcm(self.ident_f[:]), in1=bc3(cvec, 128), op=ALU.mult), reads=[bgs] + cbufs, writes=[bX])
            pC, bpC = psr.next()
            for u in range(4):
                B.op("pe", lambda e, u=u: e.matmul(pC[:, u * 128:(u + 1) * 128], lhsT=self.ones_f[:], rhs=X[:, u, :], start=True, stop=True),
                     reads=[bX] + cbufs, writes=[bpC], inc=(u == 3))
            B.op("dve", lambda e: e.tensor_tensor(out=Y[:], in0=pC[:, :].rearrange("p (u l) -> p u l", u=4), in1=bc3(bcc, 128), op=ALU.add),
                 reads=[bpC, bgs], writes=[bY])
            B.op("pool", lambda e: e.tensor_tensor(out=Y[:], in0=Y[:], in1=bcm(dc["MB"][:]), op=ALU.add), reads=[bY] + cbufs, writes=[bY])
            B.op("dve", lambda e: e.tensor_reduce(out=sm[:, 0:4], in_=Y[:], axis=AX.X, op=ALU.max), reads=[bY], writes=[bsm])
            B.op("dve", lambda e: e.tensor_tensor(out=sm[:, 4:8], in0=bcc, in1=mprev[:, 0:4], op=ALU.add), reads=[bgs, bmprev], writes=[bsm])
            B.op("dve", lambda e: e.tensor_tensor(out=sm[:, 8:12], in0=sm[:, 0:4], in1=sm[:, 4:8], op=ALU.max), reads=[bsm], writes=[bsm])
            B.op("dve", lambda e: e.tensor_scalar(out=sm[:, 12:16], in0=sm[:, 8:12], scalar1=-1.0, scalar2=None, op0=ALU.mult), reads=[bsm], writes=[bsm])
            if need_o:
                pQK, bpQK = psr.next()
                for u in range(4):
                    B.op("pe", lambda e, u=u: e.matmul(pQK[:, u * 128:(u + 1) * 128], lhsT=mq[:, u, :], rhs=mk[:, u, :], start=True, stop=True),
                         reads=[bmq, bmk], writes=[bpQK], inc=(u == 3))
                for u in range(4):
                    B.op("act", lambda e, u=u: e.activation(out=X[:, u, :], in_=Y[:, u, :], func=AF.Exp, bias=sm[:, 12 + u:13 + u]), reads=[bY, bsm], writes=[bX])
                B.op("dve", lambda e: e.tensor_tensor(out=Pm[:], in0=pQK[:, :].rearrange("p (u l) -> p u l", u=4), in1=X[:], op=ALU.mult),
                     reads=[bpQK, bX], writes=[bPm])
            yield
            if need_o:
                pT, bpT = psr.next()
                pTb = pT[:].bitcast(BF16)
                for u in range(4):
                    B.op("pe", lambda e, u=u: e.transpose(pTb[:, u * 128:(u + 1) * 128], Pm[:, u, :], self.ident_b[:]), reads=[bPm] + cbufs, writes=[bpT], inc=(u == 3))
                B.op("act", lambda e: e.activation(out=PT[:].rearrange("p u l -> p (u l)"), in_=pTb[:, 0:512], func=AF.Identity), reads=[bpT], writes=[bPT])
                B.op("dve", lambda e: e.tensor_tensor(out=sm[:, 16:20], in0=sm[:, 4:8], in1=sm[:, 8:12], op=ALU.subtract), reads=[bsm], writes=[bsm])
                B.op("act", lambda e: e.activation(out=sm[:, 16:20], in_=sm[:, 16:20], func=AF.Exp), reads=[bsm], writes=[bsm])
                B.op("act", lambda e: e.activation(out=sm[:, 20:24], in_=sm[:, 12:16], func=AF.Exp), reads=[bsm], writes=[bsm])
                yield
                cbt, bcb = Cb_cur[d]
                nbt, bnb = nb_cur[d]
                num, bnum = ring_num.next()
                hh, bhh = ring_h.next()
                for pr in range(2):
                    pN1, bpN1 = psr.next()
                    pN2, bpN2 = psr.next()
                    for uu in range(2):
                        u = pr * 2 + uu
                        B.op("pe", lambda e, u=u, uu=uu, pN1=pN1: e.matmul(pN1[:, uu * 256:(uu + 1) * 256], lhsT=mq[:, u, :], rhs=cbt[:, u, :], start=True, stop=True),
                             reads=[bmq, bcb], writes=[bpN1], inc=(uu == 1))
                    for uu in range(2):
                        u = pr * 2 + uu
                        B.op("pe", lambda e, u=u, uu=uu, pN2=pN2: e.matmul(pN2[:, uu * 256:(uu + 1) * 256], lhsT=PT[:, u, :], rhs=mv[:, u * 256:(u + 1) * 256], start=True, stop=True),
                             reads=[bPT, bmv], writes=[bpN2], inc=(uu == 1))
                    B.op("dve", lambda e, pr=pr, pN1=pN1: e.tensor_tensor(out=num[:, pr * 2:pr * 2 + 2, :], in0=pN1[:, :].rearrange("p (u l) -> p u l", u=2),
                                                                         in1=bc3n(sm[:, 16 + pr * 2:18 + pr * 2], 256), op=ALU.mult), reads=[bpN1, bsm], writes=[bnum])
                    B.op("dve", lambda e, pr=pr, pN2=pN2: e.tensor_tensor(out=num[:, pr * 2:pr * 2 + 2, :], in0=pN2[:, :].rearrange("p (u l) -> p u l", u=2),
                                                                         in1=num[:, pr * 2:pr * 2 + 2, :], op=ALU.add), reads=[bpN2, bnum], writes=[bnum])
                pDn, bpDn = psr.next()
                for u in range(4):
                    B.op("pe", lambda e, u=u: e.matmul(pDn[:, u:u + 1], lhsT=mq[:, u, :], rhs=nbt[:, u:u + 1], start=True, stop=True), reads=[bmq, bnb], writes=[bpDn], inc=False)
                for u in range(4):
                    B.op("pe", lambda e, u=u: e.matmul(pDn[:, 4 + u:5 + u], lhsT=PT[:, u, :], rhs=self.ones_b[:, 0:1], start=True, stop=True),
                         reads=[bPT] + cbufs, writes=[bpDn], inc=(u == 3))
                B.op("dve", lambda e: e.tensor_tensor(out=sm[:, 24:28], in0=pDn[:, 0:4], in1=sm[:, 16:20], op=ALU.mult), reads=[bpDn, bsm], writes=[bsm])
                B.op("dve", lambda e: e.tensor_tensor(out=sm[:, 24:28], in0=pDn[:, 4:8], in1=sm[:, 24:28], op=ALU.add), reads=[bpDn, bsm], writes=[bsm])
                B.op("dve", lambda e: e.tensor_tensor(out=sm[:, 24:28], in0=sm[:, 24:28], in1=sm[:, 24:28], op=ALU.mult), reads=[bsm], writes=[bsm])
                B.op("dve", lambda e: e.tensor_tensor(out=sm[:, 28:32], in0=sm[:, 20:24], in1=sm[:, 20:24], op=ALU.mult), reads=[bsm], writes=[bsm])
                B.op("dve", lambda e: e.tensor_tensor(out=sm[:, 24:28], in0=sm[:, 24:28], in1=sm[:, 28:32], op=ALU.max), reads=[bsm], writes=[bsm])
                B.op("pool", lambda e: e.tensor_tensor(out=sm[:, 28:32], in0=sm[:, 24:28], in1=self.nhalf[:, 0:4], op=ALU.pow), reads=[bsm] + cbufs, writes=[bsm])
                B.op("dve", lambda e: e.tensor_tensor(out=hh[:], in0=num[:], in1=bc3n(sm[:, 28:32], 256), op=ALU.mult), reads=[bnum, bsm], writes=[bhh])
                dst = (self.HF if d == 0 else self.HB)[c - 2]
                B.dma("sp", dst[:, :], hh[:].rearrange("p u l -> p (u l)"), bhh, reads=[bhh])
                yield
            pSel, bpSel = psr.next()
            B.op("pe", lambda e: e.matmul(pSel[:, 0:4], lhsT=dc["SEL"][:], rhs=sm[:, 8:12], start=True, stop=True), reads=[bsm] + cbufs, writes=[bpSel])
            mnew, bmnew = mst[d].next()
            B.op("act", lambda e: e.activation(out=mnew[:], in_=pSel[:, 0:4], func=AF.Identity), reads=[bpSel], writes=[bmnew])
            B.op("dve", lambda e: e.tensor_tensor(out=sm[:, 32:36], in0=cvec, in1=blast, op=ALU.add), reads=[bgs], writes=[bsm])
            B.op("dve", lambda e: e.tensor_tensor(out=sm[:, 32:36], in0=sm[:, 32:36], in1=mnew[:], op=ALU.subtract), reads=[bsm, bmnew], writes=[bsm])
            B.op("dve", lambda e: e.tensor_tensor(out=sm[:, 36:40], in0=blast, in1=mprev[:, 0:4], op=ALU.add), reads=[bgs, bmprev], writes=[bsm])
            B.op("dve", lambda e: e.tensor_tensor(out=sm[:, 36:40], in0=sm[:, 36:40], in1=mnew[:], op=ALU.subtract), reads=[bsm, bmnew], writes=[bsm])
            B.op("act", lambda e: e.activation(out=sm[:, 32:40], in_=sm[:, 32:40], func=AF.Exp), reads=[bsm], writes=[bsm])
            pKt, bpKt = psr.next()
            pKtb = pKt[:].bitcast(BF16)
            for u in range(4):
                B.op("pe", lambda e, u=u: e.transpose(pKtb[:, u * 128:(u + 1) * 128], mk[:, u, :], self.ident_b[:]), reads=[bmk] + cbufs, writes=[bpKt], inc=(u == 3))
            B.op("dve", lambda e: e.tensor_tensor(out=Kw[:], in0=pKtb[:, 0:512].rearrange("p (u l) -> p u l", u=4), in1=bc3(sm[:, 32:36], 128), op=ALU.mult),
                 reads=[bpKt, bsm], writes=[bKw])
            m_cur[d] = (mnew, bmnew)
            yield
            for pr in range(2):
                pC2, bpC2 = psr.next()
                for uu in range(2):
                    u = pr * 2 + uu
                    B.op("pe", lambda e, u=u, uu=uu, pC2=pC2: e.matmul(pC2[:, uu * 256:(uu + 1) * 256], lhsT=Kw[:, u, :], rhs=mv[:, u * 256:(u + 1) * 256], start=True, stop=True),
                         reads=[bKw, bmv], writes=[bpC2], inc=(uu == 1))
                B.op("pool", lambda e, pr=pr: e.tensor_tensor(out=Ct[:, pr * 2:pr * 2 + 2, :], in0=C[d][:, pr * 2:pr * 2 + 2, :], in1=bc3n(sm[:, 36 + pr * 2:38 + pr * 2], 256), op=ALU.mult),
                     reads=[bC[d], bsm], writes=[bCt])
                B.op("dve", lambda e, pr=pr, pC2=pC2: e.tensor_tensor(out=C[d][:, pr * 2:pr * 2 + 2, :], in0=pC2[:, :].rearrange("p (u l) -> p u l", u=2), in1=Ct[:, pr * 2:pr * 2 + 2, :], op=ALU.add),
                     reads=[bpC2, bCt], writes=[bC[d]])
            pN, bpN = psr.next()
            for u in range(4):
                B.op("pe", lambda e, u=u: e.matmul(pN[:, u:u + 1], lhsT=Kw[:, u, :], rhs=self.ones_b[:, 0:1], start=True, stop=True), reads=[bKw] + cbufs, writes=[bpN], inc=(u == 3))
            nold, bnold = n_cur[d]
            nnew, bnnew = nst[d].next()
            B.op("dve", lambda e: e.tensor_tensor(out=nnew[:, 4:8], in0=nold[:, 0:4], in1=sm[:, 36:40], op=ALU.mult), reads=[bnold, bsm], writes=[bnnew])
            B.op("dve", lambda e: e.tensor_tensor(out=nnew[:, 0:4], in0=pN[:, 0:4], in1=nnew[:, 4:8], op=ALU.add), reads=[bpN, bnnew], writes=[bnnew])
            nbn, bnbn = nbf[d].next()
            B.op("act", lambda e: e.activation(out=nbn[:], in_=nnew[:, 0:4], func=AF.Identity), reads=[bnnew], writes=[bnbn])
            cbn, bcbn = Cb[d].next()
            B.op("act", lambda e: e.activation(out=cbn[:], in_=C[d][:], func=AF.Identity), reads=[bC[d]], writes=[bcbn])
            n_cur[d] = (nnew, bnnew)
            nb_cur[d] = (nbn, bnbn)
            Cb_cur[d] = (cbn, bcbn)
            yield

        return ml_group

    def phaseC1(self):
        B, nc, inp = self.B, self.nc, self.inp
        st = ExitStack()
        dbg = self.debug
        wo, bwo = self.load_w_bf16(st, "c_wo", inp["w_o"], 8, 4096, 8)
        wbg, bwbg = self.load_w_bf16(st, "c_wbg", inp["w_bg"], 8, 1024, 2)
        wbm, bwbm = self.load_w_bf16(st, "c_wbm", inp["w_bm"], 8, 1024, 2)
        wout, bwout = self.load_w_bf16(st, "c_wout", inp["w_out"], 8, 1024, 2)
        nwb = B.sb(st, "c_nwb", [128, 2, 1024], F32)
        bnwb = Buf("c_nwb")
        B.dma("sp", nwb[:, 0, :], inp["gnw_bc"][:, :], bnwb, writes=[bnwb])
        B.dma("sp", nwb[:, 1, :], inp["mnw_bc"][:, :], bnwb, writes=[bnwb])
        zero = B.sb(st, "c_zero", [64, 1024], F32)
        bzero = Buf("c_zero")
        B.op("pool", lambda e: e.memset(zero[:], 0.0), writes=[bzero])
        bX1 = Buf("X1")
        B.dma("sp", self.X1[0:64, :], zero[:], bzero, reads=[bzero], writes=[bX1])
        NS = 2
        xt = B.sb(st, "c_x", [128, NS, 1024], F32)
        bxts = [Buf("c_x%d" % i) for i in range(NS)]
        xn = B.sb(st, "c_xn", [128, NS, 1024], BF16); bxn = Buf("c_xn")
        sq = B.sb(st, "c_sq", [128, 24], F32); bsq = Buf("c_sq")
        junk = None; bjunk = None
        hxT = B.sb(st, "c_hxT", [128, 8, NS * 128], BF16); bhx = Buf("c_hxT")
        oa = B.sb(st, "c_oa", [128, 1024], F32); boa = Buf("c_oa")
        ob = B.sb(st, "c_ob", [128, 1024], F32); bob = Buf("c_ob")
        gt = B.sb(st, "c_gt", [128, 1024], F32); bgt = Buf("c_gt")
        osq = B.sb(st, "c_osq", [128, 1024], F32); bosq = Buf("c_osq")
        sm = B.sb(st, "c_sm", [128, 32], F32); bsm = Buf("c_sm")
        og = B.sb(st, "c_og", [128, 1024], BF16); bog = Buf("c_og")
        brT = [B.sb(st, "c_brT%d" % i, [128, 8, NS * 128], BF16) for i in range(2)]
        bbrT = [Buf("c_brT%d" % i) for i in range(2)]
        sg = Ring(B, st, "c_sg", 4, [128, NS * 128], F32)
        yt = Ring(B, st, "c_yt", 2, [128, NS * 128], F32)
        mT = B.sb(st, "c_mT", [128, 8, NS * 128], BF16); bmT = Buf("c_mT")
        tmp = Ring(B, st, "c_tmp", 2, [128, 512], F32)
        ptr = Ring(B, st, "c_ptr", 2, [128, 512], F32, psum=True)
        pmm = Ring(B, st, "c_pmm", 5, [128, 512], F32, psum=True)
        nt = OWN_T // 128
        sts = []
        i = 0
        while i < nt:
            ns = min(NS, nt - i)
            sts.append((i, ns))
            i += ns
        if dbg.get("c1_tiles"):
            sts = sts[: dbg["c1_tiles"]]
        for (t0, ns) in sts:
            n = ns * 128
            for s in range(ns):
                B.dma("sp", xt[:, s, :], inp["x"][(t0 + s) * 128:(t0 + s + 1) * 128, :], bxts[s], writes=[bxts[s]])
            self.norm_transpose(xt, bxts, ns, xn, bxn, sq, bsq, junk, bjunk, ptr, hxT, bhx, 0, 0)
            for br in range(2):
                nh, hd = (8, 128) if br == 0 else (4, 256)
                srcf, srcb = (self.OF, self.OB) if br == 0 else (self.HF, self.HB)
                for s in range(ns):
                    c = t0 + s
                    B.dma("sp", oa[:], srcf[c], boa, writes=[boa])
                    B.dma("sp", ob[:], srcb[c], bob, writes=[bob])
                    for hf in range(2):
                        p, bp = pmm.next()
                        for k in range(8):
                            B.op("pe", lambda e, k=k, p=p, s=s, hf=hf, br=br: e.matmul(
                                p[:], lhsT=hxT[:, k, s * 128:(s + 1) * 128], rhs=wo[:, k, br * 1024 + hf * 512: br * 1024 + (hf + 1) * 512],
                                start=(k == 0), stop=(k == 7)), reads=[bhx, bwo], writes=[bp], inc=(k == 7))
                        B.op("act", lambda e, p=p, hf=hf, br=br: e.activation(out=gt[:, hf * 512:(hf + 1) * 512], in_=p[:],
                                                                            func=(AF.Silu if br == 0 else AF.Sigmoid)), reads=[bp], writes=[bgt])
                    B.op("pool", lambda e, br=br: e.tensor_tensor(out=gt[:], in0=gt[:], in1=nwb[:, br, :], op=ALU.mult), reads=[bgt, bnwb], writes=[bgt])
                    B.op("dve", lambda e: e.tensor_tensor(out=oa[:], in0=oa[:], in1=ob[:], op=ALU.add), reads=[boa, bob], writes=[boa])
                    B.op("pool", lambda e: e.tensor_tensor(out=osq[:], in0=oa[:], in1=oa[:], op=ALU.mult), reads=[boa], writes=[bosq])
                    B.op("dve", lambda e, nh=nh: e.tensor_reduce(out=sm[:, 0:nh], in_=osq[:].rearrange("p (h e) -> p h e", h=nh), axis=AX.X, op=ALU.add),
                         reads=[bosq], writes=[bsm])
                    B.op("dve", lambda e, nh=nh, hd=hd: e.tensor_scalar(out=sm[:, 8:8 + nh], in0=sm[:, 0:nh], scalar1=float(1.0 / hd), scalar2=float(EPS),
                                                                      op0=ALU.mult, op1=ALU.add), reads=[bsm], writes=[bsm])
                    B.op("pool", lambda e, nh=nh: e.tensor_tensor(out=sm[:, 16:16 + nh], in0=sm[:, 8:8 + nh], in1=self.nhalf[:, 0:nh], op=ALU.pow),
                         reads=[bsm, self.cb], writes=[bsm])
                    B.op("dve", lambda e, nh=nh, hd=hd: e.tensor_tensor(out=osq[:].rearrange("p (h e) -> p h e", h=nh), in0=oa[:].rearrange("p (h e) -> p h e", h=nh),
                                                                      in1=sm[:, 16:16 + nh].unsqueeze(2).to_broadcast([128, nh, hd]), op=ALU.mult),
                         reads=[boa, bsm], writes=[bosq])
                    B.op("dve", lambda e: e.tensor_tensor(out=og[:], in0=osq[:], in1=gt[:], op=ALU.mult), reads=[bosq, bgt], writes=[bog])
                    p, bp = ptr.next()
                    pb = p[:].bitcast(BF16)
                    for k in range(8):
                        B.op("pe", lambda e, k=k, pb=pb: e.transpose(pb[:, k * 128:(k + 1) * 128], og[:, k * 128:(k + 1) * 128], self.ident_b[:]),
                             reads=[bog, self.cb], writes=[bp], inc=(k == 7))
                    B.op("act", lambda e, pb=pb, s=s, br=br: e.activation(out=brT[br][:, :, s * 128:(s + 1) * 128], in_=pb[:, 0:1024].rearrange("p (k t) -> p k t", k=8),
                                                                          func=AF.Identity), reads=[bp], writes=[bbrT[br]])
            for ncn in range(8):
                sgs = []
                for gi in range(2):
                    p, bp = pmm.next()
                    for k in range(8):
                        B.op("pe", lambda e, k=k, p=p, gi=gi, ncn=ncn: e.matmul(p[:, 0:n], lhsT=wo[:, k, 2048 + gi * 1024 + ncn * 128: 2048 + gi * 1024 + (ncn + 1) * 128],
                                                                               rhs=hxT[:, k, 0:n], start=(k == 0), stop=(k == 7)), reads=[bhx, bwo], writes=[bp], inc=(k == 7))
                    g_, bg_ = sg.next()
                    B.op("act", lambda e, p=p, g_=g_: e.activation(out=g_[:, 0:n], in_=p[:, 0:n], func=AF.Sigmoid), reads=[bp], writes=[bg_])
                    sgs.append((g_, bg_))
                ys = []
                for br, (w, bw) in enumerate(((wbg, bwbg), (wbm, bwbm))):
                    p, bp = pmm.next()
                    for k in range(8):
                        B.op("pe", lambda e, k=k, p=p, w=w, br=br, ncn=ncn: e.matmul(p[:, 0:n], lhsT=w[:, k, ncn * 128:(ncn + 1) * 128], rhs=brT[br][:, k, 0:n],
                                                                                    start=(k == 0), stop=(k == 7)), reads=[bbrT[br], bw], writes=[bp], inc=(k == 7))
                    ys.append((p, bp))
                y_, by_ = yt.next()
                B.op("dve", lambda e, y_=y_: e.tensor_tensor(out=y_[:, 0:n], in0=ys[0][0][:, 0:n], in1=sgs[0][0][:, 0:n], op=ALU.mult), reads=[ys[0][1], sgs[0][1]], writes=[by_])
                g1, bg1 = sgs[1]
                B.op("dve", lambda e, g1=g1: e.tensor_tensor(out=g1[:, 0:n], in0=ys[1][0][:, 0:n], in1=g1[:, 0:n], op=ALU.mult), reads=[ys[1][1], bg1], writes=[bg1])
                B.op("pool", lambda e, y_=y_, g1=g1, ncn=ncn: e.tensor_tensor(out=mT[:, ncn, 0:n], in0=y_[:, 0:n], in1=g1[:, 0:n], op=ALU.add), reads=[by_, bg1], writes=[bmT])
            for s in range(ns):
                for hf in range(2):
                    p, bp = pmm.next()
                    for k in range(8):
                        B.op("pe", lambda e, k=k, p=p, s=s, hf=hf: e.matmul(p[:], lhsT=mT[:, k, s * 128:(s + 1) * 128], rhs=wout[:, k, hf * 512:(hf + 1) * 512],
                                                                           start=(k == 0), stop=(k == 7)), reads=[bmT, bwout], writes=[bp], inc=(k == 7))
                    t_, bt_ = tmp.next()
                    B.op("dve", lambda e, p=p, t_=t_, hf=hf: e.tensor_tensor(out=t_[:], in0=p[:], in1=self.gate_bc[:, 0, hf * 512:(hf + 1) * 512], op=ALU.mult),
                         reads=[bp, self.bgate], writes=[bt_])
                    B.op("pool", lambda e, t_=t_, s=s, hf=hf: e.tensor_tensor(out=xt[:, s, hf * 512:(hf + 1) * 512], in0=xt[:, s, hf * 512:(hf + 1) * 512], in1=t_[:], op=ALU.add),
                         reads=[bt_, bxts[s]], writes=[bxts[s]])
                B.dma("sp", self.X1[64 + (t0 + s) * 128: 64 + (t0 + s + 1) * 128, :], xt[:, s, :], bxts[s], reads=[bxts[s]], writes=[bX1])
        B.barrier()
        st.close()

    def precast_wup(self):
        B = self.B
        self.WUPB = B.dram("WUPB", [44, 128, 8, 128], BF16)
        self.bwupb = Buf("WUPB")
        src = self.inp["w_up"].rearrange("(k p) (c j) -> c p k j", p=128, j=128)
        for c in range(44):
            B.dma("pool", self.WUPB[c], src[c], self.bwupb, writes=[self.bwupb])

    def phaseC2(self):
        B, nc, inp = self.B, self.nc, self.inp
        st = ExitStack()
        dbg = self.debug
        wd = B.sb(st, "d_wd", [128, 22, 1024], BF16)
        bwd = Buf("d_wd")
        wdv = inp["w_down"].rearrange("(c p) n -> p c n", p=128)
        for i in range(0, 22, 6):
            j = min(22, i + 6)
            B.dma("pool", wd[:, i:j, :], wdv[:, i:j, :], bwd, writes=[bwd])
        cw = B.sb(st, "d_cw", [128, 44, 9], F32)
        nob = B.sb(st, "d_nob", [128, 1024], F32)
        bsm0 = Buf("d_small")
        B.dma("sp", cw[:], inp["ffn_cw"][:, :, :], bsm0, writes=[bsm0])
        B.dma("sp", nob[:], inp["now_bc"][:, :], bsm0, writes=[bsm0])
        wup = Ring(B, st, "d_wup", 3, [128, 2, 8, 128], BF16)
        xt = B.sb(st, "d_x", [128, 5, 1024], F32)
        bxts = [Buf("d_x%d" % i) for i in range(5)]
        xn = B.sb(st, "d_xn", [128, 5, 1024], BF16); bxn = Buf("d_xn")
        sq = B.sb(st, "d_sq", [128, 24], F32); bsq = Buf("d_sq")
        junk = B.sb(st, "d_junk", [128, 1024], BF16); bjunk = Buf("d_junk")
        hxT = B.sb(st, "d_hxT", [128, 8, 640], BF16); bhx = Buf("d_hxT")
        upad = Ring(B, st, "d_up", 4, [128, 10, 66], F32)
        acc = Ring(B, st, "d_acc", 4, [128, 8, 64], F32)
        sgt = Ring(B, st, "d_sg", 2, [128, 512], F32)
        ctmp = Ring(B, st, "d_ctmp", 3, [128, 8, 64], F32)
        aT = B.sb(st, "d_aT", [128, 22, 512], BF16); baT = Buf("d_aT")
        xo = B.sb(st, "d_xo", [128, 4, 1024], F32)
        bxo = [Buf("d_xo%d" % i) for i in range(4)]
        t2 = Ring(B, st, "d_t2", 2, [128, 512], F32)
        sq2 = B.sb(st, "d_sq2", [128, 16], F32); bsq2 = Buf("d_sq2")
        ptr = Ring(B, st, "d_ptr", 2, [128, 512], F32, psum=True)
        pu = Ring(B, st, "d_pu", 4, [128, 512], F32, psum=True)
        pd = Ring(B, st, "d_pd", 2, [128, 512], F32, psum=True)
        for (u_, bu_) in upad.slots:
            B.op("pool", lambda e, u_=u_: e.memset(u_[:], 0.0), writes=[bu_])
        nblk = dbg.get("c2_blocks", 8)
        bX1 = Buf("X1r")
        for j in range(nblk):
            r0 = 512 * j
            for s in range(5):
                B.dma("sp", xt[:, s, :], self.X1[r0 + s * 128: r0 + (s + 1) * 128, :], bxts[s], writes=[bxts[s]])
            for s in range(4):
                B.dma("sp", xo[:, s, :], self.X1[r0 + 64 + s * 128: r0 + 64 + (s + 1) * 128, :], bxo[s], writes=[bxo[s]])
            self.norm_transpose(xt, bxts, 5, xn, bxn, sq, bsq, junk, bjunk, ptr, hxT, bhx, 0, 4)
            for c in range(22):
                w, bw = wup.next()
                B.dma("sp", w[:, 0], self.WUPB[c], bw, reads=[self.bwupb], writes=[bw])
                B.dma("sp", w[:, 1], self.WUPB[22 + c], bw, reads=[self.bwupb], writes=[bw])
                accs = []
                for part in range(2):
                    ch = c + 22 * part
                    p1, bp1 = pu.next()
                    p2, bp2 = pu.next()
                    for k in range(8):
                        B.op("pe", lambda e, k=k, p1=p1, w=w, part=part: e.matmul(p1[:], lhsT=w[:, part, k, :], rhs=hxT[:, k, 0:512], start=(k == 0), stop=(k == 7)),
                             reads=[bw, bhx], writes=[bp1], inc=(k == 7))
                    for k in range(8):
                        B.op("pe", lambda e, k=k, p2=p2, w=w, part=part: e.matmul(p2[:, 0:128], lhsT=w[:, part, k, :], rhs=hxT[:, k, 512:640], start=(k == 0), stop=(k == 7)),
                             reads=[bw, bhx], writes=[bp2], inc=(k == 7))
                    u_, bu_ = upad.next()
                    B.op("act", lambda e, u_=u_, p1=p1: e.activation(out=u_[:, 0:8, 1:65], in_=p1[:].rearrange("p (r c) -> p r c", c=64), func=AF.Identity),
                         reads=[bp1], writes=[bu_])
                    B.op("act", lambda e, u_=u_, p2=p2: e.activation(out=u_[:, 8:10, 1:65], in_=p2[:, 0:128].rearrange("p (r c) -> p r c", c=64), func=AF.Identity),
                         reads=[bp2], writes=[bu_])
                    if j == 0:
                        B.op("pool", lambda e, u_=u_: e.memset(u_[:, 0:1, :], 0.0), writes=[bu_])
                    a_, ba_ = acc.next()
                    first = True
                    for dr in range(3):
                        for dc_ in range(3):
                            wsc = cw[:, ch, dr * 3 + dc_: dr * 3 + dc_ + 1]
                            src = u_[:, dr:dr + 8, dc_:dc_ + 64]
                            if part == 0:
                                if first:
                                    B.op("dve", lambda e, a_=a_, src=src, wsc=wsc: e.tensor_scalar(out=a_[:], in0=src, scalar1=wsc, scalar2=None, op0=ALU.mult),
                                         reads=[bu_, bsm0], writes=[ba_])
                                else:
                                    B.op("dve", lambda e, a_=a_, src=src, wsc=wsc: e.scalar_tensor_tensor(out=a_[:], in0=src, scalar=wsc, in1=a_[:], op0=ALU.mult, op1=ALU.add),
                                         reads=[bu_, bsm0, ba_], writes=[ba_])
                            else:
                                if first:
                                    B.op("act", lambda e, a_=a_, src=src, wsc=wsc: e.activation(out=a_[:], in_=src, func=AF.Identity, scale=wsc), reads=[bu_, bsm0], writes=[ba_])
                                else:
                                    c_, bc_ = ctmp.next()
                                    B.op("act", lambda e, c_=c_, src=src, wsc=wsc: e.activation(out=c_[:], in_=src, func=AF.Identity, scale=wsc), reads=[bu_, bsm0], writes=[bc_])
                                    B.op("pool", lambda e, a_=a_, c_=c_: e.tensor_tensor(out=a_[:], in0=a_[:], in1=c_[:], op=ALU.add), reads=[ba_, bc_], writes=[ba_])
                            first = False
                    accs.append((a_, ba_))
                s_, bs_ = sgt.next()
                B.op("act", lambda e, s_=s_: e.activation(out=s_[:], in_=accs[0][0][:].rearrange("p r c -> p (r c)"), func=AF.Silu), reads=[accs[0][1]], writes=[bs_])
                B.op("dve", lambda e, s_=s_, c=c: e.tensor_tensor(out=aT[:, c, :], in0=s_[:], in1=accs[1][0][:].rearrange("p r c -> p (r c)"), op=ALU.mult),
                     reads=[bs_, accs[1][1]], writes=[baT])
            for s in range(4):
                for hf in range(2):
                    p, bp = pd.next()
                    for c in range(22):
                        B.op("pe", lambda e, c=c, p=p, s=s, hf=hf: e.matmul(p[:], lhsT=aT[:, c, s * 128:(s + 1) * 128], rhs=wd[:, c, hf * 512:(hf + 1) * 512],
                                                                           start=(c == 0), stop=(c == 21)), reads=[baT, bwd], writes=[bp], inc=(c == 21))
                    t_, bt_ = t2.next()
                    B.op("dve", lambda e, p=p, t_=t_, hf=hf: e.tensor_tensor(out=t_[:], in0=p[:], in1=self.gate_bc[:, 1, hf * 512:(hf + 1) * 512], op=ALU.mult),
                         reads=[bp, self.bgate], writes=[bt_])
                    B.op("dve", lambda e, t_=t_, s=s, hf=hf: e.tensor_tensor(out=xo[:, s, hf * 512:(hf + 1) * 512], in0=xo[:, s, hf * 512:(hf + 1) * 512], in1=t_[:], op=ALU.add),
                         reads=[bt_, bxo[s]], writes=[bxo[s]])
                B.op("act", lambda e, s=s: e.activation(out=junk[:], in_=xo[:, s, :], func=AF.Square, accum_out=sq2[:, s:s + 1]), reads=[bxo[s]], writes=[bjunk, bsq2])
                B.op("dve", lambda e, s=s: e.tensor_scalar(out=sq2[:, 4 + s:5 + s], in0=sq2[:, s:s + 1], scalar1=float(D * EPS), scalar2=None, op0=ALU.add), reads=[bsq2], writes=[bsq2])
                B.op("pool", lambda e, s=s: e.tensor_tensor(out=sq2[:, 8 + s:9 + s], in0=sq2[:, 4 + s:5 + s], in1=self.nhalf[:, 0:1], op=ALU.pow), reads=[bsq2, self.cb], writes=[bsq2])
                B.op("dve", lambda e, s=s: e.scalar_tensor_tensor(out=xo[:, s, :], in0=xo[:, s, :], scalar=sq2[:, 8 + s:9 + s], in1=nob[:], op0=ALU.mult, op1=ALU.mult),
                     reads=[bxo[s], bsq2, bsm0], writes=[bxo[s]])
                B.op("act", lambda e, s=s: e.activation(out=xo[:, s, :], in_=xo[:, s, :], func=AF.Identity, scale=32.0), reads=[bxo[s]], writes=[bxo[s]])
                B.dma("sp", self.out[j * 512 + s * 128: j * 512 + (s + 1) * 128, :], xo[:, s, :], bxo[s], reads=[bxo[s]])
        B.barrier()
        st.close()


def build_program(debug=None):
    P = Prog(debug=debug)
    P.precast_wup()
    P.phase0()
    P.phaseA()
    P.phaseB()
    P.phaseC1()
    P.phaseC2()
    P.top.close()
    return P.B.finish(), P


_CACHE = {}


def kernel(**inputs):
    inp = {k: np.asarray(v) for k, v in inputs.items()}
    if "nc" not in _CACHE:
        _CACHE["nc"] = build_program()[0]
    nc = _CACHE["nc"]
    in_maps = [prep_core(inp, core) for core in range(8)]
    res = run_bass_kernel_spmd(nc, in_maps, core_ids=list(range(8)))
    out = np.empty((4, T, D), np.float32)
    for core in range(8):
        o = np.asarray(res.results[core]["out"], np.float32)
        b = core // 2
        if core % 2 == 0:
            out[b, 0:4096] = o
        else:
            out[b, 4096:8192] = o[::-1]
    return out
```

```python
import numpy as np
from contextlib import ExitStack

import concourse.bass as bass
import concourse.mybir as mybir
from concourse.bass_utils import run_bass_kernel_spmd

F32 = mybir.dt.float32
BF16 = mybir.dt.bfloat16
AF = mybir.ActivationFunctionType
ALU = mybir.AluOpType
AX = mybir.AxisListType

D = 1024
T = 8192
TC = 256
KD = 8
EPS = 1e-6
NEG = -1.0e30


class Buf:
    __slots__ = ("name", "w", "r", "dsem")

    def __init__(self, name):
        self.name = name
        self.w = None
        self.r = {}
        self.dsem = None


class Builder:
    def __init__(self):
        self.nc = bass.Bass("TRN2", target_bir_lowering=False)
        nc = self.nc
        self.es = ExitStack()
        self.es.enter_context(nc.allow_low_precision("bf16 matmul operands, fp32 accumulation"))
        self.engs = {"pe": nc.tensor, "act": nc.scalar, "dve": nc.vector, "pool": nc.gpsimd, "sp": nc.sync}
        self.sems = {}
        self.cnt = {}
        self.seen = {e: {} for e in self.engs}
        for e in self.engs:
            self.sems[e] = self.es.enter_context(nc.semaphore("s_" + e))
            self.cnt[e] = 0
        self.ndsem = 0
        self.nins = 0

    def sb(self, stack, name, shape, dt):
        return stack.enter_context(self.nc.sbuf_tensor(name, list(shape), dt))

    def ps(self, stack, name, shape, dt=F32):
        return stack.enter_context(self.nc.psum_tensor(name, list(shape), dt))

    def dram(self, name, shape, dt, kind="Internal"):
        return self.nc.dram_tensor(name, list(shape), dt, kind=kind).ap()

    def new_dsem(self):
        k = "d%d" % self.ndsem
        self.ndsem += 1
        self.sems[k] = self.es.enter_context(self.nc.semaphore(k))
        self.cnt[k] = 0
        return k

    def _deps(self, eng, reads, writes):
        deps = {}

        def add(k, v):
            if deps.get(k, 0) < v:
                deps[k] = v

        for b in reads:
            if b.w is not None:
                add(*b.w)
        for b in writes:
            if b.w is not None and b.w[0] != eng:
                add(*b.w)
            for k, v in b.r.items():
                if k != eng:
                    add(k, v)
        return deps

    def _emit_waits(self, eng, deps):
        e = self.engs[eng]
        seen = self.seen[eng]
        for k, v in deps.items():
            if seen.get(k, 0) >= v:
                continue
            assert v <= self.cnt[k], "wait on %s=%d never reached (issued %d)" % (k, v, self.cnt[k])
            e.wait_ge(self.sems[k], v)
            seen[k] = v

    def op(self, eng, fn, reads=(), writes=(), inc=True):
        self._emit_waits(eng, self._deps(eng, reads, writes))
        ins = fn(self.engs[eng])
        self.nins += 1
        if inc:
            self.cnt[eng] += 1
            ins.then_inc(self.sems[eng], 1)
            tok = (eng, self.cnt[eng])
        else:
            tok = (eng, self.cnt[eng] + 1)
        for b in reads:
            if b.r.get(eng, 0) < tok[1]:
                b.r[eng] = tok[1]
        for b in writes:
            b.w = tok
            b.r = {}
        return tok

    def dma(self, q, out, in_, sem_buf, reads=(), writes=()):
        self._emit_waits(q, self._deps("__dma__", reads, writes))
        ins = self.engs[q].dma_start(out=out, in_=in_)
        self.nins += 1
        if sem_buf.dsem is None:
            sem_buf.dsem = self.new_dsem()
        k = sem_buf.dsem
        self.cnt[k] += 16
        ins.then_inc(self.sems[k], 16)
        tok = (k, self.cnt[k])
        for b in reads:
            if b.r.get(k, 0) < tok[1]:
                b.r[k] = tok[1]
        for b in writes:
            b.w = tok
            b.r = {}
        return tok

    def barrier(self):
        for e in self.engs:
            self._emit_waits(e, {k: v for k, v in self.cnt.items() if k != e and v > 0})

    def finish(self):
        self.barrier()
        self.es.close()
        return self.nc


OFF_QKV, OFF_A, OFF_B, OFF_MQ, OFF_MK, OFF_MV, OFF_MI, OFF_MF, OFF_Z, OFF_MO, OFF_GG, OFF_GM, OFF_END = (
    0, 3072, 3088, 3104, 3616, 4128, 5152, 5160, 5168, 6192, 7216, 8240, 9264)
NCH = 66
OWN_T = 4224


def _col(v, n=128):
    v = np.asarray(v, np.float32).reshape(-1, n)
    return np.ascontiguousarray(v.T)


def _rep(v):
    v = np.asarray(v, np.float32).reshape(1, -1)
    return np.ascontiguousarray(np.repeat(v, 128, axis=0))


def _swapdir(a, flip):
    if not flip:
        return a
    h = a.shape[-1] // 2
    return np.concatenate([a[..., h:], a[..., :h]], axis=-1)


def prep_core(inp, core):
    b = core // 2
    flip = core % 2
    f32 = np.float32
    x = inp["x"][b]
    ctx = inp["ctx"][b]
    if flip:
        x = x[::-1]
        ctx = ctx[::-1]
    w_in = inp["w_in"][0]
    m = {}
    m["x"] = np.ascontiguousarray(x, dtype=f32)
    m["ctx"] = np.ascontiguousarray(ctx, dtype=f32)
    m["c_col"] = _col(inp["c"][b])
    m["cc_col"] = _col(inp["c_ctx"])
    m["w_ada"] = np.ascontiguousarray(inp["w_ada"][0], dtype=f32)
    b_ada = inp["b_ada"][0]
    m["b_ada_col"] = _col(b_ada)
    m["b_ada_g"] = np.ascontiguousarray(np.concatenate([_rep(b_ada[2048:3072]), _rep(b_ada[5120:6144])], axis=1))
    m["n1_col"] = _col(inp["norm1_w"][0])
    m["n2_col"] = _col(inp["norm2_w"][0])
    m["w_qkv"] = np.ascontiguousarray(w_in[:, OFF_QKV:OFF_A])
    wg = np.concatenate([_swapdir(w_in[:, OFF_A:OFF_B], flip), _swapdir(w_in[:, OFF_B:OFF_MQ], flip),
                         _swapdir(w_in[:, OFF_MI:OFF_MF], flip), _swapdir(w_in[:, OFF_MF:OFF_Z], flip)], axis=1)
    m["w_gate"] = np.ascontiguousarray(wg)
    m["w_ml"] = np.ascontiguousarray(w_in[:, OFF_MQ:OFF_MI])
    m["w_o"] = np.ascontiguousarray(w_in[:, OFF_Z:OFF_END])
    gp = np.concatenate([_swapdir(inp["gdn_dt_bias"][0].reshape(-1), flip), _swapdir(inp["gdn_a_log"][0].reshape(-1), flip),
                         _swapdir(inp["ml_igate_b"][0].reshape(-1), flip), _swapdir(inp["ml_fgate_b"][0].reshape(-1), flip)])
    m["gate_p"] = _rep(gp)
    gc = inp["gdn_conv"][0]
    if flip:
        gc = gc[::-1]
    m["gdn_cw"] = np.ascontiguousarray(gc.T.reshape(24, 128, 3).transpose(1, 0, 2), dtype=f32)
    fc = inp["ffn_conv"][0]
    if flip:
        fc = fc[::-1, ::-1]
    m["ffn_cw"] = np.ascontiguousarray(fc.reshape(9, 44, 128).transpose(2, 1, 0), dtype=f32)
    m["gnw_bc"] = _rep(np.tile(inp["gdn_norm_w"][0], 8))
    m["mnw_bc"] = _rep(inp["ml_norm_w"][0].reshape(-1))
    m["now_bc"] = _rep(inp["norm_out_w"])
    m["w_bg"] = np.ascontiguousarray(inp["w_branch_gdn"][0], dtype=f32)
    m["w_bm"] = np.ascontiguousarray(inp["w_branch_ml"][0], dtype=f32)
    m["w_out"] = np.ascontiguousarray(inp["w_out"][0], dtype=f32)
    m["w_up"] = np.ascontiguousarray(inp["w_up"][0], dtype=f32)
    m["w_down"] = np.ascontiguousarray(inp["w_down"][0], dtype=f32)
    m["smask"] = make_smask()
    return m


def make_smask():
    idx = np.arange(128)
    i = idx[None, :]
    j = idx[:, None]
    out = np.zeros((128, 14, 128), np.float32)
    for lev in range(7):
        b = 1 << lev
        same = (i // (2 * b)) == (j // (2 * b))
        f = same & ((i % (2 * b)) < b) & ((j % (2 * b)) >= b)
        g = same & ((j % (2 * b)) < b) & ((i % (2 * b)) >= b)
        out[:, lev, :] = np.where(f, -1.0, 0.0) + np.eye(128)
        out[:, 7 + lev, :] = np.where(g, -1.0, 0.0) + np.eye(128)
    return out


IN_SHAPES = {
    "x": [T, D], "ctx": [TC, D], "c_col": [128, 8], "cc_col": [128, 8], "w_ada": [D, 6144],
    "b_ada_col": [128, 48], "b_ada_g": [128, 2048], "n1_col": [128, 8], "n2_col": [128, 8],
    "w_qkv": [D, 3072], "w_gate": [D, 48], "w_ml": [D, 2048], "w_o": [D, 4096], "gate_p": [128, 48],
    "gdn_cw": [128, 24, 3], "ffn_cw": [128, 44, 9], "gnw_bc": [128, 1024], "mnw_bc": [128, 1024],
    "now_bc": [128, 1024], "w_bg": [D, D], "w_bm": [D, D], "w_out": [D, D], "w_up": [D, 5632], "w_down": [2816, D],
    "smask": [128, 14, 128],
}


class Ring:
    def __init__(self, B, stack, name, n, shape, dt, psum=False):
        self.slots = []
        for i in range(n):
            t = (B.ps if psum else B.sb)(stack, "%s%d" % (name, i), shape, dt)
            self.slots.append((t, Buf("%s%d" % (name, i))))
        self.i = 0

    def next(self):
        s = self.slots[self.i % len(self.slots)]
        self.i += 1
        return s


class Prog:
    def __init__(self, debug=None):
        self.debug = debug or {}
        self.B = Builder()
        self.nc = self.B.nc
        self.top = ExitStack()
        self.inp = {}
        for k, shp in IN_SHAPES.items():
            self.inp[k] = self.nc.dram_tensor(k, list(shp), F32, kind="ExternalInput").ap()
        self.out = self.nc.dram_tensor("out", [4096, D], F32, kind="ExternalOutput").ap()
        dk = "ExternalOutput" if self.debug.get("scratch_out") else "Internal"
        B = self.B
        self.KT = B.dram("KT", [NCH, 128, 8, 128], BF16, dk)
        self.QT = B.dram("QT", [NCH, 128, 8, 128], BF16, dk)
        self.VG = B.dram("VG", [NCH, 128, 1024], BF16, dk)
        self.MQT = B.dram("MQT", [NCH, 128, 4, 128], BF16, dk)
        self.MKT = B.dram("MKT", [NCH, 128, 4, 128], BF16, dk)
        self.MV = B.dram("MV", [NCH, 128, 1024], BF16, dk)
        self.GT = B.dram("GT", [NCH, 128, 48], F32, dk)
        self.OF = B.dram("OF", [33, 128, 1024], F32, dk)
        self.OB = B.dram("OB", [33, 128, 1024], F32, dk)
        self.HF = B.dram("HF", [33, 128, 1024], F32, dk)
        self.HB = B.dram("HB", [33, 128, 1024], F32, dk)
        self.X1 = B.dram("X1", [64 + OWN_T, D], F32, dk)
        self.consts()

    def consts(self):
        B, st = self.B, self.top
        self.ident_f = B.sb(st, "ident_f", [128, 128], F32)
        self.ident_b = B.sb(st, "ident_b", [128, 128], BF16)
        self.ones_f = B.sb(st, "ones_f", [128, 128], F32)
        self.ones_b = B.sb(st, "ones_b", [128, 128], BF16)
        self.nhalf = B.sb(st, "nhalf", [128, 512], F32)
        self.cb = Buf("consts")
        cb = self.cb
        B.op("pool", lambda e: e.memset(self.ones_f[:], 1.0), writes=[cb])
        B.op("pool", lambda e: e.memset(self.ones_b[:], 1.0), writes=[cb])
        B.op("pool", lambda e: e.memset(self.nhalf[:], -0.5), writes=[cb])
        B.op("pool", lambda e: e.memset(self.ident_f[:], 1.0), writes=[cb])
        B.op("pool", lambda e: e.affine_select(self.ident_f[:], self.ident_f[:], pattern=[[-1, 128]], compare_op=ALU.is_equal,
                                               fill=0.0, base=0, channel_multiplier=1), reads=[cb], writes=[cb])
        B.op("dve", lambda e: e.tensor_copy(out=self.ident_b[:], in_=self.ident_f[:]), reads=[cb], writes=[cb])
        self.modc = B.sb(st, "modc", [128, 6, 8], F32)
        self.bmod = Buf("modc")
        self.gate_bc = B.sb(st, "gate_bc", [128, 2, 1024], F32)
        self.bgate = Buf("gate_bc")

    def mask(self, stack, name, cmp_pat, dt=F32, val=1.0, fill=0.0):
        B = self.B
        base, cm, step, cmp = cmp_pat
        t = B.sb(stack, name, [128, 128], dt)
        tf = t
        if dt != F32:
            tf = B.sb(stack, name + "_f", [128, 128], F32)
        b = Buf(name)
        B.op("pool", lambda e: e.memset(tf[:], val), writes=[b])
        B.op("pool", lambda e: e.affine_select(tf[:], tf[:], pattern=[[step, 128]], compare_op=cmp, fill=fill,
                                               base=base, channel_multiplier=cm), reads=[b], writes=[b])
        if dt != F32:
            B.op("dve", lambda e: e.tensor_copy(out=t[:], in_=tf[:]), reads=[b], writes=[b])
        return t, b

    def phase0(self):
        B, nc, inp = self.B, self.nc, self.inp
        st = ExitStack()
        sc = B.sb(st, "p0_sc", [128, 16], F32)
        bsc = Buf("p0_sc")
        scb = B.sb(st, "p0_scb", [128, 8, 128], F32)
        bscb = Buf("p0_scb")
        bcol = B.sb(st, "p0_bcol", [128, 48], F32)
        n12 = B.sb(st, "p0_n12", [128, 16], F32)
        bg = B.sb(st, "p0_bg", [128, 2048], F32)
        bsm = Buf("p0_small")
        B.dma("sp", sc[:, 0:8], inp["c_col"][:, :], bsc, writes=[bsc])
        B.dma("sp", sc[:, 8:16], inp["cc_col"][:, :], bsc, writes=[bsc])
        B.dma("sp", bcol[:], inp["b_ada_col"][:, :], bsm, writes=[bsm])
        B.dma("sp", n12[:, 0:8], inp["n1_col"][:, :], bsm, writes=[bsm])
        B.dma("sp", n12[:, 8:16], inp["n2_col"][:, :], bsm, writes=[bsm])
        B.dma("sp", bg[:], inp["b_ada_g"][:, :], bsm, writes=[bsm])
        B.op("act", lambda e: e.activation(out=sc[:], in_=sc[:], func=AF.Silu), reads=[bsc], writes=[bsc])
        for k in range(8):
            B.op("dve", lambda e, k=k: e.tensor_scalar(out=scb[:, k, :], in0=self.ones_f[:], scalar1=sc[:, k:k + 1], scalar2=None,
                                                       op0=ALU.mult), reads=[bsc, self.cb], writes=[bscb])
        wring = Ring(B, st, "p0_w", 2, [128, 8, 512], F32)
        pcol = B.ps(st, "p0_pcol", [128, 64], F32)
        bpcol = Buf("p0_pcol")
        prow = Ring(B, st, "p0_prow", 2, [128, 512], F32, psum=True)
        wv = inp["w_ada"].rearrange("(k p) n -> p k n", p=128)
        xslot = {0: 0, 1: 1, 3: 2, 4: 3}
        for nb in range(12):
            v, half = nb // 2, nb % 2
            w, bw = wring.next()
            B.dma("sp", w[:], wv[:, :, nb * 512:(nb + 1) * 512], bw, writes=[bw])
            if v in (2, 5):
                p, bp = prow.next()
                for k in range(8):
                    B.op("pe", lambda e, k=k, p=p, w=w: e.matmul(p[:], lhsT=scb[:, k, :], rhs=w[:, k, :], start=(k == 0), stop=(k == 7)),
                         reads=[bscb, bw], writes=[bp], inc=(k == 7))
                gi = 0 if v == 2 else 1
                B.op("dve", lambda e, p=p, gi=gi, half=half: e.tensor_tensor(
                    out=self.gate_bc[:, gi, half * 512:(half + 1) * 512], in0=p[:], in1=bg[:, gi * 1024 + half * 512: gi * 1024 + (half + 1) * 512],
                    op=ALU.add), reads=[bp, bsm], writes=[self.bgate])
            else:
                for cc in range(4):
                    col = xslot[v] * 8 + half * 4 + cc
                    for k in range(8):
                        B.op("pe", lambda e, k=k, w=w, cc=cc, col=col: e.matmul(pcol[:, col:col + 1], lhsT=w[:, k, cc * 128:(cc + 1) * 128],
                                                                                 rhs=sc[:, k:k + 1], start=(k == 0), stop=(k == 7)),
                             reads=[bw, bsc], writes=[bpcol], inc=(k == 7))
                    if v in (0, 1):
                        col2 = 32 + v * 8 + half * 4 + cc
                        for k in range(8):
                            B.op("pe", lambda e, k=k, w=w, cc=cc, col2=col2: e.matmul(pcol[:, col2:col2 + 1], lhsT=w[:, k, cc * 128:(cc + 1) * 128],
                                                                                       rhs=sc[:, 8 + k:9 + k], start=(k == 0), stop=(k == 7)),
                                 reads=[bw, bsc], writes=[bpcol], inc=(k == 7))
        mc = B.sb(st, "p0_mc", [128, 6, 8], F32)
        bmc = Buf("p0_mc")
        for i, v in enumerate((0, 1, 3, 4)):
            B.op("dve", lambda e, i=i, v=v: e.tensor_tensor(out=mc[:, i, :], in0=pcol[:, i * 8:(i + 1) * 8], in1=bcol[:, v * 8:(v + 1) * 8], op=ALU.add),
                 reads=[bpcol, bsm], writes=[bmc])
        for i, v in enumerate((0, 1)):
            B.op("dve", lambda e, i=i, v=v: e.tensor_tensor(out=mc[:, 4 + i, :], in0=pcol[:, 32 + i * 8:32 + (i + 1) * 8], in1=bcol[:, v * 8:(v + 1) * 8],
                                                            op=ALU.add), reads=[bpcol, bsm], writes=[bmc])
        md = self.modc
        for dst, (sci, shi, nw) in {0: (1, 0, 0), 2: (5, 4, 0), 4: (3, 2, 1)}.items():
            B.op("dve", lambda e, dst=dst, sci=sci, nw=nw: e.scalar_tensor_tensor(out=md[:, dst, :], in0=mc[:, sci, :], scalar=1.0, in1=n12[:, nw * 8:(nw + 1) * 8],
                                                                                  op0=ALU.add, op1=ALU.mult), reads=[bmc, bsm], writes=[self.bmod])
            B.op("dve", lambda e, dst=dst, shi=shi: e.tensor_copy(out=md[:, dst + 1, :], in_=mc[:, shi, :]), reads=[bmc], writes=[self.bmod])
        B.barrier()
        st.close()

    @staticmethod
    def run_pipeline(makers, depth):
        active = []
        it = iter(makers)
        exhausted = False
        while True:
            for g in list(active):
                try:
                    next(g)
                except StopIteration:
                    active.remove(g)
            if not exhausted and len(active) < depth:
                try:
                    g = next(it)()
                    try:
                        next(g)
                        active.append(g)
                    except StopIteration:
                        pass
                except StopIteration:
                    exhausted = True
            if exhausted and not active:
                break

    def load_w_bf16(self, stack, name, ap, kchunks, ncols, nsplit=4):
        B = self.B
        t = B.sb(stack, name, [128, kchunks, ncols], BF16)
        b = Buf(name)
        v = ap.rearrange("(k p) n -> p k n", p=128)
        step = (ncols + nsplit - 1) // nsplit
        for i in range(0, ncols, step):
            j = min(ncols, i + step)
            B.dma("pool", t[:, :, i:j], v[:, :, i:j], b, writes=[b])
        return t, b

    def norm_transpose(self, xt, bxts, ns, xn, bxn, sq, bsq, junk, bjunk, ptr_ring, hxT, bhx, col0, ai, npart=128):
        B = self.B
        for s in range(ns):
            B.op("act", lambda e, s=s: e.activation(out=xn[0:npart, s, :], in_=xt[0:npart, s, :], func=AF.Square, accum_out=sq[0:npart, s:s + 1]),
                 reads=[bxts[s]], writes=[bxn, bsq])
        B.op("dve", lambda e: e.tensor_scalar(out=sq[0:npart, 8:8 + ns], in0=sq[0:npart, 0:ns], scalar1=float(D * EPS), scalar2=None, op0=ALU.add),
             reads=[bsq], writes=[bsq])
        B.op("pool", lambda e: e.tensor_tensor(out=sq[0:npart, 16:16 + ns], in0=sq[0:npart, 8:8 + ns], in1=self.nhalf[0:npart, 0:ns], op=ALU.pow),
             reads=[bsq, self.cb], writes=[bsq])
        for s in range(ns):
            B.op("dve", lambda e, s=s: e.tensor_scalar(out=xn[0:npart, s, :], in0=xt[0:npart, s, :], scalar1=sq[0:npart, 16 + s:17 + s], scalar2=32.0,
                                                       op0=ALU.mult, op1=ALU.mult), reads=[bxts[s], bsq], writes=[bxn])
        for k in range(KD):
            p, bp = ptr_ring.next()
            pb = p[:].bitcast(BF16)
            for s in range(ns):
                B.op("pe", lambda e, s=s, k=k, pb=pb: e.transpose(pb[:, s * npart:(s + 1) * npart], xn[0:npart, s, k * 128:(k + 1) * 128],
                                                                  self.ident_b[0:npart, 0:npart]),
                     reads=[bxn, self.cb], writes=[bp], inc=(s == ns - 1))
            B.op("act", lambda e, k=k, pb=pb: e.activation(out=hxT[:, k, col0:col0 + ns * npart], in_=pb[:, 0:ns * npart], func=AF.Identity,
                                                           scale=self.modc[:, ai, k:k + 1], bias=self.modc[:, ai + 1, k:k + 1]),
                 reads=[bp, self.bmod], writes=[bhx])

    def phaseA(self):
        B, nc, inp = self.B, self.nc, self.inp
        st = ExitStack()
        wqkv, bwqkv = self.load_w_bf16(st, "a_wqkv", inp["w_qkv"], 8, 3072, 6)
        wml, bwml = self.load_w_bf16(st, "a_wml", inp["w_ml"], 8, 2048, 4)
        wgt, bwgt = self.load_w_bf16(st, "a_wgt", inp["w_gate"], 8, 48, 1)
        cw = B.sb(st, "a_cw", [128, 24, 3], F32)
        gp = B.sb(st, "a_gp", [128, 48], F32)
        bsm = Buf("a_small")
        B.dma("sp", cw[:], inp["gdn_cw"][:, :, :], bsm, writes=[bsm])
        B.dma("sp", gp[:], inp["gate_p"][:, :], bsm, writes=[bsm])
        B.op("act", lambda e: e.activation(out=gp[:, 16:32], in_=gp[:, 16:32], func=AF.Exp), reads=[bsm], writes=[bsm])
        B.op("dve", lambda e: e.tensor_scalar(out=gp[:, 16:32], in0=gp[:, 16:32], scalar1=-1.0, scalar2=None, op0=ALU.mult), reads=[bsm], writes=[bsm])
        xt = B.sb(st, "a_x", [128, 4, 1024], F32)
        bxts = [Buf("a_x%d" % i) for i in range(4)]
        xh = B.sb(st, "a_xh", [2, 1, 1024], F32); bxh = Buf("a_xh")
        xn = B.sb(st, "a_xn", [128, 4, 1024], BF16); bxn = Buf("a_xn")
        xnh = B.sb(st, "a_xnh", [2, 1, 1024], BF16); bxnh = Buf("a_xnh")
        sq = B.sb(st, "a_sq", [128, 24], F32); bsq = Buf("a_sq")
        sqh = B.sb(st, "a_sqh", [128, 24], F32); bsqh = Buf("a_sqh")
        junk = None; bjunk = None
        hxT = B.sb(st, "a_hxT", [128, 8, 514], BF16); bhx = Buf("a_hxT")
        hxh = B.sb(st, "a_hxh", [128, 8, 2], BF16); bhxh = Buf("a_hxh")
        ptr = Ring(B, st, "a_ptr", 2, [128, 512], F32, psum=True)
        pz = Ring(B, st, "a_pz", 4, [128, 512], F32, psum=True)
        pmisc = B.ps(st, "a_pmisc", [128, 512], F32)
        pzh = pmisc[:, 0:64]; bpzh = Buf("a_pzh")
        pn = ptr
        zb = Ring(B, st, "a_zb", 4, [128, 514], F32)
        y1 = Ring(B, st, "a_y1", 4, [128, 512], F32)
        sqb = Ring(B, st, "a_sqb", 2, [128, 512], BF16)
        skeep = B.sb(st, "a_skeep", [128, 8, 512], BF16)
        bskeep = [Buf("a_skeep%d" % i) for i in range(8)]
        rnr = B.sb(st, "a_rnr", [8, 512], F32); brnr = Buf("a_rnr")
        ind = B.sb(st, "a_ind", [128, 8, 8], BF16); bind = Buf("a_ind")
        selr = B.sb(st, "a_selr", [8, 8, 128], F32); bselr = Buf("a_selr")
        B.op("pool", lambda e: e.memset(ind[:], 0.0), writes=[bind])
        for jj in range(8):
            B.op("pool", lambda e, jj=jj: e.memset(ind[:, jj, jj:jj + 1], 1.0), writes=[bind])
            B.op("dve", lambda e, jj=jj: e.tensor_copy(out=selr[:, jj, :], in_=self.ident_f[0:8, jj:jj + 1].to_broadcast([8, 128])), reads=[self.cb], writes=[bselr])
        pss = B.ps(st, "a_pss", [128, 512], F32); bpss = Buf("a_pss")
        kst = B.sb(st, "a_kst", [128, 4, 8, 128], BF16); bkst = Buf("a_kst")
        qst = B.sb(st, "a_qst", [128, 4, 8, 128], BF16); bqst = Buf("a_qst")
        vT = B.sb(st, "a_vT", [128, 8, 512], BF16); bvT = Buf("a_vT")
        vst = Ring(B, st, "a_vst", 1, [128, 4, 1024], BF16)
        mqst = B.sb(st, "a_mqst", [128, 4, 4, 128], BF16); bmqst = Buf("a_mqst")
        mkst = B.sb(st, "a_mkst", [128, 4, 4, 128], BF16); bmkst = Buf("a_mkst")
        graw = B.sb(st, "a_graw", [128, 4, 48], F32); bgraw = Buf("a_graw")
        gwk = B.sb(st, "a_gwk", [128, 4, 48], F32); bgwk = Buf("a_gwk")
        gsb = Ring(B, st, "a_gsb", 2, [128, 4, 48], F32)
        pg = pmisc[:, 64:256].rearrange("p (s g) -> p s g", g=48); bpg = Buf("a_pg")
        dkr = float(128 ** -0.5)

        tiles = [(inp["ctx"], 0, 2, 0, False, False)]
        for i in range(16):
            tiles.append((inp["x"], i * 512, 4, 2 + 4 * i, i > 0, i < 15))
        if self.debug.get("a_tiles"):
            tiles = tiles[: self.debug["a_tiles"]]
        for (src, t0, ns, c0, hl, hr) in tiles:
            n = ns * 128
            ai = 2 if src is inp["ctx"] else 0
            for s in range(ns):
                B.dma("sp", xt[:, s, :], src[t0 + s * 128:t0 + (s + 1) * 128, :], bxts[s], writes=[bxts[s]])
            tl = t0 - 1 if hl else t0
            tr = t0 + n if hr else t0
            B.dma("sp", xh[0:1, 0, :], src[tl:tl + 1, :], bxh, writes=[bxh])
            B.dma("sp", xh[1:2, 0, :], src[tr:tr + 1, :], bxh, writes=[bxh])
            self.norm_transpose(xt, bxts, ns, xn, bxn, sq, bsq, junk, bjunk, ptr, hxT, bhx, 1, ai)
            self.norm_transpose(xh, [bxh], 1, xnh, bxnh, sqh, bsqh, junk, bjunk, ptr, hxh, bhxh, 0, ai, npart=2)
            def chunk_gen(j, kind, jj):
                p, bp = pz.next()
                z, bz = zb.next()
                a1, ba1 = y1.next()
                for k in range(8):
                    B.op("pe", lambda e, k=k: e.matmul(p[:, 0:n], lhsT=wqkv[:, k, j * 128:(j + 1) * 128], rhs=hxT[:, k, 1:1 + n],
                                                         start=(k == 0), stop=(k == 7)), reads=[bwqkv, bhx], writes=[bp], inc=(k == 7))
                for k in range(8):
                    B.op("pe", lambda e, k=k: e.matmul(pzh[:, 2 * j:2 * j + 2], lhsT=wqkv[:, k, j * 128:(j + 1) * 128], rhs=hxh[:, k, :],
                                                         start=(k == 0), stop=(k == 7)), reads=[bwqkv, bhxh], writes=[bpzh], inc=(k == 7))
                yield
                B.op("act", lambda e: e.activation(out=z[:, 1:1 + n], in_=p[:, 0:n], func=AF.Identity), reads=[bp], writes=[bz])
                B.op("act", lambda e: e.activation(out=z[:, 0:1], in_=pzh[:, 2 * j:2 * j + 1], func=AF.Identity), reads=[bpzh], writes=[bz])
                B.op("act", lambda e: e.activation(out=z[:, n + 1:n + 2], in_=pzh[:, 2 * j + 1:2 * j + 2], func=AF.Identity), reads=[bpzh], writes=[bz])
                if not hl:
                    B.op("pool", lambda e: e.memset(z[:, 0:1], 0.0), writes=[bz])
                if not hr:
                    B.op("pool", lambda e: e.memset(z[:, n + 1:n + 2], 0.0), writes=[bz])
                yield
                B.op("dve", lambda e: e.tensor_scalar(out=a1[:, 0:n], in0=z[:, 1:1 + n], scalar1=cw[:, j, 1:2], scalar2=None, op0=ALU.mult),
                     reads=[bz, bsm], writes=[ba1])
                B.op("dve", lambda e: e.scalar_tensor_tensor(out=a1[:, 0:n], in0=z[:, 0:n], scalar=cw[:, j, 0:1], in1=a1[:, 0:n],
                                                            op0=ALU.mult, op1=ALU.add), reads=[bz, bsm, ba1], writes=[ba1])
                B.op("dve", lambda e: e.scalar_tensor_tensor(out=a1[:, 0:n], in0=z[:, 2:2 + n], scalar=cw[:, j, 2:3], in1=a1[:, 0:n],
                                                            op0=ALU.mult, op1=ALU.add), reads=[bz, bsm, ba1], writes=[ba1])
                yield
                if kind == "v":
                    B.op("act", lambda e: e.activation(out=vT[:, jj, 0:n], in_=a1[:, 0:n], func=AF.Silu), reads=[ba1], writes=[bvT])
                else:
                    B.op("act", lambda e: e.activation(out=skeep[:, jj, 0:n], in_=a1[:, 0:n], func=AF.Silu), reads=[ba1], writes=[bskeep[jj]])
                    q2, bq2 = sqb.next()
                    B.op("pool", lambda e: e.tensor_tensor(out=q2[:, 0:n], in0=skeep[:, jj, 0:n], in1=skeep[:, jj, 0:n], op=ALU.mult),
                         reads=[bskeep[jj]], writes=[bq2])
                    B.op("pe", lambda e: e.matmul(pss[0:8, 0:n], lhsT=ind[:, jj, :], rhs=q2[:, 0:n], start=(jj == 0), stop=(jj == 7)),
                         reads=[bq2, bind], writes=[bpss])

            for half in range(2):
                self.run_pipeline([(lambda jj=jj: chunk_gen(half * 8 + jj, "qk", jj)) for jj in range(8)], 4)
                B.op("act", lambda e: e.activation(out=rnr[:, 0:n], in_=pss[0:8, 0:n], func=AF.Ln, bias=float(EPS)), reads=[bpss], writes=[brnr])
                B.op("act", lambda e: e.activation(out=rnr[:, 0:n], in_=rnr[:, 0:n], func=AF.Exp, scale=-0.5), reads=[brnr], writes=[brnr])
                for jj in range(8):
                    pp, bpp = pn.next()
                    B.op("pe", lambda e, pp=pp, jj=jj: e.matmul(pp[:, 0:n], lhsT=selr[:, jj, :], rhs=rnr[:, 0:n], start=True, stop=True),
                         reads=[brnr, bselr], writes=[bpp])
                    if half == 0:
                        B.op("dve", lambda e, pp=pp, jj=jj: e.scalar_tensor_tensor(
                            out=qst[:, 0:ns, jj, :], in0=skeep[:, jj, 0:n].rearrange("p (s t) -> p s t", t=128), scalar=dkr,
                            in1=pp[:, 0:n].rearrange("p (s t) -> p s t", t=128), op0=ALU.mult, op1=ALU.mult), reads=[bskeep[jj], bpp], writes=[bqst])
                    else:
                        B.op("dve", lambda e, pp=pp, jj=jj: e.tensor_tensor(
                            out=kst[:, 0:ns, jj, :], in0=skeep[:, jj, 0:n].rearrange("p (s t) -> p s t", t=128),
                            in1=pp[:, 0:n].rearrange("p (s t) -> p s t", t=128), op=ALU.mult), reads=[bskeep[jj], bpp], writes=[bkst])
            for s in range(ns):
                for k in range(8):
                    B.op("pe", lambda e, k=k, s=s: e.matmul(pg[:, s, :], lhsT=hxT[:, k, 1 + s * 128:1 + (s + 1) * 128], rhs=wgt[:, k, :],
                                                             start=(k == 0), stop=(k == 7)), reads=[bhx, bwgt], writes=[bpg], inc=(k == 7))
            g, bg_ = gsb.next()
            self.gate_math(pg, bpg, graw, bgraw, gwk, bgwk, g, bg_, gp, bsm, ns)
            B.dma("sp", self.GT[c0:c0 + ns].rearrange("c t g -> t c g"), g[:, 0:ns, :], bg_, reads=[bg_])
            self.run_pipeline([(lambda jj=jj: chunk_gen(16 + jj, "v", jj)) for jj in range(8)], 4)
            self.v_transposes(vT, bvT, ns, vst, pn, self.VG, c0)
            B.dma("sp", self.KT[c0:c0 + ns].rearrange("c d h t -> d c h t"), kst[:, 0:ns], bkst, reads=[bkst])
            B.dma("sp", self.QT[c0:c0 + ns].rearrange("c d h t -> d c h t"), qst[:, 0:ns], bqst, reads=[bqst])
            for j in range(16):
                p, bp = pz.next()
                for k in range(8):
                    B.op("pe", lambda e, k=k, p=p, j=j: e.matmul(p[:, 0:n], lhsT=wml[:, k, j * 128:(j + 1) * 128], rhs=hxT[:, k, 1:1 + n],
                                                                  start=(k == 0), stop=(k == 7)), reads=[bwml, bhx], writes=[bp], inc=(k == 7))
                if j < 4:
                    B.op("act", lambda e, p=p, j=j: e.activation(out=mqst[:, 0:ns, j, :], in_=p[:, 0:n].rearrange("p (s t) -> p s t", t=128),
                                                                  func=AF.Identity, scale=dkr), reads=[bp], writes=[bmqst])
                elif j < 8:
                    B.op("act", lambda e, p=p, j=j: e.activation(out=mkst[:, 0:ns, j - 4, :], in_=p[:, 0:n].rearrange("p (s t) -> p s t", t=128),
                                                                  func=AF.Identity), reads=[bp], writes=[bmkst])
                else:
                    B.op("act", lambda e, p=p, j=j: e.activation(out=vT[:, j - 8, 0:n], in_=p[:, 0:n], func=AF.Identity), reads=[bp], writes=[bvT])
            self.v_transposes(vT, bvT, ns, vst, pn, self.MV, c0)
            B.dma("sp", self.MQT[c0:c0 + ns].rearrange("c d h t -> d c h t"), mqst[:, 0:ns], bmqst, reads=[bmqst])
            B.dma("sp", self.MKT[c0:c0 + ns].rearrange("c d h t -> d c h t"), mkst[:, 0:ns], bmkst, reads=[bmkst])
        B.barrier()
        st.close()

    def v_transposes(self, vT, bvT, ns, vst, pn, dst, c0):
        B = self.B
        v, bv = vst.next()
        for s in range(ns):
            pp, bpp = pn.next()
            ppb = pp[:].bitcast(BF16)
            for h in range(8):
                B.op("pe", lambda e, s=s, h=h, ppb=ppb: e.transpose(ppb[:, h * 128:(h + 1) * 128], vT[:, h, s * 128:(s + 1) * 128], self.ident_b[:]),
                     reads=[bvT, self.cb], writes=[bpp], inc=(h == 7))
            B.op("act", lambda e, s=s, ppb=ppb, v=v: e.activation(out=v[:, s, :], in_=ppb[:, 0:1024], func=AF.Identity), reads=[bpp], writes=[bv])
        B.dma("sp", dst[c0:c0 + ns].rearrange("c t e -> t c e"), v[:, 0:ns, :], bv, reads=[bv])

    def gate_math(self, pg, bpg, graw, bgraw, wk, bwk, g, bg_, gp, bgp, ns):
        B = self.B
        S = slice(0, ns)

        def bc(lo, hi):
            return gp[:, lo:hi].unsqueeze(1).to_broadcast([128, ns, hi - lo])

        B.op("act", lambda e: e.activation(out=graw[:, S, :], in_=pg[:, S, :], func=AF.Identity), reads=[bpg], writes=[bgraw])
        B.op("dve", lambda e: e.tensor_tensor(out=wk[:, S, 0:16], in0=graw[:, S, 0:16], in1=bc(0, 16), op=ALU.add), reads=[bgraw, bgp], writes=[bwk])
        B.op("act", lambda e: e.activation(out=wk[:, S, 0:16], in_=wk[:, S, 0:16], func=AF.Exp), reads=[bwk], writes=[bwk])
        B.op("act", lambda e: e.activation(out=wk[:, S, 0:16], in_=wk[:, S, 0:16], func=AF.Ln, bias=1.0), reads=[bwk], writes=[bwk])
        B.op("dve", lambda e: e.tensor_tensor(out=g[:, S, 0:16], in0=wk[:, S, 0:16], in1=bc(16, 32), op=ALU.mult), reads=[bwk, bgp], writes=[bg_])
        B.op("act", lambda e: e.activation(out=wk[:, S, 16:32], in_=graw[:, S, 16:32], func=AF.Exp, scale=-1.0), reads=[bgraw], writes=[bwk])
        B.op("dve", lambda e: e.tensor_scalar(out=wk[:, S, 16:32], in0=wk[:, S, 16:32], scalar1=1.0, scalar2=None, op0=ALU.add), reads=[bwk], writes=[bwk])
        B.op("dve", lambda e: e.reciprocal(out=g[:, S, 16:32], in_=wk[:, S, 16:32]), reads=[bwk], writes=[bg_])
        B.op("dve", lambda e: e.tensor_tensor(out=wk[:, S, 32:48], in0=graw[:, S, 32:48], in1=bc(32, 48), op=ALU.add), reads=[bgraw, bgp], writes=[bwk])
        B.op("act", lambda e: e.activation(out=wk[:, S, 32:48], in_=wk[:, S, 32:48], func=AF.Exp, scale=float(2.0 / 15.0)), reads=[bwk], writes=[bwk])
        B.op("dve", lambda e: e.tensor_scalar(out=wk[:, S, 32:48], in0=wk[:, S, 32:48], scalar1=1.0, scalar2=None, op0=ALU.add), reads=[bwk], writes=[bwk])
        B.op("dve", lambda e: e.reciprocal(out=wk[:, S, 32:48], in_=wk[:, S, 32:48]), reads=[bwk], writes=[bwk])
        B.op("dve", lambda e: e.tensor_scalar(out=g[:, S, 32:48], in0=wk[:, S, 32:48], scalar1=-30.0, scalar2=15.0, op0=ALU.mult, op1=ALU.add),
             reads=[bwk], writes=[bg_])
        B.op("act", lambda e: e.activation(out=wk[:, S, 40:48], in_=g[:, S, 40:48], func=AF.Exp, scale=-1.0), reads=[bg_], writes=[bwk])
        B.op("act", lambda e: e.activation(out=wk[:, S, 40:48], in_=wk[:, S, 40:48], func=AF.Ln, bias=1.0), reads=[bwk], writes=[bwk])
        B.op("dve", lambda e: e.tensor_scalar(out=g[:, S, 40:48], in0=wk[:, S, 40:48], scalar1=-1.0, scalar2=None, op0=ALU.mult), reads=[bwk], writes=[bg_])

    def phaseB(self):
        B, nc, inp = self.B, self.nc, self.inp
        st = ExitStack()
        dbg = self.debug
        LE, bLE = self.mask(st, "b_LE", (0, -1, 1, ALU.is_ge))
        LT, bLT = self.mask(st, "b_LT", (-1, -1, 1, ALU.is_ge))
        GE, bGE = self.mask(st, "b_GE", (0, 1, -1, ALU.is_ge))
        GT_, bGT = self.mask(st, "b_GT", (-1, 1, -1, ALU.is_ge))
        MBf, bMBf = self.mask(st, "b_MBf", (0, 1, -1, ALU.is_ge), val=0.0, fill=NEG)
        MBb, bMBb = self.mask(st, "b_MBb", (0, -1, 1, ALU.is_ge), val=0.0, fill=NEG)
        SELf, bSELf = self.mask(st, "b_SELf", (-127, 1, 0, ALU.is_equal))
        SELb, bSELb = self.mask(st, "b_SELb", (0, 1, 0, ALU.is_equal))
        smask = B.sb(st, "b_smask", [128, 14, 128], BF16)
        bsm = Buf("b_smask")
        B.dma("pool", smask[:], inp["smask"][:, :, :], bsm, writes=[bsm])
        cbufs = [bLE, bLT, bGE, bGT, bMBf, bMBb, bSELf, bSELb, bsm, self.cb]
        dirc = [dict(U=LE, S=GT_, incl=LE, strict=LT, MB=MBf, SEL=SELf),
                dict(U=GE, S=LT, incl=GE, strict=GT_, MB=MBb, SEL=SELb)]
        S = [B.sb(st, "b_S%d" % d, [128, 8, 128], F32) for d in range(2)]
        bS = [[Buf("b_S%d_%d" % (d, g)) for g in range(2)] for d in range(2)]
        Sb = [[Ring(B, st, "b_Sb%d_%d_" % (d, g), 2, [128, 4, 128], BF16) for g in range(2)] for d in range(2)]
        Sb_cur = [[None, None], [None, None]]
        C = [B.sb(st, "b_C%d" % d, [128, 4, 256], F32) for d in range(2)]
        bC = [Buf("b_C%d" % d) for d in range(2)]
        Cb = [Ring(B, st, "b_Cb%d_" % d, 2, [128, 4, 256], BF16) for d in range(2)]
        Cb_cur = [None, None]
        nst = [Ring(B, st, "b_n%d_" % d, 2, [128, 8], F32) for d in range(2)]
        nbf = [Ring(B, st, "b_nb%d_" % d, 2, [128, 4], BF16) for d in range(2)]
        n_cur = [None, None]
        nb_cur = [None, None]
        mst = [Ring(B, st, "b_m%d_" % d, 2, [128, 4], F32) for d in range(2)]
        m_cur = [None, None]
        for d in range(2):
            B.op("pool", lambda e, d=d: e.memset(S[d][:], 0.0), writes=bS[d])
            B.op("pool", lambda e, d=d: e.memset(C[d][:], 0.0), writes=[bC[d]])
            for g in range(2):
                t, b = Sb[d][g].next()
                B.op("pool", lambda e, t=t: e.memset(t[:], 0.0), writes=[b])
                Sb_cur[d][g] = (t, b)
            t, b = Cb[d].next()
            B.op("pool", lambda e, t=t: e.memset(t[:], 0.0), writes=[b])
            Cb_cur[d] = (t, b)
            t, b = nst[d].next()
            B.op("pool", lambda e, t=t: e.memset(t[:], 0.0), writes=[b])
            n_cur[d] = (t, b)
            t, b = nbf[d].next()
            B.op("pool", lambda e, t=t: e.memset(t[:], 0.0), writes=[b])
            nb_cur[d] = (t, b)
            t, b = mst[d].next()
            B.op("pool", lambda e, t=t: e.memset(t[:], 0.0), writes=[b])
            m_cur[d] = (t, b)
        def dring(name, shape, dt):
            return [Ring(B, st, "b_%s%d_" % (name, d), 2, shape, dt) for d in range(2)]
        rKT = dring("KT", [128, 8, 128], BF16)
        rQT = dring("QT", [128, 8, 128], BF16)
        rVG = dring("VG", [128, 1024], BF16)
        rGT = dring("GT", [128, 48], F32)
        rMQ = dring("MQ", [128, 4, 128], BF16)
        rMK = dring("MK", [128, 4, 128], BF16)
        rMV = dring("MV", [128, 1024], BF16)
        rgs = dring("gs", [128, 64], F32)
        psr = Ring(B, st, "b_ps", 8, [128, 512], F32, psum=True)
        NG, NM = 4, 2
        gslots = []
        for i in range(NG):
            sl = {}
            for nm, shp, dt in (("A", [128, 4, 128], F32), ("Bt", [128, 4, 128], F32), ("Ct", [128, 4, 128], F32),
                                ("attnT", [128, 4, 128], BF16), ("Qp", [128, 4, 128], BF16),
                                ("Kg", [128, 4, 128], BF16), ("kt", [128, 4, 128], BF16), ("G0", [128, 4, 128], BF16),
                                ("G1", [128, 4, 128], BF16), ("H0", [128, 4, 128], BF16), ("H1", [128, 4, 128], BF16),
                                ("IYT", [128, 4, 128], BF16), ("negW", [128, 4, 128], BF16), ("vnew", [128, 4, 128], BF16)):
                sl[nm] = (B.sb(st, "b_g%d_%s" % (i, nm), shp, dt), Buf("b_g%d_%s" % (i, nm)))
            gslots.append(sl)
        mslots = []
        for i in range(NM):
            sl = {}
            for nm, shp, dt in (("X", [128, 4, 128], F32), ("Y", [128, 4, 128], F32), ("Pm", [128, 4, 128], BF16),
                                ("PT", [128, 4, 128], BF16), ("Kw", [128, 4, 128], BF16), ("sm", [128, 64], F32), ("Ct", [128, 4, 256], F32)):
                sl[nm] = (B.sb(st, "b_m%d_%s" % (i, nm), shp, dt), Buf("b_m%d_%s" % (i, nm)))
            mslots.append(sl)
        ring_o1 = Ring(B, st, "b_o1_", 2, [128, 4, 128], F32)
        ring_o = Ring(B, st, "b_o_", 2, [128, 4, 128], F32)
        ring_num = Ring(B, st, "b_num_", 1, [128, 4, 256], F32)
        ring_h = Ring(B, st, "b_h_", 1, [128, 4, 256], F32)

        def bc3(ap2, n):
            return ap2.unsqueeze(2).to_broadcast([128, 4, n])

        def bcm(ap2, n=4):
            return ap2.unsqueeze(1).to_broadcast([128, n, 128])

        nsteps = dbg.get("b_steps", NCH)
        order = [list(range(NCH)), [1, 0] + list(range(NCH - 1, 1, -1))]
        if dbg.get("b_order"):
            order = dbg["b_order"]
            nsteps = len(order[0])
        out_lo, out_hi = 2, 2 + 33

        data = {}

        def load_step(step, d):
            c = order[d][step]
            tk, bk = rKT[d].next(); tq, bq = rQT[d].next(); tv, bv = rVG[d].next(); tg, bg = rGT[d].next()
            tmq, bmq = rMQ[d].next(); tmk, bmk = rMK[d].next(); tmv, bmv = rMV[d].next()
            B.dma("sp", tg[:], self.GT[c], bg, writes=[bg])
            B.dma("sp", tk[:], self.KT[c], bk, writes=[bk])
            B.dma("sp", tq[:], self.QT[c], bq, writes=[bq])
            B.dma("sp", tv[:], self.VG[c], bv, writes=[bv])
            B.dma("sp", tmq[:], self.MQT[c], bmq, writes=[bmq])
            B.dma("sp", tmk[:], self.MKT[c], bmk, writes=[bmk])
            B.dma("sp", tmv[:], self.MV[c], bmv, writes=[bmv])
            data[(step, d)] = dict(c=c, KT=(tk, bk), QT=(tq, bq), VG=(tv, bv), GT=(tg, bg), MQ=(tmq, bmq), MK=(tmk, bmk), MV=(tmv, bmv))

        def shared_pre(step, d):
            dd = data[(step, d)]
            tg, bg = dd["GT"]
            gs, bgs = rgs[d].next()
            dc = dirc[d]
            p, bp = psr.next()
            B.op("pe", lambda e: e.matmul(p[:, 0:8], lhsT=dc["U"][:], rhs=tg[:, d * 8:(d + 1) * 8], start=True, stop=True), reads=[bg] + cbufs, writes=[bp], inc=False)
            B.op("pe", lambda e: e.matmul(p[:, 8:12], lhsT=dc["U"][:], rhs=tg[:, 40 + d * 4:44 + d * 4], start=True, stop=True), reads=[bg] + cbufs, writes=[bp], inc=False)
            B.op("pe", lambda e: e.matmul(p[:, 12:20], lhsT=self.ones_f[:], rhs=tg[:, d * 8:(d + 1) * 8], start=True, stop=True), reads=[bg] + cbufs, writes=[bp], inc=False)
            B.op("pe", lambda e: e.matmul(p[:, 20:24], lhsT=self.ones_f[:], rhs=tg[:, 40 + d * 4:44 + d * 4], start=True, stop=True), reads=[bg] + cbufs, writes=[bp])
            B.op("act", lambda e: e.activation(out=gs[:, 0:24], in_=p[:, 0:24], func=AF.Identity), reads=[bp], writes=[bgs])
            B.op("act", lambda e: e.activation(out=gs[:, 24:32], in_=gs[:, 0:8], func=AF.Exp), reads=[bgs], writes=[bgs])
            B.op("dve", lambda e: e.tensor_tensor(out=gs[:, 32:40], in0=gs[:, 12:20], in1=gs[:, 0:8], op=ALU.subtract), reads=[bgs], writes=[bgs])
            B.op("act", lambda e: e.activation(out=gs[:, 32:40], in_=gs[:, 32:40], func=AF.Exp), reads=[bgs], writes=[bgs])
            B.op("act", lambda e: e.activation(out=gs[:, 40:48], in_=gs[:, 12:20], func=AF.Exp), reads=[bgs], writes=[bgs])
            B.op("dve", lambda e: e.tensor_tensor(out=gs[:, 48:52], in0=tg[:, 32 + d * 4:36 + d * 4], in1=gs[:, 8:12], op=ALU.subtract), reads=[bgs, bg], writes=[bgs])
            dd["gs"] = (gs, bgs)

        def gdn_group(step, d, hg, sl):
            dd = data[(step, d)]
            dc = dirc[d]
            c = dd["c"]
            need_o = out_lo <= c < out_hi
            tk, bk = dd["KT"]; tq, bq = dd["QT"]; tv, bv = dd["VG"]; tg, bg = dd["GT"]; gs, bgs = dd["gs"]
            h0 = hg * 4
            A, bA = sl["A"]; Bt, bBt = sl["Bt"]; Ct, bCt = sl["Ct"]
            attnT, battn = sl["attnT"]; Qp, bQp = sl["Qp"]; Kg, bKg = sl["Kg"]; kt, bkt = sl["kt"]
            IYT, bIYT = sl["IYT"]; negW, bnegW = sl["negW"]; vnew, bvnew = sl["vnew"]
            Qm, bQm = sl["IYT"]
            St, bSt = sl["A"]
            gcol = tg[:, d * 8 + h0:d * 8 + h0 + 4]
            bcol = tg[:, 16 + d * 8 + h0:16 + d * 8 + h0 + 4]
            eg = gs[:, 24 + h0:24 + h0 + 4]
            ekt = gs[:, 32 + h0:32 + h0 + 4]
            gte = gs[:, 40 + h0:40 + h0 + 4]
            B.op("pool", lambda e: e.tensor_tensor(out=A[:], in0=bcm(dc["U"][:]), in1=bc3(gcol, 128), op=ALU.mult), reads=[bg] + cbufs, writes=[bA])
            pD, bpD = psr.next()
            for u in range(4):
                B.op("pe", lambda e, u=u: e.matmul(pD[:, u * 128:(u + 1) * 128], lhsT=dc["S"][:], rhs=A[:, u, :], start=True, stop=True),
                     reads=[bA] + cbufs, writes=[bpD], inc=(u == 3))
            B.op("act", lambda e: e.activation(out=Bt[:].rearrange("p u l -> p (u l)"), in_=pD[:, :], func=AF.Exp), reads=[bpD], writes=[bBt])
            B.op("pool", lambda e: e.tensor_tensor(out=A[:], in0=Bt[:], in1=bcm(dc["incl"][:]), op=ALU.mult), reads=[bBt] + cbufs, writes=[bA])
            B.op("pool", lambda e: e.tensor_tensor(out=Ct[:], in0=Bt[:], in1=bcm(dc["strict"][:]), op=ALU.mult), reads=[bBt] + cbufs, writes=[bCt])
            B.op("pool", lambda e: e.tensor_tensor(out=Ct[:], in0=Ct[:], in1=bc3(bcol, 128), op=ALU.mult), reads=[bCt, bg], writes=[bCt])
            pKK, bpKK = psr.next()
            pQK, bpQK = psr.next()
            pKt, bpKt = psr.next()
            pKtb = pKt[:].bitcast(BF16)
            for u in range(4):
                B.op("pe", lambda e, u=u: e.matmul(pKK[:, u * 128:(u + 1) * 128], lhsT=tk[:, h0 + u, :], rhs=tk[:, h0 + u, :], start=True, stop=True),
                     reads=[bk], writes=[bpKK], inc=(u == 3))
            for u in range(4):
                B.op("pe", lambda e, u=u: e.matmul(pQK[:, u * 128:(u + 1) * 128], lhsT=tk[:, h0 + u, :], rhs=tq[:, h0 + u, :], start=True, stop=True),
                     reads=[bk, bq], writes=[bpQK], inc=(u == 3))
            for u in range(4):
                B.op("pe", lambda e, u=u: e.transpose(pKtb[:, u * 128:(u + 1) * 128], tk[:, h0 + u, :], self.ident_b[:]),
                     reads=[bk] + cbufs, writes=[bpKt], inc=(u == 3))
            B.op("dve", lambda e: e.tensor_tensor(out=attnT[:], in0=pQK[:, :].rearrange("p (u l) -> p u l", u=4), in1=A[:], op=ALU.mult),
                 reads=[bpQK, bA], writes=[battn])
            B.op("dve", lambda e: e.tensor_tensor(out=Qm[:], in0=pKK[:, :].rearrange("p (u l) -> p u l", u=4), in1=Ct[:], op=ALU.mult),
                 reads=[bpKK, bCt], writes=[bQm])
            B.op("pool", lambda e: e.tensor_tensor(out=Qp[:], in0=Qm[:], in1=bcm(self.ident_b[:]), op=ALU.add), reads=[bQm] + cbufs, writes=[bQp])
            B.op("dve", lambda e: e.tensor_tensor(out=Kg[:], in0=pKtb[:, 0:512].rearrange("p (u l) -> p u l", u=4), in1=bc3(eg, 128), op=ALU.mult),
                 reads=[bpKt, bgs], writes=[bKg])
            B.op("dve", lambda e: e.tensor_tensor(out=kt[:], in0=pKtb[:, 0:512].rearrange("p (u l) -> p u l", u=4), in1=bc3(ekt, 128), op=ALU.mult),
                 reads=[bpKt, bgs], writes=[bkt])
            yield
            Gc = None
            Hc = None
            for lev in range(7):
                sm = smask[:, d * 7 + lev, :]
                pY, bpY = psr.next()
                for u in range(4):
                    rhsH = self.ident_b[:] if Hc is None else Hc[0][:, u, :]
                    B.op("pe", lambda e, u=u, rhsH=rhsH: e.matmul(pY[:, u * 128:(u + 1) * 128], lhsT=Qp[:, u, :], rhs=rhsH, start=True, stop=True),
                         reads=[bQp] + cbufs + ([] if Hc is None else [Hc[1]]), writes=[bpY], inc=(u == 3))
                B.op("dve", lambda e, sm=sm: e.tensor_tensor(out=IYT[:], in0=pY[:, :].rearrange("p (u l) -> p u l", u=4), in1=bcm(sm), op=ALU.mult),
                     reads=[bpY] + cbufs, writes=[bIYT])
                yield
                Gn = sl["G%d" % (lev % 2)]
                Hn = sl["H%d" % (lev % 2)]
                pG, bpG = psr.next()
                for u in range(4):
                    rhsG = self.ident_b[:] if Gc is None else Gc[0][:, u, :]
                    B.op("pe", lambda e, u=u, rhsG=rhsG: e.matmul(pG[:, u * 128:(u + 1) * 128], lhsT=IYT[:, u, :], rhs=rhsG, start=True, stop=True),
                         reads=[bIYT] + cbufs + ([] if Gc is None else [Gc[1]]), writes=[bpG], inc=(u == 3))
                B.op("act", lambda e, Gn=Gn: e.activation(out=Gn[0][:].rearrange("p u l -> p (u l)"), in_=pG[:, :], func=AF.Identity), reads=[bpG], writes=[Gn[1]])
                if lev < 6:
                    pH, bpH = psr.next()
                    for u in range(4):
                        lhsG = self.ident_b[:] if Gc is None else Gc[0][:, u, :]
                        B.op("pe", lambda e, u=u, lhsG=lhsG: e.matmul(pH[:, u * 128:(u + 1) * 128], lhsT=lhsG, rhs=IYT[:, u, :], start=True, stop=True),
                             reads=[bIYT] + cbufs + ([] if Gc is None else [Gc[1]]), writes=[bpH], inc=(u == 3))
                    B.op("act", lambda e, Hn=Hn: e.activation(out=Hn[0][:].rearrange("p u l -> p (u l)"), in_=pH[:, :], func=AF.Identity), reads=[bpH], writes=[Hn[1]])
                    Hc = Hn
                Gc = Gn
                yield
            G, bG = Gc
            pW, bpW = psr.next()
            for u in range(4):
                B.op("pe", lambda e, u=u: e.matmul(pW[:, u * 128:(u + 1) * 128], lhsT=Kg[:, u, :], rhs=G[:, u, :], start=True, stop=True),
                     reads=[bKg, bG], writes=[bpW], inc=(u == 3))
            B.op("act", lambda e: e.activation(out=negW[:].rearrange("p u l -> p (u l)"), in_=pW[:, :], func=AF.Identity, scale=-1.0), reads=[bpW], writes=[bnegW])
            yield
            while step > 0 and ("gdn", step - 1, d, hg) not in done and ("gdn", step - 1, d, hg) in started:
                yield
            sbt, bsb = Sb_cur[d][hg]
            pV, bpV = psr.next()
            for u in range(4):
                B.op("pe", lambda e, u=u: e.matmul(pV[:, u * 128:(u + 1) * 128], lhsT=G[:, u, :], rhs=tv[:, (h0 + u) * 128:(h0 + u + 1) * 128], start=True, stop=False),
                     reads=[bG, bv], writes=[bpV], inc=False)
                B.op("pe", lambda e, u=u: e.matmul(pV[:, u * 128:(u + 1) * 128], lhsT=negW[:, u, :], rhs=sbt[:, u, :], start=False, stop=True),
                     reads=[bnegW, bsb], writes=[bpV], inc=(u == 3))
            B.op("dve", lambda e: e.tensor_tensor(out=vnew[:], in0=pV[:, :].rearrange("p (u l) -> p u l", u=4), in1=bc3(bcol, 128), op=ALU.mult),
                 reads=[bpV, bg], writes=[bvnew])
            yield
            if need_o:
                pO1, bpO1 = psr.next()
                for u in range(4):
                    B.op("pe", lambda e, u=u: e.matmul(pO1[:, u * 128:(u + 1) * 128], lhsT=tq[:, h0 + u, :], rhs=sbt[:, u, :], start=True, stop=True),
                         reads=[bq, bsb], writes=[bpO1], inc=(u == 3))
                o1, bo1 = ring_o1.next()
                B.op("dve", lambda e: e.tensor_tensor(out=o1[:], in0=pO1[:, :].rearrange("p (u l) -> p u l", u=4), in1=bc3(eg, 128), op=ALU.mult),
                     reads=[bpO1, bgs], writes=[bo1])
                pO2, bpO2 = psr.next()
                for u in range(4):
                    B.op("pe", lambda e, u=u: e.matmul(pO2[:, u * 128:(u + 1) * 128], lhsT=attnT[:, u, :], rhs=vnew[:, u, :], start=True, stop=True),
                         reads=[battn, bvnew], writes=[bpO2], inc=(u == 3))
                o, bo = ring_o.next()
                B.op("dve", lambda e: e.tensor_tensor(out=o[:], in0=pO2[:, :].rearrange("p (u l) -> p u l", u=4), in1=o1[:], op=ALU.add),
                     reads=[bpO2, bo1], writes=[bo])
                dst = (self.OF if d == 0 else self.OB)[c - 2]
                B.dma("sp", dst[:, hg * 512:(hg + 1) * 512], o[:].rearrange("p u l -> p (u l)"), bo, reads=[bo])
            pS, bpS = psr.next()
            for u in range(4):
                B.op("pe", lambda e, u=u: e.matmul(pS[:, u * 128:(u + 1) * 128], lhsT=kt[:, u, :], rhs=vnew[:, u, :], start=True, stop=True),
                     reads=[bkt, bvnew], writes=[bpS], inc=(u == 3))
            Sg = S[d][:, h0:h0 + 4, :]
            B.op("pool", lambda e: e.tensor_tensor(out=St[:], in0=Sg, in1=bc3(gte, 128), op=ALU.mult), reads=[bS[d][hg], bgs], writes=[bSt])
            B.op("dve", lambda e: e.tensor_tensor(out=Sg, in0=pS[:, :].rearrange("p (u l) -> p u l", u=4), in1=St[:], op=ALU.add),
                 reads=[bpS, bSt], writes=[bS[d][hg]])
            nsb, bnsb = Sb[d][hg].next()
            B.op("act", lambda e: e.activation(out=nsb[:], in_=Sg, func=AF.Identity), reads=[bS[d][hg]], writes=[bnsb])
            Sb_cur[d][hg] = (nsb, bnsb)
            yield

        self._b_env = dict(data=data, dirc=dirc, cbufs=cbufs, psr=psr, order=order, out_lo=out_lo, out_hi=out_hi, bc3=bc3, bcm=bcm,
                           C=C, bC=bC, Cb=Cb, Cb_cur=Cb_cur, nst=nst, nbf=nbf, n_cur=n_cur, nb_cur=nb_cur, mst=mst, m_cur=m_cur,
                           ring_num=ring_num, ring_h=ring_h)
        ml_group = self.make_ml_group()

        from collections import deque
        pending = deque()
        for step in range(nsteps):
            dirs = [d for d in range(2) if not (d == 0 and order[0][step] >= out_hi)]
            for d in dirs:
                pending.append(("load", step, d))
            for hg in range(2):
                for d in dirs:
                    if not dbg.get("b_no_gdn"):
                        pending.append(("gdn", step, d, hg))
            for d in dirs:
                if not dbg.get("b_no_ml"):
                    pending.append(("ml", step, d))
        free_g = list(range(NG))
        free_m = list(range(NM))
        done = set()
        started = set()
        active = []
        loaded = set()
        while pending or active:
            while pending:
                it = pending[0]
                if it[0] == "load":
                    _, step, d = it
                    if any((k[1] == step - 2 and k[2] == d and k not in done) for k in started):
                        break
                    load_step(step, d)
                    shared_pre(step, d)
                    pending.popleft()
                    continue
                if it[0] == "gdn":
                    _, step, d, hg = it
                    if not free_g:
                        break
                    si = free_g.pop(0)
                    started.add(it)
                    active.append((it, gdn_group(step, d, hg, gslots[si]), ("g", si)))
                    pending.popleft()
                    continue
                if it[0] == "ml":
                    _, step, d = it
                    key_prev = ("ml", step - 1, d)
                    if (step > 0 and key_prev not in done) or not free_m:
                        break
                    si = free_m.pop(0)
                    started.add(it)
                    active.append((it, ml_group(step, d, mslots[si]), ("m", si)))
                    pending.popleft()
                    continue
            for ent in list(active):
                it, gen, (kind, si) = ent
                try:
                    next(gen)
                except StopIteration:
                    active.remove(ent)
                    done.add(it)
                    (free_g if kind == "g" else free_m).append(si)
        B.barrier()
        st.close()

    def make_ml_group(self):
        B = self.B
        env = self._b_env
        data, dirc, cbufs, psr = env["data"], env["dirc"], env["cbufs"], env["psr"]
        bc3, bcm = env["bc3"], env["bcm"]
        C, bC, Cb, Cb_cur = env["C"], env["bC"], env["Cb"], env["Cb_cur"]
        nst, nbf, n_cur, nb_cur, mst, m_cur = env["nst"], env["nbf"], env["n_cur"], env["nb_cur"], env["mst"], env["m_cur"]
        ring_num, ring_h = env["ring_num"], env["ring_h"]
        out_lo, out_hi = env["out_lo"], env["out_hi"]

        def bc3n(ap2, n):
            return ap2.unsqueeze(2).to_broadcast([128, ap2.shape[1], n])

        def ml_group(step, d, sl):
            dd = data[(step, d)]
            dc = dirc[d]
            c = dd["c"]
            need_o = out_lo <= c < out_hi
            tg, bg = dd["GT"]; gs, bgs = dd["gs"]
            mq, bmq = dd["MQ"]; mk, bmk = dd["MK"]; mv, bmv = dd["MV"]
            X, bX = sl["X"]; Y, bY = sl["Y"]; Pm, bPm = sl["Pm"]; PT, bPT = sl["PT"]; Kw, bKw = sl["Kw"]
            sm, bsm = sl["sm"]; Ct, bCt = sl["Ct"]
            bcc = gs[:, 8:12]
            blast = gs[:, 20:24]
            cvec = gs[:, 48:52]
            mprev, bmprev = m_cur[d]
            B.op("pool", lambda e: e.tensor_tensor(out=X[:], in0=bcm(self.ident_f[:]), in1=bc3(cvec, 128), op=ALU.mult), reads=[bgs] + cbufs, writes=[bX])
            pC, bpC = psr.next()
            for u in range(4):
                B.op("pe", lambda e, u=u: e.matmul(pC[:, u * 128:(u + 1) * 128], lhsT=self.ones_f[:], rhs=X[:, u, :], start=True, stop=True),
                     reads=[bX] + cbufs, writes=[bpC], inc=(u == 3))
            B.op("dve", lambda e: e.tensor_tensor(out=Y[:], in0=pC[:, :].rearrange("p (u l) -> p u l", u=4), in1=bc3(bcc, 128), op=ALU.add),
                 reads=[bpC, bgs], writes=[bY])
            B.op("pool", lambda e: e.tensor_tensor(out=Y[:], in0=Y[:], in1=bcm(dc["MB"][:]), op=ALU.add), reads=[bY] + cbufs, writes=[bY])
            B.op("dve", lambda e: e.tensor_reduce(out=sm[:, 0:4], in_=Y[:], axis=AX.X, op=ALU.max), reads=[bY], writes=[bsm])
            B.op("dve", lambda e: e.tensor_tensor(out=sm[:, 4:8], in0=bcc, in1=mprev[:, 0:4], op=ALU.add), reads=[bgs, bmprev], writes=[bsm])
            B.op("dve", lambda e: e.tensor_tensor(out=sm[:, 8:12], in0=sm[:, 0:4], in1=sm[:, 4:8], op=ALU.max), reads=[bsm], writes=[bsm])
            B.op("dve", lambda e: e.tensor_scalar(out=sm[:, 12:16], in0=sm[:, 8:12], scalar1=-1.0, scalar2=None, op0=ALU.mult), reads=[bsm], writes=[bsm])
            if need_o:
                pQK, bpQK = psr.next()
                for u in range(4):
                    B.op("pe", lambda e, u=u: e.matmul(pQK[:, u * 128:(u + 1) * 128], lhsT=mq[:, u, :], rhs=mk[:, u, :], start=True, stop=True),
                         reads=[bmq, bmk], writes=[bpQK], inc=(u == 3))
                for u in range(4):
                    B.op("act", lambda e, u=u: e.activation(out=X[:, u, :], in_=Y[:, u, :], func=AF.Exp, bias=sm[:, 12 + u:13 + u]), reads=[bY, bsm], writes=[bX])
                B.op("dve", lambda e: e.tensor_tensor(out=Pm[:], in0=pQK[:, :].rearrange("p (u l) -> p u l", u=4), in1=X[:], op=ALU.mult),
                     reads=[bpQK, bX], writes=[bPm])
            yield
            if need_o:
                pT, bpT = psr.next()
                pTb = pT[:].bitcast(BF16)
                for u in range(4):
                    B.op("pe", lambda e, u=u: e.transpose(pTb[:, u * 128:(u + 1) * 128], Pm[:, u, :], self.ident_b[:]), reads=[bPm] + cbufs, writes=[bpT], inc=(u == 3))
                B.op("act", lambda e: e.activation(out=PT[:].rearrange("p u l -> p (u l)"), in_=pTb[:, 0:512], func=AF.Identity), reads=[bpT], writes=[bPT])
                B.op("dve", lambda e: e.tensor_tensor(out=sm[:, 16:20], in0=sm[:, 4:8], in1=sm[:, 8:12], op=ALU.subtract), reads=[bsm], writes=[bsm])
                B.op("act", lambda e: e.activation(out=sm[:, 16:20], in_=sm[:, 16:20], func=AF.Exp), reads=[bsm], writes=[bsm])
                B.op("act", lambda e: e.activation(out=sm[:, 20:24], in_=sm[:, 12:16], func=AF.Exp), reads=[bsm], writes=[bsm])
                yield
                cbt, bcb = Cb_cur[d]
                nbt, bnb = nb_cur[d]
                num, bnum = ring_num.next()
                hh, bhh = ring_h.next()
                for pr in range(2):
                    pN1, bpN1 = psr.next()
                    pN2, bpN2 = psr.next()
                    for uu in range(2):
                        u = pr * 2 + uu
                        B.op("pe", lambda e, u=u, uu=uu, pN1=pN1: e.matmul(pN1[:, uu * 256:(uu + 1) * 256], lhsT=mq[:, u, :], rhs=cbt[:, u, :], start=True, stop=True),
                             reads=[bmq, bcb], writes=[bpN1], inc=(uu == 1))
                    for uu in range(2):
                        u = pr * 2 + uu
                        B.op("pe", lambda e, u=u, uu=uu, pN2=pN2: e.matmul(pN2[:, uu * 256:(uu + 1) * 256], lhsT=PT[:, u, :], rhs=mv[:, u * 256:(u + 1) * 256], start=True, stop=True),
                             reads=[bPT, bmv], writes=[bpN2], inc=(uu == 1))
                    B.op("dve", lambda e, pr=pr, pN1=pN1: e.tensor_tensor(out=num[:, pr * 2:pr * 2 + 2, :], in0=pN1[:, :].rearrange("p (u l) -> p u l", u=2),
                                                                         in1=bc3n(sm[:, 16 + pr * 2:18 + pr * 2], 256), op=ALU.mult), reads=[bpN1, bsm], writes=[bnum])
                    B.op("dve", lambda e, pr=pr, pN2=pN2: e.tensor_tensor(out=num[:, pr * 2:pr * 2 + 2, :], in0=pN2[:, :].rearrange("p (u l) -> p u l", u=2),
                                                                         in1=num[:, pr * 2:pr * 2 + 2, :], op=ALU.add), reads=[bpN2, bnum], writes=[bnum])
                pDn, bpDn = psr.next()
                for u in range(4):
                    B.op("pe", lambda e, u=u: e.matmul(pDn[:, u:u + 1], lhsT=mq[:, u, :], rhs=nbt[:, u:u + 1], start=True, stop=True), reads=[bmq, bnb], writes=[bpDn], inc=False)
                for u in range(4):
                    B.op("pe", lambda e, u=u: e.matmul(pDn[:, 4 + u:5 + u], lhsT=PT[:, u, :], rhs=self.ones_b[:, 0:1], start=True, stop=True),
                         reads=[bPT] + cbufs, writes=[bpDn], inc=(u == 3))
                B.op("dve", lambda e: e.tensor_tensor(out=sm[:, 24:28], in0=pDn[:, 0:4], in1=sm[:, 16:20], op=ALU.mult), reads=[bpDn, bsm], writes=[bsm])
                B.op("dve", lambda e: e.tensor_tensor(out=sm[:, 24:28], in0=pDn[:, 4:8], in1=sm[:, 24:28], op=ALU.add), reads=[bpDn, bsm], writes=[bsm])
                B.op("dve", lambda e: e.tensor_tensor(out=sm[:, 24:28], in0=sm[:, 24:28], in1=sm[:, 24:28], op=ALU.mult), reads=[bsm], writes=[bsm])
                B.op("dve", lambda e: e.tensor_tensor(out=sm[:, 28:32], in0=sm[:, 20:24], in1=sm[:, 20:24], op=ALU.mult), reads=[bsm], writes=[bsm])
                B.op("dve", lambda e: e.tensor_tensor(out=sm[:, 24:28], in0=sm[:, 24:28], in1=sm[:, 28:32], op=ALU.max), reads=[bsm], writes=[bsm])
                B.op("pool", lambda e: e.tensor_tensor(out=sm[:, 28:32], in0=sm[:, 24:28], in1=self.nhalf[:, 0:4], op=ALU.pow), reads=[bsm] + cbufs, writes=[bsm])
                B.op("dve", lambda e: e.tensor_tensor(out=hh[:], in0=num[:], in1=bc3n(sm[:, 28:32], 256), op=ALU.mult), reads=[bnum, bsm], writes=[bhh])
                dst = (self.HF if d == 0 else self.HB)[c - 2]
                B.dma("sp", dst[:, :], hh[:].rearrange("p u l -> p (u l)"), bhh, reads=[bhh])
                yield
            pSel, bpSel = psr.next()
            B.op("pe", lambda e: e.matmul(pSel[:, 0:4], lhsT=dc["SEL"][:], rhs=sm[:, 8:12], start=True, stop=True), reads=[bsm] + cbufs, writes=[bpSel])
            mnew, bmnew = mst[d].next()
            B.op("act", lambda e: e.activation(out=mnew[:], in_=pSel[:, 0:4], func=AF.Identity), reads=[bpSel], writes=[bmnew])
            B.op("dve", lambda e: e.tensor_tensor(out=sm[:, 32:36], in0=cvec, in1=blast, op=ALU.add), reads=[bgs], writes=[bsm])
            B.op("dve", lambda e: e.tensor_tensor(out=sm[:, 32:36], in0=sm[:, 32:36], in1=mnew[:], op=ALU.subtract), reads=[bsm, bmnew], writes=[bsm])
            B.op("dve", lambda e: e.tensor_tensor(out=sm[:, 36:40], in0=blast, in1=mprev[:, 0:4], op=ALU.add), reads=[bgs, bmprev], writes=[bsm])
            B.op("dve", lambda e: e.tensor_tensor(out=sm[:, 36:40], in0=sm[:, 36:40], in1=mnew[:], op=ALU.subtract), reads=[bsm, bmnew], writes=[bsm])
            B.op("act", lambda e: e.activation(out=sm[:, 32:40], in_=sm[:, 32:40], func=AF.Exp), reads=[bsm], writes=[bsm])
            pKt, bpKt = psr.next()
            pKtb = pKt[:].bitcast(BF16)
            for u in range(4):
                B.op("pe", lambda e, u=u: e.transpose(pKtb[:, u * 128:(u + 1) * 128], mk[:, u, :], self.ident_b[:]), reads=[bmk] + cbufs, writes=[bpKt], inc=(u == 3))
            B.op("dve", lambda e: e.tensor_tensor(out=Kw[:], in0=pKtb[:, 0:512].rearrange("p (u l) -> p u l", u=4), in1=bc3(sm[:, 32:36], 128), op=ALU.mult),
                 reads=[bpKt, bsm], writes=[bKw])
            m_cur[d] = (mnew, bmnew)
            yield
            for pr in range(2):
                pC2, bpC2 = psr.next()
                for uu in range(2):
                    u = pr * 2 + uu
                    B.op("pe", lambda e, u=u, uu=uu, pC2=pC2: e.matmul(pC2[:, uu * 256:(uu + 1) * 256], lhsT=Kw[:, u, :], rhs=mv[:, u * 256:(u + 1) * 256], start=True, stop=True),
                         reads=[bKw, bmv], writes=[bpC2], inc=(uu == 1))
                B.op("pool", lambda e, pr=pr: e.tensor_tensor(out=Ct[:, pr * 2:pr * 2 + 2, :], in0=C[d][:, pr * 2:pr * 2 + 2, :], in1=bc3n(sm[:, 36 + pr * 2:38 + pr * 2], 256), op=ALU.mult),
                     reads=[bC[d], bsm], writes=[bCt])
                B.op("dve", lambda e, pr=pr, pC2=pC2: e.tensor_tensor(out=C[d][:, pr * 2:pr * 2 + 2, :], in0=pC2[:, :].rearrange("p (u l) -> p u l", u=2), in1=Ct[:, pr * 2:pr * 2 + 2, :], op=ALU.add),
                     reads=[bpC2, bCt], writes=[bC[d]])
            pN, bpN = psr.next()
            for u in range(4):
                B.op("pe", lambda e, u=u: e.matmul(pN[:, u:u + 1], lhsT=Kw[:, u, :], rhs=self.ones_b[:, 0:1], start=True, stop=True), reads=[bKw] + cbufs, writes=[bpN], inc=(u == 3))
            nold, bnold = n_cur[d]
            nnew, bnnew = nst[d].next()
            B.op("dve", lambda e: e.tensor_tensor(out=nnew[:, 4:8], in0=nold[:, 0:4], in1=sm[:, 36:40], op=ALU.mult), reads=[bnold, bsm], writes=[bnnew])
            B.op("dve", lambda e: e.tensor_tensor(out=nnew[:, 0:4], in0=pN[:, 0:4], in1=nnew[:, 4:8], op=ALU.add), reads=[bpN, bnnew], writes=[bnnew])
            nbn, bnbn = nbf[d].next()
            B.op("act", lambda e: e.activation(out=nbn[:], in_=nnew[:, 0:4], func=AF.Identity), reads=[bnnew], writes=[bnbn])
            cbn, bcbn = Cb[d].next()
            B.op("act", lambda e: e.activation(out=cbn[:], in_=C[d][:], func=AF.Identity), reads=[bC[d]], writes=[bcbn])
            n_cur[d] = (nnew, bnnew)
            nb_cur[d] = (nbn, bnbn)
            Cb_cur[d] = (cbn, bcbn)
            yield

        return ml_group

    def phaseC1(self):
        B, nc, inp = self.B, self.nc, self.inp
        st = ExitStack()
        dbg = self.debug
        wo, bwo = self.load_w_bf16(st, "c_wo", inp["w_o"], 8, 4096, 8)
        wbg, bwbg = self.load_w_bf16(st, "c_wbg", inp["w_bg"], 8, 1024, 2)
        wbm, bwbm = self.load_w_bf16(st, "c_wbm", inp["w_bm"], 8, 1024, 2)
        wout, bwout = self.load_w_bf16(st, "c_wout", inp["w_out"], 8, 1024, 2)
        nwb = B.sb(st, "c_nwb", [128, 2, 1024], F32)
        bnwb = Buf("c_nwb")
        B.dma("sp", nwb[:, 0, :], inp["gnw_bc"][:, :], bnwb, writes=[bnwb])
        B.dma("sp", nwb[:, 1, :], inp["mnw_bc"][:, :], bnwb, writes=[bnwb])
        zero = B.sb(st, "c_zero", [64, 1024], F32)
        bzero = Buf("c_zero")
        B.op("pool", lambda e: e.memset(zero[:], 0.0), writes=[bzero])
        bX1 = Buf("X1")
        B.dma("sp", self.X1[0:64, :], zero[:], bzero, reads=[bzero], writes=[bX1])
        NS = 2
        xt = B.sb(st, "c_x", [128, NS, 1024], F32)
        bxts = [Buf("c_x%d" % i) for i in range(NS)]
        xn = B.sb(st, "c_xn", [128, NS, 1024], BF16); bxn = Buf("c_xn")
        sq = B.sb(st, "c_sq", [128, 24], F32); bsq = Buf("c_sq")
        junk = None; bjunk = None
        hxT = B.sb(st, "c_hxT", [128, 8, NS * 128], BF16); bhx = Buf("c_hxT")
        oa = B.sb(st, "c_oa", [128, 1024], F32); boa = Buf("c_oa")
        ob = B.sb(st, "c_ob", [128, 1024], F32); bob = Buf("c_ob")
        gt = B.sb(st, "c_gt", [128, 1024], F32); bgt = Buf("c_gt")
        osq = B.sb(st, "c_osq", [128, 1024], F32); bosq = Buf("c_osq")
        sm = B.sb(st, "c_sm", [128, 32], F32); bsm = Buf("c_sm")
        og = B.sb(st, "c_og", [128, 1024], BF16); bog = Buf("c_og")
        brT = [B.sb(st, "c_brT%d" % i, [128, 8, NS * 128], BF16) for i in range(2)]
        bbrT = [Buf("c_brT%d" % i) for i in range(2)]
        sg = Ring(B, st, "c_sg", 4, [128, NS * 128], F32)
        yt = Ring(B, st, "c_yt", 2, [128, NS * 128], F32)
        mT = B.sb(st, "c_mT", [128, 8, NS * 128], BF16); bmT = Buf("c_mT")
        tmp = Ring(B, st, "c_tmp", 2, [128, 512], F32)
        ptr = Ring(B, st, "c_ptr", 2, [128, 512], F32, psum=True)
        pmm = Ring(B, st, "c_pmm", 5, [128, 512], F32, psum=True)
        nt = OWN_T // 128
        sts = []
        i = 0
        while i < nt:
            ns = min(NS, nt - i)
            sts.append((i, ns))
            i += ns
        if dbg.get("c1_tiles"):
            sts = sts[: dbg["c1_tiles"]]
        for (t0, ns) in sts:
            n = ns * 128
            for s in range(ns):
                B.dma("sp", xt[:, s, :], inp["x"][(t0 + s) * 128:(t0 + s + 1) * 128, :], bxts[s], writes=[bxts[s]])
            self.norm_transpose(xt, bxts, ns, xn, bxn, sq, bsq, junk, bjunk, ptr, hxT, bhx, 0, 0)
            for br in range(2):
                nh, hd = (8, 128) if br == 0 else (4, 256)
                srcf, srcb = (self.OF, self.OB) if br == 0 else (self.HF, self.HB)
                for s in range(ns):
                    c = t0 + s
                    B.dma("sp", oa[:], srcf[c], boa, writes=[boa])
                    B.dma("sp", ob[:], srcb[c], bob, writes=[bob])
                    for hf in range(2):
                        p, bp = pmm.next()
                        for k in range(8):
                            B.op("pe", lambda e, k=k, p=p, s=s, hf=hf, br=br: e.matmul(
                                p[:], lhsT=hxT[:, k, s * 128:(s + 1) * 128], rhs=wo[:, k, br * 1024 + hf * 512: br * 1024 + (hf + 1) * 512],
                                start=(k == 0), stop=(k == 7)), reads=[bhx, bwo], writes=[bp], inc=(k == 7))
                        B.op("act", lambda e, p=p, hf=hf, br=br: e.activation(out=gt[:, hf * 512:(hf + 1) * 512], in_=p[:],
                                                                            func=(AF.Silu if br == 0 else AF.Sigmoid)), reads=[bp], writes=[bgt])
                    B.op("pool", lambda e, br=br: e.tensor_tensor(out=gt[:], in0=gt[:], in1=nwb[:, br, :], op=ALU.mult), reads=[bgt, bnwb], writes=[bgt])
                    B.op("dve", lambda e: e.tensor_tensor(out=oa[:], in0=oa[:], in1=ob[:], op=ALU.add), reads=[boa, bob], writes=[boa])
                    B.op("pool", lambda e: e.tensor_tensor(out=osq[:], in0=oa[:], in1=oa[:], op=ALU.mult), reads=[boa], writes=[bosq])
                    B.op("dve", lambda e, nh=nh: e.tensor_reduce(out=sm[:, 0:nh], in_=osq[:].rearrange("p (h e) -> p h e", h=nh), axis=AX.X, op=ALU.add),
                         reads=[bosq], writes=[bsm])
                    B.op("dve", lambda e, nh=nh, hd=hd: e.tensor_scalar(out=sm[:, 8:8 + nh], in0=sm[:, 0:nh], scalar1=float(1.0 / hd), scalar2=float(EPS),
                                                                      op0=ALU.mult, op1=ALU.add), reads=[bsm], writes=[bsm])
                    B.op("pool", lambda e, nh=nh: e.tensor_tensor(out=sm[:, 16:16 + nh], in0=sm[:, 8:8 + nh], in1=self.nhalf[:, 0:nh], op=ALU.pow),
                         reads=[bsm, self.cb], writes=[bsm])
                    B.op("dve", lambda e, nh=nh, hd=hd: e.tensor_tensor(out=osq[:].rearrange("p (h e) -> p h e", h=nh), in0=oa[:].rearrange("p (h e) -> p h e", h=nh),
                                                                      in1=sm[:, 16:16 + nh].unsqueeze(2).to_broadcast([128, nh, hd]), op=ALU.mult),
                         reads=[boa, bsm], writes=[bosq])
                    B.op("dve", lambda e: e.tensor_tensor(out=og[:], in0=osq[:], in1=gt[:], op=ALU.mult), reads=[bosq, bgt], writes=[bog])
                    p, bp = ptr.next()
                    pb = p[:].bitcast(BF16)
                    for k in range(8):
                        B.op("pe", lambda e, k=k, pb=pb: e.transpose(pb[:, k * 128:(k + 1) * 128], og[:, k * 128:(k + 1) * 128], self.ident_b[:]),
                             reads=[bog, self.cb], writes=[bp], inc=(k == 7))
                    B.op("act", lambda e, pb=pb, s=s, br=br: e.activation(out=brT[br][:, :, s * 128:(s + 1) * 128], in_=pb[:, 0:1024].rearrange("p (k t) -> p k t", k=8),
                                                                          func=AF.Identity), reads=[bp], writes=[bbrT[br]])
            for ncn in range(8):
                sgs = []
                for gi in range(2):
                    p, bp = pmm.next()
                    for k in range(8):
                        B.op("pe", lambda e, k=k, p=p, gi=gi, ncn=ncn: e.matmul(p[:, 0:n], lhsT=wo[:, k, 2048 + gi * 1024 + ncn * 128: 2048 + gi * 1024 + (ncn + 1) * 128],
                                                                               rhs=hxT[:, k, 0:n], start=(k == 0), stop=(k == 7)), reads=[bhx, bwo], writes=[bp], inc=(k == 7))
                    g_, bg_ = sg.next()
                    B.op("act", lambda e, p=p, g_=g_: e.activation(out=g_[:, 0:n], in_=p[:, 0:n], func=AF.Sigmoid), reads=[bp], writes=[bg_])
                    sgs.append((g_, bg_))
                ys = []
                for br, (w, bw) in enumerate(((wbg, bwbg), (wbm, bwbm))):
                    p, bp = pmm.next()
                    for k in range(8):
                        B.op("pe", lambda e, k=k, p=p, w=w, br=br, ncn=ncn: e.matmul(p[:, 0:n], lhsT=w[:, k, ncn * 128:(ncn + 1) * 128], rhs=brT[br][:, k, 0:n],
                                                                                    start=(k == 0), stop=(k == 7)), reads=[bbrT[br], bw], writes=[bp], inc=(k == 7))
                    ys.append((p, bp))
                y_, by_ = yt.next()
                B.op("dve", lambda e, y_=y_: e.tensor_tensor(out=y_[:, 0:n], in0=ys[0][0][:, 0:n], in1=sgs[0][0][:, 0:n], op=ALU.mult), reads=[ys[0][1], sgs[0][1]], writes=[by_])
                g1, bg1 = sgs[1]
                B.op("dve", lambda e, g1=g1: e.tensor_tensor(out=g1[:, 0:n], in0=ys[1][0][:, 0:n], in1=g1[:, 0:n], op=ALU.mult), reads=[ys[1][1], bg1], writes=[bg1])
                B.op("pool", lambda e, y_=y_, g1=g1, ncn=ncn: e.tensor_tensor(out=mT[:, ncn, 0:n], in0=y_[:, 0:n], in1=g1[:, 0:n], op=ALU.add), reads=[by_, bg1], writes=[bmT])
            for s in range(ns):
                for hf in range(2):
                    p, bp = pmm.next()
                    for k in range(8):
                        B.op("pe", lambda e, k=k, p=p, s=s, hf=hf: e.matmul(p[:], lhsT=mT[:, k, s * 128:(s + 1) * 128], rhs=wout[:, k, hf * 512:(hf + 1) * 512],
                                                                           start=(k == 0), stop=(k == 7)), reads=[bmT, bwout], writes=[bp], inc=(k == 7))
                    t_, bt_ = tmp.next()
                    B.op("dve", lambda e, p=p, t_=t_, hf=hf: e.tensor_tensor(out=t_[:], in0=p[:], in1=self.gate_bc[:, 0, hf * 512:(hf + 1) * 512], op=ALU.mult),
                         reads=[bp, self.bgate], writes=[bt_])
                    B.op("pool", lambda e, t_=t_, s=s, hf=hf: e.tensor_tensor(out=xt[:, s, hf * 512:(hf + 1) * 512], in0=xt[:, s, hf * 512:(hf + 1) * 512], in1=t_[:], op=ALU.add),
                         reads=[bt_, bxts[s]], writes=[bxts[s]])
                B.dma("sp", self.X1[64 + (t0 + s) * 128: 64 + (t0 + s + 1) * 128, :], xt[:, s, :], bxts[s], reads=[bxts[s]], writes=[bX1])
        B.barrier()
        st.close()

    def precast_wup(self):
        B = self.B
        self.WUPB = B.dram("WUPB", [44, 128, 8, 128], BF16)
        self.bwupb = Buf("WUPB")
        src = self.inp["w_up"].rearrange("(k p) (c j) -> c p k j", p=128, j=128)
        for c in range(44):
            B.dma("pool", self.WUPB[c], src[c], self.bwupb, writes=[self.bwupb])

    def phaseC2(self):
        B, nc, inp = self.B, self.nc, self.inp
        st = ExitStack()
        dbg = self.debug
        wd = B.sb(st, "d_wd", [128, 22, 1024], BF16)
        bwd = Buf("d_wd")
        wdv = inp["w_down"].rearrange("(c p) n -> p c n", p=128)
        for i in range(0, 22, 6):
            j = min(22, i + 6)
            B.dma("pool", wd[:, i:j, :], wdv[:, i:j, :], bwd, writes=[bwd])
        cw = B.sb(st, "d_cw", [128, 44, 9], F32)
        nob = B.sb(st, "d_nob", [128, 1024], F32)
        bsm0 = Buf("d_small")
        B.dma("sp", cw[:], inp["ffn_cw"][:, :, :], bsm0, writes=[bsm0])
        B.dma("sp", nob[:], inp["now_bc"][:, :], bsm0, writes=[bsm0])
        DEPTH = 3
        wup = Ring(B, st, "d_wup", DEPTH + 1, [128, 2, 8, 128], BF16)
        xt = B.sb(st, "d_x", [128, 5, 1024], F32)
        bxts = [Buf("d_x%d" % i) for i in range(5)]
        xn = B.sb(st, "d_xn", [128, 5, 1024], BF16); bxn = Buf("d_xn")
        sq = B.sb(st, "d_sq", [128, 24], F32); bsq = Buf("d_sq")
        junk = B.sb(st, "d_junk", [128, 1024], BF16); bjunk = Buf("d_junk")
        hxT = B.sb(st, "d_hxT", [128, 8, 640], BF16); bhx = Buf("d_hxT")
        upad = Ring(B, st, "d_up", 2 * DEPTH, [128, 10, 66], BF16)
        dgr = Ring(B, st, "d_dg", 2 * DEPTH, [128, 9, 128], BF16)
        sgt = Ring(B, st, "d_sg", DEPTH, [128, 512], F32)
        aT = B.sb(st, "d_aT", [128, 22, 512], BF16); baT = Buf("d_aT")
        xo = B.sb(st, "d_xo", [128, 4, 1024], F32)
        bxo = [Buf("d_xo%d" % i) for i in range(4)]
        t2 = Ring(B, st, "d_t2", 2, [128, 512], F32)
        sq2 = B.sb(st, "d_sq2", [128, 16], F32); bsq2 = Buf("d_sq2")
        pr = Ring(B, st, "d_pr", 8, [128, 512], F32, psum=True)
        for (u_, bu_) in upad.slots:
            B.op("pool", lambda e, u_=u_: e.memset(u_[:], 0.0), writes=[bu_])
        nblk = dbg.get("c2_blocks", 8)
        for j in range(nblk):
            r0 = 512 * j
            for s in range(5):
                B.dma("sp", xt[:, s, :], self.X1[r0 + s * 128: r0 + (s + 1) * 128, :], bxts[s], writes=[bxts[s]])
            for s in range(4):
                B.dma("sp", xo[:, s, :], self.X1[r0 + 64 + s * 128: r0 + 64 + (s + 1) * 128, :], bxo[s], writes=[bxo[s]])
            self.norm_transpose(xt, bxts, 5, xn, bxn, sq, bsq, junk, bjunk, pr, hxT, bhx, 0, 4)

            def pair_gen(c):
                w, bw = wup.next()
                B.dma("sp", w[:, 0], self.WUPB[c], bw, reads=[self.bwupb], writes=[bw])
                B.dma("sp", w[:, 1], self.WUPB[22 + c], bw, reads=[self.bwupb], writes=[bw])
                ups, dgs = [], []
                for part in range(2):
                    ch = c + 22 * part
                    u_, bu_ = upad.next()
                    dg, bdg = dgr.next()
                    ups.append((u_, bu_))
                    dgs.append((dg, bdg))
                    B.op("dve", lambda e, dg=dg, ch=ch: e.tensor_tensor(out=dg[:], in0=self.ident_b[:].unsqueeze(1).to_broadcast([128, 9, 128]),
                                                                      in1=cw[:, ch, :].unsqueeze(2).to_broadcast([128, 9, 128]), op=ALU.mult),
                         reads=[self.cb, bsm0], writes=[bdg])
                yield
                for part in range(2):
                    u_, bu_ = ups[part]
                    p1, bp1 = pr.next()
                    p2, bp2 = pr.next()
                    for k in range(8):
                        B.op("pe", lambda e, k=k, p1=p1, part=part: e.matmul(p1[:], lhsT=w[:, part, k, :], rhs=hxT[:, k, 0:512], start=(k == 0), stop=(k == 7)),
                             reads=[bw, bhx], writes=[bp1], inc=(k == 7))
                    for k in range(8):
                        B.op("pe", lambda e, k=k, p2=p2, part=part: e.matmul(p2[:, 0:128], lhsT=w[:, part, k, :], rhs=hxT[:, k, 512:640], start=(k == 0), stop=(k == 7)),
                             reads=[bw, bhx], writes=[bp2], inc=(k == 7))
                    B.op("act", lambda e, u_=u_, p1=p1: e.activation(out=u_[:, 0:8, 1:65], in_=p1[:].rearrange("p (r c) -> p r c", c=64), func=AF.Identity),
                         reads=[bp1], writes=[bu_])
                    B.op("act", lambda e, u_=u_, p2=p2: e.activation(out=u_[:, 8:10, 1:65], in_=p2[:, 0:128].rearrange("p (r c) -> p r c", c=64), func=AF.Identity),
                         reads=[bp2], writes=[bu_])
                    if j == 0:
                        B.op("pool", lambda e, u_=u_: e.memset(u_[:, 0:1, :], 0.0), writes=[bu_])
                yield
                pcs = []
                for part in range(2):
                    u_, bu_ = ups[part]
                    dg, bdg = dgs[part]
                    pc, bpc = pr.next()
                    t = 0
                    for dr in range(3):
                        for dc_ in range(3):
                            B.op("pe", lambda e, t=t, dr=dr, dc_=dc_, pc=pc, u_=u_, dg=dg: e.matmul(
                                pc[:].rearrange("p (r c) -> p r c", c=64), lhsT=dg[:, t, :], rhs=u_[:, dr:dr + 8, dc_:dc_ + 64], start=(t == 0), stop=(t == 8)),
                                reads=[bu_, bdg], writes=[bpc], inc=(t == 8))
                            t += 1
                    pcs.append((pc, bpc))
                s_, bs_ = sgt.next()
                B.op("act", lambda e: e.activation(out=s_[:], in_=pcs[0][0][:], func=AF.Silu), reads=[pcs[0][1]], writes=[bs_])
                B.op("dve", lambda e: e.tensor_tensor(out=aT[:, c, :], in0=pcs[1][0][:], in1=s_[:], op=ALU.mult), reads=[pcs[1][1], bs_], writes=[baT])

            self.run_pipeline([(lambda c=c: pair_gen(c)) for c in range(22)], DEPTH)
            for s in range(4):
                for hf in range(2):
                    p, bp = pr.next()
                    for c in range(22):
                        B.op("pe", lambda e, c=c, p=p, s=s, hf=hf: e.matmul(p[:], lhsT=aT[:, c, s * 128:(s + 1) * 128], rhs=wd[:, c, hf * 512:(hf + 1) * 512],
                                                                           start=(c == 0), stop=(c == 21)), reads=[baT, bwd], writes=[bp], inc=(c == 21))
                    t_, bt_ = t2.next()
                    B.op("dve", lambda e, p=p, t_=t_, hf=hf: e.tensor_tensor(out=t_[:], in0=p[:], in1=self.gate_bc[:, 1, hf * 512:(hf + 1) * 512], op=ALU.mult),
                         reads=[bp, self.bgate], writes=[bt_])
                    B.op("pool", lambda e, t_=t_, s=s, hf=hf: e.tensor_tensor(out=xo[:, s, hf * 512:(hf + 1) * 512], in0=xo[:, s, hf * 512:(hf + 1) * 512], in1=t_[:], op=ALU.add),
                         reads=[bt_, bxo[s]], writes=[bxo[s]])
                B.op("act", lambda e, s=s: e.activation(out=junk[:], in_=xo[:, s, :], func=AF.Square, accum_out=sq2[:, s:s + 1]), reads=[bxo[s]], writes=[bjunk, bsq2])
                B.op("dve", lambda e, s=s: e.tensor_scalar(out=sq2[:, 4 + s:5 + s], in0=sq2[:, s:s + 1], scalar1=float(D * EPS), scalar2=None, op0=ALU.add), reads=[bsq2], writes=[bsq2])
                B.op("pool", lambda e, s=s: e.tensor_tensor(out=sq2[:, 8 + s:9 + s], in0=sq2[:, 4 + s:5 + s], in1=self.nhalf[:, 0:1], op=ALU.pow), reads=[bsq2, self.cb], writes=[bsq2])
                B.op("dve", lambda e, s=s: e.scalar_tensor_tensor(out=xo[:, s, :], in0=xo[:, s, :], scalar=sq2[:, 8 + s:9 + s], in1=nob[:], op0=ALU.mult, op1=ALU.mult),
                     reads=[bxo[s], bsq2, bsm0], writes=[bxo[s]])
                B.op("act", lambda e, s=s: e.activation(out=xo[:, s, :], in_=xo[:, s, :], func=AF.Identity, scale=32.0), reads=[bxo[s]], writes=[bxo[s]])
                B.dma("sp", self.out[j * 512 + s * 128: j * 512 + (s + 1) * 128, :], xo[:, s, :], bxo[s], reads=[bxo[s]])
        B.barrier()
        st.close()


def build_program(debug=None):
    P = Prog(debug=debug)
    P.precast_wup()
    P.phase0()
    P.phaseA()
    P.phaseB()
    P.phaseC1()
    P.phaseC2()
    P.top.close()
    return P.B.finish(), P


_CACHE = {}


def kernel(**inputs):
    inp = {k: np.asarray(v) for k, v in inputs.items()}
    if "nc" not in _CACHE:
        _CACHE["nc"] = build_program()[0]
    nc = _CACHE["nc"]
    in_maps = [prep_core(inp, core) for core in range(8)]
    res = run_bass_kernel_spmd(nc, in_maps, core_ids=list(range(8)))
    out = np.empty((4, T, D), np.float32)
    for core in range(8):
        o = np.asarray(res.results[core]["out"], np.float32)
        b = core // 2
        if core % 2 == 0:
            out[b, 0:4096] = o
        else:
            out[b, 4096:8192] = o[::-1]
    return out
```

```python
import numpy as np
from contextlib import ExitStack

import concourse.bass as bass
import concourse.mybir as mybir
from concourse.bass_utils import run_bass_kernel_spmd

F32 = mybir.dt.float32
BF16 = mybir.dt.bfloat16
AF = mybir.ActivationFunctionType
ALU = mybir.AluOpType
AX = mybir.AxisListType

D = 1024
T = 8192
TC = 256
KD = 8
EPS = 1e-6
NEG = -1.0e30


class Buf:
    __slots__ = ("name", "w", "r", "dsem")

    def __init__(self, name):
        self.name = name
        self.w = None
        self.r = {}
        self.dsem = None


class Builder:
    def __init__(self, needed=None):
        self.nc = bass.Bass("TRN2", target_bir_lowering=False)
        nc = self.nc
        self.dry = needed is None
        self.needed = needed
        self.waited = {}
        self.es = ExitStack()
        self.es.enter_context(nc.allow_low_precision("bf16 matmul operands, fp32 accumulation"))
        self.engs = {"pe": nc.tensor, "act": nc.scalar, "dve": nc.vector, "pool": nc.gpsimd, "sp": nc.sync}
        self.sems = {}
        self.cnt = {}
        self.rank = {}
        self.rank_of = {}
        self.seen = {e: {} for e in self.engs}
        for e in self.engs:
            self.sems[e] = self.es.enter_context(nc.semaphore("s_" + e))
            self.cnt[e] = 0
            self.rank[e] = 0
            self.rank_of[e] = {}
            self.waited[e] = set()
        self.ndsem = 0
        self.nins = 0

    def sb(self, stack, name, shape, dt):
        return stack.enter_context(self.nc.sbuf_tensor(name, list(shape), dt))

    def ps(self, stack, name, shape, dt=F32):
        return stack.enter_context(self.nc.psum_tensor(name, list(shape), dt))

    def dram(self, name, shape, dt, kind="Internal"):
        return self.nc.dram_tensor(name, list(shape), dt, kind=kind).ap()

    def new_dsem(self):
        k = "d%d" % self.ndsem
        self.ndsem += 1
        self.sems[k] = self.es.enter_context(self.nc.semaphore(k))
        self.cnt[k] = 0
        return k

    def _deps(self, eng, reads, writes):
        deps = {}

        def add(k, v):
            if deps.get(k, 0) < v:
                deps[k] = v

        for b in reads:
            if b.w is not None:
                add(*b.w)
        for b in writes:
            if b.w is not None and b.w[0] != eng:
                add(*b.w)
            for k, v in b.r.items():
                if k != eng:
                    add(k, v)
        return deps

    def _emit_waits(self, eng, deps):
        e = self.engs[eng]
        seen = self.seen[eng]
        for k, v in deps.items():
            if seen.get(k, 0) >= v:
                continue
            assert v <= self.cnt[k], "wait on %s=%d never reached (issued %d)" % (k, v, self.cnt[k])
            seen[k] = v
            if k in self.engs:
                if self.dry:
                    self.waited[k].add(v)
                else:
                    e.wait_ge(self.sems[k], self.rank_of[k][v])
            elif not self.dry:
                e.wait_ge(self.sems[k], v)

    def op(self, eng, fn, reads=(), writes=(), inc=True):
        self._emit_waits(eng, self._deps(eng, reads, writes))
        self.cnt[eng] += 1
        idx = self.cnt[eng]
        self.nins += 1
        if not self.dry:
            ins = fn(self.engs[eng])
            if idx in self.needed[eng]:
                self.rank[eng] += 1
                self.rank_of[eng][idx] = self.rank[eng]
                ins.then_inc(self.sems[eng], 1)
        tok = (eng, idx)
        for b in reads:
            if b.r.get(eng, 0) < idx:
                b.r[eng] = idx
        for b in writes:
            b.w = tok
            b.r = {}
        return tok

    def dma(self, q, out, in_, sem_buf, reads=(), writes=()):
        self._emit_waits(q, self._deps("__dma__", reads, writes))
        if sem_buf.dsem is None:
            sem_buf.dsem = self.new_dsem()
        k = sem_buf.dsem
        self.cnt[k] += 16
        self.nins += 1
        if not self.dry:
            ins = self.engs[q].dma_start(out=out, in_=in_)
            ins.then_inc(self.sems[k], 16)
        tok = (k, self.cnt[k])
        for b in reads:
            if b.r.get(k, 0) < tok[1]:
                b.r[k] = tok[1]
        for b in writes:
            b.w = tok
            b.r = {}
        return tok

    def barrier(self):
        for e in self.engs:
            self._emit_waits(e, {k: v for k, v in self.cnt.items() if k != e and v > 0})

    def finish(self):
        self.barrier()
        self.es.close()
        return self.nc


OFF_QKV, OFF_A, OFF_B, OFF_MQ, OFF_MK, OFF_MV, OFF_MI, OFF_MF, OFF_Z, OFF_MO, OFF_GG, OFF_GM, OFF_END = (
    0, 3072, 3088, 3104, 3616, 4128, 5152, 5160, 5168, 6192, 7216, 8240, 9264)
NCH = 66
OWN_T = 4224


def _col(v, n=128):
    v = np.asarray(v, np.float32).reshape(-1, n)
    return np.ascontiguousarray(v.T)


def _rep(v):
    v = np.asarray(v, np.float32).reshape(1, -1)
    return np.ascontiguousarray(np.repeat(v, 128, axis=0))


def _swapdir(a, flip):
    if not flip:
        return a
    h = a.shape[-1] // 2
    return np.concatenate([a[..., h:], a[..., :h]], axis=-1)


def prep_core(inp, core):
    b = core // 2
    flip = core % 2
    f32 = np.float32
    x = inp["x"][b]
    ctx = inp["ctx"][b]
    if flip:
        x = x[::-1]
        ctx = ctx[::-1]
    w_in = inp["w_in"][0]
    m = {}
    m["x"] = np.ascontiguousarray(x, dtype=f32)
    m["ctx"] = np.ascontiguousarray(ctx, dtype=f32)
    m["c_col"] = _col(inp["c"][b])
    m["cc_col"] = _col(inp["c_ctx"])
    m["w_ada"] = np.ascontiguousarray(inp["w_ada"][0], dtype=f32)
    b_ada = inp["b_ada"][0]
    m["b_ada_col"] = _col(b_ada)
    m["b_ada_g"] = np.ascontiguousarray(np.concatenate([_rep(b_ada[2048:3072]), _rep(b_ada[5120:6144])], axis=1))
    m["n1_col"] = _col(inp["norm1_w"][0])
    m["n2_col"] = _col(inp["norm2_w"][0])
    m["w_qkv"] = np.ascontiguousarray(w_in[:, OFF_QKV:OFF_A])
    wg = np.concatenate([_swapdir(w_in[:, OFF_A:OFF_B], flip), _swapdir(w_in[:, OFF_B:OFF_MQ], flip),
                         _swapdir(w_in[:, OFF_MI:OFF_MF], flip), _swapdir(w_in[:, OFF_MF:OFF_Z], flip)], axis=1)
    m["w_gate"] = np.ascontiguousarray(wg)
    m["w_ml"] = np.ascontiguousarray(w_in[:, OFF_MQ:OFF_MI])
    m["w_o"] = np.ascontiguousarray(w_in[:, OFF_Z:OFF_END])
    gp = np.concatenate([_swapdir(inp["gdn_dt_bias"][0].reshape(-1), flip), _swapdir(inp["gdn_a_log"][0].reshape(-1), flip),
                         _swapdir(inp["ml_igate_b"][0].reshape(-1), flip), _swapdir(inp["ml_fgate_b"][0].reshape(-1), flip)])
    m["gate_p"] = _rep(gp)
    gc = inp["gdn_conv"][0]
    if flip:
        gc = gc[::-1]
    m["gdn_cw"] = np.ascontiguousarray(gc.T.reshape(24, 128, 3).transpose(1, 0, 2), dtype=f32)
    fc = inp["ffn_conv"][0]
    if flip:
        fc = fc[::-1, ::-1]
    m["ffn_cw"] = np.ascontiguousarray(fc.reshape(9, 44, 128).transpose(2, 1, 0), dtype=f32)
    m["gnw_bc"] = _rep(np.tile(inp["gdn_norm_w"][0], 8))
    m["mnw_bc"] = _rep(inp["ml_norm_w"][0].reshape(-1))
    m["now_bc"] = _rep(inp["norm_out_w"])
    m["w_bg"] = np.ascontiguousarray(inp["w_branch_gdn"][0], dtype=f32)
    m["w_bm"] = np.ascontiguousarray(inp["w_branch_ml"][0], dtype=f32)
    m["w_out"] = np.ascontiguousarray(inp["w_out"][0], dtype=f32)
    m["w_up"] = np.ascontiguousarray(inp["w_up"][0], dtype=f32)
    m["w_down"] = np.ascontiguousarray(inp["w_down"][0], dtype=f32)
    m["smask"] = make_smask()
    return m


def make_smask():
    idx = np.arange(128)
    i = idx[None, :]
    j = idx[:, None]
    out = np.zeros((128, 14, 128), np.float32)
    for lev in range(7):
        b = 1 << lev
        same = (i // (2 * b)) == (j // (2 * b))
        f = same & ((i % (2 * b)) < b) & ((j % (2 * b)) >= b)
        g = same & ((j % (2 * b)) < b) & ((i % (2 * b)) >= b)
        out[:, lev, :] = np.where(f, -1.0, 0.0) + np.eye(128)
        out[:, 7 + lev, :] = np.where(g, -1.0, 0.0) + np.eye(128)
    return out


IN_SHAPES = {
    "x": [T, D], "ctx": [TC, D], "c_col": [128, 8], "cc_col": [128, 8], "w_ada": [D, 6144],
    "b_ada_col": [128, 48], "b_ada_g": [128, 2048], "n1_col": [128, 8], "n2_col": [128, 8],
    "w_qkv": [D, 3072], "w_gate": [D, 48], "w_ml": [D, 2048], "w_o": [D, 4096], "gate_p": [128, 48],
    "gdn_cw": [128, 24, 3], "ffn_cw": [128, 44, 9], "gnw_bc": [128, 1024], "mnw_bc": [128, 1024],
    "now_bc": [128, 1024], "w_bg": [D, D], "w_bm": [D, D], "w_out": [D, D], "w_up": [D, 5632], "w_down": [2816, D],
    "smask": [128, 14, 128],
}


class Ring:
    def __init__(self, B, stack, name, n, shape, dt, psum=False):
        self.slots = []
        for i in range(n):
            t = (B.ps if psum else B.sb)(stack, "%s%d" % (name, i), shape, dt)
            self.slots.append((t, Buf("%s%d" % (name, i))))
        self.i = 0

    def next(self):
        s = self.slots[self.i % len(self.slots)]
        self.i += 1
        return s


class Prog:
    def __init__(self, debug=None, needed=None):
        self.debug = debug or {}
        self.B = Builder(needed)
        self.nc = self.B.nc
        self.top = ExitStack()
        self.inp = {}
        for k, shp in IN_SHAPES.items():
            self.inp[k] = self.nc.dram_tensor(k, list(shp), F32, kind="ExternalInput").ap()
        self.out = self.nc.dram_tensor("out", [4096, D], F32, kind="ExternalOutput").ap()
        dk = "ExternalOutput" if self.debug.get("scratch_out") else "Internal"
        B = self.B
        self.KT = B.dram("KT", [NCH, 128, 8, 128], BF16, dk)
        self.QT = B.dram("QT", [NCH, 128, 8, 128], BF16, dk)
        self.VG = B.dram("VG", [NCH, 128, 1024], BF16, dk)
        self.MQT = B.dram("MQT", [NCH, 128, 4, 128], BF16, dk)
        self.MKT = B.dram("MKT", [NCH, 128, 4, 128], BF16, dk)
        self.MV = B.dram("MV", [NCH, 128, 1024], BF16, dk)
        self.GT = B.dram("GT", [NCH, 128, 48], F32, dk)
        self.OF = B.dram("OF", [33, 128, 1024], F32, dk)
        self.OB = B.dram("OB", [33, 128, 1024], F32, dk)
        self.HF = B.dram("HF", [33, 128, 1024], F32, dk)
        self.HB = B.dram("HB", [33, 128, 1024], F32, dk)
        self.X1 = B.dram("X1", [64 + OWN_T, D], F32, dk)
        self.consts()

    def consts(self):
        B, st = self.B, self.top
        self.ident_f = B.sb(st, "ident_f", [128, 128], F32)
        self.ident_b = B.sb(st, "ident_b", [128, 128], BF16)
        self.ones_f = B.sb(st, "ones_f", [128, 128], F32)
        self.ones_b = B.sb(st, "ones_b", [128, 128], BF16)
        self.nhalf = B.sb(st, "nhalf", [128, 512], F32)
        self.cb = Buf("consts")
        cb = self.cb
        B.op("pool", lambda e: e.memset(self.ones_f[:], 1.0), writes=[cb])
        B.op("pool", lambda e: e.memset(self.ones_b[:], 1.0), writes=[cb])
        B.op("pool", lambda e: e.memset(self.nhalf[:], -0.5), writes=[cb])
        B.op("pool", lambda e: e.memset(self.ident_f[:], 1.0), writes=[cb])
        B.op("pool", lambda e: e.affine_select(self.ident_f[:], self.ident_f[:], pattern=[[-1, 128]], compare_op=ALU.is_equal,
                                               fill=0.0, base=0, channel_multiplier=1), reads=[cb], writes=[cb])
        B.op("dve", lambda e: e.tensor_copy(out=self.ident_b[:], in_=self.ident_f[:]), reads=[cb], writes=[cb])
        self.modc = B.sb(st, "modc", [128, 6, 8], F32)
        self.bmod = Buf("modc")
        self.gate_bc = B.sb(st, "gate_bc", [128, 2, 1024], F32)
        self.bgate = Buf("gate_bc")

    def mask(self, stack, name, cmp_pat, dt=F32, val=1.0, fill=0.0):
        B = self.B
        base, cm, step, cmp = cmp_pat
        t = B.sb(stack, name, [128, 128], dt)
        tf = t
        if dt != F32:
            tf = B.sb(stack, name + "_f", [128, 128], F32)
        b = Buf(name)
        B.op("pool", lambda e: e.memset(tf[:], val), writes=[b])
        B.op("pool", lambda e: e.affine_select(tf[:], tf[:], pattern=[[step, 128]], compare_op=cmp, fill=fill,
                                               base=base, channel_multiplier=cm), reads=[b], writes=[b])
        if dt != F32:
            B.op("dve", lambda e: e.tensor_copy(out=t[:], in_=tf[:]), reads=[b], writes=[b])
        return t, b

    def phase0(self):
        B, nc, inp = self.B, self.nc, self.inp
        st = ExitStack()
        sc = B.sb(st, "p0_sc", [128, 16], F32)
        bsc = Buf("p0_sc")
        scb = B.sb(st, "p0_scb", [128, 8, 128], F32)
        bscb = Buf("p0_scb")
        bcol = B.sb(st, "p0_bcol", [128, 48], F32)
        n12 = B.sb(st, "p0_n12", [128, 16], F32)
        bg = B.sb(st, "p0_bg", [128, 2048], F32)
        bsm = Buf("p0_small")
        B.dma("sp", sc[:, 0:8], inp["c_col"][:, :], bsc, writes=[bsc])
        B.dma("sp", sc[:, 8:16], inp["cc_col"][:, :], bsc, writes=[bsc])
        B.dma("sp", bcol[:], inp["b_ada_col"][:, :], bsm, writes=[bsm])
        B.dma("sp", n12[:, 0:8], inp["n1_col"][:, :], bsm, writes=[bsm])
        B.dma("sp", n12[:, 8:16], inp["n2_col"][:, :], bsm, writes=[bsm])
        B.dma("sp", bg[:], inp["b_ada_g"][:, :], bsm, writes=[bsm])
        B.op("act", lambda e: e.activation(out=sc[:], in_=sc[:], func=AF.Silu), reads=[bsc], writes=[bsc])
        for k in range(8):
            B.op("dve", lambda e, k=k: e.tensor_scalar(out=scb[:, k, :], in0=self.ones_f[:], scalar1=sc[:, k:k + 1], scalar2=None,
                                                       op0=ALU.mult), reads=[bsc, self.cb], writes=[bscb])
        wring = Ring(B, st, "p0_w", 2, [128, 8, 512], F32)
        pcol = B.ps(st, "p0_pcol", [128, 64], F32)
        bpcol = Buf("p0_pcol")
        prow = Ring(B, st, "p0_prow", 2, [128, 512], F32, psum=True)
        wv = inp["w_ada"].rearrange("(k p) n -> p k n", p=128)
        xslot = {0: 0, 1: 1, 3: 2, 4: 3}
        for nb in range(12):
            v, half = nb // 2, nb % 2
            w, bw = wring.next()
            B.dma("sp", w[:], wv[:, :, nb * 512:(nb + 1) * 512], bw, writes=[bw])
            if v in (2, 5):
                p, bp = prow.next()
                for k in range(8):
                    B.op("pe", lambda e, k=k, p=p, w=w: e.matmul(p[:], lhsT=scb[:, k, :], rhs=w[:, k, :], start=(k == 0), stop=(k == 7)),
                         reads=[bscb, bw], writes=[bp], inc=(k == 7))
                gi = 0 if v == 2 else 1
                B.op("dve", lambda e, p=p, gi=gi, half=half: e.tensor_tensor(
                    out=self.gate_bc[:, gi, half * 512:(half + 1) * 512], in0=p[:], in1=bg[:, gi * 1024 + half * 512: gi * 1024 + (half + 1) * 512],
                    op=ALU.add), reads=[bp, bsm], writes=[self.bgate])
            else:
                for cc in range(4):
                    col = xslot[v] * 8 + half * 4 + cc
                    for k in range(8):
                        B.op("pe", lambda e, k=k, w=w, cc=cc, col=col: e.matmul(pcol[:, col:col + 1], lhsT=w[:, k, cc * 128:(cc + 1) * 128],
                                                                                 rhs=sc[:, k:k + 1], start=(k == 0), stop=(k == 7)),
                             reads=[bw, bsc], writes=[bpcol], inc=(k == 7))
                    if v in (0, 1):
                        col2 = 32 + v * 8 + half * 4 + cc
                        for k in range(8):
                            B.op("pe", lambda e, k=k, w=w, cc=cc, col2=col2: e.matmul(pcol[:, col2:col2 + 1], lhsT=w[:, k, cc * 128:(cc + 1) * 128],
                                                                                       rhs=sc[:, 8 + k:9 + k], start=(k == 0), stop=(k == 7)),
                                 reads=[bw, bsc], writes=[bpcol], inc=(k == 7))
        mc = B.sb(st, "p0_mc", [128, 6, 8], F32)
        bmc = Buf("p0_mc")
        for i, v in enumerate((0, 1, 3, 4)):
            B.op("dve", lambda e, i=i, v=v: e.tensor_tensor(out=mc[:, i, :], in0=pcol[:, i * 8:(i + 1) * 8], in1=bcol[:, v * 8:(v + 1) * 8], op=ALU.add),
                 reads=[bpcol, bsm], writes=[bmc])
        for i, v in enumerate((0, 1)):
            B.op("dve", lambda e, i=i, v=v: e.tensor_tensor(out=mc[:, 4 + i, :], in0=pcol[:, 32 + i * 8:32 + (i + 1) * 8], in1=bcol[:, v * 8:(v + 1) * 8],
                                                            op=ALU.add), reads=[bpcol, bsm], writes=[bmc])
        md = self.modc
        for dst, (sci, shi, nw) in {0: (1, 0, 0), 2: (5, 4, 0), 4: (3, 2, 1)}.items():
            B.op("dve", lambda e, dst=dst, sci=sci, nw=nw: e.scalar_tensor_tensor(out=md[:, dst, :], in0=mc[:, sci, :], scalar=1.0, in1=n12[:, nw * 8:(nw + 1) * 8],
                                                                                  op0=ALU.add, op1=ALU.mult), reads=[bmc, bsm], writes=[self.bmod])
            B.op("dve", lambda e, dst=dst, shi=shi: e.tensor_copy(out=md[:, dst + 1, :], in_=mc[:, shi, :]), reads=[bmc], writes=[self.bmod])
        B.barrier()
        st.close()

    @staticmethod
    def run_pipeline(makers, depth):
        active = []
        it = iter(makers)
        exhausted = False
        while True:
            for g in list(active):
                try:
                    next(g)
                except StopIteration:
                    active.remove(g)
            if not exhausted and len(active) < depth:
                try:
                    g = next(it)()
                    try:
                        next(g)
                        active.append(g)
                    except StopIteration:
                        pass
                except StopIteration:
                    exhausted = True
            if exhausted and not active:
                break

    def load_w_bf16(self, stack, name, ap, kchunks, ncols, nsplit=4):
        B = self.B
        t = B.sb(stack, name, [128, kchunks, ncols], BF16)
        b = Buf(name)
        v = ap.rearrange("(k p) n -> p k n", p=128)
        step = (ncols + nsplit - 1) // nsplit
        for i in range(0, ncols, step):
            j = min(ncols, i + step)
            B.dma("pool", t[:, :, i:j], v[:, :, i:j], b, writes=[b])
        return t, b

    def norm_transpose(self, xt, bxts, ns, xn, bxn, sq, bsq, junk, bjunk, ptr_ring, hxT, bhx, col0, ai, npart=128):
        B = self.B
        for s in range(ns):
            B.op("act", lambda e, s=s: e.activation(out=xn[0:npart, s, :], in_=xt[0:npart, s, :], func=AF.Square, accum_out=sq[0:npart, s:s + 1]),
                 reads=[bxts[s]], writes=[bxn, bsq])
        B.op("dve", lambda e: e.tensor_scalar(out=sq[0:npart, 8:8 + ns], in0=sq[0:npart, 0:ns], scalar1=float(D * EPS), scalar2=None, op0=ALU.add),
             reads=[bsq], writes=[bsq])
        B.op("pool", lambda e: e.tensor_tensor(out=sq[0:npart, 16:16 + ns], in0=sq[0:npart, 8:8 + ns], in1=self.nhalf[0:npart, 0:ns], op=ALU.pow),
             reads=[bsq, self.cb], writes=[bsq])
        for s in range(ns):
            B.op("dve", lambda e, s=s: e.tensor_scalar(out=xn[0:npart, s, :], in0=xt[0:npart, s, :], scalar1=sq[0:npart, 16 + s:17 + s], scalar2=32.0,
                                                       op0=ALU.mult, op1=ALU.mult), reads=[bxts[s], bsq], writes=[bxn])
        for k in range(KD):
            p, bp = ptr_ring.next()
            pb = p[:].bitcast(BF16)
            for s in range(ns):
                B.op("pe", lambda e, s=s, k=k, pb=pb: e.transpose(pb[:, s * npart:(s + 1) * npart], xn[0:npart, s, k * 128:(k + 1) * 128],
                                                                  self.ident_b[0:npart, 0:npart]),
                     reads=[bxn, self.cb], writes=[bp], inc=(s == ns - 1))
            B.op("act", lambda e, k=k, pb=pb: e.activation(out=hxT[:, k, col0:col0 + ns * npart], in_=pb[:, 0:ns * npart], func=AF.Identity,
                                                           scale=self.modc[:, ai, k:k + 1], bias=self.modc[:, ai + 1, k:k + 1]),
                 reads=[bp, self.bmod], writes=[bhx])

    def phaseA(self):
        B, nc, inp = self.B, self.nc, self.inp
        st = ExitStack()
        wqkv, bwqkv = self.load_w_bf16(st, "a_wqkv", inp["w_qkv"], 8, 3072, 6)
        wml, bwml = self.load_w_bf16(st, "a_wml", inp["w_ml"], 8, 2048, 4)
        wgt, bwgt = self.load_w_bf16(st, "a_wgt", inp["w_gate"], 8, 48, 1)
        cw = B.sb(st, "a_cw", [128, 24, 3], F32)
        gp = B.sb(st, "a_gp", [128, 48], F32)
        bsm = Buf("a_small")
        B.dma("sp", cw[:], inp["gdn_cw"][:, :, :], bsm, writes=[bsm])
        B.dma("sp", gp[:], inp["gate_p"][:, :], bsm, writes=[bsm])
        B.op("act", lambda e: e.activation(out=gp[:, 16:32], in_=gp[:, 16:32], func=AF.Exp), reads=[bsm], writes=[bsm])
        B.op("dve", lambda e: e.tensor_scalar(out=gp[:, 16:32], in0=gp[:, 16:32], scalar1=-1.0, scalar2=None, op0=ALU.mult), reads=[bsm], writes=[bsm])
        xt = B.sb(st, "a_x", [128, 4, 1024], F32)
        bxts = [Buf("a_x%d" % i) for i in range(4)]
        xh = B.sb(st, "a_xh", [2, 1, 1024], F32); bxh = Buf("a_xh")
        xn = B.sb(st, "a_xn", [128, 4, 1024], BF16); bxn = Buf("a_xn")
        xnh = B.sb(st, "a_xnh", [2, 1, 1024], BF16); bxnh = Buf("a_xnh")
        sq = B.sb(st, "a_sq", [128, 24], F32); bsq = Buf("a_sq")
        sqh = B.sb(st, "a_sqh", [128, 24], F32); bsqh = Buf("a_sqh")
        junk = None; bjunk = None
        hxT = B.sb(st, "a_hxT", [128, 8, 514], BF16); bhx = Buf("a_hxT")
        hxh = B.sb(st, "a_hxh", [128, 8, 2], BF16); bhxh = Buf("a_hxh")
        ptr = Ring(B, st, "a_ptr", 2, [128, 512], F32, psum=True)
        pz = Ring(B, st, "a_pz", 4, [128, 512], F32, psum=True)
        pmisc = B.ps(st, "a_pmisc", [128, 512], F32)
        pzh = pmisc[:, 0:64]; bpzh = Buf("a_pzh")
        pn = ptr
        zb = Ring(B, st, "a_zb", 4, [128, 514], F32)
        y1 = Ring(B, st, "a_y1", 4, [128, 512], F32)
        sqb = Ring(B, st, "a_sqb", 2, [128, 512], BF16)
        skeep = B.sb(st, "a_skeep", [128, 8, 512], BF16)
        bskeep = [Buf("a_skeep%d" % i) for i in range(8)]
        rnr = B.sb(st, "a_rnr", [8, 512], F32); brnr = Buf("a_rnr")
        ind = B.sb(st, "a_ind", [128, 8, 8], BF16); bind = Buf("a_ind")
        selr = B.sb(st, "a_selr", [8, 8, 128], F32); bselr = Buf("a_selr")
        B.op("pool", lambda e: e.memset(ind[:], 0.0), writes=[bind])
        for jj in range(8):
            B.op("pool", lambda e, jj=jj: e.memset(ind[:, jj, jj:jj + 1], 1.0), writes=[bind])
            B.op("dve", lambda e, jj=jj: e.tensor_copy(out=selr[:, jj, :], in_=self.ident_f[0:8, jj:jj + 1].to_broadcast([8, 128])), reads=[self.cb], writes=[bselr])
        pss = B.ps(st, "a_pss", [128, 512], F32); bpss = Buf("a_pss")
        kst = B.sb(st, "a_kst", [128, 4, 8, 128], BF16); bkst = Buf("a_kst")
        qst = B.sb(st, "a_qst", [128, 4, 8, 128], BF16); bqst = Buf("a_qst")
        vT = B.sb(st, "a_vT", [128, 8, 512], BF16); bvT = Buf("a_vT")
        vst = Ring(B, st, "a_vst", 1, [128, 4, 1024], BF16)
        mqst = B.sb(st, "a_mqst", [128, 4, 4, 128], BF16); bmqst = Buf("a_mqst")
        mkst = B.sb(st, "a_mkst", [128, 4, 4, 128], BF16); bmkst = Buf("a_mkst")
        graw = B.sb(st, "a_graw", [128, 4, 48], F32); bgraw = Buf("a_graw")
        gwk = B.sb(st, "a_gwk", [128, 4, 48], F32); bgwk = Buf("a_gwk")
        gsb = Ring(B, st, "a_gsb", 2, [128, 4, 48], F32)
        pg = pmisc[:, 64:256].rearrange("p (s g) -> p s g", g=48); bpg = Buf("a_pg")
        dkr = float(128 ** -0.5)

        tiles = [(inp["ctx"], 0, 2, 0, False, False)]
        for i in range(16):
            tiles.append((inp["x"], i * 512, 4, 2 + 4 * i, i > 0, i < 15))
        if self.debug.get("a_tiles"):
            tiles = tiles[: self.debug["a_tiles"]]
        for (src, t0, ns, c0, hl, hr) in tiles:
            n = ns * 128
            ai = 2 if src is inp["ctx"] else 0
            for s in range(ns):
                B.dma("sp", xt[:, s, :], src[t0 + s * 128:t0 + (s + 1) * 128, :], bxts[s], writes=[bxts[s]])
            tl = t0 - 1 if hl else t0
            tr = t0 + n if hr else t0
            B.dma("sp", xh[0:1, 0, :], src[tl:tl + 1, :], bxh, writes=[bxh])
            B.dma("sp", xh[1:2, 0, :], src[tr:tr + 1, :], bxh, writes=[bxh])
            self.norm_transpose(xt, bxts, ns, xn, bxn, sq, bsq, junk, bjunk, ptr, hxT, bhx, 1, ai)
            self.norm_transpose(xh, [bxh], 1, xnh, bxnh, sqh, bsqh, junk, bjunk, ptr, hxh, bhxh, 0, ai, npart=2)
            def chunk_gen(j, kind, jj):
                p, bp = pz.next()
                z, bz = zb.next()
                a1, ba1 = y1.next()
                for k in range(8):
                    B.op("pe", lambda e, k=k: e.matmul(p[:, 0:n], lhsT=wqkv[:, k, j * 128:(j + 1) * 128], rhs=hxT[:, k, 1:1 + n],
                                                         start=(k == 0), stop=(k == 7)), reads=[bwqkv, bhx], writes=[bp], inc=(k == 7))
                for k in range(8):
                    B.op("pe", lambda e, k=k: e.matmul(pzh[:, 2 * j:2 * j + 2], lhsT=wqkv[:, k, j * 128:(j + 1) * 128], rhs=hxh[:, k, :],
                                                         start=(k == 0), stop=(k == 7)), reads=[bwqkv, bhxh], writes=[bpzh], inc=(k == 7))
                yield
                B.op("act", lambda e: e.activation(out=z[:, 1:1 + n], in_=p[:, 0:n], func=AF.Identity), reads=[bp], writes=[bz])
                B.op("act", lambda e: e.activation(out=z[:, 0:1], in_=pzh[:, 2 * j:2 * j + 1], func=AF.Identity), reads=[bpzh], writes=[bz])
                B.op("act", lambda e: e.activation(out=z[:, n + 1:n + 2], in_=pzh[:, 2 * j + 1:2 * j + 2], func=AF.Identity), reads=[bpzh], writes=[bz])
                if not hl:
                    B.op("pool", lambda e: e.memset(z[:, 0:1], 0.0), writes=[bz])
                if not hr:
                    B.op("pool", lambda e: e.memset(z[:, n + 1:n + 2], 0.0), writes=[bz])
                yield
                B.op("dve", lambda e: e.tensor_scalar(out=a1[:, 0:n], in0=z[:, 1:1 + n], scalar1=cw[:, j, 1:2], scalar2=None, op0=ALU.mult),
                     reads=[bz, bsm], writes=[ba1])
                B.op("dve", lambda e: e.scalar_tensor_tensor(out=a1[:, 0:n], in0=z[:, 0:n], scalar=cw[:, j, 0:1], in1=a1[:, 0:n],
                                                            op0=ALU.mult, op1=ALU.add), reads=[bz, bsm, ba1], writes=[ba1])
                B.op("dve", lambda e: e.scalar_tensor_tensor(out=a1[:, 0:n], in0=z[:, 2:2 + n], scalar=cw[:, j, 2:3], in1=a1[:, 0:n],
                                                            op0=ALU.mult, op1=ALU.add), reads=[bz, bsm, ba1], writes=[ba1])
                yield
                if kind == "v":
                    B.op("act", lambda e: e.activation(out=vT[:, jj, 0:n], in_=a1[:, 0:n], func=AF.Silu), reads=[ba1], writes=[bvT])
                else:
                    B.op("act", lambda e: e.activation(out=skeep[:, jj, 0:n], in_=a1[:, 0:n], func=AF.Silu), reads=[ba1], writes=[bskeep[jj]])
                    q2, bq2 = sqb.next()
                    B.op("pool", lambda e: e.tensor_tensor(out=q2[:, 0:n], in0=skeep[:, jj, 0:n], in1=skeep[:, jj, 0:n], op=ALU.mult),
                         reads=[bskeep[jj]], writes=[bq2])
                    B.op("pe", lambda e: e.matmul(pss[0:8, 0:n], lhsT=ind[:, jj, :], rhs=q2[:, 0:n], start=(jj == 0), stop=(jj == 7)),
                         reads=[bq2, bind], writes=[bpss])

            for half in range(2):
                self.run_pipeline([(lambda jj=jj: chunk_gen(half * 8 + jj, "qk", jj)) for jj in range(8)], 4)
                B.op("act", lambda e: e.activation(out=rnr[:, 0:n], in_=pss[0:8, 0:n], func=AF.Ln, bias=float(EPS)), reads=[bpss], writes=[brnr])
                B.op("act", lambda e: e.activation(out=rnr[:, 0:n], in_=rnr[:, 0:n], func=AF.Exp, scale=-0.5), reads=[brnr], writes=[brnr])
                for jj in range(8):
                    pp, bpp = pn.next()
                    B.op("pe", lambda e, pp=pp, jj=jj: e.matmul(pp[:, 0:n], lhsT=selr[:, jj, :], rhs=rnr[:, 0:n], start=True, stop=True),
                         reads=[brnr, bselr], writes=[bpp])
                    if half == 0:
                        B.op("dve", lambda e, pp=pp, jj=jj: e.scalar_tensor_tensor(
                            out=qst[:, 0:ns, jj, :], in0=skeep[:, jj, 0:n].rearrange("p (s t) -> p s t", t=128), scalar=dkr,
                            in1=pp[:, 0:n].rearrange("p (s t) -> p s t", t=128), op0=ALU.mult, op1=ALU.mult), reads=[bskeep[jj], bpp], writes=[bqst])
                    else:
                        B.op("dve", lambda e, pp=pp, jj=jj: e.tensor_tensor(
                            out=kst[:, 0:ns, jj, :], in0=skeep[:, jj, 0:n].rearrange("p (s t) -> p s t", t=128),
                            in1=pp[:, 0:n].rearrange("p (s t) -> p s t", t=128), op=ALU.mult), reads=[bskeep[jj], bpp], writes=[bkst])
            for s in range(ns):
                for k in range(8):
                    B.op("pe", lambda e, k=k, s=s: e.matmul(pg[:, s, :], lhsT=hxT[:, k, 1 + s * 128:1 + (s + 1) * 128], rhs=wgt[:, k, :],
                                                             start=(k == 0), stop=(k == 7)), reads=[bhx, bwgt], writes=[bpg], inc=(k == 7))
            g, bg_ = gsb.next()
            self.gate_math(pg, bpg, graw, bgraw, gwk, bgwk, g, bg_, gp, bsm, ns)
            B.dma("sp", self.GT[c0:c0 + ns].rearrange("c t g -> t c g"), g[:, 0:ns, :], bg_, reads=[bg_])
            self.run_pipeline([(lambda jj=jj: chunk_gen(16 + jj, "v", jj)) for jj in range(8)], 4)
            self.v_transposes(vT, bvT, ns, vst, pn, self.VG, c0)
            B.dma("sp", self.KT[c0:c0 + ns].rearrange("c d h t -> d c h t"), kst[:, 0:ns], bkst, reads=[bkst])
            B.dma("sp", self.QT[c0:c0 + ns].rearrange("c d h t -> d c h t"), qst[:, 0:ns], bqst, reads=[bqst])
            for j in range(16):
                p, bp = pz.next()
                for k in range(8):
                    B.op("pe", lambda e, k=k, p=p, j=j: e.matmul(p[:, 0:n], lhsT=wml[:, k, j * 128:(j + 1) * 128], rhs=hxT[:, k, 1:1 + n],
                                                                  start=(k == 0), stop=(k == 7)), reads=[bwml, bhx], writes=[bp], inc=(k == 7))
                if j < 4:
                    B.op("act", lambda e, p=p, j=j: e.activation(out=mqst[:, 0:ns, j, :], in_=p[:, 0:n].rearrange("p (s t) -> p s t", t=128),
                                                                  func=AF.Identity, scale=dkr), reads=[bp], writes=[bmqst])
                elif j < 8:
                    B.op("act", lambda e, p=p, j=j: e.activation(out=mkst[:, 0:ns, j - 4, :], in_=p[:, 0:n].rearrange("p (s t) -> p s t", t=128),
                                                                  func=AF.Identity), reads=[bp], writes=[bmkst])
                else:
                    B.op("act", lambda e, p=p, j=j: e.activation(out=vT[:, j - 8, 0:n], in_=p[:, 0:n], func=AF.Identity), reads=[bp], writes=[bvT])
            self.v_transposes(vT, bvT, ns, vst, pn, self.MV, c0)
            B.dma("sp", self.MQT[c0:c0 + ns].rearrange("c d h t -> d c h t"), mqst[:, 0:ns], bmqst, reads=[bmqst])
            B.dma("sp", self.MKT[c0:c0 + ns].rearrange("c d h t -> d c h t"), mkst[:, 0:ns], bmkst, reads=[bmkst])
        B.barrier()
        st.close()

    def v_transposes(self, vT, bvT, ns, vst, pn, dst, c0):
        B = self.B
        v, bv = vst.next()
        for s in range(ns):
            pp, bpp = pn.next()
            ppb = pp[:].bitcast(BF16)
            for h in range(8):
                B.op("pe", lambda e, s=s, h=h, ppb=ppb: e.transpose(ppb[:, h * 128:(h + 1) * 128], vT[:, h, s * 128:(s + 1) * 128], self.ident_b[:]),
                     reads=[bvT, self.cb], writes=[bpp], inc=(h == 7))
            B.op("act", lambda e, s=s, ppb=ppb, v=v: e.activation(out=v[:, s, :], in_=ppb[:, 0:1024], func=AF.Identity), reads=[bpp], writes=[bv])
        B.dma("sp", dst[c0:c0 + ns].rearrange("c t e -> t c e"), v[:, 0:ns, :], bv, reads=[bv])

    def gate_math(self, pg, bpg, graw, bgraw, wk, bwk, g, bg_, gp, bgp, ns):
        B = self.B
        S = slice(0, ns)

        def bc(lo, hi):
            return gp[:, lo:hi].unsqueeze(1).to_broadcast([128, ns, hi - lo])

        B.op("act", lambda e: e.activation(out=graw[:, S, :], in_=pg[:, S, :], func=AF.Identity), reads=[bpg], writes=[bgraw])
        B.op("dve", lambda e: e.tensor_tensor(out=wk[:, S, 0:16], in0=graw[:, S, 0:16], in1=bc(0, 16), op=ALU.add), reads=[bgraw, bgp], writes=[bwk])
        B.op("act", lambda e: e.activation(out=wk[:, S, 0:16], in_=wk[:, S, 0:16], func=AF.Exp), reads=[bwk], writes=[bwk])
        B.op("act", lambda e: e.activation(out=wk[:, S, 0:16], in_=wk[:, S, 0:16], func=AF.Ln, bias=1.0), reads=[bwk], writes=[bwk])
        B.op("dve", lambda e: e.tensor_tensor(out=g[:, S, 0:16], in0=wk[:, S, 0:16], in1=bc(16, 32), op=ALU.mult), reads=[bwk, bgp], writes=[bg_])
        B.op("act", lambda e: e.activation(out=wk[:, S, 16:32], in_=graw[:, S, 16:32], func=AF.Exp, scale=-1.0), reads=[bgraw], writes=[bwk])
        B.op("dve", lambda e: e.tensor_scalar(out=wk[:, S, 16:32], in0=wk[:, S, 16:32], scalar1=1.0, scalar2=None, op0=ALU.add), reads=[bwk], writes=[bwk])
        B.op("dve", lambda e: e.reciprocal(out=g[:, S, 16:32], in_=wk[:, S, 16:32]), reads=[bwk], writes=[bg_])
        B.op("dve", lambda e: e.tensor_tensor(out=wk[:, S, 32:48], in0=graw[:, S, 32:48], in1=bc(32, 48), op=ALU.add), reads=[bgraw, bgp], writes=[bwk])
        B.op("act", lambda e: e.activation(out=wk[:, S, 32:48], in_=wk[:, S, 32:48], func=AF.Exp, scale=float(2.0 / 15.0)), reads=[bwk], writes=[bwk])
        B.op("dve", lambda e: e.tensor_scalar(out=wk[:, S, 32:48], in0=wk[:, S, 32:48], scalar1=1.0, scalar2=None, op0=ALU.add), reads=[bwk], writes=[bwk])
        B.op("dve", lambda e: e.reciprocal(out=wk[:, S, 32:48], in_=wk[:, S, 32:48]), reads=[bwk], writes=[bwk])
        B.op("dve", lambda e: e.tensor_scalar(out=g[:, S, 32:48], in0=wk[:, S, 32:48], scalar1=-30.0, scalar2=15.0, op0=ALU.mult, op1=ALU.add),
             reads=[bwk], writes=[bg_])
        B.op("act", lambda e: e.activation(out=wk[:, S, 40:48], in_=g[:, S, 40:48], func=AF.Exp, scale=-1.0), reads=[bg_], writes=[bwk])
        B.op("act", lambda e: e.activation(out=wk[:, S, 40:48], in_=wk[:, S, 40:48], func=AF.Ln, bias=1.0), reads=[bwk], writes=[bwk])
        B.op("dve", lambda e: e.tensor_scalar(out=g[:, S, 40:48], in0=wk[:, S, 40:48], scalar1=-1.0, scalar2=None, op0=ALU.mult), reads=[bwk], writes=[bg_])

    def phaseB(self):
        B, nc, inp = self.B, self.nc, self.inp
        st = ExitStack()
        dbg = self.debug
        LE, bLE = self.mask(st, "b_LE", (0, -1, 1, ALU.is_ge))
        LT, bLT = self.mask(st, "b_LT", (-1, -1, 1, ALU.is_ge))
        GE, bGE = self.mask(st, "b_GE", (0, 1, -1, ALU.is_ge))
        GT_, bGT = self.mask(st, "b_GT", (-1, 1, -1, ALU.is_ge))
        MBf, bMBf = self.mask(st, "b_MBf", (0, 1, -1, ALU.is_ge), val=0.0, fill=NEG)
        MBb, bMBb = self.mask(st, "b_MBb", (0, -1, 1, ALU.is_ge), val=0.0, fill=NEG)
        SELf, bSELf = self.mask(st, "b_SELf", (-127, 1, 0, ALU.is_equal))
        SELb, bSELb = self.mask(st, "b_SELb", (0, 1, 0, ALU.is_equal))
        smask = B.sb(st, "b_smask", [128, 14, 128], BF16)
        bsm = Buf("b_smask")
        B.dma("pool", smask[:], inp["smask"][:, :, :], bsm, writes=[bsm])
        cbufs = [bLE, bLT, bGE, bGT, bMBf, bMBb, bSELf, bSELb, bsm, self.cb]
        dirc = [dict(U=LE, S=GT_, incl=LE, strict=LT, MB=MBf, SEL=SELf),
                dict(U=GE, S=LT, incl=GE, strict=GT_, MB=MBb, SEL=SELb)]
        S = [B.sb(st, "b_S%d" % d, [128, 8, 128], F32) for d in range(2)]
        bS = [[Buf("b_S%d_%d" % (d, g)) for g in range(2)] for d in range(2)]
        Sb = [[Ring(B, st, "b_Sb%d_%d_" % (d, g), 2, [128, 4, 128], BF16) for g in range(2)] for d in range(2)]
        Sb_cur = [[None, None], [None, None]]
        C = [B.sb(st, "b_C%d" % d, [128, 4, 256], F32) for d in range(2)]
        bC = [Buf("b_C%d" % d) for d in range(2)]
        Cb = [Ring(B, st, "b_Cb%d_" % d, 2, [128, 4, 256], BF16) for d in range(2)]
        Cb_cur = [None, None]
        nst = [Ring(B, st, "b_n%d_" % d, 2, [128, 8], F32) for d in range(2)]
        nbf = [Ring(B, st, "b_nb%d_" % d, 2, [128, 4], BF16) for d in range(2)]
        n_cur = [None, None]
        nb_cur = [None, None]
        mst = [Ring(B, st, "b_m%d_" % d, 2, [128, 4], F32) for d in range(2)]
        m_cur = [None, None]
        for d in range(2):
            B.op("pool", lambda e, d=d: e.memset(S[d][:], 0.0), writes=bS[d])
            B.op("pool", lambda e, d=d: e.memset(C[d][:], 0.0), writes=[bC[d]])
            for g in range(2):
                t, b = Sb[d][g].next()
                B.op("pool", lambda e, t=t: e.memset(t[:], 0.0), writes=[b])
                Sb_cur[d][g] = (t, b)
            t, b = Cb[d].next()
            B.op("pool", lambda e, t=t: e.memset(t[:], 0.0), writes=[b])
            Cb_cur[d] = (t, b)
            t, b = nst[d].next()
            B.op("pool", lambda e, t=t: e.memset(t[:], 0.0), writes=[b])
            n_cur[d] = (t, b)
            t, b = nbf[d].next()
            B.op("pool", lambda e, t=t: e.memset(t[:], 0.0), writes=[b])
            nb_cur[d] = (t, b)
            t, b = mst[d].next()
            B.op("pool", lambda e, t=t: e.memset(t[:], 0.0), writes=[b])
            m_cur[d] = (t, b)
        def dring(name, shape, dt):
            return [Ring(B, st, "b_%s%d_" % (name, d), 2, shape, dt) for d in range(2)]
        rKT = dring("KT", [128, 8, 128], BF16)
        rQT = dring("QT", [128, 8, 128], BF16)
        rVG = dring("VG", [128, 1024], BF16)
        rGT = dring("GT", [128, 48], F32)
        rMQ = dring("MQ", [128, 4, 128], BF16)
        rMK = dring("MK", [128, 4, 128], BF16)
        rMV = dring("MV", [128, 1024], BF16)
        rgs = dring("gs", [128, 64], F32)
        psr = Ring(B, st, "b_ps", 8, [128, 512], F32, psum=True)
        NG, NM = 4, 2
        gslots = []
        for i in range(NG):
            sl = {}
            for nm, shp, dt in (("A", [128, 4, 128], F32), ("Bt", [128, 4, 128], F32), ("Ct", [128, 4, 128], F32),
                                ("attnT", [128, 4, 128], BF16), ("Qp", [128, 4, 128], BF16),
                                ("Kg", [128, 4, 128], BF16), ("kt", [128, 4, 128], BF16), ("G0", [128, 4, 128], BF16),
                                ("G1", [128, 4, 128], BF16), ("H0", [128, 4, 128], BF16), ("H1", [128, 4, 128], BF16),
                                ("IYT", [128, 4, 128], BF16), ("negW", [128, 4, 128], BF16), ("vnew", [128, 4, 128], BF16)):
                sl[nm] = (B.sb(st, "b_g%d_%s" % (i, nm), shp, dt), Buf("b_g%d_%s" % (i, nm)))
            gslots.append(sl)
        mslots = []
        for i in range(NM):
            sl = {}
            for nm, shp, dt in (("X", [128, 4, 128], F32), ("Y", [128, 4, 128], F32), ("Pm", [128, 4, 128], BF16),
                                ("PT", [128, 4, 128], BF16), ("Kw", [128, 4, 128], BF16), ("sm", [128, 64], F32), ("Ct", [128, 4, 256], F32)):
                sl[nm] = (B.sb(st, "b_m%d_%s" % (i, nm), shp, dt), Buf("b_m%d_%s" % (i, nm)))
            mslots.append(sl)
        ring_o1 = Ring(B, st, "b_o1_", 2, [128, 4, 128], F32)
        ring_o = Ring(B, st, "b_o_", 2, [128, 4, 128], F32)
        ring_num = Ring(B, st, "b_num_", 1, [128, 4, 256], F32)
        ring_h = Ring(B, st, "b_h_", 1, [128, 4, 256], F32)

        def bc3(ap2, n):
            return ap2.unsqueeze(2).to_broadcast([128, 4, n])

        def bcm(ap2, n=4):
            return ap2.unsqueeze(1).to_broadcast([128, n, 128])

        nsteps = dbg.get("b_steps", NCH)
        order = [list(range(NCH)), [1, 0] + list(range(NCH - 1, 1, -1))]
        if dbg.get("b_order"):
            order = dbg["b_order"]
            nsteps = len(order[0])
        out_lo, out_hi = 2, 2 + 33

        data = {}

        def load_step(step, d):
            c = order[d][step]
            tk, bk = rKT[d].next(); tq, bq = rQT[d].next(); tv, bv = rVG[d].next(); tg, bg = rGT[d].next()
            tmq, bmq = rMQ[d].next(); tmk, bmk = rMK[d].next(); tmv, bmv = rMV[d].next()
            B.dma("sp", tg[:], self.GT[c], bg, writes=[bg])
            B.dma("sp", tk[:], self.KT[c], bk, writes=[bk])
            B.dma("sp", tq[:], self.QT[c], bq, writes=[bq])
            B.dma("sp", tv[:], self.VG[c], bv, writes=[bv])
            B.dma("sp", tmq[:], self.MQT[c], bmq, writes=[bmq])
            B.dma("sp", tmk[:], self.MKT[c], bmk, writes=[bmk])
            B.dma("sp", tmv[:], self.MV[c], bmv, writes=[bmv])
            data[(step, d)] = dict(c=c, KT=(tk, bk), QT=(tq, bq), VG=(tv, bv), GT=(tg, bg), MQ=(tmq, bmq), MK=(tmk, bmk), MV=(tmv, bmv))

        def shared_pre(step, d):
            dd = data[(step, d)]
            tg, bg = dd["GT"]
            gs, bgs = rgs[d].next()
            dc = dirc[d]
            p, bp = psr.next()
            B.op("pe", lambda e: e.matmul(p[:, 0:8], lhsT=dc["U"][:], rhs=tg[:, d * 8:(d + 1) * 8], start=True, stop=True), reads=[bg] + cbufs, writes=[bp], inc=False)
            B.op("pe", lambda e: e.matmul(p[:, 8:12], lhsT=dc["U"][:], rhs=tg[:, 40 + d * 4:44 + d * 4], start=True, stop=True), reads=[bg] + cbufs, writes=[bp], inc=False)
            B.op("pe", lambda e: e.matmul(p[:, 12:20], lhsT=self.ones_f[:], rhs=tg[:, d * 8:(d + 1) * 8], start=True, stop=True), reads=[bg] + cbufs, writes=[bp], inc=False)
            B.op("pe", lambda e: e.matmul(p[:, 20:24], lhsT=self.ones_f[:], rhs=tg[:, 40 + d * 4:44 + d * 4], start=True, stop=True), reads=[bg] + cbufs, writes=[bp])
            B.op("act", lambda e: e.activation(out=gs[:, 0:24], in_=p[:, 0:24], func=AF.Identity), reads=[bp], writes=[bgs])
            B.op("act", lambda e: e.activation(out=gs[:, 24:32], in_=gs[:, 0:8], func=AF.Exp), reads=[bgs], writes=[bgs])
            B.op("dve", lambda e: e.tensor_tensor(out=gs[:, 32:40], in0=gs[:, 12:20], in1=gs[:, 0:8], op=ALU.subtract), reads=[bgs], writes=[bgs])
            B.op("act", lambda e: e.activation(out=gs[:, 32:40], in_=gs[:, 32:40], func=AF.Exp), reads=[bgs], writes=[bgs])
            B.op("act", lambda e: e.activation(out=gs[:, 40:48], in_=gs[:, 12:20], func=AF.Exp), reads=[bgs], writes=[bgs])
            B.op("dve", lambda e: e.tensor_tensor(out=gs[:, 48:52], in0=tg[:, 32 + d * 4:36 + d * 4], in1=gs[:, 8:12], op=ALU.subtract), reads=[bgs, bg], writes=[bgs])
            dd["gs"] = (gs, bgs)

        def gdn_group(step, d, hg, sl):
            dd = data[(step, d)]
            dc = dirc[d]
            c = dd["c"]
            need_o = out_lo <= c < out_hi
            tk, bk = dd["KT"]; tq, bq = dd["QT"]; tv, bv = dd["VG"]; tg, bg = dd["GT"]; gs, bgs = dd["gs"]
            h0 = hg * 4
            A, bA = sl["A"]; Bt, bBt = sl["Bt"]; Ct, bCt = sl["Ct"]
            attnT, battn = sl["attnT"]; Qp, bQp = sl["Qp"]; Kg, bKg = sl["Kg"]; kt, bkt = sl["kt"]
            IYT, bIYT = sl["IYT"]; negW, bnegW = sl["negW"]; vnew, bvnew = sl["vnew"]
            Qm, bQm = sl["IYT"]
            St, bSt = sl["A"]
            gcol = tg[:, d * 8 + h0:d * 8 + h0 + 4]
            bcol = tg[:, 16 + d * 8 + h0:16 + d * 8 + h0 + 4]
            eg = gs[:, 24 + h0:24 + h0 + 4]
            ekt = gs[:, 32 + h0:32 + h0 + 4]
            gte = gs[:, 40 + h0:40 + h0 + 4]
            for u in range(4):
                B.op("act", lambda e, u=u: e.activation(out=A[:, u, :], in_=dc["U"][:], func=AF.Identity, scale=gcol[:, u:u + 1]), reads=[bg] + cbufs, writes=[bA])
            for u in range(4):
                B.op("act", lambda e, u=u: e.activation(out=Ct[:, u, :], in_=dc["strict"][:], func=AF.Identity, scale=bcol[:, u:u + 1]), reads=[bg] + cbufs, writes=[bCt])
            yield
            pD, bpD = psr.next()
            for u in range(4):
                B.op("pe", lambda e, u=u: e.matmul(pD[:, u * 128:(u + 1) * 128], lhsT=dc["S"][:], rhs=A[:, u, :], start=True, stop=True),
                     reads=[bA] + cbufs, writes=[bpD], inc=(u == 3))
            B.op("act", lambda e: e.activation(out=Bt[:].rearrange("p u l -> p (u l)"), in_=pD[:, :], func=AF.Exp), reads=[bpD], writes=[bBt])
            yield
            B.op("dve", lambda e: e.tensor_tensor(out=A[:], in0=Bt[:], in1=bcm(dc["incl"][:]), op=ALU.mult), reads=[bBt] + cbufs, writes=[bA])
            B.op("pool", lambda e: e.tensor_tensor(out=Ct[:], in0=Ct[:], in1=Bt[:], op=ALU.mult), reads=[bCt, bBt], writes=[bCt])
            yield
            pKK, bpKK = psr.next()
            pQK, bpQK = psr.next()
            pKt, bpKt = psr.next()
            pKtb = pKt[:].bitcast(BF16)
            for u in range(4):
                B.op("pe", lambda e, u=u: e.matmul(pKK[:, u * 128:(u + 1) * 128], lhsT=tk[:, h0 + u, :], rhs=tk[:, h0 + u, :], start=True, stop=True),
                     reads=[bk], writes=[bpKK], inc=(u == 3))
            for u in range(4):
                B.op("pe", lambda e, u=u: e.matmul(pQK[:, u * 128:(u + 1) * 128], lhsT=tk[:, h0 + u, :], rhs=tq[:, h0 + u, :], start=True, stop=True),
                     reads=[bk, bq], writes=[bpQK], inc=(u == 3))
            for u in range(4):
                B.op("pe", lambda e, u=u: e.transpose(pKtb[:, u * 128:(u + 1) * 128], tk[:, h0 + u, :], self.ident_b[:]),
                     reads=[bk] + cbufs, writes=[bpKt], inc=(u == 3))
            B.op("dve", lambda e: e.tensor_tensor(out=Qm[:], in0=pKK[:, :].rearrange("p (u l) -> p u l", u=4), in1=Ct[:], op=ALU.mult),
                 reads=[bpKK, bCt], writes=[bQm])
            B.op("dve", lambda e: e.tensor_tensor(out=attnT[:], in0=pQK[:, :].rearrange("p (u l) -> p u l", u=4), in1=A[:], op=ALU.mult),
                 reads=[bpQK, bA], writes=[battn])
            B.op("dve", lambda e: e.tensor_tensor(out=Kg[:], in0=pKtb[:, 0:512].rearrange("p (u l) -> p u l", u=4), in1=bc3(eg, 128), op=ALU.mult),
                 reads=[bpKt, bgs], writes=[bKg])
            B.op("dve", lambda e: e.tensor_tensor(out=kt[:], in0=pKtb[:, 0:512].rearrange("p (u l) -> p u l", u=4), in1=bc3(ekt, 128), op=ALU.mult),
                 reads=[bpKt, bgs], writes=[bkt])
            yield
            B.op("pool", lambda e: e.tensor_tensor(out=Qp[:], in0=Qm[:], in1=bcm(self.ident_b[:]), op=ALU.add), reads=[bQm] + cbufs, writes=[bQp])
            yield
            Gc = None
            Hc = None
            for lev in range(7):
                sm = smask[:, d * 7 + lev, :]
                pY, bpY = psr.next()
                for u in range(4):
                    rhsH = self.ident_b[:] if Hc is None else Hc[0][:, u, :]
                    B.op("pe", lambda e, u=u, rhsH=rhsH: e.matmul(pY[:, u * 128:(u + 1) * 128], lhsT=Qp[:, u, :], rhs=rhsH, start=True, stop=True),
                         reads=[bQp] + cbufs + ([] if Hc is None else [Hc[1]]), writes=[bpY], inc=(u == 3))
                B.op("dve", lambda e, sm=sm: e.tensor_tensor(out=IYT[:], in0=pY[:, :].rearrange("p (u l) -> p u l", u=4), in1=bcm(sm), op=ALU.mult),
                     reads=[bpY] + cbufs, writes=[bIYT])
                yield
                Gn = sl["G%d" % (lev % 2)]
                Hn = sl["H%d" % (lev % 2)]
                pG, bpG = psr.next()
                for u in range(4):
                    rhsG = self.ident_b[:] if Gc is None else Gc[0][:, u, :]
                    B.op("pe", lambda e, u=u, rhsG=rhsG: e.matmul(pG[:, u * 128:(u + 1) * 128], lhsT=IYT[:, u, :], rhs=rhsG, start=True, stop=True),
                         reads=[bIYT] + cbufs + ([] if Gc is None else [Gc[1]]), writes=[bpG], inc=(u == 3))
                B.op("act", lambda e, Gn=Gn: e.activation(out=Gn[0][:].rearrange("p u l -> p (u l)"), in_=pG[:, :], func=AF.Identity), reads=[bpG], writes=[Gn[1]])
                if lev < 6:
                    pH, bpH = psr.next()
                    for u in range(4):
                        lhsG = self.ident_b[:] if Gc is None else Gc[0][:, u, :]
                        B.op("pe", lambda e, u=u, lhsG=lhsG: e.matmul(pH[:, u * 128:(u + 1) * 128], lhsT=lhsG, rhs=IYT[:, u, :], start=True, stop=True),
                             reads=[bIYT] + cbufs + ([] if Gc is None else [Gc[1]]), writes=[bpH], inc=(u == 3))
                    B.op("act", lambda e, Hn=Hn: e.activation(out=Hn[0][:].rearrange("p u l -> p (u l)"), in_=pH[:, :], func=AF.Identity), reads=[bpH], writes=[Hn[1]])
                    Hc = Hn
                Gc = Gn
                yield
            G, bG = Gc
            pW, bpW = psr.next()
            for u in range(4):
                B.op("pe", lambda e, u=u: e.matmul(pW[:, u * 128:(u + 1) * 128], lhsT=Kg[:, u, :], rhs=G[:, u, :], start=True, stop=True),
                     reads=[bKg, bG], writes=[bpW], inc=(u == 3))
            B.op("act", lambda e: e.activation(out=negW[:].rearrange("p u l -> p (u l)"), in_=pW[:, :], func=AF.Identity, scale=-1.0), reads=[bpW], writes=[bnegW])
            yield
            while step > 0 and ("gdn", step - 1, d, hg) not in done and ("gdn", step - 1, d, hg) in started:
                yield
            sbt, bsb = Sb_cur[d][hg]
            pV, bpV = psr.next()
            for u in range(4):
                B.op("pe", lambda e, u=u: e.matmul(pV[:, u * 128:(u + 1) * 128], lhsT=G[:, u, :], rhs=tv[:, (h0 + u) * 128:(h0 + u + 1) * 128], start=True, stop=False),
                     reads=[bG, bv], writes=[bpV], inc=False)
                B.op("pe", lambda e, u=u: e.matmul(pV[:, u * 128:(u + 1) * 128], lhsT=negW[:, u, :], rhs=sbt[:, u, :], start=False, stop=True),
                     reads=[bnegW, bsb], writes=[bpV], inc=(u == 3))
            B.op("dve", lambda e: e.tensor_tensor(out=vnew[:], in0=pV[:, :].rearrange("p (u l) -> p u l", u=4), in1=bc3(bcol, 128), op=ALU.mult),
                 reads=[bpV, bg], writes=[bvnew])
            Sg = S[d][:, h0:h0 + 4, :]
            B.op("pool", lambda e: e.tensor_tensor(out=St[:], in0=Sg, in1=bc3(gte, 128), op=ALU.mult), reads=[bS[d][hg], bgs], writes=[bSt])
            yield
            if need_o:
                pO1, bpO1 = psr.next()
                for u in range(4):
                    B.op("pe", lambda e, u=u: e.matmul(pO1[:, u * 128:(u + 1) * 128], lhsT=tq[:, h0 + u, :], rhs=sbt[:, u, :], start=True, stop=True),
                         reads=[bq, bsb], writes=[bpO1], inc=(u == 3))
                o1, bo1 = ring_o1.next()
                B.op("dve", lambda e: e.tensor_tensor(out=o1[:], in0=pO1[:, :].rearrange("p (u l) -> p u l", u=4), in1=bc3(eg, 128), op=ALU.mult),
                     reads=[bpO1, bgs], writes=[bo1])
                pO2, bpO2 = psr.next()
                for u in range(4):
                    B.op("pe", lambda e, u=u: e.matmul(pO2[:, u * 128:(u + 1) * 128], lhsT=attnT[:, u, :], rhs=vnew[:, u, :], start=True, stop=True),
                         reads=[battn, bvnew], writes=[bpO2], inc=(u == 3))
                o, bo = ring_o.next()
                B.op("dve", lambda e: e.tensor_tensor(out=o[:], in0=pO2[:, :].rearrange("p (u l) -> p u l", u=4), in1=o1[:], op=ALU.add),
                     reads=[bpO2, bo1], writes=[bo])
                dst = (self.OF if d == 0 else self.OB)[c - 2]
                B.dma("sp", dst[:, hg * 512:(hg + 1) * 512], o[:].rearrange("p u l -> p (u l)"), bo, reads=[bo])
            pS, bpS = psr.next()
            for u in range(4):
                B.op("pe", lambda e, u=u: e.matmul(pS[:, u * 128:(u + 1) * 128], lhsT=kt[:, u, :], rhs=vnew[:, u, :], start=True, stop=True),
                     reads=[bkt, bvnew], writes=[bpS], inc=(u == 3))
            B.op("dve", lambda e: e.tensor_tensor(out=Sg, in0=pS[:, :].rearrange("p (u l) -> p u l", u=4), in1=St[:], op=ALU.add),
                 reads=[bpS, bSt], writes=[bS[d][hg]])
            yield
            nsb, bnsb = Sb[d][hg].next()
            B.op("act", lambda e: e.activation(out=nsb[:], in_=Sg, func=AF.Identity), reads=[bS[d][hg]], writes=[bnsb])
            Sb_cur[d][hg] = (nsb, bnsb)
            yield

        self._b_env = dict(data=data, dirc=dirc, cbufs=cbufs, psr=psr, order=order, out_lo=out_lo, out_hi=out_hi, bc3=bc3, bcm=bcm,
                           C=C, bC=bC, Cb=Cb, Cb_cur=Cb_cur, nst=nst, nbf=nbf, n_cur=n_cur, nb_cur=nb_cur, mst=mst, m_cur=m_cur,
                           ring_num=ring_num, ring_h=ring_h)
        ml_group = self.make_ml_group()

        from collections import deque
        pending = deque()
        for step in range(nsteps):
            dirs = [d for d in range(2) if not (d == 0 and order[0][step] >= out_hi)]
            for d in dirs:
                pending.append(("load", step, d))
            for hg in range(2):
                for d in dirs:
                    if not dbg.get("b_no_gdn"):
                        pending.append(("gdn", step, d, hg))
            for d in dirs:
                if not dbg.get("b_no_ml"):
                    pending.append(("ml", step, d))
        free_g = list(range(NG))
        free_m = list(range(NM))
        done = set()
        started = set()
        active = []
        rounds = 0
        GAP = dbg.get("b_gap", 5)
        last_gstart = [-GAP]
        loaded = set()
        while pending or active:
            while pending:
                it = pending[0]
                if it[0] == "load":
                    _, step, d = it
                    if any((k[1] == step - 2 and k[2] == d and k not in done) for k in started):
                        break
                    load_step(step, d)
                    shared_pre(step, d)
                    pending.popleft()
                    continue
                if it[0] == "gdn":
                    _, step, d, hg = it
                    if not free_g or rounds - last_gstart[0] < GAP:
                        break
                    last_gstart[0] = rounds
                    si = free_g.pop(0)
                    started.add(it)
                    active.append((it, gdn_group(step, d, hg, gslots[si]), ("g", si)))
                    pending.popleft()
                    continue
                if it[0] == "ml":
                    _, step, d = it
                    key_prev = ("ml", step - 1, d)
                    if (step > 0 and key_prev not in done) or not free_m:
                        break
                    si = free_m.pop(0)
                    started.add(it)
                    active.append((it, ml_group(step, d, mslots[si]), ("m", si)))
                    pending.popleft()
                    continue
            rounds += 1
            for ent in list(active):
                it, gen, (kind, si) = ent
                try:
                    next(gen)
                except StopIteration:
                    active.remove(ent)
                    done.add(it)
                    (free_g if kind == "g" else free_m).append(si)
        B.barrier()
        st.close()

    def make_ml_group(self):
        B = self.B
        env = self._b_env
        data, dirc, cbufs, psr = env["data"], env["dirc"], env["cbufs"], env["psr"]
        bc3, bcm = env["bc3"], env["bcm"]
        C, bC, Cb, Cb_cur = env["C"], env["bC"], env["Cb"], env["Cb_cur"]
        nst, nbf, n_cur, nb_cur, mst, m_cur = env["nst"], env["nbf"], env["n_cur"], env["nb_cur"], env["mst"], env["m_cur"]
        ring_num, ring_h = env["ring_num"], env["ring_h"]
        out_lo, out_hi = env["out_lo"], env["out_hi"]

        def bc3n(ap2, n):
            return ap2.unsqueeze(2).to_broadcast([128, ap2.shape[1], n])

        def ml_group(step, d, sl):
            dd = data[(step, d)]
            dc = dirc[d]
            c = dd["c"]
            need_o = out_lo <= c < out_hi
            tg, bg = dd["GT"]; gs, bgs = dd["gs"]
            mq, bmq = dd["MQ"]; mk, bmk = dd["MK"]; mv, bmv = dd["MV"]
            X, bX = sl["X"]; Y, bY = sl["Y"]; Pm, bPm = sl["Pm"]; PT, bPT = sl["PT"]; Kw, bKw = sl["Kw"]
            sm, bsm = sl["sm"]; Ct, bCt = sl["Ct"]
            bcc = gs[:, 8:12]
            blast = gs[:, 20:24]
            cvec = gs[:, 48:52]
            mprev, bmprev = m_cur[d]
            B.op("pool", lambda e: e.tensor_tensor(out=X[:], in0=bcm(self.ident_f[:]), in1=bc3(cvec, 128), op=ALU.mult), reads=[bgs] + cbufs, writes=[bX])
            B.op("dve", lambda e: e.tensor_tensor(out=sm[:, 4:8], in0=bcc, in1=mprev[:, 0:4], op=ALU.add), reads=[bgs, bmprev], writes=[bsm])
            yield
            pC, bpC = psr.next()
            for u in range(4):
                B.op("pe", lambda e, u=u: e.matmul(pC[:, u * 128:(u + 1) * 128], lhsT=self.ones_f[:], rhs=X[:, u, :], start=True, stop=True),
                     reads=[bX] + cbufs, writes=[bpC], inc=(u == 3))
            B.op("dve", lambda e: e.tensor_tensor(out=Y[:], in0=pC[:, :].rearrange("p (u l) -> p u l", u=4), in1=bc3(bcc, 128), op=ALU.add),
                 reads=[bpC, bgs], writes=[bY])
            yield
            B.op("pool", lambda e: e.tensor_tensor(out=Y[:], in0=Y[:], in1=bcm(dc["MB"][:]), op=ALU.add), reads=[bY] + cbufs, writes=[bY])
            yield
            B.op("dve", lambda e: e.tensor_reduce(out=sm[:, 0:4], in_=Y[:], axis=AX.X, op=ALU.max), reads=[bY], writes=[bsm])
            B.op("dve", lambda e: e.tensor_tensor(out=sm[:, 8:12], in0=sm[:, 0:4], in1=sm[:, 4:8], op=ALU.max), reads=[bsm], writes=[bsm])
            B.op("dve", lambda e: e.tensor_scalar(out=sm[:, 12:16], in0=sm[:, 8:12], scalar1=-1.0, scalar2=None, op0=ALU.mult), reads=[bsm], writes=[bsm])
            B.op("dve", lambda e: e.tensor_tensor(out=sm[:, 16:20], in0=sm[:, 4:8], in1=sm[:, 8:12], op=ALU.subtract), reads=[bsm], writes=[bsm])
            yield
            if need_o:
                for u in range(4):
                    B.op("act", lambda e, u=u: e.activation(out=X[:, u, :], in_=Y[:, u, :], func=AF.Exp, bias=sm[:, 12 + u:13 + u]), reads=[bY, bsm], writes=[bX])
                B.op("act", lambda e: e.activation(out=sm[:, 16:20], in_=sm[:, 16:20], func=AF.Exp), reads=[bsm], writes=[bsm])
                B.op("act", lambda e: e.activation(out=sm[:, 20:24], in_=sm[:, 12:16], func=AF.Exp), reads=[bsm], writes=[bsm])
                yield
                pQK, bpQK = psr.next()
                for u in range(4):
                    B.op("pe", lambda e, u=u: e.matmul(pQK[:, u * 128:(u + 1) * 128], lhsT=mq[:, u, :], rhs=mk[:, u, :], start=True, stop=True),
                         reads=[bmq, bmk], writes=[bpQK], inc=(u == 3))
                B.op("dve", lambda e: e.tensor_tensor(out=Pm[:], in0=pQK[:, :].rearrange("p (u l) -> p u l", u=4), in1=X[:], op=ALU.mult),
                     reads=[bpQK, bX], writes=[bPm])
                yield
                pT, bpT = psr.next()
                pTb = pT[:].bitcast(BF16)
                for u in range(4):
                    B.op("pe", lambda e, u=u: e.transpose(pTb[:, u * 128:(u + 1) * 128], Pm[:, u, :], self.ident_b[:]), reads=[bPm] + cbufs, writes=[bpT], inc=(u == 3))
                B.op("act", lambda e: e.activation(out=PT[:].rearrange("p u l -> p (u l)"), in_=pTb[:, 0:512], func=AF.Identity), reads=[bpT], writes=[bPT])
                yield
                cbt, bcb = Cb_cur[d]
                nbt, bnb = nb_cur[d]
                pDn, bpDn = psr.next()
                for u in range(4):
                    B.op("pe", lambda e, u=u: e.matmul(pDn[:, u:u + 1], lhsT=mq[:, u, :], rhs=nbt[:, u:u + 1], start=True, stop=True), reads=[bmq, bnb], writes=[bpDn], inc=False)
                for u in range(4):
                    B.op("pe", lambda e, u=u: e.matmul(pDn[:, 4 + u:5 + u], lhsT=PT[:, u, :], rhs=self.ones_b[:, 0:1], start=True, stop=True),
                         reads=[bPT] + cbufs, writes=[bpDn], inc=(u == 3))
                B.op("dve", lambda e: e.tensor_tensor(out=sm[:, 24:28], in0=pDn[:, 0:4], in1=sm[:, 16:20], op=ALU.mult), reads=[bpDn, bsm], writes=[bsm])
                B.op("dve", lambda e: e.tensor_tensor(out=sm[:, 24:28], in0=pDn[:, 4:8], in1=sm[:, 24:28], op=ALU.add), reads=[bpDn, bsm], writes=[bsm])
                B.op("dve", lambda e: e.tensor_tensor(out=sm[:, 24:28], in0=sm[:, 24:28], in1=sm[:, 24:28], op=ALU.mult), reads=[bsm], writes=[bsm])
                B.op("dve", lambda e: e.tensor_tensor(out=sm[:, 28:32], in0=sm[:, 20:24], in1=sm[:, 20:24], op=ALU.mult), reads=[bsm], writes=[bsm])
                B.op("dve", lambda e: e.tensor_tensor(out=sm[:, 24:28], in0=sm[:, 24:28], in1=sm[:, 28:32], op=ALU.max), reads=[bsm], writes=[bsm])
                yield
                B.op("pool", lambda e: e.tensor_tensor(out=sm[:, 28:32], in0=sm[:, 24:28], in1=self.nhalf[:, 0:4], op=ALU.pow), reads=[bsm] + cbufs, writes=[bsm])
                yield
                num, bnum = ring_num.next()
                hh, bhh = ring_h.next()
                for pr in range(2):
                    pN1, bpN1 = psr.next()
                    pN2, bpN2 = psr.next()
                    for uu in range(2):
                        u = pr * 2 + uu
                        B.op("pe", lambda e, u=u, uu=uu, pN1=pN1: e.matmul(pN1[:, uu * 256:(uu + 1) * 256], lhsT=mq[:, u, :], rhs=cbt[:, u, :], start=True, stop=True),
                             reads=[bmq, bcb], writes=[bpN1], inc=(uu == 1))
                    for uu in range(2):
                        u = pr * 2 + uu
                        B.op("pe", lambda e, u=u, uu=uu, pN2=pN2: e.matmul(pN2[:, uu * 256:(uu + 1) * 256], lhsT=PT[:, u, :], rhs=mv[:, u * 256:(u + 1) * 256], start=True, stop=True),
                             reads=[bPT, bmv], writes=[bpN2], inc=(uu == 1))
                    B.op("dve", lambda e, pr=pr, pN1=pN1: e.tensor_tensor(out=num[:, pr * 2:pr * 2 + 2, :], in0=pN1[:, :].rearrange("p (u l) -> p u l", u=2),
                                                                         in1=bc3n(sm[:, 16 + pr * 2:18 + pr * 2], 256), op=ALU.mult), reads=[bpN1, bsm], writes=[bnum])
                    B.op("dve", lambda e, pr=pr, pN2=pN2: e.tensor_tensor(out=num[:, pr * 2:pr * 2 + 2, :], in0=pN2[:, :].rearrange("p (u l) -> p u l", u=2),
                                                                         in1=num[:, pr * 2:pr * 2 + 2, :], op=ALU.add), reads=[bpN2, bnum], writes=[bnum])
                B.op("dve", lambda e: e.tensor_tensor(out=hh[:], in0=num[:], in1=bc3n(sm[:, 28:32], 256), op=ALU.mult), reads=[bnum, bsm], writes=[bhh])
                dst = (self.HF if d == 0 else self.HB)[c - 2]
                B.dma("sp", dst[:, :], hh[:].rearrange("p u l -> p (u l)"), bhh, reads=[bhh])
                yield
            pSel, bpSel = psr.next()
            B.op("pe", lambda e: e.matmul(pSel[:, 0:4], lhsT=dc["SEL"][:], rhs=sm[:, 8:12], start=True, stop=True), reads=[bsm] + cbufs, writes=[bpSel])
            mnew, bmnew = mst[d].next()
            B.op("act", lambda e: e.activation(out=mnew[:], in_=pSel[:, 0:4], func=AF.Identity), reads=[bpSel], writes=[bmnew])
            yield
            B.op("dve", lambda e: e.tensor_tensor(out=sm[:, 32:36], in0=cvec, in1=blast, op=ALU.add), reads=[bgs], writes=[bsm])
            B.op("dve", lambda e: e.tensor_tensor(out=sm[:, 32:36], in0=sm[:, 32:36], in1=mnew[:], op=ALU.subtract), reads=[bsm, bmnew], writes=[bsm])
            B.op("dve", lambda e: e.tensor_tensor(out=sm[:, 36:40], in0=blast, in1=mprev[:, 0:4], op=ALU.add), reads=[bgs, bmprev], writes=[bsm])
            B.op("dve", lambda e: e.tensor_tensor(out=sm[:, 36:40], in0=sm[:, 36:40], in1=mnew[:], op=ALU.subtract), reads=[bsm, bmnew], writes=[bsm])
            yield
            B.op("act", lambda e: e.activation(out=sm[:, 32:40], in_=sm[:, 32:40], func=AF.Exp), reads=[bsm], writes=[bsm])
            yield
            pKt, bpKt = psr.next()
            pKtb = pKt[:].bitcast(BF16)
            for u in range(4):
                B.op("pe", lambda e, u=u: e.transpose(pKtb[:, u * 128:(u + 1) * 128], mk[:, u, :], self.ident_b[:]), reads=[bmk] + cbufs, writes=[bpKt], inc=(u == 3))
            B.op("dve", lambda e: e.tensor_tensor(out=Kw[:], in0=pKtb[:, 0:512].rearrange("p (u l) -> p u l", u=4), in1=bc3(sm[:, 32:36], 128), op=ALU.mult),
                 reads=[bpKt, bsm], writes=[bKw])
            m_cur[d] = (mnew, bmnew)
            for pr in range(2):
                B.op("pool", lambda e, pr=pr: e.tensor_tensor(out=Ct[:, pr * 2:pr * 2 + 2, :], in0=C[d][:, pr * 2:pr * 2 + 2, :], in1=bc3n(sm[:, 36 + pr * 2:38 + pr * 2], 256), op=ALU.mult),
                     reads=[bC[d], bsm], writes=[bCt])
            yield
            for pr in range(2):
                pC2, bpC2 = psr.next()
                for uu in range(2):
                    u = pr * 2 + uu
                    B.op("pe", lambda e, u=u, uu=uu, pC2=pC2: e.matmul(pC2[:, uu * 256:(uu + 1) * 256], lhsT=Kw[:, u, :], rhs=mv[:, u * 256:(u + 1) * 256], start=True, stop=True),
                         reads=[bKw, bmv], writes=[bpC2], inc=(uu == 1))
                B.op("dve", lambda e, pr=pr, pC2=pC2: e.tensor_tensor(out=C[d][:, pr * 2:pr * 2 + 2, :], in0=pC2[:, :].rearrange("p (u l) -> p u l", u=2), in1=Ct[:, pr * 2:pr * 2 + 2, :], op=ALU.add),
                     reads=[bpC2, bCt], writes=[bC[d]])
            pN, bpN = psr.next()
            for u in range(4):
                B.op("pe", lambda e, u=u: e.matmul(pN[:, u:u + 1], lhsT=Kw[:, u, :], rhs=self.ones_b[:, 0:1], start=True, stop=True), reads=[bKw] + cbufs, writes=[bpN], inc=(u == 3))
            nold, bnold = n_cur[d]
            nnew, bnnew = nst[d].next()
            B.op("dve", lambda e: e.tensor_tensor(out=nnew[:, 4:8], in0=nold[:, 0:4], in1=sm[:, 36:40], op=ALU.mult), reads=[bnold, bsm], writes=[bnnew])
            B.op("dve", lambda e: e.tensor_tensor(out=nnew[:, 0:4], in0=pN[:, 0:4], in1=nnew[:, 4:8], op=ALU.add), reads=[bpN, bnnew], writes=[bnnew])
            yield
            nbn, bnbn = nbf[d].next()
            B.op("act", lambda e: e.activation(out=nbn[:], in_=nnew[:, 0:4], func=AF.Identity), reads=[bnnew], writes=[bnbn])
            cbn, bcbn = Cb[d].next()
            B.op("act", lambda e: e.activation(out=cbn[:], in_=C[d][:], func=AF.Identity), reads=[bC[d]], writes=[bcbn])
            n_cur[d] = (nnew, bnnew)
            nb_cur[d] = (nbn, bnbn)
            Cb_cur[d] = (cbn, bcbn)
            yield

        return ml_group

    def phaseC1(self):
        B, nc, inp = self.B, self.nc, self.inp
        st = ExitStack()
        dbg = self.debug
        wo, bwo = self.load_w_bf16(st, "c_wo", inp["w_o"], 8, 4096, 8)
        wbg, bwbg = self.load_w_bf16(st, "c_wbg", inp["w_bg"], 8, 1024, 2)
        wbm, bwbm = self.load_w_bf16(st, "c_wbm", inp["w_bm"], 8, 1024, 2)
        wout, bwout = self.load_w_bf16(st, "c_wout", inp["w_out"], 8, 1024, 2)
        nwb = B.sb(st, "c_nwb", [128, 2, 1024], F32)
        bnwb = Buf("c_nwb")
        B.dma("sp", nwb[:, 0, :], inp["gnw_bc"][:, :], bnwb, writes=[bnwb])
        B.dma("sp", nwb[:, 1, :], inp["mnw_bc"][:, :], bnwb, writes=[bnwb])
        zero = B.sb(st, "c_zero", [64, 1024], F32)
        bzero = Buf("c_zero")
        B.op("pool", lambda e: e.memset(zero[:], 0.0), writes=[bzero])
        bX1 = Buf("X1")
        B.dma("sp", self.X1[0:64, :], zero[:], bzero, reads=[bzero], writes=[bX1])
        NS = 2
        xt = B.sb(st, "c_x", [128, NS, 1024], F32)
        bxts = [Buf("c_x%d" % i) for i in range(NS)]
        xn = B.sb(st, "c_xn", [128, NS, 1024], BF16); bxn = Buf("c_xn")
        sq = B.sb(st, "c_sq", [128, 24], F32); bsq = Buf("c_sq")
        junk = None; bjunk = None
        hxT = B.sb(st, "c_hxT", [128, 8, NS * 128], BF16); bhx = Buf("c_hxT")
        oa = B.sb(st, "c_oa", [128, 1024], F32); boa = Buf("c_oa")
        ob = B.sb(st, "c_ob", [128, 1024], F32); bob = Buf("c_ob")
        gt = B.sb(st, "c_gt", [128, 1024], F32); bgt = Buf("c_gt")
        osq = B.sb(st, "c_osq", [128, 1024], F32); bosq = Buf("c_osq")
        sm = B.sb(st, "c_sm", [128, 32], F32); bsm = Buf("c_sm")
        og = B.sb(st, "c_og", [128, 1024], BF16); bog = Buf("c_og")
        brT = [B.sb(st, "c_brT%d" % i, [128, 8, NS * 128], BF16) for i in range(2)]
        bbrT = [Buf("c_brT%d" % i) for i in range(2)]
        sg = Ring(B, st, "c_sg", 4, [128, NS * 128], F32)
        yt = Ring(B, st, "c_yt", 2, [128, NS * 128], F32)
        mT = B.sb(st, "c_mT", [128, 8, NS * 128], BF16); bmT = Buf("c_mT")
        tmp = Ring(B, st, "c_tmp", 2, [128, 512], F32)
        ptr = Ring(B, st, "c_ptr", 2, [128, 512], F32, psum=True)
        pmm = Ring(B, st, "c_pmm", 5, [128, 512], F32, psum=True)
        nt = OWN_T // 128
        sts = []
        i = 0
        while i < nt:
            ns = min(NS, nt - i)
            sts.append((i, ns))
            i += ns
        if dbg.get("c1_tiles"):
            sts = sts[: dbg["c1_tiles"]]
        for (t0, ns) in sts:
            n = ns * 128
            for s in range(ns):
                B.dma("sp", xt[:, s, :], inp["x"][(t0 + s) * 128:(t0 + s + 1) * 128, :], bxts[s], writes=[bxts[s]])
            self.norm_transpose(xt, bxts, ns, xn, bxn, sq, bsq, junk, bjunk, ptr, hxT, bhx, 0, 0)
            for br in range(2):
                nh, hd = (8, 128) if br == 0 else (4, 256)
                srcf, srcb = (self.OF, self.OB) if br == 0 else (self.HF, self.HB)
                for s in range(ns):
                    c = t0 + s
                    B.dma("sp", oa[:], srcf[c], boa, writes=[boa])
                    B.dma("sp", ob[:], srcb[c], bob, writes=[bob])
                    for hf in range(2):
                        p, bp = pmm.next()
                        for k in range(8):
                            B.op("pe", lambda e, k=k, p=p, s=s, hf=hf, br=br: e.matmul(
                                p[:], lhsT=hxT[:, k, s * 128:(s + 1) * 128], rhs=wo[:, k, br * 1024 + hf * 512: br * 1024 + (hf + 1) * 512],
                                start=(k == 0), stop=(k == 7)), reads=[bhx, bwo], writes=[bp], inc=(k == 7))
                        B.op("act", lambda e, p=p, hf=hf, br=br: e.activation(out=gt[:, hf * 512:(hf + 1) * 512], in_=p[:],
                                                                            func=(AF.Silu if br == 0 else AF.Sigmoid)), reads=[bp], writes=[bgt])
                    B.op("pool", lambda e, br=br: e.tensor_tensor(out=gt[:], in0=gt[:], in1=nwb[:, br, :], op=ALU.mult), reads=[bgt, bnwb], writes=[bgt])
                    B.op("dve", lambda e: e.tensor_tensor(out=oa[:], in0=oa[:], in1=ob[:], op=ALU.add), reads=[boa, bob], writes=[boa])
                    B.op("pool", lambda e: e.tensor_tensor(out=osq[:], in0=oa[:], in1=oa[:], op=ALU.mult), reads=[boa], writes=[bosq])
                    B.op("dve", lambda e, nh=nh: e.tensor_reduce(out=sm[:, 0:nh], in_=osq[:].rearrange("p (h e) -> p h e", h=nh), axis=AX.X, op=ALU.add),
                         reads=[bosq], writes=[bsm])
                    B.op("dve", lambda e, nh=nh, hd=hd: e.tensor_scalar(out=sm[:, 8:8 + nh], in0=sm[:, 0:nh], scalar1=float(1.0 / hd), scalar2=float(EPS),
                                                                      op0=ALU.mult, op1=ALU.add), reads=[bsm], writes=[bsm])
                    B.op("pool", lambda e, nh=nh: e.tensor_tensor(out=sm[:, 16:16 + nh], in0=sm[:, 8:8 + nh], in1=self.nhalf[:, 0:nh], op=ALU.pow),
                         reads=[bsm, self.cb], writes=[bsm])
                    B.op("dve", lambda e, nh=nh, hd=hd: e.tensor_tensor(out=osq[:].rearrange("p (h e) -> p h e", h=nh), in0=oa[:].rearrange("p (h e) -> p h e", h=nh),
                                                                      in1=sm[:, 16:16 + nh].unsqueeze(2).to_broadcast([128, nh, hd]), op=ALU.mult),
                         reads=[boa, bsm], writes=[bosq])
                    B.op("dve", lambda e: e.tensor_tensor(out=og[:], in0=osq[:], in1=gt[:], op=ALU.mult), reads=[bosq, bgt], writes=[bog])
                    p, bp = ptr.next()
                    pb = p[:].bitcast(BF16)
                    for k in range(8):
                        B.op("pe", lambda e, k=k, pb=pb: e.transpose(pb[:, k * 128:(k + 1) * 128], og[:, k * 128:(k + 1) * 128], self.ident_b[:]),
                             reads=[bog, self.cb], writes=[bp], inc=(k == 7))
                    B.op("act", lambda e, pb=pb, s=s, br=br: e.activation(out=brT[br][:, :, s * 128:(s + 1) * 128], in_=pb[:, 0:1024].rearrange("p (k t) -> p k t", k=8),
                                                                          func=AF.Identity), reads=[bp], writes=[bbrT[br]])
            for ncn in range(8):
                sgs = []
                for gi in range(2):
                    p, bp = pmm.next()
                    for k in range(8):
                        B.op("pe", lambda e, k=k, p=p, gi=gi, ncn=ncn: e.matmul(p[:, 0:n], lhsT=wo[:, k, 2048 + gi * 1024 + ncn * 128: 2048 + gi * 1024 + (ncn + 1) * 128],
                                                                               rhs=hxT[:, k, 0:n], start=(k == 0), stop=(k == 7)), reads=[bhx, bwo], writes=[bp], inc=(k == 7))
                    g_, bg_ = sg.next()
                    B.op("act", lambda e, p=p, g_=g_: e.activation(out=g_[:, 0:n], in_=p[:, 0:n], func=AF.Sigmoid), reads=[bp], writes=[bg_])
                    sgs.append((g_, bg_))
                ys = []
                for br, (w, bw) in enumerate(((wbg, bwbg), (wbm, bwbm))):
                    p, bp = pmm.next()
                    for k in range(8):
                        B.op("pe", lambda e, k=k, p=p, w=w, br=br, ncn=ncn: e.matmul(p[:, 0:n], lhsT=w[:, k, ncn * 128:(ncn + 1) * 128], rhs=brT[br][:, k, 0:n],
                                                                                    start=(k == 0), stop=(k == 7)), reads=[bbrT[br], bw], writes=[bp], inc=(k == 7))
                    ys.append((p, bp))
                y_, by_ = yt.next()
                B.op("dve", lambda e, y_=y_: e.tensor_tensor(out=y_[:, 0:n], in0=ys[0][0][:, 0:n], in1=sgs[0][0][:, 0:n], op=ALU.mult), reads=[ys[0][1], sgs[0][1]], writes=[by_])
                g1, bg1 = sgs[1]
                B.op("dve", lambda e, g1=g1: e.tensor_tensor(out=g1[:, 0:n], in0=ys[1][0][:, 0:n], in1=g1[:, 0:n], op=ALU.mult), reads=[ys[1][1], bg1], writes=[bg1])
                B.op("pool", lambda e, y_=y_, g1=g1, ncn=ncn: e.tensor_tensor(out=mT[:, ncn, 0:n], in0=y_[:, 0:n], in1=g1[:, 0:n], op=ALU.add), reads=[by_, bg1], writes=[bmT])
            for s in range(ns):
                for hf in range(2):
                    p, bp = pmm.next()
                    for k in range(8):
                        B.op("pe", lambda e, k=k, p=p, s=s, hf=hf: e.matmul(p[:], lhsT=mT[:, k, s * 128:(s + 1) * 128], rhs=wout[:, k, hf * 512:(hf + 1) * 512],
                                                                           start=(k == 0), stop=(k == 7)), reads=[bmT, bwout], writes=[bp], inc=(k == 7))
                    t_, bt_ = tmp.next()
                    B.op("dve", lambda e, p=p, t_=t_, hf=hf: e.tensor_tensor(out=t_[:], in0=p[:], in1=self.gate_bc[:, 0, hf * 512:(hf + 1) * 512], op=ALU.mult),
                         reads=[bp, self.bgate], writes=[bt_])
                    B.op("pool", lambda e, t_=t_, s=s, hf=hf: e.tensor_tensor(out=xt[:, s, hf * 512:(hf + 1) * 512], in0=xt[:, s, hf * 512:(hf + 1) * 512], in1=t_[:], op=ALU.add),
                         reads=[bt_, bxts[s]], writes=[bxts[s]])
                B.dma("sp", self.X1[64 + (t0 + s) * 128: 64 + (t0 + s + 1) * 128, :], xt[:, s, :], bxts[s], reads=[bxts[s]], writes=[bX1])
        B.barrier()
        st.close()

    def precast_wup(self):
        B = self.B
        self.WUPB = B.dram("WUPB", [44, 128, 8, 128], BF16)
        self.bwupb = Buf("WUPB")
        src = self.inp["w_up"].rearrange("(k p) (c j) -> c p k j", p=128, j=128)
        for c in range(44):
            B.dma("pool", self.WUPB[c], src[c], self.bwupb, writes=[self.bwupb])

    def phaseC2(self):
        B, nc, inp = self.B, self.nc, self.inp
        st = ExitStack()
        dbg = self.debug
        wd = B.sb(st, "d_wd", [128, 22, 1024], BF16)
        bwd = Buf("d_wd")
        wdv = inp["w_down"].rearrange("(c p) n -> p c n", p=128)
        for i in range(0, 22, 6):
            j = min(22, i + 6)
            B.dma("pool", wd[:, i:j, :], wdv[:, i:j, :], bwd, writes=[bwd])
        cw = B.sb(st, "d_cw", [128, 44, 9], F32)
        nob = B.sb(st, "d_nob", [128, 1024], F32)
        bsm0 = Buf("d_small")
        B.dma("sp", cw[:], inp["ffn_cw"][:, :, :], bsm0, writes=[bsm0])
        B.dma("sp", nob[:], inp["now_bc"][:, :], bsm0, writes=[bsm0])
        DEPTH = 3
        wup = Ring(B, st, "d_wup", DEPTH + 1, [128, 2, 8, 128], BF16)
        xt = B.sb(st, "d_x", [128, 5, 1024], F32)
        bxts = [Buf("d_x%d" % i) for i in range(5)]
        xn = B.sb(st, "d_xn", [128, 5, 1024], BF16); bxn = Buf("d_xn")
        sq = B.sb(st, "d_sq", [128, 24], F32); bsq = Buf("d_sq")
        junk = B.sb(st, "d_junk", [128, 1024], BF16); bjunk = Buf("d_junk")
        hxT = B.sb(st, "d_hxT", [128, 8, 640], BF16); bhx = Buf("d_hxT")
        upad = Ring(B, st, "d_up", 2 * DEPTH, [128, 10, 66], BF16)
        dgr = Ring(B, st, "d_dg", 2 * DEPTH, [128, 9, 128], BF16)
        sgt = Ring(B, st, "d_sg", DEPTH, [128, 512], F32)
        aT = B.sb(st, "d_aT", [128, 22, 512], BF16); baT = Buf("d_aT")
        xo = B.sb(st, "d_xo", [128, 4, 1024], F32)
        bxo = [Buf("d_xo%d" % i) for i in range(4)]
        t2 = Ring(B, st, "d_t2", 2, [128, 512], F32)
        sq2 = B.sb(st, "d_sq2", [128, 16], F32); bsq2 = Buf("d_sq2")
        pr = Ring(B, st, "d_pr", 8, [128, 512], F32, psum=True)
        for (u_, bu_) in upad.slots:
            B.op("pool", lambda e, u_=u_: e.memset(u_[:], 0.0), writes=[bu_])
        nblk = dbg.get("c2_blocks", 8)
        for j in range(nblk):
            r0 = 512 * j
            for s in range(5):
                B.dma("sp", xt[:, s, :], self.X1[r0 + s * 128: r0 + (s + 1) * 128, :], bxts[s], writes=[bxts[s]])
            for s in range(4):
                B.dma("sp", xo[:, s, :], self.X1[r0 + 64 + s * 128: r0 + 64 + (s + 1) * 128, :], bxo[s], writes=[bxo[s]])
            self.norm_transpose(xt, bxts, 5, xn, bxn, sq, bsq, junk, bjunk, pr, hxT, bhx, 0, 4)

            def pair_gen(c):
                w, bw = wup.next()
                B.dma("sp", w[:, 0], self.WUPB[c], bw, reads=[self.bwupb], writes=[bw])
                B.dma("sp", w[:, 1], self.WUPB[22 + c], bw, reads=[self.bwupb], writes=[bw])
                ups, dgs = [], []
                for part in range(2):
                    ch = c + 22 * part
                    u_, bu_ = upad.next()
                    dg, bdg = dgr.next()
                    ups.append((u_, bu_))
                    dgs.append((dg, bdg))
                    B.op("dve", lambda e, dg=dg, ch=ch: e.tensor_tensor(out=dg[:], in0=self.ident_b[:].unsqueeze(1).to_broadcast([128, 9, 128]),
                                                                      in1=cw[:, ch, :].unsqueeze(2).to_broadcast([128, 9, 128]), op=ALU.mult),
                         reads=[self.cb, bsm0], writes=[bdg])
                yield
                for part in range(2):
                    u_, bu_ = ups[part]
                    p1, bp1 = pr.next()
                    p2, bp2 = pr.next()
                    for k in range(8):
                        B.op("pe", lambda e, k=k, p1=p1, part=part: e.matmul(p1[:], lhsT=w[:, part, k, :], rhs=hxT[:, k, 0:512], start=(k == 0), stop=(k == 7)),
                             reads=[bw, bhx], writes=[bp1], inc=(k == 7))
                    for k in range(8):
                        B.op("pe", lambda e, k=k, p2=p2, part=part: e.matmul(p2[:, 0:128], lhsT=w[:, part, k, :], rhs=hxT[:, k, 512:640], start=(k == 0), stop=(k == 7)),
                             reads=[bw, bhx], writes=[bp2], inc=(k == 7))
                    B.op("act", lambda e, u_=u_, p1=p1: e.activation(out=u_[:, 0:8, 1:65], in_=p1[:].rearrange("p (r c) -> p r c", c=64), func=AF.Identity),
                         reads=[bp1], writes=[bu_])
                    B.op("act", lambda e, u_=u_, p2=p2: e.activation(out=u_[:, 8:10, 1:65], in_=p2[:, 0:128].rearrange("p (r c) -> p r c", c=64), func=AF.Identity),
                         reads=[bp2], writes=[bu_])
                    if j == 0:
                        B.op("pool", lambda e, u_=u_: e.memset(u_[:, 0:1, :], 0.0), writes=[bu_])
                yield
                pcs = []
                for part in range(2):
                    u_, bu_ = ups[part]
                    dg, bdg = dgs[part]
                    pc, bpc = pr.next()
                    t = 0
                    for dr in range(3):
                        for dc_ in range(3):
                            B.op("pe", lambda e, t=t, dr=dr, dc_=dc_, pc=pc, u_=u_, dg=dg: e.matmul(
                                pc[:].rearrange("p (r c) -> p r c", c=64), lhsT=dg[:, t, :], rhs=u_[:, dr:dr + 8, dc_:dc_ + 64], start=(t == 0), stop=(t == 8)),
                                reads=[bu_, bdg], writes=[bpc], inc=(t == 8))
                            t += 1
                    pcs.append((pc, bpc))
                s_, bs_ = sgt.next()
                B.op("act", lambda e: e.activation(out=s_[:], in_=pcs[0][0][:], func=AF.Silu), reads=[pcs[0][1]], writes=[bs_])
                B.op("dve", lambda e: e.tensor_tensor(out=aT[:, c, :], in0=pcs[1][0][:], in1=s_[:], op=ALU.mult), reads=[pcs[1][1], bs_], writes=[baT])

            self.run_pipeline([(lambda c=c: pair_gen(c)) for c in range(22)], DEPTH)
            for s in range(4):
                for hf in range(2):
                    p, bp = pr.next()
                    for c in range(22):
                        B.op("pe", lambda e, c=c, p=p, s=s, hf=hf: e.matmul(p[:], lhsT=aT[:, c, s * 128:(s + 1) * 128], rhs=wd[:, c, hf * 512:(hf + 1) * 512],
                                                                           start=(c == 0), stop=(c == 21)), reads=[baT, bwd], writes=[bp], inc=(c == 21))
                    t_, bt_ = t2.next()
                    B.op("dve", lambda e, p=p, t_=t_, hf=hf: e.tensor_tensor(out=t_[:], in0=p[:], in1=self.gate_bc[:, 1, hf * 512:(hf + 1) * 512], op=ALU.mult),
                         reads=[bp, self.bgate], writes=[bt_])
                    B.op("pool", lambda e, t_=t_, s=s, hf=hf: e.tensor_tensor(out=xo[:, s, hf * 512:(hf + 1) * 512], in0=xo[:, s, hf * 512:(hf + 1) * 512], in1=t_[:], op=ALU.add),
                         reads=[bt_, bxo[s]], writes=[bxo[s]])
                B.op("act", lambda e, s=s: e.activation(out=junk[:], in_=xo[:, s, :], func=AF.Square, accum_out=sq2[:, s:s + 1]), reads=[bxo[s]], writes=[bjunk, bsq2])
                B.op("dve", lambda e, s=s: e.tensor_scalar(out=sq2[:, 4 + s:5 + s], in0=sq2[:, s:s + 1], scalar1=float(D * EPS), scalar2=None, op0=ALU.add), reads=[bsq2], writes=[bsq2])
                B.op("pool", lambda e, s=s: e.tensor_tensor(out=sq2[:, 8 + s:9 + s], in0=sq2[:, 4 + s:5 + s], in1=self.nhalf[:, 0:1], op=ALU.pow), reads=[bsq2, self.cb], writes=[bsq2])
                B.op("dve", lambda e, s=s: e.scalar_tensor_tensor(out=xo[:, s, :], in0=xo[:, s, :], scalar=sq2[:, 8 + s:9 + s], in1=nob[:], op0=ALU.mult, op1=ALU.mult),
                     reads=[bxo[s], bsq2, bsm0], writes=[bxo[s]])
                B.op("act", lambda e, s=s: e.activation(out=xo[:, s, :], in_=xo[:, s, :], func=AF.Identity, scale=32.0), reads=[bxo[s]], writes=[bxo[s]])
                B.dma("sp", self.out[j * 512 + s * 128: j * 512 + (s + 1) * 128, :], xo[:, s, :], bxo[s], reads=[bxo[s]])
        B.barrier()
        st.close()


def _build_once(debug, needed):
    P = Prog(debug=debug, needed=needed)
    P.precast_wup()
    P.phase0()
    P.phaseA()
    P.phaseB()
    P.phaseC1()
    P.phaseC2()
    P.top.close()
    return P.B.finish(), P


def build_program(debug=None):
    _, dry = _build_once(debug, None)
    return _build_once(debug, dry.B.waited)


_CACHE = {}


def kernel(**inputs):
    inp = {k: np.asarray(v) for k, v in inputs.items()}
    if "nc" not in _CACHE:
        _CACHE["nc"] = build_program()[0]
    nc = _CACHE["nc"]
    in_maps = [prep_core(inp, core) for core in range(8)]
    res = run_bass_kernel_spmd(nc, in_maps, core_ids=list(range(8)))
    out = np.empty((4, T, D), np.float32)
    for core in range(8):
        o = np.asarray(res.results[core]["out"], np.float32)
        b = core // 2
        if core % 2 == 0:
            out[b, 0:4096] = o
        else:
            out[b, 4096:8192] = o[::-1]
    return out
```

```python
import numpy as np
from contextlib import ExitStack

import concourse.bass as bass
import concourse.mybir as mybir
from concourse.bass_utils import run_bass_kernel_spmd

F32 = mybir.dt.float32
BF16 = mybir.dt.bfloat16
AF = mybir.ActivationFunctionType
ALU = mybir.AluOpType
AX = mybir.AxisListType

D = 1024
T = 8192
TC = 256
KD = 8
EPS = 1e-6
NEG = -1.0e30


class Buf:
    __slots__ = ("name", "w", "r", "dsem")

    def __init__(self, name):
        self.name = name
        self.w = None
        self.r = {}
        self.dsem = None


class Builder:
    def __init__(self, needed=None):
        self.nc = bass.Bass("TRN2", target_bir_lowering=False)
        nc = self.nc
        self.dry = needed is None
        self.needed = needed
        self.waited = {}
        self.es = ExitStack()
        self.es.enter_context(nc.allow_low_precision("bf16 matmul operands, fp32 accumulation"))
        self.engs = {"pe": nc.tensor, "act": nc.scalar, "dve": nc.vector, "pool": nc.gpsimd, "sp": nc.sync}
        self.sems = {}
        self.cnt = {}
        self.rank = {}
        self.rank_of = {}
        self.seen = {e: {} for e in self.engs}
        for e in self.engs:
            self.sems[e] = self.es.enter_context(nc.semaphore("s_" + e))
            self.cnt[e] = 0
            self.rank[e] = 0
            self.rank_of[e] = {}
            self.waited[e] = set()
        self.ndsem = 0
        self.nins = 0

    def sb(self, stack, name, shape, dt):
        return stack.enter_context(self.nc.sbuf_tensor(name, list(shape), dt))

    def ps(self, stack, name, shape, dt=F32):
        return stack.enter_context(self.nc.psum_tensor(name, list(shape), dt))

    def dram(self, name, shape, dt, kind="Internal"):
        return self.nc.dram_tensor(name, list(shape), dt, kind=kind).ap()

    def new_dsem(self):
        k = "d%d" % self.ndsem
        self.ndsem += 1
        self.sems[k] = self.es.enter_context(self.nc.semaphore(k))
        self.cnt[k] = 0
        return k

    def _deps(self, eng, reads, writes):
        deps = {}

        def add(k, v):
            if deps.get(k, 0) < v:
                deps[k] = v

        for b in reads:
            if b.w is not None:
                add(*b.w)
        for b in writes:
            if b.w is not None and b.w[0] != eng:
                add(*b.w)
            for k, v in b.r.items():
                if k != eng:
                    add(k, v)
        return deps

    def _emit_waits(self, eng, deps):
        e = self.engs[eng]
        seen = self.seen[eng]
        for k, v in deps.items():
            if seen.get(k, 0) >= v:
                continue
            assert v <= self.cnt[k], "wait on %s=%d never reached (issued %d)" % (k, v, self.cnt[k])
            seen[k] = v
            if k in self.engs:
                if self.dry:
                    self.waited[k].add(v)
                else:
                    e.wait_ge(self.sems[k], self.rank_of[k][v])
            elif not self.dry:
                e.wait_ge(self.sems[k], v)

    def op(self, eng, fn, reads=(), writes=(), inc=True):
        self._emit_waits(eng, self._deps(eng, reads, writes))
        self.cnt[eng] += 1
        idx = self.cnt[eng]
        self.nins += 1
        if not self.dry:
            ins = fn(self.engs[eng])
            if idx in self.needed[eng]:
                self.rank[eng] += 1
                self.rank_of[eng][idx] = self.rank[eng]
                ins.then_inc(self.sems[eng], 1)
        tok = (eng, idx)
        for b in reads:
            if b.r.get(eng, 0) < idx:
                b.r[eng] = idx
        for b in writes:
            b.w = tok
            b.r = {}
        return tok

    def dma(self, q, out, in_, sem_buf, reads=(), writes=()):
        self._emit_waits(q, self._deps("__dma__", reads, writes))
        if sem_buf.dsem is None:
            sem_buf.dsem = self.new_dsem()
        k = sem_buf.dsem
        self.cnt[k] += 16
        self.nins += 1
        if not self.dry:
            ins = self.engs[q].dma_start(out=out, in_=in_)
            ins.then_inc(self.sems[k], 16)
        tok = (k, self.cnt[k])
        for b in reads:
            if b.r.get(k, 0) < tok[1]:
                b.r[k] = tok[1]
        for b in writes:
            b.w = tok
            b.r = {}
        return tok

    def barrier(self):
        for e in self.engs:
            self._emit_waits(e, {k: v for k, v in self.cnt.items() if k != e and v > 0})

    def finish(self):
        self.barrier()
        self.es.close()
        return self.nc


OFF_QKV, OFF_A, OFF_B, OFF_MQ, OFF_MK, OFF_MV, OFF_MI, OFF_MF, OFF_Z, OFF_MO, OFF_GG, OFF_GM, OFF_END = (
    0, 3072, 3088, 3104, 3616, 4128, 5152, 5160, 5168, 6192, 7216, 8240, 9264)
NCH = 66
OWN_T = 4224


def _col(v, n=128):
    v = np.asarray(v, np.float32).reshape(-1, n)
    return np.ascontiguousarray(v.T)


def _rep(v):
    v = np.asarray(v, np.float32).reshape(1, -1)
    return np.ascontiguousarray(np.repeat(v, 128, axis=0))


def _swapdir(a, flip):
    if not flip:
        return a
    h = a.shape[-1] // 2
    return np.concatenate([a[..., h:], a[..., :h]], axis=-1)


def prep_core(inp, core):
    b = core // 2
    flip = core % 2
    f32 = np.float32
    x = inp["x"][b]
    ctx = inp["ctx"][b]
    if flip:
        x = x[::-1]
        ctx = ctx[::-1]
    w_in = inp["w_in"][0]
    m = {}
    m["x"] = np.ascontiguousarray(x, dtype=f32)
    m["ctx"] = np.ascontiguousarray(ctx, dtype=f32)
    m["c_col"] = _col(inp["c"][b])
    m["cc_col"] = _col(inp["c_ctx"])
    m["w_ada"] = np.ascontiguousarray(inp["w_ada"][0], dtype=f32)
    b_ada = inp["b_ada"][0]
    m["b_ada_col"] = _col(b_ada)
    m["b_ada_g"] = np.ascontiguousarray(np.concatenate([_rep(b_ada[2048:3072]), _rep(b_ada[5120:6144])], axis=1))
    m["n1_col"] = _col(inp["norm1_w"][0])
    m["n2_col"] = _col(inp["norm2_w"][0])
    m["w_qkv"] = np.ascontiguousarray(w_in[:, OFF_QKV:OFF_A])
    wg = np.concatenate([_swapdir(w_in[:, OFF_A:OFF_B], flip), _swapdir(w_in[:, OFF_B:OFF_MQ], flip),
                         _swapdir(w_in[:, OFF_MI:OFF_MF], flip), _swapdir(w_in[:, OFF_MF:OFF_Z], flip)], axis=1)
    m["w_gate"] = np.ascontiguousarray(wg)
    m["w_ml"] = np.ascontiguousarray(w_in[:, OFF_MQ:OFF_MI])
    m["w_o"] = np.ascontiguousarray(w_in[:, OFF_Z:OFF_END])
    gp = np.concatenate([_swapdir(inp["gdn_dt_bias"][0].reshape(-1), flip), _swapdir(inp["gdn_a_log"][0].reshape(-1), flip),
                         _swapdir(inp["ml_igate_b"][0].reshape(-1), flip), _swapdir(inp["ml_fgate_b"][0].reshape(-1), flip)])
    m["gate_p"] = _rep(gp)
    gc = inp["gdn_conv"][0]
    if flip:
        gc = gc[::-1]
    m["gdn_cw"] = np.ascontiguousarray(gc.T.reshape(24, 128, 3).transpose(1, 0, 2), dtype=f32)
    fc = inp["ffn_conv"][0]
    if flip:
        fc = fc[::-1, ::-1]
    m["ffn_cw"] = np.ascontiguousarray(fc.reshape(9, 44, 128).transpose(2, 1, 0), dtype=f32)
    m["gnw_bc"] = _rep(np.tile(inp["gdn_norm_w"][0], 8))
    m["mnw_bc"] = _rep(inp["ml_norm_w"][0].reshape(-1))
    m["now_bc"] = _rep(inp["norm_out_w"])
    m["w_bg"] = np.ascontiguousarray(inp["w_branch_gdn"][0], dtype=f32)
    m["w_bm"] = np.ascontiguousarray(inp["w_branch_ml"][0], dtype=f32)
    m["w_out"] = np.ascontiguousarray(inp["w_out"][0], dtype=f32)
    m["w_up"] = np.ascontiguousarray(inp["w_up"][0], dtype=f32)
    m["w_down"] = np.ascontiguousarray(inp["w_down"][0], dtype=f32)
    m["smask"] = make_smask()
    return m


def make_smask():
    idx = np.arange(128)
    i = idx[None, :]
    j = idx[:, None]
    out = np.zeros((128, 14, 128), np.float32)
    for lev in range(7):
        b = 1 << lev
        same = (i // (2 * b)) == (j // (2 * b))
        f = same & ((i % (2 * b)) < b) & ((j % (2 * b)) >= b)
        g = same & ((j % (2 * b)) < b) & ((i % (2 * b)) >= b)
        out[:, lev, :] = np.where(f, -1.0, 0.0) + np.eye(128)
        out[:, 7 + lev, :] = np.where(g, -1.0, 0.0) + np.eye(128)
    return out


IN_SHAPES = {
    "x": [T, D], "ctx": [TC, D], "c_col": [128, 8], "cc_col": [128, 8], "w_ada": [D, 6144],
    "b_ada_col": [128, 48], "b_ada_g": [128, 2048], "n1_col": [128, 8], "n2_col": [128, 8],
    "w_qkv": [D, 3072], "w_gate": [D, 48], "w_ml": [D, 2048], "w_o": [D, 4096], "gate_p": [128, 48],
    "gdn_cw": [128, 24, 3], "ffn_cw": [128, 44, 9], "gnw_bc": [128, 1024], "mnw_bc": [128, 1024],
    "now_bc": [128, 1024], "w_bg": [D, D], "w_bm": [D, D], "w_out": [D, D], "w_up": [D, 5632], "w_down": [2816, D],
    "smask": [128, 14, 128],
}


class Ring:
    def __init__(self, B, stack, name, n, shape, dt, psum=False):
        self.slots = []
        for i in range(n):
            t = (B.ps if psum else B.sb)(stack, "%s%d" % (name, i), shape, dt)
            self.slots.append((t, Buf("%s%d" % (name, i))))
        self.i = 0

    def next(self):
        s = self.slots[self.i % len(self.slots)]
        self.i += 1
        return s


class Prog:
    def __init__(self, debug=None, needed=None):
        self.debug = debug or {}
        self.B = Builder(needed)
        self.nc = self.B.nc
        self.top = ExitStack()
        self.inp = {}
        for k, shp in IN_SHAPES.items():
            self.inp[k] = self.nc.dram_tensor(k, list(shp), F32, kind="ExternalInput").ap()
        self.out = self.nc.dram_tensor("out", [4096, D], F32, kind="ExternalOutput").ap()
        dk = "ExternalOutput" if self.debug.get("scratch_out") else "Internal"
        B = self.B
        self.KT = B.dram("KT", [NCH, 128, 8, 128], BF16, dk)
        self.QT = B.dram("QT", [NCH, 128, 8, 128], BF16, dk)
        self.VG = B.dram("VG", [NCH, 128, 1024], BF16, dk)
        self.MQT = B.dram("MQT", [NCH, 128, 4, 128], BF16, dk)
        self.MKT = B.dram("MKT", [NCH, 128, 4, 128], BF16, dk)
        self.MV = B.dram("MV", [NCH, 128, 1024], BF16, dk)
        self.GT = B.dram("GT", [NCH, 128, 48], F32, dk)
        self.OF = B.dram("OF", [33, 128, 1024], F32, dk)
        self.OB = B.dram("OB", [33, 128, 1024], F32, dk)
        self.HF = B.dram("HF", [33, 128, 1024], F32, dk)
        self.HB = B.dram("HB", [33, 128, 1024], F32, dk)
        self.X1 = B.dram("X1", [64 + OWN_T, D], F32, dk)
        self.consts()

    def consts(self):
        B, st = self.B, self.top
        self.ident_f = B.sb(st, "ident_f", [128, 128], F32)
        self.ident_b = B.sb(st, "ident_b", [128, 128], BF16)
        self.ones_f = B.sb(st, "ones_f", [128, 128], F32)
        self.ones_b = B.sb(st, "ones_b", [128, 128], BF16)
        self.nhalf = B.sb(st, "nhalf", [128, 512], F32)
        self.cb = Buf("consts")
        cb = self.cb
        B.op("pool", lambda e: e.memset(self.ones_f[:], 1.0), writes=[cb])
        B.op("pool", lambda e: e.memset(self.ones_b[:], 1.0), writes=[cb])
        B.op("pool", lambda e: e.memset(self.nhalf[:], -0.5), writes=[cb])
        B.op("pool", lambda e: e.memset(self.ident_f[:], 1.0), writes=[cb])
        B.op("pool", lambda e: e.affine_select(self.ident_f[:], self.ident_f[:], pattern=[[-1, 128]], compare_op=ALU.is_equal,
                                               fill=0.0, base=0, channel_multiplier=1), reads=[cb], writes=[cb])
        B.op("dve", lambda e: e.tensor_copy(out=self.ident_b[:], in_=self.ident_f[:]), reads=[cb], writes=[cb])
        self.modc = B.sb(st, "modc", [128, 6, 8], F32)
        self.bmod = Buf("modc")
        self.gate_bc = B.sb(st, "gate_bc", [128, 2, 1024], F32)
        self.bgate = Buf("gate_bc")

    def mask(self, stack, name, cmp_pat, dt=F32, val=1.0, fill=0.0):
        B = self.B
        base, cm, step, cmp = cmp_pat
        t = B.sb(stack, name, [128, 128], dt)
        tf = t
        if dt != F32:
            tf = B.sb(stack, name + "_f", [128, 128], F32)
        b = Buf(name)
        B.op("pool", lambda e: e.memset(tf[:], val), writes=[b])
        B.op("pool", lambda e: e.affine_select(tf[:], tf[:], pattern=[[step, 128]], compare_op=cmp, fill=fill,
                                               base=base, channel_multiplier=cm), reads=[b], writes=[b])
        if dt != F32:
            B.op("dve", lambda e: e.tensor_copy(out=t[:], in_=tf[:]), reads=[b], writes=[b])
        return t, b

    def phase0(self):
        B, nc, inp = self.B, self.nc, self.inp
        st = ExitStack()
        sc = B.sb(st, "p0_sc", [128, 16], F32)
        bsc = Buf("p0_sc")
        scb = B.sb(st, "p0_scb", [128, 8, 128], F32)
        bscb = Buf("p0_scb")
        bcol = B.sb(st, "p0_bcol", [128, 48], F32)
        n12 = B.sb(st, "p0_n12", [128, 16], F32)
        bg = B.sb(st, "p0_bg", [128, 2048], F32)
        bsm = Buf("p0_small")
        B.dma("sp", sc[:, 0:8], inp["c_col"][:, :], bsc, writes=[bsc])
        B.dma("sp", sc[:, 8:16], inp["cc_col"][:, :], bsc, writes=[bsc])
        B.dma("sp", bcol[:], inp["b_ada_col"][:, :], bsm, writes=[bsm])
        B.dma("sp", n12[:, 0:8], inp["n1_col"][:, :], bsm, writes=[bsm])
        B.dma("sp", n12[:, 8:16], inp["n2_col"][:, :], bsm, writes=[bsm])
        B.dma("sp", bg[:], inp["b_ada_g"][:, :], bsm, writes=[bsm])
        B.op("act", lambda e: e.activation(out=sc[:], in_=sc[:], func=AF.Silu), reads=[bsc], writes=[bsc])
        for k in range(8):
            B.op("dve", lambda e, k=k: e.tensor_scalar(out=scb[:, k, :], in0=self.ones_f[:], scalar1=sc[:, k:k + 1], scalar2=None,
                                                       op0=ALU.mult), reads=[bsc, self.cb], writes=[bscb])
        wring = Ring(B, st, "p0_w", 2, [128, 8, 512], F32)
        pcol = B.ps(st, "p0_pcol", [128, 64], F32)
        bpcol = Buf("p0_pcol")
        prow = Ring(B, st, "p0_prow", 2, [128, 512], F32, psum=True)
        wv = inp["w_ada"].rearrange("(k p) n -> p k n", p=128)
        xslot = {0: 0, 1: 1, 3: 2, 4: 3}
        for nb in range(12):
            v, half = nb // 2, nb % 2
            w, bw = wring.next()
            B.dma("sp", w[:], wv[:, :, nb * 512:(nb + 1) * 512], bw, writes=[bw])
            if v in (2, 5):
                p, bp = prow.next()
                for k in range(8):
                    B.op("pe", lambda e, k=k, p=p, w=w: e.matmul(p[:], lhsT=scb[:, k, :], rhs=w[:, k, :], start=(k == 0), stop=(k == 7)),
                         reads=[bscb, bw], writes=[bp], inc=(k == 7))
                gi = 0 if v == 2 else 1
                B.op("dve", lambda e, p=p, gi=gi, half=half: e.tensor_tensor(
                    out=self.gate_bc[:, gi, half * 512:(half + 1) * 512], in0=p[:], in1=bg[:, gi * 1024 + half * 512: gi * 1024 + (half + 1) * 512],
                    op=ALU.add), reads=[bp, bsm], writes=[self.bgate])
            else:
                for cc in range(4):
                    col = xslot[v] * 8 + half * 4 + cc
                    for k in range(8):
                        B.op("pe", lambda e, k=k, w=w, cc=cc, col=col: e.matmul(pcol[:, col:col + 1], lhsT=w[:, k, cc * 128:(cc + 1) * 128],
                                                                                 rhs=sc[:, k:k + 1], start=(k == 0), stop=(k == 7)),
                             reads=[bw, bsc], writes=[bpcol], inc=(k == 7))
                    if v in (0, 1):
                        col2 = 32 + v * 8 + half * 4 + cc
                        for k in range(8):
                            B.op("pe", lambda e, k=k, w=w, cc=cc, col2=col2: e.matmul(pcol[:, col2:col2 + 1], lhsT=w[:, k, cc * 128:(cc + 1) * 128],
                                                                                       rhs=sc[:, 8 + k:9 + k], start=(k == 0), stop=(k == 7)),
                                 reads=[bw, bsc], writes=[bpcol], inc=(k == 7))
        mc = B.sb(st, "p0_mc", [128, 6, 8], F32)
        bmc = Buf("p0_mc")
        for i, v in enumerate((0, 1, 3, 4)):
            B.op("dve", lambda e, i=i, v=v: e.tensor_tensor(out=mc[:, i, :], in0=pcol[:, i * 8:(i + 1) * 8], in1=bcol[:, v * 8:(v + 1) * 8], op=ALU.add),
                 reads=[bpcol, bsm], writes=[bmc])
        for i, v in enumerate((0, 1)):
            B.op("dve", lambda e, i=i, v=v: e.tensor_tensor(out=mc[:, 4 + i, :], in0=pcol[:, 32 + i * 8:32 + (i + 1) * 8], in1=bcol[:, v * 8:(v + 1) * 8],
                                                            op=ALU.add), reads=[bpcol, bsm], writes=[bmc])
        md = self.modc
        for dst, (sci, shi, nw) in {0: (1, 0, 0), 2: (5, 4, 0), 4: (3, 2, 1)}.items():
            B.op("dve", lambda e, dst=dst, sci=sci, nw=nw: e.scalar_tensor_tensor(out=md[:, dst, :], in0=mc[:, sci, :], scalar=1.0, in1=n12[:, nw * 8:(nw + 1) * 8],
                                                                                  op0=ALU.add, op1=ALU.mult), reads=[bmc, bsm], writes=[self.bmod])
            B.op("dve", lambda e, dst=dst, shi=shi: e.tensor_copy(out=md[:, dst + 1, :], in_=mc[:, shi, :]), reads=[bmc], writes=[self.bmod])
        B.barrier()
        st.close()

    @staticmethod
    def run_pipeline(makers, depth):
        active = []
        it = iter(makers)
        exhausted = False
        while True:
            for g in list(active):
                try:
                    next(g)
                except StopIteration:
                    active.remove(g)
            if not exhausted and len(active) < depth:
                try:
                    g = next(it)()
                    try:
                        next(g)
                        active.append(g)
                    except StopIteration:
                        pass
                except StopIteration:
                    exhausted = True
            if exhausted and not active:
                break

    def load_w_bf16(self, stack, name, ap, kchunks, ncols, nsplit=4):
        B = self.B
        t = B.sb(stack, name, [128, kchunks, ncols], BF16)
        b = Buf(name)
        v = ap.rearrange("(k p) n -> p k n", p=128)
        step = (ncols + nsplit - 1) // nsplit
        for i in range(0, ncols, step):
            j = min(ncols, i + step)
            B.dma("pool", t[:, :, i:j], v[:, :, i:j], b, writes=[b])
        return t, b

    def norm_transpose(self, xt, bxts, ns, xn, bxn, sq, bsq, junk, bjunk, ptr_ring, hxT, bhx, col0, ai, npart=128):
        B = self.B
        for s in range(ns):
            B.op("act", lambda e, s=s: e.activation(out=xn[0:npart, s, :], in_=xt[0:npart, s, :], func=AF.Square, accum_out=sq[0:npart, s:s + 1]),
                 reads=[bxts[s]], writes=[bxn, bsq])
        B.op("dve", lambda e: e.tensor_scalar(out=sq[0:npart, 8:8 + ns], in0=sq[0:npart, 0:ns], scalar1=float(D * EPS), scalar2=None, op0=ALU.add),
             reads=[bsq], writes=[bsq])
        B.op("pool", lambda e: e.tensor_tensor(out=sq[0:npart, 16:16 + ns], in0=sq[0:npart, 8:8 + ns], in1=self.nhalf[0:npart, 0:ns], op=ALU.pow),
             reads=[bsq, self.cb], writes=[bsq])
        for s in range(ns):
            B.op("dve", lambda e, s=s: e.tensor_scalar(out=xn[0:npart, s, :], in0=xt[0:npart, s, :], scalar1=sq[0:npart, 16 + s:17 + s], scalar2=32.0,
                                                       op0=ALU.mult, op1=ALU.mult), reads=[bxts[s], bsq], writes=[bxn])
        for k in range(KD):
            p, bp = ptr_ring.next()
            pb = p[:].bitcast(BF16)
            for s in range(ns):
                B.op("pe", lambda e, s=s, k=k, pb=pb: e.transpose(pb[:, s * npart:(s + 1) * npart], xn[0:npart, s, k * 128:(k + 1) * 128],
                                                                  self.ident_b[0:npart, 0:npart]),
                     reads=[bxn, self.cb], writes=[bp], inc=(s == ns - 1))
            B.op("act", lambda e, k=k, pb=pb: e.activation(out=hxT[:, k, col0:col0 + ns * npart], in_=pb[:, 0:ns * npart], func=AF.Identity,
                                                           scale=self.modc[:, ai, k:k + 1], bias=self.modc[:, ai + 1, k:k + 1]),
                 reads=[bp, self.bmod], writes=[bhx])

    def phaseA(self):
        B, nc, inp = self.B, self.nc, self.inp
        st = ExitStack()
        wqkv, bwqkv = self.load_w_bf16(st, "a_wqkv", inp["w_qkv"], 8, 3072, 6)
        wml, bwml = self.load_w_bf16(st, "a_wml", inp["w_ml"], 8, 2048, 4)
        wgt, bwgt = self.load_w_bf16(st, "a_wgt", inp["w_gate"], 8, 48, 1)
        cw = B.sb(st, "a_cw", [128, 24, 3], F32)
        gp = B.sb(st, "a_gp", [128, 48], F32)
        bsm = Buf("a_small")
        B.dma("sp", cw[:], inp["gdn_cw"][:, :, :], bsm, writes=[bsm])
        B.dma("sp", gp[:], inp["gate_p"][:, :], bsm, writes=[bsm])
        B.op("act", lambda e: e.activation(out=gp[:, 16:32], in_=gp[:, 16:32], func=AF.Exp), reads=[bsm], writes=[bsm])
        B.op("dve", lambda e: e.tensor_scalar(out=gp[:, 16:32], in0=gp[:, 16:32], scalar1=-1.0, scalar2=None, op0=ALU.mult), reads=[bsm], writes=[bsm])
        xt = B.sb(st, "a_x", [128, 4, 1024], F32)
        bxts = [Buf("a_x%d" % i) for i in range(4)]
        xh = B.sb(st, "a_xh", [2, 1, 1024], F32); bxh = Buf("a_xh")
        xn = B.sb(st, "a_xn", [128, 4, 1024], BF16); bxn = Buf("a_xn")
        xnh = B.sb(st, "a_xnh", [2, 1, 1024], BF16); bxnh = Buf("a_xnh")
        sq = B.sb(st, "a_sq", [128, 24], F32); bsq = Buf("a_sq")
        sqh = B.sb(st, "a_sqh", [128, 24], F32); bsqh = Buf("a_sqh")
        junk = None; bjunk = None
        hxT = B.sb(st, "a_hxT", [128, 8, 514], BF16); bhx = Buf("a_hxT")
        hxh = B.sb(st, "a_hxh", [128, 8, 2], BF16); bhxh = Buf("a_hxh")
        ptr = Ring(B, st, "a_ptr", 2, [128, 512], F32, psum=True)
        pz = Ring(B, st, "a_pz", 4, [128, 512], F32, psum=True)
        pmisc = B.ps(st, "a_pmisc", [128, 512], F32)
        pzh = pmisc[:, 0:64]; bpzh = Buf("a_pzh")
        pn = ptr
        zb = Ring(B, st, "a_zb", 4, [128, 514], F32)
        y1 = Ring(B, st, "a_y1", 4, [128, 512], F32)
        sqb = Ring(B, st, "a_sqb", 2, [128, 512], BF16)
        skeep = B.sb(st, "a_skeep", [128, 8, 512], BF16)
        bskeep = [Buf("a_skeep%d" % i) for i in range(8)]
        rnr = B.sb(st, "a_rnr", [8, 512], F32); brnr = Buf("a_rnr")
        ind = B.sb(st, "a_ind", [128, 8, 8], BF16); bind = Buf("a_ind")
        selr = B.sb(st, "a_selr", [8, 8, 128], F32); bselr = Buf("a_selr")
        B.op("pool", lambda e: e.memset(ind[:], 0.0), writes=[bind])
        for jj in range(8):
            B.op("pool", lambda e, jj=jj: e.memset(ind[:, jj, jj:jj + 1], 1.0), writes=[bind])
            B.op("dve", lambda e, jj=jj: e.tensor_copy(out=selr[:, jj, :], in_=self.ident_f[0:8, jj:jj + 1].to_broadcast([8, 128])), reads=[self.cb], writes=[bselr])
        pss = B.ps(st, "a_pss", [128, 512], F32); bpss = Buf("a_pss")
        kst = B.sb(st, "a_kst", [128, 4, 8, 128], BF16); bkst = Buf("a_kst")
        qst = B.sb(st, "a_qst", [128, 4, 8, 128], BF16); bqst = Buf("a_qst")
        vT = B.sb(st, "a_vT", [128, 8, 512], BF16); bvT = Buf("a_vT")
        vst = Ring(B, st, "a_vst", 1, [128, 4, 1024], BF16)
        mqst = B.sb(st, "a_mqst", [128, 4, 4, 128], BF16); bmqst = Buf("a_mqst")
        mkst = B.sb(st, "a_mkst", [128, 4, 4, 128], BF16); bmkst = Buf("a_mkst")
        graw = B.sb(st, "a_graw", [128, 4, 48], F32); bgraw = Buf("a_graw")
        gwk = B.sb(st, "a_gwk", [128, 4, 48], F32); bgwk = Buf("a_gwk")
        gsb = Ring(B, st, "a_gsb", 2, [128, 4, 48], F32)
        pg = pmisc[:, 64:256].rearrange("p (s g) -> p s g", g=48); bpg = Buf("a_pg")
        dkr = float(128 ** -0.5)

        tiles = [(inp["ctx"], 0, 2, 0, False, False)]
        for i in range(16):
            tiles.append((inp["x"], i * 512, 4, 2 + 4 * i, i > 0, i < 15))
        if self.debug.get("a_tiles"):
            tiles = tiles[: self.debug["a_tiles"]]
        for (src, t0, ns, c0, hl, hr) in tiles:
            n = ns * 128
            ai = 2 if src is inp["ctx"] else 0
            for s in range(ns):
                B.dma("sp", xt[:, s, :], src[t0 + s * 128:t0 + (s + 1) * 128, :], bxts[s], writes=[bxts[s]])
            tl = t0 - 1 if hl else t0
            tr = t0 + n if hr else t0
            B.dma("sp", xh[0:1, 0, :], src[tl:tl + 1, :], bxh, writes=[bxh])
            B.dma("sp", xh[1:2, 0, :], src[tr:tr + 1, :], bxh, writes=[bxh])
            self.norm_transpose(xt, bxts, ns, xn, bxn, sq, bsq, junk, bjunk, ptr, hxT, bhx, 1, ai)
            self.norm_transpose(xh, [bxh], 1, xnh, bxnh, sqh, bsqh, junk, bjunk, ptr, hxh, bhxh, 0, ai, npart=2)
            def chunk_gen(j, kind, jj):
                p, bp = pz.next()
                z, bz = zb.next()
                a1, ba1 = y1.next()
                for k in range(8):
                    B.op("pe", lambda e, k=k: e.matmul(p[:, 0:n], lhsT=wqkv[:, k, j * 128:(j + 1) * 128], rhs=hxT[:, k, 1:1 + n],
                                                         start=(k == 0), stop=(k == 7)), reads=[bwqkv, bhx], writes=[bp], inc=(k == 7))
                for k in range(8):
                    B.op("pe", lambda e, k=k: e.matmul(pzh[:, 2 * j:2 * j + 2], lhsT=wqkv[:, k, j * 128:(j + 1) * 128], rhs=hxh[:, k, :],
                                                         start=(k == 0), stop=(k == 7)), reads=[bwqkv, bhxh], writes=[bpzh], inc=(k == 7))
                yield
                B.op("act", lambda e: e.activation(out=z[:, 1:1 + n], in_=p[:, 0:n], func=AF.Identity), reads=[bp], writes=[bz])
                B.op("act", lambda e: e.activation(out=z[:, 0:1], in_=pzh[:, 2 * j:2 * j + 1], func=AF.Identity), reads=[bpzh], writes=[bz])
                B.op("act", lambda e: e.activation(out=z[:, n + 1:n + 2], in_=pzh[:, 2 * j + 1:2 * j + 2], func=AF.Identity), reads=[bpzh], writes=[bz])
                if not hl:
                    B.op("pool", lambda e: e.memset(z[:, 0:1], 0.0), writes=[bz])
                if not hr:
                    B.op("pool", lambda e: e.memset(z[:, n + 1:n + 2], 0.0), writes=[bz])
                yield
                B.op("dve", lambda e: e.tensor_scalar(out=a1[:, 0:n], in0=z[:, 1:1 + n], scalar1=cw[:, j, 1:2], scalar2=None, op0=ALU.mult),
                     reads=[bz, bsm], writes=[ba1])
                B.op("dve", lambda e: e.scalar_tensor_tensor(out=a1[:, 0:n], in0=z[:, 0:n], scalar=cw[:, j, 0:1], in1=a1[:, 0:n],
                                                            op0=ALU.mult, op1=ALU.add), reads=[bz, bsm, ba1], writes=[ba1])
                B.op("dve", lambda e: e.scalar_tensor_tensor(out=a1[:, 0:n], in0=z[:, 2:2 + n], scalar=cw[:, j, 2:3], in1=a1[:, 0:n],
                                                            op0=ALU.mult, op1=ALU.add), reads=[bz, bsm, ba1], writes=[ba1])
                yield
                if kind == "v":
                    B.op("act", lambda e: e.activation(out=vT[:, jj, 0:n], in_=a1[:, 0:n], func=AF.Silu), reads=[ba1], writes=[bvT])
                else:
                    B.op("act", lambda e: e.activation(out=skeep[:, jj, 0:n], in_=a1[:, 0:n], func=AF.Silu), reads=[ba1], writes=[bskeep[jj]])
                    q2, bq2 = sqb.next()
                    B.op("pool", lambda e: e.tensor_tensor(out=q2[:, 0:n], in0=skeep[:, jj, 0:n], in1=skeep[:, jj, 0:n], op=ALU.mult),
                         reads=[bskeep[jj]], writes=[bq2])
                    B.op("pe", lambda e: e.matmul(pss[0:8, 0:n], lhsT=ind[:, jj, :], rhs=q2[:, 0:n], start=(jj == 0), stop=(jj == 7)),
                         reads=[bq2, bind], writes=[bpss])

            for half in range(2):
                self.run_pipeline([(lambda jj=jj: chunk_gen(half * 8 + jj, "qk", jj)) for jj in range(8)], 4)
                B.op("act", lambda e: e.activation(out=rnr[:, 0:n], in_=pss[0:8, 0:n], func=AF.Ln, bias=float(EPS)), reads=[bpss], writes=[brnr])
                B.op("act", lambda e: e.activation(out=rnr[:, 0:n], in_=rnr[:, 0:n], func=AF.Exp, scale=-0.5), reads=[brnr], writes=[brnr])
                for jj in range(8):
                    pp, bpp = pn.next()
                    B.op("pe", lambda e, pp=pp, jj=jj: e.matmul(pp[:, 0:n], lhsT=selr[:, jj, :], rhs=rnr[:, 0:n], start=True, stop=True),
                         reads=[brnr, bselr], writes=[bpp])
                    if half == 0:
                        B.op("dve", lambda e, pp=pp, jj=jj: e.scalar_tensor_tensor(
                            out=qst[:, 0:ns, jj, :], in0=skeep[:, jj, 0:n].rearrange("p (s t) -> p s t", t=128), scalar=dkr,
                            in1=pp[:, 0:n].rearrange("p (s t) -> p s t", t=128), op0=ALU.mult, op1=ALU.mult), reads=[bskeep[jj], bpp], writes=[bqst])
                    else:
                        B.op("dve", lambda e, pp=pp, jj=jj: e.tensor_tensor(
                            out=kst[:, 0:ns, jj, :], in0=skeep[:, jj, 0:n].rearrange("p (s t) -> p s t", t=128),
                            in1=pp[:, 0:n].rearrange("p (s t) -> p s t", t=128), op=ALU.mult), reads=[bskeep[jj], bpp], writes=[bkst])
            for s in range(ns):
                for k in range(8):
                    B.op("pe", lambda e, k=k, s=s: e.matmul(pg[:, s, :], lhsT=hxT[:, k, 1 + s * 128:1 + (s + 1) * 128], rhs=wgt[:, k, :],
                                                             start=(k == 0), stop=(k == 7)), reads=[bhx, bwgt], writes=[bpg], inc=(k == 7))
            g, bg_ = gsb.next()
            self.gate_math(pg, bpg, graw, bgraw, gwk, bgwk, g, bg_, gp, bsm, ns)
            B.dma("sp", self.GT[c0:c0 + ns].rearrange("c t g -> t c g"), g[:, 0:ns, :], bg_, reads=[bg_])
            self.run_pipeline([(lambda jj=jj: chunk_gen(16 + jj, "v", jj)) for jj in range(8)], 4)
            self.v_transposes(vT, bvT, ns, vst, pn, self.VG, c0)
            B.dma("sp", self.KT[c0:c0 + ns].rearrange("c d h t -> d c h t"), kst[:, 0:ns], bkst, reads=[bkst])
            B.dma("sp", self.QT[c0:c0 + ns].rearrange("c d h t -> d c h t"), qst[:, 0:ns], bqst, reads=[bqst])
            for j in range(16):
                p, bp = pz.next()
                for k in range(8):
                    B.op("pe", lambda e, k=k, p=p, j=j: e.matmul(p[:, 0:n], lhsT=wml[:, k, j * 128:(j + 1) * 128], rhs=hxT[:, k, 1:1 + n],
                                                                  start=(k == 0), stop=(k == 7)), reads=[bwml, bhx], writes=[bp], inc=(k == 7))
                if j < 4:
                    B.op("act", lambda e, p=p, j=j: e.activation(out=mqst[:, 0:ns, j, :], in_=p[:, 0:n].rearrange("p (s t) -> p s t", t=128),
                                                                  func=AF.Identity, scale=dkr), reads=[bp], writes=[bmqst])
                elif j < 8:
                    B.op("act", lambda e, p=p, j=j: e.activation(out=mkst[:, 0:ns, j - 4, :], in_=p[:, 0:n].rearrange("p (s t) -> p s t", t=128),
                                                                  func=AF.Identity), reads=[bp], writes=[bmkst])
                else:
                    B.op("act", lambda e, p=p, j=j: e.activation(out=vT[:, j - 8, 0:n], in_=p[:, 0:n], func=AF.Identity), reads=[bp], writes=[bvT])
            self.v_transposes(vT, bvT, ns, vst, pn, self.MV, c0)
            B.dma("sp", self.MQT[c0:c0 + ns].rearrange("c d h t -> d c h t"), mqst[:, 0:ns], bmqst, reads=[bmqst])
            B.dma("sp", self.MKT[c0:c0 + ns].rearrange("c d h t -> d c h t"), mkst[:, 0:ns], bmkst, reads=[bmkst])
        B.barrier()
        st.close()

    def v_transposes(self, vT, bvT, ns, vst, pn, dst, c0):
        B = self.B
        v, bv = vst.next()
        for s in range(ns):
            pp, bpp = pn.next()
            ppb = pp[:].bitcast(BF16)
            for h in range(8):
                B.op("pe", lambda e, s=s, h=h, ppb=ppb: e.transpose(ppb[:, h * 128:(h + 1) * 128], vT[:, h, s * 128:(s + 1) * 128], self.ident_b[:]),
                     reads=[bvT, self.cb], writes=[bpp], inc=(h == 7))
            B.op("act", lambda e, s=s, ppb=ppb, v=v: e.activation(out=v[:, s, :], in_=ppb[:, 0:1024], func=AF.Identity), reads=[bpp], writes=[bv])
        B.dma("sp", dst[c0:c0 + ns].rearrange("c t e -> t c e"), v[:, 0:ns, :], bv, reads=[bv])

    def gate_math(self, pg, bpg, graw, bgraw, wk, bwk, g, bg_, gp, bgp, ns):
        B = self.B
        S = slice(0, ns)

        def bc(lo, hi):
            return gp[:, lo:hi].unsqueeze(1).to_broadcast([128, ns, hi - lo])

        B.op("act", lambda e: e.activation(out=graw[:, S, :], in_=pg[:, S, :], func=AF.Identity), reads=[bpg], writes=[bgraw])
        B.op("dve", lambda e: e.tensor_tensor(out=wk[:, S, 0:16], in0=graw[:, S, 0:16], in1=bc(0, 16), op=ALU.add), reads=[bgraw, bgp], writes=[bwk])
        B.op("act", lambda e: e.activation(out=wk[:, S, 0:16], in_=wk[:, S, 0:16], func=AF.Exp), reads=[bwk], writes=[bwk])
        B.op("act", lambda e: e.activation(out=wk[:, S, 0:16], in_=wk[:, S, 0:16], func=AF.Ln, bias=1.0), reads=[bwk], writes=[bwk])
        B.op("dve", lambda e: e.tensor_tensor(out=g[:, S, 0:16], in0=wk[:, S, 0:16], in1=bc(16, 32), op=ALU.mult), reads=[bwk, bgp], writes=[bg_])
        B.op("act", lambda e: e.activation(out=wk[:, S, 16:32], in_=graw[:, S, 16:32], func=AF.Exp, scale=-1.0), reads=[bgraw], writes=[bwk])
        B.op("dve", lambda e: e.tensor_scalar(out=wk[:, S, 16:32], in0=wk[:, S, 16:32], scalar1=1.0, scalar2=None, op0=ALU.add), reads=[bwk], writes=[bwk])
        B.op("dve", lambda e: e.reciprocal(out=g[:, S, 16:32], in_=wk[:, S, 16:32]), reads=[bwk], writes=[bg_])
        B.op("dve", lambda e: e.tensor_tensor(out=wk[:, S, 32:48], in0=graw[:, S, 32:48], in1=bc(32, 48), op=ALU.add), reads=[bgraw, bgp], writes=[bwk])
        B.op("act", lambda e: e.activation(out=wk[:, S, 32:48], in_=wk[:, S, 32:48], func=AF.Exp, scale=float(2.0 / 15.0)), reads=[bwk], writes=[bwk])
        B.op("dve", lambda e: e.tensor_scalar(out=wk[:, S, 32:48], in0=wk[:, S, 32:48], scalar1=1.0, scalar2=None, op0=ALU.add), reads=[bwk], writes=[bwk])
        B.op("dve", lambda e: e.reciprocal(out=wk[:, S, 32:48], in_=wk[:, S, 32:48]), reads=[bwk], writes=[bwk])
        B.op("dve", lambda e: e.tensor_scalar(out=g[:, S, 32:48], in0=wk[:, S, 32:48], scalar1=-30.0, scalar2=15.0, op0=ALU.mult, op1=ALU.add),
             reads=[bwk], writes=[bg_])
        B.op("act", lambda e: e.activation(out=wk[:, S, 40:48], in_=g[:, S, 40:48], func=AF.Exp, scale=-1.0), reads=[bg_], writes=[bwk])
        B.op("act", lambda e: e.activation(out=wk[:, S, 40:48], in_=wk[:, S, 40:48], func=AF.Ln, bias=1.0), reads=[bwk], writes=[bwk])
        B.op("dve", lambda e: e.tensor_scalar(out=g[:, S, 40:48], in0=wk[:, S, 40:48], scalar1=-1.0, scalar2=None, op0=ALU.mult), reads=[bwk], writes=[bg_])

    def phaseB(self):
        B, nc, inp = self.B, self.nc, self.inp
        st = ExitStack()
        dbg = self.debug
        LE, bLE = self.mask(st, "b_LE", (0, -1, 1, ALU.is_ge))
        LT, bLT = self.mask(st, "b_LT", (-1, -1, 1, ALU.is_ge))
        GE, bGE = self.mask(st, "b_GE", (0, 1, -1, ALU.is_ge))
        GT_, bGT = self.mask(st, "b_GT", (-1, 1, -1, ALU.is_ge))
        MBf, bMBf = self.mask(st, "b_MBf", (0, 1, -1, ALU.is_ge), val=0.0, fill=NEG)
        MBb, bMBb = self.mask(st, "b_MBb", (0, -1, 1, ALU.is_ge), val=0.0, fill=NEG)
        SELf, bSELf = self.mask(st, "b_SELf", (-127, 1, 0, ALU.is_equal))
        SELb, bSELb = self.mask(st, "b_SELb", (0, 1, 0, ALU.is_equal))
        smask = B.sb(st, "b_smask", [128, 14, 128], BF16)
        bsm = Buf("b_smask")
        B.dma("pool", smask[:], inp["smask"][:, :, :], bsm, writes=[bsm])
        cbufs = [bLE, bLT, bGE, bGT, bMBf, bMBb, bSELf, bSELb, bsm, self.cb]
        dirc = [dict(U=LE, S=GT_, incl=LE, strict=LT, MB=MBf, SEL=SELf),
                dict(U=GE, S=LT, incl=GE, strict=GT_, MB=MBb, SEL=SELb)]
        S = [B.sb(st, "b_S%d" % d, [128, 8, 128], F32) for d in range(2)]
        bS = [[Buf("b_S%d_%d" % (d, g)) for g in range(2)] for d in range(2)]
        Sb = [[Ring(B, st, "b_Sb%d_%d_" % (d, g), 2, [128, 4, 128], BF16) for g in range(2)] for d in range(2)]
        Sb_cur = [[None, None], [None, None]]
        C = [B.sb(st, "b_C%d" % d, [128, 4, 256], F32) for d in range(2)]
        bC = [Buf("b_C%d" % d) for d in range(2)]
        Cb = [Ring(B, st, "b_Cb%d_" % d, 2, [128, 4, 256], BF16) for d in range(2)]
        Cb_cur = [None, None]
        nst = [Ring(B, st, "b_n%d_" % d, 2, [128, 8], F32) for d in range(2)]
        nbf = [Ring(B, st, "b_nb%d_" % d, 2, [128, 4], BF16) for d in range(2)]
        n_cur = [None, None]
        nb_cur = [None, None]
        mst = [Ring(B, st, "b_m%d_" % d, 2, [128, 4], F32) for d in range(2)]
        m_cur = [None, None]
        for d in range(2):
            B.op("pool", lambda e, d=d: e.memset(S[d][:], 0.0), writes=bS[d])
            B.op("pool", lambda e, d=d: e.memset(C[d][:], 0.0), writes=[bC[d]])
            for g in range(2):
                t, b = Sb[d][g].next()
                B.op("pool", lambda e, t=t: e.memset(t[:], 0.0), writes=[b])
                Sb_cur[d][g] = (t, b)
            t, b = Cb[d].next()
            B.op("pool", lambda e, t=t: e.memset(t[:], 0.0), writes=[b])
            Cb_cur[d] = (t, b)
            t, b = nst[d].next()
            B.op("pool", lambda e, t=t: e.memset(t[:], 0.0), writes=[b])
            n_cur[d] = (t, b)
            t, b = nbf[d].next()
            B.op("pool", lambda e, t=t: e.memset(t[:], 0.0), writes=[b])
            nb_cur[d] = (t, b)
            t, b = mst[d].next()
            B.op("pool", lambda e, t=t: e.memset(t[:], 0.0), writes=[b])
            m_cur[d] = (t, b)
        def dring(name, shape, dt):
            return [Ring(B, st, "b_%s%d_" % (name, d), 2, shape, dt) for d in range(2)]
        rKT = dring("KT", [128, 8, 128], BF16)
        rQT = dring("QT", [128, 8, 128], BF16)
        rVG = dring("VG", [128, 1024], BF16)
        rGT = dring("GT", [128, 48], F32)
        rMQ = dring("MQ", [128, 4, 128], BF16)
        rMK = dring("MK", [128, 4, 128], BF16)
        rMV = dring("MV", [128, 1024], BF16)
        rgs = dring("gs", [128, 64], F32)
        psr = Ring(B, st, "b_ps", 8, [128, 512], F32, psum=True)
        NG, NM = 4, 2
        gslots = []
        for i in range(NG):
            sl = {}
            for nm, shp, dt in (("A", [128, 4, 128], F32), ("Bt", [128, 4, 128], F32), ("Ct", [128, 4, 128], F32),
                                ("attnT", [128, 4, 128], BF16), ("Qp", [128, 4, 128], BF16),
                                ("Kg", [128, 4, 128], BF16), ("kt", [128, 4, 128], BF16), ("G0", [128, 4, 128], BF16),
                                ("G1", [128, 4, 128], BF16), ("H0", [128, 4, 128], BF16), ("H1", [128, 4, 128], BF16),
                                ("IYT", [128, 4, 128], BF16), ("negW", [128, 4, 128], BF16), ("vnew", [128, 4, 128], BF16)):
                sl[nm] = (B.sb(st, "b_g%d_%s" % (i, nm), shp, dt), Buf("b_g%d_%s" % (i, nm)))
            gslots.append(sl)
        mslots = []
        for i in range(NM):
            sl = {}
            for nm, shp, dt in (("X", [128, 4, 128], F32), ("Y", [128, 4, 128], F32), ("Pm", [128, 4, 128], BF16),
                                ("PT", [128, 4, 128], BF16), ("Kw", [128, 4, 128], BF16), ("sm", [128, 64], F32), ("Ct", [128, 4, 256], F32)):
                sl[nm] = (B.sb(st, "b_m%d_%s" % (i, nm), shp, dt), Buf("b_m%d_%s" % (i, nm)))
            mslots.append(sl)
        ring_o1 = Ring(B, st, "b_o1_", 2, [128, 4, 128], F32)
        ring_o = Ring(B, st, "b_o_", 2, [128, 4, 128], F32)
        ring_num = Ring(B, st, "b_num_", 1, [128, 4, 256], F32)
        ring_h = Ring(B, st, "b_h_", 1, [128, 4, 256], F32)

        def bc3(ap2, n):
            return ap2.unsqueeze(2).to_broadcast([128, 4, n])

        def bcm(ap2, n=4):
            return ap2.unsqueeze(1).to_broadcast([128, n, 128])

        nsteps = dbg.get("b_steps", NCH)
        order = [list(range(NCH)), [1, 0] + list(range(NCH - 1, 1, -1))]
        if dbg.get("b_order"):
            order = dbg["b_order"]
            nsteps = len(order[0])
        out_lo, out_hi = 2, 2 + 33

        data = {}

        def load_step(step, d):
            c = order[d][step]
            tk, bk = rKT[d].next(); tq, bq = rQT[d].next(); tv, bv = rVG[d].next(); tg, bg = rGT[d].next()
            tmq, bmq = rMQ[d].next(); tmk, bmk = rMK[d].next(); tmv, bmv = rMV[d].next()
            B.dma("sp", tg[:], self.GT[c], bg, writes=[bg])
            B.dma("sp", tk[:], self.KT[c], bk, writes=[bk])
            B.dma("sp", tq[:], self.QT[c], bq, writes=[bq])
            B.dma("sp", tv[:], self.VG[c], bv, writes=[bv])
            B.dma("sp", tmq[:], self.MQT[c], bmq, writes=[bmq])
            B.dma("sp", tmk[:], self.MKT[c], bmk, writes=[bmk])
            B.dma("sp", tmv[:], self.MV[c], bmv, writes=[bmv])
            data[(step, d)] = dict(c=c, KT=(tk, bk), QT=(tq, bq), VG=(tv, bv), GT=(tg, bg), MQ=(tmq, bmq), MK=(tmk, bmk), MV=(tmv, bmv))

        def shared_pre(step, d):
            dd = data[(step, d)]
            tg, bg = dd["GT"]
            gs, bgs = rgs[d].next()
            dc = dirc[d]
            p, bp = psr.next()
            B.op("pe", lambda e: e.matmul(p[:, 0:8], lhsT=dc["U"][:], rhs=tg[:, d * 8:(d + 1) * 8], start=True, stop=True), reads=[bg] + cbufs, writes=[bp], inc=False)
            B.op("pe", lambda e: e.matmul(p[:, 8:12], lhsT=dc["U"][:], rhs=tg[:, 40 + d * 4:44 + d * 4], start=True, stop=True), reads=[bg] + cbufs, writes=[bp], inc=False)
            B.op("pe", lambda e: e.matmul(p[:, 12:20], lhsT=self.ones_f[:], rhs=tg[:, d * 8:(d + 1) * 8], start=True, stop=True), reads=[bg] + cbufs, writes=[bp], inc=False)
            B.op("pe", lambda e: e.matmul(p[:, 20:24], lhsT=self.ones_f[:], rhs=tg[:, 40 + d * 4:44 + d * 4], start=True, stop=True), reads=[bg] + cbufs, writes=[bp])
            B.op("act", lambda e: e.activation(out=gs[:, 0:24], in_=p[:, 0:24], func=AF.Identity), reads=[bp], writes=[bgs])
            B.op("act", lambda e: e.activation(out=gs[:, 24:32], in_=gs[:, 0:8], func=AF.Exp), reads=[bgs], writes=[bgs])
            B.op("dve", lambda e: e.tensor_tensor(out=gs[:, 32:40], in0=gs[:, 12:20], in1=gs[:, 0:8], op=ALU.subtract), reads=[bgs], writes=[bgs])
            B.op("act", lambda e: e.activation(out=gs[:, 32:40], in_=gs[:, 32:40], func=AF.Exp), reads=[bgs], writes=[bgs])
            B.op("act", lambda e: e.activation(out=gs[:, 40:48], in_=gs[:, 12:20], func=AF.Exp), reads=[bgs], writes=[bgs])
            B.op("dve", lambda e: e.tensor_tensor(out=gs[:, 48:52], in0=tg[:, 32 + d * 4:36 + d * 4], in1=gs[:, 8:12], op=ALU.subtract), reads=[bgs, bg], writes=[bgs])
            dd["gs"] = (gs, bgs)

        def gdn_group(step, d, hg, sl):
            dd = data[(step, d)]
            dc = dirc[d]
            c = dd["c"]
            need_o = out_lo <= c < out_hi
            tk, bk = dd["KT"]; tq, bq = dd["QT"]; tv, bv = dd["VG"]; tg, bg = dd["GT"]; gs, bgs = dd["gs"]
            h0 = hg * 4
            A, bA = sl["A"]; Bt, bBt = sl["Bt"]; Ct, bCt = sl["Ct"]
            attnT, battn = sl["attnT"]; Qp, bQp = sl["Qp"]; Kg, bKg = sl["Kg"]; kt, bkt = sl["kt"]
            IYT, bIYT = sl["IYT"]; negW, bnegW = sl["negW"]; vnew, bvnew = sl["vnew"]
            Qm, bQm = sl["IYT"]
            St, bSt = sl["A"]
            gcol = tg[:, d * 8 + h0:d * 8 + h0 + 4]
            bcol = tg[:, 16 + d * 8 + h0:16 + d * 8 + h0 + 4]
            eg = gs[:, 24 + h0:24 + h0 + 4]
            ekt = gs[:, 32 + h0:32 + h0 + 4]
            gte = gs[:, 40 + h0:40 + h0 + 4]
            for u in range(4):
                B.op("act", lambda e, u=u: e.activation(out=A[:, u, :], in_=dc["U"][:], func=AF.Identity, scale=gcol[:, u:u + 1]), reads=[bg] + cbufs, writes=[bA])
            for u in range(4):
                B.op("act", lambda e, u=u: e.activation(out=Ct[:, u, :], in_=dc["strict"][:], func=AF.Identity, scale=bcol[:, u:u + 1]), reads=[bg] + cbufs, writes=[bCt])
            yield
            pD, bpD = psr.next()
            for u in range(4):
                B.op("pe", lambda e, u=u: e.matmul(pD[:, u * 128:(u + 1) * 128], lhsT=dc["S"][:], rhs=A[:, u, :], start=True, stop=True),
                     reads=[bA] + cbufs, writes=[bpD], inc=(u == 3))
            B.op("act", lambda e: e.activation(out=Bt[:].rearrange("p u l -> p (u l)"), in_=pD[:, :], func=AF.Exp), reads=[bpD], writes=[bBt])
            yield
            B.op("dve", lambda e: e.tensor_tensor(out=A[:], in0=Bt[:], in1=bcm(dc["incl"][:]), op=ALU.mult), reads=[bBt] + cbufs, writes=[bA])
            B.op("pool", lambda e: e.tensor_tensor(out=Ct[:], in0=Ct[:], in1=Bt[:], op=ALU.mult), reads=[bCt, bBt], writes=[bCt])
            yield
            pKK, bpKK = psr.next()
            pQK, bpQK = psr.next()
            pKt, bpKt = psr.next()
            pKtb = pKt[:].bitcast(BF16)
            for u in range(4):
                B.op("pe", lambda e, u=u: e.matmul(pKK[:, u * 128:(u + 1) * 128], lhsT=tk[:, h0 + u, :], rhs=tk[:, h0 + u, :], start=True, stop=True),
                     reads=[bk], writes=[bpKK], inc=(u == 3))
            for u in range(4):
                B.op("pe", lambda e, u=u: e.matmul(pQK[:, u * 128:(u + 1) * 128], lhsT=tk[:, h0 + u, :], rhs=tq[:, h0 + u, :], start=True, stop=True),
                     reads=[bk, bq], writes=[bpQK], inc=(u == 3))
            for u in range(4):
                B.op("pe", lambda e, u=u: e.transpose(pKtb[:, u * 128:(u + 1) * 128], tk[:, h0 + u, :], self.ident_b[:]),
                     reads=[bk] + cbufs, writes=[bpKt], inc=(u == 3))
            B.op("dve", lambda e: e.tensor_tensor(out=Qm[:], in0=pKK[:, :].rearrange("p (u l) -> p u l", u=4), in1=Ct[:], op=ALU.mult),
                 reads=[bpKK, bCt], writes=[bQm])
            B.op("dve", lambda e: e.tensor_tensor(out=attnT[:], in0=pQK[:, :].rearrange("p (u l) -> p u l", u=4), in1=A[:], op=ALU.mult),
                 reads=[bpQK, bA], writes=[battn])
            B.op("dve", lambda e: e.tensor_tensor(out=Kg[:], in0=pKtb[:, 0:512].rearrange("p (u l) -> p u l", u=4), in1=bc3(eg, 128), op=ALU.mult),
                 reads=[bpKt, bgs], writes=[bKg])
            B.op("dve", lambda e: e.tensor_tensor(out=kt[:], in0=pKtb[:, 0:512].rearrange("p (u l) -> p u l", u=4), in1=bc3(ekt, 128), op=ALU.mult),
                 reads=[bpKt, bgs], writes=[bkt])
            yield
            B.op("pool", lambda e: e.tensor_tensor(out=Qp[:], in0=Qm[:], in1=bcm(self.ident_b[:]), op=ALU.add), reads=[bQm] + cbufs, writes=[bQp])
            yield
            Gc = None
            Hc = None
            for lev in range(7):
                sm = smask[:, d * 7 + lev, :]
                pY, bpY = psr.next()
                for u in range(4):
                    rhsH = self.ident_b[:] if Hc is None else Hc[0][:, u, :]
                    B.op("pe", lambda e, u=u, rhsH=rhsH: e.matmul(pY[:, u * 128:(u + 1) * 128], lhsT=Qp[:, u, :], rhs=rhsH, start=True, stop=True),
                         reads=[bQp] + cbufs + ([] if Hc is None else [Hc[1]]), writes=[bpY], inc=(u == 3))
                B.op("dve", lambda e, sm=sm: e.tensor_tensor(out=IYT[:], in0=pY[:, :].rearrange("p (u l) -> p u l", u=4), in1=bcm(sm), op=ALU.mult),
                     reads=[bpY] + cbufs, writes=[bIYT])
                yield
                Gn = sl["G%d" % (lev % 2)]
                Hn = sl["H%d" % (lev % 2)]
                pG, bpG = psr.next()
                for u in range(4):
                    rhsG = self.ident_b[:] if Gc is None else Gc[0][:, u, :]
                    B.op("pe", lambda e, u=u, rhsG=rhsG: e.matmul(pG[:, u * 128:(u + 1) * 128], lhsT=IYT[:, u, :], rhs=rhsG, start=True, stop=True),
                         reads=[bIYT] + cbufs + ([] if Gc is None else [Gc[1]]), writes=[bpG], inc=(u == 3))
                B.op("act", lambda e, Gn=Gn: e.activation(out=Gn[0][:].rearrange("p u l -> p (u l)"), in_=pG[:, :], func=AF.Identity), reads=[bpG], writes=[Gn[1]])
                if lev < 6:
                    pH, bpH = psr.next()
                    for u in range(4):
                        lhsG = self.ident_b[:] if Gc is None else Gc[0][:, u, :]
                        B.op("pe", lambda e, u=u, lhsG=lhsG: e.matmul(pH[:, u * 128:(u + 1) * 128], lhsT=lhsG, rhs=IYT[:, u, :], start=True, stop=True),
                             reads=[bIYT] + cbufs + ([] if Gc is None else [Gc[1]]), writes=[bpH], inc=(u == 3))
                    B.op("act", lambda e, Hn=Hn: e.activation(out=Hn[0][:].rearrange("p u l -> p (u l)"), in_=pH[:, :], func=AF.Identity), reads=[bpH], writes=[Hn[1]])
                    Hc = Hn
                Gc = Gn
                yield
            G, bG = Gc
            pW, bpW = psr.next()
            for u in range(4):
                B.op("pe", lambda e, u=u: e.matmul(pW[:, u * 128:(u + 1) * 128], lhsT=Kg[:, u, :], rhs=G[:, u, :], start=True, stop=True),
                     reads=[bKg, bG], writes=[bpW], inc=(u == 3))
            B.op("act", lambda e: e.activation(out=negW[:].rearrange("p u l -> p (u l)"), in_=pW[:, :], func=AF.Identity, scale=-1.0), reads=[bpW], writes=[bnegW])
            yield
            while step > 0 and ("gdn", step - 1, d, hg) not in done and ("gdn", step - 1, d, hg) in started:
                yield
            sbt, bsb = Sb_cur[d][hg]
            pV, bpV = psr.next()
            for u in range(4):
                B.op("pe", lambda e, u=u: e.matmul(pV[:, u * 128:(u + 1) * 128], lhsT=G[:, u, :], rhs=tv[:, (h0 + u) * 128:(h0 + u + 1) * 128], start=True, stop=False),
                     reads=[bG, bv], writes=[bpV], inc=False)
                B.op("pe", lambda e, u=u: e.matmul(pV[:, u * 128:(u + 1) * 128], lhsT=negW[:, u, :], rhs=sbt[:, u, :], start=False, stop=True),
                     reads=[bnegW, bsb], writes=[bpV], inc=(u == 3))
            B.op("dve", lambda e: e.tensor_tensor(out=vnew[:], in0=pV[:, :].rearrange("p (u l) -> p u l", u=4), in1=bc3(bcol, 128), op=ALU.mult),
                 reads=[bpV, bg], writes=[bvnew])
            Sg = S[d][:, h0:h0 + 4, :]
            B.op("pool", lambda e: e.tensor_tensor(out=St[:], in0=Sg, in1=bc3(gte, 128), op=ALU.mult), reads=[bS[d][hg], bgs], writes=[bSt])
            yield
            if need_o:
                pO1, bpO1 = psr.next()
                for u in range(4):
                    B.op("pe", lambda e, u=u: e.matmul(pO1[:, u * 128:(u + 1) * 128], lhsT=tq[:, h0 + u, :], rhs=sbt[:, u, :], start=True, stop=True),
                         reads=[bq, bsb], writes=[bpO1], inc=(u == 3))
                o1, bo1 = ring_o1.next()
                B.op("dve", lambda e: e.tensor_tensor(out=o1[:], in0=pO1[:, :].rearrange("p (u l) -> p u l", u=4), in1=bc3(eg, 128), op=ALU.mult),
                     reads=[bpO1, bgs], writes=[bo1])
                pO2, bpO2 = psr.next()
                for u in range(4):
                    B.op("pe", lambda e, u=u: e.matmul(pO2[:, u * 128:(u + 1) * 128], lhsT=attnT[:, u, :], rhs=vnew[:, u, :], start=True, stop=True),
                         reads=[battn, bvnew], writes=[bpO2], inc=(u == 3))
                o, bo = ring_o.next()
                B.op("dve", lambda e: e.tensor_tensor(out=o[:], in0=pO2[:, :].rearrange("p (u l) -> p u l", u=4), in1=o1[:], op=ALU.add),
                     reads=[bpO2, bo1], writes=[bo])
                dst = (self.OF if d == 0 else self.OB)[c - 2]
                B.dma("sp", dst[:, hg * 512:(hg + 1) * 512], o[:].rearrange("p u l -> p (u l)"), bo, reads=[bo])
            pS, bpS = psr.next()
            for u in range(4):
                B.op("pe", lambda e, u=u: e.matmul(pS[:, u * 128:(u + 1) * 128], lhsT=kt[:, u, :], rhs=vnew[:, u, :], start=True, stop=True),
                     reads=[bkt, bvnew], writes=[bpS], inc=(u == 3))
            B.op("dve", lambda e: e.tensor_tensor(out=Sg, in0=pS[:, :].rearrange("p (u l) -> p u l", u=4), in1=St[:], op=ALU.add),
                 reads=[bpS, bSt], writes=[bS[d][hg]])
            yield
            nsb, bnsb = Sb[d][hg].next()
            B.op("act", lambda e: e.activation(out=nsb[:], in_=Sg, func=AF.Identity), reads=[bS[d][hg]], writes=[bnsb])
            Sb_cur[d][hg] = (nsb, bnsb)
            yield

        self._b_env = dict(data=data, dirc=dirc, cbufs=cbufs, psr=psr, order=order, out_lo=out_lo, out_hi=out_hi, bc3=bc3, bcm=bcm,
                           C=C, bC=bC, Cb=Cb, Cb_cur=Cb_cur, nst=nst, nbf=nbf, n_cur=n_cur, nb_cur=nb_cur, mst=mst, m_cur=m_cur,
                           ring_num=ring_num, ring_h=ring_h)
        ml_group = self.make_ml_group()

        from collections import deque
        pending = deque()
        for step in range(nsteps):
            dirs = [d for d in range(2) if not (d == 0 and order[0][step] >= out_hi)]
            for d in dirs:
                pending.append(("load", step, d))
            for hg in range(2):
                for d in dirs:
                    if not dbg.get("b_no_gdn"):
                        pending.append(("gdn", step, d, hg))
            for d in dirs:
                if not dbg.get("b_no_ml"):
                    pending.append(("ml", step, d))
        free_g = list(range(NG))
        free_m = list(range(NM))
        done = set()
        started = set()
        active = []
        rounds = 0
        GAP = dbg.get("b_gap", 5)
        last_gstart = [-GAP]
        loaded = set()
        while pending or active:
            while pending:
                it = pending[0]
                if it[0] == "load":
                    _, step, d = it
                    if any((k[1] == step - 2 and k[2] == d and k not in done) for k in started):
                        break
                    load_step(step, d)
                    shared_pre(step, d)
                    pending.popleft()
                    continue
                if it[0] == "gdn":
                    _, step, d, hg = it
                    if not free_g or rounds - last_gstart[0] < GAP:
                        break
                    last_gstart[0] = rounds
                    si = free_g.pop(0)
                    started.add(it)
                    active.append((it, gdn_group(step, d, hg, gslots[si]), ("g", si)))
                    pending.popleft()
                    continue
                if it[0] == "ml":
                    _, step, d = it
                    key_prev = ("ml", step - 1, d)
                    if (step > 0 and key_prev not in done) or not free_m:
                        break
                    si = free_m.pop(0)
                    started.add(it)
                    active.append((it, ml_group(step, d, mslots[si]), ("m", si)))
                    pending.popleft()
                    continue
            rounds += 1
            for ent in list(active):
                it, gen, (kind, si) = ent
                try:
                    next(gen)
                except StopIteration:
                    active.remove(ent)
                    done.add(it)
                    (free_g if kind == "g" else free_m).append(si)
        B.barrier()
        st.close()

    def make_ml_group(self):
        B = self.B
        env = self._b_env
        data, dirc, cbufs, psr = env["data"], env["dirc"], env["cbufs"], env["psr"]
        bc3, bcm = env["bc3"], env["bcm"]
        C, bC, Cb, Cb_cur = env["C"], env["bC"], env["Cb"], env["Cb_cur"]
        nst, nbf, n_cur, nb_cur, mst, m_cur = env["nst"], env["nbf"], env["n_cur"], env["nb_cur"], env["mst"], env["m_cur"]
        ring_num, ring_h = env["ring_num"], env["ring_h"]
        out_lo, out_hi = env["out_lo"], env["out_hi"]

        def bc3n(ap2, n):
            return ap2.unsqueeze(2).to_broadcast([128, ap2.shape[1], n])

        def ml_group(step, d, sl):
            dd = data[(step, d)]
            dc = dirc[d]
            c = dd["c"]
            need_o = out_lo <= c < out_hi
            tg, bg = dd["GT"]; gs, bgs = dd["gs"]
            mq, bmq = dd["MQ"]; mk, bmk = dd["MK"]; mv, bmv = dd["MV"]
            X, bX = sl["X"]; Y, bY = sl["Y"]; Pm, bPm = sl["Pm"]; PT, bPT = sl["PT"]; Kw, bKw = sl["Kw"]
            sm, bsm = sl["sm"]; Ct, bCt = sl["Ct"]
            bcc = gs[:, 8:12]
            blast = gs[:, 20:24]
            cvec = gs[:, 48:52]
            mprev, bmprev = m_cur[d]
            B.op("pool", lambda e: e.tensor_tensor(out=X[:], in0=bcm(self.ident_f[:]), in1=bc3(cvec, 128), op=ALU.mult), reads=[bgs] + cbufs, writes=[bX])
            B.op("dve", lambda e: e.tensor_tensor(out=sm[:, 4:8], in0=bcc, in1=mprev[:, 0:4], op=ALU.add), reads=[bgs, bmprev], writes=[bsm])
            yield
            pC, bpC = psr.next()
            for u in range(4):
                B.op("pe", lambda e, u=u: e.matmul(pC[:, u * 128:(u + 1) * 128], lhsT=self.ones_f[:], rhs=X[:, u, :], start=True, stop=True),
                     reads=[bX] + cbufs, writes=[bpC], inc=(u == 3))
            B.op("dve", lambda e: e.tensor_tensor(out=Y[:], in0=pC[:, :].rearrange("p (u l) -> p u l", u=4), in1=bc3(bcc, 128), op=ALU.add),
                 reads=[bpC, bgs], writes=[bY])
            yield
            B.op("pool", lambda e: e.tensor_tensor(out=Y[:], in0=Y[:], in1=bcm(dc["MB"][:]), op=ALU.add), reads=[bY] + cbufs, writes=[bY])
            yield
            B.op("dve", lambda e: e.tensor_reduce(out=sm[:, 0:4], in_=Y[:], axis=AX.X, op=ALU.max), reads=[bY], writes=[bsm])
            B.op("dve", lambda e: e.tensor_tensor(out=sm[:, 8:12], in0=sm[:, 0:4], in1=sm[:, 4:8], op=ALU.max), reads=[bsm], writes=[bsm])
            B.op("dve", lambda e: e.tensor_scalar(out=sm[:, 12:16], in0=sm[:, 8:12], scalar1=-1.0, scalar2=None, op0=ALU.mult), reads=[bsm], writes=[bsm])
            B.op("dve", lambda e: e.tensor_tensor(out=sm[:, 16:20], in0=sm[:, 4:8], in1=sm[:, 8:12], op=ALU.subtract), reads=[bsm], writes=[bsm])
            yield
            if need_o:
                for u in range(4):
                    B.op("act", lambda e, u=u: e.activation(out=X[:, u, :], in_=Y[:, u, :], func=AF.Exp, bias=sm[:, 12 + u:13 + u]), reads=[bY, bsm], writes=[bX])
                B.op("act", lambda e: e.activation(out=sm[:, 16:20], in_=sm[:, 16:20], func=AF.Exp), reads=[bsm], writes=[bsm])
                B.op("act", lambda e: e.activation(out=sm[:, 20:24], in_=sm[:, 12:16], func=AF.Exp), reads=[bsm], writes=[bsm])
                yield
                pQK, bpQK = psr.next()
                for u in range(4):
                    B.op("pe", lambda e, u=u: e.matmul(pQK[:, u * 128:(u + 1) * 128], lhsT=mq[:, u, :], rhs=mk[:, u, :], start=True, stop=True),
                         reads=[bmq, bmk], writes=[bpQK], inc=(u == 3))
                B.op("dve", lambda e: e.tensor_tensor(out=Pm[:], in0=pQK[:, :].rearrange("p (u l) -> p u l", u=4), in1=X[:], op=ALU.mult),
                     reads=[bpQK, bX], writes=[bPm])
                yield
                pT, bpT = psr.next()
                pTb = pT[:].bitcast(BF16)
                for u in range(4):
                    B.op("pe", lambda e, u=u: e.transpose(pTb[:, u * 128:(u + 1) * 128], Pm[:, u, :], self.ident_b[:]), reads=[bPm] + cbufs, writes=[bpT], inc=(u == 3))
                B.op("act", lambda e: e.activation(out=PT[:].rearrange("p u l -> p (u l)"), in_=pTb[:, 0:512], func=AF.Identity), reads=[bpT], writes=[bPT])
                yield
                cbt, bcb = Cb_cur[d]
                nbt, bnb = nb_cur[d]
                pDn, bpDn = psr.next()
                for u in range(4):
                    B.op("pe", lambda e, u=u: e.matmul(pDn[:, u:u + 1], lhsT=mq[:, u, :], rhs=nbt[:, u:u + 1], start=True, stop=True), reads=[bmq, bnb], writes=[bpDn], inc=False)
                for u in range(4):
                    B.op("pe", lambda e, u=u: e.matmul(pDn[:, 4 + u:5 + u], lhsT=PT[:, u, :], rhs=self.ones_b[:, 0:1], start=True, stop=True),
                         reads=[bPT] + cbufs, writes=[bpDn], inc=(u == 3))
                B.op("dve", lambda e: e.tensor_tensor(out=sm[:, 24:28], in0=pDn[:, 0:4], in1=sm[:, 16:20], op=ALU.mult), reads=[bpDn, bsm], writes=[bsm])
                B.op("dve", lambda e: e.tensor_tensor(out=sm[:, 24:28], in0=pDn[:, 4:8], in1=sm[:, 24:28], op=ALU.add), reads=[bpDn, bsm], writes=[bsm])
                B.op("dve", lambda e: e.tensor_tensor(out=sm[:, 24:28], in0=sm[:, 24:28], in1=sm[:, 24:28], op=ALU.mult), reads=[bsm], writes=[bsm])
                B.op("dve", lambda e: e.tensor_tensor(out=sm[:, 28:32], in0=sm[:, 20:24], in1=sm[:, 20:24], op=ALU.mult), reads=[bsm], writes=[bsm])
                B.op("dve", lambda e: e.tensor_tensor(out=sm[:, 24:28], in0=sm[:, 24:28], in1=sm[:, 28:32], op=ALU.max), reads=[bsm], writes=[bsm])
                yield
                B.op("pool", lambda e: e.tensor_tensor(out=sm[:, 28:32], in0=sm[:, 24:28], in1=self.nhalf[:, 0:4], op=ALU.pow), reads=[bsm] + cbufs, writes=[bsm])
                yield
                num, bnum = ring_num.next()
                hh, bhh = ring_h.next()
                for pr in range(2):
                    pN1, bpN1 = psr.next()
                    pN2, bpN2 = psr.next()
                    for uu in range(2):
                        u = pr * 2 + uu
                        B.op("pe", lambda e, u=u, uu=uu, pN1=pN1: e.matmul(pN1[:, uu * 256:(uu + 1) * 256], lhsT=mq[:, u, :], rhs=cbt[:, u, :], start=True, stop=True),
                             reads=[bmq, bcb], writes=[bpN1], inc=(uu == 1))
                    for uu in range(2):
                        u = pr * 2 + uu
                        B.op("pe", lambda e, u=u, uu=uu, pN2=pN2: e.matmul(pN2[:, uu * 256:(uu + 1) * 256], lhsT=PT[:, u, :], rhs=mv[:, u * 256:(u + 1) * 256], start=True, stop=True),
                             reads=[bPT, bmv], writes=[bpN2], inc=(uu == 1))
                    B.op("dve", lambda e, pr=pr, pN1=pN1: e.tensor_tensor(out=num[:, pr * 2:pr * 2 + 2, :], in0=pN1[:, :].rearrange("p (u l) -> p u l", u=2),
                                                                         in1=bc3n(sm[:, 16 + pr * 2:18 + pr * 2], 256), op=ALU.mult), reads=[bpN1, bsm], writes=[bnum])
                    B.op("dve", lambda e, pr=pr, pN2=pN2: e.tensor_tensor(out=num[:, pr * 2:pr * 2 + 2, :], in0=pN2[:, :].rearrange("p (u l) -> p u l", u=2),
                                                                         in1=num[:, pr * 2:pr * 2 + 2, :], op=ALU.add), reads=[bpN2, bnum], writes=[bnum])
                B.op("dve", lambda e: e.tensor_tensor(out=hh[:], in0=num[:], in1=bc3n(sm[:, 28:32], 256), op=ALU.mult), reads=[bnum, bsm], writes=[bhh])
                dst = (self.HF if d == 0 else self.HB)[c - 2]
                B.dma("sp", dst[:, :], hh[:].rearrange("p u l -> p (u l)"), bhh, reads=[bhh])
                yield
            pSel, bpSel = psr.next()
            B.op("pe", lambda e: e.matmul(pSel[:, 0:4], lhsT=dc["SEL"][:], rhs=sm[:, 8:12], start=True, stop=True), reads=[bsm] + cbufs, writes=[bpSel])
            mnew, bmnew = mst[d].next()
            B.op("act", lambda e: e.activation(out=mnew[:], in_=pSel[:, 0:4], func=AF.Identity), reads=[bpSel], writes=[bmnew])
            yield
            B.op("dve", lambda e: e.tensor_tensor(out=sm[:, 32:36], in0=cvec, in1=blast, op=ALU.add), reads=[bgs], writes=[bsm])
            B.op("dve", lambda e: e.tensor_tensor(out=sm[:, 32:36], in0=sm[:, 32:36], in1=mnew[:], op=ALU.subtract), reads=[bsm, bmnew], writes=[bsm])
            B.op("dve", lambda e: e.tensor_tensor(out=sm[:, 36:40], in0=blast, in1=mprev[:, 0:4], op=ALU.add), reads=[bgs, bmprev], writes=[bsm])
            B.op("dve", lambda e: e.tensor_tensor(out=sm[:, 36:40], in0=sm[:, 36:40], in1=mnew[:], op=ALU.subtract), reads=[bsm, bmnew], writes=[bsm])
            yield
            B.op("act", lambda e: e.activation(out=sm[:, 32:40], in_=sm[:, 32:40], func=AF.Exp), reads=[bsm], writes=[bsm])
            yield
            pKt, bpKt = psr.next()
            pKtb = pKt[:].bitcast(BF16)
            for u in range(4):
                B.op("pe", lambda e, u=u: e.transpose(pKtb[:, u * 128:(u + 1) * 128], mk[:, u, :], self.ident_b[:]), reads=[bmk] + cbufs, writes=[bpKt], inc=(u == 3))
            B.op("dve", lambda e: e.tensor_tensor(out=Kw[:], in0=pKtb[:, 0:512].rearrange("p (u l) -> p u l", u=4), in1=bc3(sm[:, 32:36], 128), op=ALU.mult),
                 reads=[bpKt, bsm], writes=[bKw])
            m_cur[d] = (mnew, bmnew)
            for pr in range(2):
                B.op("pool", lambda e, pr=pr: e.tensor_tensor(out=Ct[:, pr * 2:pr * 2 + 2, :], in0=C[d][:, pr * 2:pr * 2 + 2, :], in1=bc3n(sm[:, 36 + pr * 2:38 + pr * 2], 256), op=ALU.mult),
                     reads=[bC[d], bsm], writes=[bCt])
            yield
            for pr in range(2):
                pC2, bpC2 = psr.next()
                for uu in range(2):
                    u = pr * 2 + uu
                    B.op("pe", lambda e, u=u, uu=uu, pC2=pC2: e.matmul(pC2[:, uu * 256:(uu + 1) * 256], lhsT=Kw[:, u, :], rhs=mv[:, u * 256:(u + 1) * 256], start=True, stop=True),
                         reads=[bKw, bmv], writes=[bpC2], inc=(uu == 1))
                B.op("dve", lambda e, pr=pr, pC2=pC2: e.tensor_tensor(out=C[d][:, pr * 2:pr * 2 + 2, :], in0=pC2[:, :].rearrange("p (u l) -> p u l", u=2), in1=Ct[:, pr * 2:pr * 2 + 2, :], op=ALU.add),
                     reads=[bpC2, bCt], writes=[bC[d]])
            pN, bpN = psr.next()
            for u in range(4):
                B.op("pe", lambda e, u=u: e.matmul(pN[:, u:u + 1], lhsT=Kw[:, u, :], rhs=self.ones_b[:, 0:1], start=True, stop=True), reads=[bKw] + cbufs, writes=[bpN], inc=(u == 3))
            nold, bnold = n_cur[d]
            nnew, bnnew = nst[d].next()
            B.op("dve", lambda e: e.tensor_tensor(out=nnew[:, 4:8], in0=nold[:, 0:4], in1=sm[:, 36:40], op=ALU.mult), reads=[bnold, bsm], writes=[bnnew])
            B.op("dve", lambda e: e.tensor_tensor(out=nnew[:, 0:4], in0=pN[:, 0:4], in1=nnew[:, 4:8], op=ALU.add), reads=[bpN, bnnew], writes=[bnnew])
            yield
            nbn, bnbn = nbf[d].next()
            B.op("act", lambda e: e.activation(out=nbn[:], in_=nnew[:, 0:4], func=AF.Identity), reads=[bnnew], writes=[bnbn])
            cbn, bcbn = Cb[d].next()
            B.op("act", lambda e: e.activation(out=cbn[:], in_=C[d][:], func=AF.Identity), reads=[bC[d]], writes=[bcbn])
            n_cur[d] = (nnew, bnnew)
            nb_cur[d] = (nbn, bnbn)
            Cb_cur[d] = (cbn, bcbn)
            yield

        return ml_group

    def phaseC1(self):
        B, nc, inp = self.B, self.nc, self.inp
        st = ExitStack()
        dbg = self.debug
        wo, bwo = self.load_w_bf16(st, "c_wo", inp["w_o"], 8, 4096, 8)
        wbg, bwbg = self.load_w_bf16(st, "c_wbg", inp["w_bg"], 8, 1024, 2)
        wbm, bwbm = self.load_w_bf16(st, "c_wbm", inp["w_bm"], 8, 1024, 2)
        wout, bwout = self.load_w_bf16(st, "c_wout", inp["w_out"], 8, 1024, 2)
        nwb = B.sb(st, "c_nwb", [128, 2, 1024], F32)
        bnwb = Buf("c_nwb")
        B.dma("sp", nwb[:, 0, :], inp["gnw_bc"][:, :], bnwb, writes=[bnwb])
        B.dma("sp", nwb[:, 1, :], inp["mnw_bc"][:, :], bnwb, writes=[bnwb])
        bX1 = Buf("X1")
        NS = 2
        xt = B.sb(st, "c_x", [128, NS, 1024], F32)
        bxts = [Buf("c_x%d" % i) for i in range(NS)]
        B.op("pool", lambda e: e.memset(xt[0:64, 0, :], 0.0), writes=[bxts[0]])
        B.dma("sp", self.X1[0:64, :], xt[0:64, 0, :], bxts[0], reads=[bxts[0]], writes=[bX1])
        xn = B.sb(st, "c_xn", [128, NS, 1024], BF16); bxn = Buf("c_xn")
        sq = B.sb(st, "c_sq", [128, 24], F32); bsq = Buf("c_sq")
        junk = None; bjunk = None
        hxT = B.sb(st, "c_hxT", [128, 8, NS * 128], BF16); bhx = Buf("c_hxT")
        oar = Ring(B, st, "c_oa", 2, [128, 1024], F32)
        obr = Ring(B, st, "c_ob", 2, [128, 1024], F32)
        gtr = Ring(B, st, "c_gt", 2, [128, 1024], BF16)
        osqr = Ring(B, st, "c_osq", 2, [128, 1024], BF16)
        smr = Ring(B, st, "c_sm", 2, [128, 32], F32)
        ogr = Ring(B, st, "c_og", 2, [128, 1024], BF16)
        brT = [B.sb(st, "c_brT%d" % i, [128, 8, NS * 128], BF16) for i in range(2)]
        bbrT = [Buf("c_brT%d" % i) for i in range(2)]
        sg = Ring(B, st, "c_sg", 6, [128, NS * 128], F32)
        mT = B.sb(st, "c_mT", [128, 8, NS * 128], BF16); bmT = Buf("c_mT")
        tmp = Ring(B, st, "c_tmp", 2, [128, 512], F32)
        ptr = Ring(B, st, "c_ptr", 2, [128, 512], F32, psum=True)
        pmm = Ring(B, st, "c_pmm", 6, [128, 512], F32, psum=True)
        nt = OWN_T // 128
        sts = []
        i = 0
        while i < nt:
            ns = min(NS, nt - i)
            sts.append((i, ns))
            i += ns
        if dbg.get("c1_tiles"):
            sts = sts[: dbg["c1_tiles"]]
        for (t0, ns) in sts:
            n = ns * 128
            for s in range(ns):
                B.dma("sp", xt[:, s, :], inp["x"][(t0 + s) * 128:(t0 + s + 1) * 128, :], bxts[s], writes=[bxts[s]])
            self.norm_transpose(xt, bxts, ns, xn, bxn, sq, bsq, junk, bjunk, ptr, hxT, bhx, 0, 0)
            def branch_gen(br, s_):
                nh, hd = (8, 128) if br == 0 else (4, 256)
                srcf, srcb = (self.OF, self.OB) if br == 0 else (self.HF, self.HB)
                c = t0 + s_
                oa, boa = oar.next()
                ob, bob = obr.next()
                gt, bgt = gtr.next()
                osq, bosq = osqr.next()
                sm, bsm = smr.next()
                og, bog = ogr.next()
                B.dma("sp", oa[:], srcf[c], boa, writes=[boa])
                B.dma("sp", ob[:], srcb[c], bob, writes=[bob])
                for hf in range(2):
                    p, bp = pmm.next()
                    for k in range(8):
                        B.op("pe", lambda e, k=k, p=p, hf=hf: e.matmul(
                            p[:], lhsT=hxT[:, k, s_ * 128:(s_ + 1) * 128], rhs=wo[:, k, br * 1024 + hf * 512: br * 1024 + (hf + 1) * 512],
                            start=(k == 0), stop=(k == 7)), reads=[bhx, bwo], writes=[bp])
                    B.op("act", lambda e, p=p, hf=hf: e.activation(out=gt[:, hf * 512:(hf + 1) * 512], in_=p[:],
                                                                  func=(AF.Silu if br == 0 else AF.Sigmoid)), reads=[bp], writes=[bgt])
                yield
                B.op("pool", lambda e: e.tensor_tensor(out=gt[:], in0=gt[:], in1=nwb[:, br, :], op=ALU.mult), reads=[bgt, bnwb], writes=[bgt])
                B.op("dve", lambda e: e.tensor_tensor(out=oa[:], in0=oa[:], in1=ob[:], op=ALU.add), reads=[boa, bob], writes=[boa])
                yield
                B.op("act", lambda e: e.activation(out=osq[:], in_=oa[:], func=AF.Square), reads=[boa], writes=[bosq])
                yield
                B.op("dve", lambda e: e.tensor_reduce(out=sm[:, 0:nh], in_=osq[:].rearrange("p (h e) -> p h e", h=nh), axis=AX.X, op=ALU.add),
                     reads=[bosq], writes=[bsm])
                B.op("dve", lambda e: e.tensor_scalar(out=sm[:, 8:8 + nh], in0=sm[:, 0:nh], scalar1=float(1.0 / hd), scalar2=float(EPS),
                                                      op0=ALU.mult, op1=ALU.add), reads=[bsm], writes=[bsm])
                yield
                B.op("pool", lambda e: e.tensor_tensor(out=sm[:, 16:16 + nh], in0=sm[:, 8:8 + nh], in1=self.nhalf[:, 0:nh], op=ALU.pow),
                     reads=[bsm, self.cb], writes=[bsm])
                yield
                B.op("dve", lambda e: e.tensor_tensor(out=osq[:].rearrange("p (h e) -> p h e", h=nh), in0=oa[:].rearrange("p (h e) -> p h e", h=nh),
                                                      in1=sm[:, 16:16 + nh].unsqueeze(2).to_broadcast([128, nh, hd]), op=ALU.mult),
                     reads=[boa, bsm], writes=[bosq])
                B.op("dve", lambda e: e.tensor_tensor(out=og[:], in0=osq[:], in1=gt[:], op=ALU.mult), reads=[bosq, bgt], writes=[bog])
                yield
                p, bp = ptr.next()
                pb = p[:].bitcast(BF16)
                for k in range(8):
                    B.op("pe", lambda e, k=k, pb=pb: e.transpose(pb[:, k * 128:(k + 1) * 128], og[:, k * 128:(k + 1) * 128], self.ident_b[:]),
                         reads=[bog, self.cb], writes=[bp])
                B.op("act", lambda e, pb=pb: e.activation(out=brT[br][:, :, s_ * 128:(s_ + 1) * 128], in_=pb[:, 0:1024].rearrange("p (k t) -> p k t", k=8),
                                                          func=AF.Identity), reads=[bp], writes=[bbrT[br]])

            self.run_pipeline([(lambda br=br, s_=s_: branch_gen(br, s_)) for br in range(2) for s_ in range(ns)], 2)

            def merge_gen(ncn):
                sgs = []
                for gi in range(2):
                    p, bp = pmm.next()
                    for k in range(8):
                        B.op("pe", lambda e, k=k, p=p, gi=gi: e.matmul(p[:, 0:n], lhsT=wo[:, k, 2048 + gi * 1024 + ncn * 128: 2048 + gi * 1024 + (ncn + 1) * 128],
                                                                      rhs=hxT[:, k, 0:n], start=(k == 0), stop=(k == 7)), reads=[bhx, bwo], writes=[bp])
                    g_, bg_ = sg.next()
                    B.op("act", lambda e, p=p, g_=g_: e.activation(out=g_[:, 0:n], in_=p[:, 0:n], func=AF.Sigmoid), reads=[bp], writes=[bg_])
                    sgs.append((g_, bg_))
                yield
                ys = []
                for br, (w, bw) in enumerate(((wbg, bwbg), (wbm, bwbm))):
                    p, bp = pmm.next()
                    for k in range(8):
                        B.op("pe", lambda e, k=k, p=p, w=w, br=br: e.matmul(p[:, 0:n], lhsT=w[:, k, ncn * 128:(ncn + 1) * 128], rhs=brT[br][:, k, 0:n],
                                                                           start=(k == 0), stop=(k == 7)), reads=[bbrT[br], bw], writes=[bp])
                    ys.append((p, bp))
                g0, bg0 = sgs[0]
                g1, bg1 = sgs[1]
                B.op("dve", lambda e: e.tensor_tensor(out=g0[:, 0:n], in0=ys[0][0][:, 0:n], in1=g0[:, 0:n], op=ALU.mult), reads=[ys[0][1], bg0], writes=[bg0])
                B.op("dve", lambda e: e.tensor_tensor(out=g1[:, 0:n], in0=ys[1][0][:, 0:n], in1=g1[:, 0:n], op=ALU.mult), reads=[ys[1][1], bg1], writes=[bg1])
                yield
                B.op("pool", lambda e: e.tensor_tensor(out=mT[:, ncn, 0:n], in0=g0[:, 0:n], in1=g1[:, 0:n], op=ALU.add), reads=[bg0, bg1], writes=[bmT])

            self.run_pipeline([(lambda ncn=ncn: merge_gen(ncn)) for ncn in range(8)], 3)

            def out_gen(s_, hf):
                p, bp = pmm.next()
                t_, bt_ = tmp.next()
                for k in range(8):
                    B.op("pe", lambda e, k=k: e.matmul(p[:], lhsT=mT[:, k, s_ * 128:(s_ + 1) * 128], rhs=wout[:, k, hf * 512:(hf + 1) * 512],
                                                         start=(k == 0), stop=(k == 7)), reads=[bmT, bwout], writes=[bp])
                B.op("dve", lambda e: e.tensor_tensor(out=t_[:], in0=p[:], in1=self.gate_bc[:, 0, hf * 512:(hf + 1) * 512], op=ALU.mult),
                     reads=[bp, self.bgate], writes=[bt_])
                yield
                B.op("pool", lambda e: e.tensor_tensor(out=xt[:, s_, hf * 512:(hf + 1) * 512], in0=xt[:, s_, hf * 512:(hf + 1) * 512], in1=t_[:], op=ALU.add),
                     reads=[bt_, bxts[s_]], writes=[bxts[s_]])
                if hf == 1:
                    B.dma("sp", self.X1[64 + (t0 + s_) * 128: 64 + (t0 + s_ + 1) * 128, :], xt[:, s_, :], bxts[s_], reads=[bxts[s_]], writes=[bX1])

            self.run_pipeline([(lambda s_=s_, hf=hf: out_gen(s_, hf)) for s_ in range(ns) for hf in range(2)], 2)
        B.barrier()
        st.close()

    def precast_wup(self):
        B = self.B
        self.WUPB = B.dram("WUPB", [44, 128, 8, 128], BF16)
        self.bwupb = Buf("WUPB")
        src = self.inp["w_up"].rearrange("(k p) (c j) -> c p k j", p=128, j=128)
        for c in range(44):
            B.dma("pool", self.WUPB[c], src[c], self.bwupb, writes=[self.bwupb])

    def phaseC2(self):
        B, nc, inp = self.B, self.nc, self.inp
        st = ExitStack()
        dbg = self.debug
        wd = B.sb(st, "d_wd", [128, 22, 1024], BF16)
        bwd = Buf("d_wd")
        wdv = inp["w_down"].rearrange("(c p) n -> p c n", p=128)
        for i in range(0, 22, 6):
            j = min(22, i + 6)
            B.dma("pool", wd[:, i:j, :], wdv[:, i:j, :], bwd, writes=[bwd])
        cw = B.sb(st, "d_cw", [128, 44, 9], F32)
        nob = B.sb(st, "d_nob", [128, 1024], F32)
        bsm0 = Buf("d_small")
        B.dma("sp", cw[:], inp["ffn_cw"][:, :, :], bsm0, writes=[bsm0])
        B.dma("sp", nob[:], inp["now_bc"][:, :], bsm0, writes=[bsm0])
        DEPTH = 3
        wup = Ring(B, st, "d_wup", DEPTH + 1, [128, 2, 8, 128], BF16)
        xt = B.sb(st, "d_x", [128, 5, 1024], F32)
        bxts = [Buf("d_x%d" % i) for i in range(5)]
        xn = B.sb(st, "d_xn", [128, 5, 1024], BF16); bxn = Buf("d_xn")
        sq = B.sb(st, "d_sq", [128, 24], F32); bsq = Buf("d_sq")
        junk = B.sb(st, "d_junk", [128, 1024], BF16); bjunk = Buf("d_junk")
        hxT = B.sb(st, "d_hxT", [128, 8, 640], BF16); bhx = Buf("d_hxT")
        upad = Ring(B, st, "d_up", 2 * DEPTH, [128, 10, 66], BF16)
        dgr = Ring(B, st, "d_dg", 2 * DEPTH, [128, 9, 128], BF16)
        sgt = Ring(B, st, "d_sg", DEPTH, [128, 512], F32)
        aT = B.sb(st, "d_aT", [128, 22, 512], BF16); baT = Buf("d_aT")
        xo = B.sb(st, "d_xo", [128, 4, 1024], F32)
        bxo = [Buf("d_xo%d" % i) for i in range(4)]
        t2 = Ring(B, st, "d_t2", 2, [128, 512], F32)
        sq2 = B.sb(st, "d_sq2", [128, 16], F32); bsq2 = Buf("d_sq2")
        pr = Ring(B, st, "d_pr", 8, [128, 512], F32, psum=True)
        for (u_, bu_) in upad.slots:
            B.op("pool", lambda e, u_=u_: e.memset(u_[:], 0.0), writes=[bu_])
        nblk = dbg.get("c2_blocks", 8)
        for j in range(nblk):
            r0 = 512 * j
            for s in range(5):
                B.dma("sp", xt[:, s, :], self.X1[r0 + s * 128: r0 + (s + 1) * 128, :], bxts[s], writes=[bxts[s]])
            for s in range(4):
                B.dma("sp", xo[:, s, :], self.X1[r0 + 64 + s * 128: r0 + 64 + (s + 1) * 128, :], bxo[s], writes=[bxo[s]])
            self.norm_transpose(xt, bxts, 5, xn, bxn, sq, bsq, junk, bjunk, pr, hxT, bhx, 0, 4)

            def pair_gen(c):
                w, bw = wup.next()
                B.dma("sp", w[:, 0], self.WUPB[c], bw, reads=[self.bwupb], writes=[bw])
                B.dma("sp", w[:, 1], self.WUPB[22 + c], bw, reads=[self.bwupb], writes=[bw])
                ups, dgs = [], []
                for part in range(2):
                    ch = c + 22 * part
                    u_, bu_ = upad.next()
                    dg, bdg = dgr.next()
                    ups.append((u_, bu_))
                    dgs.append((dg, bdg))
                    B.op("dve", lambda e, dg=dg, ch=ch: e.tensor_tensor(out=dg[:], in0=self.ident_b[:].unsqueeze(1).to_broadcast([128, 9, 128]),
                                                                      in1=cw[:, ch, :].unsqueeze(2).to_broadcast([128, 9, 128]), op=ALU.mult),
                         reads=[self.cb, bsm0], writes=[bdg])
                yield
                for part in range(2):
                    u_, bu_ = ups[part]
                    p1, bp1 = pr.next()
                    p2, bp2 = pr.next()
                    for k in range(8):
                        B.op("pe", lambda e, k=k, p1=p1, part=part: e.matmul(p1[:], lhsT=w[:, part, k, :], rhs=hxT[:, k, 0:512], start=(k == 0), stop=(k == 7)),
                             reads=[bw, bhx], writes=[bp1], inc=(k == 7))
                    for k in range(8):
                        B.op("pe", lambda e, k=k, p2=p2, part=part: e.matmul(p2[:, 0:128], lhsT=w[:, part, k, :], rhs=hxT[:, k, 512:640], start=(k == 0), stop=(k == 7)),
                             reads=[bw, bhx], writes=[bp2], inc=(k == 7))
                    B.op("act", lambda e, u_=u_, p1=p1: e.activation(out=u_[:, 0:8, 1:65], in_=p1[:].rearrange("p (r c) -> p r c", c=64), func=AF.Identity),
                         reads=[bp1], writes=[bu_])
                    B.op("act", lambda e, u_=u_, p2=p2: e.activation(out=u_[:, 8:10, 1:65], in_=p2[:, 0:128].rearrange("p (r c) -> p r c", c=64), func=AF.Identity),
                         reads=[bp2], writes=[bu_])
                    if j == 0:
                        B.op("pool", lambda e, u_=u_: e.memset(u_[:, 0:1, :], 0.0), writes=[bu_])
                yield
                pcs = []
                for part in range(2):
                    u_, bu_ = ups[part]
                    dg, bdg = dgs[part]
                    pc, bpc = pr.next()
                    t = 0
                    for dr in range(3):
                        for dc_ in range(3):
                            B.op("pe", lambda e, t=t, dr=dr, dc_=dc_, pc=pc, u_=u_, dg=dg: e.matmul(
                                pc[:].rearrange("p (r c) -> p r c", c=64), lhsT=dg[:, t, :], rhs=u_[:, dr:dr + 8, dc_:dc_ + 64], start=(t == 0), stop=(t == 8)),
                                reads=[bu_, bdg], writes=[bpc], inc=(t == 8))
                            t += 1
                    pcs.append((pc, bpc))
                s_, bs_ = sgt.next()
                B.op("act", lambda e: e.activation(out=s_[:], in_=pcs[0][0][:], func=AF.Silu), reads=[pcs[0][1]], writes=[bs_])
                B.op("dve", lambda e: e.tensor_tensor(out=aT[:, c, :], in0=pcs[1][0][:], in1=s_[:], op=ALU.mult), reads=[pcs[1][1], bs_], writes=[baT])

            self.run_pipeline([(lambda c=c: pair_gen(c)) for c in range(22)], DEPTH)
            for s in range(4):
                for hf in range(2):
                    p, bp = pr.next()
                    for c in range(22):
                        B.op("pe", lambda e, c=c, p=p, s=s, hf=hf: e.matmul(p[:], lhsT=aT[:, c, s * 128:(s + 1) * 128], rhs=wd[:, c, hf * 512:(hf + 1) * 512],
                                                                           start=(c == 0), stop=(c == 21)), reads=[baT, bwd], writes=[bp], inc=(c == 21))
                    t_, bt_ = t2.next()
                    B.op("dve", lambda e, p=p, t_=t_, hf=hf: e.tensor_tensor(out=t_[:], in0=p[:], in1=self.gate_bc[:, 1, hf * 512:(hf + 1) * 512], op=ALU.mult),
                         reads=[bp, self.bgate], writes=[bt_])
                    B.op("pool", lambda e, t_=t_, s=s, hf=hf: e.tensor_tensor(out=xo[:, s, hf * 512:(hf + 1) * 512], in0=xo[:, s, hf * 512:(hf + 1) * 512], in1=t_[:], op=ALU.add),
                         reads=[bt_, bxo[s]], writes=[bxo[s]])
                B.op("act", lambda e, s=s: e.activation(out=junk[:], in_=xo[:, s, :], func=AF.Square, accum_out=sq2[:, s:s + 1]), reads=[bxo[s]], writes=[bjunk, bsq2])
                B.op("dve", lambda e, s=s: e.tensor_scalar(out=sq2[:, 4 + s:5 + s], in0=sq2[:, s:s + 1], scalar1=float(D * EPS), scalar2=None, op0=ALU.add), reads=[bsq2], writes=[bsq2])
                B.op("pool", lambda e, s=s: e.tensor_tensor(out=sq2[:, 8 + s:9 + s], in0=sq2[:, 4 + s:5 + s], in1=self.nhalf[:, 0:1], op=ALU.pow), reads=[bsq2, self.cb], writes=[bsq2])
                B.op("dve", lambda e, s=s: e.scalar_tensor_tensor(out=xo[:, s, :], in0=xo[:, s, :], scalar=sq2[:, 8 + s:9 + s], in1=nob[:], op0=ALU.mult, op1=ALU.mult),
                     reads=[bxo[s], bsq2, bsm0], writes=[bxo[s]])
                B.op("act", lambda e, s=s: e.activation(out=xo[:, s, :], in_=xo[:, s, :], func=AF.Identity, scale=32.0), reads=[bxo[s]], writes=[bxo[s]])
                B.dma("sp", self.out[j * 512 + s * 128: j * 512 + (s + 1) * 128, :], xo[:, s, :], bxo[s], reads=[bxo[s]])
        B.barrier()
        st.close()


def _build_once(debug, needed):
    P = Prog(debug=debug, needed=needed)
    P.precast_wup()
    P.phase0()
    P.phaseA()
    P.phaseB()
    P.phaseC1()
    P.phaseC2()
    P.top.close()
    return P.B.finish(), P


def build_program(debug=None):
    _, dry = _build_once(debug, None)
    return _build_once(debug, dry.B.waited)


_CACHE = {}


def kernel(**inputs):
    inp = {k: np.asarray(v) for k, v in inputs.items()}
    if "nc" not in _CACHE:
        _CACHE["nc"] = build_program()[0]
    nc = _CACHE["nc"]
    in_maps = [prep_core(inp, core) for core in range(8)]
    res = run_bass_kernel_spmd(nc, in_maps, core_ids=list(range(8)))
    out = np.empty((4, T, D), np.float32)
    for core in range(8):
        o = np.asarray(res.results[core]["out"], np.float32)
        b = core // 2
        if core % 2 == 0:
            out[b, 0:4096] = o
        else:
            out[b, 4096:8192] = o[::-1]
    return out
```

```python
import numpy as np
from contextlib import ExitStack

import concourse.bass as bass
import concourse.mybir as mybir
from concourse.bass_utils import run_bass_kernel_spmd

F32 = mybir.dt.float32
BF16 = mybir.dt.bfloat16
AF = mybir.ActivationFunctionType
ALU = mybir.AluOpType
AX = mybir.AxisListType

D = 1024
T = 8192
TC = 256
KD = 8
EPS = 1e-6
NEG = -1.0e30


class Buf:
    __slots__ = ("name", "w", "r", "dsem")

    def __init__(self, name):
        self.name = name
        self.w = None
        self.r = {}
        self.dsem = None


class Builder:
    def __init__(self, needed=None):
        self.nc = bass.Bass("TRN2", target_bir_lowering=False)
        nc = self.nc
        self.dry = needed is None
        self.needed = needed
        self.waited = {}
        self.es = ExitStack()
        self.es.enter_context(nc.allow_low_precision("bf16 matmul operands, fp32 accumulation"))
        self.engs = {"pe": nc.tensor, "act": nc.scalar, "dve": nc.vector, "pool": nc.gpsimd, "sp": nc.sync}
        self.sems = {}
        self.cnt = {}
        self.rank = {}
        self.rank_of = {}
        self.seen = {e: {} for e in self.engs}
        for e in self.engs:
            self.sems[e] = self.es.enter_context(nc.semaphore("s_" + e))
            self.cnt[e] = 0
            self.rank[e] = 0
            self.rank_of[e] = {}
            self.waited[e] = set()
        self.ndsem = 0
        self.nins = 0

    def sb(self, stack, name, shape, dt):
        return stack.enter_context(self.nc.sbuf_tensor(name, list(shape), dt))

    def ps(self, stack, name, shape, dt=F32):
        return stack.enter_context(self.nc.psum_tensor(name, list(shape), dt))

    def dram(self, name, shape, dt, kind="Internal"):
        return self.nc.dram_tensor(name, list(shape), dt, kind=kind).ap()

    def new_dsem(self):
        k = "d%d" % self.ndsem
        self.ndsem += 1
        self.sems[k] = self.es.enter_context(self.nc.semaphore(k))
        self.cnt[k] = 0
        return k

    def _deps(self, eng, reads, writes):
        deps = {}

        def add(k, v):
            if deps.get(k, 0) < v:
                deps[k] = v

        for b in reads:
            if b.w is not None:
                add(*b.w)
        for b in writes:
            if b.w is not None and b.w[0] != eng:
                add(*b.w)
            for k, v in b.r.items():
                if k != eng:
                    add(k, v)
        return deps

    def _emit_waits(self, eng, deps):
        e = self.engs[eng]
        seen = self.seen[eng]
        for k, v in deps.items():
            if seen.get(k, 0) >= v:
                continue
            assert v <= self.cnt[k], "wait on %s=%d never reached (issued %d)" % (k, v, self.cnt[k])
            seen[k] = v
            if k in self.engs:
                if self.dry:
                    self.waited[k].add(v)
                else:
                    e.wait_ge(self.sems[k], self.rank_of[k][v])
            elif not self.dry:
                e.wait_ge(self.sems[k], v)

    def op(self, eng, fn, reads=(), writes=(), inc=True):
        self._emit_waits(eng, self._deps(eng, reads, writes))
        self.cnt[eng] += 1
        idx = self.cnt[eng]
        self.nins += 1
        if not self.dry:
            ins = fn(self.engs[eng])
            if idx in self.needed[eng]:
                self.rank[eng] += 1
                self.rank_of[eng][idx] = self.rank[eng]
                ins.then_inc(self.sems[eng], 1)
        tok = (eng, idx)
        for b in reads:
            if b.r.get(eng, 0) < idx:
                b.r[eng] = idx
        for b in writes:
            b.w = tok
            b.r = {}
        return tok

    def dma(self, q, out, in_, sem_buf, reads=(), writes=()):
        self._emit_waits(q, self._deps("__dma__", reads, writes))
        if sem_buf.dsem is None:
            sem_buf.dsem = self.new_dsem()
        k = sem_buf.dsem
        self.cnt[k] += 16
        self.nins += 1
        if not self.dry:
            ins = self.engs[q].dma_start(out=out, in_=in_)
            ins.then_inc(self.sems[k], 16)
        tok = (k, self.cnt[k])
        for b in reads:
            if b.r.get(k, 0) < tok[1]:
                b.r[k] = tok[1]
        for b in writes:
            b.w = tok
            b.r = {}
        return tok

    def cc_allreduce(self, in_ap, out_ap, groups, sem_buf, reads=(), writes=()):
        self._emit_waits("pool", self._deps("__dma__", reads, writes))
        if sem_buf.dsem is None:
            sem_buf.dsem = self.new_dsem()
        k = sem_buf.dsem
        self.cnt[k] += 1
        self.nins += 1
        if not self.dry:
            ins = self.nc.gpsimd.collective_compute("AllReduce", ALU.add, replica_groups=groups, ins=[in_ap], outs=[out_ap])
            ins.then_inc(self.sems[k])
        tok = (k, self.cnt[k])
        for b in reads:
            if b.r.get(k, 0) < tok[1]:
                b.r[k] = tok[1]
        for b in writes:
            b.w = tok
            b.r = {}
        return tok

    def barrier(self):
        for e in self.engs:
            self._emit_waits(e, {k: v for k, v in self.cnt.items() if k != e and v > 0})

    def finish(self):
        self.barrier()
        self.es.close()
        return self.nc


OFF_QKV, OFF_A, OFF_B, OFF_MQ, OFF_MK, OFF_MV, OFF_MI, OFF_MF, OFF_Z, OFF_MO, OFF_GG, OFF_GM, OFF_END = (
    0, 3072, 3088, 3104, 3616, 4128, 5152, 5160, 5168, 6192, 7216, 8240, 9264)
NCH = 66
OWN_T = 4224


def _col(v, n=128):
    v = np.asarray(v, np.float32).reshape(-1, n)
    return np.ascontiguousarray(v.T)


def _rep(v):
    v = np.asarray(v, np.float32).reshape(1, -1)
    return np.ascontiguousarray(np.repeat(v, 128, axis=0))


def _swapdir(a, flip):
    if not flip:
        return a
    h = a.shape[-1] // 2
    return np.concatenate([a[..., h:], a[..., :h]], axis=-1)


def prep_core(inp, core):
    b = core // 2
    flip = core % 2
    f32 = np.float32
    x = inp["x"][b]
    ctx = inp["ctx"][b]
    if flip:
        x = x[::-1]
        ctx = ctx[::-1]
    w_in = inp["w_in"][0]
    m = {}
    m["x"] = np.ascontiguousarray(x, dtype=f32)
    m["ctx"] = np.ascontiguousarray(ctx, dtype=f32)
    m["c_col"] = _col(inp["c"][b])
    m["cc_col"] = _col(inp["c_ctx"])
    m["w_ada"] = np.ascontiguousarray(inp["w_ada"][0], dtype=f32)
    b_ada = inp["b_ada"][0]
    m["b_ada_col"] = _col(b_ada)
    m["b_ada_g"] = np.ascontiguousarray(np.concatenate([_rep(b_ada[2048:3072]), _rep(b_ada[5120:6144])], axis=1))
    m["n1_col"] = _col(inp["norm1_w"][0])
    m["n2_col"] = _col(inp["norm2_w"][0])
    m["w_qkv"] = np.ascontiguousarray(w_in[:, OFF_QKV:OFF_A])
    wg = np.concatenate([_swapdir(w_in[:, OFF_A:OFF_B], flip), _swapdir(w_in[:, OFF_B:OFF_MQ], flip),
                         _swapdir(w_in[:, OFF_MI:OFF_MF], flip), _swapdir(w_in[:, OFF_MF:OFF_Z], flip)], axis=1)
    m["w_gate"] = np.ascontiguousarray(wg)
    m["w_ml"] = np.ascontiguousarray(w_in[:, OFF_MQ:OFF_MI])
    m["w_o"] = np.ascontiguousarray(w_in[:, OFF_Z:OFF_END])
    gp = np.concatenate([_swapdir(inp["gdn_dt_bias"][0].reshape(-1), flip), _swapdir(inp["gdn_a_log"][0].reshape(-1), flip),
                         _swapdir(inp["ml_igate_b"][0].reshape(-1), flip), _swapdir(inp["ml_fgate_b"][0].reshape(-1), flip)])
    m["gate_p"] = _rep(gp)
    gc = inp["gdn_conv"][0]
    if flip:
        gc = gc[::-1]
    m["gdn_cw"] = np.ascontiguousarray(gc.T.reshape(24, 128, 3).transpose(1, 0, 2), dtype=f32)
    fc = inp["ffn_conv"][0]
    if flip:
        fc = fc[::-1, ::-1]
    m["ffn_cw"] = np.ascontiguousarray(fc.reshape(9, 44, 128).transpose(2, 1, 0), dtype=f32)
    m["gnw_bc"] = _rep(np.tile(inp["gdn_norm_w"][0], 8))
    m["mnw_bc"] = _rep(inp["ml_norm_w"][0].reshape(-1))
    m["now_bc"] = _rep(inp["norm_out_w"])
    m["w_bg"] = np.ascontiguousarray(inp["w_branch_gdn"][0], dtype=f32)
    m["w_bm"] = np.ascontiguousarray(inp["w_branch_ml"][0], dtype=f32)
    m["w_out"] = np.ascontiguousarray(inp["w_out"][0], dtype=f32)
    m["w_up"] = np.ascontiguousarray(inp["w_up"][0], dtype=f32)
    m["w_down"] = np.ascontiguousarray(inp["w_down"][0], dtype=f32)
    m["smask"] = make_smask()
    return m


def make_smask():
    idx = np.arange(128)
    i = idx[None, :]
    j = idx[:, None]
    out = np.zeros((128, 14, 128), np.float32)
    for lev in range(7):
        b = 1 << lev
        same = (i // (2 * b)) == (j // (2 * b))
        f = same & ((i % (2 * b)) < b) & ((j % (2 * b)) >= b)
        g = same & ((j % (2 * b)) < b) & ((i % (2 * b)) >= b)
        out[:, lev, :] = np.where(f, -1.0, 0.0) + np.eye(128)
        out[:, 7 + lev, :] = np.where(g, -1.0, 0.0) + np.eye(128)
    return out


IN_SHAPES = {
    "x": [T, D], "ctx": [TC, D], "c_col": [128, 8], "cc_col": [128, 8], "w_ada": [D, 6144],
    "b_ada_col": [128, 48], "b_ada_g": [128, 2048], "n1_col": [128, 8], "n2_col": [128, 8],
    "w_qkv": [D, 3072], "w_gate": [D, 48], "w_ml": [D, 2048], "w_o": [D, 4096], "gate_p": [128, 48],
    "gdn_cw": [128, 24, 3], "ffn_cw": [128, 44, 9], "gnw_bc": [128, 1024], "mnw_bc": [128, 1024],
    "now_bc": [128, 1024], "w_bg": [D, D], "w_bm": [D, D], "w_out": [D, D], "w_up": [D, 5632], "w_down": [2816, D],
    "smask": [128, 14, 128],
}


class Ring:
    def __init__(self, B, stack, name, n, shape, dt, psum=False):
        self.slots = []
        for i in range(n):
            t = (B.ps if psum else B.sb)(stack, "%s%d" % (name, i), shape, dt)
            self.slots.append((t, Buf("%s%d" % (name, i))))
        self.i = 0

    def next(self):
        s = self.slots[self.i % len(self.slots)]
        self.i += 1
        return s


class Prog:
    def __init__(self, debug=None, needed=None):
        self.debug = debug or {}
        self.B = Builder(needed)
        self.nc = self.B.nc
        self.top = ExitStack()
        self.inp = {}
        for k, shp in IN_SHAPES.items():
            self.inp[k] = self.nc.dram_tensor(k, list(shp), F32, kind="ExternalInput").ap()
        self.out = self.nc.dram_tensor("out", [4096, D], F32, kind="ExternalOutput").ap()
        dk = "ExternalOutput" if self.debug.get("scratch_out") else "Internal"
        B = self.B
        self.KT = B.dram("KT", [NCH, 128, 8, 128], BF16, dk)
        self.QT = B.dram("QT", [NCH, 128, 8, 128], BF16, dk)
        self.VG = B.dram("VG", [NCH, 128, 1024], BF16, dk)
        self.MQT = B.dram("MQT", [NCH, 128, 4, 128], BF16, dk)
        self.MKT = B.dram("MKT", [NCH, 128, 4, 128], BF16, dk)
        self.MV = B.dram("MV", [NCH, 128, 1024], BF16, dk)
        self.GT = B.dram("GT", [NCH, 128, 48], F32, dk)
        self.OF = B.dram("OF", [33, 128, 1024], F32, dk)
        self.OB = B.dram("OB", [33, 128, 1024], F32, dk)
        self.HF = B.dram("HF", [33, 128, 1024], F32, dk)
        self.HB = B.dram("HB", [33, 128, 1024], F32, dk)
        self.X1 = B.dram("X1", [64 + OWN_T, D], F32, dk)
        self.SNAP_IN = self.nc.dram_tensor("SNAP_IN", [128, 2064], F32)
        self.SNAP_OUT = self.nc.dram_tensor("SNAP_OUT", [128, 2064], F32)
        self.consts()

    def consts(self):
        B, st = self.B, self.top
        self.ident_f = B.sb(st, "ident_f", [128, 128], F32)
        self.ident_b = B.sb(st, "ident_b", [128, 128], BF16)
        self.ones_f = B.sb(st, "ones_f", [128, 128], F32)
        self.ones_b = B.sb(st, "ones_b", [128, 128], BF16)
        self.nhalf = B.sb(st, "nhalf", [128, 512], F32)
        self.cb = Buf("consts")
        cb = self.cb
        B.op("pool", lambda e: e.memset(self.ones_f[:], 1.0), writes=[cb])
        B.op("pool", lambda e: e.memset(self.ones_b[:], 1.0), writes=[cb])
        B.op("pool", lambda e: e.memset(self.nhalf[:], -0.5), writes=[cb])
        B.op("pool", lambda e: e.memset(self.ident_f[:], 1.0), writes=[cb])
        B.op("pool", lambda e: e.affine_select(self.ident_f[:], self.ident_f[:], pattern=[[-1, 128]], compare_op=ALU.is_equal,
                                               fill=0.0, base=0, channel_multiplier=1), reads=[cb], writes=[cb])
        B.op("dve", lambda e: e.tensor_copy(out=self.ident_b[:], in_=self.ident_f[:]), reads=[cb], writes=[cb])
        self.modc = B.sb(st, "modc", [128, 6, 8], F32)
        self.bmod = Buf("modc")
        self.gate_bc = B.sb(st, "gate_bc", [128, 2, 1024], F32)
        self.bgate = Buf("gate_bc")

    def mask(self, stack, name, cmp_pat, dt=F32, val=1.0, fill=0.0):
        B = self.B
        base, cm, step, cmp = cmp_pat
        t = B.sb(stack, name, [128, 128], dt)
        tf = t
        if dt != F32:
            tf = B.sb(stack, name + "_f", [128, 128], F32)
        b = Buf(name)
        B.op("pool", lambda e: e.memset(tf[:], val), writes=[b])
        B.op("pool", lambda e: e.affine_select(tf[:], tf[:], pattern=[[step, 128]], compare_op=cmp, fill=fill,
                                               base=base, channel_multiplier=cm), reads=[b], writes=[b])
        if dt != F32:
            B.op("dve", lambda e: e.tensor_copy(out=t[:], in_=tf[:]), reads=[b], writes=[b])
        return t, b

    def phase0(self):
        B, nc, inp = self.B, self.nc, self.inp
        st = ExitStack()
        sc = B.sb(st, "p0_sc", [128, 16], F32)
        bsc = Buf("p0_sc")
        scb = B.sb(st, "p0_scb", [128, 8, 128], F32)
        bscb = Buf("p0_scb")
        bcol = B.sb(st, "p0_bcol", [128, 48], F32)
        n12 = B.sb(st, "p0_n12", [128, 16], F32)
        bg = B.sb(st, "p0_bg", [128, 2048], F32)
        bsm = Buf("p0_small")
        B.dma("sp", sc[:, 0:8], inp["c_col"][:, :], bsc, writes=[bsc])
        B.dma("sp", sc[:, 8:16], inp["cc_col"][:, :], bsc, writes=[bsc])
        B.dma("sp", bcol[:], inp["b_ada_col"][:, :], bsm, writes=[bsm])
        B.dma("sp", n12[:, 0:8], inp["n1_col"][:, :], bsm, writes=[bsm])
        B.dma("sp", n12[:, 8:16], inp["n2_col"][:, :], bsm, writes=[bsm])
        B.dma("sp", bg[:], inp["b_ada_g"][:, :], bsm, writes=[bsm])
        B.op("act", lambda e: e.activation(out=sc[:], in_=sc[:], func=AF.Silu), reads=[bsc], writes=[bsc])
        for k in range(8):
            B.op("dve", lambda e, k=k: e.tensor_scalar(out=scb[:, k, :], in0=self.ones_f[:], scalar1=sc[:, k:k + 1], scalar2=None,
                                                       op0=ALU.mult), reads=[bsc, self.cb], writes=[bscb])
        wring = Ring(B, st, "p0_w", 2, [128, 8, 512], F32)
        pcol = B.ps(st, "p0_pcol", [128, 64], F32)
        bpcol = Buf("p0_pcol")
        prow = Ring(B, st, "p0_prow", 2, [128, 512], F32, psum=True)
        wv = inp["w_ada"].rearrange("(k p) n -> p k n", p=128)
        xslot = {0: 0, 1: 1, 3: 2, 4: 3}
        for nb in range(12):
            v, half = nb // 2, nb % 2
            w, bw = wring.next()
            B.dma("sp", w[:], wv[:, :, nb * 512:(nb + 1) * 512], bw, writes=[bw])
            if v in (2, 5):
                p, bp = prow.next()
                for k in range(8):
                    B.op("pe", lambda e, k=k, p=p, w=w: e.matmul(p[:], lhsT=scb[:, k, :], rhs=w[:, k, :], start=(k == 0), stop=(k == 7)),
                         reads=[bscb, bw], writes=[bp], inc=(k == 7))
                gi = 0 if v == 2 else 1
                B.op("dve", lambda e, p=p, gi=gi, half=half: e.tensor_tensor(
                    out=self.gate_bc[:, gi, half * 512:(half + 1) * 512], in0=p[:], in1=bg[:, gi * 1024 + half * 512: gi * 1024 + (half + 1) * 512],
                    op=ALU.add), reads=[bp, bsm], writes=[self.bgate])
            else:
                for cc in range(4):
                    col = xslot[v] * 8 + half * 4 + cc
                    for k in range(8):
                        B.op("pe", lambda e, k=k, w=w, cc=cc, col=col: e.matmul(pcol[:, col:col + 1], lhsT=w[:, k, cc * 128:(cc + 1) * 128],
                                                                                 rhs=sc[:, k:k + 1], start=(k == 0), stop=(k == 7)),
                             reads=[bw, bsc], writes=[bpcol], inc=(k == 7))
                    if v in (0, 1):
                        col2 = 32 + v * 8 + half * 4 + cc
                        for k in range(8):
                            B.op("pe", lambda e, k=k, w=w, cc=cc, col2=col2: e.matmul(pcol[:, col2:col2 + 1], lhsT=w[:, k, cc * 128:(cc + 1) * 128],
                                                                                       rhs=sc[:, 8 + k:9 + k], start=(k == 0), stop=(k == 7)),
                                 reads=[bw, bsc], writes=[bpcol], inc=(k == 7))
        mc = B.sb(st, "p0_mc", [128, 6, 8], F32)
        bmc = Buf("p0_mc")
        for i, v in enumerate((0, 1, 3, 4)):
            B.op("dve", lambda e, i=i, v=v: e.tensor_tensor(out=mc[:, i, :], in0=pcol[:, i * 8:(i + 1) * 8], in1=bcol[:, v * 8:(v + 1) * 8], op=ALU.add),
                 reads=[bpcol, bsm], writes=[bmc])
        for i, v in enumerate((0, 1)):
            B.op("dve", lambda e, i=i, v=v: e.tensor_tensor(out=mc[:, 4 + i, :], in0=pcol[:, 32 + i * 8:32 + (i + 1) * 8], in1=bcol[:, v * 8:(v + 1) * 8],
                                                            op=ALU.add), reads=[bpcol, bsm], writes=[bmc])
        md = self.modc
        for dst, (sci, shi, nw) in {0: (1, 0, 0), 2: (5, 4, 0), 4: (3, 2, 1)}.items():
            B.op("dve", lambda e, dst=dst, sci=sci, nw=nw: e.scalar_tensor_tensor(out=md[:, dst, :], in0=mc[:, sci, :], scalar=1.0, in1=n12[:, nw * 8:(nw + 1) * 8],
                                                                                  op0=ALU.add, op1=ALU.mult), reads=[bmc, bsm], writes=[self.bmod])
            B.op("dve", lambda e, dst=dst, shi=shi: e.tensor_copy(out=md[:, dst + 1, :], in_=mc[:, shi, :]), reads=[bmc], writes=[self.bmod])
        B.barrier()
        st.close()

    @staticmethod
    def run_pipeline(makers, depth):
        active = []
        it = iter(makers)
        exhausted = False
        while True:
            for g in list(active):
                try:
                    next(g)
                except StopIteration:
                    active.remove(g)
            if not exhausted and len(active) < depth:
                try:
                    g = next(it)()
                    try:
                        next(g)
                        active.append(g)
                    except StopIteration:
                        pass
                except StopIteration:
                    exhausted = True
            if exhausted and not active:
                break

    def load_w_bf16(self, stack, name, ap, kchunks, ncols, nsplit=4):
        B = self.B
        t = B.sb(stack, name, [128, kchunks, ncols], BF16)
        b = Buf(name)
        v = ap.rearrange("(k p) n -> p k n", p=128)
        step = (ncols + nsplit - 1) // nsplit
        for i in range(0, ncols, step):
            j = min(ncols, i + step)
            B.dma("pool", t[:, :, i:j], v[:, :, i:j], b, writes=[b])
        return t, b

    def norm_transpose(self, xt, bxts, ns, xn, bxn, sq, bsq, junk, bjunk, ptr_ring, hxT, bhx, col0, ai, npart=128):
        B = self.B
        for s in range(ns):
            B.op("act", lambda e, s=s: e.activation(out=xn[0:npart, s, :], in_=xt[0:npart, s, :], func=AF.Square, accum_out=sq[0:npart, s:s + 1]),
                 reads=[bxts[s]], writes=[bxn, bsq])
        B.op("dve", lambda e: e.tensor_scalar(out=sq[0:npart, 8:8 + ns], in0=sq[0:npart, 0:ns], scalar1=float(D * EPS), scalar2=None, op0=ALU.add),
             reads=[bsq], writes=[bsq])
        B.op("pool", lambda e: e.tensor_tensor(out=sq[0:npart, 16:16 + ns], in0=sq[0:npart, 8:8 + ns], in1=self.nhalf[0:npart, 0:ns], op=ALU.pow),
             reads=[bsq, self.cb], writes=[bsq])
        for s in range(ns):
            B.op("dve", lambda e, s=s: e.tensor_scalar(out=xn[0:npart, s, :], in0=xt[0:npart, s, :], scalar1=sq[0:npart, 16 + s:17 + s], scalar2=32.0,
                                                       op0=ALU.mult, op1=ALU.mult), reads=[bxts[s], bsq], writes=[bxn])
        for k in range(KD):
            p, bp = ptr_ring.next()
            pb = p[:].bitcast(BF16)
            for s in range(ns):
                B.op("pe", lambda e, s=s, k=k, pb=pb: e.transpose(pb[:, s * npart:(s + 1) * npart], xn[0:npart, s, k * 128:(k + 1) * 128],
                                                                  self.ident_b[0:npart, 0:npart]),
                     reads=[bxn, self.cb], writes=[bp], inc=(s == ns - 1))
            B.op("act", lambda e, k=k, pb=pb: e.activation(out=hxT[:, k, col0:col0 + ns * npart], in_=pb[:, 0:ns * npart], func=AF.Identity,
                                                           scale=self.modc[:, ai, k:k + 1], bias=self.modc[:, ai + 1, k:k + 1]),
                 reads=[bp, self.bmod], writes=[bhx])

    def phaseA(self):
        B, nc, inp = self.B, self.nc, self.inp
        st = ExitStack()
        wqkv, bwqkv = self.load_w_bf16(st, "a_wqkv", inp["w_qkv"], 8, 3072, 6)
        wml, bwml = self.load_w_bf16(st, "a_wml", inp["w_ml"], 8, 2048, 4)
        wgt, bwgt = self.load_w_bf16(st, "a_wgt", inp["w_gate"], 8, 48, 1)
        cw = B.sb(st, "a_cw", [128, 24, 3], F32)
        gp = B.sb(st, "a_gp", [128, 48], F32)
        bsm = Buf("a_small")
        B.dma("sp", cw[:], inp["gdn_cw"][:, :, :], bsm, writes=[bsm])
        B.dma("sp", gp[:], inp["gate_p"][:, :], bsm, writes=[bsm])
        B.op("act", lambda e: e.activation(out=gp[:, 16:32], in_=gp[:, 16:32], func=AF.Exp), reads=[bsm], writes=[bsm])
        B.op("dve", lambda e: e.tensor_scalar(out=gp[:, 16:32], in0=gp[:, 16:32], scalar1=-1.0, scalar2=None, op0=ALU.mult), reads=[bsm], writes=[bsm])
        xt = B.sb(st, "a_x", [128, 4, 1024], F32)
        bxts = [Buf("a_x%d" % i) for i in range(4)]
        xh = B.sb(st, "a_xh", [2, 1, 1024], F32); bxh = Buf("a_xh")
        xn = B.sb(st, "a_xn", [128, 4, 1024], BF16); bxn = Buf("a_xn")
        xnh = B.sb(st, "a_xnh", [2, 1, 1024], BF16); bxnh = Buf("a_xnh")
        sq = B.sb(st, "a_sq", [128, 24], F32); bsq = Buf("a_sq")
        sqh = B.sb(st, "a_sqh", [128, 24], F32); bsqh = Buf("a_sqh")
        junk = None; bjunk = None
        hxT = B.sb(st, "a_hxT", [128, 8, 514], BF16); bhx = Buf("a_hxT")
        hxh = B.sb(st, "a_hxh", [128, 8, 2], BF16); bhxh = Buf("a_hxh")
        ptr = Ring(B, st, "a_ptr", 2, [128, 512], F32, psum=True)
        pz = Ring(B, st, "a_pz", 4, [128, 512], F32, psum=True)
        pmisc = B.ps(st, "a_pmisc", [128, 512], F32)
        pzh = pmisc[:, 0:64]; bpzh = Buf("a_pzh")
        pn = ptr
        zb = Ring(B, st, "a_zb", 4, [128, 514], F32)
        y1 = Ring(B, st, "a_y1", 4, [128, 512], F32)
        sqb = Ring(B, st, "a_sqb", 2, [128, 512], BF16)
        skeep = B.sb(st, "a_skeep", [128, 8, 512], BF16)
        bskeep = [Buf("a_skeep%d" % i) for i in range(8)]
        rnr = B.sb(st, "a_rnr", [8, 512], F32); brnr = Buf("a_rnr")
        ind = B.sb(st, "a_ind", [128, 8, 8], BF16); bind = Buf("a_ind")
        selr = B.sb(st, "a_selr", [8, 8, 128], F32); bselr = Buf("a_selr")
        B.op("pool", lambda e: e.memset(ind[:], 0.0), writes=[bind])
        for jj in range(8):
            B.op("pool", lambda e, jj=jj: e.memset(ind[:, jj, jj:jj + 1], 1.0), writes=[bind])
            B.op("dve", lambda e, jj=jj: e.tensor_copy(out=selr[:, jj, :], in_=self.ident_f[0:8, jj:jj + 1].to_broadcast([8, 128])), reads=[self.cb], writes=[bselr])
        pss = B.ps(st, "a_pss", [128, 512], F32); bpss = Buf("a_pss")
        kst = B.sb(st, "a_kst", [128, 4, 8, 128], BF16); bkst = Buf("a_kst")
        qst = B.sb(st, "a_qst", [128, 4, 8, 128], BF16); bqst = Buf("a_qst")
        vT = B.sb(st, "a_vT", [128, 8, 512], BF16); bvT = Buf("a_vT")
        vst = Ring(B, st, "a_vst", 1, [128, 4, 1024], BF16)
        mqst = B.sb(st, "a_mqst", [128, 4, 4, 128], BF16); bmqst = Buf("a_mqst")
        mkst = B.sb(st, "a_mkst", [128, 4, 4, 128], BF16); bmkst = Buf("a_mkst")
        graw = B.sb(st, "a_graw", [128, 4, 48], F32); bgraw = Buf("a_graw")
        gwk = B.sb(st, "a_gwk", [128, 4, 48], F32); bgwk = Buf("a_gwk")
        gsb = Ring(B, st, "a_gsb", 2, [128, 4, 48], F32)
        pg = pmisc[:, 64:256].rearrange("p (s g) -> p s g", g=48); bpg = Buf("a_pg")
        dkr = float(128 ** -0.5)

        tiles = [(inp["ctx"], 0, 2, 0, False, False)]
        for i in range(8):
            tiles.append((inp["x"], i * 512, 4, 2 + 4 * i, i > 0, True))
        tiles.append((inp["x"], 4096, 1, 34, True, True))
        if self.debug.get("a_tiles"):
            tiles = tiles[: self.debug["a_tiles"]]
        for (src, t0, ns, c0, hl, hr) in tiles:
            n = ns * 128
            ai = 2 if src is inp["ctx"] else 0
            for s in range(ns):
                B.dma("sp", xt[:, s, :], src[t0 + s * 128:t0 + (s + 1) * 128, :], bxts[s], writes=[bxts[s]])
            tl = t0 - 1 if hl else t0
            tr = t0 + n if hr else t0
            B.dma("sp", xh[0:1, 0, :], src[tl:tl + 1, :], bxh, writes=[bxh])
            B.dma("sp", xh[1:2, 0, :], src[tr:tr + 1, :], bxh, writes=[bxh])
            self.norm_transpose(xt, bxts, ns, xn, bxn, sq, bsq, junk, bjunk, ptr, hxT, bhx, 1, ai)
            self.norm_transpose(xh, [bxh], 1, xnh, bxnh, sqh, bsqh, junk, bjunk, ptr, hxh, bhxh, 0, ai, npart=2)
            def chunk_gen(j, kind, jj):
                p, bp = pz.next()
                z, bz = zb.next()
                a1, ba1 = y1.next()
                for k in range(8):
                    B.op("pe", lambda e, k=k: e.matmul(p[:, 0:n], lhsT=wqkv[:, k, j * 128:(j + 1) * 128], rhs=hxT[:, k, 1:1 + n],
                                                         start=(k == 0), stop=(k == 7)), reads=[bwqkv, bhx], writes=[bp], inc=(k == 7))
                for k in range(8):
                    B.op("pe", lambda e, k=k: e.matmul(pzh[:, 2 * j:2 * j + 2], lhsT=wqkv[:, k, j * 128:(j + 1) * 128], rhs=hxh[:, k, :],
                                                         start=(k == 0), stop=(k == 7)), reads=[bwqkv, bhxh], writes=[bpzh], inc=(k == 7))
                yield
                B.op("act", lambda e: e.activation(out=z[:, 1:1 + n], in_=p[:, 0:n], func=AF.Identity), reads=[bp], writes=[bz])
                B.op("act", lambda e: e.activation(out=z[:, 0:1], in_=pzh[:, 2 * j:2 * j + 1], func=AF.Identity), reads=[bpzh], writes=[bz])
                B.op("act", lambda e: e.activation(out=z[:, n + 1:n + 2], in_=pzh[:, 2 * j + 1:2 * j + 2], func=AF.Identity), reads=[bpzh], writes=[bz])
                if not hl:
                    B.op("pool", lambda e: e.memset(z[:, 0:1], 0.0), writes=[bz])
                if not hr:
                    B.op("pool", lambda e: e.memset(z[:, n + 1:n + 2], 0.0), writes=[bz])
                yield
                B.op("dve", lambda e: e.tensor_scalar(out=a1[:, 0:n], in0=z[:, 1:1 + n], scalar1=cw[:, j, 1:2], scalar2=None, op0=ALU.mult),
                     reads=[bz, bsm], writes=[ba1])
                B.op("dve", lambda e: e.scalar_tensor_tensor(out=a1[:, 0:n], in0=z[:, 0:n], scalar=cw[:, j, 0:1], in1=a1[:, 0:n],
                                                            op0=ALU.mult, op1=ALU.add), reads=[bz, bsm, ba1], writes=[ba1])
                B.op("dve", lambda e: e.scalar_tensor_tensor(out=a1[:, 0:n], in0=z[:, 2:2 + n], scalar=cw[:, j, 2:3], in1=a1[:, 0:n],
                                                            op0=ALU.mult, op1=ALU.add), reads=[bz, bsm, ba1], writes=[ba1])
                yield
                if kind == "v":
                    B.op("act", lambda e: e.activation(out=vT[:, jj, 0:n], in_=a1[:, 0:n], func=AF.Silu), reads=[ba1], writes=[bvT])
                else:
                    B.op("act", lambda e: e.activation(out=skeep[:, jj, 0:n], in_=a1[:, 0:n], func=AF.Silu), reads=[ba1], writes=[bskeep[jj]])
                    q2, bq2 = sqb.next()
                    B.op("pool", lambda e: e.tensor_tensor(out=q2[:, 0:n], in0=skeep[:, jj, 0:n], in1=skeep[:, jj, 0:n], op=ALU.mult),
                         reads=[bskeep[jj]], writes=[bq2])
                    B.op("pe", lambda e: e.matmul(pss[0:8, 0:n], lhsT=ind[:, jj, :], rhs=q2[:, 0:n], start=(jj == 0), stop=(jj == 7)),
                         reads=[bq2, bind], writes=[bpss])

            for half in range(2):
                self.run_pipeline([(lambda jj=jj: chunk_gen(half * 8 + jj, "qk", jj)) for jj in range(8)], 4)
                B.op("act", lambda e: e.activation(out=rnr[:, 0:n], in_=pss[0:8, 0:n], func=AF.Ln, bias=float(EPS)), reads=[bpss], writes=[brnr])
                B.op("act", lambda e: e.activation(out=rnr[:, 0:n], in_=rnr[:, 0:n], func=AF.Exp, scale=-0.5), reads=[brnr], writes=[brnr])
                for jj in range(8):
                    pp, bpp = pn.next()
                    B.op("pe", lambda e, pp=pp, jj=jj: e.matmul(pp[:, 0:n], lhsT=selr[:, jj, :], rhs=rnr[:, 0:n], start=True, stop=True),
                         reads=[brnr, bselr], writes=[bpp])
                    if half == 0:
                        B.op("dve", lambda e, pp=pp, jj=jj: e.scalar_tensor_tensor(
                            out=qst[:, 0:ns, jj, :], in0=skeep[:, jj, 0:n].rearrange("p (s t) -> p s t", t=128), scalar=dkr,
                            in1=pp[:, 0:n].rearrange("p (s t) -> p s t", t=128), op0=ALU.mult, op1=ALU.mult), reads=[bskeep[jj], bpp], writes=[bqst])
                    else:
                        B.op("dve", lambda e, pp=pp, jj=jj: e.tensor_tensor(
                            out=kst[:, 0:ns, jj, :], in0=skeep[:, jj, 0:n].rearrange("p (s t) -> p s t", t=128),
                            in1=pp[:, 0:n].rearrange("p (s t) -> p s t", t=128), op=ALU.mult), reads=[bskeep[jj], bpp], writes=[bkst])
            for s in range(ns):
                for k in range(8):
                    B.op("pe", lambda e, k=k, s=s: e.matmul(pg[:, s, :], lhsT=hxT[:, k, 1 + s * 128:1 + (s + 1) * 128], rhs=wgt[:, k, :],
                                                             start=(k == 0), stop=(k == 7)), reads=[bhx, bwgt], writes=[bpg], inc=(k == 7))
            g, bg_ = gsb.next()
            self.gate_math(pg, bpg, graw, bgraw, gwk, bgwk, g, bg_, gp, bsm, ns)
            B.dma("sp", self.GT[c0:c0 + ns].rearrange("c t g -> t c g"), g[:, 0:ns, :], bg_, reads=[bg_])
            self.run_pipeline([(lambda jj=jj: chunk_gen(16 + jj, "v", jj)) for jj in range(8)], 4)
            self.v_transposes(vT, bvT, ns, vst, pn, self.VG, c0)
            B.dma("sp", self.KT[c0:c0 + ns].rearrange("c d h t -> d c h t"), kst[:, 0:ns], bkst, reads=[bkst])
            B.dma("sp", self.QT[c0:c0 + ns].rearrange("c d h t -> d c h t"), qst[:, 0:ns], bqst, reads=[bqst])
            for j in range(16):
                p, bp = pz.next()
                for k in range(8):
                    B.op("pe", lambda e, k=k, p=p, j=j: e.matmul(p[:, 0:n], lhsT=wml[:, k, j * 128:(j + 1) * 128], rhs=hxT[:, k, 1:1 + n],
                                                                  start=(k == 0), stop=(k == 7)), reads=[bwml, bhx], writes=[bp], inc=(k == 7))
                if j < 4:
                    B.op("act", lambda e, p=p, j=j: e.activation(out=mqst[:, 0:ns, j, :], in_=p[:, 0:n].rearrange("p (s t) -> p s t", t=128),
                                                                  func=AF.Identity, scale=dkr), reads=[bp], writes=[bmqst])
                elif j < 8:
                    B.op("act", lambda e, p=p, j=j: e.activation(out=mkst[:, 0:ns, j - 4, :], in_=p[:, 0:n].rearrange("p (s t) -> p s t", t=128),
                                                                  func=AF.Identity), reads=[bp], writes=[bmkst])
                else:
                    B.op("act", lambda e, p=p, j=j: e.activation(out=vT[:, j - 8, 0:n], in_=p[:, 0:n], func=AF.Identity), reads=[bp], writes=[bvT])
            self.v_transposes(vT, bvT, ns, vst, pn, self.MV, c0)
            B.dma("sp", self.MQT[c0:c0 + ns].rearrange("c d h t -> d c h t"), mqst[:, 0:ns], bmqst, reads=[bmqst])
            B.dma("sp", self.MKT[c0:c0 + ns].rearrange("c d h t -> d c h t"), mkst[:, 0:ns], bmkst, reads=[bmkst])
        B.barrier()
        st.close()

    def v_transposes(self, vT, bvT, ns, vst, pn, dst, c0):
        B = self.B
        v, bv = vst.next()
        for s in range(ns):
            pp, bpp = pn.next()
            ppb = pp[:].bitcast(BF16)
            for h in range(8):
                B.op("pe", lambda e, s=s, h=h, ppb=ppb: e.transpose(ppb[:, h * 128:(h + 1) * 128], vT[:, h, s * 128:(s + 1) * 128], self.ident_b[:]),
                     reads=[bvT, self.cb], writes=[bpp], inc=(h == 7))
            B.op("act", lambda e, s=s, ppb=ppb, v=v: e.activation(out=v[:, s, :], in_=ppb[:, 0:1024], func=AF.Identity), reads=[bpp], writes=[bv])
        B.dma("sp", dst[c0:c0 + ns].rearrange("c t e -> t c e"), v[:, 0:ns, :], bv, reads=[bv])

    def gate_math(self, pg, bpg, graw, bgraw, wk, bwk, g, bg_, gp, bgp, ns):
        B = self.B
        S = slice(0, ns)

        def bc(lo, hi):
            return gp[:, lo:hi].unsqueeze(1).to_broadcast([128, ns, hi - lo])

        B.op("act", lambda e: e.activation(out=graw[:, S, :], in_=pg[:, S, :], func=AF.Identity), reads=[bpg], writes=[bgraw])
        B.op("dve", lambda e: e.tensor_tensor(out=wk[:, S, 0:16], in0=graw[:, S, 0:16], in1=bc(0, 16), op=ALU.add), reads=[bgraw, bgp], writes=[bwk])
        B.op("act", lambda e: e.activation(out=wk[:, S, 0:16], in_=wk[:, S, 0:16], func=AF.Exp), reads=[bwk], writes=[bwk])
        B.op("act", lambda e: e.activation(out=wk[:, S, 0:16], in_=wk[:, S, 0:16], func=AF.Ln, bias=1.0), reads=[bwk], writes=[bwk])
        B.op("dve", lambda e: e.tensor_tensor(out=g[:, S, 0:16], in0=wk[:, S, 0:16], in1=bc(16, 32), op=ALU.mult), reads=[bwk, bgp], writes=[bg_])
        B.op("act", lambda e: e.activation(out=wk[:, S, 16:32], in_=graw[:, S, 16:32], func=AF.Exp, scale=-1.0), reads=[bgraw], writes=[bwk])
        B.op("dve", lambda e: e.tensor_scalar(out=wk[:, S, 16:32], in0=wk[:, S, 16:32], scalar1=1.0, scalar2=None, op0=ALU.add), reads=[bwk], writes=[bwk])
        B.op("dve", lambda e: e.reciprocal(out=g[:, S, 16:32], in_=wk[:, S, 16:32]), reads=[bwk], writes=[bg_])
        B.op("dve", lambda e: e.tensor_tensor(out=wk[:, S, 32:48], in0=graw[:, S, 32:48], in1=bc(32, 48), op=ALU.add), reads=[bgraw, bgp], writes=[bwk])
        B.op("act", lambda e: e.activation(out=wk[:, S, 32:48], in_=wk[:, S, 32:48], func=AF.Exp, scale=float(2.0 / 15.0)), reads=[bwk], writes=[bwk])
        B.op("dve", lambda e: e.tensor_scalar(out=wk[:, S, 32:48], in0=wk[:, S, 32:48], scalar1=1.0, scalar2=None, op0=ALU.add), reads=[bwk], writes=[bwk])
        B.op("dve", lambda e: e.reciprocal(out=wk[:, S, 32:48], in_=wk[:, S, 32:48]), reads=[bwk], writes=[bwk])
        B.op("dve", lambda e: e.tensor_scalar(out=g[:, S, 32:48], in0=wk[:, S, 32:48], scalar1=-30.0, scalar2=15.0, op0=ALU.mult, op1=ALU.add),
             reads=[bwk], writes=[bg_])
        B.op("act", lambda e: e.activation(out=wk[:, S, 40:48], in_=g[:, S, 40:48], func=AF.Exp, scale=-1.0), reads=[bg_], writes=[bwk])
        B.op("act", lambda e: e.activation(out=wk[:, S, 40:48], in_=wk[:, S, 40:48], func=AF.Ln, bias=1.0), reads=[bwk], writes=[bwk])
        B.op("dve", lambda e: e.tensor_scalar(out=g[:, S, 40:48], in0=wk[:, S, 40:48], scalar1=-1.0, scalar2=None, op0=ALU.mult), reads=[bwk], writes=[bg_])

    def phaseB(self):
        B, nc, inp = self.B, self.nc, self.inp
        st = ExitStack()
        dbg = self.debug
        LE, bLE = self.mask(st, "b_LE", (0, -1, 1, ALU.is_ge))
        LT, bLT = self.mask(st, "b_LT", (-1, -1, 1, ALU.is_ge))
        GE, bGE = self.mask(st, "b_GE", (0, 1, -1, ALU.is_ge))
        GT_, bGT = self.mask(st, "b_GT", (-1, 1, -1, ALU.is_ge))
        MBf, bMBf = self.mask(st, "b_MBf", (0, 1, -1, ALU.is_ge), val=0.0, fill=NEG)
        MBb, bMBb = self.mask(st, "b_MBb", (0, -1, 1, ALU.is_ge), val=0.0, fill=NEG)
        SELf, bSELf = self.mask(st, "b_SELf", (-127, 1, 0, ALU.is_equal))
        SELb, bSELb = self.mask(st, "b_SELb", (0, 1, 0, ALU.is_equal))
        smask = B.sb(st, "b_smask", [128, 14, 128], BF16)
        bsm = Buf("b_smask")
        B.dma("pool", smask[:], inp["smask"][:, :, :], bsm, writes=[bsm])
        cbufs = [bLE, bLT, bGE, bGT, bMBf, bMBb, bSELf, bSELb, bsm, self.cb]
        dirc = [dict(U=LE, S=GT_, incl=LE, strict=LT, MB=MBf, SEL=SELf),
                dict(U=GE, S=LT, incl=GE, strict=GT_, MB=MBb, SEL=SELb)]
        S = [B.sb(st, "b_S%d" % d, [128, 8, 128], F32) for d in range(2)]
        bS = [[Buf("b_S%d_%d" % (d, g)) for g in range(2)] for d in range(2)]
        Sb = [[Ring(B, st, "b_Sb%d_%d_" % (d, g), 2, [128, 4, 128], BF16) for g in range(2)] for d in range(2)]
        Sb_cur = [[None, None], [None, None]]
        C = [B.sb(st, "b_C%d" % d, [128, 4, 256], F32) for d in range(2)]
        bC = [Buf("b_C%d" % d) for d in range(2)]
        Cb = [Ring(B, st, "b_Cb%d_" % d, 2, [128, 4, 256], BF16) for d in range(2)]
        Cb_cur = [None, None]
        nst = [Ring(B, st, "b_n%d_" % d, 2, [128, 8], F32) for d in range(2)]
        nbf = [Ring(B, st, "b_nb%d_" % d, 2, [128, 4], BF16) for d in range(2)]
        n_cur = [None, None]
        nb_cur = [None, None]
        mst = [Ring(B, st, "b_m%d_" % d, 2, [128, 4], F32) for d in range(2)]
        m_cur = [None, None]
        for d in range(2):
            B.op("pool", lambda e, d=d: e.memset(S[d][:], 0.0), writes=bS[d])
            B.op("pool", lambda e, d=d: e.memset(C[d][:], 0.0), writes=[bC[d]])
            for g in range(2):
                t, b = Sb[d][g].next()
                B.op("pool", lambda e, t=t: e.memset(t[:], 0.0), writes=[b])
                Sb_cur[d][g] = (t, b)
            t, b = Cb[d].next()
            B.op("pool", lambda e, t=t: e.memset(t[:], 0.0), writes=[b])
            Cb_cur[d] = (t, b)
            t, b = nst[d].next()
            B.op("pool", lambda e, t=t: e.memset(t[:], 0.0), writes=[b])
            n_cur[d] = (t, b)
            t, b = nbf[d].next()
            B.op("pool", lambda e, t=t: e.memset(t[:], 0.0), writes=[b])
            nb_cur[d] = (t, b)
            t, b = mst[d].next()
            B.op("pool", lambda e, t=t: e.memset(t[:], 0.0), writes=[b])
            m_cur[d] = (t, b)
        def dring(name, shape, dt):
            return [Ring(B, st, "b_%s%d_" % (name, d), 2, shape, dt) for d in range(2)]
        rKT = dring("KT", [128, 8, 128], BF16)
        rQT = dring("QT", [128, 8, 128], BF16)
        rVG = dring("VG", [128, 1024], BF16)
        rGT = dring("GT", [128, 48], F32)
        rMQ = dring("MQ", [128, 4, 128], BF16)
        rMK = dring("MK", [128, 4, 128], BF16)
        rMV = dring("MV", [128, 1024], BF16)
        rgs = dring("gs", [128, 64], F32)
        psr = Ring(B, st, "b_ps", 8, [128, 512], F32, psum=True)
        NG, NM = 4, 2
        gslots = []
        for i in range(NG):
            sl = {}
            for nm, shp, dt in (("A", [128, 4, 128], F32), ("Bt", [128, 4, 128], F32), ("Ct", [128, 4, 128], F32),
                                ("attnT", [128, 4, 128], BF16), ("Qp", [128, 4, 128], BF16),
                                ("Kg", [128, 4, 128], BF16), ("kt", [128, 4, 128], BF16), ("G0", [128, 4, 128], BF16),
                                ("G1", [128, 4, 128], BF16), ("H0", [128, 4, 128], BF16), ("H1", [128, 4, 128], BF16),
                                ("IYT", [128, 4, 128], BF16), ("negW", [128, 4, 128], BF16), ("vnew", [128, 4, 128], BF16)):
                sl[nm] = (B.sb(st, "b_g%d_%s" % (i, nm), shp, dt), Buf("b_g%d_%s" % (i, nm)))
            gslots.append(sl)
        mslots = []
        for i in range(NM):
            sl = {}
            for nm, shp, dt in (("X", [128, 4, 128], F32), ("Y", [128, 4, 128], F32), ("Pm", [128, 4, 128], BF16),
                                ("PT", [128, 4, 128], BF16), ("Kw", [128, 4, 128], BF16), ("sm", [128, 64], F32), ("Ct", [128, 4, 256], F32)):
                sl[nm] = (B.sb(st, "b_m%d_%s" % (i, nm), shp, dt), Buf("b_m%d_%s" % (i, nm)))
            mslots.append(sl)
        ring_o1 = Ring(B, st, "b_o1_", 2, [128, 4, 128], F32)
        ring_o = Ring(B, st, "b_o_", 2, [128, 4, 128], F32)
        ring_num = Ring(B, st, "b_num_", 1, [128, 4, 256], F32)
        ring_h = Ring(B, st, "b_h_", 1, [128, 4, 256], F32)

        def bc3(ap2, n):
            return ap2.unsqueeze(2).to_broadcast([128, 4, n])

        def bcm(ap2, n=4):
            return ap2.unsqueeze(1).to_broadcast([128, n, 128])

        nsteps = dbg.get("b_steps", NCH)
        order = [list(range(35)), list(range(34, 1, -1))]
        if dbg.get("b_order"):
            order = dbg["b_order"]
            nsteps = len(order[0])
        out_lo, out_hi = 2, 2 + 33

        data = {}

        def load_step(step, d):
            c = order[d][step]
            tk, bk = rKT[d].next(); tq, bq = rQT[d].next(); tv, bv = rVG[d].next(); tg, bg = rGT[d].next()
            tmq, bmq = rMQ[d].next(); tmk, bmk = rMK[d].next(); tmv, bmv = rMV[d].next()
            B.dma("sp", tg[:], self.GT[c], bg, writes=[bg])
            B.dma("sp", tk[:], self.KT[c], bk, writes=[bk])
            B.dma("sp", tq[:], self.QT[c], bq, writes=[bq])
            B.dma("sp", tv[:], self.VG[c], bv, writes=[bv])
            B.dma("sp", tmq[:], self.MQT[c], bmq, writes=[bmq])
            B.dma("sp", tmk[:], self.MKT[c], bmk, writes=[bmk])
            B.dma("sp", tmv[:], self.MV[c], bmv, writes=[bmv])
            data[(step, d)] = dict(c=c, KT=(tk, bk), QT=(tq, bq), VG=(tv, bv), GT=(tg, bg), MQ=(tmq, bmq), MK=(tmk, bmk), MV=(tmv, bmv))

        def shared_pre(step, d):
            dd = data[(step, d)]
            tg, bg = dd["GT"]
            gs, bgs = rgs[d].next()
            dc = dirc[d]
            p, bp = psr.next()
            B.op("pe", lambda e: e.matmul(p[:, 0:8], lhsT=dc["U"][:], rhs=tg[:, d * 8:(d + 1) * 8], start=True, stop=True), reads=[bg] + cbufs, writes=[bp], inc=False)
            B.op("pe", lambda e: e.matmul(p[:, 8:12], lhsT=dc["U"][:], rhs=tg[:, 40 + d * 4:44 + d * 4], start=True, stop=True), reads=[bg] + cbufs, writes=[bp], inc=False)
            B.op("pe", lambda e: e.matmul(p[:, 12:20], lhsT=self.ones_f[:], rhs=tg[:, d * 8:(d + 1) * 8], start=True, stop=True), reads=[bg] + cbufs, writes=[bp], inc=False)
            B.op("pe", lambda e: e.matmul(p[:, 20:24], lhsT=self.ones_f[:], rhs=tg[:, 40 + d * 4:44 + d * 4], start=True, stop=True), reads=[bg] + cbufs, writes=[bp])
            B.op("act", lambda e: e.activation(out=gs[:, 0:24], in_=p[:, 0:24], func=AF.Identity), reads=[bp], writes=[bgs])
            B.op("act", lambda e: e.activation(out=gs[:, 24:32], in_=gs[:, 0:8], func=AF.Exp), reads=[bgs], writes=[bgs])
            B.op("dve", lambda e: e.tensor_tensor(out=gs[:, 32:40], in0=gs[:, 12:20], in1=gs[:, 0:8], op=ALU.subtract), reads=[bgs], writes=[bgs])
            B.op("act", lambda e: e.activation(out=gs[:, 32:40], in_=gs[:, 32:40], func=AF.Exp), reads=[bgs], writes=[bgs])
            B.op("act", lambda e: e.activation(out=gs[:, 40:48], in_=gs[:, 12:20], func=AF.Exp), reads=[bgs], writes=[bgs])
            B.op("dve", lambda e: e.tensor_tensor(out=gs[:, 48:52], in0=tg[:, 32 + d * 4:36 + d * 4], in1=gs[:, 8:12], op=ALU.subtract), reads=[bgs, bg], writes=[bgs])
            dd["gs"] = (gs, bgs)

        def gdn_group(step, d, hg, sl):
            dd = data[(step, d)]
            dc = dirc[d]
            c = dd["c"]
            need_o = out_lo <= c < out_hi
            tk, bk = dd["KT"]; tq, bq = dd["QT"]; tv, bv = dd["VG"]; tg, bg = dd["GT"]; gs, bgs = dd["gs"]
            h0 = hg * 4
            A, bA = sl["A"]; Bt, bBt = sl["Bt"]; Ct, bCt = sl["Ct"]
            attnT, battn = sl["attnT"]; Qp, bQp = sl["Qp"]; Kg, bKg = sl["Kg"]; kt, bkt = sl["kt"]
            IYT, bIYT = sl["IYT"]; negW, bnegW = sl["negW"]; vnew, bvnew = sl["vnew"]
            Qm, bQm = sl["IYT"]
            St, bSt = sl["A"]
            gcol = tg[:, d * 8 + h0:d * 8 + h0 + 4]
            bcol = tg[:, 16 + d * 8 + h0:16 + d * 8 + h0 + 4]
            eg = gs[:, 24 + h0:24 + h0 + 4]
            ekt = gs[:, 32 + h0:32 + h0 + 4]
            gte = gs[:, 40 + h0:40 + h0 + 4]
            for u in range(4):
                B.op("act", lambda e, u=u: e.activation(out=A[:, u, :], in_=dc["U"][:], func=AF.Identity, scale=gcol[:, u:u + 1]), reads=[bg] + cbufs, writes=[bA])
            for u in range(4):
                B.op("act", lambda e, u=u: e.activation(out=Ct[:, u, :], in_=dc["strict"][:], func=AF.Identity, scale=bcol[:, u:u + 1]), reads=[bg] + cbufs, writes=[bCt])
            yield
            pD, bpD = psr.next()
            for u in range(4):
                B.op("pe", lambda e, u=u: e.matmul(pD[:, u * 128:(u + 1) * 128], lhsT=dc["S"][:], rhs=A[:, u, :], start=True, stop=True),
                     reads=[bA] + cbufs, writes=[bpD], inc=(u == 3))
            B.op("act", lambda e: e.activation(out=Bt[:].rearrange("p u l -> p (u l)"), in_=pD[:, :], func=AF.Exp), reads=[bpD], writes=[bBt])
            yield
            B.op("dve", lambda e: e.tensor_tensor(out=A[:], in0=Bt[:], in1=bcm(dc["incl"][:]), op=ALU.mult), reads=[bBt] + cbufs, writes=[bA])
            B.op("pool", lambda e: e.tensor_tensor(out=Ct[:], in0=Ct[:], in1=Bt[:], op=ALU.mult), reads=[bCt, bBt], writes=[bCt])
            yield
            pKK, bpKK = psr.next()
            pQK, bpQK = psr.next()
            pKt, bpKt = psr.next()
            pKtb = pKt[:].bitcast(BF16)
            for u in range(4):
                B.op("pe", lambda e, u=u: e.matmul(pKK[:, u * 128:(u + 1) * 128], lhsT=tk[:, h0 + u, :], rhs=tk[:, h0 + u, :], start=True, stop=True),
                     reads=[bk], writes=[bpKK], inc=(u == 3))
            for u in range(4):
                B.op("pe", lambda e, u=u: e.matmul(pQK[:, u * 128:(u + 1) * 128], lhsT=tk[:, h0 + u, :], rhs=tq[:, h0 + u, :], start=True, stop=True),
                     reads=[bk, bq], writes=[bpQK], inc=(u == 3))
            for u in range(4):
                B.op("pe", lambda e, u=u: e.transpose(pKtb[:, u * 128:(u + 1) * 128], tk[:, h0 + u, :], self.ident_b[:]),
                     reads=[bk] + cbufs, writes=[bpKt], inc=(u == 3))
            B.op("dve", lambda e: e.tensor_tensor(out=Qm[:], in0=pKK[:, :].rearrange("p (u l) -> p u l", u=4), in1=Ct[:], op=ALU.mult),
                 reads=[bpKK, bCt], writes=[bQm])
            B.op("dve", lambda e: e.tensor_tensor(out=attnT[:], in0=pQK[:, :].rearrange("p (u l) -> p u l", u=4), in1=A[:], op=ALU.mult),
                 reads=[bpQK, bA], writes=[battn])
            B.op("dve", lambda e: e.tensor_tensor(out=Kg[:], in0=pKtb[:, 0:512].rearrange("p (u l) -> p u l", u=4), in1=bc3(eg, 128), op=ALU.mult),
                 reads=[bpKt, bgs], writes=[bKg])
            B.op("dve", lambda e: e.tensor_tensor(out=kt[:], in0=pKtb[:, 0:512].rearrange("p (u l) -> p u l", u=4), in1=bc3(ekt, 128), op=ALU.mult),
                 reads=[bpKt, bgs], writes=[bkt])
            yield
            B.op("pool", lambda e: e.tensor_tensor(out=Qp[:], in0=Qm[:], in1=bcm(self.ident_b[:]), op=ALU.add), reads=[bQm] + cbufs, writes=[bQp])
            yield
            Gc = None
            Hc = None
            for lev in range(7):
                sm = smask[:, d * 7 + lev, :]
                pY, bpY = psr.next()
                for u in range(4):
                    rhsH = self.ident_b[:] if Hc is None else Hc[0][:, u, :]
                    B.op("pe", lambda e, u=u, rhsH=rhsH: e.matmul(pY[:, u * 128:(u + 1) * 128], lhsT=Qp[:, u, :], rhs=rhsH, start=True, stop=True),
                         reads=[bQp] + cbufs + ([] if Hc is None else [Hc[1]]), writes=[bpY], inc=(u == 3))
                B.op("dve", lambda e, sm=sm: e.tensor_tensor(out=IYT[:], in0=pY[:, :].rearrange("p (u l) -> p u l", u=4), in1=bcm(sm), op=ALU.mult),
                     reads=[bpY] + cbufs, writes=[bIYT])
                yield
                Gn = sl["G%d" % (lev % 2)]
                Hn = sl["H%d" % (lev % 2)]
                pG, bpG = psr.next()
                for u in range(4):
                    rhsG = self.ident_b[:] if Gc is None else Gc[0][:, u, :]
                    B.op("pe", lambda e, u=u, rhsG=rhsG: e.matmul(pG[:, u * 128:(u + 1) * 128], lhsT=IYT[:, u, :], rhs=rhsG, start=True, stop=True),
                         reads=[bIYT] + cbufs + ([] if Gc is None else [Gc[1]]), writes=[bpG], inc=(u == 3))
                B.op("act", lambda e, Gn=Gn: e.activation(out=Gn[0][:].rearrange("p u l -> p (u l)"), in_=pG[:, :], func=AF.Identity), reads=[bpG], writes=[Gn[1]])
                if lev < 6:
                    pH, bpH = psr.next()
                    for u in range(4):
                        lhsG = self.ident_b[:] if Gc is None else Gc[0][:, u, :]
                        B.op("pe", lambda e, u=u, lhsG=lhsG: e.matmul(pH[:, u * 128:(u + 1) * 128], lhsT=lhsG, rhs=IYT[:, u, :], start=True, stop=True),
                             reads=[bIYT] + cbufs + ([] if Gc is None else [Gc[1]]), writes=[bpH], inc=(u == 3))
                    B.op("act", lambda e, Hn=Hn: e.activation(out=Hn[0][:].rearrange("p u l -> p (u l)"), in_=pH[:, :], func=AF.Identity), reads=[bpH], writes=[Hn[1]])
                    Hc = Hn
                Gc = Gn
                yield
            G, bG = Gc
            pW, bpW = psr.next()
            for u in range(4):
                B.op("pe", lambda e, u=u: e.matmul(pW[:, u * 128:(u + 1) * 128], lhsT=Kg[:, u, :], rhs=G[:, u, :], start=True, stop=True),
                     reads=[bKg, bG], writes=[bpW], inc=(u == 3))
            B.op("act", lambda e: e.activation(out=negW[:].rearrange("p u l -> p (u l)"), in_=pW[:, :], func=AF.Identity, scale=-1.0), reads=[bpW], writes=[bnegW])
            yield
            while step > 0 and ("gdn", step - 1, d, hg) not in done and ("gdn", step - 1, d, hg) in started:
                yield
            sbt, bsb = Sb_cur[d][hg]
            pV, bpV = psr.next()
            for u in range(4):
                B.op("pe", lambda e, u=u: e.matmul(pV[:, u * 128:(u + 1) * 128], lhsT=G[:, u, :], rhs=tv[:, (h0 + u) * 128:(h0 + u + 1) * 128], start=True, stop=False),
                     reads=[bG, bv], writes=[bpV], inc=False)
                B.op("pe", lambda e, u=u: e.matmul(pV[:, u * 128:(u + 1) * 128], lhsT=negW[:, u, :], rhs=sbt[:, u, :], start=False, stop=True),
                     reads=[bnegW, bsb], writes=[bpV], inc=(u == 3))
            B.op("dve", lambda e: e.tensor_tensor(out=vnew[:], in0=pV[:, :].rearrange("p (u l) -> p u l", u=4), in1=bc3(bcol, 128), op=ALU.mult),
                 reads=[bpV, bg], writes=[bvnew])
            Sg = S[d][:, h0:h0 + 4, :]
            B.op("pool", lambda e: e.tensor_tensor(out=St[:], in0=Sg, in1=bc3(gte, 128), op=ALU.mult), reads=[bS[d][hg], bgs], writes=[bSt])
            yield
            if need_o:
                pO1, bpO1 = psr.next()
                for u in range(4):
                    B.op("pe", lambda e, u=u: e.matmul(pO1[:, u * 128:(u + 1) * 128], lhsT=tq[:, h0 + u, :], rhs=sbt[:, u, :], start=True, stop=True),
                         reads=[bq, bsb], writes=[bpO1], inc=(u == 3))
                o1, bo1 = ring_o1.next()
                B.op("dve", lambda e: e.tensor_tensor(out=o1[:], in0=pO1[:, :].rearrange("p (u l) -> p u l", u=4), in1=bc3(eg, 128), op=ALU.mult),
                     reads=[bpO1, bgs], writes=[bo1])
                pO2, bpO2 = psr.next()
                for u in range(4):
                    B.op("pe", lambda e, u=u: e.matmul(pO2[:, u * 128:(u + 1) * 128], lhsT=attnT[:, u, :], rhs=vnew[:, u, :], start=True, stop=True),
                         reads=[battn, bvnew], writes=[bpO2], inc=(u == 3))
                o, bo = ring_o.next()
                B.op("dve", lambda e: e.tensor_tensor(out=o[:], in0=pO2[:, :].rearrange("p (u l) -> p u l", u=4), in1=o1[:], op=ALU.add),
                     reads=[bpO2, bo1], writes=[bo])
                dst = (self.OF if d == 0 else self.OB)[c - 2]
                B.dma("sp", dst[:, hg * 512:(hg + 1) * 512], o[:].rearrange("p u l -> p (u l)"), bo, reads=[bo])
            pS, bpS = psr.next()
            for u in range(4):
                B.op("pe", lambda e, u=u: e.matmul(pS[:, u * 128:(u + 1) * 128], lhsT=kt[:, u, :], rhs=vnew[:, u, :], start=True, stop=True),
                     reads=[bkt, bvnew], writes=[bpS], inc=(u == 3))
            B.op("dve", lambda e: e.tensor_tensor(out=Sg, in0=pS[:, :].rearrange("p (u l) -> p u l", u=4), in1=St[:], op=ALU.add),
                 reads=[bpS, bSt], writes=[bS[d][hg]])
            yield
            nsb, bnsb = Sb[d][hg].next()
            B.op("act", lambda e: e.activation(out=nsb[:], in_=Sg, func=AF.Identity), reads=[bS[d][hg]], writes=[bnsb])
            Sb_cur[d][hg] = (nsb, bnsb)
            yield

        self._b_env = dict(data=data, dirc=dirc, cbufs=cbufs, psr=psr, order=order, out_lo=out_lo, out_hi=out_hi, bc3=bc3, bcm=bcm,
                           C=C, bC=bC, Cb=Cb, Cb_cur=Cb_cur, nst=nst, nbf=nbf, n_cur=n_cur, nb_cur=nb_cur, mst=mst, m_cur=m_cur,
                           ring_num=ring_num, ring_h=ring_h)
        ml_group = self.make_ml_group()

        bsnap_in = Buf("snap_in")
        bsnap_out = Buf("snap_out")
        smallt = B.sb(st, "b_snap_small", [128, 16], F32)
        bsmallt = Buf("b_snap_small")

        def do_snapshot():
            si, so = self.SNAP_IN.ap(), self.SNAP_OUT.ap()
            nt_, bnt_ = n_cur[0]
            mt_, bmt_ = m_cur[0]
            B.dma("sp", si[:, 0:1024], S[0][:].rearrange("p h e -> p (h e)"), bsnap_in, reads=[bS[0][0], bS[0][1]], writes=[bsnap_in])
            B.dma("sp", si[:, 1024:2048], C[0][:].rearrange("p h e -> p (h e)"), bsnap_in, reads=[bC[0]], writes=[bsnap_in])
            B.dma("sp", si[:, 2048:2052], nt_[:, 0:4], bsnap_in, reads=[bnt_], writes=[bsnap_in])
            B.dma("sp", si[:, 2052:2056], mt_[:, 0:4], bsnap_in, reads=[bmt_], writes=[bsnap_in])
            B.cc_allreduce(self.SNAP_IN.ap().opt(), self.SNAP_OUT.ap().opt(), [[0, 1], [2, 3], [4, 5], [6, 7]], bsnap_out,
                           reads=[bsnap_in], writes=[bsnap_out])
            tA, btA = ring_num.next()
            tB, btB = ring_h.next()
            B.dma("sp", S[1][:].rearrange("p h e -> p (h e)"), so[:, 0:1024], bS[1][0], reads=[bsnap_out], writes=[bS[1][0], bS[1][1]])
            B.dma("sp", tA[:].rearrange("p h e -> p (h e)"), si[:, 0:1024], btA, reads=[bsnap_in], writes=[btA])
            B.op("dve", lambda e: e.tensor_tensor(out=S[1][:].rearrange("p h e -> p (h e)"), in0=S[1][:].rearrange("p h e -> p (h e)"),
                                                  in1=tA[:].rearrange("p h e -> p (h e)"), op=ALU.subtract), reads=[bS[1][0], bS[1][1], btA], writes=[bS[1][0], bS[1][1]])
            for g in range(2):
                t, b = Sb[1][g].next()
                B.op("act", lambda e, t=t, g=g: e.activation(out=t[:], in_=S[1][:, g * 4:g * 4 + 4, :], func=AF.Identity), reads=[bS[1][g]], writes=[b])
                Sb_cur[1][g] = (t, b)
            B.dma("sp", C[1][:].rearrange("p h e -> p (h e)"), so[:, 1024:2048], bC[1], reads=[bsnap_out], writes=[bC[1]])
            B.dma("sp", tB[:].rearrange("p h e -> p (h e)"), si[:, 1024:2048], btB, reads=[bsnap_in], writes=[btB])
            B.op("dve", lambda e: e.tensor_tensor(out=C[1][:].rearrange("p h e -> p (h e)"), in0=C[1][:].rearrange("p h e -> p (h e)"),
                                                  in1=tB[:].rearrange("p h e -> p (h e)"), op=ALU.subtract), reads=[bC[1], btB], writes=[bC[1]])
            t, b = Cb[1].next()
            B.op("act", lambda e, t=t: e.activation(out=t[:], in_=C[1][:], func=AF.Identity), reads=[bC[1]], writes=[b])
            Cb_cur[1] = (t, b)
            B.dma("sp", smallt[:, 0:8], so[:, 2048:2056], bsmallt, reads=[bsnap_out], writes=[bsmallt])
            nn, bnn = nst[1].next()
            mm_, bmm = mst[1].next()
            B.op("dve", lambda e: e.tensor_tensor(out=nn[:, 0:4], in0=smallt[:, 0:4], in1=nt_[:, 0:4], op=ALU.subtract), reads=[bsmallt, bnt_], writes=[bnn])
            B.op("dve", lambda e: e.tensor_tensor(out=mm_[:, 0:4], in0=smallt[:, 4:8], in1=mt_[:, 0:4], op=ALU.subtract), reads=[bsmallt, bmt_], writes=[bmm])
            nb_, bnb_ = nbf[1].next()
            B.op("act", lambda e: e.activation(out=nb_[:], in_=nn[:, 0:4], func=AF.Identity), reads=[bnn], writes=[bnb_])
            n_cur[1] = (nn, bnn)
            nb_cur[1] = (nb_, bnb_)
            m_cur[1] = (mm_, bmm)

        from collections import deque
        pending = deque()
        def items(step, d):
            r = [("load", step, d)]
            if not dbg.get("b_no_gdn"):
                r += [("gdn", step, d, 0), ("gdn", step, d, 1)]
            if not dbg.get("b_no_ml"):
                r.append(("ml", step, d))
            return r

        SNAP_STEP = 32
        nf, nb = len(order[0]), len(order[1])
        for step in range(SNAP_STEP + 1):
            pending.extend(items(step, 0))
        pending.append(("snap",))
        for i in range(max(nf - SNAP_STEP - 1, nb)):
            if SNAP_STEP + 1 + i < nf:
                pending.extend(items(SNAP_STEP + 1 + i, 0))
            if i < nb:
                pending.extend(items(i, 1))
        free_g = list(range(NG))
        free_m = list(range(NM))
        done = set()
        started = set()
        active = []
        rounds = 0
        GAP = dbg.get("b_gap", 5)
        last_gstart = [-GAP]
        loaded = set()
        while pending or active:
            while pending:
                it = pending[0]
                if it[0] == "snap":
                    if any((k[2] == 0 and k not in done) for k in started):
                        break
                    do_snapshot()
                    pending.popleft()
                    continue
                if it[0] == "load":
                    _, step, d = it
                    if any((k[1] == step - 2 and k[2] == d and k not in done) for k in started):
                        break
                    load_step(step, d)
                    shared_pre(step, d)
                    pending.popleft()
                    continue
                if it[0] == "gdn":
                    _, step, d, hg = it
                    if not free_g or rounds - last_gstart[0] < GAP:
                        break
                    last_gstart[0] = rounds
                    si = free_g.pop(0)
                    started.add(it)
                    active.append((it, gdn_group(step, d, hg, gslots[si]), ("g", si)))
                    pending.popleft()
                    continue
                if it[0] == "ml":
                    _, step, d = it
                    key_prev = ("ml", step - 1, d)
                    if (step > 0 and key_prev not in done) or not free_m:
                        break
                    si = free_m.pop(0)
                    started.add(it)
                    active.append((it, ml_group(step, d, mslots[si]), ("m", si)))
                    pending.popleft()
                    continue
            rounds += 1
            for ent in list(active):
                it, gen, (kind, si) = ent
                try:
                    next(gen)
                except StopIteration:
                    active.remove(ent)
                    done.add(it)
                    (free_g if kind == "g" else free_m).append(si)
        B.barrier()
        st.close()

    def make_ml_group(self):
        B = self.B
        env = self._b_env
        data, dirc, cbufs, psr = env["data"], env["dirc"], env["cbufs"], env["psr"]
        bc3, bcm = env["bc3"], env["bcm"]
        C, bC, Cb, Cb_cur = env["C"], env["bC"], env["Cb"], env["Cb_cur"]
        nst, nbf, n_cur, nb_cur, mst, m_cur = env["nst"], env["nbf"], env["n_cur"], env["nb_cur"], env["mst"], env["m_cur"]
        ring_num, ring_h = env["ring_num"], env["ring_h"]
        out_lo, out_hi = env["out_lo"], env["out_hi"]

        def bc3n(ap2, n):
            return ap2.unsqueeze(2).to_broadcast([128, ap2.shape[1], n])

        def ml_group(step, d, sl):
            dd = data[(step, d)]
            dc = dirc[d]
            c = dd["c"]
            need_o = out_lo <= c < out_hi
            tg, bg = dd["GT"]; gs, bgs = dd["gs"]
            mq, bmq = dd["MQ"]; mk, bmk = dd["MK"]; mv, bmv = dd["MV"]
            X, bX = sl["X"]; Y, bY = sl["Y"]; Pm, bPm = sl["Pm"]; PT, bPT = sl["PT"]; Kw, bKw = sl["Kw"]
            sm, bsm = sl["sm"]; Ct, bCt = sl["Ct"]
            bcc = gs[:, 8:12]
            blast = gs[:, 20:24]
            cvec = gs[:, 48:52]
            mprev, bmprev = m_cur[d]
            B.op("pool", lambda e: e.tensor_tensor(out=X[:], in0=bcm(self.ident_f[:]), in1=bc3(cvec, 128), op=ALU.mult), reads=[bgs] + cbufs, writes=[bX])
            B.op("dve", lambda e: e.tensor_tensor(out=sm[:, 4:8], in0=bcc, in1=mprev[:, 0:4], op=ALU.add), reads=[bgs, bmprev], writes=[bsm])
            yield
            pC, bpC = psr.next()
            for u in range(4):
                B.op("pe", lambda e, u=u: e.matmul(pC[:, u * 128:(u + 1) * 128], lhsT=self.ones_f[:], rhs=X[:, u, :], start=True, stop=True),
                     reads=[bX] + cbufs, writes=[bpC], inc=(u == 3))
            B.op("dve", lambda e: e.tensor_tensor(out=Y[:], in0=pC[:, :].rearrange("p (u l) -> p u l", u=4), in1=bc3(bcc, 128), op=ALU.add),
                 reads=[bpC, bgs], writes=[bY])
            yield
            B.op("pool", lambda e: e.tensor_tensor(out=Y[:], in0=Y[:], in1=bcm(dc["MB"][:]), op=ALU.add), reads=[bY] + cbufs, writes=[bY])
            yield
            B.op("dve", lambda e: e.tensor_reduce(out=sm[:, 0:4], in_=Y[:], axis=AX.X, op=ALU.max), reads=[bY], writes=[bsm])
            B.op("dve", lambda e: e.tensor_tensor(out=sm[:, 8:12], in0=sm[:, 0:4], in1=sm[:, 4:8], op=ALU.max), reads=[bsm], writes=[bsm])
            B.op("dve", lambda e: e.tensor_scalar(out=sm[:, 12:16], in0=sm[:, 8:12], scalar1=-1.0, scalar2=None, op0=ALU.mult), reads=[bsm], writes=[bsm])
            B.op("dve", lambda e: e.tensor_tensor(out=sm[:, 16:20], in0=sm[:, 4:8], in1=sm[:, 8:12], op=ALU.subtract), reads=[bsm], writes=[bsm])
            yield
            if need_o:
                for u in range(4):
                    B.op("act", lambda e, u=u: e.activation(out=X[:, u, :], in_=Y[:, u, :], func=AF.Exp, bias=sm[:, 12 + u:13 + u]), reads=[bY, bsm], writes=[bX])
                B.op("act", lambda e: e.activation(out=sm[:, 16:20], in_=sm[:, 16:20], func=AF.Exp), reads=[bsm], writes=[bsm])
                B.op("act", lambda e: e.activation(out=sm[:, 20:24], in_=sm[:, 12:16], func=AF.Exp), reads=[bsm], writes=[bsm])
                yield
                pQK, bpQK = psr.next()
                for u in range(4):
                    B.op("pe", lambda e, u=u: e.matmul(pQK[:, u * 128:(u + 1) * 128], lhsT=mq[:, u, :], rhs=mk[:, u, :], start=True, stop=True),
                         reads=[bmq, bmk], writes=[bpQK], inc=(u == 3))
                B.op("dve", lambda e: e.tensor_tensor(out=Pm[:], in0=pQK[:, :].rearrange("p (u l) -> p u l", u=4), in1=X[:], op=ALU.mult),
                     reads=[bpQK, bX], writes=[bPm])
                yield
                pT, bpT = psr.next()
                pTb = pT[:].bitcast(BF16)
                for u in range(4):
                    B.op("pe", lambda e, u=u: e.transpose(pTb[:, u * 128:(u + 1) * 128], Pm[:, u, :], self.ident_b[:]), reads=[bPm] + cbufs, writes=[bpT], inc=(u == 3))
                B.op("act", lambda e: e.activation(out=PT[:].rearrange("p u l -> p (u l)"), in_=pTb[:, 0:512], func=AF.Identity), reads=[bpT], writes=[bPT])
                yield
                cbt, bcb = Cb_cur[d]
                nbt, bnb = nb_cur[d]
                pDn, bpDn = psr.next()
                for u in range(4):
                    B.op("pe", lambda e, u=u: e.matmul(pDn[:, u:u + 1], lhsT=mq[:, u, :], rhs=nbt[:, u:u + 1], start=True, stop=True), reads=[bmq, bnb], writes=[bpDn], inc=False)
                for u in range(4):
                    B.op("pe", lambda e, u=u: e.matmul(pDn[:, 4 + u:5 + u], lhsT=PT[:, u, :], rhs=self.ones_b[:, 0:1], start=True, stop=True),
                         reads=[bPT] + cbufs, writes=[bpDn], inc=(u == 3))
                B.op("dve", lambda e: e.tensor_tensor(out=sm[:, 24:28], in0=pDn[:, 0:4], in1=sm[:, 16:20], op=ALU.mult), reads=[bpDn, bsm], writes=[bsm])
                B.op("dve", lambda e: e.tensor_tensor(out=sm[:, 24:28], in0=pDn[:, 4:8], in1=sm[:, 24:28], op=ALU.add), reads=[bpDn, bsm], writes=[bsm])
                B.op("dve", lambda e: e.tensor_tensor(out=sm[:, 24:28], in0=sm[:, 24:28], in1=sm[:, 24:28], op=ALU.mult), reads=[bsm], writes=[bsm])
                B.op("dve", lambda e: e.tensor_tensor(out=sm[:, 28:32], in0=sm[:, 20:24], in1=sm[:, 20:24], op=ALU.mult), reads=[bsm], writes=[bsm])
                B.op("dve", lambda e: e.tensor_tensor(out=sm[:, 24:28], in0=sm[:, 24:28], in1=sm[:, 28:32], op=ALU.max), reads=[bsm], writes=[bsm])
                yield
                B.op("pool", lambda e: e.tensor_tensor(out=sm[:, 28:32], in0=sm[:, 24:28], in1=self.nhalf[:, 0:4], op=ALU.pow), reads=[bsm] + cbufs, writes=[bsm])
                yield
                num, bnum = ring_num.next()
                hh, bhh = ring_h.next()
                for pr in range(2):
                    pN1, bpN1 = psr.next()
                    pN2, bpN2 = psr.next()
                    for uu in range(2):
                        u = pr * 2 + uu
                        B.op("pe", lambda e, u=u, uu=uu, pN1=pN1: e.matmul(pN1[:, uu * 256:(uu + 1) * 256], lhsT=mq[:, u, :], rhs=cbt[:, u, :], start=True, stop=True),
                             reads=[bmq, bcb], writes=[bpN1], inc=(uu == 1))
                    for uu in range(2):
                        u = pr * 2 + uu
                        B.op("pe", lambda e, u=u, uu=uu, pN2=pN2: e.matmul(pN2[:, uu * 256:(uu + 1) * 256], lhsT=PT[:, u, :], rhs=mv[:, u * 256:(u + 1) * 256], start=True, stop=True),
                             reads=[bPT, bmv], writes=[bpN2], inc=(uu == 1))
                    B.op("dve", lambda e, pr=pr, pN1=pN1: e.tensor_tensor(out=num[:, pr * 2:pr * 2 + 2, :], in0=pN1[:, :].rearrange("p (u l) -> p u l", u=2),
                                                                         in1=bc3n(sm[:, 16 + pr * 2:18 + pr * 2], 256), op=ALU.mult), reads=[bpN1, bsm], writes=[bnum])
                    B.op("dve", lambda e, pr=pr, pN2=pN2: e.tensor_tensor(out=num[:, pr * 2:pr * 2 + 2, :], in0=pN2[:, :].rearrange("p (u l) -> p u l", u=2),
                                                                         in1=num[:, pr * 2:pr * 2 + 2, :], op=ALU.add), reads=[bpN2, bnum], writes=[bnum])
                B.op("dve", lambda e: e.tensor_tensor(out=hh[:], in0=num[:], in1=bc3n(sm[:, 28:32], 256), op=ALU.mult), reads=[bnum, bsm], writes=[bhh])
                dst = (self.HF if d == 0 else self.HB)[c - 2]
                B.dma("sp", dst[:, :], hh[:].rearrange("p u l -> p (u l)"), bhh, reads=[bhh])
                yield
            pSel, bpSel = psr.next()
            B.op("pe", lambda e: e.matmul(pSel[:, 0:4], lhsT=dc["SEL"][:], rhs=sm[:, 8:12], start=True, stop=True), reads=[bsm] + cbufs, writes=[bpSel])
            mnew, bmnew = mst[d].next()
            B.op("act", lambda e: e.activation(out=mnew[:], in_=pSel[:, 0:4], func=AF.Identity), reads=[bpSel], writes=[bmnew])
            yield
            B.op("dve", lambda e: e.tensor_tensor(out=sm[:, 32:36], in0=cvec, in1=blast, op=ALU.add), reads=[bgs], writes=[bsm])
            B.op("dve", lambda e: e.tensor_tensor(out=sm[:, 32:36], in0=sm[:, 32:36], in1=mnew[:], op=ALU.subtract), reads=[bsm, bmnew], writes=[bsm])
            B.op("dve", lambda e: e.tensor_tensor(out=sm[:, 36:40], in0=blast, in1=mprev[:, 0:4], op=ALU.add), reads=[bgs, bmprev], writes=[bsm])
            B.op("dve", lambda e: e.tensor_tensor(out=sm[:, 36:40], in0=sm[:, 36:40], in1=mnew[:], op=ALU.subtract), reads=[bsm, bmnew], writes=[bsm])
            yield
            B.op("act", lambda e: e.activation(out=sm[:, 32:40], in_=sm[:, 32:40], func=AF.Exp), reads=[bsm], writes=[bsm])
            yield
            pKt, bpKt = psr.next()
            pKtb = pKt[:].bitcast(BF16)
            for u in range(4):
                B.op("pe", lambda e, u=u: e.transpose(pKtb[:, u * 128:(u + 1) * 128], mk[:, u, :], self.ident_b[:]), reads=[bmk] + cbufs, writes=[bpKt], inc=(u == 3))
            B.op("dve", lambda e: e.tensor_tensor(out=Kw[:], in0=pKtb[:, 0:512].rearrange("p (u l) -> p u l", u=4), in1=bc3(sm[:, 32:36], 128), op=ALU.mult),
                 reads=[bpKt, bsm], writes=[bKw])
            m_cur[d] = (mnew, bmnew)
            for pr in range(2):
                B.op("pool", lambda e, pr=pr: e.tensor_tensor(out=Ct[:, pr * 2:pr * 2 + 2, :], in0=C[d][:, pr * 2:pr * 2 + 2, :], in1=bc3n(sm[:, 36 + pr * 2:38 + pr * 2], 256), op=ALU.mult),
                     reads=[bC[d], bsm], writes=[bCt])
            yield
            for pr in range(2):
                pC2, bpC2 = psr.next()
                for uu in range(2):
                    u = pr * 2 + uu
                    B.op("pe", lambda e, u=u, uu=uu, pC2=pC2: e.matmul(pC2[:, uu * 256:(uu + 1) * 256], lhsT=Kw[:, u, :], rhs=mv[:, u * 256:(u + 1) * 256], start=True, stop=True),
                         reads=[bKw, bmv], writes=[bpC2], inc=(uu == 1))
                B.op("dve", lambda e, pr=pr, pC2=pC2: e.tensor_tensor(out=C[d][:, pr * 2:pr * 2 + 2, :], in0=pC2[:, :].rearrange("p (u l) -> p u l", u=2), in1=Ct[:, pr * 2:pr * 2 + 2, :], op=ALU.add),
                     reads=[bpC2, bCt], writes=[bC[d]])
            pN, bpN = psr.next()
            for u in range(4):
                B.op("pe", lambda e, u=u: e.matmul(pN[:, u:u + 1], lhsT=Kw[:, u, :], rhs=self.ones_b[:, 0:1], start=True, stop=True), reads=[bKw] + cbufs, writes=[bpN], inc=(u == 3))
            nold, bnold = n_cur[d]
            nnew, bnnew = nst[d].next()
            B.op("dve", lambda e: e.tensor_tensor(out=nnew[:, 4:8], in0=nold[:, 0:4], in1=sm[:, 36:40], op=ALU.mult), reads=[bnold, bsm], writes=[bnnew])
            B.op("dve", lambda e: e.tensor_tensor(out=nnew[:, 0:4], in0=pN[:, 0:4], in1=nnew[:, 4:8], op=ALU.add), reads=[bpN, bnnew], writes=[bnnew])
            yield
            nbn, bnbn = nbf[d].next()
            B.op("act", lambda e: e.activation(out=nbn[:], in_=nnew[:, 0:4], func=AF.Identity), reads=[bnnew], writes=[bnbn])
            cbn, bcbn = Cb[d].next()
            B.op("act", lambda e: e.activation(out=cbn[:], in_=C[d][:], func=AF.Identity), reads=[bC[d]], writes=[bcbn])
            n_cur[d] = (nnew, bnnew)
            nb_cur[d] = (nbn, bnbn)
            Cb_cur[d] = (cbn, bcbn)
            yield

        return ml_group

    def phaseC1(self):
        B, nc, inp = self.B, self.nc, self.inp
        st = ExitStack()
        dbg = self.debug
        wo, bwo = self.load_w_bf16(st, "c_wo", inp["w_o"], 8, 4096, 8)
        wbg, bwbg = self.load_w_bf16(st, "c_wbg", inp["w_bg"], 8, 1024, 2)
        wbm, bwbm = self.load_w_bf16(st, "c_wbm", inp["w_bm"], 8, 1024, 2)
        wout, bwout = self.load_w_bf16(st, "c_wout", inp["w_out"], 8, 1024, 2)
        nwb = B.sb(st, "c_nwb", [128, 2, 1024], F32)
        bnwb = Buf("c_nwb")
        B.dma("sp", nwb[:, 0, :], inp["gnw_bc"][:, :], bnwb, writes=[bnwb])
        B.dma("sp", nwb[:, 1, :], inp["mnw_bc"][:, :], bnwb, writes=[bnwb])
        bX1 = Buf("X1")
        NS = 2
        xt = B.sb(st, "c_x", [128, NS, 1024], F32)
        bxts = [Buf("c_x%d" % i) for i in range(NS)]
        B.op("pool", lambda e: e.memset(xt[0:64, 0, :], 0.0), writes=[bxts[0]])
        B.dma("sp", self.X1[0:64, :], xt[0:64, 0, :], bxts[0], reads=[bxts[0]], writes=[bX1])
        xn = B.sb(st, "c_xn", [128, NS, 1024], BF16); bxn = Buf("c_xn")
        sq = B.sb(st, "c_sq", [128, 24], F32); bsq = Buf("c_sq")
        junk = None; bjunk = None
        hxT = B.sb(st, "c_hxT", [128, 8, NS * 128], BF16); bhx = Buf("c_hxT")
        oar = Ring(B, st, "c_oa", 2, [128, 1024], F32)
        obr = Ring(B, st, "c_ob", 2, [128, 1024], F32)
        gtr = Ring(B, st, "c_gt", 2, [128, 1024], BF16)
        osqr = Ring(B, st, "c_osq", 2, [128, 1024], BF16)
        smr = Ring(B, st, "c_sm", 2, [128, 32], F32)
        ogr = Ring(B, st, "c_og", 2, [128, 1024], BF16)
        brT = [B.sb(st, "c_brT%d" % i, [128, 8, NS * 128], BF16) for i in range(2)]
        bbrT = [Buf("c_brT%d" % i) for i in range(2)]
        sg = Ring(B, st, "c_sg", 6, [128, NS * 128], F32)
        mT = B.sb(st, "c_mT", [128, 8, NS * 128], BF16); bmT = Buf("c_mT")
        tmp = Ring(B, st, "c_tmp", 2, [128, 512], F32)
        ptr = Ring(B, st, "c_ptr", 2, [128, 512], F32, psum=True)
        pmm = Ring(B, st, "c_pmm", 6, [128, 512], F32, psum=True)
        nt = OWN_T // 128
        sts = []
        i = 0
        while i < nt:
            ns = min(NS, nt - i)
            sts.append((i, ns))
            i += ns
        if dbg.get("c1_tiles"):
            sts = sts[: dbg["c1_tiles"]]
        for (t0, ns) in sts:
            n = ns * 128
            for s in range(ns):
                B.dma("sp", xt[:, s, :], inp["x"][(t0 + s) * 128:(t0 + s + 1) * 128, :], bxts[s], writes=[bxts[s]])
            self.norm_transpose(xt, bxts, ns, xn, bxn, sq, bsq, junk, bjunk, ptr, hxT, bhx, 0, 0)
            def branch_gen(br, s_):
                nh, hd = (8, 128) if br == 0 else (4, 256)
                srcf, srcb = (self.OF, self.OB) if br == 0 else (self.HF, self.HB)
                c = t0 + s_
                oa, boa = oar.next()
                ob, bob = obr.next()
                gt, bgt = gtr.next()
                osq, bosq = osqr.next()
                sm, bsm = smr.next()
                og, bog = ogr.next()
                B.dma("sp", oa[:], srcf[c], boa, writes=[boa])
                B.dma("sp", ob[:], srcb[c], bob, writes=[bob])
                for hf in range(2):
                    p, bp = pmm.next()
                    for k in range(8):
                        B.op("pe", lambda e, k=k, p=p, hf=hf: e.matmul(
                            p[:], lhsT=hxT[:, k, s_ * 128:(s_ + 1) * 128], rhs=wo[:, k, br * 1024 + hf * 512: br * 1024 + (hf + 1) * 512],
                            start=(k == 0), stop=(k == 7)), reads=[bhx, bwo], writes=[bp])
                    B.op("act", lambda e, p=p, hf=hf: e.activation(out=gt[:, hf * 512:(hf + 1) * 512], in_=p[:],
                                                                  func=(AF.Silu if br == 0 else AF.Sigmoid)), reads=[bp], writes=[bgt])
                yield
                B.op("pool", lambda e: e.tensor_tensor(out=gt[:], in0=gt[:], in1=nwb[:, br, :], op=ALU.mult), reads=[bgt, bnwb], writes=[bgt])
                B.op("dve", lambda e: e.tensor_tensor(out=oa[:], in0=oa[:], in1=ob[:], op=ALU.add), reads=[boa, bob], writes=[boa])
                yield
                B.op("act", lambda e: e.activation(out=osq[:], in_=oa[:], func=AF.Square), reads=[boa], writes=[bosq])
                yield
                B.op("dve", lambda e: e.tensor_reduce(out=sm[:, 0:nh], in_=osq[:].rearrange("p (h e) -> p h e", h=nh), axis=AX.X, op=ALU.add),
                     reads=[bosq], writes=[bsm])
                B.op("dve", lambda e: e.tensor_scalar(out=sm[:, 8:8 + nh], in0=sm[:, 0:nh], scalar1=float(1.0 / hd), scalar2=float(EPS),
                                                      op0=ALU.mult, op1=ALU.add), reads=[bsm], writes=[bsm])
                yield
                B.op("pool", lambda e: e.tensor_tensor(out=sm[:, 16:16 + nh], in0=sm[:, 8:8 + nh], in1=self.nhalf[:, 0:nh], op=ALU.pow),
                     reads=[bsm, self.cb], writes=[bsm])
                yield
                B.op("dve", lambda e: e.tensor_tensor(out=osq[:].rearrange("p (h e) -> p h e", h=nh), in0=oa[:].rearrange("p (h e) -> p h e", h=nh),
                                                      in1=sm[:, 16:16 + nh].unsqueeze(2).to_broadcast([128, nh, hd]), op=ALU.mult),
                     reads=[boa, bsm], writes=[bosq])
                B.op("dve", lambda e: e.tensor_tensor(out=og[:], in0=osq[:], in1=gt[:], op=ALU.mult), reads=[bosq, bgt], writes=[bog])
                yield
                p, bp = ptr.next()
                pb = p[:].bitcast(BF16)
                for k in range(8):
                    B.op("pe", lambda e, k=k, pb=pb: e.transpose(pb[:, k * 128:(k + 1) * 128], og[:, k * 128:(k + 1) * 128], self.ident_b[:]),
                         reads=[bog, self.cb], writes=[bp])
                B.op("act", lambda e, pb=pb: e.activation(out=brT[br][:, :, s_ * 128:(s_ + 1) * 128], in_=pb[:, 0:1024].rearrange("p (k t) -> p k t", k=8),
                                                          func=AF.Identity), reads=[bp], writes=[bbrT[br]])

            self.run_pipeline([(lambda br=br, s_=s_: branch_gen(br, s_)) for br in range(2) for s_ in range(ns)], 2)

            def merge_gen(ncn):
                sgs = []
                for gi in range(2):
                    p, bp = pmm.next()
                    for k in range(8):
                        B.op("pe", lambda e, k=k, p=p, gi=gi: e.matmul(p[:, 0:n], lhsT=wo[:, k, 2048 + gi * 1024 + ncn * 128: 2048 + gi * 1024 + (ncn + 1) * 128],
                                                                      rhs=hxT[:, k, 0:n], start=(k == 0), stop=(k == 7)), reads=[bhx, bwo], writes=[bp])
                    g_, bg_ = sg.next()
                    B.op("act", lambda e, p=p, g_=g_: e.activation(out=g_[:, 0:n], in_=p[:, 0:n], func=AF.Sigmoid), reads=[bp], writes=[bg_])
                    sgs.append((g_, bg_))
                yield
                ys = []
                for br, (w, bw) in enumerate(((wbg, bwbg), (wbm, bwbm))):
                    p, bp = pmm.next()
                    for k in range(8):
                        B.op("pe", lambda e, k=k, p=p, w=w, br=br: e.matmul(p[:, 0:n], lhsT=w[:, k, ncn * 128:(ncn + 1) * 128], rhs=brT[br][:, k, 0:n],
                                                                           start=(k == 0), stop=(k == 7)), reads=[bbrT[br], bw], writes=[bp])
                    ys.append((p, bp))
                g0, bg0 = sgs[0]
                g1, bg1 = sgs[1]
                B.op("dve", lambda e: e.tensor_tensor(out=g0[:, 0:n], in0=ys[0][0][:, 0:n], in1=g0[:, 0:n], op=ALU.mult), reads=[ys[0][1], bg0], writes=[bg0])
                B.op("dve", lambda e: e.tensor_tensor(out=g1[:, 0:n], in0=ys[1][0][:, 0:n], in1=g1[:, 0:n], op=ALU.mult), reads=[ys[1][1], bg1], writes=[bg1])
                yield
                B.op("pool", lambda e: e.tensor_tensor(out=mT[:, ncn, 0:n], in0=g0[:, 0:n], in1=g1[:, 0:n], op=ALU.add), reads=[bg0, bg1], writes=[bmT])

            self.run_pipeline([(lambda ncn=ncn: merge_gen(ncn)) for ncn in range(8)], 3)

            def out_gen(s_, hf):
                p, bp = pmm.next()
                t_, bt_ = tmp.next()
                for k in range(8):
                    B.op("pe", lambda e, k=k: e.matmul(p[:], lhsT=mT[:, k, s_ * 128:(s_ + 1) * 128], rhs=wout[:, k, hf * 512:(hf + 1) * 512],
                                                         start=(k == 0), stop=(k == 7)), reads=[bmT, bwout], writes=[bp])
                B.op("dve", lambda e: e.tensor_tensor(out=t_[:], in0=p[:], in1=self.gate_bc[:, 0, hf * 512:(hf + 1) * 512], op=ALU.mult),
                     reads=[bp, self.bgate], writes=[bt_])
                yield
                B.op("pool", lambda e: e.tensor_tensor(out=xt[:, s_, hf * 512:(hf + 1) * 512], in0=xt[:, s_, hf * 512:(hf + 1) * 512], in1=t_[:], op=ALU.add),
                     reads=[bt_, bxts[s_]], writes=[bxts[s_]])
                if hf == 1:
                    B.dma("sp", self.X1[64 + (t0 + s_) * 128: 64 + (t0 + s_ + 1) * 128, :], xt[:, s_, :], bxts[s_], reads=[bxts[s_]], writes=[bX1])

            self.run_pipeline([(lambda s_=s_, hf=hf: out_gen(s_, hf)) for s_ in range(ns) for hf in range(2)], 2)
        B.barrier()
        st.close()

    def precast_wup(self):
        B = self.B
        self.WUPB = B.dram("WUPB", [44, 128, 8, 128], BF16)
        self.bwupb = Buf("WUPB")
        src = self.inp["w_up"].rearrange("(k p) (c j) -> c p k j", p=128, j=128)
        for c in range(44):
            B.dma("pool", self.WUPB[c], src[c], self.bwupb, writes=[self.bwupb])

    def phaseC2(self):
        B, nc, inp = self.B, self.nc, self.inp
        st = ExitStack()
        dbg = self.debug
        wd = B.sb(st, "d_wd", [128, 22, 1024], BF16)
        bwd = Buf("d_wd")
        wdv = inp["w_down"].rearrange("(c p) n -> p c n", p=128)
        for i in range(0, 22, 6):
            j = min(22, i + 6)
            B.dma("pool", wd[:, i:j, :], wdv[:, i:j, :], bwd, writes=[bwd])
        cw = B.sb(st, "d_cw", [128, 44, 9], F32)
        nob = B.sb(st, "d_nob", [128, 1024], F32)
        bsm0 = Buf("d_small")
        B.dma("sp", cw[:], inp["ffn_cw"][:, :, :], bsm0, writes=[bsm0])
        B.dma("sp", nob[:], inp["now_bc"][:, :], bsm0, writes=[bsm0])
        DEPTH = 3
        wup = Ring(B, st, "d_wup", DEPTH + 1, [128, 2, 8, 128], BF16)
        xt = B.sb(st, "d_x", [128, 5, 1024], F32)
        bxts = [Buf("d_x%d" % i) for i in range(5)]
        xn = B.sb(st, "d_xn", [128, 5, 1024], BF16); bxn = Buf("d_xn")
        sq = B.sb(st, "d_sq", [128, 24], F32); bsq = Buf("d_sq")
        junk = B.sb(st, "d_junk", [128, 1024], BF16); bjunk = Buf("d_junk")
        hxT = B.sb(st, "d_hxT", [128, 8, 640], BF16); bhx = Buf("d_hxT")
        upad = Ring(B, st, "d_up", 2 * DEPTH, [128, 10, 66], BF16)
        dgr = Ring(B, st, "d_dg", 2 * DEPTH, [128, 9, 128], BF16)
        sgt = Ring(B, st, "d_sg", DEPTH, [128, 512], F32)
        aT = B.sb(st, "d_aT", [128, 22, 512], BF16); baT = Buf("d_aT")
        xo = B.sb(st, "d_xo", [128, 4, 1024], F32)
        bxo = [Buf("d_xo%d" % i) for i in range(4)]
        t2 = Ring(B, st, "d_t2", 2, [128, 512], F32)
        sq2 = B.sb(st, "d_sq2", [128, 16], F32); bsq2 = Buf("d_sq2")
        pr = Ring(B, st, "d_pr", 8, [128, 512], F32, psum=True)
        for (u_, bu_) in upad.slots:
            B.op("pool", lambda e, u_=u_: e.memset(u_[:], 0.0), writes=[bu_])
        nblk = dbg.get("c2_blocks", 8)
        for j in range(nblk):
            r0 = 512 * j
            for s in range(5):
                B.dma("sp", xt[:, s, :], self.X1[r0 + s * 128: r0 + (s + 1) * 128, :], bxts[s], writes=[bxts[s]])
            for s in range(4):
                B.dma("sp", xo[:, s, :], self.X1[r0 + 64 + s * 128: r0 + 64 + (s + 1) * 128, :], bxo[s], writes=[bxo[s]])
            self.norm_transpose(xt, bxts, 5, xn, bxn, sq, bsq, junk, bjunk, pr, hxT, bhx, 0, 4)

            def pair_gen(c):
                w, bw = wup.next()
                B.dma("sp", w[:, 0], self.WUPB[c], bw, reads=[self.bwupb], writes=[bw])
                B.dma("sp", w[:, 1], self.WUPB[22 + c], bw, reads=[self.bwupb], writes=[bw])
                ups, dgs = [], []
                for part in range(2):
                    ch = c + 22 * part
                    u_, bu_ = upad.next()
                    dg, bdg = dgr.next()
                    ups.append((u_, bu_))
                    dgs.append((dg, bdg))
                    B.op("dve", lambda e, dg=dg, ch=ch: e.tensor_tensor(out=dg[:], in0=self.ident_b[:].unsqueeze(1).to_broadcast([128, 9, 128]),
                                                                      in1=cw[:, ch, :].unsqueeze(2).to_broadcast([128, 9, 128]), op=ALU.mult),
                         reads=[self.cb, bsm0], writes=[bdg])
                yield
                for part in range(2):
                    u_, bu_ = ups[part]
                    p1, bp1 = pr.next()
                    p2, bp2 = pr.next()
                    for k in range(8):
                        B.op("pe", lambda e, k=k, p1=p1, part=part: e.matmul(p1[:], lhsT=w[:, part, k, :], rhs=hxT[:, k, 0:512], start=(k == 0), stop=(k == 7)),
                             reads=[bw, bhx], writes=[bp1], inc=(k == 7))
                    for k in range(8):
                        B.op("pe", lambda e, k=k, p2=p2, part=part: e.matmul(p2[:, 0:128], lhsT=w[:, part, k, :], rhs=hxT[:, k, 512:640], start=(k == 0), stop=(k == 7)),
                             reads=[bw, bhx], writes=[bp2], inc=(k == 7))
                    B.op("act", lambda e, u_=u_, p1=p1: e.activation(out=u_[:, 0:8, 1:65], in_=p1[:].rearrange("p (r c) -> p r c", c=64), func=AF.Identity),
                         reads=[bp1], writes=[bu_])
                    B.op("act", lambda e, u_=u_, p2=p2: e.activation(out=u_[:, 8:10, 1:65], in_=p2[:, 0:128].rearrange("p (r c) -> p r c", c=64), func=AF.Identity),
                         reads=[bp2], writes=[bu_])
                    if j == 0:
                        B.op("pool", lambda e, u_=u_: e.memset(u_[:, 0:1, :], 0.0), writes=[bu_])
                yield
                pcs = []
                for part in range(2):
                    u_, bu_ = ups[part]
                    dg, bdg = dgs[part]
                    pc, bpc = pr.next()
                    t = 0
                    for dr in range(3):
                        for dc_ in range(3):
                            B.op("pe", lambda e, t=t, dr=dr, dc_=dc_, pc=pc, u_=u_, dg=dg: e.matmul(
                                pc[:].rearrange("p (r c) -> p r c", c=64), lhsT=dg[:, t, :], rhs=u_[:, dr:dr + 8, dc_:dc_ + 64], start=(t == 0), stop=(t == 8)),
                                reads=[bu_, bdg], writes=[bpc], inc=(t == 8))
                            t += 1
                    pcs.append((pc, bpc))
                s_, bs_ = sgt.next()
                B.op("act", lambda e: e.activation(out=s_[:], in_=pcs[0][0][:], func=AF.Silu), reads=[pcs[0][1]], writes=[bs_])
                B.op("dve", lambda e: e.tensor_tensor(out=aT[:, c, :], in0=pcs[1][0][:], in1=s_[:], op=ALU.mult), reads=[pcs[1][1], bs_], writes=[baT])

            self.run_pipeline([(lambda c=c: pair_gen(c)) for c in range(22)], DEPTH)
            for s in range(4):
                for hf in range(2):
                    p, bp = pr.next()
                    for c in range(22):
                        B.op("pe", lambda e, c=c, p=p, s=s, hf=hf: e.matmul(p[:], lhsT=aT[:, c, s * 128:(s + 1) * 128], rhs=wd[:, c, hf * 512:(hf + 1) * 512],
                                                                           start=(c == 0), stop=(c == 21)), reads=[baT, bwd], writes=[bp], inc=(c == 21))
                    t_, bt_ = t2.next()
                    B.op("dve", lambda e, p=p, t_=t_, hf=hf: e.tensor_tensor(out=t_[:], in0=p[:], in1=self.gate_bc[:, 1, hf * 512:(hf + 1) * 512], op=ALU.mult),
                         reads=[bp, self.bgate], writes=[bt_])
                    B.op("pool", lambda e, t_=t_, s=s, hf=hf: e.tensor_tensor(out=xo[:, s, hf * 512:(hf + 1) * 512], in0=xo[:, s, hf * 512:(hf + 1) * 512], in1=t_[:], op=ALU.add),
                         reads=[bt_, bxo[s]], writes=[bxo[s]])
                B.op("act", lambda e, s=s: e.activation(out=junk[:], in_=xo[:, s, :], func=AF.Square, accum_out=sq2[:, s:s + 1]), reads=[bxo[s]], writes=[bjunk, bsq2])
                B.op("dve", lambda e, s=s: e.tensor_scalar(out=sq2[:, 4 + s:5 + s], in0=sq2[:, s:s + 1], scalar1=float(D * EPS), scalar2=None, op0=ALU.add), reads=[bsq2], writes=[bsq2])
                B.op("pool", lambda e, s=s: e.tensor_tensor(out=sq2[:, 8 + s:9 + s], in0=sq2[:, 4 + s:5 + s], in1=self.nhalf[:, 0:1], op=ALU.pow), reads=[bsq2, self.cb], writes=[bsq2])
                B.op("dve", lambda e, s=s: e.scalar_tensor_tensor(out=xo[:, s, :], in0=xo[:, s, :], scalar=sq2[:, 8 + s:9 + s], in1=nob[:], op0=ALU.mult, op1=ALU.mult),
                     reads=[bxo[s], bsq2, bsm0], writes=[bxo[s]])
                B.op("act", lambda e, s=s: e.activation(out=xo[:, s, :], in_=xo[:, s, :], func=AF.Identity, scale=32.0), reads=[bxo[s]], writes=[bxo[s]])
                B.dma("sp", self.out[j * 512 + s * 128: j * 512 + (s + 1) * 128, :], xo[:, s, :], bxo[s], reads=[bxo[s]])
        B.barrier()
        st.close()


def _build_once(debug, needed):
    P = Prog(debug=debug, needed=needed)
    P.precast_wup()
    P.phase0()
    P.phaseA()
    P.phaseB()
    P.phaseC1()
    P.phaseC2()
    P.top.close()
    return P.B.finish(), P


def build_program(debug=None):
    _, dry = _build_once(debug, None)
    return _build_once(debug, dry.B.waited)


_CACHE = {}


def kernel(**inputs):
    inp = {k: np.asarray(v) for k, v in inputs.items()}
    if "nc" not in _CACHE:
        _CACHE["nc"] = build_program()[0]
    nc = _CACHE["nc"]
    in_maps = [prep_core(inp, core) for core in range(8)]
    res = run_bass_kernel_spmd(nc, in_maps, core_ids=list(range(8)))
    out = np.empty((4, T, D), np.float32)
    for core in range(8):
        o = np.asarray(res.results[core]["out"], np.float32)
        b = core // 2
        if core % 2 == 0:
            out[b, 0:4096] = o
        else:
            out[b, 4096:8192] = o[::-1]
    return out
```

```python
import numpy as np
from contextlib import ExitStack

import concourse.bass as bass
import concourse.mybir as mybir
from concourse.bass_utils import run_bass_kernel_spmd

F32 = mybir.dt.float32
BF16 = mybir.dt.bfloat16
AF = mybir.ActivationFunctionType
ALU = mybir.AluOpType
AX = mybir.AxisListType

D = 1024
T = 8192
TC = 256
KD = 8
EPS = 1e-6
NEG = -1.0e30


class Buf:
    __slots__ = ("name", "w", "r", "dsem")

    def __init__(self, name):
        self.name = name
        self.w = None
        self.r = {}
        self.dsem = None


class Builder:
    def __init__(self, needed=None):
        self.nc = bass.Bass("TRN2", target_bir_lowering=False)
        nc = self.nc
        self.dry = needed is None
        self.needed = needed
        self.waited = {}
        self.es = ExitStack()
        self.es.enter_context(nc.allow_low_precision("bf16 matmul operands, fp32 accumulation"))
        self.engs = {"pe": nc.tensor, "act": nc.scalar, "dve": nc.vector, "pool": nc.gpsimd, "sp": nc.sync}
        self.sems = {}
        self.cnt = {}
        self.rank = {}
        self.rank_of = {}
        self.seen = {e: {} for e in self.engs}
        for e in self.engs:
            self.sems[e] = self.es.enter_context(nc.semaphore("s_" + e))
            self.cnt[e] = 0
            self.rank[e] = 0
            self.rank_of[e] = {}
            self.waited[e] = set()
        self.ndsem = 0
        self.nins = 0

    def sb(self, stack, name, shape, dt):
        return stack.enter_context(self.nc.sbuf_tensor(name, list(shape), dt))

    def ps(self, stack, name, shape, dt=F32):
        return stack.enter_context(self.nc.psum_tensor(name, list(shape), dt))

    def dram(self, name, shape, dt, kind="Internal"):
        return self.nc.dram_tensor(name, list(shape), dt, kind=kind).ap()

    def new_dsem(self):
        k = "d%d" % self.ndsem
        self.ndsem += 1
        self.sems[k] = self.es.enter_context(self.nc.semaphore(k))
        self.cnt[k] = 0
        return k

    def _deps(self, eng, reads, writes):
        deps = {}

        def add(k, v):
            if deps.get(k, 0) < v:
                deps[k] = v

        for b in reads:
            if b.w is not None:
                add(*b.w)
        for b in writes:
            if b.w is not None and b.w[0] != eng:
                add(*b.w)
            for k, v in b.r.items():
                if k != eng:
                    add(k, v)
        return deps

    def _emit_waits(self, eng, deps):
        e = self.engs[eng]
        seen = self.seen[eng]
        for k, v in deps.items():
            if seen.get(k, 0) >= v:
                continue
            assert v <= self.cnt[k], "wait on %s=%d never reached (issued %d)" % (k, v, self.cnt[k])
            seen[k] = v
            if k in self.engs:
                if self.dry:
                    self.waited[k].add(v)
                else:
                    e.wait_ge(self.sems[k], self.rank_of[k][v])
            elif not self.dry:
                e.wait_ge(self.sems[k], v)

    def op(self, eng, fn, reads=(), writes=(), inc=True):
        self._emit_waits(eng, self._deps(eng, reads, writes))
        self.cnt[eng] += 1
        idx = self.cnt[eng]
        self.nins += 1
        if not self.dry:
            ins = fn(self.engs[eng])
            if idx in self.needed[eng]:
                self.rank[eng] += 1
                self.rank_of[eng][idx] = self.rank[eng]
                ins.then_inc(self.sems[eng], 1)
        tok = (eng, idx)
        for b in reads:
            if b.r.get(eng, 0) < idx:
                b.r[eng] = idx
        for b in writes:
            b.w = tok
            b.r = {}
        return tok

    def dma(self, q, out, in_, sem_buf, reads=(), writes=()):
        self._emit_waits(q, self._deps("__dma__", reads, writes))
        if sem_buf.dsem is None:
            sem_buf.dsem = self.new_dsem()
        k = sem_buf.dsem
        self.cnt[k] += 16
        self.nins += 1
        if not self.dry:
            ins = self.engs[q].dma_start(out=out, in_=in_)
            ins.then_inc(self.sems[k], 16)
        tok = (k, self.cnt[k])
        for b in reads:
            if b.r.get(k, 0) < tok[1]:
                b.r[k] = tok[1]
        for b in writes:
            b.w = tok
            b.r = {}
        return tok

    def cc_allreduce(self, in_ap, out_ap, groups, sem_buf, reads=(), writes=()):
        self._emit_waits("pool", self._deps("__dma__", reads, writes))
        if sem_buf.dsem is None:
            sem_buf.dsem = self.new_dsem()
        k = sem_buf.dsem
        self.cnt[k] += 1
        self.nins += 1
        if not self.dry:
            ins = self.nc.gpsimd.collective_compute("AllReduce", ALU.add, replica_groups=groups, ins=[in_ap], outs=[out_ap])
            ins.then_inc(self.sems[k])
        tok = (k, self.cnt[k])
        for b in reads:
            if b.r.get(k, 0) < tok[1]:
                b.r[k] = tok[1]
        for b in writes:
            b.w = tok
            b.r = {}
        return tok

    def barrier(self):
        for e in self.engs:
            self._emit_waits(e, {k: v for k, v in self.cnt.items() if k != e and v > 0})

    def finish(self):
        self.barrier()
        self.es.close()
        return self.nc


OFF_QKV, OFF_A, OFF_B, OFF_MQ, OFF_MK, OFF_MV, OFF_MI, OFF_MF, OFF_Z, OFF_MO, OFF_GG, OFF_GM, OFF_END = (
    0, 3072, 3088, 3104, 3616, 4128, 5152, 5160, 5168, 6192, 7216, 8240, 9264)
NCH = 66
OWN_T = 4224


def _col(v, n=128):
    v = np.asarray(v, np.float32).reshape(-1, n)
    return np.ascontiguousarray(v.T)


def _rep(v):
    v = np.asarray(v, np.float32).reshape(1, -1)
    return np.ascontiguousarray(np.repeat(v, 128, axis=0))


def _swapdir(a, flip):
    if not flip:
        return a
    h = a.shape[-1] // 2
    return np.concatenate([a[..., h:], a[..., :h]], axis=-1)


def prep_core(inp, core):
    b = core // 2
    flip = core % 2
    f32 = np.float32
    x = inp["x"][b]
    ctx = inp["ctx"][b]
    if flip:
        x = x[::-1]
        ctx = ctx[::-1]
    w_in = inp["w_in"][0]
    m = {}
    m["x"] = np.ascontiguousarray(x, dtype=f32)
    m["ctx"] = np.ascontiguousarray(ctx, dtype=f32)
    m["c_col"] = _col(inp["c"][b])
    m["cc_col"] = _col(inp["c_ctx"])
    m["w_ada"] = np.ascontiguousarray(inp["w_ada"][0], dtype=f32)
    b_ada = inp["b_ada"][0]
    m["b_ada_col"] = _col(b_ada)
    m["b_ada_g"] = np.ascontiguousarray(np.concatenate([_rep(b_ada[2048:3072]), _rep(b_ada[5120:6144])], axis=1))
    m["n1_col"] = _col(inp["norm1_w"][0])
    m["n2_col"] = _col(inp["norm2_w"][0])
    m["w_qkv"] = np.ascontiguousarray(w_in[:, OFF_QKV:OFF_A])
    wg = np.concatenate([_swapdir(w_in[:, OFF_A:OFF_B], flip), _swapdir(w_in[:, OFF_B:OFF_MQ], flip),
                         _swapdir(w_in[:, OFF_MI:OFF_MF], flip), _swapdir(w_in[:, OFF_MF:OFF_Z], flip)], axis=1)
    m["w_gate"] = np.ascontiguousarray(wg)
    m["w_ml"] = np.ascontiguousarray(w_in[:, OFF_MQ:OFF_MI])
    m["w_o"] = np.ascontiguousarray(w_in[:, OFF_Z:OFF_END])
    gp = np.concatenate([_swapdir(inp["gdn_dt_bias"][0].reshape(-1), flip), _swapdir(inp["gdn_a_log"][0].reshape(-1), flip),
                         _swapdir(inp["ml_igate_b"][0].reshape(-1), flip), _swapdir(inp["ml_fgate_b"][0].reshape(-1), flip)])
    m["gate_p"] = _rep(gp)
    gc = inp["gdn_conv"][0]
    if flip:
        gc = gc[::-1]
    m["gdn_cw"] = np.ascontiguousarray(gc.T.reshape(24, 128, 3).transpose(1, 0, 2), dtype=f32)
    fc = inp["ffn_conv"][0]
    if flip:
        fc = fc[::-1, ::-1]
    m["ffn_cw"] = np.ascontiguousarray(fc.reshape(9, 44, 128).transpose(2, 1, 0), dtype=f32)
    m["gnw_bc"] = _rep(np.tile(inp["gdn_norm_w"][0], 8))
    m["mnw_bc"] = _rep(inp["ml_norm_w"][0].reshape(-1))
    m["now_bc"] = _rep(inp["norm_out_w"])
    m["w_bg"] = np.ascontiguousarray(inp["w_branch_gdn"][0], dtype=f32)
    m["w_bm"] = np.ascontiguousarray(inp["w_branch_ml"][0], dtype=f32)
    m["w_out"] = np.ascontiguousarray(inp["w_out"][0], dtype=f32)
    m["w_up"] = np.ascontiguousarray(inp["w_up"][0], dtype=f32)
    m["w_down"] = np.ascontiguousarray(inp["w_down"][0], dtype=f32)
    m["smask"] = make_smask()
    return m


def make_smask():
    idx = np.arange(128)
    i = idx[None, :]
    j = idx[:, None]
    out = np.zeros((128, 14, 128), np.float32)
    for lev in range(7):
        b = 1 << lev
        same = (i // (2 * b)) == (j // (2 * b))
        f = same & ((i % (2 * b)) < b) & ((j % (2 * b)) >= b)
        g = same & ((j % (2 * b)) < b) & ((i % (2 * b)) >= b)
        out[:, lev, :] = np.where(f, -1.0, 0.0) + np.eye(128)
        out[:, 7 + lev, :] = np.where(g, -1.0, 0.0) + np.eye(128)
    return out


IN_SHAPES = {
    "x": [T, D], "ctx": [TC, D], "c_col": [128, 8], "cc_col": [128, 8], "w_ada": [D, 6144],
    "b_ada_col": [128, 48], "b_ada_g": [128, 2048], "n1_col": [128, 8], "n2_col": [128, 8],
    "w_qkv": [D, 3072], "w_gate": [D, 48], "w_ml": [D, 2048], "w_o": [D, 4096], "gate_p": [128, 48],
    "gdn_cw": [128, 24, 3], "ffn_cw": [128, 44, 9], "gnw_bc": [128, 1024], "mnw_bc": [128, 1024],
    "now_bc": [128, 1024], "w_bg": [D, D], "w_bm": [D, D], "w_out": [D, D], "w_up": [D, 5632], "w_down": [2816, D],
    "smask": [128, 14, 128],
}


class Ring:
    def __init__(self, B, stack, name, n, shape, dt, psum=False):
        self.slots = []
        for i in range(n):
            t = (B.ps if psum else B.sb)(stack, "%s%d" % (name, i), shape, dt)
            self.slots.append((t, Buf("%s%d" % (name, i))))
        self.i = 0

    def next(self):
        s = self.slots[self.i % len(self.slots)]
        self.i += 1
        return s


class Prog:
    def __init__(self, debug=None, needed=None):
        self.debug = debug or {}
        self.B = Builder(needed)
        self.nc = self.B.nc
        self.top = ExitStack()
        self.inp = {}
        for k, shp in IN_SHAPES.items():
            self.inp[k] = self.nc.dram_tensor(k, list(shp), F32, kind="ExternalInput").ap()
        self.out = self.nc.dram_tensor("out", [4096, D], F32, kind="ExternalOutput").ap()
        dk = "ExternalOutput" if self.debug.get("scratch_out") else "Internal"
        B = self.B
        self.KT = B.dram("KT", [NCH, 128, 8, 128], BF16, dk)
        self.QT = B.dram("QT", [NCH, 128, 8, 128], BF16, dk)
        self.VG = B.dram("VG", [NCH, 128, 1024], BF16, dk)
        self.MQT = B.dram("MQT", [NCH, 128, 4, 128], BF16, dk)
        self.MKT = B.dram("MKT", [NCH, 128, 4, 128], BF16, dk)
        self.MV = B.dram("MV", [NCH, 128, 1024], BF16, dk)
        self.GT = B.dram("GT", [NCH, 128, 48], F32, dk)
        self.OF = B.dram("OF", [33, 128, 1024], F32, dk)
        self.OB = B.dram("OB", [33, 128, 1024], F32, dk)
        self.HF = B.dram("HF", [33, 128, 1024], F32, dk)
        self.HB = B.dram("HB", [33, 128, 1024], F32, dk)
        self.X1 = B.dram("X1", [64 + OWN_T, D], F32, dk)
        self.SNAP_IN = self.nc.dram_tensor("SNAP_IN", [128, 2064], F32)
        self.SNAP_OUT = self.nc.dram_tensor("SNAP_OUT", [128, 2064], F32)
        self.consts()

    def consts(self):
        B, st = self.B, self.top
        self.ident_f = B.sb(st, "ident_f", [128, 128], F32)
        self.ident_b = B.sb(st, "ident_b", [128, 128], BF16)
        self.ones_f = B.sb(st, "ones_f", [128, 128], F32)
        self.ones_b = B.sb(st, "ones_b", [128, 128], BF16)
        self.nhalf = B.sb(st, "nhalf", [128, 512], F32)
        self.cb = Buf("consts")
        cb = self.cb
        B.op("pool", lambda e: e.memset(self.ones_f[:], 1.0), writes=[cb])
        B.op("pool", lambda e: e.memset(self.ones_b[:], 1.0), writes=[cb])
        B.op("pool", lambda e: e.memset(self.nhalf[:], -0.5), writes=[cb])
        B.op("pool", lambda e: e.memset(self.ident_f[:], 1.0), writes=[cb])
        B.op("pool", lambda e: e.affine_select(self.ident_f[:], self.ident_f[:], pattern=[[-1, 128]], compare_op=ALU.is_equal,
                                               fill=0.0, base=0, channel_multiplier=1), reads=[cb], writes=[cb])
        B.op("dve", lambda e: e.tensor_copy(out=self.ident_b[:], in_=self.ident_f[:]), reads=[cb], writes=[cb])
        self.modc = B.sb(st, "modc", [128, 6, 8], F32)
        self.bmod = Buf("modc")
        self.gate_bc = B.sb(st, "gate_bc", [128, 2, 1024], F32)
        self.bgate = Buf("gate_bc")

    def mask(self, stack, name, cmp_pat, dt=F32, val=1.0, fill=0.0):
        B = self.B
        base, cm, step, cmp = cmp_pat
        t = B.sb(stack, name, [128, 128], dt)
        tf = t
        if dt != F32:
            tf = B.sb(stack, name + "_f", [128, 128], F32)
        b = Buf(name)
        B.op("pool", lambda e: e.memset(tf[:], val), writes=[b])
        B.op("pool", lambda e: e.affine_select(tf[:], tf[:], pattern=[[step, 128]], compare_op=cmp, fill=fill,
                                               base=base, channel_multiplier=cm), reads=[b], writes=[b])
        if dt != F32:
            B.op("dve", lambda e: e.tensor_copy(out=t[:], in_=tf[:]), reads=[b], writes=[b])
        return t, b

    def phase0(self):
        B, nc, inp = self.B, self.nc, self.inp
        st = ExitStack()
        sc = B.sb(st, "p0_sc", [128, 16], F32)
        bsc = Buf("p0_sc")
        scb = B.sb(st, "p0_scb", [128, 8, 128], F32)
        bscb = Buf("p0_scb")
        bcol = B.sb(st, "p0_bcol", [128, 48], F32)
        n12 = B.sb(st, "p0_n12", [128, 16], F32)
        bg = B.sb(st, "p0_bg", [128, 2048], F32)
        bsm = Buf("p0_small")
        B.dma("sp", sc[:, 0:8], inp["c_col"][:, :], bsc, writes=[bsc])
        B.dma("sp", sc[:, 8:16], inp["cc_col"][:, :], bsc, writes=[bsc])
        B.dma("sp", bcol[:], inp["b_ada_col"][:, :], bsm, writes=[bsm])
        B.dma("sp", n12[:, 0:8], inp["n1_col"][:, :], bsm, writes=[bsm])
        B.dma("sp", n12[:, 8:16], inp["n2_col"][:, :], bsm, writes=[bsm])
        B.dma("sp", bg[:], inp["b_ada_g"][:, :], bsm, writes=[bsm])
        B.op("act", lambda e: e.activation(out=sc[:], in_=sc[:], func=AF.Silu), reads=[bsc], writes=[bsc])
        for k in range(8):
            B.op("dve", lambda e, k=k: e.tensor_scalar(out=scb[:, k, :], in0=self.ones_f[:], scalar1=sc[:, k:k + 1], scalar2=None,
                                                       op0=ALU.mult), reads=[bsc, self.cb], writes=[bscb])
        wring = Ring(B, st, "p0_w", 2, [128, 8, 512], F32)
        pcol = B.ps(st, "p0_pcol", [128, 64], F32)
        bpcol = Buf("p0_pcol")
        prow = Ring(B, st, "p0_prow", 2, [128, 512], F32, psum=True)
        wv = inp["w_ada"].rearrange("(k p) n -> p k n", p=128)
        xslot = {0: 0, 1: 1, 3: 2, 4: 3}
        for nb in range(12):
            v, half = nb // 2, nb % 2
            w, bw = wring.next()
            B.dma("sp", w[:], wv[:, :, nb * 512:(nb + 1) * 512], bw, writes=[bw])
            if v in (2, 5):
                p, bp = prow.next()
                for k in range(8):
                    B.op("pe", lambda e, k=k, p=p, w=w: e.matmul(p[:], lhsT=scb[:, k, :], rhs=w[:, k, :], start=(k == 0), stop=(k == 7)),
                         reads=[bscb, bw], writes=[bp], inc=(k == 7))
                gi = 0 if v == 2 else 1
                B.op("dve", lambda e, p=p, gi=gi, half=half: e.tensor_tensor(
                    out=self.gate_bc[:, gi, half * 512:(half + 1) * 512], in0=p[:], in1=bg[:, gi * 1024 + half * 512: gi * 1024 + (half + 1) * 512],
                    op=ALU.add), reads=[bp, bsm], writes=[self.bgate])
            else:
                for cc in range(4):
                    col = xslot[v] * 8 + half * 4 + cc
                    for k in range(8):
                        B.op("pe", lambda e, k=k, w=w, cc=cc, col=col: e.matmul(pcol[:, col:col + 1], lhsT=w[:, k, cc * 128:(cc + 1) * 128],
                                                                                 rhs=sc[:, k:k + 1], start=(k == 0), stop=(k == 7)),
                             reads=[bw, bsc], writes=[bpcol], inc=(k == 7))
                    if v in (0, 1):
                        col2 = 32 + v * 8 + half * 4 + cc
                        for k in range(8):
                            B.op("pe", lambda e, k=k, w=w, cc=cc, col2=col2: e.matmul(pcol[:, col2:col2 + 1], lhsT=w[:, k, cc * 128:(cc + 1) * 128],
                                                                                       rhs=sc[:, 8 + k:9 + k], start=(k == 0), stop=(k == 7)),
                                 reads=[bw, bsc], writes=[bpcol], inc=(k == 7))
        mc = B.sb(st, "p0_mc", [128, 6, 8], F32)
        bmc = Buf("p0_mc")
        for i, v in enumerate((0, 1, 3, 4)):
            B.op("dve", lambda e, i=i, v=v: e.tensor_tensor(out=mc[:, i, :], in0=pcol[:, i * 8:(i + 1) * 8], in1=bcol[:, v * 8:(v + 1) * 8], op=ALU.add),
                 reads=[bpcol, bsm], writes=[bmc])
        for i, v in enumerate((0, 1)):
            B.op("dve", lambda e, i=i, v=v: e.tensor_tensor(out=mc[:, 4 + i, :], in0=pcol[:, 32 + i * 8:32 + (i + 1) * 8], in1=bcol[:, v * 8:(v + 1) * 8],
                                                            op=ALU.add), reads=[bpcol, bsm], writes=[bmc])
        md = self.modc
        for dst, (sci, shi, nw) in {0: (1, 0, 0), 2: (5, 4, 0), 4: (3, 2, 1)}.items():
            B.op("dve", lambda e, dst=dst, sci=sci, nw=nw: e.scalar_tensor_tensor(out=md[:, dst, :], in0=mc[:, sci, :], scalar=1.0, in1=n12[:, nw * 8:(nw + 1) * 8],
                                                                                  op0=ALU.add, op1=ALU.mult), reads=[bmc, bsm], writes=[self.bmod])
            B.op("dve", lambda e, dst=dst, shi=shi: e.tensor_copy(out=md[:, dst + 1, :], in_=mc[:, shi, :]), reads=[bmc], writes=[self.bmod])
        B.barrier()
        st.close()

    @staticmethod
    def run_pipeline(makers, depth):
        active = []
        it = iter(makers)
        exhausted = False
        while True:
            for g in list(active):
                try:
                    next(g)
                except StopIteration:
                    active.remove(g)
            if not exhausted and len(active) < depth:
                try:
                    g = next(it)()
                    try:
                        next(g)
                        active.append(g)
                    except StopIteration:
                        pass
                except StopIteration:
                    exhausted = True
            if exhausted and not active:
                break

    def load_w_bf16(self, stack, name, ap, kchunks, ncols, nsplit=4):
        B = self.B
        t = B.sb(stack, name, [128, kchunks, ncols], BF16)
        b = Buf(name)
        v = ap.rearrange("(k p) n -> p k n", p=128)
        step = (ncols + nsplit - 1) // nsplit
        for i in range(0, ncols, step):
            j = min(ncols, i + step)
            B.dma("pool", t[:, :, i:j], v[:, :, i:j], b, writes=[b])
        return t, b

    def norm_transpose(self, xt, bxts, ns, xn, bxn, sq, bsq, junk, bjunk, ptr_ring, hxT, bhx, col0, ai, npart=128):
        B = self.B
        for s in range(ns):
            B.op("act", lambda e, s=s: e.activation(out=xn[0:npart, s, :], in_=xt[0:npart, s, :], func=AF.Square, accum_out=sq[0:npart, s:s + 1]),
                 reads=[bxts[s]], writes=[bxn, bsq])
        B.op("dve", lambda e: e.tensor_scalar(out=sq[0:npart, 8:8 + ns], in0=sq[0:npart, 0:ns], scalar1=float(D * EPS), scalar2=None, op0=ALU.add),
             reads=[bsq], writes=[bsq])
        B.op("pool", lambda e: e.tensor_tensor(out=sq[0:npart, 16:16 + ns], in0=sq[0:npart, 8:8 + ns], in1=self.nhalf[0:npart, 0:ns], op=ALU.pow),
             reads=[bsq, self.cb], writes=[bsq])
        for s in range(ns):
            B.op("dve", lambda e, s=s: e.tensor_scalar(out=xn[0:npart, s, :], in0=xt[0:npart, s, :], scalar1=sq[0:npart, 16 + s:17 + s], scalar2=32.0,
                                                       op0=ALU.mult, op1=ALU.mult), reads=[bxts[s], bsq], writes=[bxn])
        for k in range(KD):
            p, bp = ptr_ring.next()
            pb = p[:].bitcast(BF16)
            for s in range(ns):
                B.op("pe", lambda e, s=s, k=k, pb=pb: e.transpose(pb[:, s * npart:(s + 1) * npart], xn[0:npart, s, k * 128:(k + 1) * 128],
                                                                  self.ident_b[0:npart, 0:npart]),
                     reads=[bxn, self.cb], writes=[bp], inc=(s == ns - 1))
            B.op("act", lambda e, k=k, pb=pb: e.activation(out=hxT[:, k, col0:col0 + ns * npart], in_=pb[:, 0:ns * npart], func=AF.Identity,
                                                           scale=self.modc[:, ai, k:k + 1], bias=self.modc[:, ai + 1, k:k + 1]),
                 reads=[bp, self.bmod], writes=[bhx])

    def phaseA(self):
        B, nc, inp = self.B, self.nc, self.inp
        st = ExitStack()
        wqkv, bwqkv = self.load_w_bf16(st, "a_wqkv", inp["w_qkv"], 8, 3072, 6)
        wml, bwml = self.load_w_bf16(st, "a_wml", inp["w_ml"], 8, 2048, 4)
        wgt, bwgt = self.load_w_bf16(st, "a_wgt", inp["w_gate"], 8, 48, 1)
        cw = B.sb(st, "a_cw", [128, 24, 3], F32)
        gp = B.sb(st, "a_gp", [128, 48], F32)
        bsm = Buf("a_small")
        B.dma("sp", cw[:], inp["gdn_cw"][:, :, :], bsm, writes=[bsm])
        B.dma("sp", gp[:], inp["gate_p"][:, :], bsm, writes=[bsm])
        B.op("act", lambda e: e.activation(out=gp[:, 16:32], in_=gp[:, 16:32], func=AF.Exp), reads=[bsm], writes=[bsm])
        B.op("dve", lambda e: e.tensor_scalar(out=gp[:, 16:32], in0=gp[:, 16:32], scalar1=-1.0, scalar2=None, op0=ALU.mult), reads=[bsm], writes=[bsm])
        xt = B.sb(st, "a_x", [128, 4, 1024], F32)
        bxts = [Buf("a_x%d" % i) for i in range(4)]
        xh = B.sb(st, "a_xh", [2, 1, 1024], F32); bxh = Buf("a_xh")
        xn = B.sb(st, "a_xn", [128, 4, 1024], BF16); bxn = Buf("a_xn")
        xnh = B.sb(st, "a_xnh", [2, 1, 1024], BF16); bxnh = Buf("a_xnh")
        sq = B.sb(st, "a_sq", [128, 24], F32); bsq = Buf("a_sq")
        sqh = B.sb(st, "a_sqh", [128, 24], F32); bsqh = Buf("a_sqh")
        junk = None; bjunk = None
        hxT = B.sb(st, "a_hxT", [128, 8, 514], BF16); bhx = Buf("a_hxT")
        hxh = B.sb(st, "a_hxh", [128, 8, 2], BF16); bhxh = Buf("a_hxh")
        ptr = Ring(B, st, "a_ptr", 2, [128, 512], F32, psum=True)
        pz = Ring(B, st, "a_pz", 4, [128, 512], F32, psum=True)
        pmisc = B.ps(st, "a_pmisc", [128, 512], F32)
        pzh = pmisc[:, 0:64]; bpzh = Buf("a_pzh")
        pn = ptr
        zb = Ring(B, st, "a_zb", 4, [128, 514], F32)
        y1 = Ring(B, st, "a_y1", 4, [128, 512], F32)
        sqb = Ring(B, st, "a_sqb", 2, [128, 512], BF16)
        skeep = B.sb(st, "a_skeep", [128, 8, 512], BF16)
        bskeep = [Buf("a_skeep%d" % i) for i in range(8)]
        rnr = B.sb(st, "a_rnr", [8, 512], F32); brnr = Buf("a_rnr")
        ind = B.sb(st, "a_ind", [128, 8, 8], BF16); bind = Buf("a_ind")
        selr = B.sb(st, "a_selr", [8, 8, 128], F32); bselr = Buf("a_selr")
        B.op("pool", lambda e: e.memset(ind[:], 0.0), writes=[bind])
        for jj in range(8):
            B.op("pool", lambda e, jj=jj: e.memset(ind[:, jj, jj:jj + 1], 1.0), writes=[bind])
            B.op("dve", lambda e, jj=jj: e.tensor_copy(out=selr[:, jj, :], in_=self.ident_f[0:8, jj:jj + 1].to_broadcast([8, 128])), reads=[self.cb], writes=[bselr])
        pss = B.ps(st, "a_pss", [128, 512], F32); bpss = Buf("a_pss")
        kst = B.sb(st, "a_kst", [128, 4, 8, 128], BF16); bkst = Buf("a_kst")
        qst = B.sb(st, "a_qst", [128, 4, 8, 128], BF16); bqst = Buf("a_qst")
        vT = B.sb(st, "a_vT", [128, 8, 512], BF16); bvT = Buf("a_vT")
        vst = Ring(B, st, "a_vst", 1, [128, 4, 1024], BF16)
        mqst = B.sb(st, "a_mqst", [128, 4, 4, 128], BF16); bmqst = Buf("a_mqst")
        mkst = B.sb(st, "a_mkst", [128, 4, 4, 128], BF16); bmkst = Buf("a_mkst")
        graw = B.sb(st, "a_graw", [128, 4, 48], F32); bgraw = Buf("a_graw")
        gwk = B.sb(st, "a_gwk", [128, 4, 48], F32); bgwk = Buf("a_gwk")
        gsb = Ring(B, st, "a_gsb", 2, [128, 4, 48], F32)
        pg = pmisc[:, 64:256].rearrange("p (s g) -> p s g", g=48); bpg = Buf("a_pg")
        dkr = float(128 ** -0.5)

        tiles = [(inp["ctx"], 0, 2, 0, False, False)]
        for i in range(8):
            tiles.append((inp["x"], i * 512, 4, 2 + 4 * i, i > 0, True))
        tiles.append((inp["x"], 4096, 1, 34, True, True))
        if self.debug.get("a_tiles"):
            tiles = tiles[: self.debug["a_tiles"]]
        for (src, t0, ns, c0, hl, hr) in tiles:
            n = ns * 128
            ai = 2 if src is inp["ctx"] else 0
            for s in range(ns):
                B.dma("sp", xt[:, s, :], src[t0 + s * 128:t0 + (s + 1) * 128, :], bxts[s], writes=[bxts[s]])
            tl = t0 - 1 if hl else t0
            tr = t0 + n if hr else t0
            B.dma("sp", xh[0:1, 0, :], src[tl:tl + 1, :], bxh, writes=[bxh])
            B.dma("sp", xh[1:2, 0, :], src[tr:tr + 1, :], bxh, writes=[bxh])
            self.norm_transpose(xt, bxts, ns, xn, bxn, sq, bsq, junk, bjunk, ptr, hxT, bhx, 1, ai)
            self.norm_transpose(xh, [bxh], 1, xnh, bxnh, sqh, bsqh, junk, bjunk, ptr, hxh, bhxh, 0, ai, npart=2)
            def chunk_gen(j, kind, jj):
                p, bp = pz.next()
                z, bz = zb.next()
                a1, ba1 = y1.next()
                for k in range(8):
                    B.op("pe", lambda e, k=k: e.matmul(p[:, 0:n], lhsT=wqkv[:, k, j * 128:(j + 1) * 128], rhs=hxT[:, k, 1:1 + n],
                                                         start=(k == 0), stop=(k == 7)), reads=[bwqkv, bhx], writes=[bp], inc=(k == 7))
                for k in range(8):
                    B.op("pe", lambda e, k=k: e.matmul(pzh[:, 2 * j:2 * j + 2], lhsT=wqkv[:, k, j * 128:(j + 1) * 128], rhs=hxh[:, k, :],
                                                         start=(k == 0), stop=(k == 7)), reads=[bwqkv, bhxh], writes=[bpzh], inc=(k == 7))
                yield
                B.op("act", lambda e: e.activation(out=z[:, 1:1 + n], in_=p[:, 0:n], func=AF.Identity), reads=[bp], writes=[bz])
                B.op("act", lambda e: e.activation(out=z[:, 0:1], in_=pzh[:, 2 * j:2 * j + 1], func=AF.Identity), reads=[bpzh], writes=[bz])
                B.op("act", lambda e: e.activation(out=z[:, n + 1:n + 2], in_=pzh[:, 2 * j + 1:2 * j + 2], func=AF.Identity), reads=[bpzh], writes=[bz])
                if not hl:
                    B.op("pool", lambda e: e.memset(z[:, 0:1], 0.0), writes=[bz])
                if not hr:
                    B.op("pool", lambda e: e.memset(z[:, n + 1:n + 2], 0.0), writes=[bz])
                yield
                B.op("dve", lambda e: e.tensor_scalar(out=a1[:, 0:n], in0=z[:, 1:1 + n], scalar1=cw[:, j, 1:2], scalar2=None, op0=ALU.mult),
                     reads=[bz, bsm], writes=[ba1])
                B.op("dve", lambda e: e.scalar_tensor_tensor(out=a1[:, 0:n], in0=z[:, 0:n], scalar=cw[:, j, 0:1], in1=a1[:, 0:n],
                                                            op0=ALU.mult, op1=ALU.add), reads=[bz, bsm, ba1], writes=[ba1])
                B.op("dve", lambda e: e.scalar_tensor_tensor(out=a1[:, 0:n], in0=z[:, 2:2 + n], scalar=cw[:, j, 2:3], in1=a1[:, 0:n],
                                                            op0=ALU.mult, op1=ALU.add), reads=[bz, bsm, ba1], writes=[ba1])
                yield
                if kind == "v":
                    B.op("act", lambda e: e.activation(out=vT[:, jj, 0:n], in_=a1[:, 0:n], func=AF.Silu), reads=[ba1], writes=[bvT])
                else:
                    B.op("act", lambda e: e.activation(out=skeep[:, jj, 0:n], in_=a1[:, 0:n], func=AF.Silu), reads=[ba1], writes=[bskeep[jj]])
                    q2, bq2 = sqb.next()
                    B.op("pool", lambda e: e.tensor_tensor(out=q2[:, 0:n], in0=skeep[:, jj, 0:n], in1=skeep[:, jj, 0:n], op=ALU.mult),
                         reads=[bskeep[jj]], writes=[bq2])
                    B.op("pe", lambda e: e.matmul(pss[0:8, 0:n], lhsT=ind[:, jj, :], rhs=q2[:, 0:n], start=(jj == 0), stop=(jj == 7)),
                         reads=[bq2, bind], writes=[bpss])

            for half in range(2):
                self.run_pipeline([(lambda jj=jj: chunk_gen(half * 8 + jj, "qk", jj)) for jj in range(8)], 4)
                B.op("act", lambda e: e.activation(out=rnr[:, 0:n], in_=pss[0:8, 0:n], func=AF.Ln, bias=float(EPS)), reads=[bpss], writes=[brnr])
                B.op("act", lambda e: e.activation(out=rnr[:, 0:n], in_=rnr[:, 0:n], func=AF.Exp, scale=-0.5), reads=[brnr], writes=[brnr])
                for jj in range(8):
                    pp, bpp = pn.next()
                    B.op("pe", lambda e, pp=pp, jj=jj: e.matmul(pp[:, 0:n], lhsT=selr[:, jj, :], rhs=rnr[:, 0:n], start=True, stop=True),
                         reads=[brnr, bselr], writes=[bpp])
                    if half == 0:
                        B.op("dve", lambda e, pp=pp, jj=jj: e.scalar_tensor_tensor(
                            out=qst[:, 0:ns, jj, :], in0=skeep[:, jj, 0:n].rearrange("p (s t) -> p s t", t=128), scalar=dkr,
                            in1=pp[:, 0:n].rearrange("p (s t) -> p s t", t=128), op0=ALU.mult, op1=ALU.mult), reads=[bskeep[jj], bpp], writes=[bqst])
                    else:
                        B.op("dve", lambda e, pp=pp, jj=jj: e.tensor_tensor(
                            out=kst[:, 0:ns, jj, :], in0=skeep[:, jj, 0:n].rearrange("p (s t) -> p s t", t=128),
                            in1=pp[:, 0:n].rearrange("p (s t) -> p s t", t=128), op=ALU.mult), reads=[bskeep[jj], bpp], writes=[bkst])
            for s in range(ns):
                for k in range(8):
                    B.op("pe", lambda e, k=k, s=s: e.matmul(pg[:, s, :], lhsT=hxT[:, k, 1 + s * 128:1 + (s + 1) * 128], rhs=wgt[:, k, :],
                                                             start=(k == 0), stop=(k == 7)), reads=[bhx, bwgt], writes=[bpg], inc=(k == 7))
            g, bg_ = gsb.next()
            self.gate_math(pg, bpg, graw, bgraw, gwk, bgwk, g, bg_, gp, bsm, ns)
            B.dma("sp", self.GT[c0:c0 + ns].rearrange("c t g -> t c g"), g[:, 0:ns, :], bg_, reads=[bg_])
            self.run_pipeline([(lambda jj=jj: chunk_gen(16 + jj, "v", jj)) for jj in range(8)], 4)
            self.v_transposes(vT, bvT, ns, vst, pn, self.VG, c0)
            B.dma("sp", self.KT[c0:c0 + ns].rearrange("c d h t -> d c h t"), kst[:, 0:ns], bkst, reads=[bkst])
            B.dma("sp", self.QT[c0:c0 + ns].rearrange("c d h t -> d c h t"), qst[:, 0:ns], bqst, reads=[bqst])
            for j in range(16):
                p, bp = pz.next()
                for k in range(8):
                    B.op("pe", lambda e, k=k, p=p, j=j: e.matmul(p[:, 0:n], lhsT=wml[:, k, j * 128:(j + 1) * 128], rhs=hxT[:, k, 1:1 + n],
                                                                  start=(k == 0), stop=(k == 7)), reads=[bwml, bhx], writes=[bp], inc=(k == 7))
                if j < 4:
                    B.op("act", lambda e, p=p, j=j: e.activation(out=mqst[:, 0:ns, j, :], in_=p[:, 0:n].rearrange("p (s t) -> p s t", t=128),
                                                                  func=AF.Identity, scale=dkr), reads=[bp], writes=[bmqst])
                elif j < 8:
                    B.op("act", lambda e, p=p, j=j: e.activation(out=mkst[:, 0:ns, j - 4, :], in_=p[:, 0:n].rearrange("p (s t) -> p s t", t=128),
                                                                  func=AF.Identity), reads=[bp], writes=[bmkst])
                else:
                    B.op("act", lambda e, p=p, j=j: e.activation(out=vT[:, j - 8, 0:n], in_=p[:, 0:n], func=AF.Identity), reads=[bp], writes=[bvT])
            self.v_transposes(vT, bvT, ns, vst, pn, self.MV, c0)
            B.dma("sp", self.MQT[c0:c0 + ns].rearrange("c d h t -> d c h t"), mqst[:, 0:ns], bmqst, reads=[bmqst])
            B.dma("sp", self.MKT[c0:c0 + ns].rearrange("c d h t -> d c h t"), mkst[:, 0:ns], bmkst, reads=[bmkst])
        B.barrier()
        st.close()

    def v_transposes(self, vT, bvT, ns, vst, pn, dst, c0):
        B = self.B
        v, bv = vst.next()
        for s in range(ns):
            pp, bpp = pn.next()
            ppb = pp[:].bitcast(BF16)
            for h in range(8):
                B.op("pe", lambda e, s=s, h=h, ppb=ppb: e.transpose(ppb[:, h * 128:(h + 1) * 128], vT[:, h, s * 128:(s + 1) * 128], self.ident_b[:]),
                     reads=[bvT, self.cb], writes=[bpp], inc=(h == 7))
            B.op("act", lambda e, s=s, ppb=ppb, v=v: e.activation(out=v[:, s, :], in_=ppb[:, 0:1024], func=AF.Identity), reads=[bpp], writes=[bv])
        B.dma("sp", dst[c0:c0 + ns].rearrange("c t e -> t c e"), v[:, 0:ns, :], bv, reads=[bv])

    def gate_math(self, pg, bpg, graw, bgraw, wk, bwk, g, bg_, gp, bgp, ns):
        B = self.B
        S = slice(0, ns)

        def bc(lo, hi):
            return gp[:, lo:hi].unsqueeze(1).to_broadcast([128, ns, hi - lo])

        B.op("act", lambda e: e.activation(out=graw[:, S, :], in_=pg[:, S, :], func=AF.Identity), reads=[bpg], writes=[bgraw])
        B.op("dve", lambda e: e.tensor_tensor(out=wk[:, S, 0:16], in0=graw[:, S, 0:16], in1=bc(0, 16), op=ALU.add), reads=[bgraw, bgp], writes=[bwk])
        B.op("act", lambda e: e.activation(out=wk[:, S, 0:16], in_=wk[:, S, 0:16], func=AF.Exp), reads=[bwk], writes=[bwk])
        B.op("act", lambda e: e.activation(out=wk[:, S, 0:16], in_=wk[:, S, 0:16], func=AF.Ln, bias=1.0), reads=[bwk], writes=[bwk])
        B.op("dve", lambda e: e.tensor_tensor(out=g[:, S, 0:16], in0=wk[:, S, 0:16], in1=bc(16, 32), op=ALU.mult), reads=[bwk, bgp], writes=[bg_])
        B.op("act", lambda e: e.activation(out=wk[:, S, 16:32], in_=graw[:, S, 16:32], func=AF.Exp, scale=-1.0), reads=[bgraw], writes=[bwk])
        B.op("dve", lambda e: e.tensor_scalar(out=wk[:, S, 16:32], in0=wk[:, S, 16:32], scalar1=1.0, scalar2=None, op0=ALU.add), reads=[bwk], writes=[bwk])
        B.op("dve", lambda e: e.reciprocal(out=g[:, S, 16:32], in_=wk[:, S, 16:32]), reads=[bwk], writes=[bg_])
        B.op("dve", lambda e: e.tensor_tensor(out=wk[:, S, 32:48], in0=graw[:, S, 32:48], in1=bc(32, 48), op=ALU.add), reads=[bgraw, bgp], writes=[bwk])
        B.op("act", lambda e: e.activation(out=wk[:, S, 32:48], in_=wk[:, S, 32:48], func=AF.Exp, scale=float(2.0 / 15.0)), reads=[bwk], writes=[bwk])
        B.op("dve", lambda e: e.tensor_scalar(out=wk[:, S, 32:48], in0=wk[:, S, 32:48], scalar1=1.0, scalar2=None, op0=ALU.add), reads=[bwk], writes=[bwk])
        B.op("dve", lambda e: e.reciprocal(out=wk[:, S, 32:48], in_=wk[:, S, 32:48]), reads=[bwk], writes=[bwk])
        B.op("dve", lambda e: e.tensor_scalar(out=g[:, S, 32:48], in0=wk[:, S, 32:48], scalar1=-30.0, scalar2=15.0, op0=ALU.mult, op1=ALU.add),
             reads=[bwk], writes=[bg_])
        B.op("act", lambda e: e.activation(out=wk[:, S, 40:48], in_=g[:, S, 40:48], func=AF.Exp, scale=-1.0), reads=[bg_], writes=[bwk])
        B.op("act", lambda e: e.activation(out=wk[:, S, 40:48], in_=wk[:, S, 40:48], func=AF.Ln, bias=1.0), reads=[bwk], writes=[bwk])
        B.op("dve", lambda e: e.tensor_scalar(out=g[:, S, 40:48], in0=wk[:, S, 40:48], scalar1=-1.0, scalar2=None, op0=ALU.mult), reads=[bwk], writes=[bg_])

    def phaseB(self):
        B, nc, inp = self.B, self.nc, self.inp
        st = ExitStack()
        dbg = self.debug
        LE, bLE = self.mask(st, "b_LE", (0, -1, 1, ALU.is_ge))
        LT, bLT = self.mask(st, "b_LT", (-1, -1, 1, ALU.is_ge))
        GE, bGE = self.mask(st, "b_GE", (0, 1, -1, ALU.is_ge))
        GT_, bGT = self.mask(st, "b_GT", (-1, 1, -1, ALU.is_ge))
        MBf, bMBf = self.mask(st, "b_MBf", (0, 1, -1, ALU.is_ge), val=0.0, fill=NEG)
        MBb, bMBb = self.mask(st, "b_MBb", (0, -1, 1, ALU.is_ge), val=0.0, fill=NEG)
        SELf, bSELf = self.mask(st, "b_SELf", (-127, 1, 0, ALU.is_equal))
        SELb, bSELb = self.mask(st, "b_SELb", (0, 1, 0, ALU.is_equal))
        smask = B.sb(st, "b_smask", [128, 14, 128], BF16)
        bsm = Buf("b_smask")
        B.dma("pool", smask[:], inp["smask"][:, :, :], bsm, writes=[bsm])
        cbufs = [bLE, bLT, bGE, bGT, bMBf, bMBb, bSELf, bSELb, bsm, self.cb]
        dirc = [dict(U=LE, S=GT_, incl=LE, strict=LT, MB=MBf, SEL=SELf),
                dict(U=GE, S=LT, incl=GE, strict=GT_, MB=MBb, SEL=SELb)]
        S = [B.sb(st, "b_S%d" % d, [128, 8, 128], F32) for d in range(2)]
        bS = [[Buf("b_S%d_%d" % (d, g)) for g in range(2)] for d in range(2)]
        Sb = [[Ring(B, st, "b_Sb%d_%d_" % (d, g), 2, [128, 4, 128], BF16) for g in range(2)] for d in range(2)]
        Sb_cur = [[None, None], [None, None]]
        C = [B.sb(st, "b_C%d" % d, [128, 4, 256], F32) for d in range(2)]
        bC = [Buf("b_C%d" % d) for d in range(2)]
        Cb = [Ring(B, st, "b_Cb%d_" % d, 2, [128, 4, 256], BF16) for d in range(2)]
        Cb_cur = [None, None]
        nst = [Ring(B, st, "b_n%d_" % d, 2, [128, 8], F32) for d in range(2)]
        nbf = [Ring(B, st, "b_nb%d_" % d, 2, [128, 4], BF16) for d in range(2)]
        n_cur = [None, None]
        nb_cur = [None, None]
        mst = [Ring(B, st, "b_m%d_" % d, 2, [128, 4], F32) for d in range(2)]
        m_cur = [None, None]
        for d in range(2):
            B.op("pool", lambda e, d=d: e.memset(S[d][:], 0.0), writes=bS[d])
            B.op("pool", lambda e, d=d: e.memset(C[d][:], 0.0), writes=[bC[d]])
            for g in range(2):
                t, b = Sb[d][g].next()
                B.op("pool", lambda e, t=t: e.memset(t[:], 0.0), writes=[b])
                Sb_cur[d][g] = (t, b)
            t, b = Cb[d].next()
            B.op("pool", lambda e, t=t: e.memset(t[:], 0.0), writes=[b])
            Cb_cur[d] = (t, b)
            t, b = nst[d].next()
            B.op("pool", lambda e, t=t: e.memset(t[:], 0.0), writes=[b])
            n_cur[d] = (t, b)
            t, b = nbf[d].next()
            B.op("pool", lambda e, t=t: e.memset(t[:], 0.0), writes=[b])
            nb_cur[d] = (t, b)
            t, b = mst[d].next()
            B.op("pool", lambda e, t=t: e.memset(t[:], 0.0), writes=[b])
            m_cur[d] = (t, b)
        def dring(name, shape, dt):
            return [Ring(B, st, "b_%s%d_" % (name, d), 2, shape, dt) for d in range(2)]
        rKT = dring("KT", [128, 8, 128], BF16)
        rQT = dring("QT", [128, 8, 128], BF16)
        rVG = dring("VG", [128, 1024], BF16)
        rGT = dring("GT", [128, 48], F32)
        rMQ = dring("MQ", [128, 4, 128], BF16)
        rMK = dring("MK", [128, 4, 128], BF16)
        rMV = dring("MV", [128, 1024], BF16)
        rgs = dring("gs", [128, 64], F32)
        psr = Ring(B, st, "b_ps", 8, [128, 512], F32, psum=True)
        NG, NM = 4, 2
        gslots = []
        for i in range(NG):
            sl = {}
            for nm, shp, dt in (("A", [128, 4, 128], F32), ("Bt", [128, 4, 128], F32), ("Ct", [128, 4, 128], F32),
                                ("attnT", [128, 4, 128], BF16), ("Qp", [128, 4, 128], BF16),
                                ("Kg", [128, 4, 128], BF16), ("kt", [128, 4, 128], BF16), ("G0", [128, 4, 128], BF16),
                                ("G1", [128, 4, 128], BF16), ("H0", [128, 4, 128], BF16), ("H1", [128, 4, 128], BF16),
                                ("IYT", [128, 4, 128], BF16), ("negW", [128, 4, 128], BF16), ("vnew", [128, 4, 128], BF16)):
                sl[nm] = (B.sb(st, "b_g%d_%s" % (i, nm), shp, dt), Buf("b_g%d_%s" % (i, nm)))
            gslots.append(sl)
        mslots = []
        for i in range(NM):
            sl = {}
            for nm, shp, dt in (("X", [128, 4, 128], F32), ("Y", [128, 4, 128], F32), ("Pm", [128, 4, 128], BF16),
                                ("PT", [128, 4, 128], BF16), ("Kw", [128, 4, 128], BF16), ("sm", [128, 64], F32), ("Ct", [128, 4, 256], F32)):
                sl[nm] = (B.sb(st, "b_m%d_%s" % (i, nm), shp, dt), Buf("b_m%d_%s" % (i, nm)))
            mslots.append(sl)
        ring_o1 = Ring(B, st, "b_o1_", 2, [128, 4, 128], F32)
        ring_o = Ring(B, st, "b_o_", 2, [128, 4, 128], F32)
        ring_num = Ring(B, st, "b_num_", 1, [128, 4, 256], F32)
        ring_h = Ring(B, st, "b_h_", 1, [128, 4, 256], F32)

        def bc3(ap2, n):
            return ap2.unsqueeze(2).to_broadcast([128, 4, n])

        def bcm(ap2, n=4):
            return ap2.unsqueeze(1).to_broadcast([128, n, 128])

        nsteps = dbg.get("b_steps", NCH)
        order = [list(range(35)), list(range(34, 1, -1))]
        if dbg.get("b_order"):
            order = dbg["b_order"]
            nsteps = len(order[0])
        out_lo, out_hi = 2, 2 + 33

        data = {}

        def load_step(step, d):
            c = order[d][step]
            tk, bk = rKT[d].next(); tq, bq = rQT[d].next(); tv, bv = rVG[d].next(); tg, bg = rGT[d].next()
            tmq, bmq = rMQ[d].next(); tmk, bmk = rMK[d].next(); tmv, bmv = rMV[d].next()
            B.dma("sp", tg[:], self.GT[c], bg, writes=[bg])
            B.dma("sp", tk[:], self.KT[c], bk, writes=[bk])
            B.dma("sp", tq[:], self.QT[c], bq, writes=[bq])
            B.dma("sp", tv[:], self.VG[c], bv, writes=[bv])
            B.dma("sp", tmq[:], self.MQT[c], bmq, writes=[bmq])
            B.dma("sp", tmk[:], self.MKT[c], bmk, writes=[bmk])
            B.dma("sp", tmv[:], self.MV[c], bmv, writes=[bmv])
            data[(step, d)] = dict(c=c, KT=(tk, bk), QT=(tq, bq), VG=(tv, bv), GT=(tg, bg), MQ=(tmq, bmq), MK=(tmk, bmk), MV=(tmv, bmv))

        def shared_pre(step, d):
            dd = data[(step, d)]
            tg, bg = dd["GT"]
            gs, bgs = rgs[d].next()
            dc = dirc[d]
            p, bp = psr.next()
            B.op("pe", lambda e: e.matmul(p[:, 0:8], lhsT=dc["U"][:], rhs=tg[:, d * 8:(d + 1) * 8], start=True, stop=True), reads=[bg] + cbufs, writes=[bp], inc=False)
            B.op("pe", lambda e: e.matmul(p[:, 8:12], lhsT=dc["U"][:], rhs=tg[:, 40 + d * 4:44 + d * 4], start=True, stop=True), reads=[bg] + cbufs, writes=[bp], inc=False)
            B.op("pe", lambda e: e.matmul(p[:, 12:20], lhsT=self.ones_f[:], rhs=tg[:, d * 8:(d + 1) * 8], start=True, stop=True), reads=[bg] + cbufs, writes=[bp], inc=False)
            B.op("pe", lambda e: e.matmul(p[:, 20:24], lhsT=self.ones_f[:], rhs=tg[:, 40 + d * 4:44 + d * 4], start=True, stop=True), reads=[bg] + cbufs, writes=[bp])
            B.op("act", lambda e: e.activation(out=gs[:, 0:24], in_=p[:, 0:24], func=AF.Identity), reads=[bp], writes=[bgs])
            B.op("act", lambda e: e.activation(out=gs[:, 24:32], in_=gs[:, 0:8], func=AF.Exp), reads=[bgs], writes=[bgs])
            B.op("dve", lambda e: e.tensor_tensor(out=gs[:, 32:40], in0=gs[:, 12:20], in1=gs[:, 0:8], op=ALU.subtract), reads=[bgs], writes=[bgs])
            B.op("act", lambda e: e.activation(out=gs[:, 32:40], in_=gs[:, 32:40], func=AF.Exp), reads=[bgs], writes=[bgs])
            B.op("act", lambda e: e.activation(out=gs[:, 40:48], in_=gs[:, 12:20], func=AF.Exp), reads=[bgs], writes=[bgs])
            B.op("dve", lambda e: e.tensor_tensor(out=gs[:, 48:52], in0=tg[:, 32 + d * 4:36 + d * 4], in1=gs[:, 8:12], op=ALU.subtract), reads=[bgs, bg], writes=[bgs])
            dd["gs"] = (gs, bgs)

        def gdn_group(step, d, hg, sl):
            dd = data[(step, d)]
            dc = dirc[d]
            c = dd["c"]
            need_o = out_lo <= c < out_hi
            tk, bk = dd["KT"]; tq, bq = dd["QT"]; tv, bv = dd["VG"]; tg, bg = dd["GT"]; gs, bgs = dd["gs"]
            h0 = hg * 4
            A, bA = sl["A"]; Bt, bBt = sl["Bt"]; Ct, bCt = sl["Ct"]
            attnT, battn = sl["attnT"]; Qp, bQp = sl["Qp"]; Kg, bKg = sl["Kg"]; kt, bkt = sl["kt"]
            IYT, bIYT = sl["IYT"]; negW, bnegW = sl["negW"]; vnew, bvnew = sl["vnew"]
            Qm, bQm = sl["IYT"]
            St, bSt = sl["A"]
            gcol = tg[:, d * 8 + h0:d * 8 + h0 + 4]
            bcol = tg[:, 16 + d * 8 + h0:16 + d * 8 + h0 + 4]
            eg = gs[:, 24 + h0:24 + h0 + 4]
            ekt = gs[:, 32 + h0:32 + h0 + 4]
            gte = gs[:, 40 + h0:40 + h0 + 4]
            for u in range(4):
                B.op("act", lambda e, u=u: e.activation(out=A[:, u, :], in_=dc["U"][:], func=AF.Identity, scale=gcol[:, u:u + 1]), reads=[bg] + cbufs, writes=[bA])
            for u in range(4):
                B.op("act", lambda e, u=u: e.activation(out=Ct[:, u, :], in_=dc["strict"][:], func=AF.Identity, scale=bcol[:, u:u + 1]), reads=[bg] + cbufs, writes=[bCt])
            yield
            pD, bpD = psr.next()
            for u in range(4):
                B.op("pe", lambda e, u=u: e.matmul(pD[:, u * 128:(u + 1) * 128], lhsT=dc["S"][:], rhs=A[:, u, :], start=True, stop=True),
                     reads=[bA] + cbufs, writes=[bpD], inc=(u == 3))
            B.op("act", lambda e: e.activation(out=Bt[:].rearrange("p u l -> p (u l)"), in_=pD[:, :], func=AF.Exp), reads=[bpD], writes=[bBt])
            yield
            B.op("dve", lambda e: e.tensor_tensor(out=A[:], in0=Bt[:], in1=bcm(dc["incl"][:]), op=ALU.mult), reads=[bBt] + cbufs, writes=[bA])
            B.op("pool", lambda e: e.tensor_tensor(out=Ct[:], in0=Ct[:], in1=Bt[:], op=ALU.mult), reads=[bCt, bBt], writes=[bCt])
            yield
            pKK, bpKK = psr.next()
            pQK, bpQK = psr.next()
            pKt, bpKt = psr.next()
            pKtb = pKt[:].bitcast(BF16)
            for u in range(4):
                B.op("pe", lambda e, u=u: e.matmul(pKK[:, u * 128:(u + 1) * 128], lhsT=tk[:, h0 + u, :], rhs=tk[:, h0 + u, :], start=True, stop=True),
                     reads=[bk], writes=[bpKK], inc=(u == 3))
            for u in range(4):
                B.op("pe", lambda e, u=u: e.matmul(pQK[:, u * 128:(u + 1) * 128], lhsT=tk[:, h0 + u, :], rhs=tq[:, h0 + u, :], start=True, stop=True),
                     reads=[bk, bq], writes=[bpQK], inc=(u == 3))
            for u in range(4):
                B.op("pe", lambda e, u=u: e.transpose(pKtb[:, u * 128:(u + 1) * 128], tk[:, h0 + u, :], self.ident_b[:]),
                     reads=[bk] + cbufs, writes=[bpKt], inc=(u == 3))
            B.op("dve", lambda e: e.tensor_tensor(out=Qm[:], in0=pKK[:, :].rearrange("p (u l) -> p u l", u=4), in1=Ct[:], op=ALU.mult),
                 reads=[bpKK, bCt], writes=[bQm])
            B.op("dve", lambda e: e.tensor_tensor(out=attnT[:], in0=pQK[:, :].rearrange("p (u l) -> p u l", u=4), in1=A[:], op=ALU.mult),
                 reads=[bpQK, bA], writes=[battn])
            B.op("dve", lambda e: e.tensor_tensor(out=Kg[:], in0=pKtb[:, 0:512].rearrange("p (u l) -> p u l", u=4), in1=bc3(eg, 128), op=ALU.mult),
                 reads=[bpKt, bgs], writes=[bKg])
            B.op("dve", lambda e: e.tensor_tensor(out=kt[:], in0=pKtb[:, 0:512].rearrange("p (u l) -> p u l", u=4), in1=bc3(ekt, 128), op=ALU.mult),
                 reads=[bpKt, bgs], writes=[bkt])
            yield
            B.op("pool", lambda e: e.tensor_tensor(out=Qp[:], in0=Qm[:], in1=bcm(self.ident_b[:]), op=ALU.add), reads=[bQm] + cbufs, writes=[bQp])
            yield
            Gc = None
            Hc = None
            for lev in range(7):
                sm = smask[:, d * 7 + lev, :]
                pY, bpY = psr.next()
                for u in range(4):
                    rhsH = self.ident_b[:] if Hc is None else Hc[0][:, u, :]
                    B.op("pe", lambda e, u=u, rhsH=rhsH: e.matmul(pY[:, u * 128:(u + 1) * 128], lhsT=Qp[:, u, :], rhs=rhsH, start=True, stop=True),
                         reads=[bQp] + cbufs + ([] if Hc is None else [Hc[1]]), writes=[bpY], inc=(u == 3))
                B.op("dve", lambda e, sm=sm: e.tensor_tensor(out=IYT[:], in0=pY[:, :].rearrange("p (u l) -> p u l", u=4), in1=bcm(sm), op=ALU.mult),
                     reads=[bpY] + cbufs, writes=[bIYT])
                yield
                Gn = sl["G%d" % (lev % 2)]
                Hn = sl["H%d" % (lev % 2)]
                pG, bpG = psr.next()
                for u in range(4):
                    rhsG = self.ident_b[:] if Gc is None else Gc[0][:, u, :]
                    B.op("pe", lambda e, u=u, rhsG=rhsG: e.matmul(pG[:, u * 128:(u + 1) * 128], lhsT=IYT[:, u, :], rhs=rhsG, start=True, stop=True),
                         reads=[bIYT] + cbufs + ([] if Gc is None else [Gc[1]]), writes=[bpG], inc=(u == 3))
                B.op("act", lambda e, Gn=Gn: e.activation(out=Gn[0][:].rearrange("p u l -> p (u l)"), in_=pG[:, :], func=AF.Identity), reads=[bpG], writes=[Gn[1]])
                if lev < 6:
                    pH, bpH = psr.next()
                    for u in range(4):
                        lhsG = self.ident_b[:] if Gc is None else Gc[0][:, u, :]
                        B.op("pe", lambda e, u=u, lhsG=lhsG: e.matmul(pH[:, u * 128:(u + 1) * 128], lhsT=lhsG, rhs=IYT[:, u, :], start=True, stop=True),
                             reads=[bIYT] + cbufs + ([] if Gc is None else [Gc[1]]), writes=[bpH], inc=(u == 3))
                    B.op("act", lambda e, Hn=Hn: e.activation(out=Hn[0][:].rearrange("p u l -> p (u l)"), in_=pH[:, :], func=AF.Identity), reads=[bpH], writes=[Hn[1]])
                    Hc = Hn
                Gc = Gn
                yield
            G, bG = Gc
            pW, bpW = psr.next()
            for u in range(4):
                B.op("pe", lambda e, u=u: e.matmul(pW[:, u * 128:(u + 1) * 128], lhsT=Kg[:, u, :], rhs=G[:, u, :], start=True, stop=True),
                     reads=[bKg, bG], writes=[bpW], inc=(u == 3))
            B.op("act", lambda e: e.activation(out=negW[:].rearrange("p u l -> p (u l)"), in_=pW[:, :], func=AF.Identity, scale=-1.0), reads=[bpW], writes=[bnegW])
            yield
            while step > 0 and ("gdn", step - 1, d, hg) not in done and ("gdn", step - 1, d, hg) in started:
                yield
            sbt, bsb = Sb_cur[d][hg]
            pV, bpV = psr.next()
            for u in range(4):
                B.op("pe", lambda e, u=u: e.matmul(pV[:, u * 128:(u + 1) * 128], lhsT=G[:, u, :], rhs=tv[:, (h0 + u) * 128:(h0 + u + 1) * 128], start=True, stop=False),
                     reads=[bG, bv], writes=[bpV], inc=False)
                B.op("pe", lambda e, u=u: e.matmul(pV[:, u * 128:(u + 1) * 128], lhsT=negW[:, u, :], rhs=sbt[:, u, :], start=False, stop=True),
                     reads=[bnegW, bsb], writes=[bpV], inc=(u == 3))
            B.op("dve", lambda e: e.tensor_tensor(out=vnew[:], in0=pV[:, :].rearrange("p (u l) -> p u l", u=4), in1=bc3(bcol, 128), op=ALU.mult),
                 reads=[bpV, bg], writes=[bvnew])
            Sg = S[d][:, h0:h0 + 4, :]
            B.op("pool", lambda e: e.tensor_tensor(out=St[:], in0=Sg, in1=bc3(gte, 128), op=ALU.mult), reads=[bS[d][hg], bgs], writes=[bSt])
            yield
            if need_o:
                pO1, bpO1 = psr.next()
                for u in range(4):
                    B.op("pe", lambda e, u=u: e.matmul(pO1[:, u * 128:(u + 1) * 128], lhsT=tq[:, h0 + u, :], rhs=sbt[:, u, :], start=True, stop=True),
                         reads=[bq, bsb], writes=[bpO1], inc=(u == 3))
                o1, bo1 = ring_o1.next()
                B.op("dve", lambda e: e.tensor_tensor(out=o1[:], in0=pO1[:, :].rearrange("p (u l) -> p u l", u=4), in1=bc3(eg, 128), op=ALU.mult),
                     reads=[bpO1, bgs], writes=[bo1])
                pO2, bpO2 = psr.next()
                for u in range(4):
                    B.op("pe", lambda e, u=u: e.matmul(pO2[:, u * 128:(u + 1) * 128], lhsT=attnT[:, u, :], rhs=vnew[:, u, :], start=True, stop=True),
                         reads=[battn, bvnew], writes=[bpO2], inc=(u == 3))
                o, bo = ring_o.next()
                B.op("dve", lambda e: e.tensor_tensor(out=o[:], in0=pO2[:, :].rearrange("p (u l) -> p u l", u=4), in1=o1[:], op=ALU.add),
                     reads=[bpO2, bo1], writes=[bo])
                dst = (self.OF if d == 0 else self.OB)[c - 2]
                B.dma("sp", dst[:, hg * 512:(hg + 1) * 512], o[:].rearrange("p u l -> p (u l)"), bo, reads=[bo])
            pS, bpS = psr.next()
            for u in range(4):
                B.op("pe", lambda e, u=u: e.matmul(pS[:, u * 128:(u + 1) * 128], lhsT=kt[:, u, :], rhs=vnew[:, u, :], start=True, stop=True),
                     reads=[bkt, bvnew], writes=[bpS], inc=(u == 3))
            B.op("dve", lambda e: e.tensor_tensor(out=Sg, in0=pS[:, :].rearrange("p (u l) -> p u l", u=4), in1=St[:], op=ALU.add),
                 reads=[bpS, bSt], writes=[bS[d][hg]])
            yield
            nsb, bnsb = Sb[d][hg].next()
            B.op("act", lambda e: e.activation(out=nsb[:], in_=Sg, func=AF.Identity), reads=[bS[d][hg]], writes=[bnsb])
            Sb_cur[d][hg] = (nsb, bnsb)
            yield

        self._b_env = dict(data=data, dirc=dirc, cbufs=cbufs, psr=psr, order=order, out_lo=out_lo, out_hi=out_hi, bc3=bc3, bcm=bcm,
                           C=C, bC=bC, Cb=Cb, Cb_cur=Cb_cur, nst=nst, nbf=nbf, n_cur=n_cur, nb_cur=nb_cur, mst=mst, m_cur=m_cur,
                           ring_num=ring_num, ring_h=ring_h)
        ml_group = self.make_ml_group()

        bsnap_in = Buf("snap_in")
        bsnap_out = Buf("snap_out")
        smallt = B.sb(st, "b_snap_small", [128, 16], F32)
        bsmallt = Buf("b_snap_small")

        def do_snapshot():
            si, so = self.SNAP_IN.ap(), self.SNAP_OUT.ap()
            nt_, bnt_ = n_cur[0]
            mt_, bmt_ = m_cur[0]
            B.dma("sp", si[:, 0:1024], S[0][:].rearrange("p h e -> p (h e)"), bsnap_in, reads=[bS[0][0], bS[0][1]], writes=[bsnap_in])
            B.dma("sp", si[:, 1024:2048], C[0][:].rearrange("p h e -> p (h e)"), bsnap_in, reads=[bC[0]], writes=[bsnap_in])
            B.dma("sp", si[:, 2048:2052], nt_[:, 0:4], bsnap_in, reads=[bnt_], writes=[bsnap_in])
            B.dma("sp", si[:, 2052:2056], mt_[:, 0:4], bsnap_in, reads=[bmt_], writes=[bsnap_in])
            B.cc_allreduce(self.SNAP_IN.ap().opt(), self.SNAP_OUT.ap().opt(), [[0, 1], [2, 3], [4, 5], [6, 7]], bsnap_out,
                           reads=[bsnap_in], writes=[bsnap_out])
            tA, btA = ring_num.next()
            tB, btB = ring_h.next()
            B.dma("sp", S[1][:].rearrange("p h e -> p (h e)"), so[:, 0:1024], bS[1][0], reads=[bsnap_out], writes=[bS[1][0], bS[1][1]])
            B.dma("sp", tA[:].rearrange("p h e -> p (h e)"), si[:, 0:1024], btA, reads=[bsnap_in], writes=[btA])
            B.op("dve", lambda e: e.tensor_tensor(out=S[1][:].rearrange("p h e -> p (h e)"), in0=S[1][:].rearrange("p h e -> p (h e)"),
                                                  in1=tA[:].rearrange("p h e -> p (h e)"), op=ALU.subtract), reads=[bS[1][0], bS[1][1], btA], writes=[bS[1][0], bS[1][1]])
            for g in range(2):
                t, b = Sb[1][g].next()
                B.op("act", lambda e, t=t, g=g: e.activation(out=t[:], in_=S[1][:, g * 4:g * 4 + 4, :], func=AF.Identity), reads=[bS[1][g]], writes=[b])
                Sb_cur[1][g] = (t, b)
            B.dma("sp", C[1][:].rearrange("p h e -> p (h e)"), so[:, 1024:2048], bC[1], reads=[bsnap_out], writes=[bC[1]])
            B.dma("sp", tB[:].rearrange("p h e -> p (h e)"), si[:, 1024:2048], btB, reads=[bsnap_in], writes=[btB])
            B.op("dve", lambda e: e.tensor_tensor(out=C[1][:].rearrange("p h e -> p (h e)"), in0=C[1][:].rearrange("p h e -> p (h e)"),
                                                  in1=tB[:].rearrange("p h e -> p (h e)"), op=ALU.subtract), reads=[bC[1], btB], writes=[bC[1]])
            t, b = Cb[1].next()
            B.op("act", lambda e, t=t: e.activation(out=t[:], in_=C[1][:], func=AF.Identity), reads=[bC[1]], writes=[b])
            Cb_cur[1] = (t, b)
            B.dma("sp", smallt[:, 0:8], so[:, 2048:2056], bsmallt, reads=[bsnap_out], writes=[bsmallt])
            nn, bnn = nst[1].next()
            mm_, bmm = mst[1].next()
            B.op("dve", lambda e: e.tensor_tensor(out=nn[:, 0:4], in0=smallt[:, 0:4], in1=nt_[:, 0:4], op=ALU.subtract), reads=[bsmallt, bnt_], writes=[bnn])
            B.op("dve", lambda e: e.tensor_tensor(out=mm_[:, 0:4], in0=smallt[:, 4:8], in1=mt_[:, 0:4], op=ALU.subtract), reads=[bsmallt, bmt_], writes=[bmm])
            nb_, bnb_ = nbf[1].next()
            B.op("act", lambda e: e.activation(out=nb_[:], in_=nn[:, 0:4], func=AF.Identity), reads=[bnn], writes=[bnb_])
            n_cur[1] = (nn, bnn)
            nb_cur[1] = (nb_, bnb_)
            m_cur[1] = (mm_, bmm)

        from collections import deque
        pending = deque()
        def items(step, d):
            r = [("load", step, d)]
            if not dbg.get("b_no_gdn"):
                r += [("gdn", step, d, 0), ("gdn", step, d, 1)]
            if not dbg.get("b_no_ml"):
                r.append(("ml", step, d))
            return r

        SNAP_STEP = 32
        nf, nb = len(order[0]), len(order[1])
        for step in range(SNAP_STEP + 1):
            pending.extend(items(step, 0))
        pending.append(("snap",))
        for i in range(max(nf - SNAP_STEP - 1, nb)):
            if SNAP_STEP + 1 + i < nf:
                pending.extend(items(SNAP_STEP + 1 + i, 0))
            if i < nb:
                pending.extend(items(i, 1))
        free_g = list(range(NG))
        free_m = list(range(NM))
        done = set()
        started = set()
        active = []
        rounds = 0
        GAP = dbg.get("b_gap", 5)
        last_gstart = [-GAP]
        loaded = set()
        while pending or active:
            while pending:
                it = pending[0]
                if it[0] == "snap":
                    if any((k[2] == 0 and k not in done) for k in started):
                        break
                    do_snapshot()
                    pending.popleft()
                    continue
                if it[0] == "load":
                    _, step, d = it
                    if any((k[1] == step - 2 and k[2] == d and k not in done) for k in started):
                        break
                    load_step(step, d)
                    shared_pre(step, d)
                    pending.popleft()
                    continue
                if it[0] == "gdn":
                    _, step, d, hg = it
                    if not free_g or rounds - last_gstart[0] < GAP:
                        break
                    last_gstart[0] = rounds
                    si = free_g.pop(0)
                    started.add(it)
                    active.append((it, gdn_group(step, d, hg, gslots[si]), ("g", si)))
                    pending.popleft()
                    continue
                if it[0] == "ml":
                    _, step, d = it
                    key_prev = ("ml", step - 1, d)
                    if (step > 0 and key_prev not in done) or not free_m:
                        break
                    si = free_m.pop(0)
                    started.add(it)
                    active.append((it, ml_group(step, d, mslots[si]), ("m", si)))
                    pending.popleft()
                    continue
            rounds += 1
            for ent in list(active):
                it, gen, (kind, si) = ent
                try:
                    next(gen)
                except StopIteration:
                    active.remove(ent)
                    done.add(it)
                    (free_g if kind == "g" else free_m).append(si)
        B.barrier()
        st.close()

    def make_ml_group(self):
        B = self.B
        env = self._b_env
        data, dirc, cbufs, psr = env["data"], env["dirc"], env["cbufs"], env["psr"]
        bc3, bcm = env["bc3"], env["bcm"]
        C, bC, Cb, Cb_cur = env["C"], env["bC"], env["Cb"], env["Cb_cur"]
        nst, nbf, n_cur, nb_cur, mst, m_cur = env["nst"], env["nbf"], env["n_cur"], env["nb_cur"], env["mst"], env["m_cur"]
        ring_num, ring_h = env["ring_num"], env["ring_h"]
        out_lo, out_hi = env["out_lo"], env["out_hi"]

        def bc3n(ap2, n):
            return ap2.unsqueeze(2).to_broadcast([128, ap2.shape[1], n])

        def ml_group(step, d, sl):
            dd = data[(step, d)]
            dc = dirc[d]
            c = dd["c"]
            need_o = out_lo <= c < out_hi
            tg, bg = dd["GT"]; gs, bgs = dd["gs"]
            mq, bmq = dd["MQ"]; mk, bmk = dd["MK"]; mv, bmv = dd["MV"]
            X, bX = sl["X"]; Y, bY = sl["Y"]; Pm, bPm = sl["Pm"]; PT, bPT = sl["PT"]; Kw, bKw = sl["Kw"]
            sm, bsm = sl["sm"]; Ct, bCt = sl["Ct"]
            bcc = gs[:, 8:12]
            blast = gs[:, 20:24]
            cvec = gs[:, 48:52]
            mprev, bmprev = m_cur[d]
            B.op("pool", lambda e: e.tensor_tensor(out=X[:], in0=bcm(self.ident_f[:]), in1=bc3(cvec, 128), op=ALU.mult), reads=[bgs] + cbufs, writes=[bX])
            B.op("dve", lambda e: e.tensor_tensor(out=sm[:, 4:8], in0=bcc, in1=mprev[:, 0:4], op=ALU.add), reads=[bgs, bmprev], writes=[bsm])
            yield
            pC, bpC = psr.next()
            for u in range(4):
                B.op("pe", lambda e, u=u: e.matmul(pC[:, u * 128:(u + 1) * 128], lhsT=self.ones_f[:], rhs=X[:, u, :], start=True, stop=True),
                     reads=[bX] + cbufs, writes=[bpC], inc=(u == 3))
            B.op("dve", lambda e: e.tensor_tensor(out=Y[:], in0=pC[:, :].rearrange("p (u l) -> p u l", u=4), in1=bc3(bcc, 128), op=ALU.add),
                 reads=[bpC, bgs], writes=[bY])
            yield
            B.op("pool", lambda e: e.tensor_tensor(out=Y[:], in0=Y[:], in1=bcm(dc["MB"][:]), op=ALU.add), reads=[bY] + cbufs, writes=[bY])
            B.op("dve", lambda e: e.tensor_reduce(out=sm[:, 0:4], in_=Y[:], axis=AX.X, op=ALU.max), reads=[bY], writes=[bsm])
            B.op("dve", lambda e: e.tensor_tensor(out=sm[:, 8:12], in0=sm[:, 0:4], in1=sm[:, 4:8], op=ALU.max), reads=[bsm], writes=[bsm])
            B.op("dve", lambda e: e.tensor_scalar(out=sm[:, 12:16], in0=sm[:, 8:12], scalar1=-1.0, scalar2=None, op0=ALU.mult), reads=[bsm], writes=[bsm])
            B.op("dve", lambda e: e.tensor_tensor(out=sm[:, 16:20], in0=sm[:, 4:8], in1=sm[:, 8:12], op=ALU.subtract), reads=[bsm], writes=[bsm])
            yield
            if need_o:
                for u in range(4):
                    B.op("act", lambda e, u=u: e.activation(out=X[:, u, :], in_=Y[:, u, :], func=AF.Exp, bias=sm[:, 12 + u:13 + u]), reads=[bY, bsm], writes=[bX])
                B.op("act", lambda e: e.activation(out=sm[:, 16:20], in_=sm[:, 16:20], func=AF.Exp), reads=[bsm], writes=[bsm])
                B.op("act", lambda e: e.activation(out=sm[:, 20:24], in_=sm[:, 12:16], func=AF.Exp), reads=[bsm], writes=[bsm])
                pQK, bpQK = psr.next()
                for u in range(4):
                    B.op("pe", lambda e, u=u: e.matmul(pQK[:, u * 128:(u + 1) * 128], lhsT=mq[:, u, :], rhs=mk[:, u, :], start=True, stop=True),
                         reads=[bmq, bmk], writes=[bpQK], inc=(u == 3))
                B.op("dve", lambda e: e.tensor_tensor(out=Pm[:], in0=pQK[:, :].rearrange("p (u l) -> p u l", u=4), in1=X[:], op=ALU.mult),
                     reads=[bpQK, bX], writes=[bPm])
                yield
                pT, bpT = psr.next()
                pTb = pT[:].bitcast(BF16)
                for u in range(4):
                    B.op("pe", lambda e, u=u: e.transpose(pTb[:, u * 128:(u + 1) * 128], Pm[:, u, :], self.ident_b[:]), reads=[bPm] + cbufs, writes=[bpT], inc=(u == 3))
                B.op("act", lambda e: e.activation(out=PT[:].rearrange("p u l -> p (u l)"), in_=pTb[:, 0:512], func=AF.Identity), reads=[bpT], writes=[bPT])
                yield
                cbt, bcb = Cb_cur[d]
                nbt, bnb = nb_cur[d]
                pDn, bpDn = psr.next()
                for u in range(4):
                    B.op("pe", lambda e, u=u: e.matmul(pDn[:, u:u + 1], lhsT=mq[:, u, :], rhs=nbt[:, u:u + 1], start=True, stop=True), reads=[bmq, bnb], writes=[bpDn], inc=False)
                for u in range(4):
                    B.op("pe", lambda e, u=u: e.matmul(pDn[:, 4 + u:5 + u], lhsT=PT[:, u, :], rhs=self.ones_b[:, 0:1], start=True, stop=True),
                         reads=[bPT] + cbufs, writes=[bpDn], inc=(u == 3))
                B.op("dve", lambda e: e.tensor_tensor(out=sm[:, 24:28], in0=pDn[:, 0:4], in1=sm[:, 16:20], op=ALU.mult), reads=[bpDn, bsm], writes=[bsm])
                B.op("dve", lambda e: e.tensor_tensor(out=sm[:, 24:28], in0=pDn[:, 4:8], in1=sm[:, 24:28], op=ALU.add), reads=[bpDn, bsm], writes=[bsm])
                B.op("dve", lambda e: e.tensor_tensor(out=sm[:, 24:28], in0=sm[:, 24:28], in1=sm[:, 24:28], op=ALU.mult), reads=[bsm], writes=[bsm])
                B.op("dve", lambda e: e.tensor_tensor(out=sm[:, 28:32], in0=sm[:, 20:24], in1=sm[:, 20:24], op=ALU.mult), reads=[bsm], writes=[bsm])
                B.op("dve", lambda e: e.tensor_tensor(out=sm[:, 24:28], in0=sm[:, 24:28], in1=sm[:, 28:32], op=ALU.max), reads=[bsm], writes=[bsm])
                B.op("pool", lambda e: e.tensor_tensor(out=sm[:, 28:32], in0=sm[:, 24:28], in1=self.nhalf[:, 0:4], op=ALU.pow), reads=[bsm] + cbufs, writes=[bsm])
                yield
                num, bnum = ring_num.next()
                hh, bhh = ring_h.next()
                for pr in range(2):
                    pN1, bpN1 = psr.next()
                    pN2, bpN2 = psr.next()
                    for uu in range(2):
                        u = pr * 2 + uu
                        B.op("pe", lambda e, u=u, uu=uu, pN1=pN1: e.matmul(pN1[:, uu * 256:(uu + 1) * 256], lhsT=mq[:, u, :], rhs=cbt[:, u, :], start=True, stop=True),
                             reads=[bmq, bcb], writes=[bpN1], inc=(uu == 1))
                    for uu in range(2):
                        u = pr * 2 + uu
                        B.op("pe", lambda e, u=u, uu=uu, pN2=pN2: e.matmul(pN2[:, uu * 256:(uu + 1) * 256], lhsT=PT[:, u, :], rhs=mv[:, u * 256:(u + 1) * 256], start=True, stop=True),
                             reads=[bPT, bmv], writes=[bpN2], inc=(uu == 1))
                    B.op("dve", lambda e, pr=pr, pN1=pN1: e.tensor_tensor(out=num[:, pr * 2:pr * 2 + 2, :], in0=pN1[:, :].rearrange("p (u l) -> p u l", u=2),
                                                                         in1=bc3n(sm[:, 16 + pr * 2:18 + pr * 2], 256), op=ALU.mult), reads=[bpN1, bsm], writes=[bnum])
                    B.op("dve", lambda e, pr=pr, pN2=pN2: e.tensor_tensor(out=num[:, pr * 2:pr * 2 + 2, :], in0=pN2[:, :].rearrange("p (u l) -> p u l", u=2),
                                                                         in1=num[:, pr * 2:pr * 2 + 2, :], op=ALU.add), reads=[bpN2, bnum], writes=[bnum])
                B.op("dve", lambda e: e.tensor_tensor(out=hh[:], in0=num[:], in1=bc3n(sm[:, 28:32], 256), op=ALU.mult), reads=[bnum, bsm], writes=[bhh])
                dst = (self.HF if d == 0 else self.HB)[c - 2]
                B.dma("sp", dst[:, :], hh[:].rearrange("p u l -> p (u l)"), bhh, reads=[bhh])
                yield
            pSel, bpSel = psr.next()
            B.op("pe", lambda e: e.matmul(pSel[:, 0:4], lhsT=dc["SEL"][:], rhs=sm[:, 8:12], start=True, stop=True), reads=[bsm] + cbufs, writes=[bpSel])
            mnew, bmnew = mst[d].next()
            B.op("act", lambda e: e.activation(out=mnew[:], in_=pSel[:, 0:4], func=AF.Identity), reads=[bpSel], writes=[bmnew])
            B.op("dve", lambda e: e.tensor_tensor(out=sm[:, 32:36], in0=cvec, in1=blast, op=ALU.add), reads=[bgs], writes=[bsm])
            B.op("dve", lambda e: e.tensor_tensor(out=sm[:, 32:36], in0=sm[:, 32:36], in1=mnew[:], op=ALU.subtract), reads=[bsm, bmnew], writes=[bsm])
            B.op("dve", lambda e: e.tensor_tensor(out=sm[:, 36:40], in0=blast, in1=mprev[:, 0:4], op=ALU.add), reads=[bgs, bmprev], writes=[bsm])
            B.op("dve", lambda e: e.tensor_tensor(out=sm[:, 36:40], in0=sm[:, 36:40], in1=mnew[:], op=ALU.subtract), reads=[bsm, bmnew], writes=[bsm])
            B.op("act", lambda e: e.activation(out=sm[:, 32:40], in_=sm[:, 32:40], func=AF.Exp), reads=[bsm], writes=[bsm])
            yield
            pKt, bpKt = psr.next()
            pKtb = pKt[:].bitcast(BF16)
            for u in range(4):
                B.op("pe", lambda e, u=u: e.transpose(pKtb[:, u * 128:(u + 1) * 128], mk[:, u, :], self.ident_b[:]), reads=[bmk] + cbufs, writes=[bpKt], inc=(u == 3))
            B.op("dve", lambda e: e.tensor_tensor(out=Kw[:], in0=pKtb[:, 0:512].rearrange("p (u l) -> p u l", u=4), in1=bc3(sm[:, 32:36], 128), op=ALU.mult),
                 reads=[bpKt, bsm], writes=[bKw])
            m_cur[d] = (mnew, bmnew)
            for pr in range(2):
                B.op("pool", lambda e, pr=pr: e.tensor_tensor(out=Ct[:, pr * 2:pr * 2 + 2, :], in0=C[d][:, pr * 2:pr * 2 + 2, :], in1=bc3n(sm[:, 36 + pr * 2:38 + pr * 2], 256), op=ALU.mult),
                     reads=[bC[d], bsm], writes=[bCt])
            yield
            for pr in range(2):
                pC2, bpC2 = psr.next()
                for uu in range(2):
                    u = pr * 2 + uu
                    B.op("pe", lambda e, u=u, uu=uu, pC2=pC2: e.matmul(pC2[:, uu * 256:(uu + 1) * 256], lhsT=Kw[:, u, :], rhs=mv[:, u * 256:(u + 1) * 256], start=True, stop=True),
                         reads=[bKw, bmv], writes=[bpC2], inc=(uu == 1))
                B.op("dve", lambda e, pr=pr, pC2=pC2: e.tensor_tensor(out=C[d][:, pr * 2:pr * 2 + 2, :], in0=pC2[:, :].rearrange("p (u l) -> p u l", u=2), in1=Ct[:, pr * 2:pr * 2 + 2, :], op=ALU.add),
                     reads=[bpC2, bCt], writes=[bC[d]])
            pN, bpN = psr.next()
            for u in range(4):
                B.op("pe", lambda e, u=u: e.matmul(pN[:, u:u + 1], lhsT=Kw[:, u, :], rhs=self.ones_b[:, 0:1], start=True, stop=True), reads=[bKw] + cbufs, writes=[bpN], inc=(u == 3))
            nold, bnold = n_cur[d]
            nnew, bnnew = nst[d].next()
            B.op("dve", lambda e: e.tensor_tensor(out=nnew[:, 4:8], in0=nold[:, 0:4], in1=sm[:, 36:40], op=ALU.mult), reads=[bnold, bsm], writes=[bnnew])
            B.op("dve", lambda e: e.tensor_tensor(out=nnew[:, 0:4], in0=pN[:, 0:4], in1=nnew[:, 4:8], op=ALU.add), reads=[bpN, bnnew], writes=[bnnew])
            nbn, bnbn = nbf[d].next()
            B.op("act", lambda e: e.activation(out=nbn[:], in_=nnew[:, 0:4], func=AF.Identity), reads=[bnnew], writes=[bnbn])
            cbn, bcbn = Cb[d].next()
            B.op("act", lambda e: e.activation(out=cbn[:], in_=C[d][:], func=AF.Identity), reads=[bC[d]], writes=[bcbn])
            n_cur[d] = (nnew, bnnew)
            nb_cur[d] = (nbn, bnbn)
            Cb_cur[d] = (cbn, bcbn)
            yield

        return ml_group

    def phaseC1(self):
        B, nc, inp = self.B, self.nc, self.inp
        st = ExitStack()
        dbg = self.debug
        wo, bwo = self.load_w_bf16(st, "c_wo", inp["w_o"], 8, 4096, 8)
        wbg, bwbg = self.load_w_bf16(st, "c_wbg", inp["w_bg"], 8, 1024, 2)
        wbm, bwbm = self.load_w_bf16(st, "c_wbm", inp["w_bm"], 8, 1024, 2)
        wout, bwout = self.load_w_bf16(st, "c_wout", inp["w_out"], 8, 1024, 2)
        nwb = B.sb(st, "c_nwb", [128, 2, 1024], F32)
        bnwb = Buf("c_nwb")
        B.dma("sp", nwb[:, 0, :], inp["gnw_bc"][:, :], bnwb, writes=[bnwb])
        B.dma("sp", nwb[:, 1, :], inp["mnw_bc"][:, :], bnwb, writes=[bnwb])
        bX1 = Buf("X1")
        NS = 2
        xt = B.sb(st, "c_x", [128, NS, 1024], F32)
        bxts = [Buf("c_x%d" % i) for i in range(NS)]
        B.op("pool", lambda e: e.memset(xt[0:64, 0, :], 0.0), writes=[bxts[0]])
        B.dma("sp", self.X1[0:64, :], xt[0:64, 0, :], bxts[0], reads=[bxts[0]], writes=[bX1])
        xn = B.sb(st, "c_xn", [128, NS, 1024], BF16); bxn = Buf("c_xn")
        sq = B.sb(st, "c_sq", [128, 24], F32); bsq = Buf("c_sq")
        junk = None; bjunk = None
        hxT = B.sb(st, "c_hxT", [128, 8, NS * 128], BF16); bhx = Buf("c_hxT")
        oar = Ring(B, st, "c_oa", 2, [128, 1024], F32)
        obr = Ring(B, st, "c_ob", 2, [128, 1024], F32)
        gtr = Ring(B, st, "c_gt", 2, [128, 1024], BF16)
        osqr = Ring(B, st, "c_osq", 2, [128, 1024], BF16)
        smr = Ring(B, st, "c_sm", 2, [128, 32], F32)
        ogr = Ring(B, st, "c_og", 2, [128, 1024], BF16)
        brT = [B.sb(st, "c_brT%d" % i, [128, 8, NS * 128], BF16) for i in range(2)]
        bbrT = [Buf("c_brT%d" % i) for i in range(2)]
        sg = Ring(B, st, "c_sg", 6, [128, NS * 128], F32)
        mT = B.sb(st, "c_mT", [128, 8, NS * 128], BF16); bmT = Buf("c_mT")
        tmp = Ring(B, st, "c_tmp", 2, [128, 512], F32)
        ptr = Ring(B, st, "c_ptr", 2, [128, 512], F32, psum=True)
        pmm = Ring(B, st, "c_pmm", 6, [128, 512], F32, psum=True)
        nt = OWN_T // 128
        sts = []
        i = 0
        while i < nt:
            ns = min(NS, nt - i)
            sts.append((i, ns))
            i += ns
        if dbg.get("c1_tiles"):
            sts = sts[: dbg["c1_tiles"]]
        for (t0, ns) in sts:
            n = ns * 128
            for s in range(ns):
                B.dma("sp", xt[:, s, :], inp["x"][(t0 + s) * 128:(t0 + s + 1) * 128, :], bxts[s], writes=[bxts[s]])
            self.norm_transpose(xt, bxts, ns, xn, bxn, sq, bsq, junk, bjunk, ptr, hxT, bhx, 0, 0)
            def branch_gen(br, s_):
                nh, hd = (8, 128) if br == 0 else (4, 256)
                srcf, srcb = (self.OF, self.OB) if br == 0 else (self.HF, self.HB)
                c = t0 + s_
                oa, boa = oar.next()
                ob, bob = obr.next()
                gt, bgt = gtr.next()
                osq, bosq = osqr.next()
                sm, bsm = smr.next()
                og, bog = ogr.next()
                B.dma("sp", oa[:], srcf[c], boa, writes=[boa])
                B.dma("sp", ob[:], srcb[c], bob, writes=[bob])
                for hf in range(2):
                    p, bp = pmm.next()
                    for k in range(8):
                        B.op("pe", lambda e, k=k, p=p, hf=hf: e.matmul(
                            p[:], lhsT=hxT[:, k, s_ * 128:(s_ + 1) * 128], rhs=wo[:, k, br * 1024 + hf * 512: br * 1024 + (hf + 1) * 512],
                            start=(k == 0), stop=(k == 7)), reads=[bhx, bwo], writes=[bp])
                    B.op("act", lambda e, p=p, hf=hf: e.activation(out=gt[:, hf * 512:(hf + 1) * 512], in_=p[:],
                                                                  func=(AF.Silu if br == 0 else AF.Sigmoid)), reads=[bp], writes=[bgt])
                yield
                B.op("pool", lambda e: e.tensor_tensor(out=gt[:], in0=gt[:], in1=nwb[:, br, :], op=ALU.mult), reads=[bgt, bnwb], writes=[bgt])
                B.op("dve", lambda e: e.tensor_tensor(out=oa[:], in0=oa[:], in1=ob[:], op=ALU.add), reads=[boa, bob], writes=[boa])
                yield
                B.op("act", lambda e: e.activation(out=osq[:], in_=oa[:], func=AF.Square), reads=[boa], writes=[bosq])
                yield
                B.op("dve", lambda e: e.tensor_reduce(out=sm[:, 0:nh], in_=osq[:].rearrange("p (h e) -> p h e", h=nh), axis=AX.X, op=ALU.add),
                     reads=[bosq], writes=[bsm])
                B.op("dve", lambda e: e.tensor_scalar(out=sm[:, 8:8 + nh], in0=sm[:, 0:nh], scalar1=float(1.0 / hd), scalar2=float(EPS),
                                                      op0=ALU.mult, op1=ALU.add), reads=[bsm], writes=[bsm])
                yield
                B.op("pool", lambda e: e.tensor_tensor(out=sm[:, 16:16 + nh], in0=sm[:, 8:8 + nh], in1=self.nhalf[:, 0:nh], op=ALU.pow),
                     reads=[bsm, self.cb], writes=[bsm])
                yield
                B.op("dve", lambda e: e.tensor_tensor(out=osq[:].rearrange("p (h e) -> p h e", h=nh), in0=oa[:].rearrange("p (h e) -> p h e", h=nh),
                                                      in1=sm[:, 16:16 + nh].unsqueeze(2).to_broadcast([128, nh, hd]), op=ALU.mult),
                     reads=[boa, bsm], writes=[bosq])
                B.op("dve", lambda e: e.tensor_tensor(out=og[:], in0=osq[:], in1=gt[:], op=ALU.mult), reads=[bosq, bgt], writes=[bog])
                yield
                p, bp = ptr.next()
                pb = p[:].bitcast(BF16)
                for k in range(8):
                    B.op("pe", lambda e, k=k, pb=pb: e.transpose(pb[:, k * 128:(k + 1) * 128], og[:, k * 128:(k + 1) * 128], self.ident_b[:]),
                         reads=[bog, self.cb], writes=[bp])
                B.op("act", lambda e, pb=pb: e.activation(out=brT[br][:, :, s_ * 128:(s_ + 1) * 128], in_=pb[:, 0:1024].rearrange("p (k t) -> p k t", k=8),
                                                          func=AF.Identity), reads=[bp], writes=[bbrT[br]])

            self.run_pipeline([(lambda br=br, s_=s_: branch_gen(br, s_)) for br in range(2) for s_ in range(ns)], 2)

            def merge_gen(ncn):
                sgs = []
                for gi in range(2):
                    p, bp = pmm.next()
                    for k in range(8):
                        B.op("pe", lambda e, k=k, p=p, gi=gi: e.matmul(p[:, 0:n], lhsT=wo[:, k, 2048 + gi * 1024 + ncn * 128: 2048 + gi * 1024 + (ncn + 1) * 128],
                                                                      rhs=hxT[:, k, 0:n], start=(k == 0), stop=(k == 7)), reads=[bhx, bwo], writes=[bp])
                    g_, bg_ = sg.next()
                    B.op("act", lambda e, p=p, g_=g_: e.activation(out=g_[:, 0:n], in_=p[:, 0:n], func=AF.Sigmoid), reads=[bp], writes=[bg_])
                    sgs.append((g_, bg_))
                yield
                ys = []
                for br, (w, bw) in enumerate(((wbg, bwbg), (wbm, bwbm))):
                    p, bp = pmm.next()
                    for k in range(8):
                        B.op("pe", lambda e, k=k, p=p, w=w, br=br: e.matmul(p[:, 0:n], lhsT=w[:, k, ncn * 128:(ncn + 1) * 128], rhs=brT[br][:, k, 0:n],
                                                                           start=(k == 0), stop=(k == 7)), reads=[bbrT[br], bw], writes=[bp])
                    ys.append((p, bp))
                g0, bg0 = sgs[0]
                g1, bg1 = sgs[1]
                B.op("dve", lambda e: e.tensor_tensor(out=g0[:, 0:n], in0=ys[0][0][:, 0:n], in1=g0[:, 0:n], op=ALU.mult), reads=[ys[0][1], bg0], writes=[bg0])
                B.op("dve", lambda e: e.tensor_tensor(out=g1[:, 0:n], in0=ys[1][0][:, 0:n], in1=g1[:, 0:n], op=ALU.mult), reads=[ys[1][1], bg1], writes=[bg1])
                yield
                B.op("pool", lambda e: e.tensor_tensor(out=mT[:, ncn, 0:n], in0=g0[:, 0:n], in1=g1[:, 0:n], op=ALU.add), reads=[bg0, bg1], writes=[bmT])

            self.run_pipeline([(lambda ncn=ncn: merge_gen(ncn)) for ncn in range(8)], 3)

            def out_gen(s_, hf):
                p, bp = pmm.next()
                t_, bt_ = tmp.next()
                for k in range(8):
                    B.op("pe", lambda e, k=k: e.matmul(p[:], lhsT=mT[:, k, s_ * 128:(s_ + 1) * 128], rhs=wout[:, k, hf * 512:(hf + 1) * 512],
                                                         start=(k == 0), stop=(k == 7)), reads=[bmT, bwout], writes=[bp])
                B.op("dve", lambda e: e.tensor_tensor(out=t_[:], in0=p[:], in1=self.gate_bc[:, 0, hf * 512:(hf + 1) * 512], op=ALU.mult),
                     reads=[bp, self.bgate], writes=[bt_])
                yield
                B.op("pool", lambda e: e.tensor_tensor(out=xt[:, s_, hf * 512:(hf + 1) * 512], in0=xt[:, s_, hf * 512:(hf + 1) * 512], in1=t_[:], op=ALU.add),
                     reads=[bt_, bxts[s_]], writes=[bxts[s_]])
                if hf == 1:
                    B.dma("sp", self.X1[64 + (t0 + s_) * 128: 64 + (t0 + s_ + 1) * 128, :], xt[:, s_, :], bxts[s_], reads=[bxts[s_]], writes=[bX1])

            self.run_pipeline([(lambda s_=s_, hf=hf: out_gen(s_, hf)) for s_ in range(ns) for hf in range(2)], 2)
        B.barrier()
        st.close()

    def precast_wup(self):
        B = self.B
        self.WUPB = B.dram("WUPB", [44, 128, 8, 128], BF16)
        self.bwupb = Buf("WUPB")
        src = self.inp["w_up"].rearrange("(k p) (c j) -> c p k j", p=128, j=128)
        for c in range(44):
            B.dma("pool", self.WUPB[c], src[c], self.bwupb, writes=[self.bwupb])

    def phaseC2(self):
        B, nc, inp = self.B, self.nc, self.inp
        st = ExitStack()
        dbg = self.debug
        wd = B.sb(st, "d_wd", [128, 22, 1024], BF16)
        bwd = Buf("d_wd")
        wdv = inp["w_down"].rearrange("(c p) n -> p c n", p=128)
        for i in range(0, 22, 6):
            j = min(22, i + 6)
            B.dma("pool", wd[:, i:j, :], wdv[:, i:j, :], bwd, writes=[bwd])
        cw = B.sb(st, "d_cw", [128, 44, 9], F32)
        nob = B.sb(st, "d_nob", [128, 1024], F32)
        bsm0 = Buf("d_small")
        B.dma("sp", cw[:], inp["ffn_cw"][:, :, :], bsm0, writes=[bsm0])
        B.dma("sp", nob[:], inp["now_bc"][:, :], bsm0, writes=[bsm0])
        DEPTH = 3
        wup = Ring(B, st, "d_wup", DEPTH + 1, [128, 2, 8, 128], BF16)
        xt = B.sb(st, "d_x", [128, 5, 1024], F32)
        bxts = [Buf("d_x%d" % i) for i in range(5)]
        xn = B.sb(st, "d_xn", [128, 5, 1024], BF16); bxn = Buf("d_xn")
        sq = B.sb(st, "d_sq", [128, 24], F32); bsq = Buf("d_sq")
        junk = B.sb(st, "d_junk", [128, 1024], BF16); bjunk = Buf("d_junk")
        hxT = B.sb(st, "d_hxT", [128, 8, 640], BF16); bhx = Buf("d_hxT")
        upad = Ring(B, st, "d_up", 2 * DEPTH, [128, 10, 66], BF16)
        dgr = Ring(B, st, "d_dg", 2 * DEPTH, [128, 9, 128], BF16)
        sgt = Ring(B, st, "d_sg", DEPTH, [128, 512], F32)
        aT = B.sb(st, "d_aT", [128, 22, 512], BF16); baT = Buf("d_aT")
        xo = B.sb(st, "d_xo", [128, 4, 1024], F32)
        bxo = [Buf("d_xo%d" % i) for i in range(4)]
        t2 = Ring(B, st, "d_t2", 2, [128, 512], F32)
        sq2 = B.sb(st, "d_sq2", [128, 16], F32); bsq2 = Buf("d_sq2")
        pr = Ring(B, st, "d_pr", 8, [128, 512], F32, psum=True)
        for (u_, bu_) in upad.slots:
            B.op("pool", lambda e, u_=u_: e.memset(u_[:], 0.0), writes=[bu_])
        nblk = dbg.get("c2_blocks", 8)
        for j in range(nblk):
            r0 = 512 * j
            for s in range(5):
                B.dma("sp", xt[:, s, :], self.X1[r0 + s * 128: r0 + (s + 1) * 128, :], bxts[s], writes=[bxts[s]])
            for s in range(4):
                B.dma("sp", xo[:, s, :], self.X1[r0 + 64 + s * 128: r0 + 64 + (s + 1) * 128, :], bxo[s], writes=[bxo[s]])
            self.norm_transpose(xt, bxts, 5, xn, bxn, sq, bsq, junk, bjunk, pr, hxT, bhx, 0, 4)

            def pair_gen(c):
                w, bw = wup.next()
                B.dma("sp", w[:, 0], self.WUPB[c], bw, reads=[self.bwupb], writes=[bw])
                B.dma("sp", w[:, 1], self.WUPB[22 + c], bw, reads=[self.bwupb], writes=[bw])
                ups, dgs = [], []
                for part in range(2):
                    ch = c + 22 * part
                    u_, bu_ = upad.next()
                    dg, bdg = dgr.next()
                    ups.append((u_, bu_))
                    dgs.append((dg, bdg))
                    B.op("dve", lambda e, dg=dg, ch=ch: e.tensor_tensor(out=dg[:], in0=self.ident_b[:].unsqueeze(1).to_broadcast([128, 9, 128]),
                                                                      in1=cw[:, ch, :].unsqueeze(2).to_broadcast([128, 9, 128]), op=ALU.mult),
                         reads=[self.cb, bsm0], writes=[bdg])
                yield
                for part in range(2):
                    u_, bu_ = ups[part]
                    p1, bp1 = pr.next()
                    p2, bp2 = pr.next()
                    for k in range(8):
                        B.op("pe", lambda e, k=k, p1=p1, part=part: e.matmul(p1[:], lhsT=w[:, part, k, :], rhs=hxT[:, k, 0:512], start=(k == 0), stop=(k == 7)),
                             reads=[bw, bhx], writes=[bp1], inc=(k == 7))
                    for k in range(8):
                        B.op("pe", lambda e, k=k, p2=p2, part=part: e.matmul(p2[:, 0:128], lhsT=w[:, part, k, :], rhs=hxT[:, k, 512:640], start=(k == 0), stop=(k == 7)),
                             reads=[bw, bhx], writes=[bp2], inc=(k == 7))
                    B.op("act", lambda e, u_=u_, p1=p1: e.activation(out=u_[:, 0:8, 1:65], in_=p1[:].rearrange("p (r c) -> p r c", c=64), func=AF.Identity),
                         reads=[bp1], writes=[bu_])
                    B.op("act", lambda e, u_=u_, p2=p2: e.activation(out=u_[:, 8:10, 1:65], in_=p2[:, 0:128].rearrange("p (r c) -> p r c", c=64), func=AF.Identity),
                         reads=[bp2], writes=[bu_])
                    if j == 0:
                        B.op("pool", lambda e, u_=u_: e.memset(u_[:, 0:1, :], 0.0), writes=[bu_])
                yield
                pcs = []
                for part in range(2):
                    u_, bu_ = ups[part]
                    dg, bdg = dgs[part]
                    pc, bpc = pr.next()
                    t = 0
                    for dr in range(3):
                        for dc_ in range(3):
                            B.op("pe", lambda e, t=t, dr=dr, dc_=dc_, pc=pc, u_=u_, dg=dg: e.matmul(
                                pc[:].rearrange("p (r c) -> p r c", c=64), lhsT=dg[:, t, :], rhs=u_[:, dr:dr + 8, dc_:dc_ + 64], start=(t == 0), stop=(t == 8)),
                                reads=[bu_, bdg], writes=[bpc], inc=(t == 8))
                            t += 1
                    pcs.append((pc, bpc))
                s_, bs_ = sgt.next()
                B.op("act", lambda e: e.activation(out=s_[:], in_=pcs[0][0][:], func=AF.Silu), reads=[pcs[0][1]], writes=[bs_])
                B.op("dve", lambda e: e.tensor_tensor(out=aT[:, c, :], in0=pcs[1][0][:], in1=s_[:], op=ALU.mult), reads=[pcs[1][1], bs_], writes=[baT])

            self.run_pipeline([(lambda c=c: pair_gen(c)) for c in range(22)], DEPTH)
            for s in range(4):
                for hf in range(2):
                    p, bp = pr.next()
                    for c in range(22):
                        B.op("pe", lambda e, c=c, p=p, s=s, hf=hf: e.matmul(p[:], lhsT=aT[:, c, s * 128:(s + 1) * 128], rhs=wd[:, c, hf * 512:(hf + 1) * 512],
                                                                           start=(c == 0), stop=(c == 21)), reads=[baT, bwd], writes=[bp], inc=(c == 21))
                    t_, bt_ = t2.next()
                    B.op("dve", lambda e, p=p, t_=t_, hf=hf: e.tensor_tensor(out=t_[:], in0=p[:], in1=self.gate_bc[:, 1, hf * 512:(hf + 1) * 512], op=ALU.mult),
                         reads=[bp, self.bgate], writes=[bt_])
                    B.op("pool", lambda e, t_=t_, s=s, hf=hf: e.tensor_tensor(out=xo[:, s, hf * 512:(hf + 1) * 512], in0=xo[:, s, hf * 512:(hf + 1) * 512], in1=t_[:], op=ALU.add),
                         reads=[bt_, bxo[s]], writes=[bxo[s]])
                B.op("act", lambda e, s=s: e.activation(out=junk[:], in_=xo[:, s, :], func=AF.Square, accum_out=sq2[:, s:s + 1]), reads=[bxo[s]], writes=[bjunk, bsq2])
                B.op("dve", lambda e, s=s: e.tensor_scalar(out=sq2[:, 4 + s:5 + s], in0=sq2[:, s:s + 1], scalar1=float(D * EPS), scalar2=None, op0=ALU.add), reads=[bsq2], writes=[bsq2])
                B.op("pool", lambda e, s=s: e.tensor_tensor(out=sq2[:, 8 + s:9 + s], in0=sq2[:, 4 + s:5 + s], in1=self.nhalf[:, 0:1], op=ALU.pow), reads=[bsq2, self.cb], writes=[bsq2])
                B.op("dve", lambda e, s=s: e.scalar_tensor_tensor(out=xo[:, s, :], in0=xo[:, s, :], scalar=sq2[:, 8 + s:9 + s], in1=nob[:], op0=ALU.mult, op1=ALU.mult),
                     reads=[bxo[s], bsq2, bsm0], writes=[bxo[s]])
                B.op("act", lambda e, s=s: e.activation(out=xo[:, s, :], in_=xo[:, s, :], func=AF.Identity, scale=32.0), reads=[bxo[s]], writes=[bxo[s]])
                B.dma("sp", self.out[j * 512 + s * 128: j * 512 + (s + 1) * 128, :], xo[:, s, :], bxo[s], reads=[bxo[s]])
        B.barrier()
        st.close()


def _build_once(debug, needed):
    P = Prog(debug=debug, needed=needed)
    P.precast_wup()
    P.phase0()
    P.phaseA()
    P.phaseB()
    P.phaseC1()
    P.phaseC2()
    P.top.close()
    return P.B.finish(), P


def build_program(debug=None):
    _, dry = _build_once(debug, None)
    return _build_once(debug, dry.B.waited)


_CACHE = {}


def kernel(**inputs):
    inp = {k: np.asarray(v) for k, v in inputs.items()}
    if "nc" not in _CACHE:
        _CACHE["nc"] = build_program()[0]
    nc = _CACHE["nc"]
    in_maps = [prep_core(inp, core) for core in range(8)]
    res = run_bass_kernel_spmd(nc, in_maps, core_ids=list(range(8)))
    out = np.empty((4, T, D), np.float32)
    for core in range(8):
        o = np.asarray(res.results[core]["out"], np.float32)
        b = core // 2
        if core % 2 == 0:
            out[b, 0:4096] = o
        else:
            out[b, 4096:8192] = o[::-1]
    return out
```
